# Optimizing a Trainium2 kernel written in Bass

```python
import math
import jax, jax.numpy as jnp
from jax import lax
import numpy as np

D_MODEL = 1024
BATCH = 8
SEQ = 2048
DEPTH = 1
DEC_BATCH = 128
DEC_SEQ = 1
PAST_LEN = 16384
PAGE_SIZE = 128

RWKV_WIDTH = D_MODEL // 2
RWKV_HEAD = 64
RWKV_HEADS = RWKV_WIDTH // RWKV_HEAD
DECAY_LORA = 32
AAA_LORA = 32
GATE_LORA = 64
LNX_EPS = 64e-5
S5_WIDTH = D_MODEL // 2
S5_GROUP = 16
S5_GROUPS = S5_WIDTH // S5_GROUP
S5_STATE = 64
D_FF = ((8 * D_MODEL // 3 + 255) // 256) * 256
NORM_EPS = 1e-6
SHIFT_COLS = 3 * RWKV_WIDTH + DECAY_LORA + AAA_LORA + GATE_LORA
PROJ_COLS = SHIFT_COLS + S5_WIDTH + 2 * D_MODEL

kernel_name = "rwkv7_s5_gated_parallel_decode_step"


def _rms_norm(x, g):
    xf = x.astype(jnp.float32)
    y = xf * lax.rsqrt(jnp.mean(xf * xf, axis=-1, keepdims=True) + NORM_EPS)
    return (y * g.astype(jnp.float32)).astype(x.dtype)


def _rwkv7_mix(xr, wkv0, w0, w_decay_up, a0, w_aaa_up, w_gate_up, k_k, k_a, r_k, lnx_g, lnx_b):
    Bn, T, _ = xr.shape
    H, N = RWKV_HEADS, RWKV_HEAD
    c0 = 3 * RWKV_WIDTH
    r = xr[..., :RWKV_WIDTH]
    k = xr[..., RWKV_WIDTH:2 * RWKV_WIDTH]
    v = xr[..., 2 * RWKV_WIDTH:c0]
    w_lo = xr[..., c0:c0 + DECAY_LORA]
    a_lo = xr[..., c0 + DECAY_LORA:c0 + DECAY_LORA + AAA_LORA]
    g_lo = xr[..., c0 + DECAY_LORA + AAA_LORA:]
    w = -jax.nn.softplus(-(w0 + jnp.tanh(w_lo) @ w_decay_up)) - 0.5
    decay = jnp.exp(-jnp.exp(w))
    a = jax.nn.sigmoid(a0 + a_lo @ w_aaa_up)
    g = jax.nn.sigmoid(g_lo) @ w_gate_up
    heads = lambda t: t.reshape(Bn, T, H, N)
    kk = heads(k * k_k)
    kk = kk * lax.rsqrt(jnp.maximum(jnp.sum(kk * kk, axis=-1, keepdims=True), 1e-24))
    k = heads(k * (1.0 + (a - 1.0) * k_a))
    r, v, decay, a = heads(r), heads(v), heads(decay), heads(a)

    def step(S, inp):
        r_t, w_t, k_t, v_t, kk_t, b_t = inp
        sa = jnp.einsum('bhvk,bhk->bhv', S, -kk_t)
        S = S * w_t[:, :, None, :] + sa[..., :, None] * b_t[..., None, :] + v_t[..., :, None] * k_t[..., None, :]
        return S, jnp.einsum('bhvk,bhk->bhv', S, r_t)

    tm = lambda t: jnp.moveaxis(t, 1, 0)
    wkv1, o = lax.scan(step, wkv0.astype(jnp.float32), (tm(r), tm(decay), tm(k), tm(v), tm(kk), tm(kk * a)))
    o = jnp.moveaxis(o, 0, 1)
    mu = jnp.mean(o, axis=-1, keepdims=True)
    var = jnp.mean(jnp.square(o - mu), axis=-1, keepdims=True)
    o = ((o - mu) * lax.rsqrt(var + LNX_EPS)).reshape(Bn, T, RWKV_WIDTH) * lnx_g + lnx_b
    bonus = jnp.sum(r * k * r_k.reshape(H, N), axis=-1, keepdims=True) * v
    o = (o + bonus.reshape(Bn, T, RWKV_WIDTH)) * g
    return o, wkv1


def _s5_mix(u, re0, im0, lam_re, lam_im, log_dt, b_re, b_im, c_re, c_im, d_skip, glu_w1, glu_b1, glu_w2, glu_b2):
    Bn, T, _ = u.shape
    lam_re = lam_re.astype(jnp.float32)
    lam_im = lam_im.astype(jnp.float32)
    dt = jnp.exp(log_dt.astype(jnp.float32))[:, None]
    mag = jnp.exp(lam_re * dt)
    ang = lam_im * dt
    lb_re, lb_im = mag * jnp.cos(ang), mag * jnp.sin(ang)
    nr, ni = lb_re - 1.0, lb_im
    den = lam_re * lam_re + lam_im * lam_im
    f_re = (nr * lam_re + ni * lam_im) / den
    f_im = (ni * lam_re - nr * lam_im) / den
    bb_re = f_re[..., None] * b_re - f_im[..., None] * b_im
    bb_im = f_re[..., None] * b_im + f_im[..., None] * b_re
    ug = u.reshape(Bn, T, S5_GROUPS, S5_GROUP)
    bu_re = jnp.einsum('gpc,btgc->btgp', bb_re, ug)
    bu_im = jnp.einsum('gpc,btgc->btgp', bb_im, ug)
    re0 = re0.astype(jnp.float32)
    im0 = im0.astype(jnp.float32)
    bu_re = bu_re.at[:, 0].add(lb_re * re0 - lb_im * im0)
    bu_im = bu_im.at[:, 0].add(lb_re * im0 + lb_im * re0)
    a_re = jnp.broadcast_to(lb_re, bu_re.shape)
    a_im = jnp.broadcast_to(lb_im, bu_im.shape)

    def combine(e1, e2):
        a1r, a1i, b1r, b1i = e1
        a2r, a2i, b2r, b2i = e2
        return (a1r * a2r - a1i * a2i, a1r * a2i + a1i * a2r,
                a2r * b1r - a2i * b1i + b2r, a2r * b1i + a2i * b1r + b2i)

    _, _, xs_re, xs_im = lax.associative_scan(combine, (a_re, a_im, bu_re, bu_im), axis=1)
    y = jnp.einsum('gcp,btgp->btgc', c_re, xs_re) - jnp.einsum('gcp,btgp->btgc', c_im, xs_im)
    y = y.reshape(Bn, T, S5_WIDTH) + d_skip * u
    hgl = jax.nn.gelu(y)
    out = (hgl @ glu_w1 + glu_b1) * jax.nn.sigmoid(hgl @ glu_w2 + glu_b2)
    return out, xs_re[:, -1], xs_im[:, -1]


def _layer(x, shift0, wkv0, s5re0, s5im0, norm_pre_mix, norm_post_mix, norm_pre_ffn, norm_post_ffn,
           w_in, mu_shift, w0, w_decay_up, a0, w_aaa_up, w_gate_up, k_k, k_a, r_k, lnx_g, lnx_b, w_rwkv_out,
           s5_lam_re, s5_lam_im, s5_log_dt, s5_b_re, s5_b_im, s5_c_re, s5_c_im, s5_d,
           glu_w1, glu_b1, glu_w2, glu_b2, w_merge_out, w_ffn_gate, w_ffn_up, w_ffn_down):
    h = _rms_norm(x, norm_pre_mix)
    p = (h @ w_in).astype(jnp.float32)
    pr = p[..., :SHIFT_COLS]
    pr_prev = jnp.concatenate([shift0.astype(jnp.float32)[:, None, :], pr[:, :-1]], axis=1)
    xr = pr + (pr_prev - pr) * mu_shift
    oa, wkv1 = _rwkv7_mix(xr, wkv0, w0, w_decay_up, a0, w_aaa_up, w_gate_up, k_k, k_a, r_k, lnx_g, lnx_b)
    u = p[..., SHIFT_COLS:SHIFT_COLS + S5_WIDTH]
    ob, re1, im1 = _s5_mix(u, s5re0, s5im0, s5_lam_re, s5_lam_im, s5_log_dt, s5_b_re, s5_b_im,
                           s5_c_re, s5_c_im, s5_d, glu_w1, glu_b1, glu_w2, glu_b2)
    g_off = SHIFT_COLS + S5_WIDTH
    gate_a = jax.nn.sigmoid(p[..., g_off:g_off + D_MODEL])
    gate_b = jax.nn.sigmoid(p[..., g_off + D_MODEL:g_off + 2 * D_MODEL])
    merged = gate_a * (oa @ w_rwkv_out) + gate_b * ob
    mix = merged @ w_merge_out
    x = x + _rms_norm(mix, norm_post_mix).astype(x.dtype)
    h = _rms_norm(x, norm_pre_ffn)
    f = (jax.nn.silu(h @ w_ffn_gate) * (h @ w_ffn_up)) @ w_ffn_down
    x = x + _rms_norm(f, norm_post_ffn).astype(x.dtype)
    return x, pr[:, -1], wkv1, re1, im1


def setup_inputs(seed: int = 0) -> dict:
    key = jax.random.key(seed)
    ks = iter(jax.random.split(key, 64))
    L = DEPTH

    def nrm(shape, s):
        return jax.random.normal(next(ks), shape, jnp.float32) * s

    def unif(shape, lo, hi):
        return jax.random.uniform(next(ks), shape, jnp.float32, lo, hi)

    ramp = jnp.arange(RWKV_WIDTH, dtype=jnp.float32) / (RWKV_WIDTH - 1)
    n_state = jnp.arange(S5_STATE, dtype=jnp.float32)
    return {
        "x_prompt": nrm((BATCH, SEQ, D_MODEL), 1.0),
        "x_sample": nrm((DEC_BATCH, DEC_SEQ, D_MODEL), 1.0),
        "state_shift": nrm((L, DEC_BATCH, SHIFT_COLS), 1.0),
        "state_wkv": nrm((L, DEC_BATCH, RWKV_HEADS, RWKV_HEAD, RWKV_HEAD), 0.3),
        "state_s5_re": nrm((L, DEC_BATCH, S5_GROUPS, S5_STATE), 0.5),
        "state_s5_im": nrm((L, DEC_BATCH, S5_GROUPS, S5_STATE), 0.5),
        "norm_pre_mix": 1.0 + nrm((L, D_MODEL), 0.05),
        "norm_post_mix": 1.0 + nrm((L, D_MODEL), 0.05),
        "norm_pre_ffn": 1.0 + nrm((L, D_MODEL), 0.05),
        "norm_post_ffn": 1.0 + nrm((L, D_MODEL), 0.05),
        "w_in": nrm((L, D_MODEL, PROJ_COLS), D_MODEL ** -0.5),
        "mu_shift": unif((L, SHIFT_COLS), 0.0, 1.0),
        "w0": (-6.0 + 5.0 * ramp ** 0.7)[None, :] + nrm((L, RWKV_WIDTH), 0.1),
        "w_decay_up": nrm((L, DECAY_LORA, RWKV_WIDTH), 0.1),
        "a0": nrm((L, RWKV_WIDTH), 0.1),
        "w_aaa_up": nrm((L, AAA_LORA, RWKV_WIDTH), 0.1),
        "w_gate_up": nrm((L, GATE_LORA, RWKV_WIDTH), GATE_LORA ** -0.5),
        "k_k": 0.85 + nrm((L, RWKV_WIDTH), 0.05),
        "k_a": 1.0 + nrm((L, RWKV_WIDTH), 0.05),
        "r_k": nrm((L, RWKV_WIDTH), 0.1),
        "lnx_g": 1.0 + nrm((L, RWKV_WIDTH), 0.05),
        "lnx_b": nrm((L, RWKV_WIDTH), 0.01),
        "w_rwkv_out": nrm((L, RWKV_WIDTH, D_MODEL), RWKV_WIDTH ** -0.5),
        "s5_lam_re": -0.5 + nrm((L, S5_GROUPS, S5_STATE), 0.01),
        "s5_lam_im": (math.pi * n_state)[None, None, :] + nrm((L, S5_GROUPS, S5_STATE), 0.01),
        "s5_log_dt": unif((L, S5_GROUPS), math.log(1e-3), math.log(1e-1)),
        "s5_b_re": nrm((L, S5_GROUPS, S5_STATE, S5_GROUP), (2 * S5_GROUP) ** -0.5),
        "s5_b_im": nrm((L, S5_GROUPS, S5_STATE, S5_GROUP), (2 * S5_GROUP) ** -0.5),
        "s5_c_re": nrm((L, S5_GROUPS, S5_GROUP, S5_STATE), S5_STATE ** -0.5),
        "s5_c_im": nrm((L, S5_GROUPS, S5_GROUP, S5_STATE), S5_STATE ** -0.5),
        "s5_d": nrm((L, S5_WIDTH), 1.0),
        "glu_w1": nrm((L, S5_WIDTH, D_MODEL), S5_WIDTH ** -0.5),
        "glu_b1": nrm((L, D_MODEL), 0.01),
        "glu_w2": nrm((L, S5_WIDTH, D_MODEL), S5_WIDTH ** -0.5),
        "glu_b2": nrm((L, D_MODEL), 0.01),
        "w_merge_out": nrm((L, D_MODEL, D_MODEL), D_MODEL ** -0.5),
        "w_ffn_gate": nrm((L, D_MODEL, D_FF), D_MODEL ** -0.5),
        "w_ffn_up": nrm((L, D_MODEL, D_FF), D_MODEL ** -0.5),
        "w_ffn_down": nrm((L, D_FF, D_MODEL), D_FF ** -0.5),
    }


def reference(x_prompt, x_sample, state_shift, state_wkv, state_s5_re, state_s5_im,
              norm_pre_mix, norm_post_mix, norm_pre_ffn, norm_post_ffn, w_in, mu_shift,
              w0, w_decay_up, a0, w_aaa_up, w_gate_up, k_k, k_a, r_k, lnx_g, lnx_b, w_rwkv_out,
              s5_lam_re, s5_lam_im, s5_log_dt, s5_b_re, s5_b_im, s5_c_re, s5_c_im, s5_d,
              glu_w1, glu_b1, glu_w2, glu_b2, w_merge_out, w_ffn_gate, w_ffn_up, w_ffn_down):
    yp, ys = x_prompt, x_sample
    sh_p, wkv_p, re_p, im_p = [], [], [], []
    sh_s, wkv_s, re_s, im_s = [], [], [], []
    for l in range(DEPTH):
        lw = (norm_pre_mix[l], norm_post_mix[l], norm_pre_ffn[l], norm_post_ffn[l], w_in[l], mu_shift[l],
              w0[l], w_decay_up[l], a0[l], w_aaa_up[l], w_gate_up[l], k_k[l], k_a[l], r_k[l], lnx_g[l], lnx_b[l],
              w_rwkv_out[l], s5_lam_re[l], s5_lam_im[l], s5_log_dt[l], s5_b_re[l], s5_b_im[l], s5_c_re[l],
              s5_c_im[l], s5_d[l], glu_w1[l], glu_b1[l], glu_w2[l], glu_b2[l], w_merge_out[l],
              w_ffn_gate[l], w_ffn_up[l], w_ffn_down[l])
        zp_shift = jnp.zeros((x_prompt.shape[0], SHIFT_COLS), jnp.float32)
        zp_wkv = jnp.zeros((x_prompt.shape[0], RWKV_HEADS, RWKV_HEAD, RWKV_HEAD), jnp.float32)
        zp_s5 = jnp.zeros((x_prompt.shape[0], S5_GROUPS, S5_STATE), jnp.float32)
        yp, a1, a2, a3, a4 = _layer(yp, zp_shift, zp_wkv, zp_s5, zp_s5, *lw)
        sh_p.append(a1); wkv_p.append(a2); re_p.append(a3); im_p.append(a4)
        ys, b1, b2, b3, b4 = _layer(ys, state_shift[l], state_wkv[l], state_s5_re[l], state_s5_im[l], *lw)
        sh_s.append(b1); wkv_s.append(b2); re_s.append(b3); im_s.append(b4)
    dt_p, dt_s = x_prompt.dtype, x_sample.dtype
    new_shift_prompt = jnp.stack(sh_p).astype(dt_p)
    new_wkv_prompt = jnp.stack(wkv_p).astype(dt_p)
    new_s5_re_prompt = jnp.stack(re_p).astype(dt_p)
    new_s5_im_prompt = jnp.stack(im_p).astype(dt_p)
    new_shift_sample = jnp.stack(sh_s).astype(dt_s)
    new_wkv_sample = jnp.stack(wkv_s).astype(dt_s)
    new_s5_re_sample = jnp.stack(re_s).astype(dt_s)
    new_s5_im_sample = jnp.stack(im_s).astype(dt_s)
    return (yp, ys, new_shift_prompt, new_wkv_prompt, new_s5_re_prompt, new_s5_im_prompt,
            new_shift_sample, new_wkv_sample, new_s5_re_sample, new_s5_im_sample)
```

```python
import math
from contextlib import ExitStack
import numpy as np
import concourse.bass as bass
import concourse.mybir as mybir
from concourse.bass_utils import run_bass_kernel_spmd

F32 = mybir.dt.float32
BF16 = mybir.dt.bfloat16
I32 = mybir.dt.int32
AF = mybir.ActivationFunctionType
ALU = mybir.AluOpType
AX = mybir.AxisListType

T = 2048
NS = 16
TT = T + NS
D = 1024
RW = 512
SHIFT = 1664
PROJ = 4224
DFF = 2816
NCORES = 8
CDEC = math.exp(-0.5)
TBLK = [(0, 512), (512, 512), (1024, 512), (1536, 512), (2048, 16)]
SAME_ENGINE_SYNC = True


class V:
    __slots__ = ("buf", "ap")

    def __init__(self, buf, ap):
        self.buf = buf
        self.ap = ap

    def __getitem__(self, idx):
        return V(self.buf, self.ap[idx])

    def bc(self, shape):
        return V(self.buf, self.ap.to_broadcast(list(shape)))

    def re(self, s, **kw):
        return V(self.buf, self.ap.rearrange(s, **kw))

    def un(self, axis):
        return V(self.buf, self.ap.unsqueeze(axis))

    def bitcast(self, dt):
        return V(self.buf, self.ap.bitcast(dt))


class Buf:
    def __init__(self, name, handle):
        self.name = name
        self.h = handle
        self.writes = {}
        self.reads = {}
        self.dsem = None
        self.dcnt = 0
        self.is_psum = False

    def __getitem__(self, idx):
        return V(self, self.h[idx])

    def v(self):
        return V(self, self.h[:])


class Kern:
    def __init__(self, nc, es):
        self.nc = nc
        self.es = es
        self.root_es = es
        self.eng = {"pe": nc.tensor, "act": nc.scalar, "dve": nc.vector, "pool": nc.gpsimd, "sp": nc.sync}
        self.sem = {}
        self.cnt = {}
        for e in ("pe", "act", "dve", "pool"):
            self.sem[e] = es.enter_context(nc.semaphore("s_" + e))
            self.cnt[e] = 0
        self.obs = {e: {} for e in self.eng}
        self.semname = {}
        self.dsems = []
        self.nbuf = 0
        self.psf_i = 0
        self.psb_i = 0

    def sb(self, name, shape, dt):
        self.nbuf += 1
        import os
        if os.environ.get("KDBG"):
            sz = int(np.prod(shape[1:])) * (2 if dt == BF16 else 4)
            self._tot = getattr(self, "_tot", 0) + sz
            print(f"alloc {name} {shape} {sz} depth={len(getattr(self, '_saved', []))} cum_noFree={self._tot}")
        h = self.es.enter_context(self.nc.sbuf_tensor(f"{name}_{self.nbuf}", list(shape), dt))
        return Buf(name, h)

    def barrier(self):
        for e in ("pe", "act", "dve", "pool", "sp"):
            for e2 in ("pe", "act", "dve", "pool"):
                if e2 != e and self.cnt[e2]:
                    self._wait(e, e2, self.sem[e2], self.cnt[e2])
            for ob in self.dsems:
                self._wait(e, "d:" + ob.name + str(id(ob)), ob.dsem, ob.dcnt)

    def push_scope(self):
        self._saved = getattr(self, "_saved", [])
        self._saved.append(self.es)
        self.es = ExitStack()

    def pop_scope(self):
        self.barrier()
        self.es.close()
        self.es = self._saved.pop()

    def ps(self, name, shape, dt):
        self.nbuf += 1
        h = self.es.enter_context(self.nc.psum_tensor(f"{name}_{self.nbuf}", list(shape), dt))
        b = Buf(name, h)
        b.is_psum = True
        return b

    def dram(self, name, shape, dt, kind="Internal"):
        h = self.nc.dram_tensor(name, list(shape), dt, kind=kind)
        return Buf(name, h.ap())

    def newsem(self, name):
        s = self.root_es.enter_context(self.nc.semaphore(name))
        return s

    def _wait(self, e, sem_key, sem, val):
        o = self.obs[e]
        if o.get(sem_key, 0) >= val:
            return
        self.eng[e].wait_ge(sem, val)
        o[sem_key] = val

    def _deps(self, e, W, R):
        need = {}
        for b in R:
            for k, (s, v) in b.writes.items():
                if need.get(k, (None, 0))[1] < v:
                    need[k] = (s, v)
            if b.is_psum:
                for k, (s, v) in b.reads.items():
                    if k != e and need.get(k, (None, 0))[1] < v:
                        need[k] = (s, v)
        for b in W:
            for k, (s, v) in b.writes.items():
                if need.get(k, (None, 0))[1] < v:
                    need[k] = (s, v)
            for k, (s, v) in b.reads.items():
                if need.get(k, (None, 0))[1] < v:
                    need[k] = (s, v)
        for k, (s, v) in need.items():
            if k == e:
                if e == "pe" or not SAME_ENGINE_SYNC:
                    continue
            self._wait(e, k, s, v)

    def _bufs(self, vs):
        out = []
        for x in vs:
            if isinstance(x, V):
                if x.buf not in out:
                    out.append(x.buf)
            elif isinstance(x, Buf):
                if x not in out:
                    out.append(x)
        return out

    def op(self, e, fn, W, R):
        Wb = self._bufs(W)
        Rb = self._bufs(R)
        self._deps(e, Wb, Rb)
        inst = fn()
        self.cnt[e] += 1
        c = self.cnt[e]
        inst.then_inc(self.sem[e], 1)
        for b in Wb:
            b.reads = {}
            b.writes[e] = (self.sem[e], c)
        for b in Rb:
            if b not in Wb:
                b.reads[e] = (self.sem[e], c)
        return inst

    def dma(self, q, out, in_, sem_buf=None, slow=False):
        Wb = self._bufs([out])
        Rb = self._bufs([in_])
        self._deps(q, Wb, Rb)
        ob = sem_buf if sem_buf is not None else out.buf
        if ob.dsem is None:
            ob.dsem = self.newsem("d_" + ob.name + str(len(self.dsems)))
            self.dsems.append(ob)
        if slow:
            with self.nc.allow_non_contiguous_dma(reason="small parameter layout load"):
                inst = self.eng[q].dma_start(out=out.ap, in_=in_.ap)
        else:
            inst = self.eng[q].dma_start(out=out.ap, in_=in_.ap)
        ob.dcnt += 16
        inst.then_inc(ob.dsem, 16)
        key = "d:" + ob.name + str(id(ob))
        for b in Wb:
            b.reads = {}
            b.writes[key] = (ob.dsem, ob.dcnt)
        for b in Rb:
            if b not in Wb:
                b.reads[key] = (ob.dsem, ob.dcnt)
        return inst

    def finish(self):
        for ob in self.dsems:
            self._wait("sp", "d:" + ob.name + str(id(ob)), ob.dsem, ob.dcnt)
        for e in ("pe", "act", "dve", "pool"):
            if self.cnt[e]:
                self._wait("sp", e, self.sem[e], self.cnt[e])

    @staticmethod
    def _a(x):
        return x.ap if isinstance(x, V) else x

    def act(self, out, in_, func, bias=None, scale=None, accum=None):
        kw = {}
        if bias is not None:
            kw["bias"] = self._a(bias)
        if scale is not None:
            kw["scale"] = self._a(scale)
        if accum is not None:
            kw["accum_out"] = self._a(accum)
        return self.op("act", lambda: self.nc.scalar.activation(out=out.ap, in_=in_.ap, func=func, **kw),
                       [out, accum], [in_, bias, scale])

    def tt(self, e, out, in0, in1, op):
        return self.op(e, lambda: self.eng[e].tensor_tensor(out=out.ap, in0=in0.ap, in1=in1.ap, op=op),
                       [out], [in0, in1])

    def ts(self, e, out, in0, s1, s2, op0, op1=None):
        if op1 is None:
            return self.op(e, lambda: self.eng[e].tensor_scalar(out=out.ap, in0=in0.ap, scalar1=self._a(s1),
                                                               scalar2=None, op0=op0), [out], [in0, s1])
        return self.op(e, lambda: self.eng[e].tensor_scalar(out=out.ap, in0=in0.ap, scalar1=self._a(s1),
                                                           scalar2=self._a(s2), op0=op0, op1=op1),
                       [out], [in0, s1, s2])

    def stt(self, out, in0, scalar, in1, op0, op1, e="dve"):
        return self.op(e, lambda: self.eng[e].scalar_tensor_tensor(out=out.ap, in0=in0.ap, scalar=self._a(scalar),
                                                                  in1=in1.ap, op0=op0, op1=op1),
                       [out], [in0, scalar, in1])

    def copy(self, e, out, in_):
        if e == "act":
            return self.act(out, in_, AF.Copy)
        return self.op(e, lambda: self.eng[e].tensor_copy(out=out.ap, in_=in_.ap), [out], [in_])

    def memset(self, e, out, val):
        return self.op(e, lambda: self.eng[e].memset(out.ap, val), [out], [])

    def recip(self, out, in_):
        return self.op("dve", lambda: self.nc.vector.reciprocal(out=out.ap, in_=in_.ap), [out], [in_])

    def reduce(self, out, in_, op=ALU.add, axis=AX.X):
        return self.op("dve", lambda: self.nc.vector.tensor_reduce(out=out.ap, in_=in_.ap, axis=axis, op=op),
                       [out], [in_])

    def scan(self, out, d0, d1, init, op0=ALU.mult, op1=ALU.add):
        return self.op("dve", lambda: self.nc.vector.tensor_tensor_scan(out=out.ap, data0=d0.ap, data1=d1.ap,
                                                                       initial=self._a(init), op0=op0, op1=op1),
                       [out], [d0, d1, init])

    def mm(self, out, lhsT, rhs, start=True, stop=True):
        return self.op("pe", lambda: self.nc.tensor.matmul(out.ap, lhsT=lhsT.ap, rhs=rhs.ap, start=start, stop=stop),
                       [out], [lhsT, rhs])

    def tr(self, out, in_, ident):
        return self.op("pe", lambda: self.nc.tensor.transpose(out.ap, in_.ap, ident.ap), [out], [in_, ident])

    def aselect(self, out, in_, pattern, cmp, fill, base, cm):
        return self.op("pool", lambda: self.nc.gpsimd.affine_select(out=out.ap, in_=in_.ap, pattern=pattern,
                                                                   compare_op=cmp, fill=fill, base=base,
                                                                   channel_multiplier=cm), [out], [in_])

    def iota(self, out, pattern, base, cm):
        return self.op("pool", lambda: self.nc.gpsimd.iota(out.ap, pattern=pattern, base=base,
                                                          channel_multiplier=cm), [out], [])


def dap(buf, offset, ap):
    base = buf.h if not hasattr(buf.h, "ap") or isinstance(buf.h, bass.AP) else buf.h
    t = base.tensor if isinstance(base, bass.AP) else base
    return V(buf, bass.AP(t, offset, [list(x) for x in ap]))


def build(stage=99):
    nc = bass.Bass("TRN2", target_bir_lowering=False)
    es = ExitStack()
    K = Kern(nc, es)

    def din(name, shape, dt=F32):
        return Buf(name, nc.dram_tensor(name, list(shape), dt, kind="ExternalInput").ap())

    def dout(name, shape):
        return Buf(name, nc.dram_tensor(name, list(shape), F32, kind="ExternalOutput").ap())

    x_p = din("x_p", [T, D])
    x_s = din("x_s", [NS, D])
    st_shift = din("st_shift", [NS, SHIFT])
    st_wkv = din("st_wkv", [NS * 8, 64 * 64])
    st_re = din("st_re", [NS, 2048])
    st_im = din("st_im", [NS, 2048])
    g_pre_mix = din("norm_pre_mix", [D])
    g_post_mix = din("norm_post_mix", [D])
    g_pre_ffn = din("norm_pre_ffn", [D])
    g_post_ffn = din("norm_post_ffn", [D])
    w_in = din("w_in", [D, PROJ])
    mu_shift = din("mu_shift", [SHIFT])
    w0 = din("w0", [RW])
    w_decay_up = din("w_decay_up", [32, RW])
    a0 = din("a0", [RW])
    w_aaa_up = din("w_aaa_up", [32, RW])
    w_gate_up = din("w_gate_up", [64, RW])
    k_k = din("k_k", [RW])
    k_a = din("k_a", [RW])
    r_k = din("r_k", [RW])
    lnx_g = din("lnx_g", [RW])
    lnx_b = din("lnx_b", [RW])
    w_rwkv_out = din("w_rwkv_out", [RW, D])
    s5_lam_re = din("s5_lam_re", [32, 64])
    s5_lam_im = din("s5_lam_im", [32, 64])
    s5_log_dt = din("s5_log_dt", [32])
    s5_b_re = din("s5_b_re", [32, 64, 16])
    s5_b_im = din("s5_b_im", [32, 64, 16])
    s5_c_re = din("s5_c_re", [32, 16, 64])
    s5_c_im = din("s5_c_im", [32, 16, 64])
    s5_d = din("s5_d", [RW])
    glu_w1 = din("glu_w1", [RW, D])
    glu_b1 = din("glu_b1", [D])
    glu_w2 = din("glu_w2", [RW, D])
    glu_b2 = din("glu_b2", [D])
    w_merge_out = din("w_merge_out", [D, D])
    w_ffn_gate = din("w_ffn_gate", [D, DFF])
    w_ffn_up = din("w_ffn_up", [D, DFF])
    w_ffn_down = din("w_ffn_down", [DFF, D])

    y_p = dout("y_p", [T, D])
    y_s = dout("y_s", [NS, D])
    o_shift_p = dout("o_shift_p", [1, SHIFT])
    o_wkv_p = dout("o_wkv_p", [8, 64, 64])
    o_re_p = dout("o_re_p", [1, 2048])
    o_im_p = dout("o_im_p", [1, 2048])
    o_shift_s = dout("o_shift_s", [NS, SHIFT])
    o_wkv_s = dout("o_wkv_s", [NS * 8, 64 * 64])
    o_re_s = dout("o_re_s", [NS, 2048])
    o_im_s = dout("o_im_s", [NS, 2048])

    x1_d = K.dram("x1_scratch", [TT, D], F32)

    psf = [K.ps("psf", [128, 512], F32) for _ in range(6)]
    psb = [K.ps("psb", [128, 1024], BF16) for _ in range(2)]

    def PSF():
        K.psf_i += 1
        return psf[K.psf_i % len(psf)]

    def PSB():
        K.psb_i += 1
        return psb[K.psb_i % len(psb)]

    ones_f = K.sb("ones_f", [128, 128], F32)
    identf = K.sb("identf", [128, 128], F32)
    identb = K.sb("identb", [128, 128], BF16)
    epsT = K.sb("epsT", [128, 1], F32)
    K.memset("pool", ones_f.v(), 1.0)
    K.memset("pool", epsT.v(), 1e-6)
    K.aselect(identf.v(), ones_f.v(), [[-1, 128]], ALU.is_equal, 0.0, 0, 1)
    K.copy("pool", identb.v(), identf.v())

    pvec = K.sb("pvec", [128, 128], F32)
    PV = {}
    _pv = [0]

    def load_fm(name, src, ntile):
        c0 = _pv[0]
        _pv[0] += ntile
        K.dma("sp", pvec[:, c0:c0 + ntile], dap(src, 0, [[1, 128], [128, ntile]]), sem_buf=pvec, slow=True)
        PV[name] = c0
        return c0

    load_fm("g_pre_mix", g_pre_mix, 8)
    load_fm("g_pre_ffn", g_pre_ffn, 8)
    load_fm("mu", mu_shift, 13)
    load_fm("w0", w0, 4)
    load_fm("a0", a0, 4)
    load_fm("k_k", k_k, 4)
    load_fm("k_a", k_a, 4)
    load_fm("r_k", r_k, 4)
    load_fm("s5_d", s5_d, 4)
    load_fm("glu_b1", glu_b1, 8)
    load_fm("glu_b2", glu_b2, 8)

    def early():
        while getattr(K, "_saved", []):
            K.pop_scope()
        K.finish()
        es.close()
        return nc

    if stage == 0.1:
        return early()

    def pcol(name, i=0):
        c = PV[name] + i
        return pvec[:, c:c + 1]

    hT = K.sb("hT", [128, 8, TT], BF16)
    def norm_transpose(src_tile_fn, gname, dst):
        K.push_scope()
        xring = [K.sb("xring", [128, D], F32) for _ in range(2)]
        xn = [K.sb("xn", [128, D], BF16) for _ in range(2)]
        junk = K.sb("junk", [128, D], BF16)
        stat = [K.sb("stat", [128, 4], F32) for _ in range(2)]
        for i in range(17):
            rows = 128 if i < 16 else NS
            col0 = i * 128
            xt = xring[i % 2]
            st = stat[i % 2]
            xb = xn[i % 2]
            K.dma("sp", xt[:rows, :], src_tile_fn(i))
            K.act(junk[:rows, :], xt[:rows, :], AF.Square, accum=st[:rows, 0:1])
            K.act(st[:rows, 1:2], st[:rows, 0:1], AF.Sqrt, bias=epsT[:rows, :], scale=1.0 / D)
            K.recip(st[:rows, 2:3], st[:rows, 1:2])
            K.act(xb[:rows, :], xt[:rows, :], AF.Copy, scale=st[:rows, 2:3])
            pb = PSB()
            pv = pb.v().re("p (k t) -> p k t", k=8)
            for kt in range(8):
                K.tr(pv[:, kt, :rows], xb[:rows, kt * 128:(kt + 1) * 128], identb[:rows, :rows])
            g0 = PV[gname]
            K.tt("dve", dst[:, :, col0:col0 + rows], pv[:, :, :rows],
                 pvec[:, g0:g0 + 8].un(2).bc([128, 8, rows]), ALU.mult)
        K.pop_scope()

    def x_tile(i):
        return x_p[i * 128:(i + 1) * 128, :] if i < 16 else x_s[:, :]

    norm_transpose(x_tile, "g_pre_mix", hT)
    if stage == 0.2:
        return early()

    wring = [K.sb("wring", [128, 8, 128], BF16) for _ in range(3)]
    wr_i = [0]

    def load_w_cols(src, c0, ncols_total, kt_n=8, width=128):
        wr_i[0] += 1
        wb = wring[wr_i[0] % len(wring)]
        K.dma("pool", wb[:, :kt_n, :width],
              dap(src, c0, [[ncols_total, 128], [128 * ncols_total, kt_n], [1, width]]))
        return wb

    def proj_fm(wb, kt_n, src_act, tb, ps):
        col0, n = TBLK[tb]
        for kt in range(kt_n):
            K.mm(ps[:, :n], wb[:, kt, :], src_act[:, kt, col0:col0 + n], start=(kt == 0), stop=(kt == kt_n - 1))

    hgl_d = K.dram("hgl_scratch", [128, 4 * TT], BF16)
    TWO_PI = 2.0 * math.pi

    K.push_scope()
    hgl = K.sb("hgl", [128, 4, TT], BF16)
    u32 = K.sb("u32", [128, 4, TT], F32)
    ubf = K.sb("ubf", [128, 4, TT], BF16)
    for ft in range(4):
        wb = load_w_cols(w_in, SHIFT + ft * 128, PROJ)
        for tb in range(5):
            col0, n = TBLK[tb]
            ps = PSF()
            proj_fm(wb, 8, hT, tb, ps)
            K.copy("act", u32[:, ft, col0:col0 + n], ps[:, :n])
            K.copy("dve", ubf[:, ft, col0:col0 + n], ps[:, :n])

    if stage == 1.1:
        return early()
    sp_ = K.sb("s5par", [128, 16, 16], F32)
    P_LRE, P_LIM, P_DT, P_MAG, P_TH, P_FRE, P_FIM, P_LBRE, P_LBIM, P_T0, P_T1, P_T2, P_T3 = range(13)

    def sp(k):
        return sp_[:, :, k]

    K.dma("sp", sp(P_LRE), dap(s5_lam_re, 0, [[1, 128], [128, 16]]), slow=True)
    K.dma("sp", sp(P_LIM), dap(s5_lam_im, 0, [[1, 128], [128, 16]]), slow=True)
    for gl in range(2):
        K.dma("sp", sp_[gl * 64:(gl + 1) * 64, :, P_DT], dap(s5_log_dt, gl, [[0, 64], [2, 16]]), slow=True)
    K.act(sp(P_DT), sp(P_DT), AF.Exp)
    K.tt("dve", sp(P_T0), sp(P_LRE), sp(P_DT), ALU.mult)
    K.act(sp(P_MAG), sp(P_T0), AF.Exp)
    K.tt("dve", sp(P_TH), sp(P_LIM), sp(P_DT), ALU.mult)

    if stage == 1.15:
        return early()
    LC = 128
    cosT = K.sb("cosT", [128, 16, LC], F32)
    sinT = K.sb("sinT", [128, 16, LC], F32)
    preR = K.sb("preR", [128, 16, LC], F32)
    preI = K.sb("preI", [128, 16, LC], F32)
    K.push_scope()
    ji = K.sb("ji", [128, LC], I32)
    jf = K.sb("jf", [128, LC], F32)
    ang = K.sb("ang", [128, 16, LC], F32)
    rr = K.sb("rr", [128, 16, LC], F32)
    kf = K.sb("kf", [128, 16, LC], F32)
    ki = K.sb("ki", [128, 16, LC], I32)
    K.iota(ji.v(), [[1, LC]], 1, 0)
    K.copy("dve", jf.v(), ji.v())
    K.tt("dve", ang.v(), sp(P_TH).un(2).bc([128, 16, LC]), jf.v().un(1).bc([128, 16, LC]), ALU.mult)

    def sin_reduced(out, a_in, shift):
        C1 = 6.28125
        C2 = TWO_PI - C1
        K.ts("dve", rr.v(), a_in, shift, 1.0 / TWO_PI, ALU.add, ALU.mult)
        K.copy("dve", ki.v(), rr.v())
        K.copy("dve", kf.v(), ki.v())
        K.ts("dve", rr.v(), a_in, shift, None, ALU.add)
        K.stt(rr.v(), kf.v(), -C1, rr.v(), ALU.mult, ALU.add)
        K.stt(rr.v(), kf.v(), -C2, rr.v(), ALU.mult, ALU.add)
        K.ts("dve", kf.v(), rr.v(), math.pi, None, ALU.is_gt)
        K.stt(rr.v(), kf.v(), -TWO_PI, rr.v(), ALU.mult, ALU.add)
        K.ts("dve", kf.v(), rr.v(), -math.pi, None, ALU.is_lt)
        K.stt(rr.v(), kf.v(), TWO_PI, rr.v(), ALU.mult, ALU.add)
        K.ts("dve", rr.v(), rr.v(), -math.pi, math.pi, ALU.max, ALU.min)
        K.act(out, rr.v(), AF.Sin)

    sin_reduced(sinT.v(), ang.v(), 0.0)
    sin_reduced(cosT.v(), ang.v(), 0.5 * math.pi)
    K.tt("dve", sp(P_LBRE), sp(P_MAG), cosT[:, :, 0], ALU.mult)
    K.tt("dve", sp(P_LBIM), sp(P_MAG), sinT[:, :, 0], ALU.mult)
    K.tt("dve", sp(P_T0), sp(P_LRE), sp(P_LRE), ALU.mult)
    K.tt("dve", sp(P_T1), sp(P_LIM), sp(P_LIM), ALU.mult)
    K.tt("dve", sp(P_T0), sp(P_T0), sp(P_T1), ALU.add)
    K.recip(sp(P_T0), sp(P_T0))
    K.ts("dve", sp(P_T1), sp(P_LBRE), -1.0, None, ALU.add)
    K.tt("dve", sp(P_T2), sp(P_T1), sp(P_LRE), ALU.mult)
    K.tt("dve", sp(P_T3), sp(P_LBIM), sp(P_LIM), ALU.mult)
    K.tt("dve", sp(P_T2), sp(P_T2), sp(P_T3), ALU.add)
    K.tt("dve", sp(P_FRE), sp(P_T2), sp(P_T0), ALU.mult)
    K.tt("dve", sp(P_T2), sp(P_LBIM), sp(P_LRE), ALU.mult)
    K.tt("dve", sp(P_T3), sp(P_T1), sp(P_LIM), ALU.mult)
    K.tt("dve", sp(P_T2), sp(P_T2), sp(P_T3), ALU.subtract)
    K.tt("dve", sp(P_FIM), sp(P_T2), sp(P_T0), ALU.mult)
    fre_b = sp(P_FRE).un(2).bc([128, 16, LC])
    fim_b = sp(P_FIM).un(2).bc([128, 16, LC])
    K.tt("dve", rr.v(), cosT.v(), fre_b, ALU.mult)
    K.tt("dve", kf.v(), sinT.v(), fim_b, ALU.mult)
    K.tt("dve", preR.v(), rr.v(), kf.v(), ALU.add)
    K.tt("dve", rr.v(), cosT.v(), fim_b, ALU.mult)
    K.tt("dve", kf.v(), sinT.v(), fre_b, ALU.mult)
    K.tt("dve", preI.v(), rr.v(), kf.v(), ALU.subtract)
    K.pop_scope()

    if stage == 1.2:
        return early()
    BT = K.sb("BT", [128, 32, 128], BF16)
    CTb = K.sb("CTb", [128, 32, 128], BF16)
    K.push_scope()
    for part, src in enumerate((s5_b_re, s5_b_im)):
        X = K.sb("Xb", [128, 16 * 128], F32)
        K.memset("pool", X.v(), 0.0)
        for gl in range(2):
            for a in range(4):
                K.dma("sp", dap(X, gl * 64 * 2048 + 16 * gl + a * 512, [[2048, 64], [160, 4], [1, 16]]),
                      dap(src, gl * 1024 + a * 8192, [[16, 64], [2048, 4], [1, 16]]), slow=True)
        for i4 in range(4):
            ps = PSF()
            for j in range(4):
                i = i4 * 4 + j
                K.tr(ps[:, j * 128:(j + 1) * 128], X[:, i * 128:(i + 1) * 128], identf.v())
            for j in range(4):
                i = i4 * 4 + j
                K.copy("act" if j % 2 else "dve", BT[:, 2 * i + part, :], ps[:, j * 128:(j + 1) * 128])
    for part, src in enumerate((s5_c_re, s5_c_im)):
        Z = K.sb("Zc", [128, 4 * 512], F32)
        K.memset("pool", Z.v(), 0.0)
        for g8 in range(8):
            K.dma("sp", dap(Z, 16 * g8 * 2048 + 64 * g8, [[2048, 16], [512, 4], [1, 64]]),
                  dap(src, g8 * 1024, [[64, 16], [8192, 4], [1, 64]]))
        for q in range(4):
            ps = PSF()
            for j in range(4):
                K.tr(ps[:, j * 128:(j + 1) * 128], Z[:, q * 512 + j * 128:q * 512 + (j + 1) * 128], identf.v())
            i0 = q * 4
            dst = CTb.v().re("p (i two) c -> p i two c", two=2)[:, i0:i0 + 4, part, :]
            if part == 0:
                K.copy("dve", dst, ps.v().re("p (j c) -> p j c", j=4))
            else:
                K.ts("dve", dst, ps.v().re("p (j c) -> p j c", j=4), -1.0, None, ALU.mult)
    K.pop_scope()

    if stage == 1.3:
        return early()
    x1b = [K.sb("x1sb", [128, 16, NS], BF16) for _ in range(2)]
    K.push_scope()
    s0_tm = [K.sb("s0tm", [NS, 2048], F32) for _ in range(2)]
    s0 = [K.sb("s0", [128, 16, NS], F32) for _ in range(2)]
    K.dma("sp", s0_tm[0].v(), st_re.v())
    K.dma("sp", s0_tm[1].v(), st_im.v())
    for part in range(2):
        ps = PSF()
        for i in range(16):
            K.tr(ps[:, i * NS:(i + 1) * NS], s0_tm[part][:, i * 128:(i + 1) * 128], identf[:NS, :NS])
        K.copy("dve", s0[part].v(), ps[:, :16 * NS].re("p (i n) -> p i n", i=16))
    psr = PSF()
    psi = PSF()
    for i in range(16):
        K.mm(psr[:, i * NS:(i + 1) * NS], BT[:, 2 * i, :], ubf[:, i // 4, T:TT])
        K.mm(psi[:, i * NS:(i + 1) * NS], BT[:, 2 * i + 1, :], ubf[:, i // 4, T:TT])
    sw = [K.sb("sw", [128, 16, NS], F32) for _ in range(4)]
    x1 = [K.sb("x1s", [128, 16, NS], F32) for _ in range(2)]
    bc16 = lambda k: sp(k).un(2).bc([128, 16, NS])
    pr3 = psr[:, :16 * NS].re("p (i n) -> p i n", i=16)
    pi3 = psi[:, :16 * NS].re("p (i n) -> p i n", i=16)
    K.tt("dve", sw[0].v(), pr3, bc16(P_FRE), ALU.mult)
    K.tt("dve", sw[1].v(), pi3, bc16(P_FIM), ALU.mult)
    K.tt("dve", sw[2].v(), sw[0].v(), sw[1].v(), ALU.subtract)
    K.tt("dve", sw[0].v(), pi3, bc16(P_FRE), ALU.mult)
    K.tt("dve", sw[1].v(), pr3, bc16(P_FIM), ALU.mult)
    K.tt("dve", sw[3].v(), sw[0].v(), sw[1].v(), ALU.add)
    K.tt("dve", sw[0].v(), s0[0].v(), bc16(P_LBRE), ALU.mult)
    K.tt("dve", sw[1].v(), s0[1].v(), bc16(P_LBIM), ALU.mult)
    K.tt("dve", sw[0].v(), sw[0].v(), sw[1].v(), ALU.subtract)
    K.tt("dve", x1[0].v(), sw[0].v(), sw[2].v(), ALU.add)
    K.tt("dve", sw[0].v(), s0[1].v(), bc16(P_LBRE), ALU.mult)
    K.tt("dve", sw[1].v(), s0[0].v(), bc16(P_LBIM), ALU.mult)
    K.tt("dve", sw[0].v(), sw[0].v(), sw[1].v(), ALU.add)
    K.tt("dve", x1[1].v(), sw[0].v(), sw[3].v(), ALU.add)
    for part in range(2):
        K.copy("act", x1b[part].v(), x1[part].v())
        xs_tm = s0_tm[part]
        for i4 in range(4):
            ps = PSF()
            for j in range(4):
                i = i4 * 4 + j
                K.tr(ps[:NS, j * 128:(j + 1) * 128], x1[part][:, i, :], identf.v())
            K.copy("dve", xs_tm[:, i4 * 512:(i4 + 1) * 512], ps[:NS, :])
        K.dma("sp", (o_re_s if part == 0 else o_im_s).v(), xs_tm.v())
    K.pop_scope()

    if stage == 1.4:
        return early()
    carry = K.sb("carry", [128, 16, 2], F32)
    K.memset("dve", carry.v(), 0.0)
    NR = 2
    bpr = [K.sb("bpr", [128, 512], F32) for _ in range(NR)]
    bpi = [K.sb("bpi", [128, 512], F32) for _ in range(NR)]
    t1 = [K.sb("t1", [128, 512], F32) for _ in range(NR)]
    t2 = [K.sb("t2", [128, 512], F32) for _ in range(NR)]
    wre = [K.sb("wre", [128, 512], F32) for _ in range(NR)]
    wim = [K.sb("wim", [128, 512], F32) for _ in range(NR)]
    xre = [K.sb("xre", [128, 512], F32) for _ in range(NR)]
    xim = [K.sb("xim", [128, 512], F32) for _ in range(NR)]
    xbr = [K.sb("xbr", [128, 512], BF16) for _ in range(4)]
    xbi = [K.sb("xbi", [128, 512], BF16) for _ in range(4)]
    yv = [K.sb("yv", [128, 512], F32) for _ in range(2)]
    y2 = [K.sb("y2", [128, 512], F32) for _ in range(2)]
    it = 0

    def gelu_to(dst_bf, yv_, y2_, n):
        K.tt("pool", y2_[:, :n], yv_[:, :n], yv_[:, :n], ALU.mult)
        K.ts("pool", y2_[:, :n], y2_[:, :n], 0.044715, 1.0, ALU.mult, ALU.add)
        K.tt("pool", y2_[:, :n], y2_[:, :n], yv_[:, :n], ALU.mult)
        K.act(y2_[:, :n], y2_[:, :n], AF.Sigmoid, scale=1.5957691216057308)
        K.tt("pool", dst_bf, y2_[:, :n], yv_[:, :n], ALU.mult)

    for q in range(4):
        for tb in range(4):
            col0, n = TBLK[tb]
            for j in range(4):
                i = q * 4 + j
                r = it % NR
                it += 1
                psr = PSF()
                psi = PSF()
                K.mm(psr.v(), BT[:, 2 * i, :], ubf[:, q, col0:col0 + n])
                K.mm(psi.v(), BT[:, 2 * i + 1, :], ubf[:, q, col0:col0 + n])
                tR = preR[:, i, :].un(1).bc([128, 4, LC])
                tI = preI[:, i, :].un(1).bc([128, 4, LC])
                tC = cosT[:, i, :].un(1).bc([128, 4, LC])
                tS = sinT[:, i, :].un(1).bc([128, 4, LC])
                v4 = lambda b: b.v().re("p (a l) -> p a l", a=4)
                K.tt("dve", v4(t1[r]), v4(psr), tR, ALU.mult)
                K.tt("dve", v4(t2[r]), v4(psi), tI, ALU.mult)
                K.tt("pool", bpr[r].v(), t1[r].v(), t2[r].v(), ALU.subtract)
                K.tt("dve", v4(t1[r]), v4(psi), tR, ALU.mult)
                K.tt("dve", v4(t2[r]), v4(psr), tI, ALU.mult)
                K.tt("pool", bpi[r].v(), t1[r].v(), t2[r].v(), ALU.add)
                rho = sp_[:, i, P_MAG:P_MAG + 1].bc([128, LC])
                for a in range(4):
                    cs = slice(a * LC, (a + 1) * LC)
                    if a == 0:
                        ire, iim = carry[:, i, 0:1], carry[:, i, 1:2]
                    else:
                        ire, iim = xre[r][:, a * LC - 1:a * LC], xim[r][:, a * LC - 1:a * LC]
                    K.scan(wre[r][:, cs], rho, bpr[r][:, cs], ire)
                    K.scan(wim[r][:, cs], rho, bpi[r][:, cs], iim)
                    K.tt("pool", t1[r][:, cs], wre[r][:, cs], cosT[:, i, :], ALU.mult)
                    K.tt("pool", t2[r][:, cs], wim[r][:, cs], sinT[:, i, :], ALU.mult)
                    K.tt("dve", xre[r][:, cs], t1[r][:, cs], t2[r][:, cs], ALU.subtract)
                    K.tt("pool", t1[r][:, cs], wre[r][:, cs], sinT[:, i, :], ALU.mult)
                    K.tt("pool", t2[r][:, cs], wim[r][:, cs], cosT[:, i, :], ALU.mult)
                    K.tt("dve", xim[r][:, cs], t1[r][:, cs], t2[r][:, cs], ALU.add)
                K.copy("dve", carry[:, i, 0:1], xre[r][:, 511:512])
                K.copy("dve", carry[:, i, 1:2], xim[r][:, 511:512])
                K.copy("act", xbr[j].v(), xre[r].v())
                K.copy("act", xbi[j].v(), xim[r].v())
            psy = PSF()
            for j in range(4):
                i = q * 4 + j
                K.mm(psy.v(), CTb[:, 2 * i, :], xbr[j].v(), start=(j == 0), stop=False)
                K.mm(psy.v(), CTb[:, 2 * i + 1, :], xbi[j].v(), start=False, stop=(j == 3))
            yy = yv[tb % 2]
            K.stt(yy.v(), u32[:, q, col0:col0 + n], pcol("s5_d", q), psy.v(), ALU.mult, ALU.add)
            gelu_to(hgl[:, q, col0:col0 + n], yy, y2[tb % 2], n)
        psy = PSF()
        for j in range(4):
            i = q * 4 + j
            K.mm(psy[:, :NS], CTb[:, 2 * i, :], x1b[0][:, i, :], start=(j == 0), stop=False)
            K.mm(psy[:, :NS], CTb[:, 2 * i + 1, :], x1b[1][:, i, :], start=False, stop=(j == 3))
        K.stt(yv[0][:, :NS], u32[:, q, T:TT], pcol("s5_d", q), psy[:, :NS], ALU.mult, ALU.add)
        gelu_to(hgl[:, q, T:TT], yv[0], y2[0], NS)
    if stage == 1.5:
        return early()
    K.dma("sp", dap(o_re_p, 0, [[1, 128], [128, 16]]), carry[:, :, 0], slow=True)
    K.dma("sp", dap(o_im_p, 0, [[1, 128], [128, 16]]), carry[:, :, 1], slow=True)
    K.dma("sp", hgl_d.v(), hgl.v().re("p k t -> p (k t)"))
    K.pop_scope()
    if stage == 2:
        zt = K.sb("zt", [128, 4096], F32)
        K.memset("pool", zt.v(), 0.0)
        K.dma("sp", o_shift_p.v(), zt[0:1, 0:SHIFT])
        K.dma("sp", o_shift_s.v(), zt[0:NS, 0:SHIFT])
        K.dma("sp", o_wkv_p.v().re("h v k -> h (v k)"), zt[0:8, :])
        K.dma("sp", o_wkv_s.v(), zt[:, :])
        for i in range(16):
            K.dma("sp", y_p[i * 128:(i + 1) * 128, :], zt[:, 0:D])
        K.dma("sp", y_s.v(), zt[0:NS, 0:D])
        return early()
    K.push_scope()
    lora_bf = K.sb("lora_bf", [128, TT], BF16)
    lora_up = K.sb("lora_up", [128, RW], BF16)
    o_d = K.dram("o_scratch", [T, RW], F32)
    bonus_d = K.dram("bonus_scratch", [T, RW], F32)
    samp_d = K.dram("samp_scratch", [NS * 8, 7, 64], F32)
    sampo_d = K.dram("sampo_scratch", [NS, RW], F32)

    K.push_scope()
    mask4 = K.sb("mask4", [128, 4, 128], F32)
    maskNT = K.sb("maskNT", [128, 128], F32)
    resetm = K.sb("resetm", [128, T], BF16)
    blk1 = K.sb("blk1", [128, 128], F32)
    eps_ln = K.sb("eps_ln", [128, 1], F32)
    K.memset("pool", eps_ln.v(), 64e-5)
    for j in range(4):
        K.aselect(mask4[:, j, :], ones_f.v(), [[1, 128]], ALU.is_gt if j % 2 == 0 else ALU.is_ge, 0.0, 0, -1)
    K.aselect(maskNT.v(), ones_f.v(), [[-1, 128]], ALU.is_gt, 0.0, 0, 1)
    K.memset("pool", resetm.v(), 1.0)
    K.memset("pool", resetm.v().re("p (c l) -> p c l", l=128)[:, :, 0:1], 0.0)
    K.memset("pool", blk1.v(), 0.0)
    K.memset("pool", blk1[0:64, 0:64], 1.0)
    K.memset("pool", blk1[64:128, 64:128], 1.0)

    K.dma("pool", lora_up[0:32, :], w_decay_up.v())
    K.dma("pool", lora_up[32:64, :], w_aaa_up.v())
    K.dma("pool", lora_up[64:128, :], w_gate_up.v())
    sh0T = K.sb("sh0T", [128, 13, NS], F32)
    shout = [K.sb("shout", [NS + 1, 128], F32) for _ in range(2)]
    K.push_scope()
    sh_tm = K.sb("sh_tm", [NS, SHIFT], F32)
    K.dma("sp", sh_tm.v(), st_shift.v())
    ps = PSF()
    for ft in range(13):
        K.tr(ps[:, ft * NS:(ft + 1) * NS], sh_tm[:, ft * 128:(ft + 1) * 128], identf[:NS, :NS])
    K.copy("dve", sh0T.v(), ps[:, :13 * NS].re("p (f n) -> p f n", f=13))
    K.pop_scope()

    omm = K.sb("omm", [128, 13], F32)
    K.ts("dve", omm.v(), pvec[:, PV["mu"]:PV["mu"] + 13], -1.0, 1.0, ALU.mult, ALU.add)
    omka = K.sb("omka", [128, 4], F32)
    K.ts("dve", omka.v(), pvec[:, PV["k_a"]:PV["k_a"] + 4], -1.0, 1.0, ALU.mult, ALU.add)

    SC = [K.sb("scr", [128, TT], F32) for _ in range(6)]

    def proj_shift_tile(ft, pr, tmp, xr_out, xr_out_dt_bf=None, samp32=None):
        wb = load_w_cols(w_in, ft * 128, PROJ)
        for tb in range(5):
            col0, n = TBLK[tb]
            ps = PSF()
            proj_fm(wb, 8, hT, tb, ps)
            K.copy("act", pr[:, col0:col0 + n], ps[:, :n])
            K.act(tmp[:, col0:col0 + n], ps[:, :n], AF.Copy, scale=omm[:, ft:ft + 1])
        ps = PSF()
        K.tr(ps[:NS + 1, :128], pr[:, T - 1:TT], identf.v())
        so = shout[ft % 2]
        K.copy("dve", so.v(), ps[:NS + 1, :128])
        K.dma("sp", o_shift_p[0:1, ft * 128:(ft + 1) * 128], so[0:1, :])
        K.dma("sp", o_shift_s[:, ft * 128:(ft + 1) * 128], so[1:NS + 1, :])
        mu_c = pcol("mu", ft)
        K.stt(xr_out[:, 1:T], pr[:, 0:T - 1], mu_c, tmp[:, 1:T], ALU.mult, ALU.add)
        K.copy("pool", xr_out[:, 0:1], tmp[:, 0:1])
        K.stt(xr_out[:, T:TT], sh0T[:, ft, :], mu_c, tmp[:, T:TT], ALU.mult, ALU.add)
        if samp32 is not None:
            K.stt(samp32, sh0T[:, ft, :], mu_c, tmp[:, T:TT], ALU.mult, ALU.add)

    proj_shift_tile(12, SC[0], SC[1], SC[2].v())
    K.act(lora_bf[0:32, :], SC[2][0:32, :], AF.Tanh)
    K.act(lora_bf[32:64, :], SC[2][32:64, :], AF.Copy)
    K.act(lora_bf[64:128, :], SC[2][64:128, :], AF.Sigmoid)

    vbf = K.sb("vbf", [128, TT], BF16)
    vs32 = K.sb("vs32", [128, NS], F32)
    AR = K.sb("AR", [128, 16, 2, 128], BF16)
    bti = K.sb("bti", [128, T], BF16)
    kti = K.sb("kti", [128, T], BF16)
    bhat = K.sb("bhat", [128, T], BF16)
    khat = K.sb("khat", [128, T], BF16)
    Wfm = K.sb("Wfm", [128, T], BF16)
    TMq = K.sb("TMq", [128, 16, 4, 128], BF16)
    gL = K.sb("gL", [128, 16], F32)
    Sf = K.sb("Sf", [128, 64], F32)
    Sb = K.sb("Sb", [128, 64], BF16)
    smp = K.sb("smp", [NS, 6, 128], F32)
    decs = K.sb("decs", [128, NS], F32)
    avs = K.sb("avs", [128, NS], F32)
    NPR = 8
    ABK = [K.sb("ABK", [128, 2, 128], BF16) for _ in range(NPR)]
    ATMP = [K.sb("ATMP", [128, 2, 128], BF16) for _ in range(NPR)]
    P0T = [K.sb("P0T", [128, 128], BF16) for _ in range(NPR)]
    PMb = [[K.sb("PMb", [128, 3, 128], BF16) for _ in range(2)] for _ in range(NPR)]
    Ybf = [K.sb("Ybf", [128, 64], BF16) for _ in range(NPR)]
    Ubar = [K.sb("Ubar", [128, 64], F32) for _ in range(NPR)]
    Ut = [K.sb("Ut", [128, 64], BF16) for _ in range(4)]
    Otile = [K.sb("Otile", [128, 128], F32) for _ in range(2)]
    btile = [K.sb("btile", [128, 512], F32) for _ in range(2)]
    wkvo = K.sb("wkvo", [64, 128], F32)

    for hp in range(4):
        xr_r, xr_k = SC[0], SC[1]
        proj_shift_tile(0 * 4 + hp, SC[2], SC[3], xr_r.v())
        proj_shift_tile(1 * 4 + hp, SC[2], SC[3], xr_k.v())
        proj_shift_tile(2 * 4 + hp, SC[2], SC[3], vbf.v(), samp32=vs32.v())
        sigd, csum, afm, kkn = SC[2], SC[3], SC[4], SC[5]
        hc = slice(hp * 128, (hp + 1) * 128)
        for tb in range(5):
            col0, n = TBLK[tb]
            ps = PSF()
            K.mm(ps[:, :n], lora_up[0:32, hc], lora_bf[0:32, col0:col0 + n])
            K.act(sigd[:, col0:col0 + n], ps[:, :n], AF.Sigmoid, bias=pcol("w0", hp))
            ps = PSF()
            K.mm(ps[:, :n], lora_up[32:64, hc], lora_bf[32:64, col0:col0 + n])
            K.act(afm[:, col0:col0 + n], ps[:, :n], AF.Sigmoid, bias=pcol("a0", hp))
        K.act(kkn.v(), xr_k.v(), AF.Square, scale=pcol("k_k", hp))
        for tb in range(5):
            col0, n = TBLK[tb]
            ps = PSF()
            K.mm(ps[:, :n], blk1.v(), kkn[:, col0:col0 + n])
            K.ts("dve", csum[:, col0:col0 + n], ps[:, :n], 1e-24, None, ALU.max)
        K.act(csum.v(), csum.v(), AF.Ln)
        K.act(csum.v(), csum.v(), AF.Exp, scale=-0.5)
        K.stt(kkn.v(), xr_k.v(), pcol("k_k", hp), csum.v(), ALU.mult, ALU.mult)
        K.ts("pool", csum.v(), afm.v(), pcol("k_a", hp), omka[:, hp:hp + 1], ALU.mult, ALU.add)
        K.tt("pool", xr_k.v(), xr_k.v(), csum.v(), ALU.mult)
        kmod = xr_k
        K.tt("pool", afm.v(), kkn.v(), afm.v(), ALU.mult)
        bfm = afm
        K.stt(csum.v(), xr_r.v(), pcol("r_k", hp), kmod.v(), ALU.mult, ALU.mult)
        for tb in range(4):
            col0, n = TBLK[tb]
            ps = PSF()
            K.mm(ps[:, :n], blk1.v(), csum[:, col0:col0 + n])
            K.tt("dve", csum[:, col0:col0 + n], ps[:, :n], vbf[:, col0:col0 + n], ALU.mult)
        for c4 in range(4):
            ps = PSF()
            for j in range(4):
                c = c4 * 4 + j
                K.tr(ps[:, j * 128:(j + 1) * 128], csum[:, c * 128:(c + 1) * 128], identf.v())
            bt = btile[c4 % 2]
            K.copy("act", bt.v(), ps.v())
            K.dma("sp", dap(bonus_d, c4 * 512 * RW + hp * 128, [[RW, 128], [128 * RW, 4], [1, 128]]),
                  bt.v().re("p (j f) -> p j f", j=4))
        K.act(decs.v(), sigd[:, T:TT], AF.Exp, scale=-CDEC)
        K.ts("dve", avs.v(), kkn[:, T:TT], -1.0, None, ALU.mult)
        K.scan(csum[:, 0:T], resetm.v(), sigd[:, 0:T], 0.0)
        gt = sigd
        c3 = lambda b_: b_[:, 0:T].re("p (c l) -> p c l", l=128)
        K.act(gt[:, 0:T], csum[:, 0:T], AF.Exp, scale=-CDEC)
        K.tt("dve", AR[:, :, 1, :], c3(xr_r), c3(gt), ALU.mult)
        K.copy("dve", gL.v(), c3(gt)[:, :, 127])
        K.stt(AR[:, :, 0, 1:128], c3(kkn)[:, :, 1:128], -1.0, c3(gt)[:, :, 0:127], ALU.mult, ALU.mult)
        K.ts("dve", AR[:, :, 0, 0], c3(kkn)[:, :, 0], -1.0, None, ALU.mult)
        K.act(gt[:, 0:T], csum[:, 0:T], AF.Exp, scale=CDEC)
        K.tt("dve", bti.v(), bfm[:, 0:T], gt[:, 0:T], ALU.mult)
        K.tt("pool", kti.v(), kmod[:, 0:T], gt[:, 0:T], ALU.mult)
        K.tt("dve", c3(gt), c3(csum)[:, :, 127:128].bc([128, 16, 128]), c3(csum), ALU.subtract)
        K.act(gt[:, 0:T], gt[:, 0:T], AF.Exp, scale=-CDEC)
        K.tt("dve", bhat.v(), bfm[:, 0:T], gt[:, 0:T], ALU.mult)
        K.tt("pool", khat.v(), kmod[:, 0:T], gt[:, 0:T], ALU.mult)
        for c in range(16):
            pb = PSB()
            pv = pb.v().re("p (q t) -> p q t", q=8)
            cs = slice(c * 128, (c + 1) * 128)
            K.tr(pv[:, 0, :], vbf[:, cs], identb.v())
            K.tr(pv[:, 1, :], bhat[:, cs], identb.v())
            K.tr(pv[:, 2, :], khat[:, cs], identb.v())
            K.tr(pv[:, 3, :], AR[:, c, 0, :], identb.v())
            K.copy("act" if c % 2 else "dve", TMq[:, c, :, :], pv[:, 0:4, :])
        sq_src = [xr_r[:, T:TT], kmod[:, T:TT], vs32.v(), decs.v(), avs.v(), bfm[:, T:TT]]
        for half in range(2):
            ps = PSF()
            for j in range(3):
                K.tr(ps[:NS, j * 128:(j + 1) * 128], sq_src[half * 3 + j], identf.v())
            K.copy("dve", smp[:, half * 3:half * 3 + 3, :], ps[:NS, :384].re("p (q f) -> p q f", q=3))
        for q in range(6):
            K.dma("sp", dap(samp_d, (2 * hp) * 448 + q * 64, [[8 * 448, NS], [448, 2], [1, 64]]),
                  smp[:, q, :].re("p (h n) -> p h n", h=2))

        K.memset("dve", Sf.v(), 0.0)
        K.memset("pool", Sb.v(), 0.0)
        pairs = [(c, hl) for c in range(16) for hl in range(2)]

        def precompute(group):
            for (sl, c, hl) in group:
                rows = slice(hl * 64, (hl + 1) * 64)
                cs = slice(c * 128, (c + 1) * 128)
                ps = PSF()
                arv = AR[rows, c, :, :].re("p a t -> p (a t)")
                K.mm(ps[:, 0:256], bti[rows, cs], arv)
                K.mm(ps[:, 256:512], kti[rows, cs], arv)
                ps2 = PSF()
                K.mm(ps2[:, 0:128], AR[rows, c, 0, :], bti[rows, cs])
                p4 = ps.v().re("p (q t) -> p q t", q=4)
                K.tt("dve", ATMP[sl].v(), p4[:, 0::2, :], mask4[:, 0::2, :], ALU.mult)
                K.tt("dve", ABK[sl].v(), p4[:, 1::2, :], mask4[:, 1::2, :], ALU.mult)
                K.tt("dve", P0T[sl].v(), ps2[:, 0:128], maskNT.v(), ALU.mult)
            for lev in range(7):
                for gi, (sl, c, hl) in enumerate(group):
                    if lev == 0:
                        Pj, PjT, Mp = ATMP[sl][:, 0, :], P0T[sl].v(), identb.v()
                    else:
                        src = PMb[sl][lev % 2]
                        Pj, PjT, Mp = src[:, 0, :], src[:, 1, :], src[:, 2, :]
                    dst = PMb[sl][(lev + 1) % 2]
                    ps = PSF()
                    if lev < 6:
                        K.mm(ps[:, 0:128], PjT, Pj)
                        K.mm(ps[:, 128:256], Pj, PjT)
                    K.mm(ps[:, 256:384], PjT, Mp, start=True, stop=False)
                    K.mm(ps[:, 256:384], identb.v(), Mp, start=False, stop=True)
                    eng = "act" if gi % 2 else "dve"
                    if lev < 6:
                        K.copy(eng, dst.v(), ps[:, 0:384].re("p (q t) -> p q t", q=3))
                    else:
                        K.copy(eng, dst[:, 2, :], ps[:, 256:384])
            for gi, (sl, c, hl) in enumerate(group):
                rows = slice(hl * 64, (hl + 1) * 64)
                cs = slice(c * 128, (c + 1) * 128)
                Mfin = PMb[sl][1][:, 2, :]
                ps = PSF()
                K.mm(ps[:, 0:128], TMq[:, c, 3, :], Mfin)
                K.mm(ps[:, 128:192], ATMP[sl][:, 1, :], TMq[:, c, 0, rows])
                eng = "act" if gi % 2 else "dve"
                K.copy(eng, Wfm[rows, cs], ps[rows, 0:128])
                K.copy(eng, Ybf[sl].v(), ps[:, 128:192])
                ps = PSF()
                K.mm(ps[:, 0:64], Mfin, Ybf[sl].v())
                K.copy(eng, Ubar[sl].v(), ps[:, 0:64])

        def serial(group):
            for gi, (sl, c, hl) in enumerate(group):
                rows = slice(hl * 64, (hl + 1) * 64)
                cs = slice(c * 128, (c + 1) * 128)
                ut = Ut[gi % 4]
                ps = PSF()
                K.mm(ps[:, 0:64], Wfm[rows, cs], Sb[rows, :])
                K.tt("dve", ut.v(), ps[:, 0:64], Ubar[sl].v(), ALU.add)
                pso = PSF()
                K.mm(pso[:, 0:64], AR[rows, c, 1, :], Sb[rows, :], start=True, stop=False)
                K.mm(pso[:, 0:64], ABK[sl][:, 0, :], ut.v(), start=False, stop=False)
                K.mm(pso[:, 0:64], ABK[sl][:, 1, :], TMq[:, c, 0, rows], start=False, stop=True)
                ot = Otile[c % 2]
                K.copy("act", ot[:, rows], pso[:, 0:64])
                if hl == 1:
                    K.dma("sp", o_d[c * 128:(c + 1) * 128, hp * 128:(hp + 1) * 128], ot.v())
                pss = PSF()
                K.mm(pss[:, 0:64], TMq[:, c, 1, :], ut.v(), start=True, stop=False)
                K.mm(pss[:, 0:64], TMq[:, c, 2, :], TMq[:, c, 0, rows], start=False, stop=True)
                K.stt(Sf[rows, :], Sf[rows, :], gL[rows, c:c + 1], pss[rows, 0:64], ALU.mult, ALU.add)
                K.copy("act", Sb[rows, :], Sf[rows, :])

        G = 4
        groups = []
        for g0 in range(0, 32, G):
            groups.append([((g0 + i) % NPR, pairs[g0 + i][0], pairs[g0 + i][1]) for i in range(G)])
        precompute(groups[0])
        for gi in range(len(groups)):
            if gi + 1 < len(groups):
                precompute(groups[gi + 1])
            serial(groups[gi])
        ps = PSF()
        K.tr(ps[:64, 0:128], Sf.v(), identf.v())
        K.copy("dve", wkvo.v(), ps[:64, 0:128])
        K.dma("sp", o_wkv_p[2 * hp:2 * hp + 2, :, :].re("h v k -> v h k"), wkvo.v().re("v (h k) -> v h k", h=2))

    K.pop_scope()
    if stage == 3:
        zt = K.sb("zt", [128, 4096], F32)
        K.memset("pool", zt.v(), 0.0)
        K.dma("sp", o_wkv_s.v(), zt[:, :])
        for i in range(16):
            K.dma("sp", y_p[i * 128:(i + 1) * 128, :], zt[:, 0:D])
        K.dma("sp", y_s.v(), zt[0:NS, 0:D])
        return early()
    oaT = K.sb("oaT", [128, 4, TT], BF16)

    K.push_scope()
    eps_ln = K.sb("eps_ln2", [128, 1], F32)
    K.memset("pool", eps_ln.v(), 64e-5)
    lnxg = K.sb("lnxg", [128, RW], F32)
    lnxb = K.sb("lnxb", [128, RW], F32)
    K.dma("sp", lnxg.v(), dap(lnx_g, 0, [[0, 128], [1, RW]]))
    K.dma("sp", lnxb.v(), dap(lnx_b, 0, [[0, 128], [1, RW]]))
    ot_r = [K.sb("ot", [128, RW], F32) for _ in range(2)]
    bt_r = [K.sb("bt", [128, RW], F32) for _ in range(2)]
    tq = [K.sb("tq", [128, RW], F32) for _ in range(2)]
    on_r = [K.sb("on", [128, RW], F32) for _ in range(2)]
    ofb = [K.sb("ofb", [128, RW], BF16) for _ in range(2)]
    st8 = [K.sb("st8", [128, 6, 8], F32) for _ in range(2)]
    for c in range(16):
        ot, bt, tmp, on, of, s8 = ot_r[c % 2], bt_r[c % 2], tq[c % 2], on_r[c % 2], ofb[c % 2], st8[c % 2]
        K.dma("sp", ot.v(), o_d[c * 128:(c + 1) * 128, :])
        K.dma("sp", bt.v(), bonus_d[c * 128:(c + 1) * 128, :])
        h3 = lambda b_: b_.v().re("p (h n) -> p h n", h=8)
        K.reduce(s8[:, 0, :], h3(ot))
        K.act(tmp.v(), ot.v(), AF.Square)
        K.reduce(s8[:, 1, :], h3(tmp))
        K.ts("dve", s8[:, 2, :], s8[:, 0, :], 1.0 / 64, None, ALU.mult)
        K.tt("dve", s8[:, 3, :], s8[:, 2, :], s8[:, 2, :], ALU.mult)
        K.stt(s8[:, 4, :], s8[:, 1, :], 1.0 / 64, s8[:, 3, :], ALU.mult, ALU.subtract)
        K.act(s8[:, 4, :], s8[:, 4, :], AF.Sqrt, bias=eps_ln.v(), scale=1.0)
        K.recip(s8[:, 5, :], s8[:, 4, :])
        K.tt("pool", h3(on), h3(ot), s8[:, 2, :].un(2).bc([128, 8, 64]), ALU.subtract)
        K.tt("pool", h3(on), h3(on), s8[:, 5, :].un(2).bc([128, 8, 64]), ALU.mult)
        K.tt("pool", on.v(), on.v(), lnxg.v(), ALU.mult)
        K.tt("dve", on.v(), on.v(), lnxb.v(), ALU.add)
        K.tt("dve", on.v(), on.v(), bt.v(), ALU.add)
        ps = PSF()
        K.mm(ps.v(), lora_bf[64:128, c * 128:(c + 1) * 128], lora_up[64:128, :])
        K.tt("dve", of.v(), on.v(), ps.v(), ALU.mult)
        pb = PSB()
        pv = pb.v().re("p (q t) -> p q t", q=8)
        for kt in range(4):
            K.tr(pv[:, kt, :], of[:, kt * 128:(kt + 1) * 128], identb.v())
        K.copy("act", oaT[:, :, c * 128:(c + 1) * 128], pv[:, 0:4, :])
    K.pop_scope()

    K.push_scope()
    gs_tm = K.sb("gs_tm", [NS, RW], F32)
    ps = PSF()
    K.mm(ps[:NS, :], lora_bf[64:128, T:TT], lora_up[64:128, :])
    K.copy("act", gs_tm.v(), ps[:NS, :])
    K.dma("sp", dap(samp_d, 6 * 64, [[8 * 448, NS], [448, 8], [1, 64]]), gs_tm.v().re("p (h n) -> p h n", h=8))
    vec = K.sb("vec", [128, 7, 64], F32)
    K.dma("sp", vec.v().re("p q n -> p (q n)"), dap(samp_d, 0, [[448, 128], [1, 448]]))
    par = K.sb("spar", [128, 3, 64], F32)
    for tk in range(NS):
        for j, src in enumerate((lnx_g, lnx_b, r_k)):
            K.dma("sp", par[tk * 8:(tk + 1) * 8, j, :], dap(src, 0, [[64, 8], [1, 64]]), sem_buf=par)
    S0s = K.sb("S0s", [128, 64, 64], F32)
    S1s = K.sb("S1s", [128, 64, 64], F32)
    tS = K.sb("tS", [128, 64, 64], F32)
    K.dma("sp", S0s.v().re("p v k -> p (v k)"), st_wkv.v())
    r_, km_, v_, dec_, av_, b_, g_ = [vec[:, q, :] for q in range(7)]
    overv = lambda x: x.un(1).bc([128, 64, 64])
    overk = lambda x: x.un(2).bc([128, 64, 64])
    sv = K.sb("sv", [128, 8, 64], F32)
    K.tt("dve", tS.v(), S0s.v(), overv(av_), ALU.mult)
    K.reduce(sv[:, 0, :], tS.v())
    K.tt("pool", S1s.v(), S0s.v(), overv(dec_), ALU.mult)
    K.tt("dve", tS.v(), overk(sv[:, 0, :]), overv(b_), ALU.mult)
    K.tt("pool", S1s.v(), S1s.v(), tS.v(), ALU.add)
    K.tt("dve", tS.v(), overk(v_), overv(km_), ALU.mult)
    K.tt("pool", S1s.v(), S1s.v(), tS.v(), ALU.add)
    K.dma("sp", o_wkv_s.v(), S1s.v().re("p v k -> p (v k)"))
    K.tt("dve", tS.v(), S1s.v(), overv(r_), ALU.mult)
    K.reduce(sv[:, 1, :], tS.v())
    ss = K.sb("ss", [128, 8], F32)
    K.reduce(ss[:, 0:1], sv[:, 1, :])
    K.tt("dve", sv[:, 2, :], sv[:, 1, :], sv[:, 1, :], ALU.mult)
    K.reduce(ss[:, 1:2], sv[:, 2, :])
    K.ts("dve", ss[:, 2:3], ss[:, 0:1], 1.0 / 64, None, ALU.mult)
    K.tt("dve", ss[:, 3:4], ss[:, 2:3], ss[:, 2:3], ALU.mult)
    K.stt(ss[:, 4:5], ss[:, 1:2], 1.0 / 64, ss[:, 3:4], ALU.mult, ALU.subtract)
    eps2 = K.sb("eps_ln3", [128, 1], F32)
    K.memset("pool", eps2.v(), 64e-5)
    K.act(ss[:, 4:5], ss[:, 4:5], AF.Sqrt, bias=eps2.v(), scale=1.0)
    K.recip(ss[:, 5:6], ss[:, 4:5])
    K.ts("dve", sv[:, 3, :], sv[:, 1, :], ss[:, 2:3], ss[:, 5:6], ALU.subtract, ALU.mult)
    K.tt("dve", sv[:, 3, :], sv[:, 3, :], par[:, 0, :], ALU.mult)
    K.tt("dve", sv[:, 3, :], sv[:, 3, :], par[:, 1, :], ALU.add)
    K.tt("dve", sv[:, 4, :], r_, km_, ALU.mult)
    K.tt("dve", sv[:, 4, :], sv[:, 4, :], par[:, 2, :], ALU.mult)
    K.reduce(ss[:, 6:7], sv[:, 4, :])
    K.stt(sv[:, 3, :], v_, ss[:, 6:7], sv[:, 3, :], ALU.mult, ALU.add)
    K.tt("dve", sv[:, 5, :], sv[:, 3, :], g_, ALU.mult)
    K.dma("sp", dap(sampo_d, 0, [[64, 128], [1, 64]]), sv[:, 5, :])
    os_tm = K.sb("os_tm", [NS, RW], F32)
    os_bf = K.sb("os_bf", [NS, RW], BF16)
    K.dma("sp", os_tm.v(), sampo_d.v())
    K.copy("dve", os_bf.v(), os_tm.v())
    pb = PSB()
    pv = pb.v().re("p (q t) -> p q t", q=8)
    for kt in range(4):
        K.tr(pv[:, kt, :NS], os_bf[:, kt * 128:(kt + 1) * 128], identb[:NS, :NS])
    K.copy("act", oaT[:, :, T:TT], pv[:, 0:4, :NS])
    K.pop_scope()
    if stage == 4:
        zt = K.sb("zt", [128, 1024], F32)
        K.memset("pool", zt.v(), 0.0)
        for i in range(16):
            K.dma("sp", y_p[i * 128:(i + 1) * 128, :], zt[:, 0:D])
        K.dma("sp", y_s.v(), zt[0:NS, 0:D])
        return early()

    def bcast_row(name, src, n):
        t = K.sb(name, [128, n], F32)
        K.dma("sp", t.v(), dap(src, 0, [[0, 128], [1, n]]))
        return t

    def post_norm_residual(ps_halves, rows, res_src, gpost, dst, bufs):
        st, xres, tt_ = bufs
        for hf in range(2):
            K.act(tt_[:rows, hf * 512:(hf + 1) * 512], ps_halves[hf][:rows, :], AF.Square, accum=st[:rows, hf:hf + 1])
        K.tt("dve", st[:rows, 2:3], st[:rows, 0:1], st[:rows, 1:2], ALU.add)
        K.act(st[:rows, 3:4], st[:rows, 2:3], AF.Sqrt, bias=epsT[:rows, :], scale=1.0 / D)
        K.recip(st[:rows, 4:5], st[:rows, 3:4])
        K.dma("sp", xres[:rows, :], res_src)
        for hf in range(2):
            hs = slice(hf * 512, (hf + 1) * 512)
            K.stt(tt_[:rows, hs], ps_halves[hf][:rows, :], st[:rows, 4:5], gpost[:rows, hs], ALU.mult, ALU.mult)
            K.tt("pool", xres[:rows, hs], xres[:rows, hs], tt_[:rows, hs], ALU.add)
        K.dma("sp", dst, xres[:rows, :])

    K.push_scope()
    hgl = K.sb("hgl2", [128, 4, TT], BF16)
    K.dma("sp", hgl.v().re("p k t -> p (k t)"), hgl_d.v())
    mT = K.sb("mT", [128, 8, TT], BF16)
    wmo = K.sb("wmo", [128, 8, D], BF16)
    for kt in range(8):
        K.dma("pool", wmo[:, kt, :], w_merge_out[kt * 128:(kt + 1) * 128, :])
    gpost = bcast_row("gpost_mix", g_post_mix, D)
    mring = [K.sb("mring", [128, 8, 128], BF16) for _ in range(6)]
    mr_i = [0]

    def mload(src, c0, ncols_total, kt_n):
        mr_i[0] += 1
        wb = mring[mr_i[0] % len(mring)]
        K.dma("pool", wb[:, :kt_n, :], dap(src, c0, [[ncols_total, 128], [128 * ncols_total, kt_n], [1, 128]]))
        return wb

    mt = [[K.sb("mt", [128, 512], F32) for _ in range(4)] for _ in range(2)]
    GOFF = SHIFT + RW
    it = 0
    for f in range(8):
        wA = mload(w_rwkv_out, f * 128, D, 4)
        wga = mload(w_in, GOFF + f * 128, PROJ, 8)
        wgb = mload(w_in, GOFF + D + f * 128, PROJ, 8)
        w1 = mload(glu_w1, f * 128, D, 4)
        w2 = mload(glu_w2, f * 128, D, 4)
        for tb in range(5):
            col0, n = TBLK[tb]
            cs = slice(col0, col0 + n)
            t0, t1_, t2_, t3_ = mt[it % 2]
            it += 1
            psA, psga, psgb, ps1, ps2 = PSF(), PSF(), PSF(), PSF(), PSF()
            for kt in range(4):
                K.mm(ps2[:, :n], w2[:, kt, :], hgl[:, kt, cs], start=(kt == 0), stop=(kt == 3))
            for kt in range(4):
                K.mm(ps1[:, :n], w1[:, kt, :], hgl[:, kt, cs], start=(kt == 0), stop=(kt == 3))
            for kt in range(8):
                K.mm(psgb[:, :n], wgb[:, kt, :], hT[:, kt, cs], start=(kt == 0), stop=(kt == 7))
            for kt in range(8):
                K.mm(psga[:, :n], wga[:, kt, :], hT[:, kt, cs], start=(kt == 0), stop=(kt == 7))
            for kt in range(4):
                K.mm(psA[:, :n], wA[:, kt, :], oaT[:, kt, cs], start=(kt == 0), stop=(kt == 3))
            K.act(t0[:, :n], ps2[:, :n], AF.Sigmoid, bias=pcol("glu_b2", f))
            K.stt(t1_[:, :n], ps1[:, :n], pcol("glu_b1", f), t0[:, :n], ALU.add, ALU.mult)
            K.act(t2_[:, :n], psgb[:, :n], AF.Sigmoid)
            K.tt("pool", t1_[:, :n], t1_[:, :n], t2_[:, :n], ALU.mult)
            K.act(t3_[:, :n], psga[:, :n], AF.Sigmoid)
            K.tt("dve", t3_[:, :n], psA[:, :n], t3_[:, :n], ALU.mult)
            K.tt("dve", mT[:, f, cs], t3_[:, :n], t1_[:, :n], ALU.add)
    pn_bufs = [(K.sb("pn_st", [128, 8], F32), K.sb("pn_x", [128, D], F32), K.sb("pn_t", [128, D], F32))
               for _ in range(2)]
    for i in range(17):
        rows = 128 if i < 16 else NS
        tcs = slice(i * 128, i * 128 + rows)
        halves = [PSF(), PSF()]
        for hf in range(2):
            for kt in range(8):
                K.mm(halves[hf][:rows, :], mT[:, kt, tcs], wmo[:, kt, hf * 512:(hf + 1) * 512],
                     start=(kt == 0), stop=(kt == 7))
        post_norm_residual(halves, rows, x_tile(i), gpost, x1_d[i * 128:i * 128 + rows, :], pn_bufs[i % 2])
    K.pop_scope()
    K.pop_scope()

    norm_transpose(lambda i: x1_d[i * 128:i * 128 + (128 if i < 16 else NS), :], "g_pre_ffn", hT)
    K.push_scope()
    NF = DFF // 128
    aT = K.sb("aT", [128, NF, TT], BF16)
    wd = K.sb("wd", [128, NF, D], BF16)
    for kt in range(NF):
        K.dma("pool", wd[:, kt, :], w_ffn_down[kt * 128:(kt + 1) * 128, :])
    gpost2 = bcast_row("gpost_ffn", g_post_ffn, D)
    fring = [K.sb("fring", [128, 8, 128], BF16) for _ in range(4)]
    fr_i = [0]

    def fload(src, c0):
        fr_i[0] += 1
        wb = fring[fr_i[0] % len(fring)]
        K.dma("pool", wb.v(), dap(src, c0, [[DFF, 128], [128 * DFF, 8], [1, 128]]))
        return wb

    sgt = [K.sb("sgt", [128, 512], F32) for _ in range(2)]
    it = 0
    for ft in range(NF):
        wg = fload(w_ffn_gate, ft * 128)
        wu = fload(w_ffn_up, ft * 128)
        for tb in range(5):
            col0, n = TBLK[tb]
            cs = slice(col0, col0 + n)
            psg, psu = PSF(), PSF()
            for kt in range(8):
                K.mm(psg[:, :n], wg[:, kt, :], hT[:, kt, cs], start=(kt == 0), stop=(kt == 7))
            for kt in range(8):
                K.mm(psu[:, :n], wu[:, kt, :], hT[:, kt, cs], start=(kt == 0), stop=(kt == 7))
            sg = sgt[it % 2]
            it += 1
            K.act(sg[:, :n], psg[:, :n], AF.Silu)
            K.tt("dve", aT[:, ft, cs], psu[:, :n], sg[:, :n], ALU.mult)
    pn_bufs = [(K.sb("pn_st", [128, 8], F32), K.sb("pn_x", [128, D], F32), K.sb("pn_t", [128, D], F32))
               for _ in range(2)]
    for i in range(17):
        rows = 128 if i < 16 else NS
        tcs = slice(i * 128, i * 128 + rows)
        halves = [PSF(), PSF()]
        for hf in range(2):
            for kt in range(NF):
                K.mm(halves[hf][:rows, :], aT[:, kt, tcs], wd[:, kt, hf * 512:(hf + 1) * 512],
                     start=(kt == 0), stop=(kt == NF - 1))
        dst = y_p[i * 128:(i + 1) * 128, :] if i < 16 else y_s[:, :]
        post_norm_residual(halves, rows, x1_d[i * 128:i * 128 + rows, :], gpost2, dst, pn_bufs[i % 2])
    K.pop_scope()
    K.finish()
    es.close()
    return nc


_W_NAMES = ["norm_pre_mix", "norm_post_mix", "norm_pre_ffn", "norm_post_ffn", "w_in", "mu_shift", "w0",
            "w_decay_up", "a0", "w_aaa_up", "w_gate_up", "k_k", "k_a", "r_k", "lnx_g", "lnx_b", "w_rwkv_out",
            "s5_lam_re", "s5_lam_im", "s5_log_dt", "s5_b_re", "s5_b_im", "s5_c_re", "s5_c_im", "s5_d",
            "glu_w1", "glu_b1", "glu_w2", "glu_b2", "w_merge_out", "w_ffn_gate", "w_ffn_up", "w_ffn_down"]


def make_in_maps(inputs, cores):
    f = lambda a: np.ascontiguousarray(np.asarray(a, dtype=np.float32))
    shared = {n: f(inputs[n])[0] for n in _W_NAMES}
    maps = []
    for c in cores:
        m = dict(shared)
        m["x_p"] = f(inputs["x_prompt"][c])
        m["x_s"] = f(inputs["x_sample"][c * NS:(c + 1) * NS, 0])
        m["st_shift"] = f(inputs["state_shift"][0, c * NS:(c + 1) * NS])
        m["st_wkv"] = f(inputs["state_wkv"][0, c * NS:(c + 1) * NS]).reshape(NS * 8, 64 * 64)
        m["st_re"] = f(inputs["state_s5_re"][0, c * NS:(c + 1) * NS]).reshape(NS, 2048)
        m["st_im"] = f(inputs["state_s5_im"][0, c * NS:(c + 1) * NS]).reshape(NS, 2048)
        maps.append(m)
    return maps


def assemble(results):
    n = len(results)
    cat = lambda k: np.concatenate([np.asarray(r[k]) for r in results], axis=0)
    y_p = np.stack([np.asarray(r["y_p"]) for r in results], 0)
    y_s = cat("y_s").reshape(n * NS, 1, D)
    sh_p = cat("o_shift_p").reshape(1, n, SHIFT)
    wkv_p = np.stack([np.asarray(r["o_wkv_p"]) for r in results], 0).reshape(1, n, 8, 64, 64)
    re_p = cat("o_re_p").reshape(1, n, 32, 64)
    im_p = cat("o_im_p").reshape(1, n, 32, 64)
    sh_s = cat("o_shift_s").reshape(1, n * NS, SHIFT)
    wkv_s = cat("o_wkv_s").reshape(1, n * NS, 8, 64, 64)
    re_s = cat("o_re_s").reshape(1, n * NS, 32, 64)
    im_s = cat("o_im_s").reshape(1, n * NS, 32, 64)
    return tuple(np.ascontiguousarray(a, dtype=np.float32) for a in
                 (y_p, y_s, sh_p, wkv_p, re_p, im_p, sh_s, wkv_s, re_s, im_s))


def kernel(**inputs):
    nc = build()
    in_maps = make_in_maps(inputs, list(range(NCORES)))
    res = run_bass_kernel_spmd(nc, in_maps, core_ids=list(range(NCORES)))
    return assemble(res.results)
```

```python
import math
from contextlib import ExitStack
from functools import partial
import numpy as np
import concourse.bass as bass
import concourse.mybir as mybir
from concourse.bass_utils import run_bass_kernel_spmd

F32 = mybir.dt.float32
BF16 = mybir.dt.bfloat16
I32 = mybir.dt.int32
AF = mybir.ActivationFunctionType
ALU = mybir.AluOpType
AX = mybir.AxisListType

T = 2048
NS = 16
TT = T + NS
D = 1024
RW = 512
SHIFT = 1664
PROJ = 4224
DFF = 2816
NCORES = 8
CDEC = math.exp(-0.5)
TBLK = [(0, 512), (512, 512), (1024, 512), (1536, 512), (2048, 16)]
SAME_ENGINE_SYNC = True


class V:
    __slots__ = ("buf", "ap")

    def __init__(self, buf, ap):
        self.buf = buf
        self.ap = ap

    def __getitem__(self, idx):
        return V(self.buf, self.ap[idx])

    def bc(self, shape):
        return V(self.buf, self.ap.to_broadcast(list(shape)))

    def re(self, s, **kw):
        return V(self.buf, self.ap.rearrange(s, **kw))

    def un(self, axis):
        return V(self.buf, self.ap.unsqueeze(axis))

    def bitcast(self, dt):
        return V(self.buf, self.ap.bitcast(dt))


class Buf:
    def __init__(self, name, handle):
        self.name = name
        self.h = handle
        self.writes = {}
        self.reads = {}
        self.dsem = None
        self.dcnt = 0
        self.is_psum = False

    def __getitem__(self, idx):
        return V(self, self.h[idx])

    def v(self):
        return V(self, self.h[:])


class Kern:
    def __init__(self, nc, es):
        self.nc = nc
        self.es = es
        self.root_es = es
        self.eng = {"pe": nc.tensor, "act": nc.scalar, "dve": nc.vector, "pool": nc.gpsimd, "sp": nc.sync}
        self.sem = {}
        self.cnt = {}
        for e in ("pe", "act", "dve", "pool"):
            self.sem[e] = es.enter_context(nc.semaphore("s_" + e))
            self.cnt[e] = 0
        self.obs = {e: {} for e in self.eng}
        self.semname = {}
        self.dsems = []
        self.nbuf = 0
        self.psf_i = 0
        self.psb_i = 0

    def sb(self, name, shape, dt):
        self.nbuf += 1
        import os
        if os.environ.get("KDBG"):
            sz = int(np.prod(shape[1:])) * (2 if dt == BF16 else 4)
            self._tot = getattr(self, "_tot", 0) + sz
            print(f"alloc {name} {shape} {sz} depth={len(getattr(self, '_saved', []))} cum_noFree={self._tot}")
        h = self.es.enter_context(self.nc.sbuf_tensor(f"{name}_{self.nbuf}", list(shape), dt))
        return Buf(name, h)

    def barrier(self):
        for e in ("pe", "act", "dve", "pool", "sp"):
            for e2 in ("pe", "act", "dve", "pool"):
                if e2 != e and self.cnt[e2]:
                    self._wait(e, e2, self.sem[e2], self.cnt[e2])
            for ob in self.dsems:
                self._wait(e, "d:" + ob.name + str(id(ob)), ob.dsem, ob.dcnt)

    def push_scope(self):
        self._saved = getattr(self, "_saved", [])
        self._saved.append(self.es)
        self.es = ExitStack()

    def pop_scope(self):
        self.barrier()
        self.es.close()
        self.es = self._saved.pop()

    def ps(self, name, shape, dt):
        self.nbuf += 1
        h = self.es.enter_context(self.nc.psum_tensor(f"{name}_{self.nbuf}", list(shape), dt))
        b = Buf(name, h)
        b.is_psum = True
        return b

    def dram(self, name, shape, dt, kind="Internal"):
        h = self.nc.dram_tensor(name, list(shape), dt, kind=kind)
        return Buf(name, h.ap())

    def newsem(self, name):
        s = self.root_es.enter_context(self.nc.semaphore(name))
        return s

    def _wait(self, e, sem_key, sem, val):
        o = self.obs[e]
        if o.get(sem_key, 0) >= val:
            return
        self.eng[e].wait_ge(sem, val)
        o[sem_key] = val

    def _deps(self, e, W, R):
        need = {}
        for b in R:
            for k, (s, v) in b.writes.items():
                if need.get(k, (None, 0))[1] < v:
                    need[k] = (s, v)
            if b.is_psum:
                for k, (s, v) in b.reads.items():
                    if k != e and need.get(k, (None, 0))[1] < v:
                        need[k] = (s, v)
        for b in W:
            for k, (s, v) in b.writes.items():
                if need.get(k, (None, 0))[1] < v:
                    need[k] = (s, v)
            for k, (s, v) in b.reads.items():
                if need.get(k, (None, 0))[1] < v:
                    need[k] = (s, v)
        for k, (s, v) in need.items():
            if k == e:
                if e == "pe" or not SAME_ENGINE_SYNC:
                    continue
            self._wait(e, k, s, v)

    def _bufs(self, vs):
        out = []
        for x in vs:
            if isinstance(x, V):
                if x.buf not in out:
                    out.append(x.buf)
            elif isinstance(x, Buf):
                if x not in out:
                    out.append(x)
        return out

    def op(self, e, fn, W, R):
        Wb = self._bufs(W)
        Rb = self._bufs(R)
        self._deps(e, Wb, Rb)
        inst = fn()
        self.cnt[e] += 1
        c = self.cnt[e]
        inst.then_inc(self.sem[e], 1)
        for b in Wb:
            b.reads = {}
            b.writes[e] = (self.sem[e], c)
        for b in Rb:
            if b not in Wb:
                b.reads[e] = (self.sem[e], c)
        return inst

    def dma(self, q, out, in_, sem_buf=None, slow=False):
        Wb = self._bufs([out])
        Rb = self._bufs([in_])
        self._deps(q, Wb, Rb)
        ob = sem_buf if sem_buf is not None else out.buf
        if ob.dsem is None:
            ob.dsem = self.newsem("d_" + ob.name + str(len(self.dsems)))
            self.dsems.append(ob)
        if slow:
            with self.nc.allow_non_contiguous_dma(reason="small parameter layout load"):
                inst = self.eng[q].dma_start(out=out.ap, in_=in_.ap)
        else:
            inst = self.eng[q].dma_start(out=out.ap, in_=in_.ap)
        ob.dcnt += 16
        inst.then_inc(ob.dsem, 16)
        key = "d:" + ob.name + str(id(ob))
        for b in Wb:
            b.reads = {}
            b.writes[key] = (ob.dsem, ob.dcnt)
        for b in Rb:
            if b not in Wb:
                b.reads[key] = (ob.dsem, ob.dcnt)
        return inst

    def finish(self):
        for ob in self.dsems:
            self._wait("sp", "d:" + ob.name + str(id(ob)), ob.dsem, ob.dcnt)
        for e in ("pe", "act", "dve", "pool"):
            if self.cnt[e]:
                self._wait("sp", e, self.sem[e], self.cnt[e])

    @staticmethod
    def _a(x):
        return x.ap if isinstance(x, V) else x

    def act(self, out, in_, func, bias=None, scale=None, accum=None):
        kw = {}
        if bias is not None:
            kw["bias"] = self._a(bias)
        if scale is not None:
            kw["scale"] = self._a(scale)
        if accum is not None:
            kw["accum_out"] = self._a(accum)
        return self.op("act", lambda: self.nc.scalar.activation(out=out.ap, in_=in_.ap, func=func, **kw),
                       [out, accum], [in_, bias, scale])

    def tt(self, e, out, in0, in1, op):
        return self.op(e, lambda: self.eng[e].tensor_tensor(out=out.ap, in0=in0.ap, in1=in1.ap, op=op),
                       [out], [in0, in1])

    def ts(self, e, out, in0, s1, s2, op0, op1=None):
        if op1 is None:
            return self.op(e, lambda: self.eng[e].tensor_scalar(out=out.ap, in0=in0.ap, scalar1=self._a(s1),
                                                               scalar2=None, op0=op0), [out], [in0, s1])
        return self.op(e, lambda: self.eng[e].tensor_scalar(out=out.ap, in0=in0.ap, scalar1=self._a(s1),
                                                           scalar2=self._a(s2), op0=op0, op1=op1),
                       [out], [in0, s1, s2])

    def stt(self, out, in0, scalar, in1, op0, op1, e="dve"):
        return self.op(e, lambda: self.eng[e].scalar_tensor_tensor(out=out.ap, in0=in0.ap, scalar=self._a(scalar),
                                                                  in1=in1.ap, op0=op0, op1=op1),
                       [out], [in0, scalar, in1])

    def copy(self, e, out, in_):
        if e == "act":
            return self.act(out, in_, AF.Copy)
        return self.op(e, lambda: self.eng[e].tensor_copy(out=out.ap, in_=in_.ap), [out], [in_])

    def memset(self, e, out, val):
        return self.op(e, lambda: self.eng[e].memset(out.ap, val), [out], [])

    def recip(self, out, in_):
        return self.op("dve", lambda: self.nc.vector.reciprocal(out=out.ap, in_=in_.ap), [out], [in_])

    def reduce(self, out, in_, op=ALU.add, axis=AX.X):
        return self.op("dve", lambda: self.nc.vector.tensor_reduce(out=out.ap, in_=in_.ap, axis=axis, op=op),
                       [out], [in_])

    def scan(self, out, d0, d1, init, op0=ALU.mult, op1=ALU.add):
        return self.op("dve", lambda: self.nc.vector.tensor_tensor_scan(out=out.ap, data0=d0.ap, data1=d1.ap,
                                                                       initial=self._a(init), op0=op0, op1=op1),
                       [out], [d0, d1, init])

    def mm(self, out, lhsT, rhs, start=True, stop=True):
        return self.op("pe", lambda: self.nc.tensor.matmul(out.ap, lhsT=lhsT.ap, rhs=rhs.ap, start=start, stop=stop),
                       [out], [lhsT, rhs])

    def tr(self, out, in_, ident):
        return self.op("pe", lambda: self.nc.tensor.transpose(out.ap, in_.ap, ident.ap), [out], [in_, ident])

    def aselect(self, out, in_, pattern, cmp, fill, base, cm):
        return self.op("pool", lambda: self.nc.gpsimd.affine_select(out=out.ap, in_=in_.ap, pattern=pattern,
                                                                   compare_op=cmp, fill=fill, base=base,
                                                                   channel_multiplier=cm), [out], [in_])

    def iota(self, out, pattern, base, cm):
        return self.op("pool", lambda: self.nc.gpsimd.iota(out.ap, pattern=pattern, base=base,
                                                          channel_multiplier=cm), [out], [])


def run_merged_g(lists):
    idx = [0] * len(lists)
    while True:
        best, bf = -1, 2.0
        for li, L in enumerate(lists):
            if idx[li] < len(L):
                frac = idx[li] / len(L)
                if frac < bf:
                    best, bf = li, frac
        if best < 0:
            break
        lists[best][idx[best]]()
        idx[best] += 1


def dap(buf, offset, ap):
    base = buf.h if not hasattr(buf.h, "ap") or isinstance(buf.h, bass.AP) else buf.h
    t = base.tensor if isinstance(base, bass.AP) else base
    return V(buf, bass.AP(t, offset, [list(x) for x in ap]))


def build(stage=99):
    nc = bass.Bass("TRN2", target_bir_lowering=False)
    es = ExitStack()
    K = Kern(nc, es)

    def din(name, shape, dt=F32):
        return Buf(name, nc.dram_tensor(name, list(shape), dt, kind="ExternalInput").ap())

    def dout(name, shape):
        return Buf(name, nc.dram_tensor(name, list(shape), F32, kind="ExternalOutput").ap())

    x_p = din("x_p", [T, D])
    x_s = din("x_s", [NS, D])
    st_shift = din("st_shift", [NS, SHIFT])
    st_wkv = din("st_wkv", [NS * 8, 64 * 64])
    st_re = din("st_re", [NS, 2048])
    st_im = din("st_im", [NS, 2048])
    g_pre_mix = din("norm_pre_mix", [D])
    g_post_mix = din("norm_post_mix", [D])
    g_pre_ffn = din("norm_pre_ffn", [D])
    g_post_ffn = din("norm_post_ffn", [D])
    w_in = din("w_in", [D, PROJ])
    mu_shift = din("mu_shift", [SHIFT])
    w0 = din("w0", [RW])
    w_decay_up = din("w_decay_up", [32, RW])
    a0 = din("a0", [RW])
    w_aaa_up = din("w_aaa_up", [32, RW])
    w_gate_up = din("w_gate_up", [64, RW])
    k_k = din("k_k", [RW])
    k_a = din("k_a", [RW])
    r_k = din("r_k", [RW])
    lnx_g = din("lnx_g", [RW])
    lnx_b = din("lnx_b", [RW])
    w_rwkv_out = din("w_rwkv_out", [RW, D])
    s5_lam_re = din("s5_lam_re", [32, 64])
    s5_lam_im = din("s5_lam_im", [32, 64])
    s5_log_dt = din("s5_log_dt", [32])
    s5_b_re = din("s5_b_re", [32, 64, 16])
    s5_b_im = din("s5_b_im", [32, 64, 16])
    s5_c_re = din("s5_c_re", [32, 16, 64])
    s5_c_im = din("s5_c_im", [32, 16, 64])
    s5_d = din("s5_d", [RW])
    glu_w1 = din("glu_w1", [RW, D])
    glu_b1 = din("glu_b1", [D])
    glu_w2 = din("glu_w2", [RW, D])
    glu_b2 = din("glu_b2", [D])
    w_merge_out = din("w_merge_out", [D, D])
    w_ffn_gate = din("w_ffn_gate", [D, DFF])
    w_ffn_up = din("w_ffn_up", [D, DFF])
    w_ffn_down = din("w_ffn_down", [DFF, D])

    y_p = dout("y_p", [T, D])
    y_s = dout("y_s", [NS, D])
    o_shift_p = dout("o_shift_p", [1, SHIFT])
    o_wkv_p = dout("o_wkv_p", [8, 64, 64])
    o_re_p = dout("o_re_p", [1, 2048])
    o_im_p = dout("o_im_p", [1, 2048])
    o_shift_s = dout("o_shift_s", [NS, SHIFT])
    o_wkv_s = dout("o_wkv_s", [NS * 8, 64 * 64])
    o_re_s = dout("o_re_s", [NS, 2048])
    o_im_s = dout("o_im_s", [NS, 2048])

    x1_d = K.dram("x1_scratch", [TT, D], F32)

    psf = [K.ps("psf", [128, 512], F32) for _ in range(6)]
    psb = [K.ps("psb", [128, 1024], BF16) for _ in range(2)]

    def PSF():
        K.psf_i += 1
        return psf[K.psf_i % len(psf)]

    def PSB():
        K.psb_i += 1
        return psb[K.psb_i % len(psb)]

    ones_f = K.sb("ones_f", [128, 128], F32)
    identf = K.sb("identf", [128, 128], F32)
    identb = K.sb("identb", [128, 128], BF16)
    epsT = K.sb("epsT", [128, 1], F32)
    K.memset("pool", ones_f.v(), 1.0)
    K.memset("pool", epsT.v(), 1e-6)
    K.aselect(identf.v(), ones_f.v(), [[-1, 128]], ALU.is_equal, 0.0, 0, 1)
    K.copy("pool", identb.v(), identf.v())

    pvec = K.sb("pvec", [128, 128], F32)
    PV = {}
    _pv = [0]

    def load_fm(name, src, ntile):
        c0 = _pv[0]
        _pv[0] += ntile
        K.dma("sp", pvec[:, c0:c0 + ntile], dap(src, 0, [[1, 128], [128, ntile]]), sem_buf=pvec, slow=True)
        PV[name] = c0
        return c0

    load_fm("g_pre_mix", g_pre_mix, 8)
    load_fm("g_pre_ffn", g_pre_ffn, 8)
    load_fm("mu", mu_shift, 13)
    load_fm("w0", w0, 4)
    load_fm("a0", a0, 4)
    load_fm("k_k", k_k, 4)
    load_fm("k_a", k_a, 4)
    load_fm("r_k", r_k, 4)
    load_fm("s5_d", s5_d, 4)
    load_fm("glu_b1", glu_b1, 8)
    load_fm("glu_b2", glu_b2, 8)

    def early():
        while getattr(K, "_saved", []):
            K.pop_scope()
        K.finish()
        es.close()
        return nc

    if stage == 0.1:
        return early()

    def pcol(name, i=0):
        c = PV[name] + i
        return pvec[:, c:c + 1]

    hT = K.sb("hT", [128, 8, TT], BF16)
    def norm_transpose(src_tile_fn, gname, dst):
        K.push_scope()
        xring = [K.sb("xring", [128, D], F32) for _ in range(2)]
        xn = [K.sb("xn", [128, D], BF16) for _ in range(2)]
        junk = K.sb("junk", [128, D], BF16)
        stat = [K.sb("stat", [128, 4], F32) for _ in range(2)]
        for i in range(17):
            rows = 128 if i < 16 else NS
            col0 = i * 128
            xt = xring[i % 2]
            st = stat[i % 2]
            xb = xn[i % 2]
            K.dma("sp", xt[:rows, :], src_tile_fn(i))
            K.act(junk[:rows, :], xt[:rows, :], AF.Square, accum=st[:rows, 0:1])
            K.act(st[:rows, 1:2], st[:rows, 0:1], AF.Sqrt, bias=epsT[:rows, :], scale=1.0 / D)
            K.recip(st[:rows, 2:3], st[:rows, 1:2])
            K.act(xb[:rows, :], xt[:rows, :], AF.Copy, scale=st[:rows, 2:3])
            pb = PSB()
            pv = pb.v().re("p (k t) -> p k t", k=8)
            for kt in range(8):
                K.tr(pv[:, kt, :rows], xb[:rows, kt * 128:(kt + 1) * 128], identb[:rows, :rows])
            g0 = PV[gname]
            K.tt("dve", dst[:, :, col0:col0 + rows], pv[:, :, :rows],
                 pvec[:, g0:g0 + 8].un(2).bc([128, 8, rows]), ALU.mult)
        K.pop_scope()

    def x_tile(i):
        return x_p[i * 128:(i + 1) * 128, :] if i < 16 else x_s[:, :]

    norm_transpose(x_tile, "g_pre_mix", hT)
    if stage == 0.2:
        return early()

    wring = [K.sb("wring", [128, 8, 128], BF16) for _ in range(3)]
    wr_i = [0]

    def load_w_cols(src, c0, ncols_total, kt_n=8, width=128):
        wr_i[0] += 1
        wb = wring[wr_i[0] % len(wring)]
        K.dma("pool", wb[:, :kt_n, :width],
              dap(src, c0, [[ncols_total, 128], [128 * ncols_total, kt_n], [1, width]]))
        return wb

    def proj_fm(wb, kt_n, src_act, tb, ps):
        col0, n = TBLK[tb]
        for kt in range(kt_n):
            K.mm(ps[:, :n], wb[:, kt, :], src_act[:, kt, col0:col0 + n], start=(kt == 0), stop=(kt == kt_n - 1))

    hgl_d = K.dram("hgl_scratch", [128, 4 * TT], BF16)
    TWO_PI = 2.0 * math.pi

    K.push_scope()
    hgl = K.sb("hgl", [128, 4, TT], BF16)
    ubf = K.sb("ubf", [128, 4, TT], BF16)
    wu4 = [K.sb("wu4", [128, 8, 128], BF16) for _ in range(4)]
    for ft in range(4):
        K.dma("pool", wu4[ft].v(), dap(w_in, SHIFT + ft * 128, [[PROJ, 128], [128 * PROJ, 8], [1, 128]]))
        for tb in range(5):
            col0, n = TBLK[tb]
            ps = PSF()
            proj_fm(wu4[ft], 8, hT, tb, ps)
            K.copy("act" if tb % 2 else "dve", ubf[:, ft, col0:col0 + n], ps[:, :n])

    if stage == 1.1:
        return early()
    sp_ = K.sb("s5par", [128, 16, 16], F32)
    P_LRE, P_LIM, P_DT, P_MAG, P_TH, P_FRE, P_FIM, P_LBRE, P_LBIM, P_T0, P_T1, P_T2, P_T3 = range(13)

    def sp(k):
        return sp_[:, :, k]

    K.dma("sp", sp(P_LRE), dap(s5_lam_re, 0, [[1, 128], [128, 16]]), slow=True)
    K.dma("sp", sp(P_LIM), dap(s5_lam_im, 0, [[1, 128], [128, 16]]), slow=True)
    for gl in range(2):
        K.dma("sp", sp_[gl * 64:(gl + 1) * 64, :, P_DT], dap(s5_log_dt, gl, [[0, 64], [2, 16]]), slow=True)
    K.act(sp(P_DT), sp(P_DT), AF.Exp)
    K.tt("dve", sp(P_T0), sp(P_LRE), sp(P_DT), ALU.mult)
    K.act(sp(P_MAG), sp(P_T0), AF.Exp)
    K.tt("dve", sp(P_TH), sp(P_LIM), sp(P_DT), ALU.mult)

    if stage == 1.15:
        return early()
    LC = 128
    cosT = K.sb("cosT", [128, 16, LC], F32)
    sinT = K.sb("sinT", [128, 16, LC], F32)
    preR = K.sb("preR", [128, 16, LC], F32)
    preI = K.sb("preI", [128, 16, LC], F32)
    K.push_scope()
    ji = K.sb("ji", [128, LC], I32)
    jf = K.sb("jf", [128, LC], F32)
    ang = K.sb("ang", [128, 16, LC], F32)
    rr = K.sb("rr", [128, 16, LC], F32)
    kf = K.sb("kf", [128, 16, LC], F32)
    ki = K.sb("ki", [128, 16, LC], I32)
    K.iota(ji.v(), [[1, LC]], 1, 0)
    K.copy("dve", jf.v(), ji.v())
    K.tt("dve", ang.v(), sp(P_TH).un(2).bc([128, 16, LC]), jf.v().un(1).bc([128, 16, LC]), ALU.mult)

    def sin_reduced(out, a_in, shift):
        C1 = 6.28125
        C2 = TWO_PI - C1
        K.ts("dve", rr.v(), a_in, shift, 1.0 / TWO_PI, ALU.add, ALU.mult)
        K.copy("dve", ki.v(), rr.v())
        K.copy("dve", kf.v(), ki.v())
        K.ts("dve", rr.v(), a_in, shift, None, ALU.add)
        K.stt(rr.v(), kf.v(), -C1, rr.v(), ALU.mult, ALU.add)
        K.stt(rr.v(), kf.v(), -C2, rr.v(), ALU.mult, ALU.add)
        K.ts("dve", kf.v(), rr.v(), math.pi, None, ALU.is_gt)
        K.stt(rr.v(), kf.v(), -TWO_PI, rr.v(), ALU.mult, ALU.add)
        K.ts("dve", kf.v(), rr.v(), -math.pi, None, ALU.is_lt)
        K.stt(rr.v(), kf.v(), TWO_PI, rr.v(), ALU.mult, ALU.add)
        K.ts("dve", rr.v(), rr.v(), -math.pi, math.pi, ALU.max, ALU.min)
        K.act(out, rr.v(), AF.Sin)

    sin_reduced(sinT.v(), ang.v(), 0.0)
    sin_reduced(cosT.v(), ang.v(), 0.5 * math.pi)
    K.tt("dve", sp(P_LBRE), sp(P_MAG), cosT[:, :, 0], ALU.mult)
    K.tt("dve", sp(P_LBIM), sp(P_MAG), sinT[:, :, 0], ALU.mult)
    K.tt("dve", sp(P_T0), sp(P_LRE), sp(P_LRE), ALU.mult)
    K.tt("dve", sp(P_T1), sp(P_LIM), sp(P_LIM), ALU.mult)
    K.tt("dve", sp(P_T0), sp(P_T0), sp(P_T1), ALU.add)
    K.recip(sp(P_T0), sp(P_T0))
    K.ts("dve", sp(P_T1), sp(P_LBRE), -1.0, None, ALU.add)
    K.tt("dve", sp(P_T2), sp(P_T1), sp(P_LRE), ALU.mult)
    K.tt("dve", sp(P_T3), sp(P_LBIM), sp(P_LIM), ALU.mult)
    K.tt("dve", sp(P_T2), sp(P_T2), sp(P_T3), ALU.add)
    K.tt("dve", sp(P_FRE), sp(P_T2), sp(P_T0), ALU.mult)
    K.tt("dve", sp(P_T2), sp(P_LBIM), sp(P_LRE), ALU.mult)
    K.tt("dve", sp(P_T3), sp(P_T1), sp(P_LIM), ALU.mult)
    K.tt("dve", sp(P_T2), sp(P_T2), sp(P_T3), ALU.subtract)
    K.tt("dve", sp(P_FIM), sp(P_T2), sp(P_T0), ALU.mult)
    fre_b = sp(P_FRE).un(2).bc([128, 16, LC])
    fim_b = sp(P_FIM).un(2).bc([128, 16, LC])
    K.tt("dve", rr.v(), cosT.v(), fre_b, ALU.mult)
    K.tt("dve", kf.v(), sinT.v(), fim_b, ALU.mult)
    K.tt("dve", preR.v(), rr.v(), kf.v(), ALU.add)
    K.tt("dve", rr.v(), cosT.v(), fim_b, ALU.mult)
    K.tt("dve", kf.v(), sinT.v(), fre_b, ALU.mult)
    K.tt("dve", preI.v(), rr.v(), kf.v(), ALU.subtract)
    K.pop_scope()

    if stage == 1.2:
        return early()
    BT = K.sb("BT", [128, 32, 128], BF16)
    CTb = K.sb("CTb", [128, 32, 128], BF16)
    K.push_scope()
    for part, src in enumerate((s5_b_re, s5_b_im)):
        X = K.sb("Xb", [128, 16 * 128], F32)
        K.memset("pool", X.v(), 0.0)
        for gl in range(2):
            for a in range(4):
                K.dma("sp", dap(X, gl * 64 * 2048 + 16 * gl + a * 512, [[2048, 64], [160, 4], [1, 16]]),
                      dap(src, gl * 1024 + a * 8192, [[16, 64], [2048, 4], [1, 16]]), slow=True)
        for i4 in range(4):
            ps = PSF()
            for j in range(4):
                i = i4 * 4 + j
                K.tr(ps[:, j * 128:(j + 1) * 128], X[:, i * 128:(i + 1) * 128], identf.v())
            for j in range(4):
                i = i4 * 4 + j
                K.copy("act" if j % 2 else "dve", BT[:, 2 * i + part, :], ps[:, j * 128:(j + 1) * 128])
    for part, src in enumerate((s5_c_re, s5_c_im)):
        Z = K.sb("Zc", [128, 4 * 512], F32)
        K.memset("pool", Z.v(), 0.0)
        for g8 in range(8):
            K.dma("sp", dap(Z, 16 * g8 * 2048 + 64 * g8, [[2048, 16], [512, 4], [1, 64]]),
                  dap(src, g8 * 1024, [[64, 16], [8192, 4], [1, 64]]))
        for q in range(4):
            ps = PSF()
            for j in range(4):
                K.tr(ps[:, j * 128:(j + 1) * 128], Z[:, q * 512 + j * 128:q * 512 + (j + 1) * 128], identf.v())
            i0 = q * 4
            dst = CTb.v().re("p (i two) c -> p i two c", two=2)[:, i0:i0 + 4, part, :]
            if part == 0:
                K.copy("dve", dst, ps.v().re("p (j c) -> p j c", j=4))
            else:
                K.ts("dve", dst, ps.v().re("p (j c) -> p j c", j=4), -1.0, None, ALU.mult)
    K.pop_scope()

    if stage == 1.3:
        return early()
    x1b = [K.sb("x1sb", [128, 16, NS], BF16) for _ in range(2)]
    K.push_scope()
    s0_tm = [K.sb("s0tm", [NS, 2048], F32) for _ in range(2)]
    s0 = [K.sb("s0", [128, 16, NS], F32) for _ in range(2)]
    K.dma("sp", s0_tm[0].v(), st_re.v())
    K.dma("sp", s0_tm[1].v(), st_im.v())
    for part in range(2):
        ps = PSF()
        for i in range(16):
            K.tr(ps[:, i * NS:(i + 1) * NS], s0_tm[part][:, i * 128:(i + 1) * 128], identf[:NS, :NS])
        K.copy("dve", s0[part].v(), ps[:, :16 * NS].re("p (i n) -> p i n", i=16))
    psr = PSF()
    psi = PSF()
    for i in range(16):
        K.mm(psr[:, i * NS:(i + 1) * NS], BT[:, 2 * i, :], ubf[:, i // 4, T:TT])
        K.mm(psi[:, i * NS:(i + 1) * NS], BT[:, 2 * i + 1, :], ubf[:, i // 4, T:TT])
    sw = [K.sb("sw", [128, 16, NS], F32) for _ in range(4)]
    x1 = [K.sb("x1s", [128, 16, NS], F32) for _ in range(2)]
    bc16 = lambda k: sp(k).un(2).bc([128, 16, NS])
    pr3 = psr[:, :16 * NS].re("p (i n) -> p i n", i=16)
    pi3 = psi[:, :16 * NS].re("p (i n) -> p i n", i=16)
    K.tt("dve", sw[0].v(), pr3, bc16(P_FRE), ALU.mult)
    K.tt("dve", sw[1].v(), pi3, bc16(P_FIM), ALU.mult)
    K.tt("dve", sw[2].v(), sw[0].v(), sw[1].v(), ALU.subtract)
    K.tt("dve", sw[0].v(), pi3, bc16(P_FRE), ALU.mult)
    K.tt("dve", sw[1].v(), pr3, bc16(P_FIM), ALU.mult)
    K.tt("dve", sw[3].v(), sw[0].v(), sw[1].v(), ALU.add)
    K.tt("dve", sw[0].v(), s0[0].v(), bc16(P_LBRE), ALU.mult)
    K.tt("dve", sw[1].v(), s0[1].v(), bc16(P_LBIM), ALU.mult)
    K.tt("dve", sw[0].v(), sw[0].v(), sw[1].v(), ALU.subtract)
    K.tt("dve", x1[0].v(), sw[0].v(), sw[2].v(), ALU.add)
    K.tt("dve", sw[0].v(), s0[1].v(), bc16(P_LBRE), ALU.mult)
    K.tt("dve", sw[1].v(), s0[0].v(), bc16(P_LBIM), ALU.mult)
    K.tt("dve", sw[0].v(), sw[0].v(), sw[1].v(), ALU.add)
    K.tt("dve", x1[1].v(), sw[0].v(), sw[3].v(), ALU.add)
    for part in range(2):
        K.copy("act", x1b[part].v(), x1[part].v())
        xs_tm = s0_tm[part]
        for i4 in range(4):
            ps = PSF()
            for j in range(4):
                i = i4 * 4 + j
                K.tr(ps[:NS, j * 128:(j + 1) * 128], x1[part][:, i, :], identf.v())
            K.copy("dve", xs_tm[:, i4 * 512:(i4 + 1) * 512], ps[:NS, :])
        K.dma("sp", (o_re_s if part == 0 else o_im_s).v(), xs_tm.v())
    K.pop_scope()

    if stage == 1.4:
        return early()
    carry = K.sb("carry", [128, 16, 2], F32)
    K.memset("dve", carry.v(), 0.0)
    UN = [dict(bR=K.sb("bR", [128, 512], F32), bI=K.sb("bI", [128, 512], F32),
               wR=K.sb("wR", [128, 512], F32), wI=K.sb("wI", [128, 512], F32),
               t1=K.sb("t1", [128, 512], F32), t2=K.sb("t2", [128, 512], F32)) for _ in range(4)]
    xbr = [K.sb("xbr", [128, 512], BF16) for _ in range(4)]
    xbi = [K.sb("xbi", [128, 512], BF16) for _ in range(4)]
    yv = [K.sb("yv", [128, 512], F32) for _ in range(2)]
    y2 = [K.sb("y2", [128, 512], F32) for _ in range(2)]

    def gelu_to(dst_bf, yv_, y2_, n):
        K.act(y2_[:, :n], yv_[:, :n], AF.Square)
        K.act(y2_[:, :n], y2_[:, :n], AF.Identity, bias=1.0, scale=0.044715)
        K.tt("pool", y2_[:, :n], y2_[:, :n], yv_[:, :n], ALU.mult)
        K.act(y2_[:, :n], y2_[:, :n], AF.Sigmoid, scale=1.5957691216057308)
        K.tt("pool", dst_bf, y2_[:, :n], yv_[:, :n], ALU.mult)

    v4 = lambda b: b.v().re("p (a l) -> p a l", a=4)
    for q in range(4):
        for tb in range(4):
            col0, n = TBLK[tb]
            for j in range(4):
                i = q * 4 + j
                U = UN[j]
                psr = PSF()
                psi = PSF()
                K.mm(psr.v(), BT[:, 2 * i, :], ubf[:, q, col0:col0 + n])
                K.mm(psi.v(), BT[:, 2 * i + 1, :], ubf[:, q, col0:col0 + n])
                tR = preR[:, i, :].un(1).bc([128, 4, LC])
                tI = preI[:, i, :].un(1).bc([128, 4, LC])
                K.tt("dve", v4(U["t1"]), v4(psr), tR, ALU.mult)
                K.tt("dve", v4(U["t2"]), v4(psi), tI, ALU.mult)
                K.tt("pool", U["bR"].v(), U["t1"].v(), U["t2"].v(), ALU.subtract)
                K.tt("dve", v4(U["wR"]), v4(psi), tR, ALU.mult)
                K.tt("dve", v4(U["wI"]), v4(psr), tI, ALU.mult)
                K.tt("dve", U["bI"].v(), U["wR"].v(), U["wI"].v(), ALU.add)
            for a in range(4):
                cs = slice(a * LC, (a + 1) * LC)
                for j in range(4):
                    i = q * 4 + j
                    U = UN[j]
                    rho = sp_[:, i, P_MAG:P_MAG + 1].bc([128, LC])
                    if a == 0:
                        ire, iim = carry[:, i, 0:1], carry[:, i, 1:2]
                    else:
                        ire, iim = U["bR"][:, a * LC - 1:a * LC], U["bI"][:, a * LC - 1:a * LC]
                    K.scan(U["wR"][:, cs], rho, U["bR"][:, cs], ire)
                    K.scan(U["wI"][:, cs], rho, U["bI"][:, cs], iim)
                for j in range(4):
                    i = q * 4 + j
                    U = UN[j]
                    K.tt("pool", U["t1"][:, cs], U["wR"][:, cs], cosT[:, i, :], ALU.mult)
                    K.tt("dve", U["t2"][:, cs], U["wI"][:, cs], sinT[:, i, :], ALU.mult)
                    K.tt("dve", U["bR"][:, cs], U["t1"][:, cs], U["t2"][:, cs], ALU.subtract)
                    K.tt("pool", U["t1"][:, cs], U["wR"][:, cs], sinT[:, i, :], ALU.mult)
                    K.tt("dve", U["t2"][:, cs], U["wI"][:, cs], cosT[:, i, :], ALU.mult)
                    K.tt("dve", U["bI"][:, cs], U["t1"][:, cs], U["t2"][:, cs], ALU.add)
            for j in range(4):
                i = q * 4 + j
                U = UN[j]
                K.copy("dve", carry[:, i, 0:1], U["bR"][:, 511:512])
                K.copy("dve", carry[:, i, 1:2], U["bI"][:, 511:512])
                K.copy("act", xbr[j].v(), U["bR"].v())
                K.copy("act", xbi[j].v(), U["bI"].v())
            psy = PSF()
            for j in range(4):
                i = q * 4 + j
                K.mm(psy.v(), CTb[:, 2 * i, :], xbr[j].v(), start=(j == 0), stop=False)
                K.mm(psy.v(), CTb[:, 2 * i + 1, :], xbi[j].v(), start=False, stop=(j == 3))
            yy = yv[tb % 2]
            K.copy("act", yy.v(), psy.v())
            psu = PSF()
            proj_fm(wu4[q], 8, hT, tb, psu)
            K.stt(yy.v(), psu.v(), pcol("s5_d", q), yy.v(), ALU.mult, ALU.add)
            gelu_to(hgl[:, q, col0:col0 + n], yy, y2[tb % 2], n)
        psy = PSF()
        for j in range(4):
            i = q * 4 + j
            K.mm(psy[:, :NS], CTb[:, 2 * i, :], x1b[0][:, i, :], start=(j == 0), stop=False)
            K.mm(psy[:, :NS], CTb[:, 2 * i + 1, :], x1b[1][:, i, :], start=False, stop=(j == 3))
        K.copy("act", yv[0][:, :NS], psy[:, :NS])
        psu = PSF()
        proj_fm(wu4[q], 8, hT, 4, psu)
        K.stt(yv[0][:, :NS], psu[:, :NS], pcol("s5_d", q), yv[0][:, :NS], ALU.mult, ALU.add)
        gelu_to(hgl[:, q, T:TT], yv[0], y2[0], NS)
    if stage == 1.5:
        return early()
    K.dma("sp", dap(o_re_p, 0, [[1, 128], [128, 16]]), carry[:, :, 0], slow=True)
    K.dma("sp", dap(o_im_p, 0, [[1, 128], [128, 16]]), carry[:, :, 1], slow=True)
    K.dma("sp", hgl_d.v(), hgl.v().re("p k t -> p (k t)"))
    K.pop_scope()
    if stage == 2:
        zt = K.sb("zt", [128, 4096], F32)
        K.memset("pool", zt.v(), 0.0)
        K.dma("sp", o_shift_p.v(), zt[0:1, 0:SHIFT])
        K.dma("sp", o_shift_s.v(), zt[0:NS, 0:SHIFT])
        K.dma("sp", o_wkv_p.v().re("h v k -> h (v k)"), zt[0:8, :])
        K.dma("sp", o_wkv_s.v(), zt[:, :])
        for i in range(16):
            K.dma("sp", y_p[i * 128:(i + 1) * 128, :], zt[:, 0:D])
        K.dma("sp", y_s.v(), zt[0:NS, 0:D])
        return early()
    K.push_scope()
    lora_bf = K.sb("lora_bf", [128, TT], BF16)
    lora_up = K.sb("lora_up", [128, RW], BF16)
    o_d = K.dram("o_scratch", [T, RW], F32)
    bonus_d = K.dram("bonus_scratch", [T, RW], F32)
    samp_d = K.dram("samp_scratch", [NS * 8, 7, 64], F32)
    sampo_d = K.dram("sampo_scratch", [NS, RW], F32)

    K.push_scope()
    mask4 = K.sb("mask4", [128, 4, 128], F32)
    maskNT = K.sb("maskNT", [128, 128], F32)
    resetm = K.sb("resetm", [128, T], BF16)
    blk1 = K.sb("blk1", [128, 128], F32)
    eps_ln = K.sb("eps_ln", [128, 1], F32)
    K.memset("pool", eps_ln.v(), 64e-5)
    for j in range(4):
        K.aselect(mask4[:, j, :], ones_f.v(), [[1, 128]], ALU.is_gt if j % 2 == 0 else ALU.is_ge, 0.0, 0, -1)
    K.aselect(maskNT.v(), ones_f.v(), [[-1, 128]], ALU.is_gt, 0.0, 0, 1)
    K.memset("pool", resetm.v(), 1.0)
    K.memset("pool", resetm.v().re("p (c l) -> p c l", l=128)[:, :, 0:1], 0.0)
    K.memset("pool", blk1.v(), 0.0)
    K.memset("pool", blk1[0:64, 0:64], 1.0)
    K.memset("pool", blk1[64:128, 64:128], 1.0)

    K.dma("pool", lora_up[0:32, :], w_decay_up.v())
    K.dma("pool", lora_up[32:64, :], w_aaa_up.v())
    K.dma("pool", lora_up[64:128, :], w_gate_up.v())
    sh0T = K.sb("sh0T", [128, 13, NS], F32)
    shout = [K.sb("shout", [NS + 1, 128], F32) for _ in range(2)]
    K.push_scope()
    sh_tm = K.sb("sh_tm", [NS, SHIFT], F32)
    K.dma("sp", sh_tm.v(), st_shift.v())
    ps = PSF()
    for ft in range(13):
        K.tr(ps[:, ft * NS:(ft + 1) * NS], sh_tm[:, ft * 128:(ft + 1) * 128], identf[:NS, :NS])
    K.copy("dve", sh0T.v(), ps[:, :13 * NS].re("p (f n) -> p f n", f=13))
    K.pop_scope()

    omm = K.sb("omm", [128, 13], F32)
    K.ts("dve", omm.v(), pvec[:, PV["mu"]:PV["mu"] + 13], -1.0, 1.0, ALU.mult, ALU.add)
    omka = K.sb("omka", [128, 4], F32)
    K.ts("dve", omka.v(), pvec[:, PV["k_a"]:PV["k_a"] + 4], -1.0, 1.0, ALU.mult, ALU.add)

    SC = [K.sb("scr", [128, TT], F32) for _ in range(6)]

    def proj_shift_tile(ft, pr, tmp, xr_out, xr_out_dt_bf=None, samp32=None):
        wb = load_w_cols(w_in, ft * 128, PROJ)
        for tb in range(5):
            col0, n = TBLK[tb]
            ps = PSF()
            proj_fm(wb, 8, hT, tb, ps)
            K.copy("act", pr[:, col0:col0 + n], ps[:, :n])
            K.act(tmp[:, col0:col0 + n], ps[:, :n], AF.Copy, scale=omm[:, ft:ft + 1])
        ps = PSF()
        K.tr(ps[:NS + 1, :128], pr[:, T - 1:TT], identf.v())
        so = shout[ft % 2]
        K.copy("dve", so.v(), ps[:NS + 1, :128])
        K.dma("sp", o_shift_p[0:1, ft * 128:(ft + 1) * 128], so[0:1, :])
        K.dma("sp", o_shift_s[:, ft * 128:(ft + 1) * 128], so[1:NS + 1, :])
        mu_c = pcol("mu", ft)
        K.stt(xr_out[:, 1:T], pr[:, 0:T - 1], mu_c, tmp[:, 1:T], ALU.mult, ALU.add)
        K.copy("pool", xr_out[:, 0:1], tmp[:, 0:1])
        K.stt(xr_out[:, T:TT], sh0T[:, ft, :], mu_c, tmp[:, T:TT], ALU.mult, ALU.add)
        if samp32 is not None:
            K.stt(samp32, sh0T[:, ft, :], mu_c, tmp[:, T:TT], ALU.mult, ALU.add)

    proj_shift_tile(12, SC[0], SC[1], SC[2].v())
    K.act(lora_bf[0:32, :], SC[2][0:32, :], AF.Tanh)
    K.act(lora_bf[32:64, :], SC[2][32:64, :], AF.Copy)
    K.act(lora_bf[64:128, :], SC[2][64:128, :], AF.Sigmoid)

    vbf = K.sb("vbf", [128, TT], BF16)
    vs32 = K.sb("vs32", [128, NS], F32)
    AR = K.sb("AR", [128, 16, 2, 128], BF16)
    bti = K.sb("bti", [128, T], BF16)
    kti = K.sb("kti", [128, T], BF16)
    bhat = K.sb("bhat", [128, T], BF16)
    khat = K.sb("khat", [128, T], BF16)
    Wfm = K.sb("Wfm", [128, T], BF16)
    TMq = K.sb("TMq", [128, 16, 4, 128], BF16)
    gL = K.sb("gL", [128, 16], F32)
    Sf = K.sb("Sf", [128, 64], F32)
    Sb = K.sb("Sb", [128, 64], BF16)
    smp = K.sb("smp", [NS, 6, 128], F32)
    decs = K.sb("decs", [128, NS], F32)
    avs = K.sb("avs", [128, NS], F32)
    NPR = 16
    ABK = [K.sb("ABK", [128, 2, 128], BF16) for _ in range(NPR)]
    NPQ = 8
    ATMP = [K.sb("ATMP", [128, 2, 128], BF16) for _ in range(NPQ)]
    P0T = [K.sb("P0T", [128, 128], BF16) for _ in range(NPQ)]
    PMb = [[K.sb("PMb", [128, 3, 128], BF16) for _ in range(2)] for _ in range(NPQ)]
    Ybf = [K.sb("Ybf", [128, 64], BF16) for _ in range(NPQ)]
    Ubar = [K.sb("Ubar", [128, 64], F32) for _ in range(NPR)]
    Ut = [K.sb("Ut", [128, 64], BF16) for _ in range(4)]
    Otile = [K.sb("Otile", [128, 128], F32) for _ in range(2)]
    btile = [K.sb("btile", [128, 512], F32) for _ in range(2)]
    wkvo = K.sb("wkvo", [64, 128], F32)

    for hp in range(4):
        xr_r, xr_k = SC[0], SC[1]
        proj_shift_tile(0 * 4 + hp, SC[2], SC[3], xr_r.v())
        proj_shift_tile(1 * 4 + hp, SC[2], SC[3], xr_k.v())
        proj_shift_tile(2 * 4 + hp, SC[2], SC[3], vbf.v(), samp32=vs32.v())
        sigd, csum, afm, kkn = SC[2], SC[3], SC[4], SC[5]
        hc = slice(hp * 128, (hp + 1) * 128)
        for tb in range(5):
            col0, n = TBLK[tb]
            ps = PSF()
            K.mm(ps[:, :n], lora_up[0:32, hc], lora_bf[0:32, col0:col0 + n])
            K.act(sigd[:, col0:col0 + n], ps[:, :n], AF.Sigmoid, bias=pcol("w0", hp))
            ps = PSF()
            K.mm(ps[:, :n], lora_up[32:64, hc], lora_bf[32:64, col0:col0 + n])
            K.act(afm[:, col0:col0 + n], ps[:, :n], AF.Sigmoid, bias=pcol("a0", hp))
        K.act(kkn.v(), xr_k.v(), AF.Square, scale=pcol("k_k", hp))
        for tb in range(5):
            col0, n = TBLK[tb]
            ps = PSF()
            K.mm(ps[:, :n], blk1.v(), kkn[:, col0:col0 + n])
            K.ts("dve", csum[:, col0:col0 + n], ps[:, :n], 1e-24, None, ALU.max)
        K.act(csum.v(), csum.v(), AF.Ln)
        K.act(csum.v(), csum.v(), AF.Exp, scale=-0.5)
        K.stt(kkn.v(), xr_k.v(), pcol("k_k", hp), csum.v(), ALU.mult, ALU.mult)
        K.ts("pool", csum.v(), afm.v(), pcol("k_a", hp), omka[:, hp:hp + 1], ALU.mult, ALU.add)
        K.tt("pool", xr_k.v(), xr_k.v(), csum.v(), ALU.mult)
        kmod = xr_k
        K.tt("pool", afm.v(), kkn.v(), afm.v(), ALU.mult)
        bfm = afm
        K.stt(csum.v(), xr_r.v(), pcol("r_k", hp), kmod.v(), ALU.mult, ALU.mult)
        for tb in range(4):
            col0, n = TBLK[tb]
            ps = PSF()
            K.mm(ps[:, :n], blk1.v(), csum[:, col0:col0 + n])
            K.tt("dve", csum[:, col0:col0 + n], ps[:, :n], vbf[:, col0:col0 + n], ALU.mult)
        for c4 in range(4):
            ps = PSF()
            for j in range(4):
                c = c4 * 4 + j
                K.tr(ps[:, j * 128:(j + 1) * 128], csum[:, c * 128:(c + 1) * 128], identf.v())
            bt = btile[c4 % 2]
            K.copy("act", bt.v(), ps.v())
            K.dma("sp", dap(bonus_d, c4 * 512 * RW + hp * 128, [[RW, 128], [128 * RW, 4], [1, 128]]),
                  bt.v().re("p (j f) -> p j f", j=4))
        K.act(decs.v(), sigd[:, T:TT], AF.Exp, scale=-CDEC)
        K.ts("dve", avs.v(), kkn[:, T:TT], -1.0, None, ALU.mult)
        K.scan(csum[:, 0:T], resetm.v(), sigd[:, 0:T], 0.0)
        gt = sigd
        c3 = lambda b_: b_[:, 0:T].re("p (c l) -> p c l", l=128)
        K.act(gt[:, 0:T], csum[:, 0:T], AF.Exp, scale=-CDEC)
        K.tt("dve", AR[:, :, 1, :], c3(xr_r), c3(gt), ALU.mult)
        K.copy("dve", gL.v(), c3(gt)[:, :, 127])
        K.stt(AR[:, :, 0, 1:128], c3(kkn)[:, :, 1:128], -1.0, c3(gt)[:, :, 0:127], ALU.mult, ALU.mult)
        K.ts("dve", AR[:, :, 0, 0], c3(kkn)[:, :, 0], -1.0, None, ALU.mult)
        K.act(gt[:, 0:T], csum[:, 0:T], AF.Exp, scale=CDEC)
        K.tt("dve", bti.v(), bfm[:, 0:T], gt[:, 0:T], ALU.mult)
        K.tt("pool", kti.v(), kmod[:, 0:T], gt[:, 0:T], ALU.mult)
        K.tt("dve", c3(gt), c3(csum)[:, :, 127:128].bc([128, 16, 128]), c3(csum), ALU.subtract)
        K.act(gt[:, 0:T], gt[:, 0:T], AF.Exp, scale=-CDEC)
        K.tt("dve", bhat.v(), bfm[:, 0:T], gt[:, 0:T], ALU.mult)
        K.tt("pool", khat.v(), kmod[:, 0:T], gt[:, 0:T], ALU.mult)
        for c in range(16):
            pb = PSB()
            pv = pb.v().re("p (q t) -> p q t", q=8)
            cs = slice(c * 128, (c + 1) * 128)
            K.tr(pv[:, 0, :], vbf[:, cs], identb.v())
            K.tr(pv[:, 1, :], bhat[:, cs], identb.v())
            K.tr(pv[:, 2, :], khat[:, cs], identb.v())
            K.tr(pv[:, 3, :], AR[:, c, 0, :], identb.v())
            K.copy("act" if c % 2 else "dve", TMq[:, c, :, :], pv[:, 0:4, :])
        sq_src = [xr_r[:, T:TT], kmod[:, T:TT], vs32.v(), decs.v(), avs.v(), bfm[:, T:TT]]
        for half in range(2):
            ps = PSF()
            for j in range(3):
                K.tr(ps[:NS, j * 128:(j + 1) * 128], sq_src[half * 3 + j], identf.v())
            K.copy("dve", smp[:, half * 3:half * 3 + 3, :], ps[:NS, :384].re("p (q f) -> p q f", q=3))
        for q in range(6):
            K.dma("sp", dap(samp_d, (2 * hp) * 448 + q * 64, [[8 * 448, NS], [448, 2], [1, 64]]),
                  smp[:, q, :].re("p (h n) -> p h n", h=2))

        K.memset("dve", Sf.v(), 0.0)
        K.memset("pool", Sb.v(), 0.0)
        pairs = [(c, hl) for c in range(16) for hl in range(2)]

        def pre_a(sl, c, hl):
            rows = slice(hl * 64, (hl + 1) * 64)
            cs = slice(c * 128, (c + 1) * 128)
            ps = PSF()
            arv = AR[rows, c, :, :].re("p a t -> p (a t)")
            K.mm(ps[:, 0:256], bti[rows, cs], arv)
            K.mm(ps[:, 256:512], kti[rows, cs], arv)
            ps2 = PSF()
            K.mm(ps2[:, 0:128], AR[rows, c, 0, :], bti[rows, cs])
            p4 = ps.v().re("p (q t) -> p q t", q=4)
            K.tt("dve", ATMP[sl % NPQ].v(), p4[:, 0::2, :], mask4[:, 0::2, :], ALU.mult)
            K.tt("dve", ABK[sl].v(), p4[:, 1::2, :], mask4[:, 1::2, :], ALU.mult)
            K.tt("dve", P0T[sl % NPQ].v(), ps2[:, 0:128], maskNT.v(), ALU.mult)

        def pre_lev(sl, lev, gi):
            if lev == 0:
                Pj, PjT, Mp = ATMP[sl % NPQ][:, 0, :], P0T[sl % NPQ].v(), identb.v()
            else:
                src = PMb[sl % NPQ][lev % 2]
                Pj, PjT, Mp = src[:, 0, :], src[:, 1, :], src[:, 2, :]
            dst = PMb[sl % NPQ][(lev + 1) % 2]
            ps = PSF()
            if lev < 6:
                K.mm(ps[:, 0:128], PjT, Pj)
                K.mm(ps[:, 128:256], Pj, PjT)
            K.mm(ps[:, 256:384], PjT, Mp, start=True, stop=False)
            K.mm(ps[:, 256:384], identb.v(), Mp, start=False, stop=True)
            eng = "act" if (gi % 4) != 3 else "dve"
            if lev < 6:
                K.copy(eng, dst.v(), ps[:, 0:384].re("p (q t) -> p q t", q=3))
            else:
                K.copy(eng, dst[:, 2, :], ps[:, 256:384])

        def pre_w(sl, c, hl, gi):
            rows = slice(hl * 64, (hl + 1) * 64)
            cs = slice(c * 128, (c + 1) * 128)
            Mfin = PMb[sl % NPQ][1][:, 2, :]
            ps = PSF()
            K.mm(ps[:, 0:128], TMq[:, c, 3, :], Mfin)
            K.mm(ps[:, 128:192], ATMP[sl % NPQ][:, 1, :], TMq[:, c, 0, rows])
            eng = "act" if gi % 2 else "dve"
            K.copy(eng, Wfm[rows, cs], ps[rows, 0:128])
            K.copy(eng, Ybf[sl % NPQ].v(), ps[:, 128:192])

        def pre_u(sl, gi):
            Mfin = PMb[sl % NPQ][1][:, 2, :]
            ps = PSF()
            K.mm(ps[:, 0:64], Mfin, Ybf[sl % NPQ].v())
            K.copy("act" if gi % 2 else "dve", Ubar[sl].v(), ps[:, 0:64])

        def precompute(group):
            th = []
            for (sl, c, hl) in group:
                th.append(partial(pre_a, sl, c, hl))
            for lev in range(7):
                for gi, (sl, c, hl) in enumerate(group):
                    th.append(partial(pre_lev, sl, lev, gi))
            for gi, (sl, c, hl) in enumerate(group):
                th.append(partial(pre_w, sl, c, hl, gi))
            for gi, (sl, c, hl) in enumerate(group):
                th.append(partial(pre_u, sl, gi))
            return th

        def ser_u(sl, c, hl, gi):
            rows = slice(hl * 64, (hl + 1) * 64)
            cs = slice(c * 128, (c + 1) * 128)
            ut = Ut[gi % 4]
            ps = PSF()
            K.mm(ps[:, 0:64], Wfm[rows, cs], Sb[rows, :])
            K.tt("dve", ut.v(), ps[:, 0:64], Ubar[sl].v(), ALU.add)

        def ser_os(sl, c, hl, gi):
            rows = slice(hl * 64, (hl + 1) * 64)
            ut = Ut[gi % 4]
            pso = PSF()
            K.mm(pso[:, 0:64], AR[rows, c, 1, :], Sb[rows, :], start=True, stop=False)
            K.mm(pso[:, 0:64], ABK[sl][:, 0, :], ut.v(), start=False, stop=False)
            K.mm(pso[:, 0:64], ABK[sl][:, 1, :], TMq[:, c, 0, rows], start=False, stop=True)
            pss = PSF()
            K.mm(pss[:, 0:64], TMq[:, c, 1, :], ut.v(), start=True, stop=False)
            K.mm(pss[:, 0:64], TMq[:, c, 2, :], TMq[:, c, 0, rows], start=False, stop=True)
            K.stt(Sf[rows, :], Sf[rows, :], gL[rows, c:c + 1], pss[rows, 0:64], ALU.mult, ALU.add)
            K.copy("act", Sb[rows, :], Sf[rows, :])
            ot = Otile[c % 2]
            K.copy("act", ot[:, rows], pso[:, 0:64])
            if hl == 1:
                K.dma("sp", o_d[c * 128:(c + 1) * 128, hp * 128:(hp + 1) * 128], ot.v())

        def serial(group):
            th = []
            for gi, (sl, c, hl) in enumerate(group):
                th.append(partial(ser_u, sl, c, hl, gi))
                th.append(partial(ser_os, sl, c, hl, gi))
            return th

        def run_merged(lists):
            idx = [0] * len(lists)
            while True:
                best, bf = -1, 2.0
                for li, L in enumerate(lists):
                    if idx[li] < len(L):
                        frac = idx[li] / len(L)
                        if frac < bf:
                            best, bf = li, frac
                if best < 0:
                    break
                lists[best][idx[best]]()
                idx[best] += 1

        G = 8
        groups = []
        for g0 in range(0, 32, G):
            groups.append([((g0 + i) % NPR, pairs[g0 + i][0], pairs[g0 + i][1]) for i in range(G)])
        run_merged([precompute(groups[0])])
        for gi_ in range(len(groups)):
            lists = [serial(groups[gi_])]
            if gi_ + 1 < len(groups):
                lists.append(precompute(groups[gi_ + 1]))
            run_merged(lists)
        ps = PSF()
        K.tr(ps[:64, 0:128], Sf.v(), identf.v())
        K.copy("dve", wkvo.v(), ps[:64, 0:128])
        K.dma("sp", o_wkv_p[2 * hp:2 * hp + 2, :, :].re("h v k -> v h k"), wkvo.v().re("v (h k) -> v h k", h=2))

    K.pop_scope()
    if stage == 3:
        zt = K.sb("zt", [128, 4096], F32)
        K.memset("pool", zt.v(), 0.0)
        K.dma("sp", o_wkv_s.v(), zt[:, :])
        for i in range(16):
            K.dma("sp", y_p[i * 128:(i + 1) * 128, :], zt[:, 0:D])
        K.dma("sp", y_s.v(), zt[0:NS, 0:D])
        return early()
    oaT = K.sb("oaT", [128, 4, TT], BF16)

    K.push_scope()
    eps_ln = K.sb("eps_ln2", [128, 1], F32)
    K.memset("pool", eps_ln.v(), 64e-5)
    lnxg = K.sb("lnxg", [128, RW], F32)
    lnxb = K.sb("lnxb", [128, RW], F32)
    K.dma("sp", lnxg.v(), dap(lnx_g, 0, [[0, 128], [1, RW]]))
    K.dma("sp", lnxb.v(), dap(lnx_b, 0, [[0, 128], [1, RW]]))
    NOR = 3
    ot_r = [K.sb("ot", [128, RW], F32) for _ in range(NOR)]
    bt_r = [K.sb("bt", [128, RW], F32) for _ in range(NOR)]
    tq = [K.sb("tq", [128, RW], F32) for _ in range(NOR)]
    on_r = [K.sb("on", [128, RW], F32) for _ in range(NOR)]
    ofb = [K.sb("ofb", [128, RW], BF16) for _ in range(NOR)]
    st8 = [K.sb("st8", [128, 6, 8], F32) for _ in range(NOR)]
    h3 = lambda b_: b_.v().re("p (h n) -> p h n", h=8)

    def out_a(c):
        r_ = c % NOR
        ot, bt, tmp, s8 = ot_r[r_], bt_r[r_], tq[r_], st8[r_]
        K.dma("sp", ot.v(), o_d[c * 128:(c + 1) * 128, :])
        K.dma("sp", bt.v(), bonus_d[c * 128:(c + 1) * 128, :])
        K.reduce(s8[:, 0, :], h3(ot))
        K.act(tmp.v(), ot.v(), AF.Square)
        K.reduce(s8[:, 1, :], h3(tmp))
        K.ts("dve", s8[:, 2, :], s8[:, 0, :], 1.0 / 64, None, ALU.mult)
        K.tt("dve", s8[:, 3, :], s8[:, 2, :], s8[:, 2, :], ALU.mult)
        K.stt(s8[:, 4, :], s8[:, 1, :], 1.0 / 64, s8[:, 3, :], ALU.mult, ALU.subtract)
        K.act(s8[:, 4, :], s8[:, 4, :], AF.Sqrt, bias=eps_ln.v(), scale=1.0)
        K.recip(s8[:, 5, :], s8[:, 4, :])

    def out_b(c):
        r_ = c % NOR
        ot, bt, on, of, s8 = ot_r[r_], bt_r[r_], on_r[r_], ofb[r_], st8[r_]
        K.tt("pool", h3(on), h3(ot), s8[:, 2, :].un(2).bc([128, 8, 64]), ALU.subtract)
        K.tt("dve", h3(on), h3(on), s8[:, 5, :].un(2).bc([128, 8, 64]), ALU.mult)
        K.tt("pool", on.v(), on.v(), lnxg.v(), ALU.mult)
        K.tt("pool", bt.v(), bt.v(), lnxb.v(), ALU.add)
        K.tt("dve", on.v(), on.v(), bt.v(), ALU.add)
        ps = PSF()
        K.mm(ps.v(), lora_bf[64:128, c * 128:(c + 1) * 128], lora_up[64:128, :])
        K.tt("dve", of.v(), on.v(), ps.v(), ALU.mult)
        pb = PSB()
        pv = pb.v().re("p (q t) -> p q t", q=8)
        for kt in range(4):
            K.tr(pv[:, kt, :], of[:, kt * 128:(kt + 1) * 128], identb.v())
        K.copy("act", oaT[:, :, c * 128:(c + 1) * 128], pv[:, 0:4, :])

    OUT = []
    for c in range(16):
        OUT.append(partial(out_a, c))
        if c >= 1:
            OUT.append(partial(out_b, c - 1))
    OUT.append(partial(out_b, 15))

    gs_tm = K.sb("gs_tm", [NS, RW], F32)
    ps = PSF()
    K.mm(ps[:NS, :], lora_bf[64:128, T:TT], lora_up[64:128, :])
    K.copy("act", gs_tm.v(), ps[:NS, :])
    K.dma("sp", dap(samp_d, 6 * 64, [[8 * 448, NS], [448, 8], [1, 64]]), gs_tm.v().re("p (h n) -> p h n", h=8))
    vec = K.sb("vec", [128, 7, 64], F32)
    K.dma("sp", vec.v().re("p q n -> p (q n)"), dap(samp_d, 0, [[448, 128], [1, 448]]))
    par = K.sb("spar", [128, 3, 64], F32)
    for tk in range(NS):
        for j, src in enumerate((lnx_g, lnx_b, r_k)):
            K.dma("sp", par[tk * 8:(tk + 1) * 8, j, :], dap(src, 0, [[64, 8], [1, 64]]), sem_buf=par)
    S0s = K.sb("S0s", [128, 64, 64], F32)
    S1s = K.sb("S1s", [128, 64, 64], F32)
    tS = K.sb("tS", [128, 64, 64], F32)
    K.dma("sp", S0s.v().re("p v k -> p (v k)"), st_wkv.v())
    r_, km_, v_, dec_, av_, b_, g_ = [vec[:, q, :] for q in range(7)]
    overv = lambda x: x.un(1).bc([128, 64, 64])
    overk = lambda x: x.un(2).bc([128, 64, 64])
    sv = K.sb("sv", [128, 8, 64], F32)
    ss = K.sb("ss", [128, 8], F32)
    eps2 = K.sb("eps_ln3", [128, 1], F32)
    K.memset("pool", eps2.v(), 64e-5)
    os_tm = K.sb("os_tm", [NS, RW], F32)
    os_bf = K.sb("os_bf", [NS, RW], BF16)
    SMP = [
        lambda: K.tt("dve", tS.v(), S0s.v(), overv(av_), ALU.mult),
        lambda: K.reduce(sv[:, 0, :], tS.v()),
        lambda: K.tt("pool", S1s.v(), S0s.v(), overv(dec_), ALU.mult),
        lambda: K.tt("dve", tS.v(), overk(sv[:, 0, :]), overv(b_), ALU.mult),
        lambda: K.tt("pool", S1s.v(), S1s.v(), tS.v(), ALU.add),
        lambda: K.tt("dve", tS.v(), overk(v_), overv(km_), ALU.mult),
        lambda: K.tt("pool", S1s.v(), S1s.v(), tS.v(), ALU.add),
        lambda: K.dma("sp", o_wkv_s.v(), S1s.v().re("p v k -> p (v k)")),
        lambda: K.tt("dve", tS.v(), S1s.v(), overv(r_), ALU.mult),
        lambda: K.reduce(sv[:, 1, :], tS.v()),
        lambda: K.reduce(ss[:, 0:1], sv[:, 1, :]),
        lambda: K.tt("dve", sv[:, 2, :], sv[:, 1, :], sv[:, 1, :], ALU.mult),
        lambda: K.reduce(ss[:, 1:2], sv[:, 2, :]),
        lambda: K.ts("dve", ss[:, 2:3], ss[:, 0:1], 1.0 / 64, None, ALU.mult),
        lambda: K.tt("dve", ss[:, 3:4], ss[:, 2:3], ss[:, 2:3], ALU.mult),
        lambda: K.stt(ss[:, 4:5], ss[:, 1:2], 1.0 / 64, ss[:, 3:4], ALU.mult, ALU.subtract),
        lambda: K.act(ss[:, 4:5], ss[:, 4:5], AF.Sqrt, bias=eps2.v(), scale=1.0),
        lambda: K.recip(ss[:, 5:6], ss[:, 4:5]),
        lambda: K.ts("dve", sv[:, 3, :], sv[:, 1, :], ss[:, 2:3], ss[:, 5:6], ALU.subtract, ALU.mult),
        lambda: K.tt("dve", sv[:, 3, :], sv[:, 3, :], par[:, 0, :], ALU.mult),
        lambda: K.tt("dve", sv[:, 3, :], sv[:, 3, :], par[:, 1, :], ALU.add),
        lambda: K.tt("dve", sv[:, 4, :], r_, km_, ALU.mult),
        lambda: K.tt("dve", sv[:, 4, :], sv[:, 4, :], par[:, 2, :], ALU.mult),
        lambda: K.reduce(ss[:, 6:7], sv[:, 4, :]),
        lambda: K.stt(sv[:, 3, :], v_, ss[:, 6:7], sv[:, 3, :], ALU.mult, ALU.add),
        lambda: K.tt("dve", sv[:, 5, :], sv[:, 3, :], g_, ALU.mult),
        lambda: K.dma("sp", dap(sampo_d, 0, [[64, 128], [1, 64]]), sv[:, 5, :]),
        lambda: K.dma("sp", os_tm.v(), sampo_d.v()),
        lambda: K.copy("dve", os_bf.v(), os_tm.v()),
    ]

    def smp_fin():
        pb = PSB()
        pv = pb.v().re("p (q t) -> p q t", q=8)
        for kt in range(4):
            K.tr(pv[:, kt, :NS], os_bf[:, kt * 128:(kt + 1) * 128], identb[:NS, :NS])
        K.copy("act", oaT[:, :, T:TT], pv[:, 0:4, :NS])

    SMP.append(smp_fin)
    run_merged_g([OUT, SMP])
    K.pop_scope()
    if stage == 4:
        zt = K.sb("zt", [128, 1024], F32)
        K.memset("pool", zt.v(), 0.0)
        for i in range(16):
            K.dma("sp", y_p[i * 128:(i + 1) * 128, :], zt[:, 0:D])
        K.dma("sp", y_s.v(), zt[0:NS, 0:D])
        return early()

    def bcast_row(name, src, n):
        t = K.sb(name, [128, n], F32)
        K.dma("sp", t.v(), dap(src, 0, [[0, 128], [1, n]]))
        return t

    def post_norm_residual(ps_halves, rows, res_src, gpost, dst, bufs, nt=None):
        st, xres, tt_ = bufs[:3]
        for hf in range(2):
            K.act(tt_[:rows, hf * 512:(hf + 1) * 512], ps_halves[hf][:rows, :], AF.Square, accum=st[:rows, hf:hf + 1])
        K.tt("dve", st[:rows, 2:3], st[:rows, 0:1], st[:rows, 1:2], ALU.add)
        K.act(st[:rows, 3:4], st[:rows, 2:3], AF.Sqrt, bias=epsT[:rows, :], scale=1.0 / D)
        K.recip(st[:rows, 4:5], st[:rows, 3:4])
        K.dma("sp", xres[:rows, :], res_src)
        for hf in range(2):
            hs = slice(hf * 512, (hf + 1) * 512)
            K.stt(tt_[:rows, hs], ps_halves[hf][:rows, :], st[:rows, 4:5], gpost[:rows, hs], ALU.mult, ALU.mult)
            K.tt("pool", xres[:rows, hs], xres[:rows, hs], tt_[:rows, hs], ALU.add)
        K.dma("sp", dst, xres[:rows, :])
        if nt is not None:
            gname, dstT, col0 = nt
            xb = bufs[3]
            K.act(tt_[:rows, :], xres[:rows, :], AF.Square, accum=st[:rows, 5:6])
            K.act(st[:rows, 6:7], st[:rows, 5:6], AF.Sqrt, bias=epsT[:rows, :], scale=1.0 / D)
            K.recip(st[:rows, 7:8], st[:rows, 6:7])
            K.act(xb[:rows, :], xres[:rows, :], AF.Copy, scale=st[:rows, 7:8])
            pb = PSB()
            pv = pb.v().re("p (k t) -> p k t", k=8)
            for kt in range(8):
                K.tr(pv[:, kt, :rows], xb[:rows, kt * 128:(kt + 1) * 128], identb[:rows, :rows])
            g0 = PV[gname]
            K.tt("dve", dstT[:, :, col0:col0 + rows], pv[:, :, :rows],
                 pvec[:, g0:g0 + 8].un(2).bc([128, 8, rows]), ALU.mult)

    K.push_scope()
    hgl = K.sb("hgl2", [128, 4, TT], BF16)
    K.dma("sp", hgl.v().re("p k t -> p (k t)"), hgl_d.v())
    mT = K.sb("mT", [128, 8, TT], BF16)
    wmo = K.sb("wmo", [128, 8, D], BF16)
    for kt in range(8):
        K.dma("pool", wmo[:, kt, :], w_merge_out[kt * 128:(kt + 1) * 128, :])
    gpost = bcast_row("gpost_mix", g_post_mix, D)
    mring = [K.sb("mring", [128, 8, 128], BF16) for _ in range(6)]
    mr_i = [0]

    def mload(src, c0, ncols_total, kt_n):
        mr_i[0] += 1
        wb = mring[mr_i[0] % len(mring)]
        K.dma("pool", wb[:, :kt_n, :], dap(src, c0, [[ncols_total, 128], [128 * ncols_total, kt_n], [1, 128]]))
        return wb

    mt = [[K.sb("mt", [128, 512], F32) for _ in range(4)] for _ in range(2)]
    GOFF = SHIFT + RW
    it = 0
    for f in range(8):
        wA = mload(w_rwkv_out, f * 128, D, 4)
        wga = mload(w_in, GOFF + f * 128, PROJ, 8)
        wgb = mload(w_in, GOFF + D + f * 128, PROJ, 8)
        w1 = mload(glu_w1, f * 128, D, 4)
        w2 = mload(glu_w2, f * 128, D, 4)
        for tb in range(5):
            col0, n = TBLK[tb]
            cs = slice(col0, col0 + n)
            t0, t1_, t2_, t3_ = mt[it % 2]
            it += 1
            psA, psga, psgb, ps1, ps2 = PSF(), PSF(), PSF(), PSF(), PSF()
            for kt in range(4):
                K.mm(ps2[:, :n], w2[:, kt, :], hgl[:, kt, cs], start=(kt == 0), stop=(kt == 3))
            for kt in range(4):
                K.mm(ps1[:, :n], w1[:, kt, :], hgl[:, kt, cs], start=(kt == 0), stop=(kt == 3))
            for kt in range(8):
                K.mm(psgb[:, :n], wgb[:, kt, :], hT[:, kt, cs], start=(kt == 0), stop=(kt == 7))
            for kt in range(8):
                K.mm(psga[:, :n], wga[:, kt, :], hT[:, kt, cs], start=(kt == 0), stop=(kt == 7))
            for kt in range(4):
                K.mm(psA[:, :n], wA[:, kt, :], oaT[:, kt, cs], start=(kt == 0), stop=(kt == 3))
            K.act(t0[:, :n], ps2[:, :n], AF.Sigmoid, bias=pcol("glu_b2", f))
            K.stt(t1_[:, :n], ps1[:, :n], pcol("glu_b1", f), t0[:, :n], ALU.add, ALU.mult)
            K.act(t2_[:, :n], psgb[:, :n], AF.Sigmoid)
            K.tt("pool", t1_[:, :n], t1_[:, :n], t2_[:, :n], ALU.mult)
            K.act(t3_[:, :n], psga[:, :n], AF.Sigmoid)
            K.tt("dve", t3_[:, :n], psA[:, :n], t3_[:, :n], ALU.mult)
            K.tt("dve", mT[:, f, cs], t3_[:, :n], t1_[:, :n], ALU.add)
    pn_bufs = [(K.sb("pn_st", [128, 8], F32), K.sb("pn_x", [128, D], F32), K.sb("pn_t", [128, D], F32),
                K.sb("pn_xb", [128, D], BF16)) for _ in range(2)]
    for i in range(17):
        rows = 128 if i < 16 else NS
        tcs = slice(i * 128, i * 128 + rows)
        halves = [PSF(), PSF()]
        for hf in range(2):
            for kt in range(8):
                K.mm(halves[hf][:rows, :], mT[:, kt, tcs], wmo[:, kt, hf * 512:(hf + 1) * 512],
                     start=(kt == 0), stop=(kt == 7))
        post_norm_residual(halves, rows, x_tile(i), gpost, x1_d[i * 128:i * 128 + rows, :], pn_bufs[i % 2],
                           nt=("g_pre_ffn", hT, i * 128))
    K.pop_scope()
    K.pop_scope()

    K.push_scope()
    NF = DFF // 128
    aT = K.sb("aT", [128, NF, TT], BF16)
    wd = K.sb("wd", [128, NF, D], BF16)
    for kt in range(NF):
        K.dma("pool", wd[:, kt, :], w_ffn_down[kt * 128:(kt + 1) * 128, :])
    gpost2 = bcast_row("gpost_ffn", g_post_ffn, D)
    fring = [K.sb("fring", [128, 8, 128], BF16) for _ in range(4)]
    fr_i = [0]

    def fload(src, c0):
        fr_i[0] += 1
        wb = fring[fr_i[0] % len(fring)]
        K.dma("pool", wb.v(), dap(src, c0, [[DFF, 128], [128 * DFF, 8], [1, 128]]))
        return wb

    sgt = [K.sb("sgt", [128, 512], F32) for _ in range(2)]
    it = 0
    for ft in range(NF):
        wg = fload(w_ffn_gate, ft * 128)
        wu = fload(w_ffn_up, ft * 128)
        for tb in range(5):
            col0, n = TBLK[tb]
            cs = slice(col0, col0 + n)
            psg, psu = PSF(), PSF()
            for kt in range(8):
                K.mm(psg[:, :n], wg[:, kt, :], hT[:, kt, cs], start=(kt == 0), stop=(kt == 7))
            for kt in range(8):
                K.mm(psu[:, :n], wu[:, kt, :], hT[:, kt, cs], start=(kt == 0), stop=(kt == 7))
            sg = sgt[it % 2]
            it += 1
            K.act(sg[:, :n], psg[:, :n], AF.Silu)
            K.tt("dve", aT[:, ft, cs], psu[:, :n], sg[:, :n], ALU.mult)
    pn_bufs = [(K.sb("pn_st", [128, 8], F32), K.sb("pn_x", [128, D], F32), K.sb("pn_t", [128, D], F32))
               for _ in range(2)]
    for i in range(17):
        rows = 128 if i < 16 else NS
        tcs = slice(i * 128, i * 128 + rows)
        halves = [PSF(), PSF()]
        for hf in range(2):
            for kt in range(NF):
                K.mm(halves[hf][:rows, :], aT[:, kt, tcs], wd[:, kt, hf * 512:(hf + 1) * 512],
                     start=(kt == 0), stop=(kt == NF - 1))
        dst = y_p[i * 128:(i + 1) * 128, :] if i < 16 else y_s[:, :]
        post_norm_residual(halves, rows, x1_d[i * 128:i * 128 + rows, :], gpost2, dst, pn_bufs[i % 2])
    K.pop_scope()
    K.finish()
    es.close()
    return nc


_W_NAMES = ["norm_pre_mix", "norm_post_mix", "norm_pre_ffn", "norm_post_ffn", "w_in", "mu_shift", "w0",
            "w_decay_up", "a0", "w_aaa_up", "w_gate_up", "k_k", "k_a", "r_k", "lnx_g", "lnx_b", "w_rwkv_out",
            "s5_lam_re", "s5_lam_im", "s5_log_dt", "s5_b_re", "s5_b_im", "s5_c_re", "s5_c_im", "s5_d",
            "glu_w1", "glu_b1", "glu_w2", "glu_b2", "w_merge_out", "w_ffn_gate", "w_ffn_up", "w_ffn_down"]


def make_in_maps(inputs, cores):
    f = lambda a: np.ascontiguousarray(np.asarray(a, dtype=np.float32))
    shared = {n: f(inputs[n])[0] for n in _W_NAMES}
    maps = []
    for c in cores:
        m = dict(shared)
        m["x_p"] = f(inputs["x_prompt"][c])
        m["x_s"] = f(inputs["x_sample"][c * NS:(c + 1) * NS, 0])
        m["st_shift"] = f(inputs["state_shift"][0, c * NS:(c + 1) * NS])
        m["st_wkv"] = f(inputs["state_wkv"][0, c * NS:(c + 1) * NS]).reshape(NS * 8, 64 * 64)
        m["st_re"] = f(inputs["state_s5_re"][0, c * NS:(c + 1) * NS]).reshape(NS, 2048)
        m["st_im"] = f(inputs["state_s5_im"][0, c * NS:(c + 1) * NS]).reshape(NS, 2048)
        maps.append(m)
    return maps


def assemble(results):
    n = len(results)
    cat = lambda k: np.concatenate([np.asarray(r[k]) for r in results], axis=0)
    y_p = np.stack([np.asarray(r["y_p"]) for r in results], 0)
    y_s = cat("y_s").reshape(n * NS, 1, D)
    sh_p = cat("o_shift_p").reshape(1, n, SHIFT)
    wkv_p = np.stack([np.asarray(r["o_wkv_p"]) for r in results], 0).reshape(1, n, 8, 64, 64)
    re_p = cat("o_re_p").reshape(1, n, 32, 64)
    im_p = cat("o_im_p").reshape(1, n, 32, 64)
    sh_s = cat("o_shift_s").reshape(1, n * NS, SHIFT)
    wkv_s = cat("o_wkv_s").reshape(1, n * NS, 8, 64, 64)
    re_s = cat("o_re_s").reshape(1, n * NS, 32, 64)
    im_s = cat("o_im_s").reshape(1, n * NS, 32, 64)
    return tuple(np.ascontiguousarray(a, dtype=np.float32) for a in
                 (y_p, y_s, sh_p, wkv_p, re_p, im_p, sh_s, wkv_s, re_s, im_s))


def kernel(**inputs):
    nc = build()
    in_maps = make_in_maps(inputs, list(range(NCORES)))
    res = run_bass_kernel_spmd(nc, in_maps, core_ids=list(range(NCORES)))
    return assemble(res.results)
```

```python
import math
from contextlib import ExitStack
from functools import partial
import numpy as np
import concourse.bass as bass
import concourse.mybir as mybir
from concourse.bass_utils import run_bass_kernel_spmd

F32 = mybir.dt.float32
BF16 = mybir.dt.bfloat16
I32 = mybir.dt.int32
AF = mybir.ActivationFunctionType
ALU = mybir.AluOpType
AX = mybir.AxisListType

T = 2048
NS = 16
TT = T + NS
D = 1024
RW = 512
SHIFT = 1664
PROJ = 4224
DFF = 2816
NCORES = 8
CDEC = math.exp(-0.5)
TBLK = [(0, 512), (512, 512), (1024, 512), (1536, 512), (2048, 16)]
SAME_ENGINE_SYNC = True


class V:
    __slots__ = ("buf", "ap")

    def __init__(self, buf, ap):
        self.buf = buf
        self.ap = ap

    def __getitem__(self, idx):
        return V(self.buf, self.ap[idx])

    def bc(self, shape):
        return V(self.buf, self.ap.to_broadcast(list(shape)))

    def re(self, s, **kw):
        return V(self.buf, self.ap.rearrange(s, **kw))

    def un(self, axis):
        return V(self.buf, self.ap.unsqueeze(axis))

    def bitcast(self, dt):
        return V(self.buf, self.ap.bitcast(dt))


class Buf:
    def __init__(self, name, handle):
        self.name = name
        self.h = handle
        self.writes = {}
        self.reads = {}
        self.dsem = None
        self.dcnt = 0
        self.is_psum = False

    def __getitem__(self, idx):
        return V(self, self.h[idx])

    def v(self):
        return V(self, self.h[:])


class Kern:
    def __init__(self, nc, es):
        self.nc = nc
        self.es = es
        self.root_es = es
        self.eng = {"pe": nc.tensor, "act": nc.scalar, "dve": nc.vector, "pool": nc.gpsimd, "sp": nc.sync}
        self.sem = {}
        self.cnt = {}
        for e in ("pe", "act", "dve", "pool"):
            self.sem[e] = es.enter_context(nc.semaphore("s_" + e))
            self.cnt[e] = 0
        self.obs = {e: {} for e in self.eng}
        self.semname = {}
        self.dsems = []
        self.nbuf = 0
        self.psf_i = 0
        self.psb_i = 0

    def sb(self, name, shape, dt):
        self.nbuf += 1
        import os
        if os.environ.get("KDBG"):
            sz = int(np.prod(shape[1:])) * (2 if dt == BF16 else 4)
            self._tot = getattr(self, "_tot", 0) + sz
            print(f"alloc {name} {shape} {sz} depth={len(getattr(self, '_saved', []))} cum_noFree={self._tot}")
        h = self.es.enter_context(self.nc.sbuf_tensor(f"{name}_{self.nbuf}", list(shape), dt))
        return Buf(name, h)

    def barrier(self):
        for e in ("pe", "act", "dve", "pool", "sp"):
            for e2 in ("pe", "act", "dve", "pool"):
                if e2 != e and self.cnt[e2]:
                    self._wait(e, e2, self.sem[e2], self.cnt[e2])
            for ob in self.dsems:
                self._wait(e, "d:" + ob.name + str(id(ob)), ob.dsem, ob.dcnt)

    def push_scope(self):
        self._saved = getattr(self, "_saved", [])
        self._saved.append(self.es)
        self.es = ExitStack()

    def pop_scope(self):
        self.barrier()
        self.es.close()
        self.es = self._saved.pop()

    def ps(self, name, shape, dt):
        self.nbuf += 1
        h = self.es.enter_context(self.nc.psum_tensor(f"{name}_{self.nbuf}", list(shape), dt))
        b = Buf(name, h)
        b.is_psum = True
        return b

    def dram(self, name, shape, dt, kind="Internal"):
        h = self.nc.dram_tensor(name, list(shape), dt, kind=kind)
        return Buf(name, h.ap())

    def newsem(self, name):
        s = self.root_es.enter_context(self.nc.semaphore(name))
        return s

    def _wait(self, e, sem_key, sem, val):
        o = self.obs[e]
        if o.get(sem_key, 0) >= val:
            return
        self.eng[e].wait_ge(sem, val)
        o[sem_key] = val

    def _deps(self, e, W, R):
        need = {}
        for b in R:
            for k, (s, v) in b.writes.items():
                if need.get(k, (None, 0))[1] < v:
                    need[k] = (s, v)
            if b.is_psum:
                for k, (s, v) in b.reads.items():
                    if k != e and need.get(k, (None, 0))[1] < v:
                        need[k] = (s, v)
        for b in W:
            for k, (s, v) in b.writes.items():
                if need.get(k, (None, 0))[1] < v:
                    need[k] = (s, v)
            for k, (s, v) in b.reads.items():
                if need.get(k, (None, 0))[1] < v:
                    need[k] = (s, v)
        for k, (s, v) in need.items():
            if k == e:
                if e == "pe" or not SAME_ENGINE_SYNC:
                    continue
            self._wait(e, k, s, v)

    def _bufs(self, vs):
        out = []
        for x in vs:
            if isinstance(x, V):
                if x.buf not in out:
                    out.append(x.buf)
            elif isinstance(x, Buf):
                if x not in out:
                    out.append(x)
        return out

    def op(self, e, fn, W, R):
        Wb = self._bufs(W)
        Rb = self._bufs(R)
        self._deps(e, Wb, Rb)
        inst = fn()
        self.cnt[e] += 1
        c = self.cnt[e]
        inst.then_inc(self.sem[e], 1)
        for b in Wb:
            b.reads = {}
            b.writes[e] = (self.sem[e], c)
        for b in Rb:
            if b not in Wb:
                b.reads[e] = (self.sem[e], c)
        return inst

    def dma(self, q, out, in_, sem_buf=None, slow=False):
        Wb = self._bufs([out])
        Rb = self._bufs([in_])
        self._deps(q, Wb, Rb)
        ob = sem_buf if sem_buf is not None else out.buf
        if ob.dsem is None:
            ob.dsem = self.newsem("d_" + ob.name + str(len(self.dsems)))
            self.dsems.append(ob)
        if slow:
            with self.nc.allow_non_contiguous_dma(reason="small parameter layout load"):
                inst = self.eng[q].dma_start(out=out.ap, in_=in_.ap)
        else:
            inst = self.eng[q].dma_start(out=out.ap, in_=in_.ap)
        ob.dcnt += 16
        inst.then_inc(ob.dsem, 16)
        key = "d:" + ob.name + str(id(ob))
        for b in Wb:
            b.reads = {}
            b.writes[key] = (ob.dsem, ob.dcnt)
        for b in Rb:
            if b not in Wb:
                b.reads[key] = (ob.dsem, ob.dcnt)
        return inst

    def finish(self):
        for ob in self.dsems:
            self._wait("sp", "d:" + ob.name + str(id(ob)), ob.dsem, ob.dcnt)
        for e in ("pe", "act", "dve", "pool"):
            if self.cnt[e]:
                self._wait("sp", e, self.sem[e], self.cnt[e])

    @staticmethod
    def _a(x):
        return x.ap if isinstance(x, V) else x

    def act(self, out, in_, func, bias=None, scale=None, accum=None):
        kw = {}
        if bias is not None:
            kw["bias"] = self._a(bias)
        if scale is not None:
            kw["scale"] = self._a(scale)
        if accum is not None:
            kw["accum_out"] = self._a(accum)
        return self.op("act", lambda: self.nc.scalar.activation(out=out.ap, in_=in_.ap, func=func, **kw),
                       [out, accum], [in_, bias, scale])

    def tt(self, e, out, in0, in1, op):
        return self.op(e, lambda: self.eng[e].tensor_tensor(out=out.ap, in0=in0.ap, in1=in1.ap, op=op),
                       [out], [in0, in1])

    def ts(self, e, out, in0, s1, s2, op0, op1=None):
        if op1 is None:
            return self.op(e, lambda: self.eng[e].tensor_scalar(out=out.ap, in0=in0.ap, scalar1=self._a(s1),
                                                               scalar2=None, op0=op0), [out], [in0, s1])
        return self.op(e, lambda: self.eng[e].tensor_scalar(out=out.ap, in0=in0.ap, scalar1=self._a(s1),
                                                           scalar2=self._a(s2), op0=op0, op1=op1),
                       [out], [in0, s1, s2])

    def stt(self, out, in0, scalar, in1, op0, op1, e="dve"):
        return self.op(e, lambda: self.eng[e].scalar_tensor_tensor(out=out.ap, in0=in0.ap, scalar=self._a(scalar),
                                                                  in1=in1.ap, op0=op0, op1=op1),
                       [out], [in0, scalar, in1])

    def copy(self, e, out, in_):
        if e == "act":
            return self.act(out, in_, AF.Copy)
        return self.op(e, lambda: self.eng[e].tensor_copy(out=out.ap, in_=in_.ap), [out], [in_])

    def memset(self, e, out, val):
        return self.op(e, lambda: self.eng[e].memset(out.ap, val), [out], [])

    def recip(self, out, in_):
        return self.op("dve", lambda: self.nc.vector.reciprocal(out=out.ap, in_=in_.ap), [out], [in_])

    def reduce(self, out, in_, op=ALU.add, axis=AX.X):
        return self.op("dve", lambda: self.nc.vector.tensor_reduce(out=out.ap, in_=in_.ap, axis=axis, op=op),
                       [out], [in_])

    def scan(self, out, d0, d1, init, op0=ALU.mult, op1=ALU.add):
        return self.op("dve", lambda: self.nc.vector.tensor_tensor_scan(out=out.ap, data0=d0.ap, data1=d1.ap,
                                                                       initial=self._a(init), op0=op0, op1=op1),
                       [out], [d0, d1, init])

    def mm(self, out, lhsT, rhs, start=True, stop=True):
        return self.op("pe", lambda: self.nc.tensor.matmul(out.ap, lhsT=lhsT.ap, rhs=rhs.ap, start=start, stop=stop),
                       [out], [lhsT, rhs])

    def tr(self, out, in_, ident):
        return self.op("pe", lambda: self.nc.tensor.transpose(out.ap, in_.ap, ident.ap), [out], [in_, ident])

    def aselect(self, out, in_, pattern, cmp, fill, base, cm):
        return self.op("pool", lambda: self.nc.gpsimd.affine_select(out=out.ap, in_=in_.ap, pattern=pattern,
                                                                   compare_op=cmp, fill=fill, base=base,
                                                                   channel_multiplier=cm), [out], [in_])

    def iota(self, out, pattern, base, cm):
        return self.op("pool", lambda: self.nc.gpsimd.iota(out.ap, pattern=pattern, base=base,
                                                          channel_multiplier=cm), [out], [])


def run_merged_g(lists):
    idx = [0] * len(lists)
    while True:
        best, bf = -1, 2.0
        for li, L in enumerate(lists):
            if idx[li] < len(L):
                frac = idx[li] / len(L)
                if frac < bf:
                    best, bf = li, frac
        if best < 0:
            break
        lists[best][idx[best]]()
        idx[best] += 1


def merge_lists(lists):
    out = []
    idx = [0] * len(lists)
    while True:
        best, bf = -1, 2.0
        for li, L in enumerate(lists):
            if idx[li] < len(L):
                frac = idx[li] / len(L)
                if frac < bf:
                    best, bf = li, frac
        if best < 0:
            break
        out.append(lists[best][idx[best]])
        idx[best] += 1
    return out


def dap(buf, offset, ap):
    base = buf.h if not hasattr(buf.h, "ap") or isinstance(buf.h, bass.AP) else buf.h
    t = base.tensor if isinstance(base, bass.AP) else base
    return V(buf, bass.AP(t, offset, [list(x) for x in ap]))


def build(stage=99):
    nc = bass.Bass("TRN2", target_bir_lowering=False)
    es = ExitStack()
    K = Kern(nc, es)

    def din(name, shape, dt=F32):
        return Buf(name, nc.dram_tensor(name, list(shape), dt, kind="ExternalInput").ap())

    def dout(name, shape):
        return Buf(name, nc.dram_tensor(name, list(shape), F32, kind="ExternalOutput").ap())

    x_p = din("x_p", [T, D])
    x_s = din("x_s", [NS, D])
    st_shift = din("st_shift", [NS, SHIFT])
    st_wkv = din("st_wkv", [NS * 8, 64 * 64])
    st_re = din("st_re", [NS, 2048])
    st_im = din("st_im", [NS, 2048])
    g_pre_mix = din("norm_pre_mix", [D])
    g_post_mix = din("norm_post_mix", [D])
    g_pre_ffn = din("norm_pre_ffn", [D])
    g_post_ffn = din("norm_post_ffn", [D])
    w_in = din("w_in", [D, PROJ])
    mu_shift = din("mu_shift", [SHIFT])
    w0 = din("w0", [RW])
    w_decay_up = din("w_decay_up", [32, RW])
    a0 = din("a0", [RW])
    w_aaa_up = din("w_aaa_up", [32, RW])
    w_gate_up = din("w_gate_up", [64, RW])
    k_k = din("k_k", [RW])
    k_a = din("k_a", [RW])
    r_k = din("r_k", [RW])
    lnx_g = din("lnx_g", [RW])
    lnx_b = din("lnx_b", [RW])
    w_rwkv_out = din("w_rwkv_out", [RW, D])
    s5_lam_re = din("s5_lam_re", [32, 64])
    s5_lam_im = din("s5_lam_im", [32, 64])
    s5_log_dt = din("s5_log_dt", [32])
    s5_b_re = din("s5_b_re", [32, 64, 16])
    s5_b_im = din("s5_b_im", [32, 64, 16])
    s5_c_re = din("s5_c_re", [32, 16, 64])
    s5_c_im = din("s5_c_im", [32, 16, 64])
    s5_d = din("s5_d", [RW])
    glu_w1 = din("glu_w1", [RW, D])
    glu_b1 = din("glu_b1", [D])
    glu_w2 = din("glu_w2", [RW, D])
    glu_b2 = din("glu_b2", [D])
    w_merge_out = din("w_merge_out", [D, D])
    w_ffn_gate = din("w_ffn_gate", [D, DFF])
    w_ffn_up = din("w_ffn_up", [D, DFF])
    w_ffn_down = din("w_ffn_down", [DFF, D])

    y_p = dout("y_p", [T, D])
    y_s = dout("y_s", [NS, D])
    o_shift_p = dout("o_shift_p", [1, SHIFT])
    o_wkv_p = dout("o_wkv_p", [8, 64, 64])
    o_re_p = dout("o_re_p", [1, 2048])
    o_im_p = dout("o_im_p", [1, 2048])
    o_shift_s = dout("o_shift_s", [NS, SHIFT])
    o_wkv_s = dout("o_wkv_s", [NS * 8, 64 * 64])
    o_re_s = dout("o_re_s", [NS, 2048])
    o_im_s = dout("o_im_s", [NS, 2048])

    x1_d = K.dram("x1_scratch", [TT, D], F32)

    psf = [K.ps("psf", [128, 512], F32) for _ in range(6)]
    psb = [K.ps("psb", [128, 1024], BF16) for _ in range(2)]

    def PSF():
        K.psf_i += 1
        return psf[K.psf_i % len(psf)]

    def PSB():
        K.psb_i += 1
        return psb[K.psb_i % len(psb)]

    ones_f = K.sb("ones_f", [128, 128], F32)
    identf = K.sb("identf", [128, 128], F32)
    identb = K.sb("identb", [128, 128], BF16)
    epsT = K.sb("epsT", [128, 1], F32)
    K.memset("pool", ones_f.v(), 1.0)
    K.memset("pool", epsT.v(), 1e-6)
    K.aselect(identf.v(), ones_f.v(), [[-1, 128]], ALU.is_equal, 0.0, 0, 1)
    K.copy("pool", identb.v(), identf.v())

    pvec = K.sb("pvec", [128, 128], F32)
    PV = {}
    _pv = [0]

    def load_fm(name, src, ntile):
        c0 = _pv[0]
        _pv[0] += ntile
        K.dma("sp", pvec[:, c0:c0 + ntile], dap(src, 0, [[1, 128], [128, ntile]]), sem_buf=pvec, slow=True)
        PV[name] = c0
        return c0

    load_fm("g_pre_mix", g_pre_mix, 8)
    load_fm("g_pre_ffn", g_pre_ffn, 8)
    load_fm("mu", mu_shift, 13)
    load_fm("w0", w0, 4)
    load_fm("a0", a0, 4)
    load_fm("k_k", k_k, 4)
    load_fm("k_a", k_a, 4)
    load_fm("r_k", r_k, 4)
    load_fm("s5_d", s5_d, 4)
    load_fm("glu_b1", glu_b1, 8)
    load_fm("glu_b2", glu_b2, 8)

    def early():
        while getattr(K, "_saved", []):
            K.pop_scope()
        K.finish()
        es.close()
        return nc

    if stage == 0.1:
        return early()

    def pcol(name, i=0):
        c = PV[name] + i
        return pvec[:, c:c + 1]

    hT = K.sb("hT", [128, 8, TT], BF16)
    def norm_transpose(src_tile_fn, gname, dst):
        K.push_scope()
        NB = 3
        xring = [K.sb("xring", [128, D], F32) for _ in range(NB)]
        xn = [K.sb("xn", [128, D], BF16) for _ in range(NB)]
        junk = K.sb("junk", [128, D], BF16)
        stat = [K.sb("stat", [128, 4], F32) for _ in range(NB)]

        def nt_a(i):
            rows = 128 if i < 16 else NS
            xt, st, xb = xring[i % NB], stat[i % NB], xn[i % NB]
            K.dma("sp", xt[:rows, :], src_tile_fn(i))
            K.act(junk[:rows, :], xt[:rows, :], AF.Square, accum=st[:rows, 0:1])
            K.act(st[:rows, 1:2], st[:rows, 0:1], AF.Sqrt, bias=epsT[:rows, :], scale=1.0 / D)
            K.recip(st[:rows, 2:3], st[:rows, 1:2])
            K.act(xb[:rows, :], xt[:rows, :], AF.Copy, scale=st[:rows, 2:3])

        def nt_b(i):
            rows = 128 if i < 16 else NS
            col0 = i * 128
            xb = xn[i % NB]
            pb = PSB()
            pv = pb.v().re("p (k t) -> p k t", k=8)
            for kt in range(8):
                K.tr(pv[:, kt, :rows], xb[:rows, kt * 128:(kt + 1) * 128], identb[:rows, :rows])
            g0 = PV[gname]
            K.tt("dve", dst[:, :, col0:col0 + rows], pv[:, :, :rows],
                 pvec[:, g0:g0 + 8].un(2).bc([128, 8, rows]), ALU.mult)

        for step in range(17 + 1):
            if step < 17:
                nt_a(step)
            if step >= 1:
                nt_b(step - 1)
        K.pop_scope()

    def x_tile(i):
        return x_p[i * 128:(i + 1) * 128, :] if i < 16 else x_s[:, :]

    norm_transpose(x_tile, "g_pre_mix", hT)
    if stage == 0.2:
        return early()

    wring = [K.sb("wring", [128, 8, 128], BF16) for _ in range(3)]
    wr_i = [0]

    def load_w_cols(src, c0, ncols_total, kt_n=8, width=128):
        wr_i[0] += 1
        wb = wring[wr_i[0] % len(wring)]
        K.dma("pool", wb[:, :kt_n, :width],
              dap(src, c0, [[ncols_total, 128], [128 * ncols_total, kt_n], [1, width]]))
        return wb

    def proj_fm(wb, kt_n, src_act, tb, ps):
        col0, n = TBLK[tb]
        for kt in range(kt_n):
            K.mm(ps[:, :n], wb[:, kt, :], src_act[:, kt, col0:col0 + n], start=(kt == 0), stop=(kt == kt_n - 1))

    hgl_d = K.dram("hgl_scratch", [128, 4 * TT], BF16)
    TWO_PI = 2.0 * math.pi

    K.push_scope()
    hgl = K.sb("hgl", [128, 4, TT], BF16)
    ubf = K.sb("ubf", [128, 4, TT], BF16)
    wu4 = [K.sb("wu4", [128, 8, 128], BF16) for _ in range(4)]
    for ft in range(4):
        K.dma("pool", wu4[ft].v(), dap(w_in, SHIFT + ft * 128, [[PROJ, 128], [128 * PROJ, 8], [1, 128]]))
        for tb in range(5):
            col0, n = TBLK[tb]
            ps = PSF()
            proj_fm(wu4[ft], 8, hT, tb, ps)
            K.copy("act" if tb % 2 else "dve", ubf[:, ft, col0:col0 + n], ps[:, :n])

    if stage == 1.1:
        return early()
    sp_ = K.sb("s5par", [128, 16, 16], F32)
    P_LRE, P_LIM, P_DT, P_MAG, P_TH, P_FRE, P_FIM, P_LBRE, P_LBIM, P_T0, P_T1, P_T2, P_T3 = range(13)

    def sp(k):
        return sp_[:, :, k]

    K.dma("sp", sp(P_LRE), dap(s5_lam_re, 0, [[1, 128], [128, 16]]), slow=True)
    K.dma("sp", sp(P_LIM), dap(s5_lam_im, 0, [[1, 128], [128, 16]]), slow=True)
    for gl in range(2):
        K.dma("sp", sp_[gl * 64:(gl + 1) * 64, :, P_DT], dap(s5_log_dt, gl, [[0, 64], [2, 16]]), slow=True)
    K.act(sp(P_DT), sp(P_DT), AF.Exp)
    K.tt("dve", sp(P_T0), sp(P_LRE), sp(P_DT), ALU.mult)
    K.act(sp(P_MAG), sp(P_T0), AF.Exp)
    K.tt("dve", sp(P_TH), sp(P_LIM), sp(P_DT), ALU.mult)

    if stage == 1.15:
        return early()
    LC = 128
    cosT = K.sb("cosT", [128, 16, LC], F32)
    sinT = K.sb("sinT", [128, 16, LC], F32)
    preR = K.sb("preR", [128, 16, LC], F32)
    preI = K.sb("preI", [128, 16, LC], F32)
    K.push_scope()
    ji = K.sb("ji", [128, LC], I32)
    jf = K.sb("jf", [128, LC], F32)
    ang = K.sb("ang", [128, 16, LC], F32)
    rr = K.sb("rr", [128, 16, LC], F32)
    kf = K.sb("kf", [128, 16, LC], F32)
    ki = K.sb("ki", [128, 16, LC], I32)
    K.iota(ji.v(), [[1, LC]], 1, 0)
    K.copy("dve", jf.v(), ji.v())
    K.tt("dve", ang.v(), sp(P_TH).un(2).bc([128, 16, LC]), jf.v().un(1).bc([128, 16, LC]), ALU.mult)

    def sin_reduced(out, a_in, shift):
        C1 = 6.28125
        C2 = TWO_PI - C1
        K.ts("dve", rr.v(), a_in, shift, 1.0 / TWO_PI, ALU.add, ALU.mult)
        K.copy("dve", ki.v(), rr.v())
        K.copy("dve", kf.v(), ki.v())
        K.ts("dve", rr.v(), a_in, shift, None, ALU.add)
        K.stt(rr.v(), kf.v(), -C1, rr.v(), ALU.mult, ALU.add)
        K.stt(rr.v(), kf.v(), -C2, rr.v(), ALU.mult, ALU.add)
        K.ts("dve", kf.v(), rr.v(), math.pi, None, ALU.is_gt)
        K.stt(rr.v(), kf.v(), -TWO_PI, rr.v(), ALU.mult, ALU.add)
        K.ts("dve", kf.v(), rr.v(), -math.pi, None, ALU.is_lt)
        K.stt(rr.v(), kf.v(), TWO_PI, rr.v(), ALU.mult, ALU.add)
        K.ts("dve", rr.v(), rr.v(), -math.pi, math.pi, ALU.max, ALU.min)
        K.act(out, rr.v(), AF.Sin)

    sin_reduced(sinT.v(), ang.v(), 0.0)
    sin_reduced(cosT.v(), ang.v(), 0.5 * math.pi)
    K.tt("dve", sp(P_LBRE), sp(P_MAG), cosT[:, :, 0], ALU.mult)
    K.tt("dve", sp(P_LBIM), sp(P_MAG), sinT[:, :, 0], ALU.mult)
    K.tt("dve", sp(P_T0), sp(P_LRE), sp(P_LRE), ALU.mult)
    K.tt("dve", sp(P_T1), sp(P_LIM), sp(P_LIM), ALU.mult)
    K.tt("dve", sp(P_T0), sp(P_T0), sp(P_T1), ALU.add)
    K.recip(sp(P_T0), sp(P_T0))
    K.ts("dve", sp(P_T1), sp(P_LBRE), -1.0, None, ALU.add)
    K.tt("dve", sp(P_T2), sp(P_T1), sp(P_LRE), ALU.mult)
    K.tt("dve", sp(P_T3), sp(P_LBIM), sp(P_LIM), ALU.mult)
    K.tt("dve", sp(P_T2), sp(P_T2), sp(P_T3), ALU.add)
    K.tt("dve", sp(P_FRE), sp(P_T2), sp(P_T0), ALU.mult)
    K.tt("dve", sp(P_T2), sp(P_LBIM), sp(P_LRE), ALU.mult)
    K.tt("dve", sp(P_T3), sp(P_T1), sp(P_LIM), ALU.mult)
    K.tt("dve", sp(P_T2), sp(P_T2), sp(P_T3), ALU.subtract)
    K.tt("dve", sp(P_FIM), sp(P_T2), sp(P_T0), ALU.mult)
    fre_b = sp(P_FRE).un(2).bc([128, 16, LC])
    fim_b = sp(P_FIM).un(2).bc([128, 16, LC])
    K.tt("dve", rr.v(), cosT.v(), fre_b, ALU.mult)
    K.tt("dve", kf.v(), sinT.v(), fim_b, ALU.mult)
    K.tt("dve", preR.v(), rr.v(), kf.v(), ALU.add)
    K.tt("dve", rr.v(), cosT.v(), fim_b, ALU.mult)
    K.tt("dve", kf.v(), sinT.v(), fre_b, ALU.mult)
    K.tt("dve", preI.v(), rr.v(), kf.v(), ALU.subtract)
    K.pop_scope()

    if stage == 1.2:
        return early()
    BT = K.sb("BT", [128, 32, 128], BF16)
    CTb = K.sb("CTb", [128, 32, 128], BF16)
    K.push_scope()
    for part, src in enumerate((s5_b_re, s5_b_im)):
        X = K.sb("Xb", [128, 16 * 128], F32)
        K.memset("pool", X.v(), 0.0)
        for gl in range(2):
            for a in range(4):
                K.dma("sp", dap(X, gl * 64 * 2048 + 16 * gl + a * 512, [[2048, 64], [160, 4], [1, 16]]),
                      dap(src, gl * 1024 + a * 8192, [[16, 64], [2048, 4], [1, 16]]), slow=True)
        for i4 in range(4):
            ps = PSF()
            for j in range(4):
                i = i4 * 4 + j
                K.tr(ps[:, j * 128:(j + 1) * 128], X[:, i * 128:(i + 1) * 128], identf.v())
            for j in range(4):
                i = i4 * 4 + j
                K.copy("act" if j % 2 else "dve", BT[:, 2 * i + part, :], ps[:, j * 128:(j + 1) * 128])
    for part, src in enumerate((s5_c_re, s5_c_im)):
        Z = K.sb("Zc", [128, 4 * 512], F32)
        K.memset("pool", Z.v(), 0.0)
        for g8 in range(8):
            K.dma("sp", dap(Z, 16 * g8 * 2048 + 64 * g8, [[2048, 16], [512, 4], [1, 64]]),
                  dap(src, g8 * 1024, [[64, 16], [8192, 4], [1, 64]]))
        for q in range(4):
            ps = PSF()
            for j in range(4):
                K.tr(ps[:, j * 128:(j + 1) * 128], Z[:, q * 512 + j * 128:q * 512 + (j + 1) * 128], identf.v())
            i0 = q * 4
            dst = CTb.v().re("p (i two) c -> p i two c", two=2)[:, i0:i0 + 4, part, :]
            if part == 0:
                K.copy("dve", dst, ps.v().re("p (j c) -> p j c", j=4))
            else:
                K.ts("dve", dst, ps.v().re("p (j c) -> p j c", j=4), -1.0, None, ALU.mult)
    K.pop_scope()

    if stage == 1.3:
        return early()
    x1b = [K.sb("x1sb", [128, 16, NS], BF16) for _ in range(2)]
    K.push_scope()
    s0_tm = [K.sb("s0tm", [NS, 2048], F32) for _ in range(2)]
    s0 = [K.sb("s0", [128, 16, NS], F32) for _ in range(2)]
    K.dma("sp", s0_tm[0].v(), st_re.v())
    K.dma("sp", s0_tm[1].v(), st_im.v())
    for part in range(2):
        ps = PSF()
        for i in range(16):
            K.tr(ps[:, i * NS:(i + 1) * NS], s0_tm[part][:, i * 128:(i + 1) * 128], identf[:NS, :NS])
        K.copy("dve", s0[part].v(), ps[:, :16 * NS].re("p (i n) -> p i n", i=16))
    psr = PSF()
    psi = PSF()
    for i in range(16):
        K.mm(psr[:, i * NS:(i + 1) * NS], BT[:, 2 * i, :], ubf[:, i // 4, T:TT])
        K.mm(psi[:, i * NS:(i + 1) * NS], BT[:, 2 * i + 1, :], ubf[:, i // 4, T:TT])
    sw = [K.sb("sw", [128, 16, NS], F32) for _ in range(4)]
    x1 = [K.sb("x1s", [128, 16, NS], F32) for _ in range(2)]
    bc16 = lambda k: sp(k).un(2).bc([128, 16, NS])
    pr3 = psr[:, :16 * NS].re("p (i n) -> p i n", i=16)
    pi3 = psi[:, :16 * NS].re("p (i n) -> p i n", i=16)
    K.tt("dve", sw[0].v(), pr3, bc16(P_FRE), ALU.mult)
    K.tt("dve", sw[1].v(), pi3, bc16(P_FIM), ALU.mult)
    K.tt("dve", sw[2].v(), sw[0].v(), sw[1].v(), ALU.subtract)
    K.tt("dve", sw[0].v(), pi3, bc16(P_FRE), ALU.mult)
    K.tt("dve", sw[1].v(), pr3, bc16(P_FIM), ALU.mult)
    K.tt("dve", sw[3].v(), sw[0].v(), sw[1].v(), ALU.add)
    K.tt("dve", sw[0].v(), s0[0].v(), bc16(P_LBRE), ALU.mult)
    K.tt("dve", sw[1].v(), s0[1].v(), bc16(P_LBIM), ALU.mult)
    K.tt("dve", sw[0].v(), sw[0].v(), sw[1].v(), ALU.subtract)
    K.tt("dve", x1[0].v(), sw[0].v(), sw[2].v(), ALU.add)
    K.tt("dve", sw[0].v(), s0[1].v(), bc16(P_LBRE), ALU.mult)
    K.tt("dve", sw[1].v(), s0[0].v(), bc16(P_LBIM), ALU.mult)
    K.tt("dve", sw[0].v(), sw[0].v(), sw[1].v(), ALU.add)
    K.tt("dve", x1[1].v(), sw[0].v(), sw[3].v(), ALU.add)
    for part in range(2):
        K.copy("act", x1b[part].v(), x1[part].v())
        xs_tm = s0_tm[part]
        for i4 in range(4):
            ps = PSF()
            for j in range(4):
                i = i4 * 4 + j
                K.tr(ps[:NS, j * 128:(j + 1) * 128], x1[part][:, i, :], identf.v())
            K.copy("dve", xs_tm[:, i4 * 512:(i4 + 1) * 512], ps[:NS, :])
        K.dma("sp", (o_re_s if part == 0 else o_im_s).v(), xs_tm.v())
    K.pop_scope()

    if stage == 1.4:
        return early()
    carry = K.sb("carry", [128, 16, 2], F32)
    K.memset("dve", carry.v(), 0.0)
    UN = [dict(bR=K.sb("bR", [128, 512], F32), bI=K.sb("bI", [128, 512], F32),
               wR=K.sb("wR", [128, 512], F32), wI=K.sb("wI", [128, 512], F32),
               t1=K.sb("t1", [128, 512], F32), t2=K.sb("t2", [128, 512], F32)) for _ in range(4)]
    xbr = [K.sb("xbr", [128, 512], BF16) for _ in range(4)]
    xbi = [K.sb("xbi", [128, 512], BF16) for _ in range(4)]
    yv = [K.sb("yv", [128, 512], F32) for _ in range(2)]
    y2 = [K.sb("y2", [128, 512], F32) for _ in range(2)]

    def gelu_to(dst_bf, yv_, y2_, n):
        K.act(y2_[:, :n], yv_[:, :n], AF.Square)
        K.act(y2_[:, :n], y2_[:, :n], AF.Identity, bias=1.0, scale=0.044715)
        K.tt("pool", y2_[:, :n], y2_[:, :n], yv_[:, :n], ALU.mult)
        K.act(y2_[:, :n], y2_[:, :n], AF.Sigmoid, scale=1.5957691216057308)
        K.tt("pool", dst_bf, y2_[:, :n], yv_[:, :n], ALU.mult)

    v4 = lambda b: b.v().re("p (a l) -> p a l", a=4)
    for q in range(4):
        for tb in range(4):
            col0, n = TBLK[tb]
            for j in range(4):
                i = q * 4 + j
                U = UN[j]
                psr = PSF()
                psi = PSF()
                K.mm(psr.v(), BT[:, 2 * i, :], ubf[:, q, col0:col0 + n])
                K.mm(psi.v(), BT[:, 2 * i + 1, :], ubf[:, q, col0:col0 + n])
                tR = preR[:, i, :].un(1).bc([128, 4, LC])
                tI = preI[:, i, :].un(1).bc([128, 4, LC])
                K.tt("dve", v4(U["t1"]), v4(psr), tR, ALU.mult)
                K.tt("dve", v4(U["t2"]), v4(psi), tI, ALU.mult)
                K.tt("pool", U["bR"].v(), U["t1"].v(), U["t2"].v(), ALU.subtract)
                K.tt("dve", v4(U["wR"]), v4(psi), tR, ALU.mult)
                K.tt("dve", v4(U["wI"]), v4(psr), tI, ALU.mult)
                K.tt("dve", U["bI"].v(), U["wR"].v(), U["wI"].v(), ALU.add)
            for a in range(4):
                cs = slice(a * LC, (a + 1) * LC)
                for j in range(4):
                    i = q * 4 + j
                    U = UN[j]
                    rho = sp_[:, i, P_MAG:P_MAG + 1].bc([128, LC])
                    if a == 0:
                        ire, iim = carry[:, i, 0:1], carry[:, i, 1:2]
                    else:
                        ire, iim = U["bR"][:, a * LC - 1:a * LC], U["bI"][:, a * LC - 1:a * LC]
                    K.scan(U["wR"][:, cs], rho, U["bR"][:, cs], ire)
                    K.scan(U["wI"][:, cs], rho, U["bI"][:, cs], iim)
                for j in range(4):
                    i = q * 4 + j
                    U = UN[j]
                    K.tt("pool", U["t1"][:, cs], U["wR"][:, cs], cosT[:, i, :], ALU.mult)
                    K.tt("dve", U["t2"][:, cs], U["wI"][:, cs], sinT[:, i, :], ALU.mult)
                    K.tt("dve", U["bR"][:, cs], U["t1"][:, cs], U["t2"][:, cs], ALU.subtract)
                    K.tt("pool", U["t1"][:, cs], U["wR"][:, cs], sinT[:, i, :], ALU.mult)
                    K.tt("dve", U["t2"][:, cs], U["wI"][:, cs], cosT[:, i, :], ALU.mult)
                    K.tt("dve", U["bI"][:, cs], U["t1"][:, cs], U["t2"][:, cs], ALU.add)
            for j in range(4):
                i = q * 4 + j
                U = UN[j]
                K.copy("dve", carry[:, i, 0:1], U["bR"][:, 511:512])
                K.copy("dve", carry[:, i, 1:2], U["bI"][:, 511:512])
                K.copy("act", xbr[j].v(), U["bR"].v())
                K.copy("act", xbi[j].v(), U["bI"].v())
            psy = PSF()
            for j in range(4):
                i = q * 4 + j
                K.mm(psy.v(), CTb[:, 2 * i, :], xbr[j].v(), start=(j == 0), stop=False)
                K.mm(psy.v(), CTb[:, 2 * i + 1, :], xbi[j].v(), start=False, stop=(j == 3))
            yy = yv[tb % 2]
            K.copy("act", yy.v(), psy.v())
            psu = PSF()
            proj_fm(wu4[q], 8, hT, tb, psu)
            K.stt(yy.v(), psu.v(), pcol("s5_d", q), yy.v(), ALU.mult, ALU.add)
            gelu_to(hgl[:, q, col0:col0 + n], yy, y2[tb % 2], n)
        psy = PSF()
        for j in range(4):
            i = q * 4 + j
            K.mm(psy[:, :NS], CTb[:, 2 * i, :], x1b[0][:, i, :], start=(j == 0), stop=False)
            K.mm(psy[:, :NS], CTb[:, 2 * i + 1, :], x1b[1][:, i, :], start=False, stop=(j == 3))
        K.copy("act", yv[0][:, :NS], psy[:, :NS])
        psu = PSF()
        proj_fm(wu4[q], 8, hT, 4, psu)
        K.stt(yv[0][:, :NS], psu[:, :NS], pcol("s5_d", q), yv[0][:, :NS], ALU.mult, ALU.add)
        gelu_to(hgl[:, q, T:TT], yv[0], y2[0], NS)
    if stage == 1.5:
        return early()
    K.dma("sp", dap(o_re_p, 0, [[1, 128], [128, 16]]), carry[:, :, 0], slow=True)
    K.dma("sp", dap(o_im_p, 0, [[1, 128], [128, 16]]), carry[:, :, 1], slow=True)
    K.dma("sp", hgl_d.v(), hgl.v().re("p k t -> p (k t)"))
    K.pop_scope()
    if stage == 2:
        zt = K.sb("zt", [128, 4096], F32)
        K.memset("pool", zt.v(), 0.0)
        K.dma("sp", o_shift_p.v(), zt[0:1, 0:SHIFT])
        K.dma("sp", o_shift_s.v(), zt[0:NS, 0:SHIFT])
        K.dma("sp", o_wkv_p.v().re("h v k -> h (v k)"), zt[0:8, :])
        K.dma("sp", o_wkv_s.v(), zt[:, :])
        for i in range(16):
            K.dma("sp", y_p[i * 128:(i + 1) * 128, :], zt[:, 0:D])
        K.dma("sp", y_s.v(), zt[0:NS, 0:D])
        return early()
    K.push_scope()
    lora_bf = K.sb("lora_bf", [128, TT], BF16)
    lora_up = K.sb("lora_up", [128, RW], BF16)
    o_d = K.dram("o_scratch", [T, RW], F32)
    bonus_d = K.dram("bonus_scratch", [T, RW], F32)
    samp_d = K.dram("samp_scratch", [NS * 8, 7, 64], F32)
    sampo_d = K.dram("sampo_scratch", [NS, RW], F32)

    K.push_scope()
    mask4 = K.sb("mask4", [128, 4, 128], F32)
    maskNT = K.sb("maskNT", [128, 128], F32)
    resetm = K.sb("resetm", [128, T], BF16)
    blk1 = K.sb("blk1", [128, 128], F32)
    eps_ln = K.sb("eps_ln", [128, 1], F32)
    K.memset("pool", eps_ln.v(), 64e-5)
    for j in range(4):
        K.aselect(mask4[:, j, :], ones_f.v(), [[1, 128]], ALU.is_gt if j % 2 == 0 else ALU.is_ge, 0.0, 0, -1)
    K.aselect(maskNT.v(), ones_f.v(), [[-1, 128]], ALU.is_gt, 0.0, 0, 1)
    K.memset("pool", resetm.v(), 1.0)
    K.memset("pool", resetm.v().re("p (c l) -> p c l", l=128)[:, :, 0:1], 0.0)
    K.memset("pool", blk1.v(), 0.0)
    K.memset("pool", blk1[0:64, 0:64], 1.0)
    K.memset("pool", blk1[64:128, 64:128], 1.0)

    K.dma("pool", lora_up[0:32, :], w_decay_up.v())
    K.dma("pool", lora_up[32:64, :], w_aaa_up.v())
    K.dma("pool", lora_up[64:128, :], w_gate_up.v())
    sh0T = K.sb("sh0T", [128, 13, NS], F32)
    shout = [K.sb("shout", [NS + 1, 128], F32) for _ in range(2)]
    K.push_scope()
    sh_tm = K.sb("sh_tm", [NS, SHIFT], F32)
    K.dma("sp", sh_tm.v(), st_shift.v())
    ps = PSF()
    for ft in range(13):
        K.tr(ps[:, ft * NS:(ft + 1) * NS], sh_tm[:, ft * 128:(ft + 1) * 128], identf[:NS, :NS])
    K.copy("dve", sh0T.v(), ps[:, :13 * NS].re("p (f n) -> p f n", f=13))
    K.pop_scope()

    omm = K.sb("omm", [128, 13], F32)
    K.ts("dve", omm.v(), pvec[:, PV["mu"]:PV["mu"] + 13], -1.0, 1.0, ALU.mult, ALU.add)
    omka = K.sb("omka", [128, 4], F32)
    K.ts("dve", omka.v(), pvec[:, PV["k_a"]:PV["k_a"] + 4], -1.0, 1.0, ALU.mult, ALU.add)

    SC = [K.sb("scr", [128, TT], F32) for _ in range(6)]

    def proj_shift_tile(ft, pr, tmp, xr_out, xr_out_dt_bf=None, samp32=None):
        wb = load_w_cols(w_in, ft * 128, PROJ)
        for tb in range(5):
            col0, n = TBLK[tb]
            ps = PSF()
            proj_fm(wb, 8, hT, tb, ps)
            K.copy("act", pr[:, col0:col0 + n], ps[:, :n])
            K.act(tmp[:, col0:col0 + n], ps[:, :n], AF.Copy, scale=omm[:, ft:ft + 1])
        ps = PSF()
        K.tr(ps[:NS + 1, :128], pr[:, T - 1:TT], identf.v())
        so = shout[ft % 2]
        K.copy("dve", so.v(), ps[:NS + 1, :128])
        K.dma("sp", o_shift_p[0:1, ft * 128:(ft + 1) * 128], so[0:1, :])
        K.dma("sp", o_shift_s[:, ft * 128:(ft + 1) * 128], so[1:NS + 1, :])
        mu_c = pcol("mu", ft)
        K.stt(xr_out[:, 1:T], pr[:, 0:T - 1], mu_c, tmp[:, 1:T], ALU.mult, ALU.add)
        K.copy("pool", xr_out[:, 0:1], tmp[:, 0:1])
        K.stt(xr_out[:, T:TT], sh0T[:, ft, :], mu_c, tmp[:, T:TT], ALU.mult, ALU.add)
        if samp32 is not None:
            K.stt(samp32, sh0T[:, ft, :], mu_c, tmp[:, T:TT], ALU.mult, ALU.add)

    proj_shift_tile(12, SC[0], SC[1], SC[2].v())
    K.act(lora_bf[0:32, :], SC[2][0:32, :], AF.Tanh)
    K.act(lora_bf[32:64, :], SC[2][32:64, :], AF.Copy)
    K.act(lora_bf[64:128, :], SC[2][64:128, :], AF.Sigmoid)

    vbf = K.sb("vbf", [128, TT], BF16)
    vs32 = K.sb("vs32", [128, NS], F32)
    AR = K.sb("AR", [128, 16, 2, 128], BF16)
    bti = K.sb("bti", [128, T], BF16)
    kti = K.sb("kti", [128, T], BF16)
    bhat = K.sb("bhat", [128, T], BF16)
    khat = K.sb("khat", [128, T], BF16)
    Wfm = K.sb("Wfm", [128, T], BF16)
    TMq = K.sb("TMq", [128, 16, 4, 128], BF16)
    gL = K.sb("gL", [128, 16], F32)
    Sf = K.sb("Sf", [128, 64], F32)
    Sb = K.sb("Sb", [128, 64], BF16)
    smp = K.sb("smp", [NS, 6, 128], F32)
    decs = K.sb("decs", [128, NS], F32)
    avs = K.sb("avs", [128, NS], F32)
    NPR = 16
    ABK = [K.sb("ABK", [128, 2, 128], BF16) for _ in range(NPR)]
    NPQ = 8
    ATMP = [K.sb("ATMP", [128, 2, 128], BF16) for _ in range(NPQ)]
    P0T = [K.sb("P0T", [128, 128], BF16) for _ in range(NPQ)]
    PMb = [[K.sb("PMb", [128, 3, 128], BF16) for _ in range(2)] for _ in range(NPQ)]
    Ybf = [K.sb("Ybf", [128, 64], BF16) for _ in range(NPQ)]
    Ubar = [K.sb("Ubar", [128, 64], F32) for _ in range(NPR)]
    Ut = [K.sb("Ut", [128, 64], BF16) for _ in range(4)]
    Otile = [K.sb("Otile", [128, 128], F32) for _ in range(2)]
    btile = [K.sb("btile", [128, 512], F32) for _ in range(2)]
    wkvo = K.sb("wkvo", [64, 128], F32)

    def derivedA(hp):
        xr_r, xr_k = SC[0], SC[1]
        proj_shift_tile(0 * 4 + hp, SC[2], SC[3], xr_r.v())
        yield
        proj_shift_tile(1 * 4 + hp, SC[2], SC[3], xr_k.v())
        yield
        proj_shift_tile(2 * 4 + hp, SC[2], SC[3], vbf.v(), samp32=vs32.v())
        yield
        sigd, csum, afm, kkn = SC[2], SC[3], SC[4], SC[5]
        hc = slice(hp * 128, (hp + 1) * 128)
        for tb in range(5):
            col0, n = TBLK[tb]
            ps = PSF()
            K.mm(ps[:, :n], lora_up[0:32, hc], lora_bf[0:32, col0:col0 + n])
            K.act(sigd[:, col0:col0 + n], ps[:, :n], AF.Sigmoid, bias=pcol("w0", hp))
            yield
            ps = PSF()
            K.mm(ps[:, :n], lora_up[32:64, hc], lora_bf[32:64, col0:col0 + n])
            K.act(afm[:, col0:col0 + n], ps[:, :n], AF.Sigmoid, bias=pcol("a0", hp))
            yield
        K.act(kkn.v(), xr_k.v(), AF.Square, scale=pcol("k_k", hp))
        yield
        for tb in range(5):
            col0, n = TBLK[tb]
            ps = PSF()
            K.mm(ps[:, :n], blk1.v(), kkn[:, col0:col0 + n])
            K.ts("dve", csum[:, col0:col0 + n], ps[:, :n], 1e-24, None, ALU.max)
            yield
        K.act(csum.v(), csum.v(), AF.Ln)
        yield
        K.act(csum.v(), csum.v(), AF.Exp, scale=-0.5)
        yield
        K.stt(kkn.v(), xr_k.v(), pcol("k_k", hp), csum.v(), ALU.mult, ALU.mult)
        yield
        K.ts("pool", csum.v(), afm.v(), pcol("k_a", hp), omka[:, hp:hp + 1], ALU.mult, ALU.add)
        yield
        K.tt("pool", xr_k.v(), xr_k.v(), csum.v(), ALU.mult)
        yield
        kmod = xr_k
        K.tt("pool", afm.v(), kkn.v(), afm.v(), ALU.mult)
        yield
        bfm = afm
        K.stt(csum.v(), xr_r.v(), pcol("r_k", hp), kmod.v(), ALU.mult, ALU.mult)
        yield
        for tb in range(4):
            col0, n = TBLK[tb]
            ps = PSF()
            K.mm(ps[:, :n], blk1.v(), csum[:, col0:col0 + n])
            K.tt("dve", csum[:, col0:col0 + n], ps[:, :n], vbf[:, col0:col0 + n], ALU.mult)
            yield
        for c4 in range(4):
            ps = PSF()
            for j in range(4):
                c = c4 * 4 + j
                K.tr(ps[:, j * 128:(j + 1) * 128], csum[:, c * 128:(c + 1) * 128], identf.v())
            bt = btile[c4 % 2]
            K.copy("act", bt.v(), ps.v())
            yield
            K.dma("sp", dap(bonus_d, c4 * 512 * RW + hp * 128, [[RW, 128], [128 * RW, 4], [1, 128]]),
                  bt.v().re("p (j f) -> p j f", j=4))
            yield
        K.act(decs.v(), sigd[:, T:TT], AF.Exp, scale=-CDEC)
        yield
        K.ts("dve", avs.v(), kkn[:, T:TT], -1.0, None, ALU.mult)
        yield
        yield

    def derivedB(hp):
        xr_r, xr_k = SC[0], SC[1]
        sigd, csum, afm, kkn = SC[2], SC[3], SC[4], SC[5]
        kmod, bfm = xr_k, afm
        K.scan(csum[:, 0:T], resetm.v(), sigd[:, 0:T], 0.0)
        gt = sigd
        c3 = lambda b_: b_[:, 0:T].re("p (c l) -> p c l", l=128)
        K.act(gt[:, 0:T], csum[:, 0:T], AF.Exp, scale=-CDEC)
        K.tt("dve", AR[:, :, 1, :], c3(xr_r), c3(gt), ALU.mult)
        K.copy("dve", gL.v(), c3(gt)[:, :, 127])
        K.stt(AR[:, :, 0, 1:128], c3(kkn)[:, :, 1:128], -1.0, c3(gt)[:, :, 0:127], ALU.mult, ALU.mult)
        K.ts("dve", AR[:, :, 0, 0], c3(kkn)[:, :, 0], -1.0, None, ALU.mult)
        K.act(gt[:, 0:T], csum[:, 0:T], AF.Exp, scale=CDEC)
        K.tt("dve", bti.v(), bfm[:, 0:T], gt[:, 0:T], ALU.mult)
        K.tt("pool", kti.v(), kmod[:, 0:T], gt[:, 0:T], ALU.mult)
        K.tt("dve", c3(gt), c3(csum)[:, :, 127:128].bc([128, 16, 128]), c3(csum), ALU.subtract)
        K.act(gt[:, 0:T], gt[:, 0:T], AF.Exp, scale=-CDEC)
        K.tt("dve", bhat.v(), bfm[:, 0:T], gt[:, 0:T], ALU.mult)
        K.tt("pool", khat.v(), kmod[:, 0:T], gt[:, 0:T], ALU.mult)
        for c in range(16):
            pb = PSB()
            pv = pb.v().re("p (q t) -> p q t", q=8)
            cs = slice(c * 128, (c + 1) * 128)
            K.tr(pv[:, 0, :], vbf[:, cs], identb.v())
            K.tr(pv[:, 1, :], bhat[:, cs], identb.v())
            K.tr(pv[:, 2, :], khat[:, cs], identb.v())
            K.tr(pv[:, 3, :], AR[:, c, 0, :], identb.v())
            K.copy("act" if c % 2 else "dve", TMq[:, c, :, :], pv[:, 0:4, :])
        sq_src = [xr_r[:, T:TT], kmod[:, T:TT], vs32.v(), decs.v(), avs.v(), bfm[:, T:TT]]
        for half in range(2):
            ps = PSF()
            for j in range(3):
                K.tr(ps[:NS, j * 128:(j + 1) * 128], sq_src[half * 3 + j], identf.v())
            K.copy("dve", smp[:, half * 3:half * 3 + 3, :], ps[:NS, :384].re("p (q f) -> p q f", q=3))
        for q in range(6):
            K.dma("sp", dap(samp_d, (2 * hp) * 448 + q * 64, [[8 * 448, NS], [448, 2], [1, 64]]),
                  smp[:, q, :].re("p (h n) -> p h n", h=2))


    def chunk_thunks(hp):
        CH = []
        CH.append(lambda: K.memset('dve', Sf.v(), 0.0))
        CH.append(lambda: K.memset('pool', Sb.v(), 0.0))
        pairs = [(c, hl) for c in range(16) for hl in range(2)]

        def pre_a(sl, c, hl):
            rows = slice(hl * 64, (hl + 1) * 64)
            cs = slice(c * 128, (c + 1) * 128)
            ps = PSF()
            arv = AR[rows, c, :, :].re("p a t -> p (a t)")
            K.mm(ps[:, 0:256], bti[rows, cs], arv)
            K.mm(ps[:, 256:512], kti[rows, cs], arv)
            ps2 = PSF()
            K.mm(ps2[:, 0:128], AR[rows, c, 0, :], bti[rows, cs])
            p4 = ps.v().re("p (q t) -> p q t", q=4)
            K.tt("dve", ATMP[sl % NPQ].v(), p4[:, 0::2, :], mask4[:, 0::2, :], ALU.mult)
            K.tt("dve", ABK[sl].v(), p4[:, 1::2, :], mask4[:, 1::2, :], ALU.mult)
            K.tt("dve", P0T[sl % NPQ].v(), ps2[:, 0:128], maskNT.v(), ALU.mult)

        def pre_lev(sl, lev, gi):
            if lev == 0:
                Pj, PjT, Mp = ATMP[sl % NPQ][:, 0, :], P0T[sl % NPQ].v(), identb.v()
            else:
                src = PMb[sl % NPQ][lev % 2]
                Pj, PjT, Mp = src[:, 0, :], src[:, 1, :], src[:, 2, :]
            dst = PMb[sl % NPQ][(lev + 1) % 2]
            ps = PSF()
            if lev < 6:
                K.mm(ps[:, 0:128], PjT, Pj)
                K.mm(ps[:, 128:256], Pj, PjT)
            K.mm(ps[:, 256:384], PjT, Mp, start=True, stop=False)
            K.mm(ps[:, 256:384], identb.v(), Mp, start=False, stop=True)
            eng = "act" if (gi % 4) != 3 else "dve"
            if lev < 6:
                K.copy(eng, dst.v(), ps[:, 0:384].re("p (q t) -> p q t", q=3))
            else:
                K.copy(eng, dst[:, 2, :], ps[:, 256:384])

        def pre_w(sl, c, hl, gi):
            rows = slice(hl * 64, (hl + 1) * 64)
            cs = slice(c * 128, (c + 1) * 128)
            Mfin = PMb[sl % NPQ][1][:, 2, :]
            ps = PSF()
            K.mm(ps[:, 0:128], TMq[:, c, 3, :], Mfin)
            K.mm(ps[:, 128:192], ATMP[sl % NPQ][:, 1, :], TMq[:, c, 0, rows])
            eng = "act" if gi % 2 else "dve"
            K.copy(eng, Wfm[rows, cs], ps[rows, 0:128])
            K.copy(eng, Ybf[sl % NPQ].v(), ps[:, 128:192])

        def pre_u(sl, gi):
            Mfin = PMb[sl % NPQ][1][:, 2, :]
            ps = PSF()
            K.mm(ps[:, 0:64], Mfin, Ybf[sl % NPQ].v())
            K.copy("act" if gi % 2 else "dve", Ubar[sl].v(), ps[:, 0:64])

        def precompute(group):
            th = []
            for (sl, c, hl) in group:
                th.append(partial(pre_a, sl, c, hl))
            for lev in range(7):
                for gi, (sl, c, hl) in enumerate(group):
                    th.append(partial(pre_lev, sl, lev, gi))
            for gi, (sl, c, hl) in enumerate(group):
                th.append(partial(pre_w, sl, c, hl, gi))
            for gi, (sl, c, hl) in enumerate(group):
                th.append(partial(pre_u, sl, gi))
            return th

        def ser_u(sl, c, hl, gi):
            rows = slice(hl * 64, (hl + 1) * 64)
            cs = slice(c * 128, (c + 1) * 128)
            ut = Ut[gi % 4]
            ps = PSF()
            K.mm(ps[:, 0:64], Wfm[rows, cs], Sb[rows, :])
            K.tt("dve", ut.v(), ps[:, 0:64], Ubar[sl].v(), ALU.add)

        def ser_os(sl, c, hl, gi):
            rows = slice(hl * 64, (hl + 1) * 64)
            ut = Ut[gi % 4]
            pso = PSF()
            K.mm(pso[:, 0:64], AR[rows, c, 1, :], Sb[rows, :], start=True, stop=False)
            K.mm(pso[:, 0:64], ABK[sl][:, 0, :], ut.v(), start=False, stop=False)
            K.mm(pso[:, 0:64], ABK[sl][:, 1, :], TMq[:, c, 0, rows], start=False, stop=True)
            pss = PSF()
            K.mm(pss[:, 0:64], TMq[:, c, 1, :], ut.v(), start=True, stop=False)
            K.mm(pss[:, 0:64], TMq[:, c, 2, :], TMq[:, c, 0, rows], start=False, stop=True)
            K.stt(Sf[rows, :], Sf[rows, :], gL[rows, c:c + 1], pss[rows, 0:64], ALU.mult, ALU.add)
            K.copy("act", Sb[rows, :], Sf[rows, :])
            ot = Otile[c % 2]
            K.copy("act", ot[:, rows], pso[:, 0:64])
            if hl == 1:
                K.dma("sp", o_d[c * 128:(c + 1) * 128, hp * 128:(hp + 1) * 128], ot.v())

        def serial(group):
            th = []
            for gi, (sl, c, hl) in enumerate(group):
                th.append(partial(ser_u, sl, c, hl, gi))
                th.append(partial(ser_os, sl, c, hl, gi))
            return th

        def run_merged(lists):
            idx = [0] * len(lists)
            while True:
                best, bf = -1, 2.0
                for li, L in enumerate(lists):
                    if idx[li] < len(L):
                        frac = idx[li] / len(L)
                        if frac < bf:
                            best, bf = li, frac
                if best < 0:
                    break
                lists[best][idx[best]]()
                idx[best] += 1

        G = 8
        groups = []
        for g0 in range(0, 32, G):
            groups.append([((g0 + i) % NPR, pairs[g0 + i][0], pairs[g0 + i][1]) for i in range(G)])
        CH.extend(precompute(groups[0]))
        for gi_ in range(len(groups)):
            lists = [serial(groups[gi_])]
            if gi_ + 1 < len(groups):
                lists.append(precompute(groups[gi_ + 1]))
            CH.extend(merge_lists(lists))
        def fin_state():
            ps = PSF()
            K.tr(ps[:64, 0:128], Sf.v(), identf.v())
            K.copy("dve", wkvo.v(), ps[:64, 0:128])
            K.dma("sp", o_wkv_p[2 * hp:2 * hp + 2, :, :].re("h v k -> v h k"), wkvo.v().re("v (h k) -> v h k", h=2))

        CH.append(fin_state)
        return CH

    def drain(gen):
        for _ in gen:
            pass

    NA_EST = 150
    drain(derivedA(0))
    derivedB(0)
    for hp in range(4):
        CH = chunk_thunks(hp)
        if hp + 1 < 4:
            gen = derivedA(hp + 1)
            per = max(1, len(CH) // NA_EST)
            alive = True
            for ti, th in enumerate(CH):
                th()
                if alive and ti % per == per - 1:
                    try:
                        next(gen)
                    except StopIteration:
                        alive = False
            if alive:
                drain(gen)
            derivedB(hp + 1)
        else:
            for th in CH:
                th()

    K.pop_scope()
    if stage == 3:
        zt = K.sb("zt", [128, 4096], F32)
        K.memset("pool", zt.v(), 0.0)
        K.dma("sp", o_wkv_s.v(), zt[:, :])
        for i in range(16):
            K.dma("sp", y_p[i * 128:(i + 1) * 128, :], zt[:, 0:D])
        K.dma("sp", y_s.v(), zt[0:NS, 0:D])
        return early()
    oaT = K.sb("oaT", [128, 4, TT], BF16)

    K.push_scope()
    eps_ln = K.sb("eps_ln2", [128, 1], F32)
    K.memset("pool", eps_ln.v(), 64e-5)
    lnxg = K.sb("lnxg", [128, RW], F32)
    lnxb = K.sb("lnxb", [128, RW], F32)
    K.dma("sp", lnxg.v(), dap(lnx_g, 0, [[0, 128], [1, RW]]))
    K.dma("sp", lnxb.v(), dap(lnx_b, 0, [[0, 128], [1, RW]]))
    NOR = 3
    ot_r = [K.sb("ot", [128, RW], F32) for _ in range(NOR)]
    bt_r = [K.sb("bt", [128, RW], F32) for _ in range(NOR)]
    tq = [K.sb("tq", [128, RW], F32) for _ in range(NOR)]
    on_r = [K.sb("on", [128, RW], F32) for _ in range(NOR)]
    ofb = [K.sb("ofb", [128, RW], BF16) for _ in range(NOR)]
    st8 = [K.sb("st8", [128, 6, 8], F32) for _ in range(NOR)]
    h3 = lambda b_: b_.v().re("p (h n) -> p h n", h=8)

    def out_a(c):
        r_ = c % NOR
        ot, bt, tmp, s8 = ot_r[r_], bt_r[r_], tq[r_], st8[r_]
        K.dma("sp", ot.v(), o_d[c * 128:(c + 1) * 128, :])
        K.dma("sp", bt.v(), bonus_d[c * 128:(c + 1) * 128, :])
        K.reduce(s8[:, 0, :], h3(ot))
        K.act(tmp.v(), ot.v(), AF.Square)
        K.reduce(s8[:, 1, :], h3(tmp))
        K.ts("dve", s8[:, 2, :], s8[:, 0, :], 1.0 / 64, None, ALU.mult)
        K.tt("dve", s8[:, 3, :], s8[:, 2, :], s8[:, 2, :], ALU.mult)
        K.stt(s8[:, 4, :], s8[:, 1, :], 1.0 / 64, s8[:, 3, :], ALU.mult, ALU.subtract)
        K.act(s8[:, 4, :], s8[:, 4, :], AF.Sqrt, bias=eps_ln.v(), scale=1.0)
        K.recip(s8[:, 5, :], s8[:, 4, :])

    def out_b(c):
        r_ = c % NOR
        ot, bt, on, of, s8 = ot_r[r_], bt_r[r_], on_r[r_], ofb[r_], st8[r_]
        K.tt("pool", h3(on), h3(ot), s8[:, 2, :].un(2).bc([128, 8, 64]), ALU.subtract)
        K.tt("dve", h3(on), h3(on), s8[:, 5, :].un(2).bc([128, 8, 64]), ALU.mult)
        K.tt("pool", on.v(), on.v(), lnxg.v(), ALU.mult)
        K.tt("pool", bt.v(), bt.v(), lnxb.v(), ALU.add)
        K.tt("dve", on.v(), on.v(), bt.v(), ALU.add)
        ps = PSF()
        K.mm(ps.v(), lora_bf[64:128, c * 128:(c + 1) * 128], lora_up[64:128, :])
        K.tt("dve", of.v(), on.v(), ps.v(), ALU.mult)
        pb = PSB()
        pv = pb.v().re("p (q t) -> p q t", q=8)
        for kt in range(4):
            K.tr(pv[:, kt, :], of[:, kt * 128:(kt + 1) * 128], identb.v())
        K.copy("act", oaT[:, :, c * 128:(c + 1) * 128], pv[:, 0:4, :])

    OUT = []
    for c in range(16):
        OUT.append(partial(out_a, c))
        if c >= 1:
            OUT.append(partial(out_b, c - 1))
    OUT.append(partial(out_b, 15))

    gs_tm = K.sb("gs_tm", [NS, RW], F32)
    ps = PSF()
    K.mm(ps[:NS, :], lora_bf[64:128, T:TT], lora_up[64:128, :])
    K.copy("act", gs_tm.v(), ps[:NS, :])
    K.dma("sp", dap(samp_d, 6 * 64, [[8 * 448, NS], [448, 8], [1, 64]]), gs_tm.v().re("p (h n) -> p h n", h=8))
    vec = K.sb("vec", [128, 7, 64], F32)
    K.dma("sp", vec.v().re("p q n -> p (q n)"), dap(samp_d, 0, [[448, 128], [1, 448]]))
    par = K.sb("spar", [128, 3, 64], F32)
    for tk in range(NS):
        for j, src in enumerate((lnx_g, lnx_b, r_k)):
            K.dma("sp", par[tk * 8:(tk + 1) * 8, j, :], dap(src, 0, [[64, 8], [1, 64]]), sem_buf=par)
    S0s = K.sb("S0s", [128, 64, 64], F32)
    S1s = K.sb("S1s", [128, 64, 64], F32)
    tS = K.sb("tS", [128, 64, 64], F32)
    K.dma("sp", S0s.v().re("p v k -> p (v k)"), st_wkv.v())
    r_, km_, v_, dec_, av_, b_, g_ = [vec[:, q, :] for q in range(7)]
    overv = lambda x: x.un(1).bc([128, 64, 64])
    overk = lambda x: x.un(2).bc([128, 64, 64])
    sv = K.sb("sv", [128, 8, 64], F32)
    ss = K.sb("ss", [128, 8], F32)
    eps2 = K.sb("eps_ln3", [128, 1], F32)
    K.memset("pool", eps2.v(), 64e-5)
    os_tm = K.sb("os_tm", [NS, RW], F32)
    os_bf = K.sb("os_bf", [NS, RW], BF16)
    SMP = [
        lambda: K.tt("dve", tS.v(), S0s.v(), overv(av_), ALU.mult),
        lambda: K.reduce(sv[:, 0, :], tS.v()),
        lambda: K.tt("pool", S1s.v(), S0s.v(), overv(dec_), ALU.mult),
        lambda: K.tt("dve", tS.v(), overk(sv[:, 0, :]), overv(b_), ALU.mult),
        lambda: K.tt("pool", S1s.v(), S1s.v(), tS.v(), ALU.add),
        lambda: K.tt("dve", tS.v(), overk(v_), overv(km_), ALU.mult),
        lambda: K.tt("pool", S1s.v(), S1s.v(), tS.v(), ALU.add),
        lambda: K.dma("sp", o_wkv_s.v(), S1s.v().re("p v k -> p (v k)")),
        lambda: K.tt("dve", tS.v(), S1s.v(), overv(r_), ALU.mult),
        lambda: K.reduce(sv[:, 1, :], tS.v()),
        lambda: K.reduce(ss[:, 0:1], sv[:, 1, :]),
        lambda: K.tt("dve", sv[:, 2, :], sv[:, 1, :], sv[:, 1, :], ALU.mult),
        lambda: K.reduce(ss[:, 1:2], sv[:, 2, :]),
        lambda: K.ts("dve", ss[:, 2:3], ss[:, 0:1], 1.0 / 64, None, ALU.mult),
        lambda: K.tt("dve", ss[:, 3:4], ss[:, 2:3], ss[:, 2:3], ALU.mult),
        lambda: K.stt(ss[:, 4:5], ss[:, 1:2], 1.0 / 64, ss[:, 3:4], ALU.mult, ALU.subtract),
        lambda: K.act(ss[:, 4:5], ss[:, 4:5], AF.Sqrt, bias=eps2.v(), scale=1.0),
        lambda: K.recip(ss[:, 5:6], ss[:, 4:5]),
        lambda: K.ts("dve", sv[:, 3, :], sv[:, 1, :], ss[:, 2:3], ss[:, 5:6], ALU.subtract, ALU.mult),
        lambda: K.tt("dve", sv[:, 3, :], sv[:, 3, :], par[:, 0, :], ALU.mult),
        lambda: K.tt("dve", sv[:, 3, :], sv[:, 3, :], par[:, 1, :], ALU.add),
        lambda: K.tt("dve", sv[:, 4, :], r_, km_, ALU.mult),
        lambda: K.tt("dve", sv[:, 4, :], sv[:, 4, :], par[:, 2, :], ALU.mult),
        lambda: K.reduce(ss[:, 6:7], sv[:, 4, :]),
        lambda: K.stt(sv[:, 3, :], v_, ss[:, 6:7], sv[:, 3, :], ALU.mult, ALU.add),
        lambda: K.tt("dve", sv[:, 5, :], sv[:, 3, :], g_, ALU.mult),
        lambda: K.dma("sp", dap(sampo_d, 0, [[64, 128], [1, 64]]), sv[:, 5, :]),
        lambda: K.dma("sp", os_tm.v(), sampo_d.v()),
        lambda: K.copy("dve", os_bf.v(), os_tm.v()),
    ]

    def smp_fin():
        pb = PSB()
        pv = pb.v().re("p (q t) -> p q t", q=8)
        for kt in range(4):
            K.tr(pv[:, kt, :NS], os_bf[:, kt * 128:(kt + 1) * 128], identb[:NS, :NS])
        K.copy("act", oaT[:, :, T:TT], pv[:, 0:4, :NS])

    SMP.append(smp_fin)
    run_merged_g([OUT, SMP])
    K.pop_scope()
    if stage == 4:
        zt = K.sb("zt", [128, 1024], F32)
        K.memset("pool", zt.v(), 0.0)
        for i in range(16):
            K.dma("sp", y_p[i * 128:(i + 1) * 128, :], zt[:, 0:D])
        K.dma("sp", y_s.v(), zt[0:NS, 0:D])
        return early()

    def bcast_row(name, src, n):
        t = K.sb(name, [128, n], F32)
        K.dma("sp", t.v(), dap(src, 0, [[0, 128], [1, n]]))
        return t

    def pn_b(ps_halves, rows, res_src, gpost, dst, bufs):
        st, xres, tt_ = bufs[:3]
        K.dma("sp", xres[:rows, :], res_src)
        for hf in range(2):
            K.act(tt_[:rows, hf * 512:(hf + 1) * 512], ps_halves[hf][:rows, :], AF.Square, accum=st[:rows, hf:hf + 1])
        K.tt("dve", st[:rows, 2:3], st[:rows, 0:1], st[:rows, 1:2], ALU.add)
        K.act(st[:rows, 3:4], st[:rows, 2:3], AF.Sqrt, bias=epsT[:rows, :], scale=1.0 / D)
        K.recip(st[:rows, 4:5], st[:rows, 3:4])
        for hf in range(2):
            hs = slice(hf * 512, (hf + 1) * 512)
            K.stt(tt_[:rows, hs], ps_halves[hf][:rows, :], st[:rows, 4:5], gpost[:rows, hs], ALU.mult, ALU.mult)
        K.tt("dve", xres[:rows, :], xres[:rows, :], tt_[:rows, :], ALU.add)
        K.dma("sp", dst, xres[:rows, :])

    def pn_c(rows, bufs, nt):
        st, xres, tt_, xb = bufs
        gname, dstT, col0 = nt
        K.act(tt_[:rows, :], xres[:rows, :], AF.Square, accum=st[:rows, 5:6])
        K.act(st[:rows, 6:7], st[:rows, 5:6], AF.Sqrt, bias=epsT[:rows, :], scale=1.0 / D)
        K.recip(st[:rows, 7:8], st[:rows, 6:7])
        K.act(xb[:rows, :], xres[:rows, :], AF.Copy, scale=st[:rows, 7:8])
        pb = PSB()
        pv = pb.v().re("p (k t) -> p k t", k=8)
        for kt in range(8):
            K.tr(pv[:, kt, :rows], xb[:rows, kt * 128:(kt + 1) * 128], identb[:rows, :rows])
        g0 = PV[gname]
        K.tt("dve", dstT[:, :, col0:col0 + rows], pv[:, :, :rows],
             pvec[:, g0:g0 + 8].un(2).bc([128, 8, rows]), ALU.mult)

    K.push_scope()
    hgl = K.sb("hgl2", [128, 4, TT], BF16)
    K.dma("sp", hgl.v().re("p k t -> p (k t)"), hgl_d.v())
    mT = K.sb("mT", [128, 8, TT], BF16)
    wmo = K.sb("wmo", [128, 8, D], BF16)
    for kt in range(8):
        K.dma("pool", wmo[:, kt, :], w_merge_out[kt * 128:(kt + 1) * 128, :])
    gpost = bcast_row("gpost_mix", g_post_mix, D)
    mring = [K.sb("mring", [128, 8, 128], BF16) for _ in range(6)]
    mr_i = [0]

    def mload(src, c0, ncols_total, kt_n):
        mr_i[0] += 1
        wb = mring[mr_i[0] % len(mring)]
        K.dma("pool", wb[:, :kt_n, :], dap(src, c0, [[ncols_total, 128], [128 * ncols_total, kt_n], [1, 128]]))
        return wb

    mt = [[K.sb("mt", [128, 512], F32) for _ in range(4)] for _ in range(2)]
    GOFF = SHIFT + RW
    it = 0
    for f in range(8):
        wA = mload(w_rwkv_out, f * 128, D, 4)
        wga = mload(w_in, GOFF + f * 128, PROJ, 8)
        wgb = mload(w_in, GOFF + D + f * 128, PROJ, 8)
        w1 = mload(glu_w1, f * 128, D, 4)
        w2 = mload(glu_w2, f * 128, D, 4)
        for tb in range(5):
            col0, n = TBLK[tb]
            cs = slice(col0, col0 + n)
            t0, t1_, t2_, t3_ = mt[it % 2]
            it += 1
            psA, psga, psgb, ps1, ps2 = PSF(), PSF(), PSF(), PSF(), PSF()
            for kt in range(4):
                K.mm(ps2[:, :n], w2[:, kt, :], hgl[:, kt, cs], start=(kt == 0), stop=(kt == 3))
            for kt in range(4):
                K.mm(ps1[:, :n], w1[:, kt, :], hgl[:, kt, cs], start=(kt == 0), stop=(kt == 3))
            for kt in range(8):
                K.mm(psgb[:, :n], wgb[:, kt, :], hT[:, kt, cs], start=(kt == 0), stop=(kt == 7))
            for kt in range(8):
                K.mm(psga[:, :n], wga[:, kt, :], hT[:, kt, cs], start=(kt == 0), stop=(kt == 7))
            for kt in range(4):
                K.mm(psA[:, :n], wA[:, kt, :], oaT[:, kt, cs], start=(kt == 0), stop=(kt == 3))
            K.act(t0[:, :n], ps2[:, :n], AF.Sigmoid, bias=pcol("glu_b2", f))
            K.stt(t1_[:, :n], ps1[:, :n], pcol("glu_b1", f), t0[:, :n], ALU.add, ALU.mult)
            K.act(t2_[:, :n], psgb[:, :n], AF.Sigmoid)
            K.tt("pool", t1_[:, :n], t1_[:, :n], t2_[:, :n], ALU.mult)
            K.act(t3_[:, :n], psga[:, :n], AF.Sigmoid)
            K.tt("dve", t3_[:, :n], psA[:, :n], t3_[:, :n], ALU.mult)
            K.tt("dve", mT[:, f, cs], t3_[:, :n], t1_[:, :n], ALU.add)
    pn_bufs = [(K.sb("pn_st", [128, 8], F32), K.sb("pn_x", [128, D], F32), K.sb("pn_t", [128, D], F32),
                K.sb("pn_xb", [128, D], BF16)) for _ in range(3)]
    mix_ps = {}

    def mix_a(i):
        rows = 128 if i < 16 else NS
        tcs = slice(i * 128, i * 128 + rows)
        halves = [PSF(), PSF()]
        for hf in range(2):
            for kt in range(8):
                K.mm(halves[hf][:rows, :], mT[:, kt, tcs], wmo[:, kt, hf * 512:(hf + 1) * 512],
                     start=(kt == 0), stop=(kt == 7))
        mix_ps[i] = halves

    def mix_b(i):
        rows = 128 if i < 16 else NS
        pn_b(mix_ps[i], rows, x_tile(i), gpost, x1_d[i * 128:i * 128 + rows, :], pn_bufs[i % 3])

    def mix_c(i):
        rows = 128 if i < 16 else NS
        pn_c(rows, pn_bufs[i % 3], ("g_pre_ffn", hT, i * 128))

    for step in range(17 + 2):
        if step < 17:
            mix_a(step)
        if 0 <= step - 1 < 17:
            mix_b(step - 1)
        if 0 <= step - 2 < 17:
            mix_c(step - 2)
    K.pop_scope()
    K.pop_scope()

    K.push_scope()
    NF = DFF // 128
    aT = K.sb("aT", [128, NF, TT], BF16)
    wd = K.sb("wd", [128, NF, D], BF16)
    gpost2 = bcast_row("gpost_ffn", g_post_ffn, D)
    fring = [K.sb("fring", [128, 8, 128], BF16) for _ in range(4)]
    fr_i = [0]

    def fload(src, c0):
        fr_i[0] += 1
        wb = fring[fr_i[0] % len(fring)]
        K.dma("pool", wb.v(), dap(src, c0, [[DFF, 128], [128 * DFF, 8], [1, 128]]))
        return wb

    sgt = [K.sb("sgt", [128, 512], F32) for _ in range(2)]
    it = 0
    for ft in range(NF):
        wg = fload(w_ffn_gate, ft * 128)
        wu = fload(w_ffn_up, ft * 128)
        if ft >= 1:
            K.dma("pool", wd[:, ft - 1, :], w_ffn_down[(ft - 1) * 128:ft * 128, :])
        if ft == NF - 1:
            K.dma("pool", wd[:, ft, :], w_ffn_down[ft * 128:(ft + 1) * 128, :])
        for tb in range(5):
            col0, n = TBLK[tb]
            cs = slice(col0, col0 + n)
            psg, psu = PSF(), PSF()
            for kt in range(8):
                K.mm(psg[:, :n], wg[:, kt, :], hT[:, kt, cs], start=(kt == 0), stop=(kt == 7))
            for kt in range(8):
                K.mm(psu[:, :n], wu[:, kt, :], hT[:, kt, cs], start=(kt == 0), stop=(kt == 7))
            sg = sgt[it % 2]
            it += 1
            K.act(sg[:, :n], psg[:, :n], AF.Silu)
            K.tt("dve", aT[:, ft, cs], psu[:, :n], sg[:, :n], ALU.mult)
    pn_bufs = [(K.sb("pn_st", [128, 8], F32), K.sb("pn_x", [128, D], F32), K.sb("pn_t", [128, D], F32))
               for _ in range(2)]
    dn_ps = {}

    def dn_a(i):
        rows = 128 if i < 16 else NS
        tcs = slice(i * 128, i * 128 + rows)
        halves = [PSF(), PSF()]
        for hf in range(2):
            for kt in range(NF):
                K.mm(halves[hf][:rows, :], aT[:, kt, tcs], wd[:, kt, hf * 512:(hf + 1) * 512],
                     start=(kt == 0), stop=(kt == NF - 1))
        dn_ps[i] = halves

    def dn_b(i):
        rows = 128 if i < 16 else NS
        dst = y_p[i * 128:(i + 1) * 128, :] if i < 16 else y_s[:, :]
        pn_b(dn_ps[i], rows, x1_d[i * 128:i * 128 + rows, :], gpost2, dst, pn_bufs[i % 2])

    for step in range(17 + 1):
        if step < 17:
            dn_a(step)
        if step - 1 >= 0:
            dn_b(step - 1)
    K.pop_scope()
    K.finish()
    es.close()
    return nc


_W_NAMES = ["norm_pre_mix", "norm_post_mix", "norm_pre_ffn", "norm_post_ffn", "w_in", "mu_shift", "w0",
            "w_decay_up", "a0", "w_aaa_up", "w_gate_up", "k_k", "k_a", "r_k", "lnx_g", "lnx_b", "w_rwkv_out",
            "s5_lam_re", "s5_lam_im", "s5_log_dt", "s5_b_re", "s5_b_im", "s5_c_re", "s5_c_im", "s5_d",
            "glu_w1", "glu_b1", "glu_w2", "glu_b2", "w_merge_out", "w_ffn_gate", "w_ffn_up", "w_ffn_down"]


def make_in_maps(inputs, cores):
    f = lambda a: np.ascontiguousarray(np.asarray(a, dtype=np.float32))
    shared = {n: f(inputs[n])[0] for n in _W_NAMES}
    maps = []
    for c in cores:
        m = dict(shared)
        m["x_p"] = f(inputs["x_prompt"][c])
        m["x_s"] = f(inputs["x_sample"][c * NS:(c + 1) * NS, 0])
        m["st_shift"] = f(inputs["state_shift"][0, c * NS:(c + 1) * NS])
        m["st_wkv"] = f(inputs["state_wkv"][0, c * NS:(c + 1) * NS]).reshape(NS * 8, 64 * 64)
        m["st_re"] = f(inputs["state_s5_re"][0, c * NS:(c + 1) * NS]).reshape(NS, 2048)
        m["st_im"] = f(inputs["state_s5_im"][0, c * NS:(c + 1) * NS]).reshape(NS, 2048)
        maps.append(m)
    return maps


def assemble(results):
    n = len(results)
    cat = lambda k: np.concatenate([np.asarray(r[k]) for r in results], axis=0)
    y_p = np.stack([np.asarray(r["y_p"]) for r in results], 0)
    y_s = cat("y_s").reshape(n * NS, 1, D)
    sh_p = cat("o_shift_p").reshape(1, n, SHIFT)
    wkv_p = np.stack([np.asarray(r["o_wkv_p"]) for r in results], 0).reshape(1, n, 8, 64, 64)
    re_p = cat("o_re_p").reshape(1, n, 32, 64)
    im_p = cat("o_im_p").reshape(1, n, 32, 64)
    sh_s = cat("o_shift_s").reshape(1, n * NS, SHIFT)
    wkv_s = cat("o_wkv_s").reshape(1, n * NS, 8, 64, 64)
    re_s = cat("o_re_s").reshape(1, n * NS, 32, 64)
    im_s = cat("o_im_s").reshape(1, n * NS, 32, 64)
    return tuple(np.ascontiguousarray(a, dtype=np.float32) for a in
                 (y_p, y_s, sh_p, wkv_p, re_p, im_p, sh_s, wkv_s, re_s, im_s))


def kernel(**inputs):
    nc = build()
    in_maps = make_in_maps(inputs, list(range(NCORES)))
    res = run_bass_kernel_spmd(nc, in_maps, core_ids=list(range(NCORES)))
    return assemble(res.results)
```

```python
import math
from contextlib import ExitStack
from functools import partial
import numpy as np
import concourse.bass as bass
import concourse.mybir as mybir
from concourse.bass_utils import run_bass_kernel_spmd

F32 = mybir.dt.float32
BF16 = mybir.dt.bfloat16
I32 = mybir.dt.int32
AF = mybir.ActivationFunctionType
ALU = mybir.AluOpType
AX = mybir.AxisListType

T = 2048
NS = 16
TT = T + NS
D = 1024
RW = 512
SHIFT = 1664
PROJ = 4224
DFF = 2816
NCORES = 8
CDEC = math.exp(-0.5)
TBLK = [(0, 512), (512, 512), (1024, 512), (1536, 512), (2048, 16)]
SAME_ENGINE_SYNC = True


class V:
    __slots__ = ("buf", "ap")

    def __init__(self, buf, ap):
        self.buf = buf
        self.ap = ap

    def __getitem__(self, idx):
        return V(self.buf, self.ap[idx])

    def bc(self, shape):
        return V(self.buf, self.ap.to_broadcast(list(shape)))

    def re(self, s, **kw):
        return V(self.buf, self.ap.rearrange(s, **kw))

    def un(self, axis):
        return V(self.buf, self.ap.unsqueeze(axis))

    def bitcast(self, dt):
        return V(self.buf, self.ap.bitcast(dt))


class Buf:
    def __init__(self, name, handle):
        self.name = name
        self.h = handle
        self.writes = {}
        self.reads = {}
        self.dsem = None
        self.dcnt = 0
        self.is_psum = False

    def __getitem__(self, idx):
        return V(self, self.h[idx])

    def v(self):
        return V(self, self.h[:])


class Kern:
    def __init__(self, nc, es):
        self.nc = nc
        self.es = es
        self.root_es = es
        self.eng = {"pe": nc.tensor, "act": nc.scalar, "dve": nc.vector, "pool": nc.gpsimd, "sp": nc.sync}
        self.sem = {}
        self.cnt = {}
        for e in ("pe", "act", "dve", "pool"):
            self.sem[e] = es.enter_context(nc.semaphore("s_" + e))
            self.cnt[e] = 0
        self.obs = {e: {} for e in self.eng}
        self.semname = {}
        self.dsems = []
        self.nbuf = 0
        self.psf_i = 0
        self.psb_i = 0

    def sb(self, name, shape, dt):
        self.nbuf += 1
        import os
        if os.environ.get("KDBG"):
            sz = int(np.prod(shape[1:])) * (2 if dt == BF16 else 4)
            self._tot = getattr(self, "_tot", 0) + sz
            print(f"alloc {name} {shape} {sz} depth={len(getattr(self, '_saved', []))} cum_noFree={self._tot}")
        h = self.es.enter_context(self.nc.sbuf_tensor(f"{name}_{self.nbuf}", list(shape), dt))
        return Buf(name, h)

    def barrier(self):
        for e in ("pe", "act", "dve", "pool", "sp"):
            for e2 in ("pe", "act", "dve", "pool"):
                if e2 != e and self.cnt[e2]:
                    self._wait(e, e2, self.sem[e2], self.cnt[e2])
            for ob in self.dsems:
                self._wait(e, "d:" + ob.name + str(id(ob)), ob.dsem, ob.dcnt)

    def push_scope(self):
        self._saved = getattr(self, "_saved", [])
        self._saved.append(self.es)
        self.es = ExitStack()

    def pop_scope(self):
        self.barrier()
        self.es.close()
        self.es = self._saved.pop()

    def ps(self, name, shape, dt):
        self.nbuf += 1
        h = self.es.enter_context(self.nc.psum_tensor(f"{name}_{self.nbuf}", list(shape), dt))
        b = Buf(name, h)
        b.is_psum = True
        return b

    def dram(self, name, shape, dt, kind="Internal"):
        h = self.nc.dram_tensor(name, list(shape), dt, kind=kind)
        return Buf(name, h.ap())

    def newsem(self, name):
        s = self.root_es.enter_context(self.nc.semaphore(name))
        return s

    def _wait(self, e, sem_key, sem, val):
        o = self.obs[e]
        if o.get(sem_key, 0) >= val:
            return
        self.eng[e].wait_ge(sem, val)
        o[sem_key] = val

    def _deps(self, e, W, R):
        need = {}
        for b in R:
            for k, (s, v) in b.writes.items():
                if need.get(k, (None, 0))[1] < v:
                    need[k] = (s, v)
            if b.is_psum:
                for k, (s, v) in b.reads.items():
                    if k != e and need.get(k, (None, 0))[1] < v:
                        need[k] = (s, v)
        for b in W:
            for k, (s, v) in b.writes.items():
                if need.get(k, (None, 0))[1] < v:
                    need[k] = (s, v)
            for k, (s, v) in b.reads.items():
                if need.get(k, (None, 0))[1] < v:
                    need[k] = (s, v)
        for k, (s, v) in need.items():
            if k == e:
                if e == "pe" or not SAME_ENGINE_SYNC:
                    continue
            self._wait(e, k, s, v)

    def _bufs(self, vs):
        out = []
        for x in vs:
            if isinstance(x, V):
                if x.buf not in out:
                    out.append(x.buf)
            elif isinstance(x, Buf):
                if x not in out:
                    out.append(x)
        return out

    def op(self, e, fn, W, R):
        Wb = self._bufs(W)
        Rb = self._bufs(R)
        self._deps(e, Wb, Rb)
        inst = fn()
        self.cnt[e] += 1
        c = self.cnt[e]
        inst.then_inc(self.sem[e], 1)
        for b in Wb:
            b.reads = {}
            b.writes[e] = (self.sem[e], c)
        for b in Rb:
            if b not in Wb:
                b.reads[e] = (self.sem[e], c)
        return inst

    def dma(self, q, out, in_, sem_buf=None, slow=False):
        Wb = self._bufs([out])
        Rb = self._bufs([in_])
        self._deps(q, Wb, Rb)
        ob = sem_buf if sem_buf is not None else out.buf
        if ob.dsem is None:
            ob.dsem = self.newsem("d_" + ob.name + str(len(self.dsems)))
            self.dsems.append(ob)
        if slow:
            with self.nc.allow_non_contiguous_dma(reason="small parameter layout load"):
                inst = self.eng[q].dma_start(out=out.ap, in_=in_.ap)
        else:
            inst = self.eng[q].dma_start(out=out.ap, in_=in_.ap)
        ob.dcnt += 16
        inst.then_inc(ob.dsem, 16)
        key = "d:" + ob.name + str(id(ob))
        for b in Wb:
            b.reads = {}
            b.writes[key] = (ob.dsem, ob.dcnt)
        for b in Rb:
            if b not in Wb:
                b.reads[key] = (ob.dsem, ob.dcnt)
        return inst

    def finish(self):
        for ob in self.dsems:
            self._wait("sp", "d:" + ob.name + str(id(ob)), ob.dsem, ob.dcnt)
        for e in ("pe", "act", "dve", "pool"):
            if self.cnt[e]:
                self._wait("sp", e, self.sem[e], self.cnt[e])

    @staticmethod
    def _a(x):
        return x.ap if isinstance(x, V) else x

    def act(self, out, in_, func, bias=None, scale=None, accum=None):
        kw = {}
        if bias is not None:
            kw["bias"] = self._a(bias)
        if scale is not None:
            kw["scale"] = self._a(scale)
        if accum is not None:
            kw["accum_out"] = self._a(accum)
        return self.op("act", lambda: self.nc.scalar.activation(out=out.ap, in_=in_.ap, func=func, **kw),
                       [out, accum], [in_, bias, scale])

    def tt(self, e, out, in0, in1, op):
        return self.op(e, lambda: self.eng[e].tensor_tensor(out=out.ap, in0=in0.ap, in1=in1.ap, op=op),
                       [out], [in0, in1])

    def ts(self, e, out, in0, s1, s2, op0, op1=None):
        if op1 is None:
            return self.op(e, lambda: self.eng[e].tensor_scalar(out=out.ap, in0=in0.ap, scalar1=self._a(s1),
                                                               scalar2=None, op0=op0), [out], [in0, s1])
        return self.op(e, lambda: self.eng[e].tensor_scalar(out=out.ap, in0=in0.ap, scalar1=self._a(s1),
                                                           scalar2=self._a(s2), op0=op0, op1=op1),
                       [out], [in0, s1, s2])

    def stt(self, out, in0, scalar, in1, op0, op1, e="dve"):
        return self.op(e, lambda: self.eng[e].scalar_tensor_tensor(out=out.ap, in0=in0.ap, scalar=self._a(scalar),
                                                                  in1=in1.ap, op0=op0, op1=op1),
                       [out], [in0, scalar, in1])

    def copy(self, e, out, in_):
        if e == "act":
            return self.act(out, in_, AF.Copy)
        return self.op(e, lambda: self.eng[e].tensor_copy(out=out.ap, in_=in_.ap), [out], [in_])

    def memset(self, e, out, val):
        return self.op(e, lambda: self.eng[e].memset(out.ap, val), [out], [])

    def recip(self, out, in_):
        return self.op("dve", lambda: self.nc.vector.reciprocal(out=out.ap, in_=in_.ap), [out], [in_])

    def reduce(self, out, in_, op=ALU.add, axis=AX.X):
        return self.op("dve", lambda: self.nc.vector.tensor_reduce(out=out.ap, in_=in_.ap, axis=axis, op=op),
                       [out], [in_])

    def scan(self, out, d0, d1, init, op0=ALU.mult, op1=ALU.add):
        return self.op("dve", lambda: self.nc.vector.tensor_tensor_scan(out=out.ap, data0=d0.ap, data1=d1.ap,
                                                                       initial=self._a(init), op0=op0, op1=op1),
                       [out], [d0, d1, init])

    def mm(self, out, lhsT, rhs, start=True, stop=True):
        return self.op("pe", lambda: self.nc.tensor.matmul(out.ap, lhsT=lhsT.ap, rhs=rhs.ap, start=start, stop=stop),
                       [out], [lhsT, rhs])

    def tr(self, out, in_, ident):
        return self.op("pe", lambda: self.nc.tensor.transpose(out.ap, in_.ap, ident.ap), [out], [in_, ident])

    def aselect(self, out, in_, pattern, cmp, fill, base, cm):
        return self.op("pool", lambda: self.nc.gpsimd.affine_select(out=out.ap, in_=in_.ap, pattern=pattern,
                                                                   compare_op=cmp, fill=fill, base=base,
                                                                   channel_multiplier=cm), [out], [in_])

    def iota(self, out, pattern, base, cm):
        return self.op("pool", lambda: self.nc.gpsimd.iota(out.ap, pattern=pattern, base=base,
                                                          channel_multiplier=cm), [out], [])


def run_merged_g(lists):
    idx = [0] * len(lists)
    while True:
        best, bf = -1, 2.0
        for li, L in enumerate(lists):
            if idx[li] < len(L):
                frac = idx[li] / len(L)
                if frac < bf:
                    best, bf = li, frac
        if best < 0:
            break
        lists[best][idx[best]]()
        idx[best] += 1


def merge_lists(lists):
    out = []
    idx = [0] * len(lists)
    while True:
        best, bf = -1, 2.0
        for li, L in enumerate(lists):
            if idx[li] < len(L):
                frac = idx[li] / len(L)
                if frac < bf:
                    best, bf = li, frac
        if best < 0:
            break
        out.append(lists[best][idx[best]])
        idx[best] += 1
    return out


def dap(buf, offset, ap):
    base = buf.h if not hasattr(buf.h, "ap") or isinstance(buf.h, bass.AP) else buf.h
    t = base.tensor if isinstance(base, bass.AP) else base
    return V(buf, bass.AP(t, offset, [list(x) for x in ap]))


def build(stage=99):
    nc = bass.Bass("TRN2", target_bir_lowering=False)
    es = ExitStack()
    K = Kern(nc, es)

    def din(name, shape, dt=F32):
        return Buf(name, nc.dram_tensor(name, list(shape), dt, kind="ExternalInput").ap())

    def dout(name, shape):
        return Buf(name, nc.dram_tensor(name, list(shape), F32, kind="ExternalOutput").ap())

    x_p = din("x_p", [T, D])
    x_s = din("x_s", [NS, D])
    st_shift = din("st_shift", [NS, SHIFT])
    st_wkv = din("st_wkv", [NS * 8, 64 * 64])
    st_re = din("st_re", [NS, 2048])
    st_im = din("st_im", [NS, 2048])
    g_pre_mix = din("norm_pre_mix", [D])
    g_post_mix = din("norm_post_mix", [D])
    g_pre_ffn = din("norm_pre_ffn", [D])
    g_post_ffn = din("norm_post_ffn", [D])
    w_in = din("w_in", [D, PROJ])
    mu_shift = din("mu_shift", [SHIFT])
    w0 = din("w0", [RW])
    w_decay_up = din("w_decay_up", [32, RW])
    a0 = din("a0", [RW])
    w_aaa_up = din("w_aaa_up", [32, RW])
    w_gate_up = din("w_gate_up", [64, RW])
    k_k = din("k_k", [RW])
    k_a = din("k_a", [RW])
    r_k = din("r_k", [RW])
    lnx_g = din("lnx_g", [RW])
    lnx_b = din("lnx_b", [RW])
    w_rwkv_out = din("w_rwkv_out", [RW, D])
    s5_lam_re = din("s5_lam_re", [32, 64])
    s5_lam_im = din("s5_lam_im", [32, 64])
    s5_log_dt = din("s5_log_dt", [32])
    s5_b_re = din("s5_b_re", [32, 64, 16])
    s5_b_im = din("s5_b_im", [32, 64, 16])
    s5_c_re = din("s5_c_re", [32, 16, 64])
    s5_c_im = din("s5_c_im", [32, 16, 64])
    s5_d = din("s5_d", [RW])
    glu_w1 = din("glu_w1", [RW, D])
    glu_b1 = din("glu_b1", [D])
    glu_w2 = din("glu_w2", [RW, D])
    glu_b2 = din("glu_b2", [D])
    w_merge_out = din("w_merge_out", [D, D])
    w_ffn_gate = din("w_ffn_gate", [D, DFF])
    w_ffn_up = din("w_ffn_up", [D, DFF])
    w_ffn_down = din("w_ffn_down", [DFF, D])

    y_p = dout("y_p", [T, D])
    y_s = dout("y_s", [NS, D])
    o_shift_p = dout("o_shift_p", [1, SHIFT])
    o_wkv_p = dout("o_wkv_p", [8, 64, 64])
    o_re_p = dout("o_re_p", [1, 2048])
    o_im_p = dout("o_im_p", [1, 2048])
    o_shift_s = dout("o_shift_s", [NS, SHIFT])
    o_wkv_s = dout("o_wkv_s", [NS * 8, 64 * 64])
    o_re_s = dout("o_re_s", [NS, 2048])
    o_im_s = dout("o_im_s", [NS, 2048])

    x1_d = K.dram("x1_scratch", [TT, D], F32)

    psf = [K.ps("psf", [128, 512], F32) for _ in range(6)]
    psb = [K.ps("psb", [128, 1024], BF16) for _ in range(2)]

    def PSF():
        K.psf_i += 1
        return psf[K.psf_i % len(psf)]

    def PSB():
        K.psb_i += 1
        return psb[K.psb_i % len(psb)]

    ones_f = K.sb("ones_f", [128, 128], F32)
    identf = K.sb("identf", [128, 128], F32)
    identb = K.sb("identb", [128, 128], BF16)
    epsT = K.sb("epsT", [128, 1], F32)
    K.memset("pool", ones_f.v(), 1.0)
    K.memset("pool", epsT.v(), 1e-6)
    K.aselect(identf.v(), ones_f.v(), [[-1, 128]], ALU.is_equal, 0.0, 0, 1)
    K.copy("pool", identb.v(), identf.v())

    pvec = K.sb("pvec", [128, 128], F32)
    PV = {}
    _pv = [0]

    def load_fm(name, src, ntile):
        c0 = _pv[0]
        _pv[0] += ntile
        K.dma("sp", pvec[:, c0:c0 + ntile], dap(src, 0, [[1, 128], [128, ntile]]), sem_buf=pvec, slow=True)
        PV[name] = c0
        return c0

    load_fm("g_pre_mix", g_pre_mix, 8)
    load_fm("g_pre_ffn", g_pre_ffn, 8)
    load_fm("mu", mu_shift, 13)
    load_fm("w0", w0, 4)
    load_fm("a0", a0, 4)
    load_fm("k_k", k_k, 4)
    load_fm("k_a", k_a, 4)
    load_fm("r_k", r_k, 4)
    load_fm("s5_d", s5_d, 4)
    load_fm("glu_b1", glu_b1, 8)
    load_fm("glu_b2", glu_b2, 8)

    def early():
        while getattr(K, "_saved", []):
            K.pop_scope()
        K.finish()
        es.close()
        return nc

    if stage == 0.1:
        return early()

    def pcol(name, i=0):
        c = PV[name] + i
        return pvec[:, c:c + 1]

    hT = K.sb("hT", [128, 8, TT], BF16)
    def norm_transpose(src_tile_fn, gname, dst):
        K.push_scope()
        NB = 3
        xring = [K.sb("xring", [128, D], F32) for _ in range(NB)]
        xn = [K.sb("xn", [128, D], BF16) for _ in range(NB)]
        junk = K.sb("junk", [128, D], BF16)
        stat = [K.sb("stat", [128, 4], F32) for _ in range(NB)]

        def nt_a(i):
            rows = 128 if i < 16 else NS
            xt, st, xb = xring[i % NB], stat[i % NB], xn[i % NB]
            K.dma("sp", xt[:rows, :], src_tile_fn(i))
            K.act(junk[:rows, :], xt[:rows, :], AF.Square, accum=st[:rows, 0:1])
            K.act(st[:rows, 1:2], st[:rows, 0:1], AF.Sqrt, bias=epsT[:rows, :], scale=1.0 / D)
            K.recip(st[:rows, 2:3], st[:rows, 1:2])
            K.act(xb[:rows, :], xt[:rows, :], AF.Copy, scale=st[:rows, 2:3])

        def nt_b(i):
            rows = 128 if i < 16 else NS
            col0 = i * 128
            xb = xn[i % NB]
            pb = PSB()
            pv = pb.v().re("p (k t) -> p k t", k=8)
            for kt in range(8):
                K.tr(pv[:, kt, :rows], xb[:rows, kt * 128:(kt + 1) * 128], identb[:rows, :rows])
            g0 = PV[gname]
            K.tt("dve", dst[:, :, col0:col0 + rows], pv[:, :, :rows],
                 pvec[:, g0:g0 + 8].un(2).bc([128, 8, rows]), ALU.mult)

        for step in range(17 + 1):
            if step < 17:
                nt_a(step)
            if step >= 1:
                nt_b(step - 1)
        K.pop_scope()

    def x_tile(i):
        return x_p[i * 128:(i + 1) * 128, :] if i < 16 else x_s[:, :]

    norm_transpose(x_tile, "g_pre_mix", hT)
    if stage == 0.2:
        return early()

    wring = [K.sb("wring", [128, 8, 128], BF16) for _ in range(2)]
    wr_i = [0]

    def load_w_cols(src, c0, ncols_total, kt_n=8, width=128):
        wr_i[0] += 1
        wb = wring[wr_i[0] % len(wring)]
        K.dma("pool", wb[:, :kt_n, :width],
              dap(src, c0, [[ncols_total, 128], [128 * ncols_total, kt_n], [1, width]]))
        return wb

    def proj_fm(wb, kt_n, src_act, tb, ps):
        col0, n = TBLK[tb]
        for kt in range(kt_n):
            K.mm(ps[:, :n], wb[:, kt, :], src_act[:, kt, col0:col0 + n], start=(kt == 0), stop=(kt == kt_n - 1))

    hgl_d = K.dram("hgl_scratch", [128, 4 * TT], BF16)
    TWO_PI = 2.0 * math.pi

    K.push_scope()
    hgl = K.sb("hgl", [128, 4, TT], BF16)
    ubf = K.sb("ubf", [128, 4, TT], BF16)
    wu4 = [K.sb("wu4", [128, 8, 128], BF16) for _ in range(4)]
    for ft in range(4):
        K.dma("pool", wu4[ft].v(), dap(w_in, SHIFT + ft * 128, [[PROJ, 128], [128 * PROJ, 8], [1, 128]]))
        for tb in range(5):
            col0, n = TBLK[tb]
            ps = PSF()
            proj_fm(wu4[ft], 8, hT, tb, ps)
            K.copy("act" if tb % 2 else "dve", ubf[:, ft, col0:col0 + n], ps[:, :n])

    if stage == 1.1:
        return early()
    sp_ = K.sb("s5par", [128, 16, 16], F32)
    P_LRE, P_LIM, P_DT, P_MAG, P_TH, P_FRE, P_FIM, P_LBRE, P_LBIM, P_T0, P_T1, P_T2, P_T3 = range(13)

    def sp(k):
        return sp_[:, :, k]

    K.dma("sp", sp(P_LRE), dap(s5_lam_re, 0, [[1, 128], [128, 16]]), slow=True)
    K.dma("sp", sp(P_LIM), dap(s5_lam_im, 0, [[1, 128], [128, 16]]), slow=True)
    for gl in range(2):
        K.dma("sp", sp_[gl * 64:(gl + 1) * 64, :, P_DT], dap(s5_log_dt, gl, [[0, 64], [2, 16]]), slow=True)
    K.act(sp(P_DT), sp(P_DT), AF.Exp)
    K.tt("dve", sp(P_T0), sp(P_LRE), sp(P_DT), ALU.mult)
    K.act(sp(P_MAG), sp(P_T0), AF.Exp)
    K.tt("dve", sp(P_TH), sp(P_LIM), sp(P_DT), ALU.mult)

    if stage == 1.15:
        return early()
    LC = 128
    cosT = K.sb("cosT", [128, 16, LC], F32)
    sinT = K.sb("sinT", [128, 16, LC], F32)
    preR = K.sb("preR", [128, 16, LC], F32)
    preI = K.sb("preI", [128, 16, LC], F32)
    K.push_scope()
    ji = K.sb("ji", [128, LC], I32)
    jf = K.sb("jf", [128, LC], F32)
    ang = K.sb("ang", [128, 16, LC], F32)
    rr = K.sb("rr", [128, 16, LC], F32)
    kf = K.sb("kf", [128, 16, LC], F32)
    ki = K.sb("ki", [128, 16, LC], I32)
    K.iota(ji.v(), [[1, LC]], 1, 0)
    K.copy("dve", jf.v(), ji.v())
    K.tt("dve", ang.v(), sp(P_TH).un(2).bc([128, 16, LC]), jf.v().un(1).bc([128, 16, LC]), ALU.mult)

    def sin_reduced(out, a_in, shift):
        C1 = 6.28125
        C2 = TWO_PI - C1
        K.ts("dve", rr.v(), a_in, shift, 1.0 / TWO_PI, ALU.add, ALU.mult)
        K.copy("dve", ki.v(), rr.v())
        K.copy("dve", kf.v(), ki.v())
        K.ts("dve", rr.v(), a_in, shift, None, ALU.add)
        K.stt(rr.v(), kf.v(), -C1, rr.v(), ALU.mult, ALU.add)
        K.stt(rr.v(), kf.v(), -C2, rr.v(), ALU.mult, ALU.add)
        K.ts("dve", kf.v(), rr.v(), math.pi, None, ALU.is_gt)
        K.stt(rr.v(), kf.v(), -TWO_PI, rr.v(), ALU.mult, ALU.add)
        K.ts("dve", kf.v(), rr.v(), -math.pi, None, ALU.is_lt)
        K.stt(rr.v(), kf.v(), TWO_PI, rr.v(), ALU.mult, ALU.add)
        K.ts("dve", rr.v(), rr.v(), -math.pi, math.pi, ALU.max, ALU.min)
        K.act(out, rr.v(), AF.Sin)

    sin_reduced(sinT.v(), ang.v(), 0.0)
    sin_reduced(cosT.v(), ang.v(), 0.5 * math.pi)
    K.tt("dve", sp(P_LBRE), sp(P_MAG), cosT[:, :, 0], ALU.mult)
    K.tt("dve", sp(P_LBIM), sp(P_MAG), sinT[:, :, 0], ALU.mult)
    K.tt("dve", sp(P_T0), sp(P_LRE), sp(P_LRE), ALU.mult)
    K.tt("dve", sp(P_T1), sp(P_LIM), sp(P_LIM), ALU.mult)
    K.tt("dve", sp(P_T0), sp(P_T0), sp(P_T1), ALU.add)
    K.recip(sp(P_T0), sp(P_T0))
    K.ts("dve", sp(P_T1), sp(P_LBRE), -1.0, None, ALU.add)
    K.tt("dve", sp(P_T2), sp(P_T1), sp(P_LRE), ALU.mult)
    K.tt("dve", sp(P_T3), sp(P_LBIM), sp(P_LIM), ALU.mult)
    K.tt("dve", sp(P_T2), sp(P_T2), sp(P_T3), ALU.add)
    K.tt("dve", sp(P_FRE), sp(P_T2), sp(P_T0), ALU.mult)
    K.tt("dve", sp(P_T2), sp(P_LBIM), sp(P_LRE), ALU.mult)
    K.tt("dve", sp(P_T3), sp(P_T1), sp(P_LIM), ALU.mult)
    K.tt("dve", sp(P_T2), sp(P_T2), sp(P_T3), ALU.subtract)
    K.tt("dve", sp(P_FIM), sp(P_T2), sp(P_T0), ALU.mult)
    fre_b = sp(P_FRE).un(2).bc([128, 16, LC])
    fim_b = sp(P_FIM).un(2).bc([128, 16, LC])
    K.tt("dve", rr.v(), cosT.v(), fre_b, ALU.mult)
    K.tt("dve", kf.v(), sinT.v(), fim_b, ALU.mult)
    K.tt("dve", preR.v(), rr.v(), kf.v(), ALU.add)
    K.tt("dve", rr.v(), cosT.v(), fim_b, ALU.mult)
    K.tt("dve", kf.v(), sinT.v(), fre_b, ALU.mult)
    K.tt("dve", preI.v(), rr.v(), kf.v(), ALU.subtract)
    K.pop_scope()

    if stage == 1.2:
        return early()
    BT = K.sb("BT", [128, 32, 128], BF16)
    CTb = K.sb("CTb", [128, 32, 128], BF16)
    K.push_scope()
    for part, src in enumerate((s5_b_re, s5_b_im)):
        X = K.sb("Xb", [128, 16 * 128], F32)
        K.memset("pool", X.v(), 0.0)
        for gl in range(2):
            for a in range(4):
                K.dma("sp", dap(X, gl * 64 * 2048 + 16 * gl + a * 512, [[2048, 64], [160, 4], [1, 16]]),
                      dap(src, gl * 1024 + a * 8192, [[16, 64], [2048, 4], [1, 16]]), slow=True)
        for i4 in range(4):
            ps = PSF()
            for j in range(4):
                i = i4 * 4 + j
                K.tr(ps[:, j * 128:(j + 1) * 128], X[:, i * 128:(i + 1) * 128], identf.v())
            for j in range(4):
                i = i4 * 4 + j
                K.copy("act" if j % 2 else "dve", BT[:, 2 * i + part, :], ps[:, j * 128:(j + 1) * 128])
    for part, src in enumerate((s5_c_re, s5_c_im)):
        Z = K.sb("Zc", [128, 4 * 512], F32)
        K.memset("pool", Z.v(), 0.0)
        for g8 in range(8):
            K.dma("sp", dap(Z, 16 * g8 * 2048 + 64 * g8, [[2048, 16], [512, 4], [1, 64]]),
                  dap(src, g8 * 1024, [[64, 16], [8192, 4], [1, 64]]))
        for q in range(4):
            ps = PSF()
            for j in range(4):
                K.tr(ps[:, j * 128:(j + 1) * 128], Z[:, q * 512 + j * 128:q * 512 + (j + 1) * 128], identf.v())
            i0 = q * 4
            dst = CTb.v().re("p (i two) c -> p i two c", two=2)[:, i0:i0 + 4, part, :]
            if part == 0:
                K.copy("dve", dst, ps.v().re("p (j c) -> p j c", j=4))
            else:
                K.ts("dve", dst, ps.v().re("p (j c) -> p j c", j=4), -1.0, None, ALU.mult)
    K.pop_scope()

    if stage == 1.3:
        return early()
    x1b = [K.sb("x1sb", [128, 16, NS], BF16) for _ in range(2)]
    K.push_scope()
    s0_tm = [K.sb("s0tm", [NS, 2048], F32) for _ in range(2)]
    s0 = [K.sb("s0", [128, 16, NS], F32) for _ in range(2)]
    K.dma("sp", s0_tm[0].v(), st_re.v())
    K.dma("sp", s0_tm[1].v(), st_im.v())
    for part in range(2):
        ps = PSF()
        for i in range(16):
            K.tr(ps[:, i * NS:(i + 1) * NS], s0_tm[part][:, i * 128:(i + 1) * 128], identf[:NS, :NS])
        K.copy("dve", s0[part].v(), ps[:, :16 * NS].re("p (i n) -> p i n", i=16))
    psr = PSF()
    psi = PSF()
    for i in range(16):
        K.mm(psr[:, i * NS:(i + 1) * NS], BT[:, 2 * i, :], ubf[:, i // 4, T:TT])
        K.mm(psi[:, i * NS:(i + 1) * NS], BT[:, 2 * i + 1, :], ubf[:, i // 4, T:TT])
    sw = [K.sb("sw", [128, 16, NS], F32) for _ in range(4)]
    x1 = [K.sb("x1s", [128, 16, NS], F32) for _ in range(2)]
    bc16 = lambda k: sp(k).un(2).bc([128, 16, NS])
    pr3 = psr[:, :16 * NS].re("p (i n) -> p i n", i=16)
    pi3 = psi[:, :16 * NS].re("p (i n) -> p i n", i=16)
    K.tt("dve", sw[0].v(), pr3, bc16(P_FRE), ALU.mult)
    K.tt("dve", sw[1].v(), pi3, bc16(P_FIM), ALU.mult)
    K.tt("dve", sw[2].v(), sw[0].v(), sw[1].v(), ALU.subtract)
    K.tt("dve", sw[0].v(), pi3, bc16(P_FRE), ALU.mult)
    K.tt("dve", sw[1].v(), pr3, bc16(P_FIM), ALU.mult)
    K.tt("dve", sw[3].v(), sw[0].v(), sw[1].v(), ALU.add)
    K.tt("dve", sw[0].v(), s0[0].v(), bc16(P_LBRE), ALU.mult)
    K.tt("dve", sw[1].v(), s0[1].v(), bc16(P_LBIM), ALU.mult)
    K.tt("dve", sw[0].v(), sw[0].v(), sw[1].v(), ALU.subtract)
    K.tt("dve", x1[0].v(), sw[0].v(), sw[2].v(), ALU.add)
    K.tt("dve", sw[0].v(), s0[1].v(), bc16(P_LBRE), ALU.mult)
    K.tt("dve", sw[1].v(), s0[0].v(), bc16(P_LBIM), ALU.mult)
    K.tt("dve", sw[0].v(), sw[0].v(), sw[1].v(), ALU.add)
    K.tt("dve", x1[1].v(), sw[0].v(), sw[3].v(), ALU.add)
    for part in range(2):
        K.copy("act", x1b[part].v(), x1[part].v())
        xs_tm = s0_tm[part]
        for i4 in range(4):
            ps = PSF()
            for j in range(4):
                i = i4 * 4 + j
                K.tr(ps[:NS, j * 128:(j + 1) * 128], x1[part][:, i, :], identf.v())
            K.copy("dve", xs_tm[:, i4 * 512:(i4 + 1) * 512], ps[:NS, :])
        K.dma("sp", (o_re_s if part == 0 else o_im_s).v(), xs_tm.v())
    K.pop_scope()

    if stage == 1.4:
        return early()
    carry = K.sb("carry", [128, 16, 2], F32)
    K.memset("dve", carry.v(), 0.0)
    UN = [dict(bR=K.sb("bR", [128, 512], F32), bI=K.sb("bI", [128, 512], F32),
               wR=K.sb("wR", [128, 512], F32), wI=K.sb("wI", [128, 512], F32),
               t1=K.sb("t1", [128, 512], F32), t2=K.sb("t2", [128, 512], F32)) for _ in range(4)]
    xbr = [K.sb("xbr", [128, 512], BF16) for _ in range(4)]
    xbi = [K.sb("xbi", [128, 512], BF16) for _ in range(4)]
    yv = [K.sb("yv", [128, 512], F32) for _ in range(2)]
    y2 = [K.sb("y2", [128, 512], F32) for _ in range(2)]

    def gelu_to(dst_bf, yv_, y2_, n):
        K.act(y2_[:, :n], yv_[:, :n], AF.Square)
        K.act(y2_[:, :n], y2_[:, :n], AF.Identity, bias=1.0, scale=0.044715)
        K.tt("pool", y2_[:, :n], y2_[:, :n], yv_[:, :n], ALU.mult)
        K.act(y2_[:, :n], y2_[:, :n], AF.Sigmoid, scale=1.5957691216057308)
        K.tt("pool", dst_bf, y2_[:, :n], yv_[:, :n], ALU.mult)

    v4 = lambda b: b.v().re("p (a l) -> p a l", a=4)
    for q in range(4):
        for tb in range(4):
            col0, n = TBLK[tb]
            for j in range(4):
                i = q * 4 + j
                U = UN[j]
                psr = PSF()
                psi = PSF()
                K.mm(psr.v(), BT[:, 2 * i, :], ubf[:, q, col0:col0 + n])
                K.mm(psi.v(), BT[:, 2 * i + 1, :], ubf[:, q, col0:col0 + n])
                tR = preR[:, i, :].un(1).bc([128, 4, LC])
                tI = preI[:, i, :].un(1).bc([128, 4, LC])
                K.tt("dve", v4(U["t1"]), v4(psr), tR, ALU.mult)
                K.tt("dve", v4(U["t2"]), v4(psi), tI, ALU.mult)
                K.tt("pool", U["bR"].v(), U["t1"].v(), U["t2"].v(), ALU.subtract)
                K.tt("dve", v4(U["wR"]), v4(psi), tR, ALU.mult)
                K.tt("dve", v4(U["wI"]), v4(psr), tI, ALU.mult)
                K.tt("dve", U["bI"].v(), U["wR"].v(), U["wI"].v(), ALU.add)
            for a in range(4):
                cs = slice(a * LC, (a + 1) * LC)
                for j in range(4):
                    i = q * 4 + j
                    U = UN[j]
                    rho = sp_[:, i, P_MAG:P_MAG + 1].bc([128, LC])
                    if a == 0:
                        ire, iim = carry[:, i, 0:1], carry[:, i, 1:2]
                    else:
                        ire, iim = U["bR"][:, a * LC - 1:a * LC], U["bI"][:, a * LC - 1:a * LC]
                    K.scan(U["wR"][:, cs], rho, U["bR"][:, cs], ire)
                    K.scan(U["wI"][:, cs], rho, U["bI"][:, cs], iim)
                for j in range(4):
                    i = q * 4 + j
                    U = UN[j]
                    K.tt("pool", U["t1"][:, cs], U["wR"][:, cs], cosT[:, i, :], ALU.mult)
                    K.tt("dve", U["t2"][:, cs], U["wI"][:, cs], sinT[:, i, :], ALU.mult)
                    K.tt("dve", U["bR"][:, cs], U["t1"][:, cs], U["t2"][:, cs], ALU.subtract)
                    K.tt("pool", U["t1"][:, cs], U["wR"][:, cs], sinT[:, i, :], ALU.mult)
                    K.tt("dve", U["t2"][:, cs], U["wI"][:, cs], cosT[:, i, :], ALU.mult)
                    K.tt("dve", U["bI"][:, cs], U["t1"][:, cs], U["t2"][:, cs], ALU.add)
            for j in range(4):
                i = q * 4 + j
                U = UN[j]
                K.copy("dve", carry[:, i, 0:1], U["bR"][:, 511:512])
                K.copy("dve", carry[:, i, 1:2], U["bI"][:, 511:512])
                K.copy("act", xbr[j].v(), U["bR"].v())
                K.copy("act", xbi[j].v(), U["bI"].v())
            psy = PSF()
            for j in range(4):
                i = q * 4 + j
                K.mm(psy.v(), CTb[:, 2 * i, :], xbr[j].v(), start=(j == 0), stop=False)
                K.mm(psy.v(), CTb[:, 2 * i + 1, :], xbi[j].v(), start=False, stop=(j == 3))
            yy = yv[tb % 2]
            K.copy("act", yy.v(), psy.v())
            psu = PSF()
            proj_fm(wu4[q], 8, hT, tb, psu)
            K.stt(yy.v(), psu.v(), pcol("s5_d", q), yy.v(), ALU.mult, ALU.add)
            gelu_to(hgl[:, q, col0:col0 + n], yy, y2[tb % 2], n)
        psy = PSF()
        for j in range(4):
            i = q * 4 + j
            K.mm(psy[:, :NS], CTb[:, 2 * i, :], x1b[0][:, i, :], start=(j == 0), stop=False)
            K.mm(psy[:, :NS], CTb[:, 2 * i + 1, :], x1b[1][:, i, :], start=False, stop=(j == 3))
        K.copy("act", yv[0][:, :NS], psy[:, :NS])
        psu = PSF()
        proj_fm(wu4[q], 8, hT, 4, psu)
        K.stt(yv[0][:, :NS], psu[:, :NS], pcol("s5_d", q), yv[0][:, :NS], ALU.mult, ALU.add)
        gelu_to(hgl[:, q, T:TT], yv[0], y2[0], NS)
    if stage == 1.5:
        return early()
    K.dma("sp", dap(o_re_p, 0, [[1, 128], [128, 16]]), carry[:, :, 0], slow=True)
    K.dma("sp", dap(o_im_p, 0, [[1, 128], [128, 16]]), carry[:, :, 1], slow=True)
    K.dma("sp", hgl_d.v(), hgl.v().re("p k t -> p (k t)"))
    K.pop_scope()
    if stage == 2:
        zt = K.sb("zt", [128, 4096], F32)
        K.memset("pool", zt.v(), 0.0)
        K.dma("sp", o_shift_p.v(), zt[0:1, 0:SHIFT])
        K.dma("sp", o_shift_s.v(), zt[0:NS, 0:SHIFT])
        K.dma("sp", o_wkv_p.v().re("h v k -> h (v k)"), zt[0:8, :])
        K.dma("sp", o_wkv_s.v(), zt[:, :])
        for i in range(16):
            K.dma("sp", y_p[i * 128:(i + 1) * 128, :], zt[:, 0:D])
        K.dma("sp", y_s.v(), zt[0:NS, 0:D])
        return early()
    K.push_scope()
    lnxg = K.sb("lnxg", [128, RW], F32)
    lnxb = K.sb("lnxb", [128, RW], F32)
    K.dma("sp", lnxg.v(), dap(lnx_g, 0, [[0, 128], [1, RW]]))
    K.dma("sp", lnxb.v(), dap(lnx_b, 0, [[0, 128], [1, RW]]))
    par = K.sb("spar", [128, 3, 64], F32)
    for tk in range(NS):
        for j, src in enumerate((lnx_g, lnx_b, r_k)):
            K.dma("sp", par[tk * 8:(tk + 1) * 8, j, :], dap(src, 0, [[64, 8], [1, 64]]), sem_buf=par)

    lora_bf = K.sb("lora_bf", [128, TT], BF16)
    lora_up = K.sb("lora_up", [128, RW], BF16)
    oaT = K.sb("oaT", [128, 4, TT], BF16)
    o_d = K.dram("o_scratch", [T, RW], F32)
    bonus_d = K.dram("bonus_scratch", [T, RW], F32)
    samp_d = K.dram("samp_scratch", [NS * 8, 7, 64], F32)
    sampo_d = K.dram("sampo_scratch", [NS, RW], F32)

    K.push_scope()
    mask4 = K.sb("mask4", [128, 4, 128], F32)
    maskNT = K.sb("maskNT", [128, 128], F32)
    resetm = K.sb("resetm", [128, T], BF16)
    blk1 = K.sb("blk1", [128, 128], F32)
    eps_ln = K.sb("eps_ln", [128, 1], F32)
    K.memset("pool", eps_ln.v(), 64e-5)
    for j in range(4):
        K.aselect(mask4[:, j, :], ones_f.v(), [[1, 128]], ALU.is_gt if j % 2 == 0 else ALU.is_ge, 0.0, 0, -1)
    K.aselect(maskNT.v(), ones_f.v(), [[-1, 128]], ALU.is_gt, 0.0, 0, 1)
    K.memset("pool", resetm.v(), 1.0)
    K.memset("pool", resetm.v().re("p (c l) -> p c l", l=128)[:, :, 0:1], 0.0)
    K.memset("pool", blk1.v(), 0.0)
    K.memset("pool", blk1[0:64, 0:64], 1.0)
    K.memset("pool", blk1[64:128, 64:128], 1.0)

    K.dma("pool", lora_up[0:32, :], w_decay_up.v())
    K.dma("pool", lora_up[32:64, :], w_aaa_up.v())
    K.dma("pool", lora_up[64:128, :], w_gate_up.v())
    sh0T = K.sb("sh0T", [128, 13, NS], F32)
    shout = [K.sb("shout", [NS + 1, 128], F32) for _ in range(1)]
    K.push_scope()
    sh_tm = K.sb("sh_tm", [NS, SHIFT], F32)
    K.dma("sp", sh_tm.v(), st_shift.v())
    ps = PSF()
    for ft in range(13):
        K.tr(ps[:, ft * NS:(ft + 1) * NS], sh_tm[:, ft * 128:(ft + 1) * 128], identf[:NS, :NS])
    K.copy("dve", sh0T.v(), ps[:, :13 * NS].re("p (f n) -> p f n", f=13))
    K.pop_scope()

    omm = K.sb("omm", [128, 13], F32)
    K.ts("dve", omm.v(), pvec[:, PV["mu"]:PV["mu"] + 13], -1.0, 1.0, ALU.mult, ALU.add)
    omka = K.sb("omka", [128, 4], F32)
    K.ts("dve", omka.v(), pvec[:, PV["k_a"]:PV["k_a"] + 4], -1.0, 1.0, ALU.mult, ALU.add)

    SC = [K.sb("scr", [128, TT], F32) for _ in range(6)]

    def proj_shift_tile(ft, pr, tmp, xr_out, xr_out_dt_bf=None, samp32=None):
        wb = load_w_cols(w_in, ft * 128, PROJ)
        for tb in range(5):
            col0, n = TBLK[tb]
            ps = PSF()
            proj_fm(wb, 8, hT, tb, ps)
            K.copy("act", pr[:, col0:col0 + n], ps[:, :n])
            K.act(tmp[:, col0:col0 + n], ps[:, :n], AF.Copy, scale=omm[:, ft:ft + 1])
        ps = PSF()
        K.tr(ps[:NS + 1, :128], pr[:, T - 1:TT], identf.v())
        so = shout[0]
        K.copy("dve", so.v(), ps[:NS + 1, :128])
        K.dma("sp", o_shift_p[0:1, ft * 128:(ft + 1) * 128], so[0:1, :])
        K.dma("sp", o_shift_s[:, ft * 128:(ft + 1) * 128], so[1:NS + 1, :])
        mu_c = pcol("mu", ft)
        K.stt(xr_out[:, 1:T], pr[:, 0:T - 1], mu_c, tmp[:, 1:T], ALU.mult, ALU.add)
        K.copy("pool", xr_out[:, 0:1], tmp[:, 0:1])
        K.stt(xr_out[:, T:TT], sh0T[:, ft, :], mu_c, tmp[:, T:TT], ALU.mult, ALU.add)
        if samp32 is not None:
            K.stt(samp32, sh0T[:, ft, :], mu_c, tmp[:, T:TT], ALU.mult, ALU.add)

    proj_shift_tile(12, SC[0], SC[1], SC[2].v())
    K.act(lora_bf[0:32, :], SC[2][0:32, :], AF.Tanh)
    K.act(lora_bf[32:64, :], SC[2][32:64, :], AF.Copy)
    K.act(lora_bf[64:128, :], SC[2][64:128, :], AF.Sigmoid)

    vbf = K.sb("vbf", [128, TT], BF16)
    vs32 = K.sb("vs32", [128, NS], F32)
    AR = K.sb("AR", [128, 16, 2, 128], BF16)
    bti = K.sb("bti", [128, T], BF16)
    kti = K.sb("kti", [128, T], BF16)
    bhat = K.sb("bhat", [128, T], BF16)
    khat = K.sb("khat", [128, T], BF16)
    Wfm = bhat
    TMq = K.sb("TMq", [128, 16, 4, 128], BF16)
    gL = K.sb("gL", [128, 16], F32)
    Sf = K.sb("Sf", [128, 64], F32)
    Sb = K.sb("Sb", [128, 64], BF16)
    smp = K.sb("smp", [NS, 3, 128], F32)
    decs = K.sb("decs", [128, NS], F32)
    avs = K.sb("avs", [128, NS], F32)
    NPR = 16
    ABK = [K.sb("ABK", [128, 2, 128], BF16) for _ in range(NPR)]
    NPQ = 8
    ATMP = [K.sb("ATMP", [128, 2, 128], BF16) for _ in range(NPQ)]
    P0T = [K.sb("P0T", [128, 128], BF16) for _ in range(NPQ)]
    PMb = [[K.sb("PMb", [128, 3, 128], BF16) for _ in range(2)] for _ in range(NPQ)]
    Ybf = [K.sb("Ybf", [128, 64], BF16) for _ in range(NPQ)]
    Ubar = [K.sb("Ubar", [128, 64], F32) for _ in range(NPR)]
    Ut = [K.sb("Ut", [128, 64], BF16) for _ in range(4)]
    Otile = [K.sb("Otile", [128, 128], F32) for _ in range(2)]
    btile = [K.sb("btile", [128, 512], F32) for _ in range(1)]
    wkvo = K.sb("wkvo", [64, 128], F32)

    def derivedA(hp):
        xr_r, xr_k = SC[0], SC[1]
        proj_shift_tile(0 * 4 + hp, SC[2], SC[3], xr_r.v())
        yield
        proj_shift_tile(1 * 4 + hp, SC[2], SC[3], xr_k.v())
        yield
        proj_shift_tile(2 * 4 + hp, SC[2], SC[3], vbf.v(), samp32=vs32.v())
        yield
        sigd, csum, afm, kkn = SC[2], SC[3], SC[4], SC[5]
        hc = slice(hp * 128, (hp + 1) * 128)
        for tb in range(5):
            col0, n = TBLK[tb]
            ps = PSF()
            K.mm(ps[:, :n], lora_up[0:32, hc], lora_bf[0:32, col0:col0 + n])
            K.act(sigd[:, col0:col0 + n], ps[:, :n], AF.Sigmoid, bias=pcol("w0", hp))
            yield
            ps = PSF()
            K.mm(ps[:, :n], lora_up[32:64, hc], lora_bf[32:64, col0:col0 + n])
            K.act(afm[:, col0:col0 + n], ps[:, :n], AF.Sigmoid, bias=pcol("a0", hp))
            yield
        K.act(kkn.v(), xr_k.v(), AF.Square, scale=pcol("k_k", hp))
        yield
        for tb in range(5):
            col0, n = TBLK[tb]
            ps = PSF()
            K.mm(ps[:, :n], blk1.v(), kkn[:, col0:col0 + n])
            K.ts("dve", csum[:, col0:col0 + n], ps[:, :n], 1e-24, None, ALU.max)
            yield
        K.act(csum.v(), csum.v(), AF.Ln)
        yield
        K.act(csum.v(), csum.v(), AF.Exp, scale=-0.5)
        yield
        K.stt(kkn.v(), xr_k.v(), pcol("k_k", hp), csum.v(), ALU.mult, ALU.mult)
        yield
        K.ts("pool", csum.v(), afm.v(), pcol("k_a", hp), omka[:, hp:hp + 1], ALU.mult, ALU.add)
        yield
        K.tt("pool", xr_k.v(), xr_k.v(), csum.v(), ALU.mult)
        yield
        kmod = xr_k
        K.tt("pool", afm.v(), kkn.v(), afm.v(), ALU.mult)
        yield
        bfm = afm
        K.stt(csum.v(), xr_r.v(), pcol("r_k", hp), kmod.v(), ALU.mult, ALU.mult)
        yield
        for tb in range(4):
            col0, n = TBLK[tb]
            ps = PSF()
            K.mm(ps[:, :n], blk1.v(), csum[:, col0:col0 + n])
            K.tt("dve", csum[:, col0:col0 + n], ps[:, :n], vbf[:, col0:col0 + n], ALU.mult)
            yield
        for c4 in range(4):
            ps = PSF()
            for j in range(4):
                c = c4 * 4 + j
                K.tr(ps[:, j * 128:(j + 1) * 128], csum[:, c * 128:(c + 1) * 128], identf.v())
            bt = btile[0]
            K.copy("act", bt.v(), ps.v())
            yield
            K.dma("sp", dap(bonus_d, c4 * 512 * RW + hp * 128, [[RW, 128], [128 * RW, 4], [1, 128]]),
                  bt.v().re("p (j f) -> p j f", j=4))
            yield
        K.act(decs.v(), sigd[:, T:TT], AF.Exp, scale=-CDEC)
        yield
        K.ts("dve", avs.v(), kkn[:, T:TT], -1.0, None, ALU.mult)
        yield
        yield

    def derivedB(hp):
        xr_r, xr_k = SC[0], SC[1]
        sigd, csum, afm, kkn = SC[2], SC[3], SC[4], SC[5]
        kmod, bfm = xr_k, afm
        K.scan(csum[:, 0:T], resetm.v(), sigd[:, 0:T], 0.0)
        gt = sigd
        c3 = lambda b_: b_[:, 0:T].re("p (c l) -> p c l", l=128)
        K.act(gt[:, 0:T], csum[:, 0:T], AF.Exp, scale=-CDEC)
        K.tt("dve", AR[:, :, 1, :], c3(xr_r), c3(gt), ALU.mult)
        K.copy("dve", gL.v(), c3(gt)[:, :, 127])
        K.stt(AR[:, :, 0, 1:128], c3(kkn)[:, :, 1:128], -1.0, c3(gt)[:, :, 0:127], ALU.mult, ALU.mult)
        K.ts("dve", AR[:, :, 0, 0], c3(kkn)[:, :, 0], -1.0, None, ALU.mult)
        K.act(gt[:, 0:T], csum[:, 0:T], AF.Exp, scale=CDEC)
        K.tt("dve", bti.v(), bfm[:, 0:T], gt[:, 0:T], ALU.mult)
        K.tt("pool", kti.v(), kmod[:, 0:T], gt[:, 0:T], ALU.mult)
        K.tt("dve", c3(gt), c3(csum)[:, :, 127:128].bc([128, 16, 128]), c3(csum), ALU.subtract)
        K.act(gt[:, 0:T], gt[:, 0:T], AF.Exp, scale=-CDEC)
        K.tt("dve", bhat.v(), bfm[:, 0:T], gt[:, 0:T], ALU.mult)
        K.tt("pool", khat.v(), kmod[:, 0:T], gt[:, 0:T], ALU.mult)
        for c in range(16):
            pb = PSB()
            pv = pb.v().re("p (q t) -> p q t", q=8)
            cs = slice(c * 128, (c + 1) * 128)
            K.tr(pv[:, 0, :], vbf[:, cs], identb.v())
            K.tr(pv[:, 1, :], bhat[:, cs], identb.v())
            K.tr(pv[:, 2, :], khat[:, cs], identb.v())
            K.tr(pv[:, 3, :], AR[:, c, 0, :], identb.v())
            K.copy("act" if c % 2 else "dve", TMq[:, c, :, :], pv[:, 0:4, :])
        sq_src = [xr_r[:, T:TT], kmod[:, T:TT], vs32.v(), decs.v(), avs.v(), bfm[:, T:TT]]
        for half in range(2):
            ps = PSF()
            for j in range(3):
                K.tr(ps[:NS, j * 128:(j + 1) * 128], sq_src[half * 3 + j], identf.v())
            K.copy("dve", smp.v(), ps[:NS, :384].re("p (q f) -> p q f", q=3))
            for j in range(3):
                q = half * 3 + j
                K.dma("sp", dap(samp_d, (2 * hp) * 448 + q * 64, [[8 * 448, NS], [448, 2], [1, 64]]),
                      smp[:, j, :].re("p (h n) -> p h n", h=2))


    def chunk_thunks(hp):
        CH = []
        CH.append(lambda: K.memset('dve', Sf.v(), 0.0))
        CH.append(lambda: K.memset('pool', Sb.v(), 0.0))
        pairs = [(c, hl) for c in range(16) for hl in range(2)]

        def pre_a(sl, c, hl):
            rows = slice(hl * 64, (hl + 1) * 64)
            cs = slice(c * 128, (c + 1) * 128)
            ps = PSF()
            arv = AR[rows, c, :, :].re("p a t -> p (a t)")
            K.mm(ps[:, 0:256], bti[rows, cs], arv)
            K.mm(ps[:, 256:512], kti[rows, cs], arv)
            ps2 = PSF()
            K.mm(ps2[:, 0:128], AR[rows, c, 0, :], bti[rows, cs])
            p4 = ps.v().re("p (q t) -> p q t", q=4)
            K.tt("dve", ATMP[sl % NPQ].v(), p4[:, 0::2, :], mask4[:, 0::2, :], ALU.mult)
            K.tt("dve", ABK[sl].v(), p4[:, 1::2, :], mask4[:, 1::2, :], ALU.mult)
            K.tt("dve", P0T[sl % NPQ].v(), ps2[:, 0:128], maskNT.v(), ALU.mult)

        def pre_lev(sl, lev, gi):
            if lev == 0:
                Pj, PjT, Mp = ATMP[sl % NPQ][:, 0, :], P0T[sl % NPQ].v(), identb.v()
            else:
                src = PMb[sl % NPQ][lev % 2]
                Pj, PjT, Mp = src[:, 0, :], src[:, 1, :], src[:, 2, :]
            dst = PMb[sl % NPQ][(lev + 1) % 2]
            ps = PSF()
            if lev < 6:
                K.mm(ps[:, 0:128], PjT, Pj)
                K.mm(ps[:, 128:256], Pj, PjT)
            K.mm(ps[:, 256:384], PjT, Mp, start=True, stop=False)
            K.mm(ps[:, 256:384], identb.v(), Mp, start=False, stop=True)
            eng = "act" if (gi % 4) != 3 else "dve"
            if lev < 6:
                K.copy(eng, dst.v(), ps[:, 0:384].re("p (q t) -> p q t", q=3))
            else:
                K.copy(eng, dst[:, 2, :], ps[:, 256:384])

        def pre_w(sl, c, hl, gi):
            rows = slice(hl * 64, (hl + 1) * 64)
            cs = slice(c * 128, (c + 1) * 128)
            Mfin = PMb[sl % NPQ][1][:, 2, :]
            ps = PSF()
            K.mm(ps[:, 0:128], TMq[:, c, 3, :], Mfin)
            K.mm(ps[:, 128:192], ATMP[sl % NPQ][:, 1, :], TMq[:, c, 0, rows])
            eng = "act" if gi % 2 else "dve"
            K.copy(eng, Wfm[rows, cs], ps[rows, 0:128])
            K.copy(eng, Ybf[sl % NPQ].v(), ps[:, 128:192])

        def pre_u(sl, gi):
            Mfin = PMb[sl % NPQ][1][:, 2, :]
            ps = PSF()
            K.mm(ps[:, 0:64], Mfin, Ybf[sl % NPQ].v())
            K.copy("act" if gi % 2 else "dve", Ubar[sl].v(), ps[:, 0:64])

        def precompute(group):
            th = []
            for (sl, c, hl) in group:
                th.append(partial(pre_a, sl, c, hl))
            for lev in range(7):
                for gi, (sl, c, hl) in enumerate(group):
                    th.append(partial(pre_lev, sl, lev, gi))
            for gi, (sl, c, hl) in enumerate(group):
                th.append(partial(pre_w, sl, c, hl, gi))
            for gi, (sl, c, hl) in enumerate(group):
                th.append(partial(pre_u, sl, gi))
            return th

        def ser_u(sl, c, hl, gi):
            rows = slice(hl * 64, (hl + 1) * 64)
            cs = slice(c * 128, (c + 1) * 128)
            ut = Ut[gi % 4]
            ps = PSF()
            K.mm(ps[:, 0:64], Wfm[rows, cs], Sb[rows, :])
            K.tt("dve", ut.v(), ps[:, 0:64], Ubar[sl].v(), ALU.add)

        def ser_os(sl, c, hl, gi):
            rows = slice(hl * 64, (hl + 1) * 64)
            ut = Ut[gi % 4]
            pso = PSF()
            K.mm(pso[:, 0:64], AR[rows, c, 1, :], Sb[rows, :], start=True, stop=False)
            K.mm(pso[:, 0:64], ABK[sl][:, 0, :], ut.v(), start=False, stop=False)
            K.mm(pso[:, 0:64], ABK[sl][:, 1, :], TMq[:, c, 0, rows], start=False, stop=True)
            pss = PSF()
            K.mm(pss[:, 0:64], TMq[:, c, 1, :], ut.v(), start=True, stop=False)
            K.mm(pss[:, 0:64], TMq[:, c, 2, :], TMq[:, c, 0, rows], start=False, stop=True)
            K.stt(Sf[rows, :], Sf[rows, :], gL[rows, c:c + 1], pss[rows, 0:64], ALU.mult, ALU.add)
            K.copy("act", Sb[rows, :], Sf[rows, :])
            ot = Otile[c % 2]
            K.copy("act", ot[:, rows], pso[:, 0:64])
            if hl == 1:
                K.dma("sp", o_d[c * 128:(c + 1) * 128, hp * 128:(hp + 1) * 128], ot.v())

        def serial(group):
            th = []
            for gi, (sl, c, hl) in enumerate(group):
                th.append(partial(ser_u, sl, c, hl, gi))
                th.append(partial(ser_os, sl, c, hl, gi))
            return th

        def run_merged(lists):
            idx = [0] * len(lists)
            while True:
                best, bf = -1, 2.0
                for li, L in enumerate(lists):
                    if idx[li] < len(L):
                        frac = idx[li] / len(L)
                        if frac < bf:
                            best, bf = li, frac
                if best < 0:
                    break
                lists[best][idx[best]]()
                idx[best] += 1

        G = 8
        groups = []
        for g0 in range(0, 32, G):
            groups.append([((g0 + i) % NPR, pairs[g0 + i][0], pairs[g0 + i][1]) for i in range(G)])
        CH.extend(precompute(groups[0]))
        for gi_ in range(len(groups)):
            lists = [serial(groups[gi_])]
            if gi_ + 1 < len(groups):
                lists.append(precompute(groups[gi_ + 1]))
            CH.extend(merge_lists(lists))
        def fin_state():
            ps = PSF()
            K.tr(ps[:64, 0:128], Sf.v(), identf.v())
            K.copy("dve", wkvo.v(), ps[:64, 0:128])
            K.dma("sp", o_wkv_p[2 * hp:2 * hp + 2, :, :].re("h v k -> v h k"), wkvo.v().re("v (h k) -> v h k", h=2))

        CH.append(fin_state)
        return CH

    def drain(gen):
        for _ in gen:
            pass

    NA_EST = 150
    drain(derivedA(0))
    derivedB(0)
    for hp in range(4):
        CH = chunk_thunks(hp)
        OUTL = []
        gen = derivedA(hp + 1) if hp + 1 < 4 else None
        per = max(1, len(CH) // NA_EST)
        pero = max(1, len(CH) // max(1, len(OUTL)))
        oi = 0
        for ti, th in enumerate(CH):
            th()
            if gen is not None and ti % per == per - 1:
                try:
                    next(gen)
                except StopIteration:
                    gen = None
            if oi < len(OUTL) and ti % pero == pero - 1:
                OUTL[oi]()
                oi += 1
        while oi < len(OUTL):
            OUTL[oi]()
            oi += 1
        if gen is not None:
            drain(gen)
        if hp + 1 < 4:
            derivedB(hp + 1)

    K.pop_scope()
    if stage == 3:
        zt = K.sb("zt", [128, 4096], F32)
        K.memset("pool", zt.v(), 0.0)
        K.dma("sp", o_wkv_s.v(), zt[:, :])
        for i in range(16):
            K.dma("sp", y_p[i * 128:(i + 1) * 128, :], zt[:, 0:D])
        K.dma("sp", y_s.v(), zt[0:NS, 0:D])
        return early()

    def bcast_row(name, src, n):
        t = K.sb(name, [128, n], F32)
        K.dma("sp", t.v(), dap(src, 0, [[0, 128], [1, n]]))
        return t

    K.push_scope()
    hgl = K.sb("hgl2", [128, 4, TT], BF16)
    K.dma("sp", hgl.v().re("p k t -> p (k t)"), hgl_d.v())
    wmo = K.sb("wmo", [128, 8, D], BF16)
    for kt in range(8):
        K.dma("pool", wmo[:, kt, :], w_merge_out[kt * 128:(kt + 1) * 128, :])
    gpost = bcast_row("gpost_mix", g_post_mix, D)
    mring = [K.sb("mring", [128, 8, 128], BF16) for _ in range(10)]
    mr_i = [0]

    def mload(src, c0, ncols_total, kt_n):
        mr_i[0] += 1
        wb = mring[mr_i[0] % len(mring)]
        K.dma("pool", wb[:, :kt_n, :], dap(src, c0, [[ncols_total, 128], [128 * ncols_total, kt_n], [1, 128]]))
        return wb

    GOFF = SHIFT + RW
    mw = {}

    def mg_load(f):
        mw[f] = (mload(w_rwkv_out, f * 128, D, 4), mload(w_in, GOFF + f * 128, PROJ, 8),
                 mload(w_in, GOFF + D + f * 128, PROJ, 8), mload(glu_w1, f * 128, D, 4), mload(glu_w2, f * 128, D, 4))

    mg_load(0)
    K.push_scope()
    eps_ln = K.sb("eps_ln2", [128, 1], F32)
    K.memset("pool", eps_ln.v(), 64e-5)
    NORF = 3
    fot_r = [K.sb("fot", [128, RW], F32) for _ in range(NORF)]
    fbt_r = [K.sb("fbt", [128, RW], F32) for _ in range(NORF)]
    fon_r = [K.sb("fon", [128, RW], F32) for _ in range(NORF)]
    fofb = [K.sb("fofb", [128, RW], BF16) for _ in range(NORF)]
    fst8 = [K.sb("fst8", [128, 6, 8], F32) for _ in range(NORF)]
    fh3 = lambda b_: b_.v().re("p (h n) -> p h n", h=8)

    def fout_a(c):
        r_ = c % NORF
        ot, bt, tmp, s8 = fot_r[r_], fbt_r[r_], fon_r[r_], fst8[r_]
        K.dma("sp", ot.v(), o_d[c * 128:(c + 1) * 128, :])
        K.dma("sp", bt.v(), bonus_d[c * 128:(c + 1) * 128, :])
        K.reduce(s8[:, 0, :], fh3(ot))
        K.act(tmp.v(), ot.v(), AF.Square)
        K.reduce(s8[:, 1, :], fh3(tmp))
        K.ts("dve", s8[:, 2, :], s8[:, 0, :], 1.0 / 64, None, ALU.mult)
        K.tt("dve", s8[:, 3, :], s8[:, 2, :], s8[:, 2, :], ALU.mult)
        K.stt(s8[:, 4, :], s8[:, 1, :], 1.0 / 64, s8[:, 3, :], ALU.mult, ALU.subtract)
        K.act(s8[:, 4, :], s8[:, 4, :], AF.Sqrt, bias=eps_ln.v(), scale=1.0)
        K.recip(s8[:, 5, :], s8[:, 4, :])
        K.stt(s8[:, 3, :], s8[:, 2, :], -1.0, s8[:, 5, :], ALU.mult, ALU.mult)

    def fout_b(c):
        r_ = c % NORF
        ot, bt, on, of, s8 = fot_r[r_], fbt_r[r_], fon_r[r_], fofb[r_], fst8[r_]
        for h in range(8):
            K.act(on[:, h * 64:(h + 1) * 64], ot[:, h * 64:(h + 1) * 64], AF.Identity,
                  bias=s8[:, 3, h:h + 1], scale=s8[:, 5, h:h + 1])
        K.tt("pool", bt.v(), bt.v(), lnxb.v(), ALU.add)
        K.tt("dve", on.v(), on.v(), lnxg.v(), ALU.mult)
        K.tt("dve", on.v(), on.v(), bt.v(), ALU.add)
        ps = PSF()
        K.mm(ps.v(), lora_bf[64:128, c * 128:(c + 1) * 128], lora_up[64:128, :])
        K.tt("dve", of.v(), on.v(), ps.v(), ALU.mult)
        pb = PSB()
        pv = pb.v().re("p (q t) -> p q t", q=8)
        for kt in range(4):
            K.tr(pv[:, kt, :], of[:, kt * 128:(kt + 1) * 128], identb.v())
        K.copy("act", oaT[:, :, c * 128:(c + 1) * 128], pv[:, 0:4, :])

    OUT = []
    for c in range(16 + 2):
        if c < 16:
            OUT.append(partial(fout_a, c))
        if 0 <= c - 2 < 16:
            OUT.append(partial(fout_b, c - 2))

    gs_tm = K.sb("gs_tm", [NS, RW], F32)
    ps = PSF()
    K.mm(ps[:NS, :], lora_bf[64:128, T:TT], lora_up[64:128, :])
    K.copy("act", gs_tm.v(), ps[:NS, :])
    K.dma("sp", dap(samp_d, 6 * 64, [[8 * 448, NS], [448, 8], [1, 64]]), gs_tm.v().re("p (h n) -> p h n", h=8))
    vec = K.sb("vec", [128, 7, 64], F32)
    K.dma("sp", vec.v().re("p q n -> p (q n)"), dap(samp_d, 0, [[448, 128], [1, 448]]))
    S0s = K.sb("S0s", [128, 64, 64], F32)
    S1s = K.sb("S1s", [128, 64, 64], F32)
    tS = K.sb("tS", [128, 64, 64], F32)
    K.dma("sp", S0s.v().re("p v k -> p (v k)"), st_wkv.v())
    r_, km_, v_, dec_, av_, b_, g_ = [vec[:, q, :] for q in range(7)]
    overv = lambda x: x.un(1).bc([128, 64, 64])
    overk = lambda x: x.un(2).bc([128, 64, 64])
    sv = K.sb("sv", [128, 8, 64], F32)
    ss = K.sb("ss", [128, 8], F32)
    eps2 = K.sb("eps_ln3", [128, 1], F32)
    K.memset("pool", eps2.v(), 64e-5)
    os_tm = K.sb("os_tm", [NS, RW], F32)
    os_bf = K.sb("os_bf", [NS, RW], BF16)
    SMP = [
        lambda: K.tt("dve", tS.v(), S0s.v(), overv(av_), ALU.mult),
        lambda: K.reduce(sv[:, 0, :], tS.v()),
        lambda: K.tt("pool", S1s.v(), S0s.v(), overv(dec_), ALU.mult),
        lambda: K.tt("dve", tS.v(), overk(sv[:, 0, :]), overv(b_), ALU.mult),
        lambda: K.tt("dve", S1s.v(), S1s.v(), tS.v(), ALU.add),
        lambda: K.tt("dve", tS.v(), overk(v_), overv(km_), ALU.mult),
        lambda: K.tt("dve", S1s.v(), S1s.v(), tS.v(), ALU.add),
        lambda: K.dma("sp", o_wkv_s.v(), S1s.v().re("p v k -> p (v k)")),
        lambda: K.tt("dve", tS.v(), S1s.v(), overv(r_), ALU.mult),
        lambda: K.reduce(sv[:, 1, :], tS.v()),
        lambda: K.reduce(ss[:, 0:1], sv[:, 1, :]),
        lambda: K.tt("dve", sv[:, 2, :], sv[:, 1, :], sv[:, 1, :], ALU.mult),
        lambda: K.reduce(ss[:, 1:2], sv[:, 2, :]),
        lambda: K.ts("dve", ss[:, 2:3], ss[:, 0:1], 1.0 / 64, None, ALU.mult),
        lambda: K.tt("dve", ss[:, 3:4], ss[:, 2:3], ss[:, 2:3], ALU.mult),
        lambda: K.stt(ss[:, 4:5], ss[:, 1:2], 1.0 / 64, ss[:, 3:4], ALU.mult, ALU.subtract),
        lambda: K.act(ss[:, 4:5], ss[:, 4:5], AF.Sqrt, bias=eps2.v(), scale=1.0),
        lambda: K.recip(ss[:, 5:6], ss[:, 4:5]),
        lambda: K.ts("dve", sv[:, 3, :], sv[:, 1, :], ss[:, 2:3], ss[:, 5:6], ALU.subtract, ALU.mult),
        lambda: K.tt("dve", sv[:, 3, :], sv[:, 3, :], par[:, 0, :], ALU.mult),
        lambda: K.tt("dve", sv[:, 3, :], sv[:, 3, :], par[:, 1, :], ALU.add),
        lambda: K.tt("dve", sv[:, 4, :], r_, km_, ALU.mult),
        lambda: K.tt("dve", sv[:, 4, :], sv[:, 4, :], par[:, 2, :], ALU.mult),
        lambda: K.reduce(ss[:, 6:7], sv[:, 4, :]),
        lambda: K.stt(sv[:, 3, :], v_, ss[:, 6:7], sv[:, 3, :], ALU.mult, ALU.add),
        lambda: K.tt("dve", sv[:, 5, :], sv[:, 3, :], g_, ALU.mult),
        lambda: K.dma("sp", dap(sampo_d, 0, [[64, 128], [1, 64]]), sv[:, 5, :]),
        lambda: K.dma("sp", os_tm.v(), sampo_d.v()),
        lambda: K.copy("dve", os_bf.v(), os_tm.v()),
    ]

    def smp_fin():
        pb = PSB()
        pv = pb.v().re("p (q t) -> p q t", q=8)
        for kt in range(4):
            K.tr(pv[:, kt, :NS], os_bf[:, kt * 128:(kt + 1) * 128], identb[:NS, :NS])
        K.copy("act", oaT[:, :, T:TT], pv[:, 0:4, :NS])

    SMP.append(smp_fin)
    run_merged_g([OUT, SMP])
    K.pop_scope()
    if stage == 4:
        zt = K.sb("zt", [128, 1024], F32)
        K.memset("pool", zt.v(), 0.0)
        for i in range(16):
            K.dma("sp", y_p[i * 128:(i + 1) * 128, :], zt[:, 0:D])
        K.dma("sp", y_s.v(), zt[0:NS, 0:D])
        return early()


    def pn_b(ps_halves, rows, res_src, gpost, dst, bufs):
        st, xres, tt_ = bufs[:3]
        K.dma("sp", xres[:rows, :], res_src)
        for hf in range(2):
            K.act(tt_[:rows, hf * 512:(hf + 1) * 512], ps_halves[hf][:rows, :], AF.Square, accum=st[:rows, hf:hf + 1])
        K.tt("dve", st[:rows, 2:3], st[:rows, 0:1], st[:rows, 1:2], ALU.add)
        K.act(st[:rows, 3:4], st[:rows, 2:3], AF.Sqrt, bias=epsT[:rows, :], scale=1.0 / D)
        K.recip(st[:rows, 4:5], st[:rows, 3:4])
        for hf in range(2):
            hs = slice(hf * 512, (hf + 1) * 512)
            K.stt(tt_[:rows, hs], ps_halves[hf][:rows, :], st[:rows, 4:5], gpost[:rows, hs], ALU.mult, ALU.mult)
        K.tt("dve", xres[:rows, :], xres[:rows, :], tt_[:rows, :], ALU.add)
        K.dma("sp", dst, xres[:rows, :])

    def pn_c(rows, bufs, nt):
        st, xres, tt_, xb = bufs
        gname, dstT, col0 = nt
        K.act(tt_[:rows, :], xres[:rows, :], AF.Square, accum=st[:rows, 5:6])
        K.act(st[:rows, 6:7], st[:rows, 5:6], AF.Sqrt, bias=epsT[:rows, :], scale=1.0 / D)
        K.recip(st[:rows, 7:8], st[:rows, 6:7])
        K.act(xb[:rows, :], xres[:rows, :], AF.Copy, scale=st[:rows, 7:8])
        pb = PSB()
        pv = pb.v().re("p (k t) -> p k t", k=8)
        for kt in range(8):
            K.tr(pv[:, kt, :rows], xb[:rows, kt * 128:(kt + 1) * 128], identb[:rows, :rows])
        g0 = PV[gname]
        K.tt("dve", dstT[:, :, col0:col0 + rows], pv[:, :, :rows],
             pvec[:, g0:g0 + 8].un(2).bc([128, 8, rows]), ALU.mult)

    mT = K.sb("mT", [128, 8, TT], BF16)
    mt = [[K.sb("mt", [128, 512], F32) for _ in range(4)] for _ in range(2)]
    mps = {}

    def mg_a1(f, tb, it):
        col0, n = TBLK[tb]
        cs = slice(col0, col0 + n)
        wA, wga, wgb, w1, w2 = mw[f]
        ps2, ps1 = PSF(), PSF()
        for kt in range(4):
            K.mm(ps2[:, :n], w2[:, kt, :], hgl[:, kt, cs], start=(kt == 0), stop=(kt == 3))
        for kt in range(4):
            K.mm(ps1[:, :n], w1[:, kt, :], hgl[:, kt, cs], start=(kt == 0), stop=(kt == 3))
        mps[(it, 1)] = (ps2, ps1)

    def mg_a2(f, tb, it):
        col0, n = TBLK[tb]
        cs = slice(col0, col0 + n)
        wA, wga, wgb, w1, w2 = mw[f]
        psgb = PSF()
        for kt in range(8):
            K.mm(psgb[:, :n], wgb[:, kt, :], hT[:, kt, cs], start=(kt == 0), stop=(kt == 7))
        mps[(it, 2)] = psgb

    def mg_a3(f, tb, it):
        col0, n = TBLK[tb]
        cs = slice(col0, col0 + n)
        wA, wga, wgb, w1, w2 = mw[f]
        psga, psA = PSF(), PSF()
        for kt in range(8):
            K.mm(psga[:, :n], wga[:, kt, :], hT[:, kt, cs], start=(kt == 0), stop=(kt == 7))
        for kt in range(4):
            K.mm(psA[:, :n], wA[:, kt, :], oaT[:, kt, cs], start=(kt == 0), stop=(kt == 3))
        mps[(it, 3)] = (psga, psA)

    def mg_b1(f, tb, it):
        col0, n = TBLK[tb]
        t0, t1_, t2_, t3_ = mt[it % 2]
        ps2, ps1 = mps.pop((it, 1))
        K.act(t0[:, :n], ps2[:, :n], AF.Sigmoid, bias=pcol("glu_b2", f))
        K.stt(t1_[:, :n], ps1[:, :n], pcol("glu_b1", f), t0[:, :n], ALU.add, ALU.mult)

    def mg_b2(f, tb, it):
        col0, n = TBLK[tb]
        t0, t1_, t2_, t3_ = mt[it % 2]
        psgb = mps.pop((it, 2))
        K.act(t2_[:, :n], psgb[:, :n], AF.Sigmoid)
        K.tt("dve", t1_[:, :n], t1_[:, :n], t2_[:, :n], ALU.mult)

    def mg_b3(f, tb, it):
        col0, n = TBLK[tb]
        cs = slice(col0, col0 + n)
        t0, t1_, t2_, t3_ = mt[it % 2]
        psga, psA = mps.pop((it, 3))
        K.act(t3_[:, :n], psga[:, :n], AF.Sigmoid)
        K.tt("dve", t3_[:, :n], psA[:, :n], t3_[:, :n], ALU.mult)
        K.tt("pool" if n > 16 else "dve", mT[:, f, cs], t3_[:, :n], t1_[:, :n], ALU.add)

    its = [(f, tb) for f in range(8) for tb in range(5)]
    for it, (f, tb) in enumerate(its):
        if tb == 0 and f + 1 < 8:
            mg_load(f + 1)
        mg_a1(f, tb, it)
        if it >= 1:
            pf, ptb = its[it - 1]
            mg_b3(pf, ptb, it - 1)
        mg_a2(f, tb, it)
        mg_b1(f, tb, it)
        mg_a3(f, tb, it)
        mg_b2(f, tb, it)
    mg_b3(its[-1][0], its[-1][1], len(its) - 1)
    pn_bufs = [(K.sb("pn_st", [128, 8], F32), K.sb("pn_x", [128, D], F32), K.sb("pn_t", [128, D], F32),
                K.sb("pn_xb", [128, D], BF16)) for _ in range(3)]
    mix_ps = {}

    def mix_a(i):
        rows = 128 if i < 16 else NS
        tcs = slice(i * 128, i * 128 + rows)
        halves = [PSF(), PSF()]
        for hf in range(2):
            for kt in range(8):
                K.mm(halves[hf][:rows, :], mT[:, kt, tcs], wmo[:, kt, hf * 512:(hf + 1) * 512],
                     start=(kt == 0), stop=(kt == 7))
        mix_ps[i] = halves

    def mix_b(i):
        rows = 128 if i < 16 else NS
        pn_b(mix_ps[i], rows, x_tile(i), gpost, x1_d[i * 128:i * 128 + rows, :], pn_bufs[i % 3])

    def mix_c(i):
        rows = 128 if i < 16 else NS
        pn_c(rows, pn_bufs[i % 3], ("g_pre_ffn", hT, i * 128))

    for step in range(17 + 2):
        if step < 17:
            mix_a(step)
        if 0 <= step - 1 < 17:
            mix_b(step - 1)
        if 0 <= step - 2 < 17:
            mix_c(step - 2)
    K.pop_scope()
    K.pop_scope()

    K.push_scope()
    NF = DFF // 128
    aT = K.sb("aT", [128, NF, TT], BF16)
    wd = K.sb("wd", [128, NF, D], BF16)
    gpost2 = bcast_row("gpost_ffn", g_post_ffn, D)
    fring = [K.sb("fring", [128, 8, 128], BF16) for _ in range(4)]
    fr_i = [0]

    def fload(src, c0):
        fr_i[0] += 1
        wb = fring[fr_i[0] % len(fring)]
        K.dma("pool", wb.v(), dap(src, c0, [[DFF, 128], [128 * DFF, 8], [1, 128]]))
        return wb

    sgt = [K.sb("sgt", [128, 512], F32) for _ in range(2)]
    it = 0
    for ft in range(NF):
        wg = fload(w_ffn_gate, ft * 128)
        wu = fload(w_ffn_up, ft * 128)
        if ft >= 1:
            K.dma("pool", wd[:, ft - 1, :], w_ffn_down[(ft - 1) * 128:ft * 128, :])
        if ft == NF - 1:
            K.dma("pool", wd[:, ft, :], w_ffn_down[ft * 128:(ft + 1) * 128, :])
        for tb in range(5):
            col0, n = TBLK[tb]
            cs = slice(col0, col0 + n)
            psg, psu = PSF(), PSF()
            for kt in range(8):
                K.mm(psg[:, :n], wg[:, kt, :], hT[:, kt, cs], start=(kt == 0), stop=(kt == 7))
            for kt in range(8):
                K.mm(psu[:, :n], wu[:, kt, :], hT[:, kt, cs], start=(kt == 0), stop=(kt == 7))
            sg = sgt[it % 2]
            it += 1
            K.act(sg[:, :n], psg[:, :n], AF.Silu)
            K.tt("dve", aT[:, ft, cs], psu[:, :n], sg[:, :n], ALU.mult)
    pn_bufs = [(K.sb("pn_st", [128, 8], F32), K.sb("pn_x", [128, D], F32), K.sb("pn_t", [128, D], F32))
               for _ in range(2)]
    dn_ps = {}

    def dn_a(i):
        rows = 128 if i < 16 else NS
        tcs = slice(i * 128, i * 128 + rows)
        halves = [PSF(), PSF()]
        for hf in range(2):
            for kt in range(NF):
                K.mm(halves[hf][:rows, :], aT[:, kt, tcs], wd[:, kt, hf * 512:(hf + 1) * 512],
                     start=(kt == 0), stop=(kt == NF - 1))
        dn_ps[i] = halves

    def dn_b(i):
        rows = 128 if i < 16 else NS
        dst = y_p[i * 128:(i + 1) * 128, :] if i < 16 else y_s[:, :]
        pn_b(dn_ps[i], rows, x1_d[i * 128:i * 128 + rows, :], gpost2, dst, pn_bufs[i % 2])

    for step in range(17 + 1):
        if step < 17:
            dn_a(step)
        if step - 1 >= 0:
            dn_b(step - 1)
    K.pop_scope()
    K.finish()
    es.close()
    return nc


_W_NAMES = ["norm_pre_mix", "norm_post_mix", "norm_pre_ffn", "norm_post_ffn", "w_in", "mu_shift", "w0",
            "w_decay_up", "a0", "w_aaa_up", "w_gate_up", "k_k", "k_a", "r_k", "lnx_g", "lnx_b", "w_rwkv_out",
            "s5_lam_re", "s5_lam_im", "s5_log_dt", "s5_b_re", "s5_b_im", "s5_c_re", "s5_c_im", "s5_d",
            "glu_w1", "glu_b1", "glu_w2", "glu_b2", "w_merge_out", "w_ffn_gate", "w_ffn_up", "w_ffn_down"]


def make_in_maps(inputs, cores):
    f = lambda a: np.ascontiguousarray(np.asarray(a, dtype=np.float32))
    shared = {n: f(inputs[n])[0] for n in _W_NAMES}
    maps = []
    for c in cores:
        m = dict(shared)
        m["x_p"] = f(inputs["x_prompt"][c])
        m["x_s"] = f(inputs["x_sample"][c * NS:(c + 1) * NS, 0])
        m["st_shift"] = f(inputs["state_shift"][0, c * NS:(c + 1) * NS])
        m["st_wkv"] = f(inputs["state_wkv"][0, c * NS:(c + 1) * NS]).reshape(NS * 8, 64 * 64)
        m["st_re"] = f(inputs["state_s5_re"][0, c * NS:(c + 1) * NS]).reshape(NS, 2048)
        m["st_im"] = f(inputs["state_s5_im"][0, c * NS:(c + 1) * NS]).reshape(NS, 2048)
        maps.append(m)
    return maps


def assemble(results):
    n = len(results)
    cat = lambda k: np.concatenate([np.asarray(r[k]) for r in results], axis=0)
    y_p = np.stack([np.asarray(r["y_p"]) for r in results], 0)
    y_s = cat("y_s").reshape(n * NS, 1, D)
    sh_p = cat("o_shift_p").reshape(1, n, SHIFT)
    wkv_p = np.stack([np.asarray(r["o_wkv_p"]) for r in results], 0).reshape(1, n, 8, 64, 64)
    re_p = cat("o_re_p").reshape(1, n, 32, 64)
    im_p = cat("o_im_p").reshape(1, n, 32, 64)
    sh_s = cat("o_shift_s").reshape(1, n * NS, SHIFT)
    wkv_s = cat("o_wkv_s").reshape(1, n * NS, 8, 64, 64)
    re_s = cat("o_re_s").reshape(1, n * NS, 32, 64)
    im_s = cat("o_im_s").reshape(1, n * NS, 32, 64)
    return tuple(np.ascontiguousarray(a, dtype=np.float32) for a in
                 (y_p, y_s, sh_p, wkv_p, re_p, im_p, sh_s, wkv_s, re_s, im_s))


def kernel(**inputs):
    nc = build()
    in_maps = make_in_maps(inputs, list(range(NCORES)))
    res = run_bass_kernel_spmd(nc, in_maps, core_ids=list(range(NCORES)))
    return assemble(res.results)
```

```python
import math
from contextlib import ExitStack
from functools import partial
import numpy as np
import concourse.bass as bass
import concourse.mybir as mybir
from concourse.bass_utils import run_bass_kernel_spmd

F32 = mybir.dt.float32
BF16 = mybir.dt.bfloat16
I32 = mybir.dt.int32
AF = mybir.ActivationFunctionType
ALU = mybir.AluOpType
AX = mybir.AxisListType

T = 2048
NS = 16
TT = T + NS
D = 1024
RW = 512
SHIFT = 1664
PROJ = 4224
DFF = 2816
NCORES = 8
CDEC = math.exp(-0.5)
TBLK = [(0, 512), (512, 512), (1024, 512), (1536, 512), (2048, 16)]
SAME_ENGINE_SYNC = True


class V:
    __slots__ = ("buf", "ap")

    def __init__(self, buf, ap):
        self.buf = buf
        self.ap = ap

    def __getitem__(self, idx):
        return V(self.buf, self.ap[idx])

    def bc(self, shape):
        return V(self.buf, self.ap.to_broadcast(list(shape)))

    def re(self, s, **kw):
        return V(self.buf, self.ap.rearrange(s, **kw))

    def un(self, axis):
        return V(self.buf, self.ap.unsqueeze(axis))

    def bitcast(self, dt):
        return V(self.buf, self.ap.bitcast(dt))


class Buf:
    def __init__(self, name, handle):
        self.name = name
        self.h = handle
        self.writes = {}
        self.reads = {}
        self.dsem = None
        self.dcnt = 0
        self.is_psum = False

    def __getitem__(self, idx):
        return V(self, self.h[idx])

    def v(self):
        return V(self, self.h[:])


class Kern:
    def __init__(self, nc, es):
        self.nc = nc
        self.es = es
        self.root_es = es
        self.eng = {"pe": nc.tensor, "act": nc.scalar, "dve": nc.vector, "pool": nc.gpsimd, "sp": nc.sync}
        self.sem = {}
        self.cnt = {}
        for e in ("pe", "act", "dve", "pool"):
            self.sem[e] = es.enter_context(nc.semaphore("s_" + e))
            self.cnt[e] = 0
        self.obs = {e: {} for e in self.eng}
        self.semname = {}
        self.dsems = []
        self.nbuf = 0
        self.psf_i = 0
        self.psb_i = 0

    def sb(self, name, shape, dt):
        self.nbuf += 1
        import os
        if os.environ.get("KDBG"):
            sz = int(np.prod(shape[1:])) * (2 if dt == BF16 else 4)
            self._tot = getattr(self, "_tot", 0) + sz
            print(f"alloc {name} {shape} {sz} depth={len(getattr(self, '_saved', []))} cum_noFree={self._tot}")
        h = self.es.enter_context(self.nc.sbuf_tensor(f"{name}_{self.nbuf}", list(shape), dt))
        return Buf(name, h)

    def barrier(self):
        for e in ("pe", "act", "dve", "pool", "sp"):
            for e2 in ("pe", "act", "dve", "pool"):
                if e2 != e and self.cnt[e2]:
                    self._wait(e, e2, self.sem[e2], self.cnt[e2])
            for ob in self.dsems:
                self._wait(e, "d:" + ob.name + str(id(ob)), ob.dsem, ob.dcnt)

    def push_scope(self):
        self._saved = getattr(self, "_saved", [])
        self._saved.append(self.es)
        self.es = ExitStack()

    def pop_scope(self):
        self.barrier()
        self.es.close()
        self.es = self._saved.pop()

    def ps(self, name, shape, dt):
        self.nbuf += 1
        h = self.es.enter_context(self.nc.psum_tensor(f"{name}_{self.nbuf}", list(shape), dt))
        b = Buf(name, h)
        b.is_psum = True
        return b

    def dram(self, name, shape, dt, kind="Internal"):
        h = self.nc.dram_tensor(name, list(shape), dt, kind=kind)
        return Buf(name, h.ap())

    def newsem(self, name):
        s = self.root_es.enter_context(self.nc.semaphore(name))
        return s

    def _wait(self, e, sem_key, sem, val):
        o = self.obs[e]
        if o.get(sem_key, 0) >= val:
            return
        self.eng[e].wait_ge(sem, val)
        o[sem_key] = val

    def _deps(self, e, W, R):
        need = {}
        for b in R:
            for k, (s, v) in b.writes.items():
                if need.get(k, (None, 0))[1] < v:
                    need[k] = (s, v)
            if b.is_psum:
                for k, (s, v) in b.reads.items():
                    if k != e and need.get(k, (None, 0))[1] < v:
                        need[k] = (s, v)
        for b in W:
            for k, (s, v) in b.writes.items():
                if need.get(k, (None, 0))[1] < v:
                    need[k] = (s, v)
            for k, (s, v) in b.reads.items():
                if need.get(k, (None, 0))[1] < v:
                    need[k] = (s, v)
        for k, (s, v) in need.items():
            if k == e:
                if e == "pe" or not SAME_ENGINE_SYNC:
                    continue
            self._wait(e, k, s, v)

    def _bufs(self, vs):
        out = []
        for x in vs:
            if isinstance(x, V):
                if x.buf not in out:
                    out.append(x.buf)
            elif isinstance(x, Buf):
                if x not in out:
                    out.append(x)
        return out

    def op(self, e, fn, W, R):
        Wb = self._bufs(W)
        Rb = self._bufs(R)
        self._deps(e, Wb, Rb)
        inst = fn()
        self.cnt[e] += 1
        c = self.cnt[e]
        inst.then_inc(self.sem[e], 1)
        for b in Wb:
            b.reads = {}
            b.writes[e] = (self.sem[e], c)
        for b in Rb:
            if b not in Wb:
                b.reads[e] = (self.sem[e], c)
        return inst

    def dma(self, q, out, in_, sem_buf=None, slow=False):
        Wb = self._bufs([out])
        Rb = self._bufs([in_])
        self._deps(q, Wb, Rb)
        ob = sem_buf if sem_buf is not None else out.buf
        if ob.dsem is None:
            ob.dsem = self.newsem("d_" + ob.name + str(len(self.dsems)))
            self.dsems.append(ob)
        if slow:
            with self.nc.allow_non_contiguous_dma(reason="small parameter layout load"):
                inst = self.eng[q].dma_start(out=out.ap, in_=in_.ap)
        else:
            inst = self.eng[q].dma_start(out=out.ap, in_=in_.ap)
        ob.dcnt += 16
        inst.then_inc(ob.dsem, 16)
        key = "d:" + ob.name + str(id(ob))
        for b in Wb:
            b.reads = {}
            b.writes[key] = (ob.dsem, ob.dcnt)
        for b in Rb:
            if b not in Wb:
                b.reads[key] = (ob.dsem, ob.dcnt)
        return inst

    def finish(self):
        for ob in self.dsems:
            self._wait("sp", "d:" + ob.name + str(id(ob)), ob.dsem, ob.dcnt)
        for e in ("pe", "act", "dve", "pool"):
            if self.cnt[e]:
                self._wait("sp", e, self.sem[e], self.cnt[e])

    @staticmethod
    def _a(x):
        return x.ap if isinstance(x, V) else x

    def act(self, out, in_, func, bias=None, scale=None, accum=None):
        kw = {}
        if bias is not None:
            kw["bias"] = self._a(bias)
        if scale is not None:
            kw["scale"] = self._a(scale)
        if accum is not None:
            kw["accum_out"] = self._a(accum)
        return self.op("act", lambda: self.nc.scalar.activation(out=out.ap, in_=in_.ap, func=func, **kw),
                       [out, accum], [in_, bias, scale])

    def tt(self, e, out, in0, in1, op):
        return self.op(e, lambda: self.eng[e].tensor_tensor(out=out.ap, in0=in0.ap, in1=in1.ap, op=op),
                       [out], [in0, in1])

    def ts(self, e, out, in0, s1, s2, op0, op1=None):
        if op1 is None:
            return self.op(e, lambda: self.eng[e].tensor_scalar(out=out.ap, in0=in0.ap, scalar1=self._a(s1),
                                                               scalar2=None, op0=op0), [out], [in0, s1])
        return self.op(e, lambda: self.eng[e].tensor_scalar(out=out.ap, in0=in0.ap, scalar1=self._a(s1),
                                                           scalar2=self._a(s2), op0=op0, op1=op1),
                       [out], [in0, s1, s2])

    def stt(self, out, in0, scalar, in1, op0, op1, e="dve"):
        return self.op(e, lambda: self.eng[e].scalar_tensor_tensor(out=out.ap, in0=in0.ap, scalar=self._a(scalar),
                                                                  in1=in1.ap, op0=op0, op1=op1),
                       [out], [in0, scalar, in1])

    def copy(self, e, out, in_):
        if e == "act":
            return self.act(out, in_, AF.Copy)
        return self.op(e, lambda: self.eng[e].tensor_copy(out=out.ap, in_=in_.ap), [out], [in_])

    def memset(self, e, out, val):
        return self.op(e, lambda: self.eng[e].memset(out.ap, val), [out], [])

    def recip(self, out, in_):
        return self.op("dve", lambda: self.nc.vector.reciprocal(out=out.ap, in_=in_.ap), [out], [in_])

    def reduce(self, out, in_, op=ALU.add, axis=AX.X):
        return self.op("dve", lambda: self.nc.vector.tensor_reduce(out=out.ap, in_=in_.ap, axis=axis, op=op),
                       [out], [in_])

    def scan(self, out, d0, d1, init, op0=ALU.mult, op1=ALU.add):
        return self.op("dve", lambda: self.nc.vector.tensor_tensor_scan(out=out.ap, data0=d0.ap, data1=d1.ap,
                                                                       initial=self._a(init), op0=op0, op1=op1),
                       [out], [d0, d1, init])

    def mm(self, out, lhsT, rhs, start=True, stop=True):
        return self.op("pe", lambda: self.nc.tensor.matmul(out.ap, lhsT=lhsT.ap, rhs=rhs.ap, start=start, stop=stop),
                       [out], [lhsT, rhs])

    def tr(self, out, in_, ident):
        return self.op("pe", lambda: self.nc.tensor.transpose(out.ap, in_.ap, ident.ap), [out], [in_, ident])

    def aselect(self, out, in_, pattern, cmp, fill, base, cm):
        return self.op("pool", lambda: self.nc.gpsimd.affine_select(out=out.ap, in_=in_.ap, pattern=pattern,
                                                                   compare_op=cmp, fill=fill, base=base,
                                                                   channel_multiplier=cm), [out], [in_])

    def iota(self, out, pattern, base, cm):
        return self.op("pool", lambda: self.nc.gpsimd.iota(out.ap, pattern=pattern, base=base,
                                                          channel_multiplier=cm), [out], [])


def run_merged_g(lists):
    idx = [0] * len(lists)
    while True:
        best, bf = -1, 2.0
        for li, L in enumerate(lists):
            if idx[li] < len(L):
                frac = idx[li] / len(L)
                if frac < bf:
                    best, bf = li, frac
        if best < 0:
            break
        lists[best][idx[best]]()
        idx[best] += 1


def merge_lists(lists):
    out = []
    idx = [0] * len(lists)
    while True:
        best, bf = -1, 2.0
        for li, L in enumerate(lists):
            if idx[li] < len(L):
                frac = idx[li] / len(L)
                if frac < bf:
                    best, bf = li, frac
        if best < 0:
            break
        out.append(lists[best][idx[best]])
        idx[best] += 1
    return out


def dap(buf, offset, ap):
    base = buf.h if not hasattr(buf.h, "ap") or isinstance(buf.h, bass.AP) else buf.h
    t = base.tensor if isinstance(base, bass.AP) else base
    return V(buf, bass.AP(t, offset, [list(x) for x in ap]))


def build(stage=99):
    nc = bass.Bass("TRN2", target_bir_lowering=False)
    es = ExitStack()
    K = Kern(nc, es)

    def din(name, shape, dt=F32):
        return Buf(name, nc.dram_tensor(name, list(shape), dt, kind="ExternalInput").ap())

    def dout(name, shape):
        return Buf(name, nc.dram_tensor(name, list(shape), F32, kind="ExternalOutput").ap())

    x_p = din("x_p", [T, D])
    x_s = din("x_s", [NS, D])
    st_shift = din("st_shift", [NS, SHIFT])
    st_wkv = din("st_wkv", [NS * 8, 64 * 64])
    st_re = din("st_re", [NS, 2048])
    st_im = din("st_im", [NS, 2048])
    g_pre_mix = din("norm_pre_mix", [D])
    g_post_mix = din("norm_post_mix", [D])
    g_pre_ffn = din("norm_pre_ffn", [D])
    g_post_ffn = din("norm_post_ffn", [D])
    w_in = din("w_in", [D, PROJ])
    mu_shift = din("mu_shift", [SHIFT])
    w0 = din("w0", [RW])
    w_decay_up = din("w_decay_up", [32, RW])
    a0 = din("a0", [RW])
    w_aaa_up = din("w_aaa_up", [32, RW])
    w_gate_up = din("w_gate_up", [64, RW])
    k_k = din("k_k", [RW])
    k_a = din("k_a", [RW])
    r_k = din("r_k", [RW])
    lnx_g = din("lnx_g", [RW])
    lnx_b = din("lnx_b", [RW])
    w_rwkv_out = din("w_rwkv_out", [RW, D])
    s5_lam_re = din("s5_lam_re", [32, 64])
    s5_lam_im = din("s5_lam_im", [32, 64])
    s5_log_dt = din("s5_log_dt", [32])
    s5_b_re = din("s5_b_re", [32, 64, 16])
    s5_b_im = din("s5_b_im", [32, 64, 16])
    s5_c_re = din("s5_c_re", [32, 16, 64])
    s5_c_im = din("s5_c_im", [32, 16, 64])
    s5_d = din("s5_d", [RW])
    glu_w1 = din("glu_w1", [RW, D])
    glu_b1 = din("glu_b1", [D])
    glu_w2 = din("glu_w2", [RW, D])
    glu_b2 = din("glu_b2", [D])
    w_merge_out = din("w_merge_out", [D, D])
    w_ffn_gate = din("w_ffn_gate", [D, DFF])
    w_ffn_up = din("w_ffn_up", [D, DFF])
    w_ffn_down = din("w_ffn_down", [DFF, D])

    y_p = dout("y_p", [T, D])
    y_s = dout("y_s", [NS, D])
    o_shift_p = dout("o_shift_p", [1, SHIFT])
    o_wkv_p = dout("o_wkv_p", [8, 64, 64])
    o_re_p = dout("o_re_p", [1, 2048])
    o_im_p = dout("o_im_p", [1, 2048])
    o_shift_s = dout("o_shift_s", [NS, SHIFT])
    o_wkv_s = dout("o_wkv_s", [NS * 8, 64 * 64])
    o_re_s = dout("o_re_s", [NS, 2048])
    o_im_s = dout("o_im_s", [NS, 2048])

    x1_d = K.dram("x1_scratch", [TT, D], F32)

    psf = [K.ps("psf", [128, 512], F32) for _ in range(6)]
    psb = [K.ps("psb", [128, 1024], BF16) for _ in range(2)]

    def PSF():
        K.psf_i += 1
        return psf[K.psf_i % len(psf)]

    def PSB():
        K.psb_i += 1
        return psb[K.psb_i % len(psb)]

    ones_f = K.sb("ones_f", [128, 128], F32)
    identf = K.sb("identf", [128, 128], F32)
    identb = K.sb("identb", [128, 128], BF16)
    epsT = K.sb("epsT", [128, 1], F32)
    K.memset("pool", ones_f.v(), 1.0)
    K.memset("pool", epsT.v(), 1e-6)
    K.aselect(identf.v(), ones_f.v(), [[-1, 128]], ALU.is_equal, 0.0, 0, 1)
    K.copy("pool", identb.v(), identf.v())

    pvec = K.sb("pvec", [128, 128], F32)
    PV = {}
    _pv = [0]

    def load_fm(name, src, ntile):
        c0 = _pv[0]
        _pv[0] += ntile
        K.dma("sp", pvec[:, c0:c0 + ntile], dap(src, 0, [[1, 128], [128, ntile]]), sem_buf=pvec, slow=True)
        PV[name] = c0
        return c0

    load_fm("g_pre_mix", g_pre_mix, 8)
    load_fm("g_pre_ffn", g_pre_ffn, 8)
    load_fm("mu", mu_shift, 13)
    load_fm("w0", w0, 4)
    load_fm("a0", a0, 4)
    load_fm("k_k", k_k, 4)
    load_fm("k_a", k_a, 4)
    load_fm("r_k", r_k, 4)
    load_fm("s5_d", s5_d, 4)
    load_fm("glu_b1", glu_b1, 8)
    load_fm("glu_b2", glu_b2, 8)

    def early():
        while getattr(K, "_saved", []):
            K.pop_scope()
        K.finish()
        es.close()
        return nc

    if stage == 0.1:
        return early()

    def pcol(name, i=0):
        c = PV[name] + i
        return pvec[:, c:c + 1]

    hT = K.sb("hT", [128, 8, TT], BF16)
    def norm_transpose(src_tile_fn, gname, dst):
        K.push_scope()
        NB = 3
        xring = [K.sb("xring", [128, D], F32) for _ in range(NB)]
        xn = [K.sb("xn", [128, D], BF16) for _ in range(NB)]
        junk = K.sb("junk", [128, D], BF16)
        stat = [K.sb("stat", [128, 4], F32) for _ in range(NB)]

        def nt_a(i):
            rows = 128 if i < 16 else NS
            xt, st, xb = xring[i % NB], stat[i % NB], xn[i % NB]
            K.dma("sp", xt[:rows, :], src_tile_fn(i))
            K.act(junk[:rows, :], xt[:rows, :], AF.Square, accum=st[:rows, 0:1])
            K.act(st[:rows, 1:2], st[:rows, 0:1], AF.Sqrt, bias=epsT[:rows, :], scale=1.0 / D)
            K.recip(st[:rows, 2:3], st[:rows, 1:2])
            K.act(xb[:rows, :], xt[:rows, :], AF.Copy, scale=st[:rows, 2:3])

        def nt_b(i):
            rows = 128 if i < 16 else NS
            col0 = i * 128
            xb = xn[i % NB]
            pb = PSB()
            pv = pb.v().re("p (k t) -> p k t", k=8)
            for kt in range(8):
                K.tr(pv[:, kt, :rows], xb[:rows, kt * 128:(kt + 1) * 128], identb[:rows, :rows])
            g0 = PV[gname]
            K.tt("dve", dst[:, :, col0:col0 + rows], pv[:, :, :rows],
                 pvec[:, g0:g0 + 8].un(2).bc([128, 8, rows]), ALU.mult)

        for step in range(17 + 1):
            if step < 17:
                nt_a(step)
            if step >= 1:
                nt_b(step - 1)
        K.pop_scope()

    def x_tile(i):
        return x_p[i * 128:(i + 1) * 128, :] if i < 16 else x_s[:, :]

    norm_transpose(x_tile, "g_pre_mix", hT)
    if stage == 0.2:
        return early()

    wring = [K.sb("wring", [128, 8, 128], BF16) for _ in range(2)]
    wr_i = [0]

    def load_w_cols(src, c0, ncols_total, kt_n=8, width=128):
        wr_i[0] += 1
        wb = wring[wr_i[0] % len(wring)]
        K.dma("pool", wb[:, :kt_n, :width],
              dap(src, c0, [[ncols_total, 128], [128 * ncols_total, kt_n], [1, width]]))
        return wb

    def proj_fm(wb, kt_n, src_act, tb, ps):
        col0, n = TBLK[tb]
        for kt in range(kt_n):
            K.mm(ps[:, :n], wb[:, kt, :], src_act[:, kt, col0:col0 + n], start=(kt == 0), stop=(kt == kt_n - 1))

    hgl_d = K.dram("hgl_scratch", [128, 4 * TT], BF16)
    TWO_PI = 2.0 * math.pi

    K.push_scope()
    hgl = K.sb("hgl", [128, 4, TT], BF16)
    ubf = K.sb("ubf", [128, 4, TT], BF16)
    wu4 = [K.sb("wu4", [128, 8, 128], BF16) for _ in range(4)]
    for ft in range(4):
        K.dma("pool", wu4[ft].v(), dap(w_in, SHIFT + ft * 128, [[PROJ, 128], [128 * PROJ, 8], [1, 128]]))
        for tb in range(5):
            col0, n = TBLK[tb]
            ps = PSF()
            proj_fm(wu4[ft], 8, hT, tb, ps)
            K.copy("act" if tb % 2 else "dve", ubf[:, ft, col0:col0 + n], ps[:, :n])

    if stage == 1.1:
        return early()
    sp_ = K.sb("s5par", [128, 16, 24], F32)
    (P_LRE, P_LIM, P_DT, P_MAG, P_TH, P_FRE, P_FIM, P_LBRE, P_LBIM, P_T0, P_T1, P_T2, P_T3,
     P_TH2, P_MAG2, P_G0R, P_G0I, P_C1, P_S1) = range(19)

    def sp(k):
        return sp_[:, :, k]

    K.dma("sp", sp(P_LRE), dap(s5_lam_re, 0, [[1, 128], [128, 16]]), slow=True)
    K.dma("sp", sp(P_LIM), dap(s5_lam_im, 0, [[1, 128], [128, 16]]), slow=True)
    for gl in range(2):
        K.dma("sp", sp_[gl * 64:(gl + 1) * 64, :, P_DT], dap(s5_log_dt, gl, [[0, 64], [2, 16]]), slow=True)
    K.act(sp(P_DT), sp(P_DT), AF.Exp)
    K.tt("dve", sp(P_T0), sp(P_LRE), sp(P_DT), ALU.mult)
    K.act(sp(P_MAG), sp(P_T0), AF.Exp)
    K.tt("dve", sp(P_MAG2), sp(P_MAG), sp(P_MAG), ALU.mult)
    K.tt("dve", sp(P_TH), sp(P_LIM), sp(P_DT), ALU.mult)
    K.ts("dve", sp(P_TH2), sp(P_TH), 2.0, None, ALU.mult)

    LC = 128
    cosT = K.sb("cosT", [128, 16, LC], F32)
    sinT = K.sb("sinT", [128, 16, LC], F32)
    BT = [K.sb("BT", [128, 32, 128], BF16) for _ in range(2)]
    CTb = K.sb("CTb", [128, 32, 128], BF16)
    VTb = K.sb("VTb", [128, 32, 128], BF16)
    Kt0 = K.sb("Kt0", [128, 4, 128], BF16)
    K.push_scope()
    ji = K.sb("ji", [128, LC], I32)
    jf = K.sb("jf", [128, LC], F32)
    ang = K.sb("ang", [128, 16, LC], F32)
    rr = K.sb("rr", [128, 16, LC], F32)
    kf = K.sb("kf", [128, 16, LC], F32)
    ki = K.sb("ki", [128, 16, LC], I32)
    K.iota(ji.v(), [[1, LC]], 1, 0)
    K.copy("dve", jf.v(), ji.v())
    K.tt("dve", ang.v(), sp(P_TH2).un(2).bc([128, 16, LC]), jf.v().un(1).bc([128, 16, LC]), ALU.mult)

    def sin_reduced(out, a_in, shift, rr_, kf_, ki_):
        C1 = 6.28125
        C2 = TWO_PI - C1
        K.ts("dve", rr_, a_in, shift, 1.0 / TWO_PI, ALU.add, ALU.mult)
        K.copy("dve", ki_, rr_)
        K.copy("dve", kf_, ki_)
        K.ts("dve", rr_, a_in, shift, None, ALU.add)
        K.stt(rr_, kf_, -C1, rr_, ALU.mult, ALU.add)
        K.stt(rr_, kf_, -C2, rr_, ALU.mult, ALU.add)
        K.ts("dve", kf_, rr_, math.pi, None, ALU.is_gt)
        K.stt(rr_, kf_, -TWO_PI, rr_, ALU.mult, ALU.add)
        K.ts("dve", kf_, rr_, -math.pi, None, ALU.is_lt)
        K.stt(rr_, kf_, TWO_PI, rr_, ALU.mult, ALU.add)
        K.ts("dve", rr_, rr_, -math.pi, math.pi, ALU.max, ALU.min)
        K.act(out, rr_, AF.Sin)

    sin_reduced(sinT.v(), ang.v(), 0.0, rr.v(), kf.v(), ki.v())
    sin_reduced(cosT.v(), ang.v(), 0.5 * math.pi, rr.v(), kf.v(), ki.v())
    sin_reduced(sp(P_S1), sp(P_TH), 0.0, rr[:, :, 0], kf[:, :, 0], ki[:, :, 0])
    sin_reduced(sp(P_C1), sp(P_TH), 0.5 * math.pi, rr[:, :, 0], kf[:, :, 0], ki[:, :, 0])
    K.tt("dve", sp(P_LBRE), sp(P_MAG), sp(P_C1), ALU.mult)
    K.tt("dve", sp(P_LBIM), sp(P_MAG), sp(P_S1), ALU.mult)
    K.tt("dve", sp(P_T0), sp(P_LRE), sp(P_LRE), ALU.mult)
    K.tt("dve", sp(P_T1), sp(P_LIM), sp(P_LIM), ALU.mult)
    K.tt("dve", sp(P_T0), sp(P_T0), sp(P_T1), ALU.add)
    K.recip(sp(P_T0), sp(P_T0))
    K.ts("dve", sp(P_T1), sp(P_LBRE), -1.0, None, ALU.add)
    K.tt("dve", sp(P_T2), sp(P_T1), sp(P_LRE), ALU.mult)
    K.tt("dve", sp(P_T3), sp(P_LBIM), sp(P_LIM), ALU.mult)
    K.tt("dve", sp(P_T2), sp(P_T2), sp(P_T3), ALU.add)
    K.tt("dve", sp(P_FRE), sp(P_T2), sp(P_T0), ALU.mult)
    K.tt("dve", sp(P_T2), sp(P_LBIM), sp(P_LRE), ALU.mult)
    K.tt("dve", sp(P_T3), sp(P_T1), sp(P_LIM), ALU.mult)
    K.tt("dve", sp(P_T2), sp(P_T2), sp(P_T3), ALU.subtract)
    K.tt("dve", sp(P_FIM), sp(P_T2), sp(P_T0), ALU.mult)
    K.tt("dve", sp(P_T0), sp(P_LBRE), sp(P_FRE), ALU.mult)
    K.tt("dve", sp(P_T1), sp(P_LBIM), sp(P_FIM), ALU.mult)
    K.tt("dve", sp(P_G0R), sp(P_T0), sp(P_T1), ALU.subtract)
    K.tt("dve", sp(P_T0), sp(P_LBRE), sp(P_FIM), ALU.mult)
    K.tt("dve", sp(P_T1), sp(P_LBIM), sp(P_FRE), ALU.mult)
    K.tt("dve", sp(P_G0I), sp(P_T0), sp(P_T1), ALU.add)
    K.pop_scope()
    if stage == 1.2:
        return early()

    K.push_scope()
    Xr = K.sb("Xr", [128, 16 * 128], F32)
    Xi = K.sb("Xi", [128, 16 * 128], F32)
    for X, src in ((Xr, s5_b_re), (Xi, s5_b_im)):
        K.memset("pool", X.v(), 0.0)
        for gl in range(2):
            for a in range(4):
                K.dma("sp", dap(X, gl * 64 * 2048 + 16 * gl + a * 512, [[2048, 64], [160, 4], [1, 16]]),
                      dap(src, gl * 1024 + a * 8192, [[16, 64], [2048, 4], [1, 16]]), slow=True)
    CTf = [K.sb("CTf", [128, 16, 128], F32) for _ in range(2)]
    Zc = K.sb("Zc", [128, 4 * 512], F32)
    for part, src in enumerate((s5_c_re, s5_c_im)):
        K.memset("pool", Zc.v(), 0.0)
        for g8 in range(8):
            K.dma("sp", dap(Zc, 16 * g8 * 2048 + 64 * g8, [[2048, 16], [512, 4], [1, 64]]),
                  dap(src, g8 * 1024, [[64, 16], [8192, 4], [1, 64]]))
        for q in range(4):
            ps = PSF()
            for j in range(4):
                K.tr(ps[:, j * 128:(j + 1) * 128], Zc[:, q * 512 + j * 128:q * 512 + (j + 1) * 128], identf.v())
            dst = CTf[part][:, q * 4:q * 4 + 4, :]
            if part == 0:
                K.copy("dve", dst, ps.v().re("p (j c) -> p j c", j=4))
            else:
                K.ts("dve", dst, ps.v().re("p (j c) -> p j c", j=4), -1.0, None, ALU.mult)
    c4v = CTb.v().re("p (i two) c -> p i two c", two=2)
    v4v = VTb.v().re("p (i two) c -> p i two c", two=2)
    K.copy("act", c4v[:, :, 0, :], CTf[0].v())
    K.copy("act", c4v[:, :, 1, :], CTf[1].v())
    T1 = K.sb("T1", [128, 16, 128], F32)
    T2 = K.sb("T2", [128, 16, 128], F32)
    bcs = lambda k: sp(k).un(2).bc([128, 16, 128])
    K.tt("dve", T1.v(), CTf[0].v(), bcs(P_LBRE), ALU.mult)
    K.tt("pool", T2.v(), CTf[1].v(), bcs(P_LBIM), ALU.mult)
    K.tt("dve", v4v[:, :, 0, :], T1.v(), T2.v(), ALU.add)
    K.tt("dve", T1.v(), CTf[1].v(), bcs(P_LBRE), ALU.mult)
    K.tt("pool", T2.v(), CTf[0].v(), bcs(P_LBIM), ALU.mult)
    K.tt("dve", v4v[:, :, 1, :], T1.v(), T2.v(), ALU.subtract)
    Xs = [K.sb("Xs", [128, 16, 128], F32) for _ in range(2)]
    x3 = lambda b_: b_.v().re("p (i c) -> p i c", i=16)
    for v, (gr, gi) in enumerate(((P_G0R, P_G0I), (P_FRE, P_FIM))):
        K.tt("dve", T1.v(), x3(Xr), bcs(gr), ALU.mult)
        K.tt("pool", T2.v(), x3(Xi), bcs(gi), ALU.mult)
        K.tt("dve", Xs[0].v(), T1.v(), T2.v(), ALU.subtract)
        K.tt("dve", T1.v(), x3(Xr), bcs(gi), ALU.mult)
        K.tt("pool", T2.v(), x3(Xi), bcs(gr), ALU.mult)
        K.tt("dve", Xs[1].v(), T1.v(), T2.v(), ALU.add)
        for part in range(2):
            for i4 in range(4):
                ps = PSF()
                for j in range(4):
                    K.tr(ps[:, j * 128:(j + 1) * 128], Xs[part][:, i4 * 4 + j, :], identf.v())
                dstb = BT[v].v().re("p (i two) c -> p i two c", two=2)[:, i4 * 4:i4 * 4 + 4, part, :]
                K.copy("act" if i4 % 2 else "dve", dstb, ps.v().re("p (j c) -> p j c", j=4))
        if v == 1:
            for q in range(4):
                ps = PSF()
                for j in range(4):
                    i = q * 4 + j
                    K.mm(ps[:, 0:128], Xs[0][:, i, :], CTf[0][:, i, :], start=(j == 0), stop=False)
                    K.mm(ps[:, 0:128], Xs[1][:, i, :], CTf[1][:, i, :], start=False, stop=(j == 3))
                K.copy("dve", Kt0[:, q, :], ps[:, 0:128])
    K.pop_scope()
    if stage == 1.3:
        return early()

    x1b = [K.sb("x1sb", [128, 16, NS], BF16) for _ in range(2)]
    K.push_scope()
    s0_tm = [K.sb("s0tm", [NS, 2048], F32) for _ in range(2)]
    s0 = [K.sb("s0", [128, 16, NS], F32) for _ in range(2)]
    K.dma("sp", s0_tm[0].v(), st_re.v())
    K.dma("sp", s0_tm[1].v(), st_im.v())
    for part in range(2):
        ps = PSF()
        for i in range(16):
            K.tr(ps[:, i * NS:(i + 1) * NS], s0_tm[part][:, i * 128:(i + 1) * 128], identf[:NS, :NS])
        K.copy("dve", s0[part].v(), ps[:, :16 * NS].re("p (i n) -> p i n", i=16))
    psr = PSF()
    psi = PSF()
    for i in range(16):
        K.mm(psr[:, i * NS:(i + 1) * NS], BT[1][:, 2 * i, :], ubf[:, i // 4, T:TT])
        K.mm(psi[:, i * NS:(i + 1) * NS], BT[1][:, 2 * i + 1, :], ubf[:, i // 4, T:TT])
    sw = [K.sb("sw", [128, 16, NS], F32) for _ in range(2)]
    x1 = [K.sb("x1s", [128, 16, NS], F32) for _ in range(2)]
    bc16 = lambda k: sp(k).un(2).bc([128, 16, NS])
    pr3 = psr[:, :16 * NS].re("p (i n) -> p i n", i=16)
    pi3 = psi[:, :16 * NS].re("p (i n) -> p i n", i=16)
    K.tt("dve", sw[0].v(), s0[0].v(), bc16(P_LBRE), ALU.mult)
    K.tt("dve", sw[1].v(), s0[1].v(), bc16(P_LBIM), ALU.mult)
    K.tt("dve", sw[0].v(), sw[0].v(), sw[1].v(), ALU.subtract)
    K.tt("dve", x1[0].v(), sw[0].v(), pr3, ALU.add)
    K.tt("dve", sw[0].v(), s0[1].v(), bc16(P_LBRE), ALU.mult)
    K.tt("dve", sw[1].v(), s0[0].v(), bc16(P_LBIM), ALU.mult)
    K.tt("dve", sw[0].v(), sw[0].v(), sw[1].v(), ALU.add)
    K.tt("dve", x1[1].v(), sw[0].v(), pi3, ALU.add)
    for part in range(2):
        K.copy("act", x1b[part].v(), x1[part].v())
        xs_tm = s0_tm[part]
        for i4 in range(4):
            ps = PSF()
            for j in range(4):
                i = i4 * 4 + j
                K.tr(ps[:NS, j * 128:(j + 1) * 128], x1[part][:, i, :], identf.v())
            K.copy("dve", xs_tm[:, i4 * 512:(i4 + 1) * 512], ps[:NS, :])
        K.dma("sp", (o_re_s if part == 0 else o_im_s).v(), xs_tm.v())
    K.pop_scope()
    if stage == 1.4:
        return early()

    NH = 256
    carry = K.sb("carry", [128, 16, 2], F32)
    K.memset("dve", carry.v(), 0.0)
    UN = [dict(bR=K.sb("bR", [128, NH], F32), bI=K.sb("bI", [128, NH], F32),
               wR=K.sb("wR", [128, NH], F32), wI=K.sb("wI", [128, NH], F32),
               t1=K.sb("t1", [128, NH], F32), t2=K.sb("t2", [128, NH], F32)) for _ in range(4)]
    xbr = [K.sb("xbr", [128, NH + 1], BF16) for _ in range(4)]
    xbi = [K.sb("xbi", [128, NH + 1], BF16) for _ in range(4)]
    yv = [K.sb("yv", [128, 512], F32) for _ in range(2)]
    y2 = [K.sb("y2", [128, 512], F32) for _ in range(2)]

    def gelu_to(dst_bf, yv_, y2_):
        K.act(y2_, yv_, AF.Square)
        K.act(y2_, y2_, AF.Identity, bias=1.0, scale=0.044715)
        K.tt("pool", y2_, y2_, yv_, ALU.mult)
        K.act(y2_, y2_, AF.Sigmoid, scale=1.5957691216057308)
        K.tt("pool", dst_bf, y2_, yv_, ALU.mult)

    v2 = lambda b: b.v().re("p (a l) -> p a l", a=2)
    for q in range(4):
        for tb in range(4):
            col0, n = TBLK[tb]
            upair = ubf[:, q, col0:col0 + n].re("p (c j) -> p j c", j=2)
            ue, uo = upair[:, 0, :], upair[:, 1, :]
            for j in range(4):
                i = q * 4 + j
                U = UN[j]
                psr = PSF()
                psi = PSF()
                K.mm(psr[:, :NH], BT[0][:, 2 * i, :], ue, start=True, stop=False)
                K.mm(psr[:, :NH], BT[1][:, 2 * i, :], uo, start=False, stop=True)
                K.mm(psi[:, :NH], BT[0][:, 2 * i + 1, :], ue, start=True, stop=False)
                K.mm(psi[:, :NH], BT[1][:, 2 * i + 1, :], uo, start=False, stop=True)
                tC = cosT[:, i, :].un(1).bc([128, 2, LC])
                tS = sinT[:, i, :].un(1).bc([128, 2, LC])
                pr2 = psr[:, :NH].re("p (a l) -> p a l", a=2)
                pi2 = psi[:, :NH].re("p (a l) -> p a l", a=2)
                K.tt("dve", v2(U["t1"]), pr2, tC, ALU.mult)
                K.tt("dve", v2(U["t2"]), pi2, tS, ALU.mult)
                K.tt("pool", U["bR"].v(), U["t1"].v(), U["t2"].v(), ALU.add)
                K.tt("dve", v2(U["wR"]), pi2, tC, ALU.mult)
                K.tt("dve", v2(U["wI"]), pr2, tS, ALU.mult)
                K.tt("dve", U["bI"].v(), U["wR"].v(), U["wI"].v(), ALU.subtract)
                K.copy("act", xbr[j][:, 0:1], carry[:, i, 0:1])
                K.copy("act", xbi[j][:, 0:1], carry[:, i, 1:2])
            for a in range(2):
                cs = slice(a * LC, (a + 1) * LC)
                for j in range(4):
                    i = q * 4 + j
                    U = UN[j]
                    rho = sp_[:, i, P_MAG2:P_MAG2 + 1].bc([128, LC])
                    if a == 0:
                        ire, iim = carry[:, i, 0:1], carry[:, i, 1:2]
                    else:
                        ire, iim = U["bR"][:, a * LC - 1:a * LC], U["bI"][:, a * LC - 1:a * LC]
                    K.scan(U["wR"][:, cs], rho, U["bR"][:, cs], ire)
                    K.scan(U["wI"][:, cs], rho, U["bI"][:, cs], iim)
                for j in range(4):
                    i = q * 4 + j
                    U = UN[j]
                    K.tt("pool", U["t1"][:, cs], U["wR"][:, cs], cosT[:, i, :], ALU.mult)
                    K.tt("dve", U["t2"][:, cs], U["wI"][:, cs], sinT[:, i, :], ALU.mult)
                    K.tt("dve", U["bR"][:, cs], U["t1"][:, cs], U["t2"][:, cs], ALU.subtract)
                    K.tt("pool", U["t1"][:, cs], U["wR"][:, cs], sinT[:, i, :], ALU.mult)
                    K.tt("dve", U["t2"][:, cs], U["wI"][:, cs], cosT[:, i, :], ALU.mult)
                    K.tt("dve", U["bI"][:, cs], U["t1"][:, cs], U["t2"][:, cs], ALU.add)
            for j in range(4):
                i = q * 4 + j
                U = UN[j]
                K.copy("dve", carry[:, i, 0:1], U["bR"][:, NH - 1:NH])
                K.copy("dve", carry[:, i, 1:2], U["bI"][:, NH - 1:NH])
                K.copy("act", xbr[j][:, 1:NH + 1], U["bR"].v())
                K.copy("act", xbi[j][:, 1:NH + 1], U["bI"].v())
            pse = PSF()
            pso = PSF()
            for j in range(4):
                i = q * 4 + j
                K.mm(pso[:, :NH], CTb[:, 2 * i, :], xbr[j][:, 1:NH + 1], start=(j == 0), stop=False)
                K.mm(pso[:, :NH], CTb[:, 2 * i + 1, :], xbi[j][:, 1:NH + 1], start=False, stop=(j == 3))
            for j in range(4):
                i = q * 4 + j
                K.mm(pse[:, :NH], VTb[:, 2 * i, :], xbr[j][:, 0:NH], start=(j == 0), stop=False)
                K.mm(pse[:, :NH], VTb[:, 2 * i + 1, :], xbi[j][:, 0:NH], start=False, stop=False)
            K.mm(pse[:, :NH], Kt0[:, q, :], ue, start=False, stop=True)
            yy = yv[tb % 2]
            yyp = yy.v().re("p (j c) -> p j c", j=2)
            K.copy("act", yyp[:, 0, :], pse[:, :NH])
            K.copy("act", yyp[:, 1, :], pso[:, :NH])
            psu = PSF()
            proj_fm(wu4[q], 8, hT, tb, psu)
            K.stt(yyp, psu.v().re("p (c j) -> p j c", j=2), pcol("s5_d", q), yyp, ALU.mult, ALU.add)
            y2p = y2[tb % 2].v().re("p (j c) -> p j c", j=2)
            gelu_to(hgl[:, q, col0:col0 + n].re("p (c j) -> p j c", j=2), yyp, y2p)
        psy = PSF()
        for j in range(4):
            i = q * 4 + j
            K.mm(psy[:, :NS], CTb[:, 2 * i, :], x1b[0][:, i, :], start=(j == 0), stop=False)
            K.mm(psy[:, :NS], CTb[:, 2 * i + 1, :], x1b[1][:, i, :], start=False, stop=(j == 3))
        K.copy("act", yv[0][:, :NS], psy[:, :NS])
        psu = PSF()
        proj_fm(wu4[q], 8, hT, 4, psu)
        K.stt(yv[0][:, :NS], psu[:, :NS], pcol("s5_d", q), yv[0][:, :NS], ALU.mult, ALU.add)
        gelu_to(hgl[:, q, T:TT], yv[0][:, :NS], y2[0][:, :NS])
    if stage == 1.5:
        return early()
    K.dma("sp", dap(o_re_p, 0, [[1, 128], [128, 16]]), carry[:, :, 0], slow=True)
    K.dma("sp", dap(o_im_p, 0, [[1, 128], [128, 16]]), carry[:, :, 1], slow=True)
    K.dma("sp", hgl_d.v(), hgl.v().re("p k t -> p (k t)"))
    K.pop_scope()
    if stage == 2:
        zt = K.sb("zt", [128, 4096], F32)
        K.memset("pool", zt.v(), 0.0)
        K.dma("sp", o_shift_p.v(), zt[0:1, 0:SHIFT])
        K.dma("sp", o_shift_s.v(), zt[0:NS, 0:SHIFT])
        K.dma("sp", o_wkv_p.v().re("h v k -> h (v k)"), zt[0:8, :])
        K.dma("sp", o_wkv_s.v(), zt[:, :])
        for i in range(16):
            K.dma("sp", y_p[i * 128:(i + 1) * 128, :], zt[:, 0:D])
        K.dma("sp", y_s.v(), zt[0:NS, 0:D])
        return early()
    K.push_scope()
    lnxg = K.sb("lnxg", [128, RW], F32)
    lnxb = K.sb("lnxb", [128, RW], F32)
    K.dma("sp", lnxg.v(), dap(lnx_g, 0, [[0, 128], [1, RW]]))
    K.dma("sp", lnxb.v(), dap(lnx_b, 0, [[0, 128], [1, RW]]))
    par = K.sb("spar", [128, 3, 64], F32)
    for tk in range(NS):
        for j, src in enumerate((lnx_g, lnx_b, r_k)):
            K.dma("sp", par[tk * 8:(tk + 1) * 8, j, :], dap(src, 0, [[64, 8], [1, 64]]), sem_buf=par)

    lora_bf = K.sb("lora_bf", [128, TT], BF16)
    lora_up = K.sb("lora_up", [128, RW], BF16)
    oaT = K.sb("oaT", [128, 4, TT], BF16)
    o_d = K.dram("o_scratch", [T, RW], F32)
    bonus_d = K.dram("bonus_scratch", [T, RW], F32)
    samp_d = K.dram("samp_scratch", [NS * 8, 7, 64], F32)
    sampo_d = K.dram("sampo_scratch", [NS, RW], F32)

    K.push_scope()
    mask4 = K.sb("mask4", [128, 4, 128], F32)
    maskNT = K.sb("maskNT", [128, 128], F32)
    resetm = K.sb("resetm", [128, T], BF16)
    blk1 = K.sb("blk1", [128, 128], F32)
    eps_ln = K.sb("eps_ln", [128, 1], F32)
    K.memset("pool", eps_ln.v(), 64e-5)
    for j in range(4):
        K.aselect(mask4[:, j, :], ones_f.v(), [[1, 128]], ALU.is_gt if j % 2 == 0 else ALU.is_ge, 0.0, 0, -1)
    K.aselect(maskNT.v(), ones_f.v(), [[-1, 128]], ALU.is_gt, 0.0, 0, 1)
    K.memset("pool", resetm.v(), 1.0)
    K.memset("pool", resetm.v().re("p (c l) -> p c l", l=128)[:, :, 0:1], 0.0)
    K.memset("pool", blk1.v(), 0.0)
    K.memset("pool", blk1[0:64, 0:64], 1.0)
    K.memset("pool", blk1[64:128, 64:128], 1.0)

    K.dma("pool", lora_up[0:32, :], w_decay_up.v())
    K.dma("pool", lora_up[32:64, :], w_aaa_up.v())
    K.dma("pool", lora_up[64:128, :], w_gate_up.v())
    sh0T = K.sb("sh0T", [128, 13, NS], F32)
    shout = [K.sb("shout", [NS + 1, 128], F32) for _ in range(1)]
    K.push_scope()
    sh_tm = K.sb("sh_tm", [NS, SHIFT], F32)
    K.dma("sp", sh_tm.v(), st_shift.v())
    ps = PSF()
    for ft in range(13):
        K.tr(ps[:, ft * NS:(ft + 1) * NS], sh_tm[:, ft * 128:(ft + 1) * 128], identf[:NS, :NS])
    K.copy("dve", sh0T.v(), ps[:, :13 * NS].re("p (f n) -> p f n", f=13))
    K.pop_scope()

    omm = K.sb("omm", [128, 13], F32)
    K.ts("dve", omm.v(), pvec[:, PV["mu"]:PV["mu"] + 13], -1.0, 1.0, ALU.mult, ALU.add)
    omka = K.sb("omka", [128, 4], F32)
    K.ts("dve", omka.v(), pvec[:, PV["k_a"]:PV["k_a"] + 4], -1.0, 1.0, ALU.mult, ALU.add)

    SC = [K.sb("scr", [128, TT], F32) for _ in range(6)]

    def proj_shift_tile(ft, pr, tmp, xr_out, xr_out_dt_bf=None, samp32=None):
        wb = load_w_cols(w_in, ft * 128, PROJ)
        for tb in range(5):
            col0, n = TBLK[tb]
            ps = PSF()
            proj_fm(wb, 8, hT, tb, ps)
            K.copy("act", pr[:, col0:col0 + n], ps[:, :n])
            K.act(tmp[:, col0:col0 + n], ps[:, :n], AF.Copy, scale=omm[:, ft:ft + 1])
        ps = PSF()
        K.tr(ps[:NS + 1, :128], pr[:, T - 1:TT], identf.v())
        so = shout[0]
        K.copy("dve", so.v(), ps[:NS + 1, :128])
        K.dma("sp", o_shift_p[0:1, ft * 128:(ft + 1) * 128], so[0:1, :])
        K.dma("sp", o_shift_s[:, ft * 128:(ft + 1) * 128], so[1:NS + 1, :])
        mu_c = pcol("mu", ft)
        K.stt(xr_out[:, 1:T], pr[:, 0:T - 1], mu_c, tmp[:, 1:T], ALU.mult, ALU.add)
        K.copy("pool", xr_out[:, 0:1], tmp[:, 0:1])
        K.stt(xr_out[:, T:TT], sh0T[:, ft, :], mu_c, tmp[:, T:TT], ALU.mult, ALU.add)
        if samp32 is not None:
            K.stt(samp32, sh0T[:, ft, :], mu_c, tmp[:, T:TT], ALU.mult, ALU.add)

    proj_shift_tile(12, SC[0], SC[1], SC[2].v())
    K.act(lora_bf[0:32, :], SC[2][0:32, :], AF.Tanh)
    K.act(lora_bf[32:64, :], SC[2][32:64, :], AF.Copy)
    K.act(lora_bf[64:128, :], SC[2][64:128, :], AF.Sigmoid)

    vbf = K.sb("vbf", [128, TT], BF16)
    vs32 = K.sb("vs32", [128, NS], F32)
    AR = K.sb("AR", [128, 16, 2, 128], BF16)
    bti = K.sb("bti", [128, T], BF16)
    kti = K.sb("kti", [128, T], BF16)
    bhat = K.sb("bhat", [128, T], BF16)
    khat = K.sb("khat", [128, T], BF16)
    Wfm = bhat
    TMq = K.sb("TMq", [128, 16, 4, 128], BF16)
    gL = K.sb("gL", [128, 16], F32)
    Sf = K.sb("Sf", [128, 64], F32)
    Sb = K.sb("Sb", [128, 64], BF16)
    smp = K.sb("smp", [NS, 3, 128], F32)
    decs = K.sb("decs", [128, NS], F32)
    avs = K.sb("avs", [128, NS], F32)
    NPR = 16
    ABK = [K.sb("ABK", [128, 2, 128], BF16) for _ in range(NPR)]
    NPQ = 8
    ATMP = [K.sb("ATMP", [128, 2, 128], BF16) for _ in range(NPQ)]
    P0T = [K.sb("P0T", [128, 128], BF16) for _ in range(NPQ)]
    PMb = [[K.sb("PMb", [128, 3, 128], BF16) for _ in range(2)] for _ in range(NPQ)]
    Ybf = [K.sb("Ybf", [128, 64], BF16) for _ in range(NPQ)]
    Ubar = [K.sb("Ubar", [128, 64], F32) for _ in range(NPR)]
    Ut = [K.sb("Ut", [128, 64], BF16) for _ in range(4)]
    Otile = [K.sb("Otile", [128, 128], F32) for _ in range(2)]
    btile = [K.sb("btile", [128, 512], F32) for _ in range(1)]
    wkvo = K.sb("wkvo", [64, 128], F32)

    def derivedA(hp):
        xr_r, xr_k = SC[0], SC[1]
        proj_shift_tile(0 * 4 + hp, SC[2], SC[3], xr_r.v())
        yield
        proj_shift_tile(1 * 4 + hp, SC[2], SC[3], xr_k.v())
        yield
        proj_shift_tile(2 * 4 + hp, SC[2], SC[3], vbf.v(), samp32=vs32.v())
        yield
        sigd, csum, afm, kkn = SC[2], SC[3], SC[4], SC[5]
        hc = slice(hp * 128, (hp + 1) * 128)
        for tb in range(5):
            col0, n = TBLK[tb]
            ps = PSF()
            K.mm(ps[:, :n], lora_up[0:32, hc], lora_bf[0:32, col0:col0 + n])
            K.act(sigd[:, col0:col0 + n], ps[:, :n], AF.Sigmoid, bias=pcol("w0", hp))
            yield
            ps = PSF()
            K.mm(ps[:, :n], lora_up[32:64, hc], lora_bf[32:64, col0:col0 + n])
            K.act(afm[:, col0:col0 + n], ps[:, :n], AF.Sigmoid, bias=pcol("a0", hp))
            yield
        K.act(kkn.v(), xr_k.v(), AF.Square, scale=pcol("k_k", hp))
        yield
        for tb in range(5):
            col0, n = TBLK[tb]
            ps = PSF()
            K.mm(ps[:, :n], blk1.v(), kkn[:, col0:col0 + n])
            K.ts("dve", csum[:, col0:col0 + n], ps[:, :n], 1e-24, None, ALU.max)
            yield
        K.act(csum.v(), csum.v(), AF.Ln)
        yield
        K.act(csum.v(), csum.v(), AF.Exp, scale=-0.5)
        yield
        K.stt(kkn.v(), xr_k.v(), pcol("k_k", hp), csum.v(), ALU.mult, ALU.mult)
        yield
        K.ts("pool", csum.v(), afm.v(), pcol("k_a", hp), omka[:, hp:hp + 1], ALU.mult, ALU.add)
        yield
        K.tt("pool", xr_k.v(), xr_k.v(), csum.v(), ALU.mult)
        yield
        kmod = xr_k
        K.tt("pool", afm.v(), kkn.v(), afm.v(), ALU.mult)
        yield
        bfm = afm
        K.stt(csum.v(), xr_r.v(), pcol("r_k", hp), kmod.v(), ALU.mult, ALU.mult)
        yield
        for tb in range(4):
            col0, n = TBLK[tb]
            ps = PSF()
            K.mm(ps[:, :n], blk1.v(), csum[:, col0:col0 + n])
            K.tt("dve", csum[:, col0:col0 + n], ps[:, :n], vbf[:, col0:col0 + n], ALU.mult)
            yield
        for c4 in range(4):
            ps = PSF()
            for j in range(4):
                c = c4 * 4 + j
                K.tr(ps[:, j * 128:(j + 1) * 128], csum[:, c * 128:(c + 1) * 128], identf.v())
            bt = btile[0]
            K.copy("act", bt.v(), ps.v())
            yield
            K.dma("sp", dap(bonus_d, c4 * 512 * RW + hp * 128, [[RW, 128], [128 * RW, 4], [1, 128]]),
                  bt.v().re("p (j f) -> p j f", j=4))
            yield
        K.act(decs.v(), sigd[:, T:TT], AF.Exp, scale=-CDEC)
        yield
        K.ts("dve", avs.v(), kkn[:, T:TT], -1.0, None, ALU.mult)
        yield
        yield

    def derivedB(hp):
        xr_r, xr_k = SC[0], SC[1]
        sigd, csum, afm, kkn = SC[2], SC[3], SC[4], SC[5]
        kmod, bfm = xr_k, afm
        K.scan(csum[:, 0:T], resetm.v(), sigd[:, 0:T], 0.0)
        gt = sigd
        c3 = lambda b_: b_[:, 0:T].re("p (c l) -> p c l", l=128)
        K.act(gt[:, 0:T], csum[:, 0:T], AF.Exp, scale=-CDEC)
        K.tt("dve", AR[:, :, 1, :], c3(xr_r), c3(gt), ALU.mult)
        K.copy("dve", gL.v(), c3(gt)[:, :, 127])
        K.stt(AR[:, :, 0, 1:128], c3(kkn)[:, :, 1:128], -1.0, c3(gt)[:, :, 0:127], ALU.mult, ALU.mult)
        K.ts("dve", AR[:, :, 0, 0], c3(kkn)[:, :, 0], -1.0, None, ALU.mult)
        K.act(gt[:, 0:T], csum[:, 0:T], AF.Exp, scale=CDEC)
        K.tt("dve", bti.v(), bfm[:, 0:T], gt[:, 0:T], ALU.mult)
        K.tt("pool", kti.v(), kmod[:, 0:T], gt[:, 0:T], ALU.mult)
        K.tt("dve", c3(gt), c3(csum)[:, :, 127:128].bc([128, 16, 128]), c3(csum), ALU.subtract)
        K.act(gt[:, 0:T], gt[:, 0:T], AF.Exp, scale=-CDEC)
        K.tt("dve", bhat.v(), bfm[:, 0:T], gt[:, 0:T], ALU.mult)
        K.tt("pool", khat.v(), kmod[:, 0:T], gt[:, 0:T], ALU.mult)
        for c in range(16):
            pb = PSB()
            pv = pb.v().re("p (q t) -> p q t", q=8)
            cs = slice(c * 128, (c + 1) * 128)
            K.tr(pv[:, 0, :], vbf[:, cs], identb.v())
            K.tr(pv[:, 1, :], bhat[:, cs], identb.v())
            K.tr(pv[:, 2, :], khat[:, cs], identb.v())
            K.tr(pv[:, 3, :], AR[:, c, 0, :], identb.v())
            K.copy("act" if c % 2 else "dve", TMq[:, c, :, :], pv[:, 0:4, :])
        sq_src = [xr_r[:, T:TT], kmod[:, T:TT], vs32.v(), decs.v(), avs.v(), bfm[:, T:TT]]
        for half in range(2):
            ps = PSF()
            for j in range(3):
                K.tr(ps[:NS, j * 128:(j + 1) * 128], sq_src[half * 3 + j], identf.v())
            K.copy("dve", smp.v(), ps[:NS, :384].re("p (q f) -> p q f", q=3))
            for j in range(3):
                q = half * 3 + j
                K.dma("sp", dap(samp_d, (2 * hp) * 448 + q * 64, [[8 * 448, NS], [448, 2], [1, 64]]),
                      smp[:, j, :].re("p (h n) -> p h n", h=2))


    def chunk_thunks(hp):
        CH = []
        CH.append(lambda: K.memset('dve', Sf.v(), 0.0))
        CH.append(lambda: K.memset('pool', Sb.v(), 0.0))
        pairs = [(c, hl) for c in range(16) for hl in range(2)]

        def pre_a(sl, c, hl):
            rows = slice(hl * 64, (hl + 1) * 64)
            cs = slice(c * 128, (c + 1) * 128)
            ps = PSF()
            arv = AR[rows, c, :, :].re("p a t -> p (a t)")
            K.mm(ps[:, 0:256], bti[rows, cs], arv)
            K.mm(ps[:, 256:512], kti[rows, cs], arv)
            ps2 = PSF()
            K.mm(ps2[:, 0:128], AR[rows, c, 0, :], bti[rows, cs])
            p4 = ps.v().re("p (q t) -> p q t", q=4)
            K.tt("dve", ATMP[sl % NPQ].v(), p4[:, 0::2, :], mask4[:, 0::2, :], ALU.mult)
            K.tt("dve", ABK[sl].v(), p4[:, 1::2, :], mask4[:, 1::2, :], ALU.mult)
            K.tt("dve", P0T[sl % NPQ].v(), ps2[:, 0:128], maskNT.v(), ALU.mult)

        def pre_lev(sl, lev, gi):
            if lev == 0:
                Pj, PjT, Mp = ATMP[sl % NPQ][:, 0, :], P0T[sl % NPQ].v(), identb.v()
            else:
                src = PMb[sl % NPQ][lev % 2]
                Pj, PjT, Mp = src[:, 0, :], src[:, 1, :], src[:, 2, :]
            dst = PMb[sl % NPQ][(lev + 1) % 2]
            ps = PSF()
            if lev < 6:
                K.mm(ps[:, 0:128], PjT, Pj)
                K.mm(ps[:, 128:256], Pj, PjT)
            K.mm(ps[:, 256:384], PjT, Mp, start=True, stop=False)
            K.mm(ps[:, 256:384], identb.v(), Mp, start=False, stop=True)
            eng = "act" if (gi % 4) != 3 else "dve"
            if lev < 6:
                K.copy(eng, dst.v(), ps[:, 0:384].re("p (q t) -> p q t", q=3))
            else:
                K.copy(eng, dst[:, 2, :], ps[:, 256:384])

        def pre_w(sl, c, hl, gi):
            rows = slice(hl * 64, (hl + 1) * 64)
            cs = slice(c * 128, (c + 1) * 128)
            Mfin = PMb[sl % NPQ][1][:, 2, :]
            ps = PSF()
            K.mm(ps[:, 0:128], TMq[:, c, 3, :], Mfin)
            K.mm(ps[:, 128:192], ATMP[sl % NPQ][:, 1, :], TMq[:, c, 0, rows])
            eng = "act" if gi % 2 else "dve"
            K.copy(eng, Wfm[rows, cs], ps[rows, 0:128])
            K.copy(eng, Ybf[sl % NPQ].v(), ps[:, 128:192])

        def pre_u(sl, gi):
            Mfin = PMb[sl % NPQ][1][:, 2, :]
            ps = PSF()
            K.mm(ps[:, 0:64], Mfin, Ybf[sl % NPQ].v())
            K.copy("act" if gi % 2 else "dve", Ubar[sl].v(), ps[:, 0:64])

        def precompute(group):
            th = []
            for (sl, c, hl) in group:
                th.append(partial(pre_a, sl, c, hl))
            for lev in range(7):
                for gi, (sl, c, hl) in enumerate(group):
                    th.append(partial(pre_lev, sl, lev, gi))
            for gi, (sl, c, hl) in enumerate(group):
                th.append(partial(pre_w, sl, c, hl, gi))
            for gi, (sl, c, hl) in enumerate(group):
                th.append(partial(pre_u, sl, gi))
            return th

        def ser_u(sl, c, hl, gi):
            rows = slice(hl * 64, (hl + 1) * 64)
            cs = slice(c * 128, (c + 1) * 128)
            ut = Ut[gi % 4]
            ps = PSF()
            K.mm(ps[:, 0:64], Wfm[rows, cs], Sb[rows, :])
            K.tt("dve", ut.v(), ps[:, 0:64], Ubar[sl].v(), ALU.add)

        def ser_os(sl, c, hl, gi):
            rows = slice(hl * 64, (hl + 1) * 64)
            ut = Ut[gi % 4]
            pso = PSF()
            K.mm(pso[:, 0:64], AR[rows, c, 1, :], Sb[rows, :], start=True, stop=False)
            K.mm(pso[:, 0:64], ABK[sl][:, 0, :], ut.v(), start=False, stop=False)
            K.mm(pso[:, 0:64], ABK[sl][:, 1, :], TMq[:, c, 0, rows], start=False, stop=True)
            pss = PSF()
            K.mm(pss[:, 0:64], TMq[:, c, 1, :], ut.v(), start=True, stop=False)
            K.mm(pss[:, 0:64], TMq[:, c, 2, :], TMq[:, c, 0, rows], start=False, stop=True)
            K.stt(Sf[rows, :], Sf[rows, :], gL[rows, c:c + 1], pss[rows, 0:64], ALU.mult, ALU.add)
            K.copy("act", Sb[rows, :], Sf[rows, :])
            ot = Otile[c % 2]
            K.copy("act", ot[:, rows], pso[:, 0:64])
            if hl == 1:
                K.dma("sp", o_d[c * 128:(c + 1) * 128, hp * 128:(hp + 1) * 128], ot.v())

        def serial(group):
            th = []
            for gi, (sl, c, hl) in enumerate(group):
                th.append(partial(ser_u, sl, c, hl, gi))
                th.append(partial(ser_os, sl, c, hl, gi))
            return th

        def run_merged(lists):
            idx = [0] * len(lists)
            while True:
                best, bf = -1, 2.0
                for li, L in enumerate(lists):
                    if idx[li] < len(L):
                        frac = idx[li] / len(L)
                        if frac < bf:
                            best, bf = li, frac
                if best < 0:
                    break
                lists[best][idx[best]]()
                idx[best] += 1

        G = 8
        groups = []
        for g0 in range(0, 32, G):
            groups.append([((g0 + i) % NPR, pairs[g0 + i][0], pairs[g0 + i][1]) for i in range(G)])
        CH.extend(precompute(groups[0]))
        for gi_ in range(len(groups)):
            lists = [serial(groups[gi_])]
            if gi_ + 1 < len(groups):
                lists.append(precompute(groups[gi_ + 1]))
            CH.extend(merge_lists(lists))
        def fin_state():
            ps = PSF()
            K.tr(ps[:64, 0:128], Sf.v(), identf.v())
            K.copy("dve", wkvo.v(), ps[:64, 0:128])
            K.dma("sp", o_wkv_p[2 * hp:2 * hp + 2, :, :].re("h v k -> v h k"), wkvo.v().re("v (h k) -> v h k", h=2))

        CH.append(fin_state)
        return CH

    def drain(gen):
        for _ in gen:
            pass

    NA_EST = 150
    drain(derivedA(0))
    derivedB(0)
    for hp in range(4):
        CH = chunk_thunks(hp)
        OUTL = []
        gen = derivedA(hp + 1) if hp + 1 < 4 else None
        per = max(1, len(CH) // NA_EST)
        pero = max(1, len(CH) // max(1, len(OUTL)))
        oi = 0
        for ti, th in enumerate(CH):
            th()
            if gen is not None and ti % per == per - 1:
                try:
                    next(gen)
                except StopIteration:
                    gen = None
            if oi < len(OUTL) and ti % pero == pero - 1:
                OUTL[oi]()
                oi += 1
        while oi < len(OUTL):
            OUTL[oi]()
            oi += 1
        if gen is not None:
            drain(gen)
        if hp + 1 < 4:
            derivedB(hp + 1)

    K.pop_scope()
    if stage == 3:
        zt = K.sb("zt", [128, 4096], F32)
        K.memset("pool", zt.v(), 0.0)
        K.dma("sp", o_wkv_s.v(), zt[:, :])
        for i in range(16):
            K.dma("sp", y_p[i * 128:(i + 1) * 128, :], zt[:, 0:D])
        K.dma("sp", y_s.v(), zt[0:NS, 0:D])
        return early()

    def bcast_row(name, src, n):
        t = K.sb(name, [128, n], F32)
        K.dma("sp", t.v(), dap(src, 0, [[0, 128], [1, n]]))
        return t

    K.push_scope()
    hgl = K.sb("hgl2", [128, 4, TT], BF16)
    K.dma("sp", hgl.v().re("p k t -> p (k t)"), hgl_d.v())
    wmo = K.sb("wmo", [128, 8, D], BF16)
    for kt in range(8):
        K.dma("pool", wmo[:, kt, :], w_merge_out[kt * 128:(kt + 1) * 128, :])
    gpost = bcast_row("gpost_mix", g_post_mix, D)
    mring = [K.sb("mring", [128, 8, 128], BF16) for _ in range(10)]
    mr_i = [0]

    def mload(src, c0, ncols_total, kt_n):
        mr_i[0] += 1
        wb = mring[mr_i[0] % len(mring)]
        K.dma("pool", wb[:, :kt_n, :], dap(src, c0, [[ncols_total, 128], [128 * ncols_total, kt_n], [1, 128]]))
        return wb

    GOFF = SHIFT + RW
    mw = {}

    def mg_load(f):
        mw[f] = (mload(w_rwkv_out, f * 128, D, 4), mload(w_in, GOFF + f * 128, PROJ, 8),
                 mload(w_in, GOFF + D + f * 128, PROJ, 8), mload(glu_w1, f * 128, D, 4), mload(glu_w2, f * 128, D, 4))

    mg_load(0)
    K.push_scope()
    eps_ln = K.sb("eps_ln2", [128, 1], F32)
    K.memset("pool", eps_ln.v(), 64e-5)
    NORF = 3
    fot_r = [K.sb("fot", [128, RW], F32) for _ in range(NORF)]
    fbt_r = [K.sb("fbt", [128, RW], F32) for _ in range(NORF)]
    fon_r = [K.sb("fon", [128, RW], F32) for _ in range(NORF)]
    fofb = [K.sb("fofb", [128, RW], BF16) for _ in range(NORF)]
    fst8 = [K.sb("fst8", [128, 6, 8], F32) for _ in range(NORF)]
    fh3 = lambda b_: b_.v().re("p (h n) -> p h n", h=8)

    def fout_a(c):
        r_ = c % NORF
        ot, bt, tmp, s8 = fot_r[r_], fbt_r[r_], fon_r[r_], fst8[r_]
        K.dma("sp", ot.v(), o_d[c * 128:(c + 1) * 128, :])
        K.dma("sp", bt.v(), bonus_d[c * 128:(c + 1) * 128, :])
        K.reduce(s8[:, 0, :], fh3(ot))
        K.act(tmp.v(), ot.v(), AF.Square)
        K.reduce(s8[:, 1, :], fh3(tmp))
        K.ts("dve", s8[:, 2, :], s8[:, 0, :], 1.0 / 64, None, ALU.mult)
        K.tt("dve", s8[:, 3, :], s8[:, 2, :], s8[:, 2, :], ALU.mult)
        K.stt(s8[:, 4, :], s8[:, 1, :], 1.0 / 64, s8[:, 3, :], ALU.mult, ALU.subtract)
        K.act(s8[:, 4, :], s8[:, 4, :], AF.Sqrt, bias=eps_ln.v(), scale=1.0)
        K.recip(s8[:, 5, :], s8[:, 4, :])
        K.stt(s8[:, 3, :], s8[:, 2, :], -1.0, s8[:, 5, :], ALU.mult, ALU.mult)

    def fout_b(c):
        r_ = c % NORF
        ot, bt, on, of, s8 = fot_r[r_], fbt_r[r_], fon_r[r_], fofb[r_], fst8[r_]
        for h in range(8):
            K.act(on[:, h * 64:(h + 1) * 64], ot[:, h * 64:(h + 1) * 64], AF.Identity,
                  bias=s8[:, 3, h:h + 1], scale=s8[:, 5, h:h + 1])
        K.tt("pool", bt.v(), bt.v(), lnxb.v(), ALU.add)
        K.tt("dve", on.v(), on.v(), lnxg.v(), ALU.mult)
        K.tt("dve", on.v(), on.v(), bt.v(), ALU.add)
        ps = PSF()
        K.mm(ps.v(), lora_bf[64:128, c * 128:(c + 1) * 128], lora_up[64:128, :])
        K.tt("dve", of.v(), on.v(), ps.v(), ALU.mult)
        pb = PSB()
        pv = pb.v().re("p (q t) -> p q t", q=8)
        for kt in range(4):
            K.tr(pv[:, kt, :], of[:, kt * 128:(kt + 1) * 128], identb.v())
        K.copy("act", oaT[:, :, c * 128:(c + 1) * 128], pv[:, 0:4, :])

    OUT = []
    for c in range(16 + 2):
        if c < 16:
            OUT.append(partial(fout_a, c))
        if 0 <= c - 2 < 16:
            OUT.append(partial(fout_b, c - 2))

    gs_tm = K.sb("gs_tm", [NS, RW], F32)
    ps = PSF()
    K.mm(ps[:NS, :], lora_bf[64:128, T:TT], lora_up[64:128, :])
    K.copy("act", gs_tm.v(), ps[:NS, :])
    K.dma("sp", dap(samp_d, 6 * 64, [[8 * 448, NS], [448, 8], [1, 64]]), gs_tm.v().re("p (h n) -> p h n", h=8))
    vec = K.sb("vec", [128, 7, 64], F32)
    K.dma("sp", vec.v().re("p q n -> p (q n)"), dap(samp_d, 0, [[448, 128], [1, 448]]))
    S0s = K.sb("S0s", [128, 64, 64], F32)
    S1s = K.sb("S1s", [128, 64, 64], F32)
    tS = K.sb("tS", [128, 64, 64], F32)
    K.dma("sp", S0s.v().re("p v k -> p (v k)"), st_wkv.v())
    r_, km_, v_, dec_, av_, b_, g_ = [vec[:, q, :] for q in range(7)]
    overv = lambda x: x.un(1).bc([128, 64, 64])
    overk = lambda x: x.un(2).bc([128, 64, 64])
    sv = K.sb("sv", [128, 8, 64], F32)
    ss = K.sb("ss", [128, 8], F32)
    eps2 = K.sb("eps_ln3", [128, 1], F32)
    K.memset("pool", eps2.v(), 64e-5)
    os_tm = K.sb("os_tm", [NS, RW], F32)
    os_bf = K.sb("os_bf", [NS, RW], BF16)
    SMP = [
        lambda: K.tt("dve", tS.v(), S0s.v(), overv(av_), ALU.mult),
        lambda: K.reduce(sv[:, 0, :], tS.v()),
        lambda: K.tt("pool", S1s.v(), S0s.v(), overv(dec_), ALU.mult),
        lambda: K.tt("dve", tS.v(), overk(sv[:, 0, :]), overv(b_), ALU.mult),
        lambda: K.tt("dve", S1s.v(), S1s.v(), tS.v(), ALU.add),
        lambda: K.tt("dve", tS.v(), overk(v_), overv(km_), ALU.mult),
        lambda: K.tt("dve", S1s.v(), S1s.v(), tS.v(), ALU.add),
        lambda: K.dma("sp", o_wkv_s.v(), S1s.v().re("p v k -> p (v k)")),
        lambda: K.tt("dve", tS.v(), S1s.v(), overv(r_), ALU.mult),
        lambda: K.reduce(sv[:, 1, :], tS.v()),
        lambda: K.reduce(ss[:, 0:1], sv[:, 1, :]),
        lambda: K.tt("dve", sv[:, 2, :], sv[:, 1, :], sv[:, 1, :], ALU.mult),
        lambda: K.reduce(ss[:, 1:2], sv[:, 2, :]),
        lambda: K.ts("dve", ss[:, 2:3], ss[:, 0:1], 1.0 / 64, None, ALU.mult),
        lambda: K.tt("dve", ss[:, 3:4], ss[:, 2:3], ss[:, 2:3], ALU.mult),
        lambda: K.stt(ss[:, 4:5], ss[:, 1:2], 1.0 / 64, ss[:, 3:4], ALU.mult, ALU.subtract),
        lambda: K.act(ss[:, 4:5], ss[:, 4:5], AF.Sqrt, bias=eps2.v(), scale=1.0),
        lambda: K.recip(ss[:, 5:6], ss[:, 4:5]),
        lambda: K.ts("dve", sv[:, 3, :], sv[:, 1, :], ss[:, 2:3], ss[:, 5:6], ALU.subtract, ALU.mult),
        lambda: K.tt("dve", sv[:, 3, :], sv[:, 3, :], par[:, 0, :], ALU.mult),
        lambda: K.tt("dve", sv[:, 3, :], sv[:, 3, :], par[:, 1, :], ALU.add),
        lambda: K.tt("dve", sv[:, 4, :], r_, km_, ALU.mult),
        lambda: K.tt("dve", sv[:, 4, :], sv[:, 4, :], par[:, 2, :], ALU.mult),
        lambda: K.reduce(ss[:, 6:7], sv[:, 4, :]),
        lambda: K.stt(sv[:, 3, :], v_, ss[:, 6:7], sv[:, 3, :], ALU.mult, ALU.add),
        lambda: K.tt("dve", sv[:, 5, :], sv[:, 3, :], g_, ALU.mult),
        lambda: K.dma("sp", dap(sampo_d, 0, [[64, 128], [1, 64]]), sv[:, 5, :]),
        lambda: K.dma("sp", os_tm.v(), sampo_d.v()),
        lambda: K.copy("dve", os_bf.v(), os_tm.v()),
    ]

    def smp_fin():
        pb = PSB()
        pv = pb.v().re("p (q t) -> p q t", q=8)
        for kt in range(4):
            K.tr(pv[:, kt, :NS], os_bf[:, kt * 128:(kt + 1) * 128], identb[:NS, :NS])
        K.copy("act", oaT[:, :, T:TT], pv[:, 0:4, :NS])

    SMP.append(smp_fin)
    run_merged_g([OUT, SMP])
    K.pop_scope()
    if stage == 4:
        zt = K.sb("zt", [128, 1024], F32)
        K.memset("pool", zt.v(), 0.0)
        for i in range(16):
            K.dma("sp", y_p[i * 128:(i + 1) * 128, :], zt[:, 0:D])
        K.dma("sp", y_s.v(), zt[0:NS, 0:D])
        return early()


    def pn_b(ps_halves, rows, res_src, gpost, dst, bufs):
        st, xres, tt_ = bufs[:3]
        K.dma("sp", xres[:rows, :], res_src)
        for hf in range(2):
            K.act(tt_[:rows, hf * 512:(hf + 1) * 512], ps_halves[hf][:rows, :], AF.Square, accum=st[:rows, hf:hf + 1])
        K.tt("dve", st[:rows, 2:3], st[:rows, 0:1], st[:rows, 1:2], ALU.add)
        K.act(st[:rows, 3:4], st[:rows, 2:3], AF.Sqrt, bias=epsT[:rows, :], scale=1.0 / D)
        K.recip(st[:rows, 4:5], st[:rows, 3:4])
        for hf in range(2):
            hs = slice(hf * 512, (hf + 1) * 512)
            K.stt(tt_[:rows, hs], ps_halves[hf][:rows, :], st[:rows, 4:5], gpost[:rows, hs], ALU.mult, ALU.mult)
        K.tt("dve", xres[:rows, :], xres[:rows, :], tt_[:rows, :], ALU.add)
        K.dma("sp", dst, xres[:rows, :])

    def pn_c(rows, bufs, nt):
        st, xres, tt_, xb = bufs
        gname, dstT, col0 = nt
        K.act(tt_[:rows, :], xres[:rows, :], AF.Square, accum=st[:rows, 5:6])
        K.act(st[:rows, 6:7], st[:rows, 5:6], AF.Sqrt, bias=epsT[:rows, :], scale=1.0 / D)
        K.recip(st[:rows, 7:8], st[:rows, 6:7])
        K.act(xb[:rows, :], xres[:rows, :], AF.Copy, scale=st[:rows, 7:8])
        pb = PSB()
        pv = pb.v().re("p (k t) -> p k t", k=8)
        for kt in range(8):
            K.tr(pv[:, kt, :rows], xb[:rows, kt * 128:(kt + 1) * 128], identb[:rows, :rows])
        g0 = PV[gname]
        K.tt("dve", dstT[:, :, col0:col0 + rows], pv[:, :, :rows],
             pvec[:, g0:g0 + 8].un(2).bc([128, 8, rows]), ALU.mult)

    mT = K.sb("mT", [128, 8, TT], BF16)
    mt = [[K.sb("mt", [128, 512], F32) for _ in range(4)] for _ in range(2)]
    mps = {}

    def mg_a1(f, tb, it):
        col0, n = TBLK[tb]
        cs = slice(col0, col0 + n)
        wA, wga, wgb, w1, w2 = mw[f]
        ps2, ps1 = PSF(), PSF()
        for kt in range(4):
            K.mm(ps2[:, :n], w2[:, kt, :], hgl[:, kt, cs], start=(kt == 0), stop=(kt == 3))
        for kt in range(4):
            K.mm(ps1[:, :n], w1[:, kt, :], hgl[:, kt, cs], start=(kt == 0), stop=(kt == 3))
        mps[(it, 1)] = (ps2, ps1)

    def mg_a2(f, tb, it):
        col0, n = TBLK[tb]
        cs = slice(col0, col0 + n)
        wA, wga, wgb, w1, w2 = mw[f]
        psgb = PSF()
        for kt in range(8):
            K.mm(psgb[:, :n], wgb[:, kt, :], hT[:, kt, cs], start=(kt == 0), stop=(kt == 7))
        mps[(it, 2)] = psgb

    def mg_a3(f, tb, it):
        col0, n = TBLK[tb]
        cs = slice(col0, col0 + n)
        wA, wga, wgb, w1, w2 = mw[f]
        psga, psA = PSF(), PSF()
        for kt in range(8):
            K.mm(psga[:, :n], wga[:, kt, :], hT[:, kt, cs], start=(kt == 0), stop=(kt == 7))
        for kt in range(4):
            K.mm(psA[:, :n], wA[:, kt, :], oaT[:, kt, cs], start=(kt == 0), stop=(kt == 3))
        mps[(it, 3)] = (psga, psA)

    def mg_b1(f, tb, it):
        col0, n = TBLK[tb]
        t0, t1_, t2_, t3_ = mt[it % 2]
        ps2, ps1 = mps.pop((it, 1))
        K.act(t0[:, :n], ps2[:, :n], AF.Sigmoid, bias=pcol("glu_b2", f))
        K.stt(t1_[:, :n], ps1[:, :n], pcol("glu_b1", f), t0[:, :n], ALU.add, ALU.mult)

    def mg_b2(f, tb, it):
        col0, n = TBLK[tb]
        t0, t1_, t2_, t3_ = mt[it % 2]
        psgb = mps.pop((it, 2))
        K.act(t2_[:, :n], psgb[:, :n], AF.Sigmoid)
        K.tt("dve", t1_[:, :n], t1_[:, :n], t2_[:, :n], ALU.mult)

    def mg_b3(f, tb, it):
        col0, n = TBLK[tb]
        cs = slice(col0, col0 + n)
        t0, t1_, t2_, t3_ = mt[it % 2]
        psga, psA = mps.pop((it, 3))
        K.act(t3_[:, :n], psga[:, :n], AF.Sigmoid)
        K.tt("dve", t3_[:, :n], psA[:, :n], t3_[:, :n], ALU.mult)
        K.tt("pool" if n > 16 else "dve", mT[:, f, cs], t3_[:, :n], t1_[:, :n], ALU.add)

    its = [(f, tb) for f in range(8) for tb in range(5)]
    for it, (f, tb) in enumerate(its):
        if tb == 0 and f + 1 < 8:
            mg_load(f + 1)
        mg_a1(f, tb, it)
        if it >= 1:
            pf, ptb = its[it - 1]
            mg_b3(pf, ptb, it - 1)
        mg_a2(f, tb, it)
        mg_b1(f, tb, it)
        mg_a3(f, tb, it)
        mg_b2(f, tb, it)
    mg_b3(its[-1][0], its[-1][1], len(its) - 1)
    pn_bufs = [(K.sb("pn_st", [128, 8], F32), K.sb("pn_x", [128, D], F32), K.sb("pn_t", [128, D], F32),
                K.sb("pn_xb", [128, D], BF16)) for _ in range(3)]
    mix_ps = {}

    def mix_a(i):
        rows = 128 if i < 16 else NS
        tcs = slice(i * 128, i * 128 + rows)
        halves = [PSF(), PSF()]
        for hf in range(2):
            for kt in range(8):
                K.mm(halves[hf][:rows, :], mT[:, kt, tcs], wmo[:, kt, hf * 512:(hf + 1) * 512],
                     start=(kt == 0), stop=(kt == 7))
        mix_ps[i] = halves

    def mix_b(i):
        rows = 128 if i < 16 else NS
        pn_b(mix_ps[i], rows, x_tile(i), gpost, x1_d[i * 128:i * 128 + rows, :], pn_bufs[i % 3])

    def mix_c(i):
        rows = 128 if i < 16 else NS
        pn_c(rows, pn_bufs[i % 3], ("g_pre_ffn", hT, i * 128))

    for step in range(17 + 2):
        if step < 17:
            mix_a(step)
        if 0 <= step - 1 < 17:
            mix_b(step - 1)
        if 0 <= step - 2 < 17:
            mix_c(step - 2)
    K.pop_scope()
    K.pop_scope()

    K.push_scope()
    NF = DFF // 128
    aT = K.sb("aT", [128, NF, TT], BF16)
    wd = K.sb("wd", [128, NF, D], BF16)
    gpost2 = bcast_row("gpost_ffn", g_post_ffn, D)
    fring = [K.sb("fring", [128, 8, 128], BF16) for _ in range(4)]
    fr_i = [0]

    def fload(src, c0):
        fr_i[0] += 1
        wb = fring[fr_i[0] % len(fring)]
        K.dma("pool", wb.v(), dap(src, c0, [[DFF, 128], [128 * DFF, 8], [1, 128]]))
        return wb

    sgt = [K.sb("sgt", [128, 512], F32) for _ in range(2)]
    it = 0
    for ft in range(NF):
        wg = fload(w_ffn_gate, ft * 128)
        wu = fload(w_ffn_up, ft * 128)
        if ft >= 1:
            K.dma("pool", wd[:, ft - 1, :], w_ffn_down[(ft - 1) * 128:ft * 128, :])
        if ft == NF - 1:
            K.dma("pool", wd[:, ft, :], w_ffn_down[ft * 128:(ft + 1) * 128, :])
        for tb in range(5):
            col0, n = TBLK[tb]
            cs = slice(col0, col0 + n)
            psg, psu = PSF(), PSF()
            for kt in range(8):
                K.mm(psg[:, :n], wg[:, kt, :], hT[:, kt, cs], start=(kt == 0), stop=(kt == 7))
            for kt in range(8):
                K.mm(psu[:, :n], wu[:, kt, :], hT[:, kt, cs], start=(kt == 0), stop=(kt == 7))
            sg = sgt[it % 2]
            it += 1
            K.act(sg[:, :n], psg[:, :n], AF.Silu)
            K.tt("dve", aT[:, ft, cs], psu[:, :n], sg[:, :n], ALU.mult)
    pn_bufs = [(K.sb("pn_st", [128, 8], F32), K.sb("pn_x", [128, D], F32), K.sb("pn_t", [128, D], F32))
               for _ in range(2)]
    dn_ps = {}

    def dn_a(i):
        rows = 128 if i < 16 else NS
        tcs = slice(i * 128, i * 128 + rows)
        halves = [PSF(), PSF()]
        for hf in range(2):
            for kt in range(NF):
                K.mm(halves[hf][:rows, :], aT[:, kt, tcs], wd[:, kt, hf * 512:(hf + 1) * 512],
                     start=(kt == 0), stop=(kt == NF - 1))
        dn_ps[i] = halves

    def dn_b(i):
        rows = 128 if i < 16 else NS
        dst = y_p[i * 128:(i + 1) * 128, :] if i < 16 else y_s[:, :]
        pn_b(dn_ps[i], rows, x1_d[i * 128:i * 128 + rows, :], gpost2, dst, pn_bufs[i % 2])

    for step in range(17 + 1):
        if step < 17:
            dn_a(step)
        if step - 1 >= 0:
            dn_b(step - 1)
    K.pop_scope()
    K.finish()
    es.close()
    return nc


_W_NAMES = ["norm_pre_mix", "norm_post_mix", "norm_pre_ffn", "norm_post_ffn", "w_in", "mu_shift", "w0",
            "w_decay_up", "a0", "w_aaa_up", "w_gate_up", "k_k", "k_a", "r_k", "lnx_g", "lnx_b", "w_rwkv_out",
            "s5_lam_re", "s5_lam_im", "s5_log_dt", "s5_b_re", "s5_b_im", "s5_c_re", "s5_c_im", "s5_d",
            "glu_w1", "glu_b1", "glu_w2", "glu_b2", "w_merge_out", "w_ffn_gate", "w_ffn_up", "w_ffn_down"]


def make_in_maps(inputs, cores):
    f = lambda a: np.ascontiguousarray(np.asarray(a, dtype=np.float32))
    shared = {n: f(inputs[n])[0] for n in _W_NAMES}
    maps = []
    for c in cores:
        m = dict(shared)
        m["x_p"] = f(inputs["x_prompt"][c])
        m["x_s"] = f(inputs["x_sample"][c * NS:(c + 1) * NS, 0])
        m["st_shift"] = f(inputs["state_shift"][0, c * NS:(c + 1) * NS])
        m["st_wkv"] = f(inputs["state_wkv"][0, c * NS:(c + 1) * NS]).reshape(NS * 8, 64 * 64)
        m["st_re"] = f(inputs["state_s5_re"][0, c * NS:(c + 1) * NS]).reshape(NS, 2048)
        m["st_im"] = f(inputs["state_s5_im"][0, c * NS:(c + 1) * NS]).reshape(NS, 2048)
        maps.append(m)
    return maps


def assemble(results):
    n = len(results)
    cat = lambda k: np.concatenate([np.asarray(r[k]) for r in results], axis=0)
    y_p = np.stack([np.asarray(r["y_p"]) for r in results], 0)
    y_s = cat("y_s").reshape(n * NS, 1, D)
    sh_p = cat("o_shift_p").reshape(1, n, SHIFT)
    wkv_p = np.stack([np.asarray(r["o_wkv_p"]) for r in results], 0).reshape(1, n, 8, 64, 64)
    re_p = cat("o_re_p").reshape(1, n, 32, 64)
    im_p = cat("o_im_p").reshape(1, n, 32, 64)
    sh_s = cat("o_shift_s").reshape(1, n * NS, SHIFT)
    wkv_s = cat("o_wkv_s").reshape(1, n * NS, 8, 64, 64)
    re_s = cat("o_re_s").reshape(1, n * NS, 32, 64)
    im_s = cat("o_im_s").reshape(1, n * NS, 32, 64)
    return tuple(np.ascontiguousarray(a, dtype=np.float32) for a in
                 (y_p, y_s, sh_p, wkv_p, re_p, im_p, sh_s, wkv_s, re_s, im_s))


def kernel(**inputs):
    nc = build()
    in_maps = make_in_maps(inputs, list(range(NCORES)))
    res = run_bass_kernel_spmd(nc, in_maps, core_ids=list(range(NCORES)))
    return assemble(res.results)
```

```python
import math
from contextlib import ExitStack
from functools import partial
import numpy as np
import concourse.bass as bass
import concourse.mybir as mybir
from concourse.bass_utils import run_bass_kernel_spmd

F32 = mybir.dt.float32
BF16 = mybir.dt.bfloat16
I32 = mybir.dt.int32
AF = mybir.ActivationFunctionType
ALU = mybir.AluOpType
AX = mybir.AxisListType

T = 2048
NS = 16
TT = T + NS
D = 1024
RW = 512
SHIFT = 1664
PROJ = 4224
DFF = 2816
NCORES = 8
CDEC = math.exp(-0.5)
TBLK = [(0, 512), (512, 512), (1024, 512), (1536, 512), (2048, 16)]
SAME_ENGINE_SYNC = True
LS = 4


class V:
    __slots__ = ("buf", "ap")

    def __init__(self, buf, ap):
        self.buf = buf
        self.ap = ap

    def __getitem__(self, idx):
        return V(self.buf, self.ap[idx])

    def bc(self, shape):
        return V(self.buf, self.ap.to_broadcast(list(shape)))

    def re(self, s, **kw):
        return V(self.buf, self.ap.rearrange(s, **kw))

    def un(self, axis):
        return V(self.buf, self.ap.unsqueeze(axis))

    def bitcast(self, dt):
        return V(self.buf, self.ap.bitcast(dt))


class Buf:
    def __init__(self, name, handle):
        self.name = name
        self.h = handle
        self.writes = {}
        self.reads = {}
        self.dsem = None
        self.dcnt = 0
        self.is_psum = False

    def __getitem__(self, idx):
        return V(self, self.h[idx])

    def v(self):
        return V(self, self.h[:])


class Kern:
    def __init__(self, nc, es):
        self.nc = nc
        self.es = es
        self.root_es = es
        self.eng = {"pe": nc.tensor, "act": nc.scalar, "dve": nc.vector, "pool": nc.gpsimd, "sp": nc.sync}
        self.sem = {}
        self.cnt = {}
        for e in ("pe", "act", "dve", "pool"):
            self.sem[e] = es.enter_context(nc.semaphore("s_" + e))
            self.cnt[e] = 0
        self.obs = {e: {} for e in self.eng}
        self.semname = {}
        self.dsems = []
        self.nbuf = 0
        self.psf_i = 0
        self.psb_i = 0

    def sb(self, name, shape, dt):
        self.nbuf += 1
        import os
        if os.environ.get("KDBG"):
            sz = int(np.prod(shape[1:])) * (2 if dt == BF16 else 4)
            self._tot = getattr(self, "_tot", 0) + sz
            print(f"alloc {name} {shape} {sz} depth={len(getattr(self, '_saved', []))} cum_noFree={self._tot}")
        h = self.es.enter_context(self.nc.sbuf_tensor(f"{name}_{self.nbuf}", list(shape), dt))
        return Buf(name, h)

    def barrier(self):
        for e in ("pe", "act", "dve", "pool", "sp"):
            for e2 in ("pe", "act", "dve", "pool"):
                if e2 != e and self.cnt[e2]:
                    self._wait(e, e2, self.sem[e2], self.cnt[e2])
            for ob in self.dsems:
                self._wait(e, "d:" + ob.name + str(id(ob)), ob.dsem, ob.dcnt)

    def push_scope(self):
        self._saved = getattr(self, "_saved", [])
        self._saved.append(self.es)
        self.es = ExitStack()

    def pop_scope(self):
        self.barrier()
        self.es.close()
        self.es = self._saved.pop()

    def ps(self, name, shape, dt):
        self.nbuf += 1
        h = self.es.enter_context(self.nc.psum_tensor(f"{name}_{self.nbuf}", list(shape), dt))
        b = Buf(name, h)
        b.is_psum = True
        return b

    def dram(self, name, shape, dt, kind="Internal"):
        h = self.nc.dram_tensor(name, list(shape), dt, kind=kind)
        return Buf(name, h.ap())

    def newsem(self, name):
        s = self.root_es.enter_context(self.nc.semaphore(name))
        return s

    def _wait(self, e, sem_key, sem, val):
        o = self.obs[e]
        if o.get(sem_key, 0) >= val:
            return
        self.eng[e].wait_ge(sem, val)
        o[sem_key] = val

    def _deps(self, e, W, R):
        need = {}
        for b in R:
            for k, (s, v) in b.writes.items():
                if need.get(k, (None, 0))[1] < v:
                    need[k] = (s, v)
            if b.is_psum:
                for k, (s, v) in b.reads.items():
                    if k != e and need.get(k, (None, 0))[1] < v:
                        need[k] = (s, v)
        for b in W:
            for k, (s, v) in b.writes.items():
                if need.get(k, (None, 0))[1] < v:
                    need[k] = (s, v)
            for k, (s, v) in b.reads.items():
                if need.get(k, (None, 0))[1] < v:
                    need[k] = (s, v)
        for k, (s, v) in need.items():
            if k == e:
                if e == "pe" or not SAME_ENGINE_SYNC:
                    continue
            self._wait(e, k, s, v)

    def _bufs(self, vs):
        out = []
        for x in vs:
            if isinstance(x, V):
                if x.buf not in out:
                    out.append(x.buf)
            elif isinstance(x, Buf):
                if x not in out:
                    out.append(x)
        return out

    def op(self, e, fn, W, R):
        Wb = self._bufs(W)
        Rb = self._bufs(R)
        self._deps(e, Wb, Rb)
        inst = fn()
        self.cnt[e] += 1
        c = self.cnt[e]
        inst.then_inc(self.sem[e], 1)
        for b in Wb:
            b.reads = {}
            b.writes[e] = (self.sem[e], c)
        for b in Rb:
            if b not in Wb:
                b.reads[e] = (self.sem[e], c)
        return inst

    def dma(self, q, out, in_, sem_buf=None, slow=False):
        Wb = self._bufs([out])
        Rb = self._bufs([in_])
        self._deps(q, Wb, Rb)
        ob = sem_buf if sem_buf is not None else out.buf
        if ob.dsem is None:
            ob.dsem = self.newsem("d_" + ob.name + str(len(self.dsems)))
            self.dsems.append(ob)
        if slow:
            with self.nc.allow_non_contiguous_dma(reason="small parameter layout load"):
                inst = self.eng[q].dma_start(out=out.ap, in_=in_.ap)
        else:
            inst = self.eng[q].dma_start(out=out.ap, in_=in_.ap)
        ob.dcnt += 16
        inst.then_inc(ob.dsem, 16)
        key = "d:" + ob.name + str(id(ob))
        for b in Wb:
            b.reads = {}
            b.writes[key] = (ob.dsem, ob.dcnt)
        for b in Rb:
            if b not in Wb:
                b.reads[key] = (ob.dsem, ob.dcnt)
        return inst

    def finish(self):
        for ob in self.dsems:
            self._wait("sp", "d:" + ob.name + str(id(ob)), ob.dsem, ob.dcnt)
        for e in ("pe", "act", "dve", "pool"):
            if self.cnt[e]:
                self._wait("sp", e, self.sem[e], self.cnt[e])

    @staticmethod
    def _a(x):
        return x.ap if isinstance(x, V) else x

    def act(self, out, in_, func, bias=None, scale=None, accum=None):
        kw = {}
        if bias is not None:
            kw["bias"] = self._a(bias)
        if scale is not None:
            kw["scale"] = self._a(scale)
        if accum is not None:
            kw["accum_out"] = self._a(accum)
        return self.op("act", lambda: self.nc.scalar.activation(out=out.ap, in_=in_.ap, func=func, **kw),
                       [out, accum], [in_, bias, scale])

    def tt(self, e, out, in0, in1, op):
        return self.op(e, lambda: self.eng[e].tensor_tensor(out=out.ap, in0=in0.ap, in1=in1.ap, op=op),
                       [out], [in0, in1])

    def ts(self, e, out, in0, s1, s2, op0, op1=None):
        if op1 is None:
            return self.op(e, lambda: self.eng[e].tensor_scalar(out=out.ap, in0=in0.ap, scalar1=self._a(s1),
                                                               scalar2=None, op0=op0), [out], [in0, s1])
        return self.op(e, lambda: self.eng[e].tensor_scalar(out=out.ap, in0=in0.ap, scalar1=self._a(s1),
                                                           scalar2=self._a(s2), op0=op0, op1=op1),
                       [out], [in0, s1, s2])

    def stt(self, out, in0, scalar, in1, op0, op1, e="dve"):
        return self.op(e, lambda: self.eng[e].scalar_tensor_tensor(out=out.ap, in0=in0.ap, scalar=self._a(scalar),
                                                                  in1=in1.ap, op0=op0, op1=op1),
                       [out], [in0, scalar, in1])

    def copy(self, e, out, in_):
        if e == "act":
            return self.act(out, in_, AF.Copy)
        return self.op(e, lambda: self.eng[e].tensor_copy(out=out.ap, in_=in_.ap), [out], [in_])

    def memset(self, e, out, val):
        return self.op(e, lambda: self.eng[e].memset(out.ap, val), [out], [])

    def recip(self, out, in_):
        return self.op("dve", lambda: self.nc.vector.reciprocal(out=out.ap, in_=in_.ap), [out], [in_])

    def reduce(self, out, in_, op=ALU.add, axis=AX.X):
        return self.op("dve", lambda: self.nc.vector.tensor_reduce(out=out.ap, in_=in_.ap, axis=axis, op=op),
                       [out], [in_])

    def scan(self, out, d0, d1, init, op0=ALU.mult, op1=ALU.add):
        return self.op("dve", lambda: self.nc.vector.tensor_tensor_scan(out=out.ap, data0=d0.ap, data1=d1.ap,
                                                                       initial=self._a(init), op0=op0, op1=op1),
                       [out], [d0, d1, init])

    def mm(self, out, lhsT, rhs, start=True, stop=True):
        return self.op("pe", lambda: self.nc.tensor.matmul(out.ap, lhsT=lhsT.ap, rhs=rhs.ap, start=start, stop=stop),
                       [out], [lhsT, rhs])

    def tr(self, out, in_, ident):
        return self.op("pe", lambda: self.nc.tensor.transpose(out.ap, in_.ap, ident.ap), [out], [in_, ident])

    def aselect(self, out, in_, pattern, cmp, fill, base, cm):
        return self.op("pool", lambda: self.nc.gpsimd.affine_select(out=out.ap, in_=in_.ap, pattern=pattern,
                                                                   compare_op=cmp, fill=fill, base=base,
                                                                   channel_multiplier=cm), [out], [in_])

    def iota(self, out, pattern, base, cm):
        return self.op("pool", lambda: self.nc.gpsimd.iota(out.ap, pattern=pattern, base=base,
                                                          channel_multiplier=cm), [out], [])


def run_merged_g(lists):
    idx = [0] * len(lists)
    while True:
        best, bf = -1, 2.0
        for li, L in enumerate(lists):
            if idx[li] < len(L):
                frac = idx[li] / len(L)
                if frac < bf:
                    best, bf = li, frac
        if best < 0:
            break
        lists[best][idx[best]]()
        idx[best] += 1


def merge_lists(lists):
    out = []
    idx = [0] * len(lists)
    while True:
        best, bf = -1, 2.0
        for li, L in enumerate(lists):
            if idx[li] < len(L):
                frac = idx[li] / len(L)
                if frac < bf:
                    best, bf = li, frac
        if best < 0:
            break
        out.append(lists[best][idx[best]])
        idx[best] += 1
    return out


def dap(buf, offset, ap):
    base = buf.h if not hasattr(buf.h, "ap") or isinstance(buf.h, bass.AP) else buf.h
    t = base.tensor if isinstance(base, bass.AP) else base
    return V(buf, bass.AP(t, offset, [list(x) for x in ap]))


def build(stage=99):
    nc = bass.Bass("TRN2", target_bir_lowering=False)
    es = ExitStack()
    K = Kern(nc, es)

    def din(name, shape, dt=F32):
        return Buf(name, nc.dram_tensor(name, list(shape), dt, kind="ExternalInput").ap())

    def dout(name, shape):
        return Buf(name, nc.dram_tensor(name, list(shape), F32, kind="ExternalOutput").ap())

    x_p = din("x_p", [T, D])
    x_s = din("x_s", [NS, D])
    st_shift = din("st_shift", [NS, SHIFT])
    st_wkv = din("st_wkv", [NS * 8, 64 * 64])
    st_re = din("st_re", [NS, 2048])
    st_im = din("st_im", [NS, 2048])
    g_pre_mix = din("norm_pre_mix", [D])
    g_post_mix = din("norm_post_mix", [D])
    g_pre_ffn = din("norm_pre_ffn", [D])
    g_post_ffn = din("norm_post_ffn", [D])
    w_in = din("w_in", [D, PROJ])
    mu_shift = din("mu_shift", [SHIFT])
    w0 = din("w0", [RW])
    w_decay_up = din("w_decay_up", [32, RW])
    a0 = din("a0", [RW])
    w_aaa_up = din("w_aaa_up", [32, RW])
    w_gate_up = din("w_gate_up", [64, RW])
    k_k = din("k_k", [RW])
    k_a = din("k_a", [RW])
    r_k = din("r_k", [RW])
    lnx_g = din("lnx_g", [RW])
    lnx_b = din("lnx_b", [RW])
    w_rwkv_out = din("w_rwkv_out", [RW, D])
    s5_lam_re = din("s5_lam_re", [32, 64])
    s5_lam_im = din("s5_lam_im", [32, 64])
    s5_log_dt = din("s5_log_dt", [32])
    s5_b_re = din("s5_b_re", [32, 64, 16])
    s5_b_im = din("s5_b_im", [32, 64, 16])
    s5_c_re = din("s5_c_re", [32, 16, 64])
    s5_c_im = din("s5_c_im", [32, 16, 64])
    s5_d = din("s5_d", [RW])
    glu_w1 = din("glu_w1", [RW, D])
    glu_b1 = din("glu_b1", [D])
    glu_w2 = din("glu_w2", [RW, D])
    glu_b2 = din("glu_b2", [D])
    w_merge_out = din("w_merge_out", [D, D])
    w_ffn_gate = din("w_ffn_gate", [D, DFF])
    w_ffn_up = din("w_ffn_up", [D, DFF])
    w_ffn_down = din("w_ffn_down", [DFF, D])

    y_p = dout("y_p", [T, D])
    y_s = dout("y_s", [NS, D])
    o_shift_p = dout("o_shift_p", [1, SHIFT])
    o_wkv_p = dout("o_wkv_p", [8, 64, 64])
    o_re_p = dout("o_re_p", [1, 2048])
    o_im_p = dout("o_im_p", [1, 2048])
    o_shift_s = dout("o_shift_s", [NS, SHIFT])
    o_wkv_s = dout("o_wkv_s", [NS * 8, 64 * 64])
    o_re_s = dout("o_re_s", [NS, 2048])
    o_im_s = dout("o_im_s", [NS, 2048])

    x1_d = K.dram("x1_scratch", [TT, D], F32)

    psf = [K.ps("psf", [128, 512], F32) for _ in range(6)]
    psb = [K.ps("psb", [128, 1024], BF16) for _ in range(2)]

    def PSF():
        K.psf_i += 1
        return psf[K.psf_i % len(psf)]

    def PSB():
        K.psb_i += 1
        return psb[K.psb_i % len(psb)]

    ones_f = K.sb("ones_f", [128, 128], F32)
    identf = K.sb("identf", [128, 128], F32)
    identb = K.sb("identb", [128, 128], BF16)
    epsT = K.sb("epsT", [128, 1], F32)
    K.memset("pool", ones_f.v(), 1.0)
    K.memset("pool", epsT.v(), 1e-6)
    K.aselect(identf.v(), ones_f.v(), [[-1, 128]], ALU.is_equal, 0.0, 0, 1)
    K.copy("pool", identb.v(), identf.v())

    pvec = K.sb("pvec", [128, 128], F32)
    PV = {}
    _pv = [0]

    def load_fm(name, src, ntile):
        c0 = _pv[0]
        _pv[0] += ntile
        K.dma("sp", pvec[:, c0:c0 + ntile], dap(src, 0, [[1, 128], [128, ntile]]), sem_buf=pvec, slow=True)
        PV[name] = c0
        return c0

    load_fm("g_pre_mix", g_pre_mix, 8)
    load_fm("g_pre_ffn", g_pre_ffn, 8)
    load_fm("mu", mu_shift, 13)
    load_fm("w0", w0, 4)
    load_fm("a0", a0, 4)
    load_fm("k_k", k_k, 4)
    load_fm("k_a", k_a, 4)
    load_fm("r_k", r_k, 4)
    load_fm("s5_d", s5_d, 4)
    load_fm("glu_b1", glu_b1, 8)
    load_fm("glu_b2", glu_b2, 8)

    def early():
        while getattr(K, "_saved", []):
            K.pop_scope()
        K.finish()
        es.close()
        return nc

    if stage == 0.1:
        return early()

    def pcol(name, i=0):
        c = PV[name] + i
        return pvec[:, c:c + 1]

    hT = K.sb("hT", [128, 8, TT], BF16)
    def norm_transpose(src_tile_fn, gname, dst):
        K.push_scope()
        NB = 3
        xring = [K.sb("xring", [128, D], F32) for _ in range(NB)]
        xn = [K.sb("xn", [128, D], BF16) for _ in range(NB)]
        junk = K.sb("junk", [128, D], BF16)
        stat = [K.sb("stat", [128, 4], F32) for _ in range(NB)]

        def nt_a(i):
            rows = 128 if i < 16 else NS
            xt, st, xb = xring[i % NB], stat[i % NB], xn[i % NB]
            K.dma("sp", xt[:rows, :], src_tile_fn(i))
            K.act(junk[:rows, :], xt[:rows, :], AF.Square, accum=st[:rows, 0:1])
            K.act(st[:rows, 1:2], st[:rows, 0:1], AF.Sqrt, bias=epsT[:rows, :], scale=1.0 / D)
            K.recip(st[:rows, 2:3], st[:rows, 1:2])
            K.act(xb[:rows, :], xt[:rows, :], AF.Copy, scale=st[:rows, 2:3])

        def nt_b(i):
            rows = 128 if i < 16 else NS
            col0 = i * 128
            xb = xn[i % NB]
            pb = PSB()
            pv = pb.v().re("p (k t) -> p k t", k=8)
            for kt in range(8):
                K.tr(pv[:, kt, :rows], xb[:rows, kt * 128:(kt + 1) * 128], identb[:rows, :rows])
            g0 = PV[gname]
            K.tt("dve", dst[:, :, col0:col0 + rows], pv[:, :, :rows],
                 pvec[:, g0:g0 + 8].un(2).bc([128, 8, rows]), ALU.mult)

        for step in range(17 + 1):
            if step < 17:
                nt_a(step)
            if step >= 1:
                nt_b(step - 1)
        K.pop_scope()

    def x_tile(i):
        return x_p[i * 128:(i + 1) * 128, :] if i < 16 else x_s[:, :]

    norm_transpose(x_tile, "g_pre_mix", hT)
    if stage == 0.2:
        return early()

    wring = [K.sb("wring", [128, 8, 128], BF16) for _ in range(2)]
    wr_i = [0]

    def load_w_cols(src, c0, ncols_total, kt_n=8, width=128):
        wr_i[0] += 1
        wb = wring[wr_i[0] % len(wring)]
        K.dma("pool", wb[:, :kt_n, :width],
              dap(src, c0, [[ncols_total, 128], [128 * ncols_total, kt_n], [1, width]]))
        return wb

    def proj_fm(wb, kt_n, src_act, tb, ps):
        col0, n = TBLK[tb]
        for kt in range(kt_n):
            K.mm(ps[:, :n], wb[:, kt, :], src_act[:, kt, col0:col0 + n], start=(kt == 0), stop=(kt == kt_n - 1))

    hgl_d = K.dram("hgl_scratch", [128, 4 * TT], BF16)
    TWO_PI = 2.0 * math.pi

    K.push_scope()
    ubf = K.sb("ubf", [128, 4, TT], BF16)
    wu4 = [K.sb("wu4", [128, 8, 128], BF16) for _ in range(4)]
    for ft in range(4):
        K.dma("pool", wu4[ft].v(), dap(w_in, SHIFT + ft * 128, [[PROJ, 128], [128 * PROJ, 8], [1, 128]]))
        for tb in range(5):
            col0, n = TBLK[tb]
            ps = PSF()
            proj_fm(wu4[ft], 8, hT, tb, ps)
            K.copy("act" if tb % 2 else "dve", ubf[:, ft, col0:col0 + n], ps[:, :n])

    if stage == 1.1:
        return early()
    sp_ = K.sb("s5par", [128, 16, 24], F32)
    (P_LRE, P_LIM, P_DT, P_MAG, P_TH, P_FRE, P_FIM, P_LBRE, P_LBIM, P_T0, P_T1, P_T2, P_T3,
     P_TH2, P_MAG2, P_G0R, P_G0I, P_C1, P_S1) = range(19)

    def sp(k):
        return sp_[:, :, k]

    K.dma("sp", sp(P_LRE), dap(s5_lam_re, 0, [[1, 128], [128, 16]]), slow=True)
    K.dma("sp", sp(P_LIM), dap(s5_lam_im, 0, [[1, 128], [128, 16]]), slow=True)
    for gl in range(2):
        K.dma("sp", sp_[gl * 64:(gl + 1) * 64, :, P_DT], dap(s5_log_dt, gl, [[0, 64], [2, 16]]), slow=True)
    K.act(sp(P_DT), sp(P_DT), AF.Exp)
    K.tt("dve", sp(P_T0), sp(P_LRE), sp(P_DT), ALU.mult)
    K.act(sp(P_MAG), sp(P_T0), AF.Exp)
    K.copy("dve", sp(P_MAG2), sp(P_MAG))
    for _k in range(LS - 1):
        K.tt("dve", sp(P_MAG2), sp(P_MAG2), sp(P_MAG), ALU.mult)
    K.tt("dve", sp(P_TH), sp(P_LIM), sp(P_DT), ALU.mult)
    K.ts("dve", sp(P_TH2), sp(P_TH), float(LS), None, ALU.mult)

    NH = 512 // LS
    LC = min(128, NH)
    NRC = NH // LC
    pw = K.sb("pw", [128, 16, 2 * (LS + 1)], F32)
    cosT = K.sb("cosT", [128, 16, LC], F32)
    sinT = K.sb("sinT", [128, 16, LC], F32)
    BT = [K.sb("BT", [128, 32, 128], BF16) for _ in range(LS)]
    CTb = K.sb("CTb", [128, 32, 128], BF16)
    VTb = [K.sb("VTb", [128, 32, 128], BF16) for _ in range(LS - 1)]
    Kt = [K.sb("Kt", [128, 4, 128], BF16) for _ in range(LS - 1)]
    K.push_scope()
    ji = K.sb("ji", [128, LC], I32)
    jf = K.sb("jf", [128, LC], F32)
    ang = K.sb("ang", [128, 16, LC], F32)
    rr = K.sb("rr", [128, 16, LC], F32)
    kf = K.sb("kf", [128, 16, LC], F32)
    ki = K.sb("ki", [128, 16, LC], I32)
    K.iota(ji.v(), [[1, LC]], 1, 0)
    K.copy("dve", jf.v(), ji.v())
    K.tt("dve", ang.v(), sp(P_TH2).un(2).bc([128, 16, LC]), jf.v().un(1).bc([128, 16, LC]), ALU.mult)

    def sin_reduced(out, a_in, shift, rr_, kf_, ki_):
        C1 = 6.28125
        C2 = TWO_PI - C1
        K.ts("dve", rr_, a_in, shift, 1.0 / TWO_PI, ALU.add, ALU.mult)
        K.copy("dve", ki_, rr_)
        K.copy("dve", kf_, ki_)
        K.ts("dve", rr_, a_in, shift, None, ALU.add)
        K.stt(rr_, kf_, -C1, rr_, ALU.mult, ALU.add)
        K.stt(rr_, kf_, -C2, rr_, ALU.mult, ALU.add)
        K.ts("dve", kf_, rr_, math.pi, None, ALU.is_gt)
        K.stt(rr_, kf_, -TWO_PI, rr_, ALU.mult, ALU.add)
        K.ts("dve", kf_, rr_, -math.pi, None, ALU.is_lt)
        K.stt(rr_, kf_, TWO_PI, rr_, ALU.mult, ALU.add)
        K.ts("dve", rr_, rr_, -math.pi, math.pi, ALU.max, ALU.min)
        K.act(out, rr_, AF.Sin)

    sin_reduced(sinT.v(), ang.v(), 0.0, rr.v(), kf.v(), ki.v())
    sin_reduced(cosT.v(), ang.v(), 0.5 * math.pi, rr.v(), kf.v(), ki.v())
    sin_reduced(sp(P_S1), sp(P_TH), 0.0, rr[:, :, 0], kf[:, :, 0], ki[:, :, 0])
    sin_reduced(sp(P_C1), sp(P_TH), 0.5 * math.pi, rr[:, :, 0], kf[:, :, 0], ki[:, :, 0])
    K.tt("dve", sp(P_LBRE), sp(P_MAG), sp(P_C1), ALU.mult)
    K.tt("dve", sp(P_LBIM), sp(P_MAG), sp(P_S1), ALU.mult)
    K.tt("dve", sp(P_T0), sp(P_LRE), sp(P_LRE), ALU.mult)
    K.tt("dve", sp(P_T1), sp(P_LIM), sp(P_LIM), ALU.mult)
    K.tt("dve", sp(P_T0), sp(P_T0), sp(P_T1), ALU.add)
    K.recip(sp(P_T0), sp(P_T0))
    K.ts("dve", sp(P_T1), sp(P_LBRE), -1.0, None, ALU.add)
    K.tt("dve", sp(P_T2), sp(P_T1), sp(P_LRE), ALU.mult)
    K.tt("dve", sp(P_T3), sp(P_LBIM), sp(P_LIM), ALU.mult)
    K.tt("dve", sp(P_T2), sp(P_T2), sp(P_T3), ALU.add)
    K.tt("dve", sp(P_FRE), sp(P_T2), sp(P_T0), ALU.mult)
    K.tt("dve", sp(P_T2), sp(P_LBIM), sp(P_LRE), ALU.mult)
    K.tt("dve", sp(P_T3), sp(P_T1), sp(P_LIM), ALU.mult)
    K.tt("dve", sp(P_T2), sp(P_T2), sp(P_T3), ALU.subtract)
    K.tt("dve", sp(P_FIM), sp(P_T2), sp(P_T0), ALU.mult)
    K.memset("dve", pw[:, :, 0], 1.0)
    K.memset("dve", pw[:, :, 1], 0.0)
    for k in range(1, LS + 1):
        pr_, pi_ = pw[:, :, 2 * (k - 1)], pw[:, :, 2 * (k - 1) + 1]
        K.tt("dve", sp(P_T0), pr_, sp(P_LBRE), ALU.mult)
        K.tt("dve", sp(P_T1), pi_, sp(P_LBIM), ALU.mult)
        K.tt("dve", pw[:, :, 2 * k], sp(P_T0), sp(P_T1), ALU.subtract)
        K.tt("dve", sp(P_T0), pr_, sp(P_LBIM), ALU.mult)
        K.tt("dve", sp(P_T1), pi_, sp(P_LBRE), ALU.mult)
        K.tt("dve", pw[:, :, 2 * k + 1], sp(P_T0), sp(P_T1), ALU.add)
    K.pop_scope()
    if stage == 1.2:
        return early()

    K.push_scope()
    CTf = [K.sb("CTf", [128, 16, 128], F32) for _ in range(2)]
    K.push_scope()
    Zc = K.sb("Zc", [128, 4 * 512], F32)
    for part, src in enumerate((s5_c_re, s5_c_im)):
        K.memset("pool", Zc.v(), 0.0)
        for g8 in range(8):
            K.dma("sp", dap(Zc, 16 * g8 * 2048 + 64 * g8, [[2048, 16], [512, 4], [1, 64]]),
                  dap(src, g8 * 1024, [[64, 16], [8192, 4], [1, 64]]))
        for q in range(4):
            ps = PSF()
            for j in range(4):
                K.tr(ps[:, j * 128:(j + 1) * 128], Zc[:, q * 512 + j * 128:q * 512 + (j + 1) * 128], identf.v())
            dst = CTf[part][:, q * 4:q * 4 + 4, :]
            if part == 0:
                K.copy("dve", dst, ps.v().re("p (j c) -> p j c", j=4))
            else:
                K.ts("dve", dst, ps.v().re("p (j c) -> p j c", j=4), -1.0, None, ALU.mult)
    K.pop_scope()
    Xr = K.sb("Xr", [128, 16 * 128], F32)
    Xi = K.sb("Xi", [128, 16 * 128], F32)
    for X, src in ((Xr, s5_b_re), (Xi, s5_b_im)):
        K.memset("pool", X.v(), 0.0)
        for gl in range(2):
            for a in range(4):
                K.dma("sp", dap(X, gl * 64 * 2048 + 16 * gl + a * 512, [[2048, 64], [160, 4], [1, 16]]),
                      dap(src, gl * 1024 + a * 8192, [[16, 64], [2048, 4], [1, 16]]), slow=True)
    c4v = CTb.v().re("p (i two) c -> p i two c", two=2)
    K.copy("act", c4v[:, :, 0, :], CTf[0].v())
    K.copy("act", c4v[:, :, 1, :], CTf[1].v())
    T1 = K.sb("T1", [128, 16, 128], F32)
    Xs = [K.sb("Xs", [128, 16, 128], F32) for _ in range(2)]
    T2 = Xs[0]
    bcv = lambda vv: vv.un(2).bc([128, 16, 128])
    for jv in range(LS - 1):
        lr_, li_ = pw[:, :, 2 * (jv + 1)], pw[:, :, 2 * (jv + 1) + 1]
        v4v = VTb[jv].v().re("p (i two) c -> p i two c", two=2)
        K.tt("dve", T1.v(), CTf[0].v(), bcv(lr_), ALU.mult)
        K.tt("pool", T2.v(), CTf[1].v(), bcv(li_), ALU.mult)
        K.tt("dve", v4v[:, :, 0, :], T1.v(), T2.v(), ALU.add)
        K.tt("dve", T1.v(), CTf[1].v(), bcv(lr_), ALU.mult)
        K.tt("pool", T2.v(), CTf[0].v(), bcv(li_), ALU.mult)
        K.tt("dve", v4v[:, :, 1, :], T1.v(), T2.v(), ALU.subtract)
    gsc = K.sb("gsc", [128, 16, 4], F32)
    x3 = lambda b_: b_.v().re("p (i c) -> p i c", i=16)
    for v in range(LS):
        kpow = LS - 1 - v
        lr_, li_ = pw[:, :, 2 * kpow], pw[:, :, 2 * kpow + 1]
        K.tt("dve", gsc[:, :, 2], lr_, sp(P_FRE), ALU.mult)
        K.tt("dve", gsc[:, :, 3], li_, sp(P_FIM), ALU.mult)
        K.tt("dve", gsc[:, :, 0], gsc[:, :, 2], gsc[:, :, 3], ALU.subtract)
        K.tt("dve", gsc[:, :, 2], lr_, sp(P_FIM), ALU.mult)
        K.tt("dve", gsc[:, :, 3], li_, sp(P_FRE), ALU.mult)
        K.tt("dve", gsc[:, :, 1], gsc[:, :, 2], gsc[:, :, 3], ALU.add)
        gr, gi = bcv(gsc[:, :, 0]), bcv(gsc[:, :, 1])
        K.tt("dve", Xs[0].v(), x3(Xr), gr, ALU.mult)
        K.tt("pool", T1.v(), x3(Xi), gi, ALU.mult)
        K.tt("dve", Xs[0].v(), Xs[0].v(), T1.v(), ALU.subtract)
        K.tt("dve", Xs[1].v(), x3(Xr), gi, ALU.mult)
        K.tt("pool", T1.v(), x3(Xi), gr, ALU.mult)
        K.tt("dve", Xs[1].v(), Xs[1].v(), T1.v(), ALU.add)
        for part in range(2):
            for i4 in range(4):
                ps = PSF()
                for j in range(4):
                    K.tr(ps[:, j * 128:(j + 1) * 128], Xs[part][:, i4 * 4 + j, :], identf.v())
                dstb = BT[v].v().re("p (i two) c -> p i two c", two=2)[:, i4 * 4:i4 * 4 + 4, part, :]
                K.copy("act" if i4 % 2 else "dve", dstb, ps.v().re("p (j c) -> p j c", j=4))
        if kpow <= LS - 2:
            for q in range(4):
                ps = PSF()
                for j in range(4):
                    i = q * 4 + j
                    K.mm(ps[:, 0:128], Xs[0][:, i, :], CTf[0][:, i, :], start=(j == 0), stop=False)
                    K.mm(ps[:, 0:128], Xs[1][:, i, :], CTf[1][:, i, :], start=False, stop=(j == 3))
                K.copy("dve", Kt[kpow][:, q, :], ps[:, 0:128])
    K.pop_scope()
    if stage == 1.3:
        return early()

    x1b = [K.sb("x1sb", [128, 16, NS], BF16) for _ in range(2)]
    K.push_scope()
    s0_tm = [K.sb("s0tm", [NS, 2048], F32) for _ in range(2)]
    s0 = [K.sb("s0", [128, 16, NS], F32) for _ in range(2)]
    K.dma("sp", s0_tm[0].v(), st_re.v())
    K.dma("sp", s0_tm[1].v(), st_im.v())
    for part in range(2):
        ps = PSF()
        for i in range(16):
            K.tr(ps[:, i * NS:(i + 1) * NS], s0_tm[part][:, i * 128:(i + 1) * 128], identf[:NS, :NS])
        K.copy("dve", s0[part].v(), ps[:, :16 * NS].re("p (i n) -> p i n", i=16))
    psr = PSF()
    psi = PSF()
    for i in range(16):
        K.mm(psr[:, i * NS:(i + 1) * NS], BT[LS - 1][:, 2 * i, :], ubf[:, i // 4, T:TT])
        K.mm(psi[:, i * NS:(i + 1) * NS], BT[LS - 1][:, 2 * i + 1, :], ubf[:, i // 4, T:TT])
    sw = [K.sb("sw", [128, 16, NS], F32) for _ in range(2)]
    x1 = [K.sb("x1s", [128, 16, NS], F32) for _ in range(2)]
    bc16 = lambda k: sp(k).un(2).bc([128, 16, NS])
    pr3 = psr[:, :16 * NS].re("p (i n) -> p i n", i=16)
    pi3 = psi[:, :16 * NS].re("p (i n) -> p i n", i=16)
    K.tt("dve", sw[0].v(), s0[0].v(), bc16(P_LBRE), ALU.mult)
    K.tt("dve", sw[1].v(), s0[1].v(), bc16(P_LBIM), ALU.mult)
    K.tt("dve", sw[0].v(), sw[0].v(), sw[1].v(), ALU.subtract)
    K.tt("dve", x1[0].v(), sw[0].v(), pr3, ALU.add)
    K.tt("dve", sw[0].v(), s0[1].v(), bc16(P_LBRE), ALU.mult)
    K.tt("dve", sw[1].v(), s0[0].v(), bc16(P_LBIM), ALU.mult)
    K.tt("dve", sw[0].v(), sw[0].v(), sw[1].v(), ALU.add)
    K.tt("dve", x1[1].v(), sw[0].v(), pi3, ALU.add)
    for part in range(2):
        K.copy("act", x1b[part].v(), x1[part].v())
        xs_tm = s0_tm[part]
        for i4 in range(4):
            ps = PSF()
            for j in range(4):
                i = i4 * 4 + j
                K.tr(ps[:NS, j * 128:(j + 1) * 128], x1[part][:, i, :], identf.v())
            K.copy("dve", xs_tm[:, i4 * 512:(i4 + 1) * 512], ps[:NS, :])
        K.dma("sp", (o_re_s if part == 0 else o_im_s).v(), xs_tm.v())
    K.pop_scope()
    if stage == 1.4:
        return early()

    hgl = K.sb("hgl", [128, 4, TT], BF16)
    carry = K.sb("carry", [128, 16, 2], F32)
    K.memset("dve", carry.v(), 0.0)
    UN = [dict(bR=K.sb("bR", [128, NH], F32), bI=K.sb("bI", [128, NH], F32),
               wR=K.sb("wR", [128, NH], F32), wI=K.sb("wI", [128, NH], F32),
               t1=K.sb("t1", [128, NH], F32), t2=K.sb("t2", [128, NH], F32)) for _ in range(4)]
    xbr = [K.sb("xbr", [128, NH + 1], BF16) for _ in range(4)]
    xbi = [K.sb("xbi", [128, NH + 1], BF16) for _ in range(4)]
    yv = [K.sb("yv", [128, 512], F32) for _ in range(2)]
    y2 = [K.sb("y2", [128, 512], F32) for _ in range(2)]

    def gelu_to(dst_bf, yv_, y2_):
        K.act(y2_, yv_, AF.Square)
        K.act(y2_, y2_, AF.Identity, bias=1.0, scale=0.044715)
        K.tt("pool", y2_, y2_, yv_, ALU.mult)
        K.act(y2_, y2_, AF.Sigmoid, scale=1.5957691216057308)
        K.tt("pool", dst_bf, y2_, yv_, ALU.mult)

    vr = lambda b: b.v().re("p (a l) -> p a l", a=NRC)
    for q in range(4):
        for tb in range(4):
            col0, n = TBLK[tb]
            ugrp = ubf[:, q, col0:col0 + n].re("p (c j) -> p j c", j=LS)
            for j in range(4):
                i = q * 4 + j
                U = UN[j]
                psr = PSF()
                psi = PSF()
                for v in range(LS):
                    K.mm(psr[:, :NH], BT[v][:, 2 * i, :], ugrp[:, v, :], start=(v == 0), stop=(v == LS - 1))
                for v in range(LS):
                    K.mm(psi[:, :NH], BT[v][:, 2 * i + 1, :], ugrp[:, v, :], start=(v == 0), stop=(v == LS - 1))
                tC = cosT[:, i, :].un(1).bc([128, NRC, LC])
                tS = sinT[:, i, :].un(1).bc([128, NRC, LC])
                pr2 = psr[:, :NH].re("p (a l) -> p a l", a=NRC)
                pi2 = psi[:, :NH].re("p (a l) -> p a l", a=NRC)
                K.tt("dve", vr(U["t1"]), pr2, tC, ALU.mult)
                K.tt("dve", vr(U["t2"]), pi2, tS, ALU.mult)
                K.tt("pool", U["bR"].v(), U["t1"].v(), U["t2"].v(), ALU.add)
                K.tt("dve", vr(U["wR"]), pi2, tC, ALU.mult)
                K.tt("dve", vr(U["wI"]), pr2, tS, ALU.mult)
                K.tt("dve", U["bI"].v(), U["wR"].v(), U["wI"].v(), ALU.subtract)
                K.copy("act", xbr[j][:, 0:1], carry[:, i, 0:1])
                K.copy("act", xbi[j][:, 0:1], carry[:, i, 1:2])
            for a in range(NRC):
                cs = slice(a * LC, (a + 1) * LC)
                for j in range(4):
                    i = q * 4 + j
                    U = UN[j]
                    rho = sp_[:, i, P_MAG2:P_MAG2 + 1].bc([128, LC])
                    if a == 0:
                        ire, iim = carry[:, i, 0:1], carry[:, i, 1:2]
                    else:
                        ire, iim = U["bR"][:, a * LC - 1:a * LC], U["bI"][:, a * LC - 1:a * LC]
                    K.scan(U["wR"][:, cs], rho, U["bR"][:, cs], ire)
                    K.scan(U["wI"][:, cs], rho, U["bI"][:, cs], iim)
                for j in range(4):
                    i = q * 4 + j
                    U = UN[j]
                    K.tt("pool", U["t1"][:, cs], U["wR"][:, cs], cosT[:, i, :], ALU.mult)
                    K.tt("dve", U["t2"][:, cs], U["wI"][:, cs], sinT[:, i, :], ALU.mult)
                    K.tt("dve", U["bR"][:, cs], U["t1"][:, cs], U["t2"][:, cs], ALU.subtract)
                    K.tt("pool", U["t1"][:, cs], U["wR"][:, cs], sinT[:, i, :], ALU.mult)
                    K.tt("dve", U["t2"][:, cs], U["wI"][:, cs], cosT[:, i, :], ALU.mult)
                    K.tt("dve", U["bI"][:, cs], U["t1"][:, cs], U["t2"][:, cs], ALU.add)
            for j in range(4):
                i = q * 4 + j
                U = UN[j]
                K.copy("dve", carry[:, i, 0:1], U["bR"][:, NH - 1:NH])
                K.copy("dve", carry[:, i, 1:2], U["bI"][:, NH - 1:NH])
                K.copy("act", xbr[j][:, 1:NH + 1], U["bR"].v())
                K.copy("act", xbi[j][:, 1:NH + 1], U["bI"].v())
            yy = yv[tb % 2]
            yyp = yy.v().re("p (j c) -> p j c", j=LS)
            for jo in range(LS):
                psy = PSF()
                if jo == LS - 1:
                    for j in range(4):
                        i = q * 4 + j
                        K.mm(psy[:, :NH], CTb[:, 2 * i, :], xbr[j][:, 1:NH + 1], start=(j == 0), stop=False)
                        K.mm(psy[:, :NH], CTb[:, 2 * i + 1, :], xbi[j][:, 1:NH + 1], start=False, stop=(j == 3))
                else:
                    for j in range(4):
                        i = q * 4 + j
                        K.mm(psy[:, :NH], VTb[jo][:, 2 * i, :], xbr[j][:, 0:NH], start=(j == 0), stop=False)
                        K.mm(psy[:, :NH], VTb[jo][:, 2 * i + 1, :], xbi[j][:, 0:NH], start=False, stop=False)
                    for ii in range(jo + 1):
                        K.mm(psy[:, :NH], Kt[jo - ii][:, q, :], ugrp[:, ii, :], start=False, stop=(ii == jo))
                K.copy("act", yyp[:, jo, :], psy[:, :NH])
            psu = PSF()
            proj_fm(wu4[q], 8, hT, tb, psu)
            K.stt(yyp, psu.v().re("p (c j) -> p j c", j=LS), pcol("s5_d", q), yyp, ALU.mult, ALU.add)
            y2p = y2[tb % 2].v().re("p (j c) -> p j c", j=LS)
            gelu_to(hgl[:, q, col0:col0 + n].re("p (c j) -> p j c", j=LS), yyp, y2p)
        psy = PSF()
        for j in range(4):
            i = q * 4 + j
            K.mm(psy[:, :NS], CTb[:, 2 * i, :], x1b[0][:, i, :], start=(j == 0), stop=False)
            K.mm(psy[:, :NS], CTb[:, 2 * i + 1, :], x1b[1][:, i, :], start=False, stop=(j == 3))
        K.copy("act", yv[0][:, :NS], psy[:, :NS])
        psu = PSF()
        proj_fm(wu4[q], 8, hT, 4, psu)
        K.stt(yv[0][:, :NS], psu[:, :NS], pcol("s5_d", q), yv[0][:, :NS], ALU.mult, ALU.add)
        gelu_to(hgl[:, q, T:TT], yv[0][:, :NS], y2[0][:, :NS])
    if stage == 1.5:
        return early()
    K.dma("sp", dap(o_re_p, 0, [[1, 128], [128, 16]]), carry[:, :, 0], slow=True)
    K.dma("sp", dap(o_im_p, 0, [[1, 128], [128, 16]]), carry[:, :, 1], slow=True)
    K.dma("sp", hgl_d.v(), hgl.v().re("p k t -> p (k t)"))
    K.pop_scope()
    if stage == 2:
        zt = K.sb("zt", [128, 4096], F32)
        K.memset("pool", zt.v(), 0.0)
        K.dma("sp", o_shift_p.v(), zt[0:1, 0:SHIFT])
        K.dma("sp", o_shift_s.v(), zt[0:NS, 0:SHIFT])
        K.dma("sp", o_wkv_p.v().re("h v k -> h (v k)"), zt[0:8, :])
        K.dma("sp", o_wkv_s.v(), zt[:, :])
        for i in range(16):
            K.dma("sp", y_p[i * 128:(i + 1) * 128, :], zt[:, 0:D])
        K.dma("sp", y_s.v(), zt[0:NS, 0:D])
        return early()
    K.push_scope()
    lnxg = K.sb("lnxg", [128, RW], F32)
    lnxb = K.sb("lnxb", [128, RW], F32)
    K.dma("sp", lnxg.v(), dap(lnx_g, 0, [[0, 128], [1, RW]]))
    K.dma("sp", lnxb.v(), dap(lnx_b, 0, [[0, 128], [1, RW]]))
    par = K.sb("spar", [128, 3, 64], F32)
    for tk in range(NS):
        for j, src in enumerate((lnx_g, lnx_b, r_k)):
            K.dma("sp", par[tk * 8:(tk + 1) * 8, j, :], dap(src, 0, [[64, 8], [1, 64]]), sem_buf=par)

    lora_bf = K.sb("lora_bf", [128, TT], BF16)
    lora_up = K.sb("lora_up", [128, RW], BF16)
    oaT = K.sb("oaT", [128, 4, TT], BF16)
    o_d = K.dram("o_scratch", [T, RW], F32)
    bonus_d = K.dram("bonus_scratch", [T, RW], F32)
    samp_d = K.dram("samp_scratch", [NS * 8, 7, 64], F32)
    sampo_d = K.dram("sampo_scratch", [NS, RW], F32)

    K.push_scope()
    mask4 = K.sb("mask4", [128, 4, 128], F32)
    maskNT = K.sb("maskNT", [128, 128], F32)
    resetm = K.sb("resetm", [128, T], BF16)
    blk1 = K.sb("blk1", [128, 128], F32)
    eps_ln = K.sb("eps_ln", [128, 1], F32)
    K.memset("pool", eps_ln.v(), 64e-5)
    for j in range(4):
        K.aselect(mask4[:, j, :], ones_f.v(), [[1, 128]], ALU.is_gt if j % 2 == 0 else ALU.is_ge, 0.0, 0, -1)
    K.aselect(maskNT.v(), ones_f.v(), [[-1, 128]], ALU.is_gt, 0.0, 0, 1)
    K.memset("pool", resetm.v(), 1.0)
    K.memset("pool", resetm.v().re("p (c l) -> p c l", l=128)[:, :, 0:1], 0.0)
    K.memset("pool", blk1.v(), 0.0)
    K.memset("pool", blk1[0:64, 0:64], 1.0)
    K.memset("pool", blk1[64:128, 64:128], 1.0)

    K.dma("pool", lora_up[0:32, :], w_decay_up.v())
    K.dma("pool", lora_up[32:64, :], w_aaa_up.v())
    K.dma("pool", lora_up[64:128, :], w_gate_up.v())
    sh0T = K.sb("sh0T", [128, 13, NS], F32)
    shout = [K.sb("shout", [NS + 1, 128], F32) for _ in range(1)]
    K.push_scope()
    sh_tm = K.sb("sh_tm", [NS, SHIFT], F32)
    K.dma("sp", sh_tm.v(), st_shift.v())
    ps = PSF()
    for ft in range(13):
        K.tr(ps[:, ft * NS:(ft + 1) * NS], sh_tm[:, ft * 128:(ft + 1) * 128], identf[:NS, :NS])
    K.copy("dve", sh0T.v(), ps[:, :13 * NS].re("p (f n) -> p f n", f=13))
    K.pop_scope()

    omm = K.sb("omm", [128, 13], F32)
    K.ts("dve", omm.v(), pvec[:, PV["mu"]:PV["mu"] + 13], -1.0, 1.0, ALU.mult, ALU.add)
    omka = K.sb("omka", [128, 4], F32)
    K.ts("dve", omka.v(), pvec[:, PV["k_a"]:PV["k_a"] + 4], -1.0, 1.0, ALU.mult, ALU.add)

    SC = [K.sb("scr", [128, TT], F32) for _ in range(6)]

    def proj_shift_tile(ft, pr, tmp, xr_out, xr_out_dt_bf=None, samp32=None):
        wb = load_w_cols(w_in, ft * 128, PROJ)
        for tb in range(5):
            col0, n = TBLK[tb]
            ps = PSF()
            proj_fm(wb, 8, hT, tb, ps)
            K.copy("act", pr[:, col0:col0 + n], ps[:, :n])
            K.act(tmp[:, col0:col0 + n], ps[:, :n], AF.Copy, scale=omm[:, ft:ft + 1])
        ps = PSF()
        K.tr(ps[:NS + 1, :128], pr[:, T - 1:TT], identf.v())
        so = shout[0]
        K.copy("dve", so.v(), ps[:NS + 1, :128])
        K.dma("sp", o_shift_p[0:1, ft * 128:(ft + 1) * 128], so[0:1, :])
        K.dma("sp", o_shift_s[:, ft * 128:(ft + 1) * 128], so[1:NS + 1, :])
        mu_c = pcol("mu", ft)
        K.stt(xr_out[:, 1:T], pr[:, 0:T - 1], mu_c, tmp[:, 1:T], ALU.mult, ALU.add)
        K.copy("pool", xr_out[:, 0:1], tmp[:, 0:1])
        K.stt(xr_out[:, T:TT], sh0T[:, ft, :], mu_c, tmp[:, T:TT], ALU.mult, ALU.add)
        if samp32 is not None:
            K.stt(samp32, sh0T[:, ft, :], mu_c, tmp[:, T:TT], ALU.mult, ALU.add)

    proj_shift_tile(12, SC[0], SC[1], SC[2].v())
    K.act(lora_bf[0:32, :], SC[2][0:32, :], AF.Tanh)
    K.act(lora_bf[32:64, :], SC[2][32:64, :], AF.Copy)
    K.act(lora_bf[64:128, :], SC[2][64:128, :], AF.Sigmoid)

    vbf = K.sb("vbf", [128, TT], BF16)
    vs32 = K.sb("vs32", [128, NS], F32)
    AR = K.sb("AR", [128, 16, 2, 128], BF16)
    bti = K.sb("bti", [128, T], BF16)
    kti = K.sb("kti", [128, T], BF16)
    bhat = K.sb("bhat", [128, T], BF16)
    khat = K.sb("khat", [128, T], BF16)
    Wfm = bhat
    TMq = K.sb("TMq", [128, 16, 4, 128], BF16)
    gL = K.sb("gL", [128, 16], F32)
    Sf = K.sb("Sf", [128, 64], F32)
    Sb = K.sb("Sb", [128, 64], BF16)
    smp = K.sb("smp", [NS, 3, 128], F32)
    decs = K.sb("decs", [128, NS], F32)
    avs = K.sb("avs", [128, NS], F32)
    NPR = 16
    ABK = [K.sb("ABK", [128, 2, 128], BF16) for _ in range(NPR)]
    NPQ = 8
    ATMP = [K.sb("ATMP", [128, 2, 128], BF16) for _ in range(NPQ)]
    P0T = [K.sb("P0T", [128, 128], BF16) for _ in range(NPQ)]
    PMb = [[K.sb("PMb", [128, 3, 128], BF16) for _ in range(2)] for _ in range(NPQ)]
    Ybf = [K.sb("Ybf", [128, 64], BF16) for _ in range(NPQ)]
    Ubar = [K.sb("Ubar", [128, 64], F32) for _ in range(NPR)]
    Ut = [K.sb("Ut", [128, 64], BF16) for _ in range(4)]
    Otile = [K.sb("Otile", [128, 128], F32) for _ in range(2)]
    btile = [K.sb("btile", [128, 512], F32) for _ in range(1)]
    wkvo = K.sb("wkvo", [64, 128], F32)

    def derivedA(hp):
        xr_r, xr_k = SC[0], SC[1]
        proj_shift_tile(0 * 4 + hp, SC[2], SC[3], xr_r.v())
        yield
        proj_shift_tile(1 * 4 + hp, SC[2], SC[3], xr_k.v())
        yield
        proj_shift_tile(2 * 4 + hp, SC[2], SC[3], vbf.v(), samp32=vs32.v())
        yield
        sigd, csum, afm, kkn = SC[2], SC[3], SC[4], SC[5]
        hc = slice(hp * 128, (hp + 1) * 128)
        for tb in range(5):
            col0, n = TBLK[tb]
            ps = PSF()
            K.mm(ps[:, :n], lora_up[0:32, hc], lora_bf[0:32, col0:col0 + n])
            K.act(sigd[:, col0:col0 + n], ps[:, :n], AF.Sigmoid, bias=pcol("w0", hp))
            yield
            ps = PSF()
            K.mm(ps[:, :n], lora_up[32:64, hc], lora_bf[32:64, col0:col0 + n])
            K.act(afm[:, col0:col0 + n], ps[:, :n], AF.Sigmoid, bias=pcol("a0", hp))
            yield
        K.act(kkn.v(), xr_k.v(), AF.Square, scale=pcol("k_k", hp))
        yield
        for tb in range(5):
            col0, n = TBLK[tb]
            ps = PSF()
            K.mm(ps[:, :n], blk1.v(), kkn[:, col0:col0 + n])
            K.ts("dve", csum[:, col0:col0 + n], ps[:, :n], 1e-24, None, ALU.max)
            yield
        K.act(csum.v(), csum.v(), AF.Ln)
        yield
        K.act(csum.v(), csum.v(), AF.Exp, scale=-0.5)
        yield
        K.stt(kkn.v(), xr_k.v(), pcol("k_k", hp), csum.v(), ALU.mult, ALU.mult)
        yield
        K.ts("pool", csum.v(), afm.v(), pcol("k_a", hp), omka[:, hp:hp + 1], ALU.mult, ALU.add)
        yield
        K.tt("pool", xr_k.v(), xr_k.v(), csum.v(), ALU.mult)
        yield
        kmod = xr_k
        K.tt("pool", afm.v(), kkn.v(), afm.v(), ALU.mult)
        yield
        bfm = afm
        K.stt(csum.v(), xr_r.v(), pcol("r_k", hp), kmod.v(), ALU.mult, ALU.mult)
        yield
        for tb in range(4):
            col0, n = TBLK[tb]
            ps = PSF()
            K.mm(ps[:, :n], blk1.v(), csum[:, col0:col0 + n])
            K.tt("dve", csum[:, col0:col0 + n], ps[:, :n], vbf[:, col0:col0 + n], ALU.mult)
            yield
        for c4 in range(4):
            ps = PSF()
            for j in range(4):
                c = c4 * 4 + j
                K.tr(ps[:, j * 128:(j + 1) * 128], csum[:, c * 128:(c + 1) * 128], identf.v())
            bt = btile[0]
            K.copy("act", bt.v(), ps.v())
            yield
            K.dma("sp", dap(bonus_d, c4 * 512 * RW + hp * 128, [[RW, 128], [128 * RW, 4], [1, 128]]),
                  bt.v().re("p (j f) -> p j f", j=4))
            yield
        K.act(decs.v(), sigd[:, T:TT], AF.Exp, scale=-CDEC)
        yield
        K.ts("dve", avs.v(), kkn[:, T:TT], -1.0, None, ALU.mult)
        yield
        yield

    def derivedB(hp):
        xr_r, xr_k = SC[0], SC[1]
        sigd, csum, afm, kkn = SC[2], SC[3], SC[4], SC[5]
        kmod, bfm = xr_k, afm
        K.scan(csum[:, 0:T], resetm.v(), sigd[:, 0:T], 0.0)
        gt = sigd
        c3 = lambda b_: b_[:, 0:T].re("p (c l) -> p c l", l=128)
        K.act(gt[:, 0:T], csum[:, 0:T], AF.Exp, scale=-CDEC)
        K.tt("dve", AR[:, :, 1, :], c3(xr_r), c3(gt), ALU.mult)
        K.copy("dve", gL.v(), c3(gt)[:, :, 127])
        K.stt(AR[:, :, 0, 1:128], c3(kkn)[:, :, 1:128], -1.0, c3(gt)[:, :, 0:127], ALU.mult, ALU.mult)
        K.ts("dve", AR[:, :, 0, 0], c3(kkn)[:, :, 0], -1.0, None, ALU.mult)
        K.act(gt[:, 0:T], csum[:, 0:T], AF.Exp, scale=CDEC)
        K.tt("dve", bti.v(), bfm[:, 0:T], gt[:, 0:T], ALU.mult)
        K.tt("pool", kti.v(), kmod[:, 0:T], gt[:, 0:T], ALU.mult)
        K.tt("dve", c3(gt), c3(csum)[:, :, 127:128].bc([128, 16, 128]), c3(csum), ALU.subtract)
        K.act(gt[:, 0:T], gt[:, 0:T], AF.Exp, scale=-CDEC)
        K.tt("dve", bhat.v(), bfm[:, 0:T], gt[:, 0:T], ALU.mult)
        K.tt("pool", khat.v(), kmod[:, 0:T], gt[:, 0:T], ALU.mult)
        for c in range(16):
            pb = PSB()
            pv = pb.v().re("p (q t) -> p q t", q=8)
            cs = slice(c * 128, (c + 1) * 128)
            K.tr(pv[:, 0, :], vbf[:, cs], identb.v())
            K.tr(pv[:, 1, :], bhat[:, cs], identb.v())
            K.tr(pv[:, 2, :], khat[:, cs], identb.v())
            K.tr(pv[:, 3, :], AR[:, c, 0, :], identb.v())
            K.copy("act" if c % 2 else "dve", TMq[:, c, :, :], pv[:, 0:4, :])
        sq_src = [xr_r[:, T:TT], kmod[:, T:TT], vs32.v(), decs.v(), avs.v(), bfm[:, T:TT]]
        for half in range(2):
            ps = PSF()
            for j in range(3):
                K.tr(ps[:NS, j * 128:(j + 1) * 128], sq_src[half * 3 + j], identf.v())
            K.copy("dve", smp.v(), ps[:NS, :384].re("p (q f) -> p q f", q=3))
            for j in range(3):
                q = half * 3 + j
                K.dma("sp", dap(samp_d, (2 * hp) * 448 + q * 64, [[8 * 448, NS], [448, 2], [1, 64]]),
                      smp[:, j, :].re("p (h n) -> p h n", h=2))


    def chunk_thunks(hp):
        CH = []
        CH.append(lambda: K.memset('dve', Sf.v(), 0.0))
        CH.append(lambda: K.memset('pool', Sb.v(), 0.0))
        pairs = [(c, hl) for c in range(16) for hl in range(2)]

        def pre_a(sl, c, hl):
            rows = slice(hl * 64, (hl + 1) * 64)
            cs = slice(c * 128, (c + 1) * 128)
            ps = PSF()
            arv = AR[rows, c, :, :].re("p a t -> p (a t)")
            K.mm(ps[:, 0:256], bti[rows, cs], arv)
            K.mm(ps[:, 256:512], kti[rows, cs], arv)
            ps2 = PSF()
            K.mm(ps2[:, 0:128], AR[rows, c, 0, :], bti[rows, cs])
            p4 = ps.v().re("p (q t) -> p q t", q=4)
            K.tt("dve", ATMP[sl % NPQ].v(), p4[:, 0::2, :], mask4[:, 0::2, :], ALU.mult)
            K.tt("dve", ABK[sl].v(), p4[:, 1::2, :], mask4[:, 1::2, :], ALU.mult)
            K.tt("dve", P0T[sl % NPQ].v(), ps2[:, 0:128], maskNT.v(), ALU.mult)

        def pre_lev(sl, lev, gi):
            if lev == 0:
                Pj, PjT, Mp = ATMP[sl % NPQ][:, 0, :], P0T[sl % NPQ].v(), identb.v()
            else:
                src = PMb[sl % NPQ][lev % 2]
                Pj, PjT, Mp = src[:, 0, :], src[:, 1, :], src[:, 2, :]
            dst = PMb[sl % NPQ][(lev + 1) % 2]
            ps = PSF()
            if lev < 6:
                K.mm(ps[:, 0:128], PjT, Pj)
                K.mm(ps[:, 128:256], Pj, PjT)
            K.mm(ps[:, 256:384], PjT, Mp, start=True, stop=False)
            K.mm(ps[:, 256:384], identb.v(), Mp, start=False, stop=True)
            eng = "act" if (gi % 4) != 3 else "dve"
            if lev < 6:
                K.copy(eng, dst.v(), ps[:, 0:384].re("p (q t) -> p q t", q=3))
            else:
                K.copy(eng, dst[:, 2, :], ps[:, 256:384])

        def pre_w(sl, c, hl, gi):
            rows = slice(hl * 64, (hl + 1) * 64)
            cs = slice(c * 128, (c + 1) * 128)
            Mfin = PMb[sl % NPQ][1][:, 2, :]
            ps = PSF()
            K.mm(ps[:, 0:128], TMq[:, c, 3, :], Mfin)
            K.mm(ps[:, 128:192], ATMP[sl % NPQ][:, 1, :], TMq[:, c, 0, rows])
            eng = "act" if gi % 2 else "dve"
            K.copy(eng, Wfm[rows, cs], ps[rows, 0:128])
            K.copy(eng, Ybf[sl % NPQ].v(), ps[:, 128:192])

        def pre_u(sl, gi):
            Mfin = PMb[sl % NPQ][1][:, 2, :]
            ps = PSF()
            K.mm(ps[:, 0:64], Mfin, Ybf[sl % NPQ].v())
            K.copy("act" if gi % 2 else "dve", Ubar[sl].v(), ps[:, 0:64])

        def precompute(group):
            th = []
            for (sl, c, hl) in group:
                th.append(partial(pre_a, sl, c, hl))
            for lev in range(7):
                for gi, (sl, c, hl) in enumerate(group):
                    th.append(partial(pre_lev, sl, lev, gi))
            for gi, (sl, c, hl) in enumerate(group):
                th.append(partial(pre_w, sl, c, hl, gi))
            for gi, (sl, c, hl) in enumerate(group):
                th.append(partial(pre_u, sl, gi))
            return th

        def ser_u(sl, c, hl, gi):
            rows = slice(hl * 64, (hl + 1) * 64)
            cs = slice(c * 128, (c + 1) * 128)
            ut = Ut[gi % 4]
            ps = PSF()
            K.mm(ps[:, 0:64], Wfm[rows, cs], Sb[rows, :])
            K.tt("dve", ut.v(), ps[:, 0:64], Ubar[sl].v(), ALU.add)

        def ser_os(sl, c, hl, gi):
            rows = slice(hl * 64, (hl + 1) * 64)
            ut = Ut[gi % 4]
            pso = PSF()
            K.mm(pso[:, 0:64], AR[rows, c, 1, :], Sb[rows, :], start=True, stop=False)
            K.mm(pso[:, 0:64], ABK[sl][:, 0, :], ut.v(), start=False, stop=False)
            K.mm(pso[:, 0:64], ABK[sl][:, 1, :], TMq[:, c, 0, rows], start=False, stop=True)
            pss = PSF()
            K.mm(pss[:, 0:64], TMq[:, c, 1, :], ut.v(), start=True, stop=False)
            K.mm(pss[:, 0:64], TMq[:, c, 2, :], TMq[:, c, 0, rows], start=False, stop=True)
            K.stt(Sf[rows, :], Sf[rows, :], gL[rows, c:c + 1], pss[rows, 0:64], ALU.mult, ALU.add)
            K.copy("act", Sb[rows, :], Sf[rows, :])
            ot = Otile[c % 2]
            K.copy("act", ot[:, rows], pso[:, 0:64])
            if hl == 1:
                K.dma("sp", o_d[c * 128:(c + 1) * 128, hp * 128:(hp + 1) * 128], ot.v())

        def serial(group):
            th = []
            for gi, (sl, c, hl) in enumerate(group):
                th.append(partial(ser_u, sl, c, hl, gi))
                th.append(partial(ser_os, sl, c, hl, gi))
            return th

        def run_merged(lists):
            idx = [0] * len(lists)
            while True:
                best, bf = -1, 2.0
                for li, L in enumerate(lists):
                    if idx[li] < len(L):
                        frac = idx[li] / len(L)
                        if frac < bf:
                            best, bf = li, frac
                if best < 0:
                    break
                lists[best][idx[best]]()
                idx[best] += 1

        G = 8
        groups = []
        for g0 in range(0, 32, G):
            groups.append([((g0 + i) % NPR, pairs[g0 + i][0], pairs[g0 + i][1]) for i in range(G)])
        CH.extend(precompute(groups[0]))
        for gi_ in range(len(groups)):
            lists = [serial(groups[gi_])]
            if gi_ + 1 < len(groups):
                lists.append(precompute(groups[gi_ + 1]))
            CH.extend(merge_lists(lists))
        def fin_state():
            ps = PSF()
            K.tr(ps[:64, 0:128], Sf.v(), identf.v())
            K.copy("dve", wkvo.v(), ps[:64, 0:128])
            K.dma("sp", o_wkv_p[2 * hp:2 * hp + 2, :, :].re("h v k -> v h k"), wkvo.v().re("v (h k) -> v h k", h=2))

        CH.append(fin_state)
        return CH

    def drain(gen):
        for _ in gen:
            pass

    NA_EST = 150
    drain(derivedA(0))
    derivedB(0)
    for hp in range(4):
        CH = chunk_thunks(hp)
        OUTL = []
        gen = derivedA(hp + 1) if hp + 1 < 4 else None
        per = max(1, len(CH) // NA_EST)
        pero = max(1, len(CH) // max(1, len(OUTL)))
        oi = 0
        for ti, th in enumerate(CH):
            th()
            if gen is not None and ti % per == per - 1:
                try:
                    next(gen)
                except StopIteration:
                    gen = None
            if oi < len(OUTL) and ti % pero == pero - 1:
                OUTL[oi]()
                oi += 1
        while oi < len(OUTL):
            OUTL[oi]()
            oi += 1
        if gen is not None:
            drain(gen)
        if hp + 1 < 4:
            derivedB(hp + 1)

    K.pop_scope()
    if stage == 3:
        zt = K.sb("zt", [128, 4096], F32)
        K.memset("pool", zt.v(), 0.0)
        K.dma("sp", o_wkv_s.v(), zt[:, :])
        for i in range(16):
            K.dma("sp", y_p[i * 128:(i + 1) * 128, :], zt[:, 0:D])
        K.dma("sp", y_s.v(), zt[0:NS, 0:D])
        return early()

    def bcast_row(name, src, n):
        t = K.sb(name, [128, n], F32)
        K.dma("sp", t.v(), dap(src, 0, [[0, 128], [1, n]]))
        return t

    K.push_scope()
    hgl = K.sb("hgl2", [128, 4, TT], BF16)
    K.dma("sp", hgl.v().re("p k t -> p (k t)"), hgl_d.v())
    wmo = K.sb("wmo", [128, 8, D], BF16)
    for kt in range(8):
        K.dma("pool", wmo[:, kt, :], w_merge_out[kt * 128:(kt + 1) * 128, :])
    gpost = bcast_row("gpost_mix", g_post_mix, D)
    mring = [K.sb("mring", [128, 8, 128], BF16) for _ in range(10)]
    mr_i = [0]

    def mload(src, c0, ncols_total, kt_n):
        mr_i[0] += 1
        wb = mring[mr_i[0] % len(mring)]
        K.dma("pool", wb[:, :kt_n, :], dap(src, c0, [[ncols_total, 128], [128 * ncols_total, kt_n], [1, 128]]))
        return wb

    GOFF = SHIFT + RW
    mw = {}

    def mg_load(f):
        mw[f] = (mload(w_rwkv_out, f * 128, D, 4), mload(w_in, GOFF + f * 128, PROJ, 8),
                 mload(w_in, GOFF + D + f * 128, PROJ, 8), mload(glu_w1, f * 128, D, 4), mload(glu_w2, f * 128, D, 4))

    mg_load(0)
    K.push_scope()
    eps_ln = K.sb("eps_ln2", [128, 1], F32)
    K.memset("pool", eps_ln.v(), 64e-5)
    NORF = 3
    fot_r = [K.sb("fot", [128, RW], F32) for _ in range(NORF)]
    fbt_r = [K.sb("fbt", [128, RW], F32) for _ in range(NORF)]
    fon_r = [K.sb("fon", [128, RW], F32) for _ in range(NORF)]
    fofb = [K.sb("fofb", [128, RW], BF16) for _ in range(NORF)]
    fst8 = [K.sb("fst8", [128, 6, 8], F32) for _ in range(NORF)]
    fh3 = lambda b_: b_.v().re("p (h n) -> p h n", h=8)

    def fout_a(c):
        r_ = c % NORF
        ot, bt, tmp, s8 = fot_r[r_], fbt_r[r_], fon_r[r_], fst8[r_]
        K.dma("sp", ot.v(), o_d[c * 128:(c + 1) * 128, :])
        K.dma("sp", bt.v(), bonus_d[c * 128:(c + 1) * 128, :])
        K.reduce(s8[:, 0, :], fh3(ot))
        K.act(tmp.v(), ot.v(), AF.Square)
        K.reduce(s8[:, 1, :], fh3(tmp))
        K.ts("dve", s8[:, 2, :], s8[:, 0, :], 1.0 / 64, None, ALU.mult)
        K.tt("dve", s8[:, 3, :], s8[:, 2, :], s8[:, 2, :], ALU.mult)
        K.stt(s8[:, 4, :], s8[:, 1, :], 1.0 / 64, s8[:, 3, :], ALU.mult, ALU.subtract)
        K.act(s8[:, 4, :], s8[:, 4, :], AF.Sqrt, bias=eps_ln.v(), scale=1.0)
        K.recip(s8[:, 5, :], s8[:, 4, :])
        K.stt(s8[:, 3, :], s8[:, 2, :], -1.0, s8[:, 5, :], ALU.mult, ALU.mult)

    def fout_b(c):
        r_ = c % NORF
        ot, bt, on, of, s8 = fot_r[r_], fbt_r[r_], fon_r[r_], fofb[r_], fst8[r_]
        for h in range(8):
            K.act(on[:, h * 64:(h + 1) * 64], ot[:, h * 64:(h + 1) * 64], AF.Identity,
                  bias=s8[:, 3, h:h + 1], scale=s8[:, 5, h:h + 1])
        K.tt("pool", bt.v(), bt.v(), lnxb.v(), ALU.add)
        K.tt("dve", on.v(), on.v(), lnxg.v(), ALU.mult)
        K.tt("dve", on.v(), on.v(), bt.v(), ALU.add)
        ps = PSF()
        K.mm(ps.v(), lora_bf[64:128, c * 128:(c + 1) * 128], lora_up[64:128, :])
        K.tt("dve", of.v(), on.v(), ps.v(), ALU.mult)
        pb = PSB()
        pv = pb.v().re("p (q t) -> p q t", q=8)
        for kt in range(4):
            K.tr(pv[:, kt, :], of[:, kt * 128:(kt + 1) * 128], identb.v())
        K.copy("act", oaT[:, :, c * 128:(c + 1) * 128], pv[:, 0:4, :])

    OUT = []
    for c in range(16 + 2):
        if c < 16:
            OUT.append(partial(fout_a, c))
        if 0 <= c - 2 < 16:
            OUT.append(partial(fout_b, c - 2))

    gs_tm = K.sb("gs_tm", [NS, RW], F32)
    ps = PSF()
    K.mm(ps[:NS, :], lora_bf[64:128, T:TT], lora_up[64:128, :])
    K.copy("act", gs_tm.v(), ps[:NS, :])
    K.dma("sp", dap(samp_d, 6 * 64, [[8 * 448, NS], [448, 8], [1, 64]]), gs_tm.v().re("p (h n) -> p h n", h=8))
    vec = K.sb("vec", [128, 7, 64], F32)
    K.dma("sp", vec.v().re("p q n -> p (q n)"), dap(samp_d, 0, [[448, 128], [1, 448]]))
    S0s = K.sb("S0s", [128, 64, 64], F32)
    S1s = K.sb("S1s", [128, 64, 64], F32)
    tS = K.sb("tS", [128, 64, 64], F32)
    K.dma("sp", S0s.v().re("p v k -> p (v k)"), st_wkv.v())
    r_, km_, v_, dec_, av_, b_, g_ = [vec[:, q, :] for q in range(7)]
    overv = lambda x: x.un(1).bc([128, 64, 64])
    overk = lambda x: x.un(2).bc([128, 64, 64])
    sv = K.sb("sv", [128, 8, 64], F32)
    ss = K.sb("ss", [128, 8], F32)
    eps2 = K.sb("eps_ln3", [128, 1], F32)
    K.memset("pool", eps2.v(), 64e-5)
    os_tm = K.sb("os_tm", [NS, RW], F32)
    os_bf = K.sb("os_bf", [NS, RW], BF16)
    SMP = [
        lambda: K.tt("dve", tS.v(), S0s.v(), overv(av_), ALU.mult),
        lambda: K.reduce(sv[:, 0, :], tS.v()),
        lambda: K.tt("pool", S1s.v(), S0s.v(), overv(dec_), ALU.mult),
        lambda: K.tt("dve", tS.v(), overk(sv[:, 0, :]), overv(b_), ALU.mult),
        lambda: K.tt("dve", S1s.v(), S1s.v(), tS.v(), ALU.add),
        lambda: K.tt("dve", tS.v(), overk(v_), overv(km_), ALU.mult),
        lambda: K.tt("dve", S1s.v(), S1s.v(), tS.v(), ALU.add),
        lambda: K.dma("sp", o_wkv_s.v(), S1s.v().re("p v k -> p (v k)")),
        lambda: K.tt("dve", tS.v(), S1s.v(), overv(r_), ALU.mult),
        lambda: K.reduce(sv[:, 1, :], tS.v()),
        lambda: K.reduce(ss[:, 0:1], sv[:, 1, :]),
        lambda: K.tt("dve", sv[:, 2, :], sv[:, 1, :], sv[:, 1, :], ALU.mult),
        lambda: K.reduce(ss[:, 1:2], sv[:, 2, :]),
        lambda: K.ts("dve", ss[:, 2:3], ss[:, 0:1], 1.0 / 64, None, ALU.mult),
        lambda: K.tt("dve", ss[:, 3:4], ss[:, 2:3], ss[:, 2:3], ALU.mult),
        lambda: K.stt(ss[:, 4:5], ss[:, 1:2], 1.0 / 64, ss[:, 3:4], ALU.mult, ALU.subtract),
        lambda: K.act(ss[:, 4:5], ss[:, 4:5], AF.Sqrt, bias=eps2.v(), scale=1.0),
        lambda: K.recip(ss[:, 5:6], ss[:, 4:5]),
        lambda: K.ts("dve", sv[:, 3, :], sv[:, 1, :], ss[:, 2:3], ss[:, 5:6], ALU.subtract, ALU.mult),
        lambda: K.tt("dve", sv[:, 3, :], sv[:, 3, :], par[:, 0, :], ALU.mult),
        lambda: K.tt("dve", sv[:, 3, :], sv[:, 3, :], par[:, 1, :], ALU.add),
        lambda: K.tt("dve", sv[:, 4, :], r_, km_, ALU.mult),
        lambda: K.tt("dve", sv[:, 4, :], sv[:, 4, :], par[:, 2, :], ALU.mult),
        lambda: K.reduce(ss[:, 6:7], sv[:, 4, :]),
        lambda: K.stt(sv[:, 3, :], v_, ss[:, 6:7], sv[:, 3, :], ALU.mult, ALU.add),
        lambda: K.tt("dve", sv[:, 5, :], sv[:, 3, :], g_, ALU.mult),
        lambda: K.dma("sp", dap(sampo_d, 0, [[64, 128], [1, 64]]), sv[:, 5, :]),
        lambda: K.dma("sp", os_tm.v(), sampo_d.v()),
        lambda: K.copy("dve", os_bf.v(), os_tm.v()),
    ]

    def smp_fin():
        pb = PSB()
        pv = pb.v().re("p (q t) -> p q t", q=8)
        for kt in range(4):
            K.tr(pv[:, kt, :NS], os_bf[:, kt * 128:(kt + 1) * 128], identb[:NS, :NS])
        K.copy("act", oaT[:, :, T:TT], pv[:, 0:4, :NS])

    SMP.append(smp_fin)
    run_merged_g([OUT, SMP])
    K.pop_scope()
    if stage == 4:
        zt = K.sb("zt", [128, 1024], F32)
        K.memset("pool", zt.v(), 0.0)
        for i in range(16):
            K.dma("sp", y_p[i * 128:(i + 1) * 128, :], zt[:, 0:D])
        K.dma("sp", y_s.v(), zt[0:NS, 0:D])
        return early()


    def pn_b(ps_halves, rows, res_src, gpost, dst, bufs):
        st, xres, tt_ = bufs[:3]
        K.dma("sp", xres[:rows, :], res_src)
        for hf in range(2):
            K.act(tt_[:rows, hf * 512:(hf + 1) * 512], ps_halves[hf][:rows, :], AF.Square, accum=st[:rows, hf:hf + 1])
        K.tt("dve", st[:rows, 2:3], st[:rows, 0:1], st[:rows, 1:2], ALU.add)
        K.act(st[:rows, 3:4], st[:rows, 2:3], AF.Sqrt, bias=epsT[:rows, :], scale=1.0 / D)
        K.recip(st[:rows, 4:5], st[:rows, 3:4])
        for hf in range(2):
            hs = slice(hf * 512, (hf + 1) * 512)
            K.stt(tt_[:rows, hs], ps_halves[hf][:rows, :], st[:rows, 4:5], gpost[:rows, hs], ALU.mult, ALU.mult)
        K.tt("dve", xres[:rows, :], xres[:rows, :], tt_[:rows, :], ALU.add)
        K.dma("sp", dst, xres[:rows, :])

    def pn_c(rows, bufs, nt):
        st, xres, tt_, xb = bufs
        gname, dstT, col0 = nt
        K.act(tt_[:rows, :], xres[:rows, :], AF.Square, accum=st[:rows, 5:6])
        K.act(st[:rows, 6:7], st[:rows, 5:6], AF.Sqrt, bias=epsT[:rows, :], scale=1.0 / D)
        K.recip(st[:rows, 7:8], st[:rows, 6:7])
        K.act(xb[:rows, :], xres[:rows, :], AF.Copy, scale=st[:rows, 7:8])
        pb = PSB()
        pv = pb.v().re("p (k t) -> p k t", k=8)
        for kt in range(8):
            K.tr(pv[:, kt, :rows], xb[:rows, kt * 128:(kt + 1) * 128], identb[:rows, :rows])
        g0 = PV[gname]
        K.tt("dve", dstT[:, :, col0:col0 + rows], pv[:, :, :rows],
             pvec[:, g0:g0 + 8].un(2).bc([128, 8, rows]), ALU.mult)

    mT = K.sb("mT", [128, 8, TT], BF16)
    mt = [[K.sb("mt", [128, 512], F32) for _ in range(4)] for _ in range(2)]
    mps = {}

    def mg_a1(f, tb, it):
        col0, n = TBLK[tb]
        cs = slice(col0, col0 + n)
        wA, wga, wgb, w1, w2 = mw[f]
        ps2, ps1 = PSF(), PSF()
        for kt in range(4):
            K.mm(ps2[:, :n], w2[:, kt, :], hgl[:, kt, cs], start=(kt == 0), stop=(kt == 3))
        for kt in range(4):
            K.mm(ps1[:, :n], w1[:, kt, :], hgl[:, kt, cs], start=(kt == 0), stop=(kt == 3))
        mps[(it, 1)] = (ps2, ps1)

    def mg_a2(f, tb, it):
        col0, n = TBLK[tb]
        cs = slice(col0, col0 + n)
        wA, wga, wgb, w1, w2 = mw[f]
        psgb = PSF()
        for kt in range(8):
            K.mm(psgb[:, :n], wgb[:, kt, :], hT[:, kt, cs], start=(kt == 0), stop=(kt == 7))
        mps[(it, 2)] = psgb

    def mg_a3(f, tb, it):
        col0, n = TBLK[tb]
        cs = slice(col0, col0 + n)
        wA, wga, wgb, w1, w2 = mw[f]
        psga, psA = PSF(), PSF()
        for kt in range(8):
            K.mm(psga[:, :n], wga[:, kt, :], hT[:, kt, cs], start=(kt == 0), stop=(kt == 7))
        for kt in range(4):
            K.mm(psA[:, :n], wA[:, kt, :], oaT[:, kt, cs], start=(kt == 0), stop=(kt == 3))
        mps[(it, 3)] = (psga, psA)

    def mg_b1(f, tb, it):
        col0, n = TBLK[tb]
        t0, t1_, t2_, t3_ = mt[it % 2]
        ps2, ps1 = mps.pop((it, 1))
        K.act(t0[:, :n], ps2[:, :n], AF.Sigmoid, bias=pcol("glu_b2", f))
        K.stt(t1_[:, :n], ps1[:, :n], pcol("glu_b1", f), t0[:, :n], ALU.add, ALU.mult)

    def mg_b2(f, tb, it):
        col0, n = TBLK[tb]
        t0, t1_, t2_, t3_ = mt[it % 2]
        psgb = mps.pop((it, 2))
        K.act(t2_[:, :n], psgb[:, :n], AF.Sigmoid)
        K.tt("dve", t1_[:, :n], t1_[:, :n], t2_[:, :n], ALU.mult)

    def mg_b3(f, tb, it):
        col0, n = TBLK[tb]
        cs = slice(col0, col0 + n)
        t0, t1_, t2_, t3_ = mt[it % 2]
        psga, psA = mps.pop((it, 3))
        K.act(t3_[:, :n], psga[:, :n], AF.Sigmoid)
        K.tt("dve", t3_[:, :n], psA[:, :n], t3_[:, :n], ALU.mult)
        K.tt("pool" if n > 16 else "dve", mT[:, f, cs], t3_[:, :n], t1_[:, :n], ALU.add)

    its = [(f, tb) for f in range(8) for tb in range(5)]
    for it, (f, tb) in enumerate(its):
        if tb == 0 and f + 1 < 8:
            mg_load(f + 1)
        mg_a1(f, tb, it)
        if it >= 1:
            pf, ptb = its[it - 1]
            mg_b3(pf, ptb, it - 1)
        mg_a2(f, tb, it)
        mg_b1(f, tb, it)
        mg_a3(f, tb, it)
        mg_b2(f, tb, it)
    mg_b3(its[-1][0], its[-1][1], len(its) - 1)
    pn_bufs = [(K.sb("pn_st", [128, 8], F32), K.sb("pn_x", [128, D], F32), K.sb("pn_t", [128, D], F32),
                K.sb("pn_xb", [128, D], BF16)) for _ in range(3)]
    mix_ps = {}

    def mix_a(i):
        rows = 128 if i < 16 else NS
        tcs = slice(i * 128, i * 128 + rows)
        halves = [PSF(), PSF()]
        for hf in range(2):
            for kt in range(8):
                K.mm(halves[hf][:rows, :], mT[:, kt, tcs], wmo[:, kt, hf * 512:(hf + 1) * 512],
                     start=(kt == 0), stop=(kt == 7))
        mix_ps[i] = halves

    def mix_b(i):
        rows = 128 if i < 16 else NS
        pn_b(mix_ps[i], rows, x_tile(i), gpost, x1_d[i * 128:i * 128 + rows, :], pn_bufs[i % 3])

    def mix_c(i):
        rows = 128 if i < 16 else NS
        pn_c(rows, pn_bufs[i % 3], ("g_pre_ffn", hT, i * 128))

    for step in range(17 + 2):
        if step < 17:
            mix_a(step)
        if 0 <= step - 1 < 17:
            mix_b(step - 1)
        if 0 <= step - 2 < 17:
            mix_c(step - 2)
    K.pop_scope()
    K.pop_scope()

    K.push_scope()
    NF = DFF // 128
    aT = K.sb("aT", [128, NF, TT], BF16)
    wd = K.sb("wd", [128, NF, D], BF16)
    gpost2 = bcast_row("gpost_ffn", g_post_ffn, D)
    fring = [K.sb("fring", [128, 8, 128], BF16) for _ in range(4)]
    fr_i = [0]

    def fload(src, c0):
        fr_i[0] += 1
        wb = fring[fr_i[0] % len(fring)]
        K.dma("pool", wb.v(), dap(src, c0, [[DFF, 128], [128 * DFF, 8], [1, 128]]))
        return wb

    sgt = [K.sb("sgt", [128, 512], F32) for _ in range(2)]
    it = 0
    for ft in range(NF):
        wg = fload(w_ffn_gate, ft * 128)
        wu = fload(w_ffn_up, ft * 128)
        if ft >= 1:
            K.dma("pool", wd[:, ft - 1, :], w_ffn_down[(ft - 1) * 128:ft * 128, :])
        if ft == NF - 1:
            K.dma("pool", wd[:, ft, :], w_ffn_down[ft * 128:(ft + 1) * 128, :])
        for tb in range(5):
            col0, n = TBLK[tb]
            cs = slice(col0, col0 + n)
            psg, psu = PSF(), PSF()
            for kt in range(8):
                K.mm(psg[:, :n], wg[:, kt, :], hT[:, kt, cs], start=(kt == 0), stop=(kt == 7))
            for kt in range(8):
                K.mm(psu[:, :n], wu[:, kt, :], hT[:, kt, cs], start=(kt == 0), stop=(kt == 7))
            sg = sgt[it % 2]
            it += 1
            K.act(sg[:, :n], psg[:, :n], AF.Silu)
            K.tt("dve", aT[:, ft, cs], psu[:, :n], sg[:, :n], ALU.mult)
    pn_bufs = [(K.sb("pn_st", [128, 8], F32), K.sb("pn_x", [128, D], F32), K.sb("pn_t", [128, D], F32))
               for _ in range(2)]
    dn_ps = {}

    def dn_a(i):
        rows = 128 if i < 16 else NS
        tcs = slice(i * 128, i * 128 + rows)
        halves = [PSF(), PSF()]
        for hf in range(2):
            for kt in range(NF):
                K.mm(halves[hf][:rows, :], aT[:, kt, tcs], wd[:, kt, hf * 512:(hf + 1) * 512],
                     start=(kt == 0), stop=(kt == NF - 1))
        dn_ps[i] = halves

    def dn_b(i):
        rows = 128 if i < 16 else NS
        dst = y_p[i * 128:(i + 1) * 128, :] if i < 16 else y_s[:, :]
        pn_b(dn_ps[i], rows, x1_d[i * 128:i * 128 + rows, :], gpost2, dst, pn_bufs[i % 2])

    for step in range(17 + 1):
        if step < 17:
            dn_a(step)
        if step - 1 >= 0:
            dn_b(step - 1)
    K.pop_scope()
    K.finish()
    es.close()
    return nc


_W_NAMES = ["norm_pre_mix", "norm_post_mix", "norm_pre_ffn", "norm_post_ffn", "w_in", "mu_shift", "w0",
            "w_decay_up", "a0", "w_aaa_up", "w_gate_up", "k_k", "k_a", "r_k", "lnx_g", "lnx_b", "w_rwkv_out",
            "s5_lam_re", "s5_lam_im", "s5_log_dt", "s5_b_re", "s5_b_im", "s5_c_re", "s5_c_im", "s5_d",
            "glu_w1", "glu_b1", "glu_w2", "glu_b2", "w_merge_out", "w_ffn_gate", "w_ffn_up", "w_ffn_down"]


def make_in_maps(inputs, cores):
    f = lambda a: np.ascontiguousarray(np.asarray(a, dtype=np.float32))
    shared = {n: f(inputs[n])[0] for n in _W_NAMES}
    maps = []
    for c in cores:
        m = dict(shared)
        m["x_p"] = f(inputs["x_prompt"][c])
        m["x_s"] = f(inputs["x_sample"][c * NS:(c + 1) * NS, 0])
        m["st_shift"] = f(inputs["state_shift"][0, c * NS:(c + 1) * NS])
        m["st_wkv"] = f(inputs["state_wkv"][0, c * NS:(c + 1) * NS]).reshape(NS * 8, 64 * 64)
        m["st_re"] = f(inputs["state_s5_re"][0, c * NS:(c + 1) * NS]).reshape(NS, 2048)
        m["st_im"] = f(inputs["state_s5_im"][0, c * NS:(c + 1) * NS]).reshape(NS, 2048)
        maps.append(m)
    return maps


def assemble(results):
    n = len(results)
    cat = lambda k: np.concatenate([np.asarray(r[k]) for r in results], axis=0)
    y_p = np.stack([np.asarray(r["y_p"]) for r in results], 0)
    y_s = cat("y_s").reshape(n * NS, 1, D)
    sh_p = cat("o_shift_p").reshape(1, n, SHIFT)
    wkv_p = np.stack([np.asarray(r["o_wkv_p"]) for r in results], 0).reshape(1, n, 8, 64, 64)
    re_p = cat("o_re_p").reshape(1, n, 32, 64)
    im_p = cat("o_im_p").reshape(1, n, 32, 64)
    sh_s = cat("o_shift_s").reshape(1, n * NS, SHIFT)
    wkv_s = cat("o_wkv_s").reshape(1, n * NS, 8, 64, 64)
    re_s = cat("o_re_s").reshape(1, n * NS, 32, 64)
    im_s = cat("o_im_s").reshape(1, n * NS, 32, 64)
    return tuple(np.ascontiguousarray(a, dtype=np.float32) for a in
                 (y_p, y_s, sh_p, wkv_p, re_p, im_p, sh_s, wkv_s, re_s, im_s))


def kernel(**inputs):
    nc = build()
    in_maps = make_in_maps(inputs, list(range(NCORES)))
    res = run_bass_kernel_spmd(nc, in_maps, core_ids=list(range(NCORES)))
    return assemble(res.results)
```

```python
import math
from contextlib import ExitStack
from functools import partial
import numpy as np
import concourse.bass as bass
import concourse.mybir as mybir
from concourse.bass_utils import run_bass_kernel_spmd

F32 = mybir.dt.float32
BF16 = mybir.dt.bfloat16
I32 = mybir.dt.int32
AF = mybir.ActivationFunctionType
ALU = mybir.AluOpType
AX = mybir.AxisListType

T = 2048
NS = 16
TT = T + NS
D = 1024
RW = 512
SHIFT = 1664
PROJ = 4224
DFF = 2816
NCORES = 8
CDEC = math.exp(-0.5)
TBLK = [(0, 512), (512, 512), (1024, 512), (1536, 512), (2048, 16)]
SAME_ENGINE_SYNC = True
LS = 4


class V:
    __slots__ = ("buf", "ap")

    def __init__(self, buf, ap):
        self.buf = buf
        self.ap = ap

    def __getitem__(self, idx):
        return V(self.buf, self.ap[idx])

    def bc(self, shape):
        return V(self.buf, self.ap.to_broadcast(list(shape)))

    def re(self, s, **kw):
        return V(self.buf, self.ap.rearrange(s, **kw))

    def un(self, axis):
        return V(self.buf, self.ap.unsqueeze(axis))

    def bitcast(self, dt):
        return V(self.buf, self.ap.bitcast(dt))


class Buf:
    def __init__(self, name, handle):
        self.name = name
        self.h = handle
        self.writes = {}
        self.reads = {}
        self.dsem = None
        self.dcnt = 0
        self.is_psum = False

    def __getitem__(self, idx):
        return V(self, self.h[idx])

    def v(self):
        return V(self, self.h[:])


class Kern:
    def __init__(self, nc, es):
        self.nc = nc
        self.es = es
        self.root_es = es
        self.eng = {"pe": nc.tensor, "act": nc.scalar, "dve": nc.vector, "pool": nc.gpsimd, "sp": nc.sync}
        self.sem = {}
        self.cnt = {}
        for e in ("pe", "act", "dve", "pool"):
            self.sem[e] = es.enter_context(nc.semaphore("s_" + e))
            self.cnt[e] = 0
        self.obs = {e: {} for e in self.eng}
        self.semname = {}
        self.dsems = []
        self.nbuf = 0
        self.psf_i = 0
        self.psb_i = 0

    def sb(self, name, shape, dt):
        self.nbuf += 1
        import os
        if os.environ.get("KDBG"):
            sz = int(np.prod(shape[1:])) * (2 if dt == BF16 else 4)
            self._tot = getattr(self, "_tot", 0) + sz
            print(f"alloc {name} {shape} {sz} depth={len(getattr(self, '_saved', []))} cum_noFree={self._tot}")
        h = self.es.enter_context(self.nc.sbuf_tensor(f"{name}_{self.nbuf}", list(shape), dt))
        return Buf(name, h)

    def barrier(self):
        for e in ("pe", "act", "dve", "pool", "sp"):
            for e2 in ("pe", "act", "dve", "pool"):
                if e2 != e and self.cnt[e2]:
                    self._wait(e, e2, self.sem[e2], self.cnt[e2])
            for ob in self.dsems:
                self._wait(e, "d:" + ob.name + str(id(ob)), ob.dsem, ob.dcnt)

    def push_scope(self):
        self._saved = getattr(self, "_saved", [])
        self._saved.append(self.es)
        self.es = ExitStack()

    def pop_scope(self):
        self.barrier()
        self.es.close()
        self.es = self._saved.pop()

    def ps(self, name, shape, dt):
        self.nbuf += 1
        h = self.es.enter_context(self.nc.psum_tensor(f"{name}_{self.nbuf}", list(shape), dt))
        b = Buf(name, h)
        b.is_psum = True
        return b

    def dram(self, name, shape, dt, kind="Internal"):
        h = self.nc.dram_tensor(name, list(shape), dt, kind=kind)
        return Buf(name, h.ap())

    def newsem(self, name):
        s = self.root_es.enter_context(self.nc.semaphore(name))
        return s

    def _wait(self, e, sem_key, sem, val):
        o = self.obs[e]
        if o.get(sem_key, 0) >= val:
            return
        self.eng[e].wait_ge(sem, val)
        o[sem_key] = val

    def _deps(self, e, W, R):
        need = {}
        for b in R:
            for k, (s, v) in b.writes.items():
                if need.get(k, (None, 0))[1] < v:
                    need[k] = (s, v)
            if b.is_psum:
                for k, (s, v) in b.reads.items():
                    if k != e and need.get(k, (None, 0))[1] < v:
                        need[k] = (s, v)
        for b in W:
            for k, (s, v) in b.writes.items():
                if need.get(k, (None, 0))[1] < v:
                    need[k] = (s, v)
            for k, (s, v) in b.reads.items():
                if need.get(k, (None, 0))[1] < v:
                    need[k] = (s, v)
        for k, (s, v) in need.items():
            if k == e:
                if e == "pe" or not SAME_ENGINE_SYNC:
                    continue
            self._wait(e, k, s, v)

    def _bufs(self, vs):
        out = []
        for x in vs:
            if isinstance(x, V):
                if x.buf not in out:
                    out.append(x.buf)
            elif isinstance(x, Buf):
                if x not in out:
                    out.append(x)
        return out

    def op(self, e, fn, W, R):
        Wb = self._bufs(W)
        Rb = self._bufs(R)
        self._deps(e, Wb, Rb)
        inst = fn()
        self.cnt[e] += 1
        c = self.cnt[e]
        inst.then_inc(self.sem[e], 1)
        for b in Wb:
            b.reads = {}
            b.writes[e] = (self.sem[e], c)
        for b in Rb:
            if b not in Wb:
                b.reads[e] = (self.sem[e], c)
        return inst

    def dma(self, q, out, in_, sem_buf=None, slow=False):
        Wb = self._bufs([out])
        Rb = self._bufs([in_])
        self._deps(q, Wb, Rb)
        ob = sem_buf if sem_buf is not None else out.buf
        if ob.dsem is None:
            ob.dsem = self.newsem("d_" + ob.name + str(len(self.dsems)))
            self.dsems.append(ob)
        if slow:
            with self.nc.allow_non_contiguous_dma(reason="small parameter layout load"):
                inst = self.eng[q].dma_start(out=out.ap, in_=in_.ap)
        else:
            inst = self.eng[q].dma_start(out=out.ap, in_=in_.ap)
        ob.dcnt += 16
        inst.then_inc(ob.dsem, 16)
        key = "d:" + ob.name + str(id(ob))
        for b in Wb:
            b.reads = {}
            b.writes[key] = (ob.dsem, ob.dcnt)
        for b in Rb:
            if b not in Wb:
                b.reads[key] = (ob.dsem, ob.dcnt)
        return inst

    def finish(self):
        for ob in self.dsems:
            self._wait("sp", "d:" + ob.name + str(id(ob)), ob.dsem, ob.dcnt)
        for e in ("pe", "act", "dve", "pool"):
            if self.cnt[e]:
                self._wait("sp", e, self.sem[e], self.cnt[e])

    @staticmethod
    def _a(x):
        return x.ap if isinstance(x, V) else x

    def act(self, out, in_, func, bias=None, scale=None, accum=None):
        kw = {}
        if bias is not None:
            kw["bias"] = self._a(bias)
        if scale is not None:
            kw["scale"] = self._a(scale)
        if accum is not None:
            kw["accum_out"] = self._a(accum)
        return self.op("act", lambda: self.nc.scalar.activation(out=out.ap, in_=in_.ap, func=func, **kw),
                       [out, accum], [in_, bias, scale])

    def tt(self, e, out, in0, in1, op):
        return self.op(e, lambda: self.eng[e].tensor_tensor(out=out.ap, in0=in0.ap, in1=in1.ap, op=op),
                       [out], [in0, in1])

    def ts(self, e, out, in0, s1, s2, op0, op1=None):
        if op1 is None:
            return self.op(e, lambda: self.eng[e].tensor_scalar(out=out.ap, in0=in0.ap, scalar1=self._a(s1),
                                                               scalar2=None, op0=op0), [out], [in0, s1])
        return self.op(e, lambda: self.eng[e].tensor_scalar(out=out.ap, in0=in0.ap, scalar1=self._a(s1),
                                                           scalar2=self._a(s2), op0=op0, op1=op1),
                       [out], [in0, s1, s2])

    def stt(self, out, in0, scalar, in1, op0, op1, e="dve"):
        return self.op(e, lambda: self.eng[e].scalar_tensor_tensor(out=out.ap, in0=in0.ap, scalar=self._a(scalar),
                                                                  in1=in1.ap, op0=op0, op1=op1),
                       [out], [in0, scalar, in1])

    def copy(self, e, out, in_):
        if e == "act":
            return self.act(out, in_, AF.Copy)
        return self.op(e, lambda: self.eng[e].tensor_copy(out=out.ap, in_=in_.ap), [out], [in_])

    def memset(self, e, out, val):
        return self.op(e, lambda: self.eng[e].memset(out.ap, val), [out], [])

    def recip(self, out, in_):
        return self.op("dve", lambda: self.nc.vector.reciprocal(out=out.ap, in_=in_.ap), [out], [in_])

    def reduce(self, out, in_, op=ALU.add, axis=AX.X):
        return self.op("dve", lambda: self.nc.vector.tensor_reduce(out=out.ap, in_=in_.ap, axis=axis, op=op),
                       [out], [in_])

    def scan(self, out, d0, d1, init, op0=ALU.mult, op1=ALU.add):
        return self.op("dve", lambda: self.nc.vector.tensor_tensor_scan(out=out.ap, data0=d0.ap, data1=d1.ap,
                                                                       initial=self._a(init), op0=op0, op1=op1),
                       [out], [d0, d1, init])

    def mm(self, out, lhsT, rhs, start=True, stop=True):
        return self.op("pe", lambda: self.nc.tensor.matmul(out.ap, lhsT=lhsT.ap, rhs=rhs.ap, start=start, stop=stop),
                       [out], [lhsT, rhs])

    def tr(self, out, in_, ident):
        return self.op("pe", lambda: self.nc.tensor.transpose(out.ap, in_.ap, ident.ap), [out], [in_, ident])

    def aselect(self, out, in_, pattern, cmp, fill, base, cm):
        return self.op("pool", lambda: self.nc.gpsimd.affine_select(out=out.ap, in_=in_.ap, pattern=pattern,
                                                                   compare_op=cmp, fill=fill, base=base,
                                                                   channel_multiplier=cm), [out], [in_])

    def iota(self, out, pattern, base, cm):
        return self.op("pool", lambda: self.nc.gpsimd.iota(out.ap, pattern=pattern, base=base,
                                                          channel_multiplier=cm), [out], [])


def run_merged_g(lists):
    idx = [0] * len(lists)
    while True:
        best, bf = -1, 2.0
        for li, L in enumerate(lists):
            if idx[li] < len(L):
                frac = idx[li] / len(L)
                if frac < bf:
                    best, bf = li, frac
        if best < 0:
            break
        lists[best][idx[best]]()
        idx[best] += 1


def merge_lists(lists):
    out = []
    idx = [0] * len(lists)
    while True:
        best, bf = -1, 2.0
        for li, L in enumerate(lists):
            if idx[li] < len(L):
                frac = idx[li] / len(L)
                if frac < bf:
                    best, bf = li, frac
        if best < 0:
            break
        out.append(lists[best][idx[best]])
        idx[best] += 1
    return out


def dap(buf, offset, ap):
    base = buf.h if not hasattr(buf.h, "ap") or isinstance(buf.h, bass.AP) else buf.h
    t = base.tensor if isinstance(base, bass.AP) else base
    return V(buf, bass.AP(t, offset, [list(x) for x in ap]))


def build(stage=99):
    nc = bass.Bass("TRN2", target_bir_lowering=False)
    es = ExitStack()
    K = Kern(nc, es)

    def din(name, shape, dt=F32):
        return Buf(name, nc.dram_tensor(name, list(shape), dt, kind="ExternalInput").ap())

    def dout(name, shape):
        return Buf(name, nc.dram_tensor(name, list(shape), F32, kind="ExternalOutput").ap())

    x_p = din("x_p", [T, D])
    x_s = din("x_s", [NS, D])
    st_shift = din("st_shift", [NS, SHIFT])
    st_wkv = din("st_wkv", [NS * 8, 64 * 64])
    st_re = din("st_re", [NS, 2048])
    st_im = din("st_im", [NS, 2048])
    g_pre_mix = din("norm_pre_mix", [D])
    g_post_mix = din("norm_post_mix", [D])
    g_pre_ffn = din("norm_pre_ffn", [D])
    g_post_ffn = din("norm_post_ffn", [D])
    w_in = din("w_in", [D, PROJ])
    mu_shift = din("mu_shift", [SHIFT])
    w0 = din("w0", [RW])
    w_decay_up = din("w_decay_up", [32, RW])
    a0 = din("a0", [RW])
    w_aaa_up = din("w_aaa_up", [32, RW])
    w_gate_up = din("w_gate_up", [64, RW])
    k_k = din("k_k", [RW])
    k_a = din("k_a", [RW])
    r_k = din("r_k", [RW])
    lnx_g = din("lnx_g", [RW])
    lnx_b = din("lnx_b", [RW])
    w_rwkv_out = din("w_rwkv_out", [RW, D])
    s5_lam_re = din("s5_lam_re", [32, 64])
    s5_lam_im = din("s5_lam_im", [32, 64])
    s5_log_dt = din("s5_log_dt", [32])
    s5_b_re = din("s5_b_re", [32, 64, 16])
    s5_b_im = din("s5_b_im", [32, 64, 16])
    s5_c_re = din("s5_c_re", [32, 16, 64])
    s5_c_im = din("s5_c_im", [32, 16, 64])
    s5_d = din("s5_d", [RW])
    glu_w1 = din("glu_w1", [RW, D])
    glu_b1 = din("glu_b1", [D])
    glu_w2 = din("glu_w2", [RW, D])
    glu_b2 = din("glu_b2", [D])
    w_merge_out = din("w_merge_out", [D, D])
    w_ffn_gate = din("w_ffn_gate", [D, DFF])
    w_ffn_up = din("w_ffn_up", [D, DFF])
    w_ffn_down = din("w_ffn_down", [DFF, D])

    y_p = dout("y_p", [T, D])
    y_s = dout("y_s", [NS, D])
    o_shift_p = dout("o_shift_p", [1, SHIFT])
    o_wkv_p = dout("o_wkv_p", [8, 64, 64])
    o_re_p = dout("o_re_p", [1, 2048])
    o_im_p = dout("o_im_p", [1, 2048])
    o_shift_s = dout("o_shift_s", [NS, SHIFT])
    o_wkv_s = dout("o_wkv_s", [NS * 8, 64 * 64])
    o_re_s = dout("o_re_s", [NS, 2048])
    o_im_s = dout("o_im_s", [NS, 2048])

    x1_d = K.dram("x1_scratch", [TT, D], F32)

    psf = [K.ps("psf", [128, 512], F32) for _ in range(8)]

    def PSF():
        K.psf_i += 1
        return psf[K.psf_i % len(psf)]

    class BV:
        def __init__(self, buf):
            self.buf = buf

        def v(self):
            return V(self.buf, self.buf.h[:].bitcast(BF16))

        def __getitem__(self, idx):
            return self.v()[idx]

    def PSB():
        return BV(PSF())

    ones_f = K.sb("ones_f", [128, 128], F32)
    identf = K.sb("identf", [128, 128], F32)
    identb = K.sb("identb", [128, 128], BF16)
    epsT = K.sb("epsT", [128, 1], F32)
    K.memset("pool", ones_f.v(), 1.0)
    K.memset("pool", epsT.v(), 1e-6)
    K.aselect(identf.v(), ones_f.v(), [[-1, 128]], ALU.is_equal, 0.0, 0, 1)
    K.copy("pool", identb.v(), identf.v())

    pvec = K.sb("pvec", [128, 128], F32)
    PV = {}
    _pv = [0]

    def load_fm(name, src, ntile):
        c0 = _pv[0]
        _pv[0] += ntile
        K.dma("sp", pvec[:, c0:c0 + ntile], dap(src, 0, [[1, 128], [128, ntile]]), sem_buf=pvec, slow=True)
        PV[name] = c0
        return c0

    load_fm("g_pre_mix", g_pre_mix, 8)
    load_fm("g_pre_ffn", g_pre_ffn, 8)
    load_fm("mu", mu_shift, 13)
    load_fm("w0", w0, 4)
    load_fm("a0", a0, 4)
    load_fm("k_k", k_k, 4)
    load_fm("k_a", k_a, 4)
    load_fm("r_k", r_k, 4)
    load_fm("s5_d", s5_d, 4)
    load_fm("glu_b1", glu_b1, 8)
    load_fm("glu_b2", glu_b2, 8)

    def early():
        while getattr(K, "_saved", []):
            K.pop_scope()
        K.finish()
        es.close()
        return nc

    if stage == 0.1:
        return early()

    def pcol(name, i=0):
        c = PV[name] + i
        return pvec[:, c:c + 1]

    hT = K.sb("hT", [128, 8, TT], BF16)
    def norm_transpose(src_tile_fn, gname, dst):
        K.push_scope()
        NB = 3
        xring = [K.sb("xring", [128, D], F32) for _ in range(NB)]
        xn = [K.sb("xn", [128, D], BF16) for _ in range(NB)]
        junk = K.sb("junk", [128, D], BF16)
        stat = [K.sb("stat", [128, 4], F32) for _ in range(NB)]

        def nt_a(i):
            rows = 128 if i < 16 else NS
            xt, st, xb = xring[i % NB], stat[i % NB], xn[i % NB]
            K.dma("sp", xt[:rows, :], src_tile_fn(i))
            K.act(junk[:rows, :], xt[:rows, :], AF.Square, accum=st[:rows, 0:1])
            K.act(st[:rows, 1:2], st[:rows, 0:1], AF.Sqrt, bias=epsT[:rows, :], scale=1.0 / D)
            K.recip(st[:rows, 2:3], st[:rows, 1:2])
            K.act(xb[:rows, :], xt[:rows, :], AF.Copy, scale=st[:rows, 2:3])

        def nt_b(i):
            rows = 128 if i < 16 else NS
            col0 = i * 128
            xb = xn[i % NB]
            pb = PSB()
            pv = pb.v().re("p (k t) -> p k t", k=8)
            for kt in range(8):
                K.tr(pv[:, kt, :rows], xb[:rows, kt * 128:(kt + 1) * 128], identb[:rows, :rows])
            g0 = PV[gname]
            K.tt("dve", dst[:, :, col0:col0 + rows], pv[:, :, :rows],
                 pvec[:, g0:g0 + 8].un(2).bc([128, 8, rows]), ALU.mult)

        for step in range(17 + 1):
            if step < 17:
                nt_a(step)
            if step >= 1:
                nt_b(step - 1)
        K.pop_scope()

    def x_tile(i):
        return x_p[i * 128:(i + 1) * 128, :] if i < 16 else x_s[:, :]

    norm_transpose(x_tile, "g_pre_mix", hT)
    if stage == 0.2:
        return early()

    wring = [K.sb("wring", [128, 8, 128], BF16) for _ in range(2)]
    wr_i = [0]

    def load_w_cols(src, c0, ncols_total, kt_n=8, width=128):
        wr_i[0] += 1
        wb = wring[wr_i[0] % len(wring)]
        K.dma("pool", wb[:, :kt_n, :width],
              dap(src, c0, [[ncols_total, 128], [128 * ncols_total, kt_n], [1, width]]))
        return wb

    def proj_fm(wb, kt_n, src_act, tb, ps):
        col0, n = TBLK[tb]
        for kt in range(kt_n):
            K.mm(ps[:, :n], wb[:, kt, :], src_act[:, kt, col0:col0 + n], start=(kt == 0), stop=(kt == kt_n - 1))

    hgl_d = K.dram("hgl_scratch", [128, 4 * TT], BF16)
    TWO_PI = 2.0 * math.pi

    K.push_scope()
    ubf = K.sb("ubf", [128, 4, TT], BF16)
    wu4 = [K.sb("wu4", [128, 8, 128], BF16) for _ in range(4)]
    for ft in range(4):
        K.dma("pool", wu4[ft].v(), dap(w_in, SHIFT + ft * 128, [[PROJ, 128], [128 * PROJ, 8], [1, 128]]))
        for tb in range(5):
            col0, n = TBLK[tb]
            ps = PSF()
            proj_fm(wu4[ft], 8, hT, tb, ps)
            K.copy("act" if tb % 2 else "dve", ubf[:, ft, col0:col0 + n], ps[:, :n])

    if stage == 1.1:
        return early()
    sp_ = K.sb("s5par", [128, 16, 24], F32)
    (P_LRE, P_LIM, P_DT, P_MAG, P_TH, P_FRE, P_FIM, P_LBRE, P_LBIM, P_T0, P_T1, P_T2, P_T3,
     P_TH2, P_MAG2, P_G0R, P_G0I, P_C1, P_S1) = range(19)

    def sp(k):
        return sp_[:, :, k]

    K.dma("sp", sp(P_LRE), dap(s5_lam_re, 0, [[1, 128], [128, 16]]), slow=True)
    K.dma("sp", sp(P_LIM), dap(s5_lam_im, 0, [[1, 128], [128, 16]]), slow=True)
    for gl in range(2):
        K.dma("sp", sp_[gl * 64:(gl + 1) * 64, :, P_DT], dap(s5_log_dt, gl, [[0, 64], [2, 16]]), slow=True)
    K.act(sp(P_DT), sp(P_DT), AF.Exp)
    K.tt("dve", sp(P_T0), sp(P_LRE), sp(P_DT), ALU.mult)
    K.act(sp(P_MAG), sp(P_T0), AF.Exp)
    K.copy("dve", sp(P_MAG2), sp(P_MAG))
    for _k in range(LS - 1):
        K.tt("dve", sp(P_MAG2), sp(P_MAG2), sp(P_MAG), ALU.mult)
    K.tt("dve", sp(P_TH), sp(P_LIM), sp(P_DT), ALU.mult)
    K.ts("dve", sp(P_TH2), sp(P_TH), float(LS), None, ALU.mult)

    NH = 512 // LS
    LC = min(128, NH)
    NRC = NH // LC
    pw = K.sb("pw", [128, 16, 2 * (LS + 1)], F32)
    cosT = K.sb("cosT", [128, 16, LC], F32)
    sinT = K.sb("sinT", [128, 16, LC], F32)
    BT = [K.sb("BT", [128, 32, 128], BF16) for _ in range(LS)]
    CTb = K.sb("CTb", [128, 32, 128], BF16)
    VTb = [K.sb("VTb", [128, 32, 128], BF16) for _ in range(LS - 1)]
    Kt = [K.sb("Kt", [128, 4, 128], BF16) for _ in range(LS - 1)]
    K.push_scope()
    ji = K.sb("ji", [128, LC], I32)
    jf = K.sb("jf", [128, LC], F32)
    ang = K.sb("ang", [128, 16, LC], F32)
    rr = K.sb("rr", [128, 16, LC], F32)
    kf = K.sb("kf", [128, 16, LC], F32)
    ki = K.sb("ki", [128, 16, LC], I32)
    K.iota(ji.v(), [[1, LC]], 1, 0)
    K.copy("dve", jf.v(), ji.v())
    K.tt("dve", ang.v(), sp(P_TH2).un(2).bc([128, 16, LC]), jf.v().un(1).bc([128, 16, LC]), ALU.mult)

    def sin_reduced(out, a_in, shift, rr_, kf_, ki_):
        C1 = 6.28125
        C2 = TWO_PI - C1
        K.ts("dve", rr_, a_in, shift, 1.0 / TWO_PI, ALU.add, ALU.mult)
        K.copy("dve", ki_, rr_)
        K.copy("dve", kf_, ki_)
        K.ts("dve", rr_, a_in, shift, None, ALU.add)
        K.stt(rr_, kf_, -C1, rr_, ALU.mult, ALU.add)
        K.stt(rr_, kf_, -C2, rr_, ALU.mult, ALU.add)
        K.ts("dve", kf_, rr_, math.pi, None, ALU.is_gt)
        K.stt(rr_, kf_, -TWO_PI, rr_, ALU.mult, ALU.add)
        K.ts("dve", kf_, rr_, -math.pi, None, ALU.is_lt)
        K.stt(rr_, kf_, TWO_PI, rr_, ALU.mult, ALU.add)
        K.ts("dve", rr_, rr_, -math.pi, math.pi, ALU.max, ALU.min)
        K.act(out, rr_, AF.Sin)

    sin_reduced(sinT.v(), ang.v(), 0.0, rr.v(), kf.v(), ki.v())
    sin_reduced(cosT.v(), ang.v(), 0.5 * math.pi, rr.v(), kf.v(), ki.v())
    sin_reduced(sp(P_S1), sp(P_TH), 0.0, rr[:, :, 0], kf[:, :, 0], ki[:, :, 0])
    sin_reduced(sp(P_C1), sp(P_TH), 0.5 * math.pi, rr[:, :, 0], kf[:, :, 0], ki[:, :, 0])
    K.tt("dve", sp(P_LBRE), sp(P_MAG), sp(P_C1), ALU.mult)
    K.tt("dve", sp(P_LBIM), sp(P_MAG), sp(P_S1), ALU.mult)
    K.tt("dve", sp(P_T0), sp(P_LRE), sp(P_LRE), ALU.mult)
    K.tt("dve", sp(P_T1), sp(P_LIM), sp(P_LIM), ALU.mult)
    K.tt("dve", sp(P_T0), sp(P_T0), sp(P_T1), ALU.add)
    K.recip(sp(P_T0), sp(P_T0))
    K.ts("dve", sp(P_T1), sp(P_LBRE), -1.0, None, ALU.add)
    K.tt("dve", sp(P_T2), sp(P_T1), sp(P_LRE), ALU.mult)
    K.tt("dve", sp(P_T3), sp(P_LBIM), sp(P_LIM), ALU.mult)
    K.tt("dve", sp(P_T2), sp(P_T2), sp(P_T3), ALU.add)
    K.tt("dve", sp(P_FRE), sp(P_T2), sp(P_T0), ALU.mult)
    K.tt("dve", sp(P_T2), sp(P_LBIM), sp(P_LRE), ALU.mult)
    K.tt("dve", sp(P_T3), sp(P_T1), sp(P_LIM), ALU.mult)
    K.tt("dve", sp(P_T2), sp(P_T2), sp(P_T3), ALU.subtract)
    K.tt("dve", sp(P_FIM), sp(P_T2), sp(P_T0), ALU.mult)
    K.memset("dve", pw[:, :, 0], 1.0)
    K.memset("dve", pw[:, :, 1], 0.0)
    for k in range(1, LS + 1):
        pr_, pi_ = pw[:, :, 2 * (k - 1)], pw[:, :, 2 * (k - 1) + 1]
        K.tt("dve", sp(P_T0), pr_, sp(P_LBRE), ALU.mult)
        K.tt("dve", sp(P_T1), pi_, sp(P_LBIM), ALU.mult)
        K.tt("dve", pw[:, :, 2 * k], sp(P_T0), sp(P_T1), ALU.subtract)
        K.tt("dve", sp(P_T0), pr_, sp(P_LBIM), ALU.mult)
        K.tt("dve", sp(P_T1), pi_, sp(P_LBRE), ALU.mult)
        K.tt("dve", pw[:, :, 2 * k + 1], sp(P_T0), sp(P_T1), ALU.add)
    K.pop_scope()
    if stage == 1.2:
        return early()

    K.push_scope()
    CTf = [K.sb("CTf", [128, 16, 128], F32) for _ in range(2)]
    K.push_scope()
    Zc = K.sb("Zc", [128, 4 * 512], F32)
    for part, src in enumerate((s5_c_re, s5_c_im)):
        K.memset("pool", Zc.v(), 0.0)
        for g8 in range(8):
            K.dma("sp", dap(Zc, 16 * g8 * 2048 + 64 * g8, [[2048, 16], [512, 4], [1, 64]]),
                  dap(src, g8 * 1024, [[64, 16], [8192, 4], [1, 64]]))
        for q in range(4):
            ps = PSF()
            for j in range(4):
                K.tr(ps[:, j * 128:(j + 1) * 128], Zc[:, q * 512 + j * 128:q * 512 + (j + 1) * 128], identf.v())
            dst = CTf[part][:, q * 4:q * 4 + 4, :]
            if part == 0:
                K.copy("dve", dst, ps.v().re("p (j c) -> p j c", j=4))
            else:
                K.ts("dve", dst, ps.v().re("p (j c) -> p j c", j=4), -1.0, None, ALU.mult)
    K.pop_scope()
    Xr = K.sb("Xr", [128, 16 * 128], F32)
    Xi = K.sb("Xi", [128, 16 * 128], F32)
    for X, src in ((Xr, s5_b_re), (Xi, s5_b_im)):
        K.memset("pool", X.v(), 0.0)
        for gl in range(2):
            for a in range(4):
                K.dma("sp", dap(X, gl * 64 * 2048 + 16 * gl + a * 512, [[2048, 64], [160, 4], [1, 16]]),
                      dap(src, gl * 1024 + a * 8192, [[16, 64], [2048, 4], [1, 16]]), slow=True)
    c4v = CTb.v().re("p (i two) c -> p i two c", two=2)
    K.copy("act", c4v[:, :, 0, :], CTf[0].v())
    K.copy("act", c4v[:, :, 1, :], CTf[1].v())
    T1 = K.sb("T1", [128, 16, 128], F32)
    Xs = [K.sb("Xs", [128, 16, 128], F32) for _ in range(2)]
    T2 = Xs[0]
    bcv = lambda vv: vv.un(2).bc([128, 16, 128])
    for jv in range(LS - 1):
        lr_, li_ = pw[:, :, 2 * (jv + 1)], pw[:, :, 2 * (jv + 1) + 1]
        v4v = VTb[jv].v().re("p (i two) c -> p i two c", two=2)
        K.tt("dve", T1.v(), CTf[0].v(), bcv(lr_), ALU.mult)
        K.tt("dve", T2.v(), CTf[1].v(), bcv(li_), ALU.mult)
        K.tt("dve", v4v[:, :, 0, :], T1.v(), T2.v(), ALU.add)
        K.tt("dve", T1.v(), CTf[1].v(), bcv(lr_), ALU.mult)
        K.tt("dve", T2.v(), CTf[0].v(), bcv(li_), ALU.mult)
        K.tt("dve", v4v[:, :, 1, :], T1.v(), T2.v(), ALU.subtract)
    gsc = K.sb("gsc", [128, 16, 4], F32)
    x3 = lambda b_: b_.v().re("p (i c) -> p i c", i=16)
    for v in range(LS):
        kpow = LS - 1 - v
        lr_, li_ = pw[:, :, 2 * kpow], pw[:, :, 2 * kpow + 1]
        K.tt("dve", gsc[:, :, 2], lr_, sp(P_FRE), ALU.mult)
        K.tt("dve", gsc[:, :, 3], li_, sp(P_FIM), ALU.mult)
        K.tt("dve", gsc[:, :, 0], gsc[:, :, 2], gsc[:, :, 3], ALU.subtract)
        K.tt("dve", gsc[:, :, 2], lr_, sp(P_FIM), ALU.mult)
        K.tt("dve", gsc[:, :, 3], li_, sp(P_FRE), ALU.mult)
        K.tt("dve", gsc[:, :, 1], gsc[:, :, 2], gsc[:, :, 3], ALU.add)
        gr, gi = bcv(gsc[:, :, 0]), bcv(gsc[:, :, 1])
        K.tt("dve", Xs[0].v(), x3(Xr), gr, ALU.mult)
        K.tt("dve", T1.v(), x3(Xi), gi, ALU.mult)
        K.tt("dve", Xs[0].v(), Xs[0].v(), T1.v(), ALU.subtract)
        K.tt("dve", Xs[1].v(), x3(Xr), gi, ALU.mult)
        K.tt("dve", T1.v(), x3(Xi), gr, ALU.mult)
        K.tt("dve", Xs[1].v(), Xs[1].v(), T1.v(), ALU.add)
        for part in range(2):
            for i4 in range(4):
                ps = PSF()
                for j in range(4):
                    K.tr(ps[:, j * 128:(j + 1) * 128], Xs[part][:, i4 * 4 + j, :], identf.v())
                dstb = BT[v].v().re("p (i two) c -> p i two c", two=2)[:, i4 * 4:i4 * 4 + 4, part, :]
                K.copy("act" if i4 % 2 else "dve", dstb, ps.v().re("p (j c) -> p j c", j=4))
        if kpow <= LS - 2:
            for q in range(4):
                ps = PSF()
                for j in range(4):
                    i = q * 4 + j
                    K.mm(ps[:, 0:128], Xs[0][:, i, :], CTf[0][:, i, :], start=(j == 0), stop=False)
                    K.mm(ps[:, 0:128], Xs[1][:, i, :], CTf[1][:, i, :], start=False, stop=(j == 3))
                K.copy("dve", Kt[kpow][:, q, :], ps[:, 0:128])
    K.pop_scope()
    if stage == 1.3:
        return early()

    x1b = [K.sb("x1sb", [128, 16, NS], BF16) for _ in range(2)]
    K.push_scope()
    s0_tm = [K.sb("s0tm", [NS, 2048], F32) for _ in range(2)]
    s0 = [K.sb("s0", [128, 16, NS], F32) for _ in range(2)]
    K.dma("sp", s0_tm[0].v(), st_re.v())
    K.dma("sp", s0_tm[1].v(), st_im.v())
    for part in range(2):
        ps = PSF()
        for i in range(16):
            K.tr(ps[:, i * NS:(i + 1) * NS], s0_tm[part][:, i * 128:(i + 1) * 128], identf[:NS, :NS])
        K.copy("dve", s0[part].v(), ps[:, :16 * NS].re("p (i n) -> p i n", i=16))
    psr = PSF()
    psi = PSF()
    for i in range(16):
        K.mm(psr[:, i * NS:(i + 1) * NS], BT[LS - 1][:, 2 * i, :], ubf[:, i // 4, T:TT])
        K.mm(psi[:, i * NS:(i + 1) * NS], BT[LS - 1][:, 2 * i + 1, :], ubf[:, i // 4, T:TT])
    sw = [K.sb("sw", [128, 16, NS], F32) for _ in range(2)]
    x1 = [K.sb("x1s", [128, 16, NS], F32) for _ in range(2)]
    bc16 = lambda k: sp(k).un(2).bc([128, 16, NS])
    pr3 = psr[:, :16 * NS].re("p (i n) -> p i n", i=16)
    pi3 = psi[:, :16 * NS].re("p (i n) -> p i n", i=16)
    K.tt("dve", sw[0].v(), s0[0].v(), bc16(P_LBRE), ALU.mult)
    K.tt("dve", sw[1].v(), s0[1].v(), bc16(P_LBIM), ALU.mult)
    K.tt("dve", sw[0].v(), sw[0].v(), sw[1].v(), ALU.subtract)
    K.tt("dve", x1[0].v(), sw[0].v(), pr3, ALU.add)
    K.tt("dve", sw[0].v(), s0[1].v(), bc16(P_LBRE), ALU.mult)
    K.tt("dve", sw[1].v(), s0[0].v(), bc16(P_LBIM), ALU.mult)
    K.tt("dve", sw[0].v(), sw[0].v(), sw[1].v(), ALU.add)
    K.tt("dve", x1[1].v(), sw[0].v(), pi3, ALU.add)
    for part in range(2):
        K.copy("act", x1b[part].v(), x1[part].v())
        xs_tm = s0_tm[part]
        for i4 in range(4):
            ps = PSF()
            for j in range(4):
                i = i4 * 4 + j
                K.tr(ps[:NS, j * 128:(j + 1) * 128], x1[part][:, i, :], identf.v())
            K.copy("dve", xs_tm[:, i4 * 512:(i4 + 1) * 512], ps[:NS, :])
        K.dma("sp", (o_re_s if part == 0 else o_im_s).v(), xs_tm.v())
    K.pop_scope()
    if stage == 1.4:
        return early()

    hgl = K.sb("hgl", [128, 4, TT], BF16)
    carry = K.sb("carry", [128, 16, 2], F32)
    K.memset("dve", carry.v(), 0.0)
    UN = [dict(bR=K.sb("bR", [128, NH], F32), bI=K.sb("bI", [128, NH], F32),
               wR=K.sb("wR", [128, NH], F32), wI=K.sb("wI", [128, NH], F32),
               t1=K.sb("t1", [128, NH], F32), t2=K.sb("t2", [128, NH], F32)) for _ in range(4)]
    xbr = [K.sb("xbr", [128, NH + 1], BF16) for _ in range(4)]
    xbi = [K.sb("xbi", [128, NH + 1], BF16) for _ in range(4)]
    yv = [K.sb("yv", [128, 512], F32) for _ in range(2)]
    y2 = [K.sb("y2", [128, 512], F32) for _ in range(2)]

    def gelu_to(dst_bf, yv_, y2_):
        K.act(y2_, yv_, AF.Square)
        K.act(y2_, y2_, AF.Identity, bias=1.0, scale=0.044715)
        K.tt("pool", y2_, y2_, yv_, ALU.mult)
        K.act(y2_, y2_, AF.Sigmoid, scale=1.5957691216057308)
        K.tt("pool", dst_bf, y2_, yv_, ALU.mult)

    vr = lambda b: b.v().re("p (a l) -> p a l", a=NRC)
    for q in range(4):
        for tb in range(4):
            col0, n = TBLK[tb]
            ugrp = ubf[:, q, col0:col0 + n].re("p (c j) -> p j c", j=LS)
            for j in range(4):
                i = q * 4 + j
                U = UN[j]
                psr = PSF()
                psi = PSF()
                for v in range(LS):
                    K.mm(psr[:, :NH], BT[v][:, 2 * i, :], ugrp[:, v, :], start=(v == 0), stop=(v == LS - 1))
                for v in range(LS):
                    K.mm(psi[:, :NH], BT[v][:, 2 * i + 1, :], ugrp[:, v, :], start=(v == 0), stop=(v == LS - 1))
                tC = cosT[:, i, :].un(1).bc([128, NRC, LC])
                tS = sinT[:, i, :].un(1).bc([128, NRC, LC])
                pr2 = psr[:, :NH].re("p (a l) -> p a l", a=NRC)
                pi2 = psi[:, :NH].re("p (a l) -> p a l", a=NRC)
                K.tt("dve", vr(U["t1"]), pr2, tC, ALU.mult)
                K.tt("dve", vr(U["t2"]), pi2, tS, ALU.mult)
                K.tt("pool", U["bR"].v(), U["t1"].v(), U["t2"].v(), ALU.add)
                K.tt("dve", vr(U["wR"]), pi2, tC, ALU.mult)
                K.tt("dve", vr(U["wI"]), pr2, tS, ALU.mult)
                K.tt("dve", U["bI"].v(), U["wR"].v(), U["wI"].v(), ALU.subtract)
                K.copy("act", xbr[j][:, 0:1], carry[:, i, 0:1])
                K.copy("act", xbi[j][:, 0:1], carry[:, i, 1:2])
            for a in range(NRC):
                cs = slice(a * LC, (a + 1) * LC)
                for j in range(4):
                    i = q * 4 + j
                    U = UN[j]
                    rho = sp_[:, i, P_MAG2:P_MAG2 + 1].bc([128, LC])
                    if a == 0:
                        ire, iim = carry[:, i, 0:1], carry[:, i, 1:2]
                    else:
                        ire, iim = U["bR"][:, a * LC - 1:a * LC], U["bI"][:, a * LC - 1:a * LC]
                    K.scan(U["wR"][:, cs], rho, U["bR"][:, cs], ire)
                    K.scan(U["wI"][:, cs], rho, U["bI"][:, cs], iim)
                for j in range(4):
                    i = q * 4 + j
                    U = UN[j]
                    K.tt("pool", U["t1"][:, cs], U["wR"][:, cs], cosT[:, i, :], ALU.mult)
                    K.tt("dve", U["t2"][:, cs], U["wI"][:, cs], sinT[:, i, :], ALU.mult)
                    K.tt("dve", U["bR"][:, cs], U["t1"][:, cs], U["t2"][:, cs], ALU.subtract)
                    K.tt("pool", U["t1"][:, cs], U["wR"][:, cs], sinT[:, i, :], ALU.mult)
                    K.tt("dve", U["t2"][:, cs], U["wI"][:, cs], cosT[:, i, :], ALU.mult)
                    K.tt("dve", U["bI"][:, cs], U["t1"][:, cs], U["t2"][:, cs], ALU.add)
            for j in range(4):
                i = q * 4 + j
                U = UN[j]
                K.copy("dve", carry[:, i, 0:1], U["bR"][:, NH - 1:NH])
                K.copy("dve", carry[:, i, 1:2], U["bI"][:, NH - 1:NH])
                K.copy("act", xbr[j][:, 1:NH + 1], U["bR"].v())
                K.copy("act", xbi[j][:, 1:NH + 1], U["bI"].v())
            yy = yv[tb % 2]
            yyp = yy.v().re("p (j c) -> p j c", j=LS)
            for jo in range(LS):
                psy = PSF()
                if jo == LS - 1:
                    for j in range(4):
                        i = q * 4 + j
                        K.mm(psy[:, :NH], CTb[:, 2 * i, :], xbr[j][:, 1:NH + 1], start=(j == 0), stop=False)
                        K.mm(psy[:, :NH], CTb[:, 2 * i + 1, :], xbi[j][:, 1:NH + 1], start=False, stop=(j == 3))
                else:
                    for j in range(4):
                        i = q * 4 + j
                        K.mm(psy[:, :NH], VTb[jo][:, 2 * i, :], xbr[j][:, 0:NH], start=(j == 0), stop=False)
                        K.mm(psy[:, :NH], VTb[jo][:, 2 * i + 1, :], xbi[j][:, 0:NH], start=False, stop=False)
                    for ii in range(jo + 1):
                        K.mm(psy[:, :NH], Kt[jo - ii][:, q, :], ugrp[:, ii, :], start=False, stop=(ii == jo))
                K.copy("act", yyp[:, jo, :], psy[:, :NH])
            psu = PSF()
            proj_fm(wu4[q], 8, hT, tb, psu)
            K.stt(yyp, psu.v().re("p (c j) -> p j c", j=LS), pcol("s5_d", q), yyp, ALU.mult, ALU.add)
            y2p = y2[tb % 2].v().re("p (j c) -> p j c", j=LS)
            gelu_to(hgl[:, q, col0:col0 + n].re("p (c j) -> p j c", j=LS), yyp, y2p)
        psy = PSF()
        for j in range(4):
            i = q * 4 + j
            K.mm(psy[:, :NS], CTb[:, 2 * i, :], x1b[0][:, i, :], start=(j == 0), stop=False)
            K.mm(psy[:, :NS], CTb[:, 2 * i + 1, :], x1b[1][:, i, :], start=False, stop=(j == 3))
        K.copy("act", yv[0][:, :NS], psy[:, :NS])
        psu = PSF()
        proj_fm(wu4[q], 8, hT, 4, psu)
        K.stt(yv[0][:, :NS], psu[:, :NS], pcol("s5_d", q), yv[0][:, :NS], ALU.mult, ALU.add)
        gelu_to(hgl[:, q, T:TT], yv[0][:, :NS], y2[0][:, :NS])
    if stage == 1.5:
        return early()
    K.dma("sp", dap(o_re_p, 0, [[1, 128], [128, 16]]), carry[:, :, 0], slow=True)
    K.dma("sp", dap(o_im_p, 0, [[1, 128], [128, 16]]), carry[:, :, 1], slow=True)
    K.dma("sp", hgl_d.v(), hgl.v().re("p k t -> p (k t)"))
    K.pop_scope()
    if stage == 2:
        zt = K.sb("zt", [128, 4096], F32)
        K.memset("pool", zt.v(), 0.0)
        K.dma("sp", o_shift_p.v(), zt[0:1, 0:SHIFT])
        K.dma("sp", o_shift_s.v(), zt[0:NS, 0:SHIFT])
        K.dma("sp", o_wkv_p.v().re("h v k -> h (v k)"), zt[0:8, :])
        K.dma("sp", o_wkv_s.v(), zt[:, :])
        for i in range(16):
            K.dma("sp", y_p[i * 128:(i + 1) * 128, :], zt[:, 0:D])
        K.dma("sp", y_s.v(), zt[0:NS, 0:D])
        return early()
    K.push_scope()
    lnxg = K.sb("lnxg", [128, RW], F32)
    lnxb = K.sb("lnxb", [128, RW], F32)
    K.dma("sp", lnxg.v(), dap(lnx_g, 0, [[0, 128], [1, RW]]))
    K.dma("sp", lnxb.v(), dap(lnx_b, 0, [[0, 128], [1, RW]]))
    par = K.sb("spar", [128, 3, 64], F32)
    for tk in range(NS):
        for j, src in enumerate((lnx_g, lnx_b, r_k)):
            K.dma("sp", par[tk * 8:(tk + 1) * 8, j, :], dap(src, 0, [[64, 8], [1, 64]]), sem_buf=par)

    lora_bf = K.sb("lora_bf", [128, TT], BF16)
    lora_up = K.sb("lora_up", [128, RW], BF16)
    oaT = K.sb("oaT", [128, 4, TT], BF16)
    o_d = K.dram("o_scratch", [T, RW], F32)
    bonus_d = K.dram("bonus_scratch", [T, RW], F32)
    samp_d = K.dram("samp_scratch", [NS * 8, 7, 64], F32)
    sampo_d = K.dram("sampo_scratch", [NS, RW], F32)

    K.push_scope()
    mask4 = K.sb("mask4", [128, 4, 128], F32)
    maskNT = K.sb("maskNT", [128, 128], F32)
    resetm = K.sb("resetm", [128, T], BF16)
    blk1 = K.sb("blk1", [128, 128], F32)
    eps_ln = K.sb("eps_ln", [128, 1], F32)
    K.memset("pool", eps_ln.v(), 64e-5)
    for j in range(4):
        K.aselect(mask4[:, j, :], ones_f.v(), [[1, 128]], ALU.is_gt if j % 2 == 0 else ALU.is_ge, 0.0, 0, -1)
    K.aselect(maskNT.v(), ones_f.v(), [[-1, 128]], ALU.is_gt, 0.0, 0, 1)
    K.memset("pool", resetm.v(), 1.0)
    K.memset("pool", resetm.v().re("p (c l) -> p c l", l=128)[:, :, 0:1], 0.0)
    K.memset("pool", blk1.v(), 0.0)
    K.memset("pool", blk1[0:64, 0:64], 1.0)
    K.memset("pool", blk1[64:128, 64:128], 1.0)

    K.dma("pool", lora_up[0:32, :], w_decay_up.v())
    K.dma("pool", lora_up[32:64, :], w_aaa_up.v())
    K.dma("pool", lora_up[64:128, :], w_gate_up.v())
    sh0T = K.sb("sh0T", [128, 13, NS], F32)
    shout = [K.sb("shout", [NS + 1, 128], F32) for _ in range(1)]
    K.push_scope()
    sh_tm = K.sb("sh_tm", [NS, SHIFT], F32)
    K.dma("sp", sh_tm.v(), st_shift.v())
    ps = PSF()
    for ft in range(13):
        K.tr(ps[:, ft * NS:(ft + 1) * NS], sh_tm[:, ft * 128:(ft + 1) * 128], identf[:NS, :NS])
    K.copy("dve", sh0T.v(), ps[:, :13 * NS].re("p (f n) -> p f n", f=13))
    K.pop_scope()

    omm = K.sb("omm", [128, 13], F32)
    K.ts("dve", omm.v(), pvec[:, PV["mu"]:PV["mu"] + 13], -1.0, 1.0, ALU.mult, ALU.add)
    omka = K.sb("omka", [128, 4], F32)
    K.ts("dve", omka.v(), pvec[:, PV["k_a"]:PV["k_a"] + 4], -1.0, 1.0, ALU.mult, ALU.add)

    SC = [K.sb("scr", [128, TT], F32) for _ in range(6)]

    def proj_shift_tile(ft, pr, tmp, xr_out, xr_out_dt_bf=None, samp32=None):
        wb = load_w_cols(w_in, ft * 128, PROJ)
        for tb in range(5):
            col0, n = TBLK[tb]
            ps = PSF()
            proj_fm(wb, 8, hT, tb, ps)
            K.copy("act", pr[:, col0:col0 + n], ps[:, :n])
            K.act(tmp[:, col0:col0 + n], ps[:, :n], AF.Copy, scale=omm[:, ft:ft + 1])
        ps = PSF()
        K.tr(ps[:NS + 1, :128], pr[:, T - 1:TT], identf.v())
        so = shout[0]
        K.copy("dve", so.v(), ps[:NS + 1, :128])
        K.dma("sp", o_shift_p[0:1, ft * 128:(ft + 1) * 128], so[0:1, :])
        K.dma("sp", o_shift_s[:, ft * 128:(ft + 1) * 128], so[1:NS + 1, :])
        mu_c = pcol("mu", ft)
        K.stt(xr_out[:, 1:T], pr[:, 0:T - 1], mu_c, tmp[:, 1:T], ALU.mult, ALU.add)
        K.copy("pool", xr_out[:, 0:1], tmp[:, 0:1])
        K.stt(xr_out[:, T:TT], sh0T[:, ft, :], mu_c, tmp[:, T:TT], ALU.mult, ALU.add)
        if samp32 is not None:
            K.stt(samp32, sh0T[:, ft, :], mu_c, tmp[:, T:TT], ALU.mult, ALU.add)

    proj_shift_tile(12, SC[0], SC[1], SC[2].v())
    K.act(lora_bf[0:32, :], SC[2][0:32, :], AF.Tanh)
    K.act(lora_bf[32:64, :], SC[2][32:64, :], AF.Copy)
    K.act(lora_bf[64:128, :], SC[2][64:128, :], AF.Sigmoid)

    vbf = K.sb("vbf", [128, TT], BF16)
    vs32 = K.sb("vs32", [128, NS], F32)
    AR = K.sb("AR", [128, 16, 2, 128], BF16)
    bti = K.sb("bti", [128, T], BF16)
    kti = K.sb("kti", [128, T], BF16)
    bhat = K.sb("bhat", [128, T], BF16)
    khat = K.sb("khat", [128, T], BF16)
    Wfm = bhat
    TMq = K.sb("TMq", [128, 16, 4, 128], BF16)
    gL = K.sb("gL", [128, 16], F32)
    Sf = K.sb("Sf", [128, 64], F32)
    Sb = K.sb("Sb", [128, 64], BF16)
    smp = K.sb("smp", [NS, 3, 128], F32)
    decs = K.sb("decs", [128, NS], F32)
    avs = K.sb("avs", [128, NS], F32)
    NPR = 16
    ABK = [K.sb("ABK", [128, 2, 128], BF16) for _ in range(NPR)]
    NPQ = 8
    ATMP = [K.sb("ATMP", [128, 2, 128], BF16) for _ in range(NPQ)]
    P0T = [K.sb("P0T", [128, 128], BF16) for _ in range(NPQ)]
    PMb = [[K.sb("PMb", [128, 3, 128], BF16) for _ in range(2)] for _ in range(NPQ)]
    Ybf = [K.sb("Ybf", [128, 64], BF16) for _ in range(NPQ)]
    QTb = [[K.sb("QTb", [128, 128], BF16) for _ in range(2)] for _ in range(NPQ)]
    Ubar = [K.sb("Ubar", [128, 64], F32) for _ in range(NPR)]
    Ut = [K.sb("Ut", [128, 64], BF16) for _ in range(4)]
    Otile = [K.sb("Otile", [128, 128], F32) for _ in range(2)]
    btile = [K.sb("btile", [128, 512], F32) for _ in range(1)]
    wkvo = K.sb("wkvo", [64, 128], F32)

    def derivedA(hp):
        xr_r, xr_k = SC[0], SC[1]
        proj_shift_tile(0 * 4 + hp, SC[2], SC[3], xr_r.v())
        yield
        proj_shift_tile(1 * 4 + hp, SC[2], SC[3], xr_k.v())
        yield
        proj_shift_tile(2 * 4 + hp, SC[2], SC[3], vbf.v(), samp32=vs32.v())
        yield
        sigd, csum, afm, kkn = SC[2], SC[3], SC[4], SC[5]
        hc = slice(hp * 128, (hp + 1) * 128)
        for tb in range(5):
            col0, n = TBLK[tb]
            ps = PSF()
            K.mm(ps[:, :n], lora_up[0:32, hc], lora_bf[0:32, col0:col0 + n])
            K.act(sigd[:, col0:col0 + n], ps[:, :n], AF.Sigmoid, bias=pcol("w0", hp))
            yield
            ps = PSF()
            K.mm(ps[:, :n], lora_up[32:64, hc], lora_bf[32:64, col0:col0 + n])
            K.act(afm[:, col0:col0 + n], ps[:, :n], AF.Sigmoid, bias=pcol("a0", hp))
            yield
        K.act(kkn.v(), xr_k.v(), AF.Square, scale=pcol("k_k", hp))
        yield
        for tb in range(5):
            col0, n = TBLK[tb]
            ps = PSF()
            K.mm(ps[:, :n], blk1.v(), kkn[:, col0:col0 + n])
            K.ts("dve", csum[:, col0:col0 + n], ps[:, :n], 1e-24, None, ALU.max)
            yield
        K.act(csum.v(), csum.v(), AF.Ln)
        yield
        K.act(csum.v(), csum.v(), AF.Exp, scale=-0.5)
        yield
        K.stt(kkn.v(), xr_k.v(), pcol("k_k", hp), csum.v(), ALU.mult, ALU.mult)
        yield
        K.ts("pool", csum.v(), afm.v(), pcol("k_a", hp), omka[:, hp:hp + 1], ALU.mult, ALU.add)
        yield
        K.tt("pool", xr_k.v(), xr_k.v(), csum.v(), ALU.mult)
        yield
        kmod = xr_k
        K.tt("pool", afm.v(), kkn.v(), afm.v(), ALU.mult)
        yield
        bfm = afm
        K.stt(csum.v(), xr_r.v(), pcol("r_k", hp), kmod.v(), ALU.mult, ALU.mult)
        yield
        for tb in range(4):
            col0, n = TBLK[tb]
            ps = PSF()
            K.mm(ps[:, :n], blk1.v(), csum[:, col0:col0 + n])
            K.tt("dve", csum[:, col0:col0 + n], ps[:, :n], vbf[:, col0:col0 + n], ALU.mult)
            yield
        for c4 in range(4):
            ps = PSF()
            for j in range(4):
                c = c4 * 4 + j
                K.tr(ps[:, j * 128:(j + 1) * 128], csum[:, c * 128:(c + 1) * 128], identf.v())
            bt = btile[0]
            K.copy("act", bt.v(), ps.v())
            yield
            K.dma("sp", dap(bonus_d, c4 * 512 * RW + hp * 128, [[RW, 128], [128 * RW, 4], [1, 128]]),
                  bt.v().re("p (j f) -> p j f", j=4))
            yield
        K.act(decs.v(), sigd[:, T:TT], AF.Exp, scale=-CDEC)
        yield
        K.ts("dve", avs.v(), kkn[:, T:TT], -1.0, None, ALU.mult)
        yield
        yield

    def derivedB(hp):
        xr_r, xr_k = SC[0], SC[1]
        sigd, csum, afm, kkn = SC[2], SC[3], SC[4], SC[5]
        kmod, bfm = xr_k, afm
        K.scan(csum[:, 0:T], resetm.v(), sigd[:, 0:T], 0.0)
        gt = sigd
        c3 = lambda b_: b_[:, 0:T].re("p (c l) -> p c l", l=128)
        K.act(gt[:, 0:T], csum[:, 0:T], AF.Exp, scale=-CDEC)
        K.tt("dve", AR[:, :, 1, :], c3(xr_r), c3(gt), ALU.mult)
        K.copy("dve", gL.v(), c3(gt)[:, :, 127])
        K.stt(AR[:, :, 0, 1:128], c3(kkn)[:, :, 1:128], -1.0, c3(gt)[:, :, 0:127], ALU.mult, ALU.mult)
        K.ts("dve", AR[:, :, 0, 0], c3(kkn)[:, :, 0], -1.0, None, ALU.mult)
        K.act(gt[:, 0:T], csum[:, 0:T], AF.Exp, scale=CDEC)
        K.tt("dve", bti.v(), bfm[:, 0:T], gt[:, 0:T], ALU.mult)
        K.tt("dve", kti.v(), kmod[:, 0:T], gt[:, 0:T], ALU.mult)
        K.tt("dve", c3(gt), c3(csum)[:, :, 127:128].bc([128, 16, 128]), c3(csum), ALU.subtract)
        K.act(gt[:, 0:T], gt[:, 0:T], AF.Exp, scale=-CDEC)
        K.tt("dve", bhat.v(), bfm[:, 0:T], gt[:, 0:T], ALU.mult)
        K.tt("dve", khat.v(), kmod[:, 0:T], gt[:, 0:T], ALU.mult)
        for c in range(16):
            pb = PSB()
            pv = pb.v().re("p (q t) -> p q t", q=8)
            cs = slice(c * 128, (c + 1) * 128)
            K.tr(pv[:, 0, :], vbf[:, cs], identb.v())
            K.tr(pv[:, 1, :], bhat[:, cs], identb.v())
            K.tr(pv[:, 2, :], khat[:, cs], identb.v())
            K.tr(pv[:, 3, :], AR[:, c, 0, :], identb.v())
            K.copy("act" if c % 2 else "dve", TMq[:, c, :, :], pv[:, 0:4, :])
        sq_src = [xr_r[:, T:TT], kmod[:, T:TT], vs32.v(), decs.v(), avs.v(), bfm[:, T:TT]]
        for half in range(2):
            ps = PSF()
            for j in range(3):
                K.tr(ps[:NS, j * 128:(j + 1) * 128], sq_src[half * 3 + j], identf.v())
            K.copy("dve", smp.v(), ps[:NS, :384].re("p (q f) -> p q f", q=3))
            for j in range(3):
                q = half * 3 + j
                K.dma("sp", dap(samp_d, (2 * hp) * 448 + q * 64, [[8 * 448, NS], [448, 2], [1, 64]]),
                      smp[:, j, :].re("p (h n) -> p h n", h=2))


    def chunk_thunks(hp):
        CH = []
        CH.append(lambda: K.memset('dve', Sf.v(), 0.0))
        CH.append(lambda: K.memset('pool', Sb.v(), 0.0))
        pairs = [(c, hl) for c in range(16) for hl in range(2)]

        def pre_a(sl, c, hl):
            rows = slice(hl * 64, (hl + 1) * 64)
            cs = slice(c * 128, (c + 1) * 128)
            ps = PSF()
            arv = AR[rows, c, :, :].re("p a t -> p (a t)")
            K.mm(ps[:, 0:256], bti[rows, cs], arv)
            K.mm(ps[:, 256:512], kti[rows, cs], arv)
            ps2 = PSF()
            K.mm(ps2[:, 0:128], AR[rows, c, 0, :], bti[rows, cs])
            p4 = ps.v().re("p (q t) -> p q t", q=4)
            K.tt("dve", ATMP[sl % NPQ].v(), p4[:, 0::2, :], mask4[:, 0::2, :], ALU.mult)
            K.tt("dve", ABK[sl].v(), p4[:, 1::2, :], mask4[:, 1::2, :], ALU.mult)
            K.tt("dve", P0T[sl % NPQ].v(), ps2[:, 0:128], maskNT.v(), ALU.mult)
            K.tt("pool", QTb[sl % NPQ][0].v(), P0T[sl % NPQ].v(), identb.v(), ALU.add)

        def pre_lev(sl, lev, gi):
            if lev == 0:
                Pj, PjT, Mp = ATMP[sl % NPQ][:, 0, :], P0T[sl % NPQ].v(), identb.v()
            else:
                src = PMb[sl % NPQ][lev % 2]
                Pj, PjT, Mp = src[:, 0, :], src[:, 1, :], src[:, 2, :]
            dst = PMb[sl % NPQ][(lev + 1) % 2]
            ps = PSF()
            if lev < 6:
                K.mm(ps[:, 0:128], PjT, Pj)
                K.mm(ps[:, 128:256], Pj, PjT)
            K.mm(ps[:, 256:384], QTb[sl % NPQ][lev % 2].v(), Mp, start=True, stop=True)
            eng = "act" if (gi % 8) not in (2, 5, 7) else "dve"
            if lev < 6:
                K.copy(eng, dst.v(), ps[:, 0:384].re("p (q t) -> p q t", q=3))
                K.tt("pool", QTb[sl % NPQ][(lev + 1) % 2].v(), dst[:, 1, :], identb.v(), ALU.add)
            else:
                K.copy(eng, dst[:, 2, :], ps[:, 256:384])

        def pre_w(sl, c, hl, gi):
            rows = slice(hl * 64, (hl + 1) * 64)
            cs = slice(c * 128, (c + 1) * 128)
            Mfin = PMb[sl % NPQ][1][:, 2, :]
            ps = PSF()
            K.mm(ps[:, 0:128], TMq[:, c, 3, :], Mfin)
            K.mm(ps[:, 128:192], ATMP[sl % NPQ][:, 1, :], TMq[:, c, 0, rows])
            eng = "act" if gi % 2 else "dve"
            K.copy(eng, Wfm[rows, cs], ps[rows, 0:128])
            K.copy(eng, Ybf[sl % NPQ].v(), ps[:, 128:192])

        def pre_u(sl, gi):
            Mfin = PMb[sl % NPQ][1][:, 2, :]
            ps = PSF()
            K.mm(ps[:, 0:64], Mfin, Ybf[sl % NPQ].v())
            K.copy("act" if gi % 2 else "dve", Ubar[sl].v(), ps[:, 0:64])

        def precompute(group):
            th = []
            for (sl, c, hl) in group:
                th.append(partial(pre_a, sl, c, hl))
            for lev in range(7):
                for gi, (sl, c, hl) in enumerate(group):
                    th.append(partial(pre_lev, sl, lev, gi))
            for gi, (sl, c, hl) in enumerate(group):
                th.append(partial(pre_w, sl, c, hl, gi))
            for gi, (sl, c, hl) in enumerate(group):
                th.append(partial(pre_u, sl, gi))
            return th

        def ser_u(sl, c, hl, gi):
            rows = slice(hl * 64, (hl + 1) * 64)
            cs = slice(c * 128, (c + 1) * 128)
            ut = Ut[gi % 4]
            ps = PSF()
            K.mm(ps[:, 0:64], Wfm[rows, cs], Sb[rows, :])
            K.tt("dve", ut.v(), ps[:, 0:64], Ubar[sl].v(), ALU.add)

        def ser_os(sl, c, hl, gi):
            rows = slice(hl * 64, (hl + 1) * 64)
            ut = Ut[gi % 4]
            pso = PSF()
            K.mm(pso[:, 0:64], AR[rows, c, 1, :], Sb[rows, :], start=True, stop=False)
            K.mm(pso[:, 0:64], ABK[sl][:, 0, :], ut.v(), start=False, stop=False)
            K.mm(pso[:, 0:64], ABK[sl][:, 1, :], TMq[:, c, 0, rows], start=False, stop=True)
            pss = PSF()
            K.mm(pss[:, 0:64], TMq[:, c, 1, :], ut.v(), start=True, stop=False)
            K.mm(pss[:, 0:64], TMq[:, c, 2, :], TMq[:, c, 0, rows], start=False, stop=True)
            K.stt(Sf[rows, :], Sf[rows, :], gL[rows, c:c + 1], pss[rows, 0:64], ALU.mult, ALU.add)
            K.copy("act", Sb[rows, :], Sf[rows, :])
            ot = Otile[c % 2]
            K.copy("act", ot[:, rows], pso[:, 0:64])
            if hl == 1:
                K.dma("sp", o_d[c * 128:(c + 1) * 128, hp * 128:(hp + 1) * 128], ot.v())

        def serial(group):
            th = []
            for gi, (sl, c, hl) in enumerate(group):
                th.append(partial(ser_u, sl, c, hl, gi))
                th.append(partial(ser_os, sl, c, hl, gi))
            return th

        def run_merged(lists):
            idx = [0] * len(lists)
            while True:
                best, bf = -1, 2.0
                for li, L in enumerate(lists):
                    if idx[li] < len(L):
                        frac = idx[li] / len(L)
                        if frac < bf:
                            best, bf = li, frac
                if best < 0:
                    break
                lists[best][idx[best]]()
                idx[best] += 1

        G = 8
        groups = []
        for g0 in range(0, 32, G):
            groups.append([((g0 + i) % NPR, pairs[g0 + i][0], pairs[g0 + i][1]) for i in range(G)])
        CH.extend(precompute(groups[0]))
        for gi_ in range(len(groups)):
            lists = [serial(groups[gi_])]
            if gi_ + 1 < len(groups):
                lists.append(precompute(groups[gi_ + 1]))
            CH.extend(merge_lists(lists))
        def fin_state():
            ps = PSF()
            K.tr(ps[:64, 0:128], Sf.v(), identf.v())
            K.copy("dve", wkvo.v(), ps[:64, 0:128])
            K.dma("sp", o_wkv_p[2 * hp:2 * hp + 2, :, :].re("h v k -> v h k"), wkvo.v().re("v (h k) -> v h k", h=2))

        CH.append(fin_state)
        return CH

    def drain(gen):
        for _ in gen:
            pass

    NA_EST = 150
    drain(derivedA(0))
    derivedB(0)
    for hp in range(4):
        CH = chunk_thunks(hp)
        OUTL = []
        gen = derivedA(hp + 1) if hp + 1 < 4 else None
        per = max(1, len(CH) // NA_EST)
        pero = max(1, len(CH) // max(1, len(OUTL)))
        oi = 0
        for ti, th in enumerate(CH):
            th()
            if gen is not None and ti % per == per - 1:
                try:
                    next(gen)
                except StopIteration:
                    gen = None
            if oi < len(OUTL) and ti % pero == pero - 1:
                OUTL[oi]()
                oi += 1
        while oi < len(OUTL):
            OUTL[oi]()
            oi += 1
        if gen is not None:
            drain(gen)
        if hp + 1 < 4:
            derivedB(hp + 1)

    K.pop_scope()
    if stage == 3:
        zt = K.sb("zt", [128, 4096], F32)
        K.memset("pool", zt.v(), 0.0)
        K.dma("sp", o_wkv_s.v(), zt[:, :])
        for i in range(16):
            K.dma("sp", y_p[i * 128:(i + 1) * 128, :], zt[:, 0:D])
        K.dma("sp", y_s.v(), zt[0:NS, 0:D])
        return early()

    def bcast_row(name, src, n):
        t = K.sb(name, [128, n], F32)
        K.dma("sp", t.v(), dap(src, 0, [[0, 128], [1, n]]))
        return t

    K.push_scope()
    hgl = K.sb("hgl2", [128, 4, TT], BF16)
    K.dma("sp", hgl.v().re("p k t -> p (k t)"), hgl_d.v())
    wmo = K.sb("wmo", [128, 8, D], BF16)
    for kt in range(8):
        K.dma("pool", wmo[:, kt, :], w_merge_out[kt * 128:(kt + 1) * 128, :])
    gpost = bcast_row("gpost_mix", g_post_mix, D)
    mring = [K.sb("mring", [128, 8, 128], BF16) for _ in range(10)]
    mr_i = [0]

    def mload(src, c0, ncols_total, kt_n):
        mr_i[0] += 1
        wb = mring[mr_i[0] % len(mring)]
        K.dma("pool", wb[:, :kt_n, :], dap(src, c0, [[ncols_total, 128], [128 * ncols_total, kt_n], [1, 128]]))
        return wb

    GOFF = SHIFT + RW
    mw = {}

    def mg_load(f):
        mw[f] = (mload(w_rwkv_out, f * 128, D, 4), mload(w_in, GOFF + f * 128, PROJ, 8),
                 mload(w_in, GOFF + D + f * 128, PROJ, 8), mload(glu_w1, f * 128, D, 4), mload(glu_w2, f * 128, D, 4))

    mg_load(0)
    K.push_scope()
    eps_ln = K.sb("eps_ln2", [128, 1], F32)
    K.memset("pool", eps_ln.v(), 64e-5)
    NORF = 3
    fot_r = [K.sb("fot", [128, RW], F32) for _ in range(NORF)]
    fbt_r = [K.sb("fbt", [128, RW], F32) for _ in range(NORF)]
    fon_r = [K.sb("fon", [128, RW], F32) for _ in range(NORF)]
    fofb = [K.sb("fofb", [128, RW], BF16) for _ in range(NORF)]
    fst8 = [K.sb("fst8", [128, 6, 8], F32) for _ in range(NORF)]
    fh3 = lambda b_: b_.v().re("p (h n) -> p h n", h=8)

    def fout_a(c):
        r_ = c % NORF
        ot, bt, tmp, s8 = fot_r[r_], fbt_r[r_], fon_r[r_], fst8[r_]
        K.dma("sp", ot.v(), o_d[c * 128:(c + 1) * 128, :])
        K.dma("sp", bt.v(), bonus_d[c * 128:(c + 1) * 128, :])
        K.reduce(s8[:, 0, :], fh3(ot))
        K.act(tmp.v(), ot.v(), AF.Square)
        K.reduce(s8[:, 1, :], fh3(tmp))
        K.ts("dve", s8[:, 2, :], s8[:, 0, :], 1.0 / 64, None, ALU.mult)
        K.tt("dve", s8[:, 3, :], s8[:, 2, :], s8[:, 2, :], ALU.mult)
        K.stt(s8[:, 4, :], s8[:, 1, :], 1.0 / 64, s8[:, 3, :], ALU.mult, ALU.subtract)
        K.act(s8[:, 4, :], s8[:, 4, :], AF.Sqrt, bias=eps_ln.v(), scale=1.0)
        K.recip(s8[:, 5, :], s8[:, 4, :])
        K.stt(s8[:, 3, :], s8[:, 2, :], -1.0, s8[:, 5, :], ALU.mult, ALU.mult)

    def fout_b(c):
        r_ = c % NORF
        ot, bt, on, of, s8 = fot_r[r_], fbt_r[r_], fon_r[r_], fofb[r_], fst8[r_]
        for h in range(8):
            K.act(on[:, h * 64:(h + 1) * 64], ot[:, h * 64:(h + 1) * 64], AF.Identity,
                  bias=s8[:, 3, h:h + 1], scale=s8[:, 5, h:h + 1])
        K.tt("pool", bt.v(), bt.v(), lnxb.v(), ALU.add)
        K.tt("dve", on.v(), on.v(), lnxg.v(), ALU.mult)
        K.tt("dve", on.v(), on.v(), bt.v(), ALU.add)
        ps = PSF()
        K.mm(ps.v(), lora_bf[64:128, c * 128:(c + 1) * 128], lora_up[64:128, :])
        K.tt("dve", of.v(), on.v(), ps.v(), ALU.mult)
        pb = PSB()
        pv = pb.v().re("p (q t) -> p q t", q=8)
        for kt in range(4):
            K.tr(pv[:, kt, :], of[:, kt * 128:(kt + 1) * 128], identb.v())
        K.copy("act", oaT[:, :, c * 128:(c + 1) * 128], pv[:, 0:4, :])

    OUT = []
    for c in range(16 + 2):
        if c < 16:
            OUT.append(partial(fout_a, c))
        if 0 <= c - 2 < 16:
            OUT.append(partial(fout_b, c - 2))

    gs_tm = K.sb("gs_tm", [NS, RW], F32)
    ps = PSF()
    K.mm(ps[:NS, :], lora_bf[64:128, T:TT], lora_up[64:128, :])
    K.copy("act", gs_tm.v(), ps[:NS, :])
    K.dma("sp", dap(samp_d, 6 * 64, [[8 * 448, NS], [448, 8], [1, 64]]), gs_tm.v().re("p (h n) -> p h n", h=8))
    vec = K.sb("vec", [128, 7, 64], F32)
    K.dma("sp", vec.v().re("p q n -> p (q n)"), dap(samp_d, 0, [[448, 128], [1, 448]]))
    S0s = K.sb("S0s", [128, 64, 64], F32)
    S1s = K.sb("S1s", [128, 64, 64], F32)
    tS = K.sb("tS", [128, 64, 64], F32)
    K.dma("sp", S0s.v().re("p v k -> p (v k)"), st_wkv.v())
    r_, km_, v_, dec_, av_, b_, g_ = [vec[:, q, :] for q in range(7)]
    overv = lambda x: x.un(1).bc([128, 64, 64])
    overk = lambda x: x.un(2).bc([128, 64, 64])
    sv = K.sb("sv", [128, 8, 64], F32)
    ss = K.sb("ss", [128, 8], F32)
    eps2 = K.sb("eps_ln3", [128, 1], F32)
    K.memset("pool", eps2.v(), 64e-5)
    os_tm = K.sb("os_tm", [NS, RW], F32)
    os_bf = K.sb("os_bf", [NS, RW], BF16)
    SMP = [
        lambda: K.tt("dve", tS.v(), S0s.v(), overv(av_), ALU.mult),
        lambda: K.reduce(sv[:, 0, :], tS.v()),
        lambda: K.tt("pool", S1s.v(), S0s.v(), overv(dec_), ALU.mult),
        lambda: K.tt("dve", tS.v(), overk(sv[:, 0, :]), overv(b_), ALU.mult),
        lambda: K.tt("dve", S1s.v(), S1s.v(), tS.v(), ALU.add),
        lambda: K.tt("dve", tS.v(), overk(v_), overv(km_), ALU.mult),
        lambda: K.tt("dve", S1s.v(), S1s.v(), tS.v(), ALU.add),
        lambda: K.dma("sp", o_wkv_s.v(), S1s.v().re("p v k -> p (v k)")),
        lambda: K.tt("dve", tS.v(), S1s.v(), overv(r_), ALU.mult),
        lambda: K.reduce(sv[:, 1, :], tS.v()),
        lambda: K.reduce(ss[:, 0:1], sv[:, 1, :]),
        lambda: K.tt("dve", sv[:, 2, :], sv[:, 1, :], sv[:, 1, :], ALU.mult),
        lambda: K.reduce(ss[:, 1:2], sv[:, 2, :]),
        lambda: K.ts("dve", ss[:, 2:3], ss[:, 0:1], 1.0 / 64, None, ALU.mult),
        lambda: K.tt("dve", ss[:, 3:4], ss[:, 2:3], ss[:, 2:3], ALU.mult),
        lambda: K.stt(ss[:, 4:5], ss[:, 1:2], 1.0 / 64, ss[:, 3:4], ALU.mult, ALU.subtract),
        lambda: K.act(ss[:, 4:5], ss[:, 4:5], AF.Sqrt, bias=eps2.v(), scale=1.0),
        lambda: K.recip(ss[:, 5:6], ss[:, 4:5]),
        lambda: K.ts("dve", sv[:, 3, :], sv[:, 1, :], ss[:, 2:3], ss[:, 5:6], ALU.subtract, ALU.mult),
        lambda: K.tt("dve", sv[:, 3, :], sv[:, 3, :], par[:, 0, :], ALU.mult),
        lambda: K.tt("dve", sv[:, 3, :], sv[:, 3, :], par[:, 1, :], ALU.add),
        lambda: K.tt("dve", sv[:, 4, :], r_, km_, ALU.mult),
        lambda: K.tt("dve", sv[:, 4, :], sv[:, 4, :], par[:, 2, :], ALU.mult),
        lambda: K.reduce(ss[:, 6:7], sv[:, 4, :]),
        lambda: K.stt(sv[:, 3, :], v_, ss[:, 6:7], sv[:, 3, :], ALU.mult, ALU.add),
        lambda: K.tt("dve", sv[:, 5, :], sv[:, 3, :], g_, ALU.mult),
        lambda: K.dma("sp", dap(sampo_d, 0, [[64, 128], [1, 64]]), sv[:, 5, :]),
        lambda: K.dma("sp", os_tm.v(), sampo_d.v()),
        lambda: K.copy("dve", os_bf.v(), os_tm.v()),
    ]

    def smp_fin():
        pb = PSB()
        pv = pb.v().re("p (q t) -> p q t", q=8)
        for kt in range(4):
            K.tr(pv[:, kt, :NS], os_bf[:, kt * 128:(kt + 1) * 128], identb[:NS, :NS])
        K.copy("act", oaT[:, :, T:TT], pv[:, 0:4, :NS])

    SMP.append(smp_fin)
    run_merged_g([OUT, SMP])
    K.pop_scope()
    if stage == 4:
        zt = K.sb("zt", [128, 1024], F32)
        K.memset("pool", zt.v(), 0.0)
        for i in range(16):
            K.dma("sp", y_p[i * 128:(i + 1) * 128, :], zt[:, 0:D])
        K.dma("sp", y_s.v(), zt[0:NS, 0:D])
        return early()


    def pn_b(ps_halves, rows, res_src, gpost, dst, bufs):
        st, xres, tt_ = bufs[:3]
        K.dma("sp", xres[:rows, :], res_src)
        for hf in range(2):
            K.act(tt_[:rows, hf * 512:(hf + 1) * 512], ps_halves[hf][:rows, :], AF.Square, accum=st[:rows, hf:hf + 1])
        K.tt("dve", st[:rows, 2:3], st[:rows, 0:1], st[:rows, 1:2], ALU.add)
        K.act(st[:rows, 3:4], st[:rows, 2:3], AF.Sqrt, bias=epsT[:rows, :], scale=1.0 / D)
        K.recip(st[:rows, 4:5], st[:rows, 3:4])
        for hf in range(2):
            hs = slice(hf * 512, (hf + 1) * 512)
            K.stt(tt_[:rows, hs], ps_halves[hf][:rows, :], st[:rows, 4:5], gpost[:rows, hs], ALU.mult, ALU.mult)
        K.tt("dve", xres[:rows, :], xres[:rows, :], tt_[:rows, :], ALU.add)
        K.dma("sp", dst, xres[:rows, :])

    def pn_c(rows, bufs, nt):
        st, xres, tt_, xb = bufs
        gname, dstT, col0 = nt
        K.act(tt_[:rows, :], xres[:rows, :], AF.Square, accum=st[:rows, 5:6])
        K.act(st[:rows, 6:7], st[:rows, 5:6], AF.Sqrt, bias=epsT[:rows, :], scale=1.0 / D)
        K.recip(st[:rows, 7:8], st[:rows, 6:7])
        K.act(xb[:rows, :], xres[:rows, :], AF.Copy, scale=st[:rows, 7:8])
        pb = PSB()
        pv = pb.v().re("p (k t) -> p k t", k=8)
        for kt in range(8):
            K.tr(pv[:, kt, :rows], xb[:rows, kt * 128:(kt + 1) * 128], identb[:rows, :rows])
        g0 = PV[gname]
        K.tt("dve", dstT[:, :, col0:col0 + rows], pv[:, :, :rows],
             pvec[:, g0:g0 + 8].un(2).bc([128, 8, rows]), ALU.mult)

    mT = K.sb("mT", [128, 8, TT], BF16)
    mt = [[K.sb("mt", [128, 512], F32) for _ in range(4)] for _ in range(2)]
    mps = {}

    def mg_a1(f, tb, it):
        col0, n = TBLK[tb]
        cs = slice(col0, col0 + n)
        wA, wga, wgb, w1, w2 = mw[f]
        ps2, ps1 = PSF(), PSF()
        for kt in range(4):
            K.mm(ps2[:, :n], w2[:, kt, :], hgl[:, kt, cs], start=(kt == 0), stop=(kt == 3))
        for kt in range(4):
            K.mm(ps1[:, :n], w1[:, kt, :], hgl[:, kt, cs], start=(kt == 0), stop=(kt == 3))
        mps[(it, 1)] = (ps2, ps1)

    def mg_a2(f, tb, it):
        col0, n = TBLK[tb]
        cs = slice(col0, col0 + n)
        wA, wga, wgb, w1, w2 = mw[f]
        psgb = PSF()
        for kt in range(8):
            K.mm(psgb[:, :n], wgb[:, kt, :], hT[:, kt, cs], start=(kt == 0), stop=(kt == 7))
        mps[(it, 2)] = psgb

    def mg_a3(f, tb, it):
        col0, n = TBLK[tb]
        cs = slice(col0, col0 + n)
        wA, wga, wgb, w1, w2 = mw[f]
        psga, psA = PSF(), PSF()
        for kt in range(8):
            K.mm(psga[:, :n], wga[:, kt, :], hT[:, kt, cs], start=(kt == 0), stop=(kt == 7))
        for kt in range(4):
            K.mm(psA[:, :n], wA[:, kt, :], oaT[:, kt, cs], start=(kt == 0), stop=(kt == 3))
        mps[(it, 3)] = (psga, psA)

    def mg_b1(f, tb, it):
        col0, n = TBLK[tb]
        t0, t1_, t2_, t3_ = mt[it % 2]
        ps2, ps1 = mps.pop((it, 1))
        K.act(t0[:, :n], ps2[:, :n], AF.Sigmoid, bias=pcol("glu_b2", f))
        K.stt(t1_[:, :n], ps1[:, :n], pcol("glu_b1", f), t0[:, :n], ALU.add, ALU.mult)

    def mg_b2(f, tb, it):
        col0, n = TBLK[tb]
        t0, t1_, t2_, t3_ = mt[it % 2]
        psgb = mps.pop((it, 2))
        K.act(t2_[:, :n], psgb[:, :n], AF.Sigmoid)
        K.tt("dve", t1_[:, :n], t1_[:, :n], t2_[:, :n], ALU.mult)

    def mg_b3(f, tb, it):
        col0, n = TBLK[tb]
        cs = slice(col0, col0 + n)
        t0, t1_, t2_, t3_ = mt[it % 2]
        psga, psA = mps.pop((it, 3))
        K.act(t3_[:, :n], psga[:, :n], AF.Sigmoid)
        K.tt("dve", t3_[:, :n], psA[:, :n], t3_[:, :n], ALU.mult)
        K.tt("pool" if n > 16 else "dve", mT[:, f, cs], t3_[:, :n], t1_[:, :n], ALU.add)

    its = [(f, tb) for f in range(8) for tb in range(5)]
    for it, (f, tb) in enumerate(its):
        if tb == 0 and f + 1 < 8:
            mg_load(f + 1)
        mg_a1(f, tb, it)
        if it >= 1:
            pf, ptb = its[it - 1]
            mg_b3(pf, ptb, it - 1)
        mg_a2(f, tb, it)
        mg_b1(f, tb, it)
        mg_a3(f, tb, it)
        mg_b2(f, tb, it)
    mg_b3(its[-1][0], its[-1][1], len(its) - 1)
    pn_bufs = [(K.sb("pn_st", [128, 8], F32), K.sb("pn_x", [128, D], F32), K.sb("pn_t", [128, D], F32),
                K.sb("pn_xb", [128, D], BF16)) for _ in range(3)]
    mix_ps = {}

    def mix_a(i):
        rows = 128 if i < 16 else NS
        tcs = slice(i * 128, i * 128 + rows)
        halves = [PSF(), PSF()]
        for hf in range(2):
            for kt in range(8):
                K.mm(halves[hf][:rows, :], mT[:, kt, tcs], wmo[:, kt, hf * 512:(hf + 1) * 512],
                     start=(kt == 0), stop=(kt == 7))
        mix_ps[i] = halves

    def mix_b(i):
        rows = 128 if i < 16 else NS
        pn_b(mix_ps[i], rows, x_tile(i), gpost, x1_d[i * 128:i * 128 + rows, :], pn_bufs[i % 3])

    def mix_c(i):
        rows = 128 if i < 16 else NS
        pn_c(rows, pn_bufs[i % 3], ("g_pre_ffn", hT, i * 128))

    for step in range(17 + 2):
        if step < 17:
            mix_a(step)
        if 0 <= step - 1 < 17:
            mix_b(step - 1)
        if 0 <= step - 2 < 17:
            mix_c(step - 2)
    K.pop_scope()
    K.pop_scope()

    K.push_scope()
    NF = DFF // 128
    aT = K.sb("aT", [128, NF, TT], BF16)
    wd = K.sb("wd", [128, NF, D], BF16)
    gpost2 = bcast_row("gpost_ffn", g_post_ffn, D)
    fring = [K.sb("fring", [128, 8, 128], BF16) for _ in range(4)]
    fr_i = [0]

    def fload(src, c0):
        fr_i[0] += 1
        wb = fring[fr_i[0] % len(fring)]
        K.dma("pool", wb.v(), dap(src, c0, [[DFF, 128], [128 * DFF, 8], [1, 128]]))
        return wb

    sgt = [K.sb("sgt", [128, 512], F32) for _ in range(2)]
    it = 0
    for ft in range(NF):
        wg = fload(w_ffn_gate, ft * 128)
        wu = fload(w_ffn_up, ft * 128)
        if ft >= 1:
            K.dma("pool", wd[:, ft - 1, :], w_ffn_down[(ft - 1) * 128:ft * 128, :])
        if ft == NF - 1:
            K.dma("pool", wd[:, ft, :], w_ffn_down[ft * 128:(ft + 1) * 128, :])
        for tb in range(5):
            col0, n = TBLK[tb]
            cs = slice(col0, col0 + n)
            psg, psu = PSF(), PSF()
            for kt in range(8):
                K.mm(psg[:, :n], wg[:, kt, :], hT[:, kt, cs], start=(kt == 0), stop=(kt == 7))
            for kt in range(8):
                K.mm(psu[:, :n], wu[:, kt, :], hT[:, kt, cs], start=(kt == 0), stop=(kt == 7))
            sg = sgt[it % 2]
            it += 1
            K.act(sg[:, :n], psg[:, :n], AF.Silu)
            K.tt("dve", aT[:, ft, cs], psu[:, :n], sg[:, :n], ALU.mult)
    pn_bufs = [(K.sb("pn_st", [128, 8], F32), K.sb("pn_x", [128, D], F32), K.sb("pn_t", [128, D], F32))
               for _ in range(2)]
    dn_ps = {}

    def dn_a(i):
        rows = 128 if i < 16 else NS
        tcs = slice(i * 128, i * 128 + rows)
        halves = [PSF(), PSF()]
        for hf in range(2):
            for kt in range(NF):
                K.mm(halves[hf][:rows, :], aT[:, kt, tcs], wd[:, kt, hf * 512:(hf + 1) * 512],
                     start=(kt == 0), stop=(kt == NF - 1))
        dn_ps[i] = halves

    def dn_b(i):
        rows = 128 if i < 16 else NS
        dst = y_p[i * 128:(i + 1) * 128, :] if i < 16 else y_s[:, :]
        pn_b(dn_ps[i], rows, x1_d[i * 128:i * 128 + rows, :], gpost2, dst, pn_bufs[i % 2])

    for step in range(17 + 1):
        if step < 17:
            dn_a(step)
        if step - 1 >= 0:
            dn_b(step - 1)
    K.pop_scope()
    K.finish()
    es.close()
    return nc


_W_NAMES = ["norm_pre_mix", "norm_post_mix", "norm_pre_ffn", "norm_post_ffn", "w_in", "mu_shift", "w0",
            "w_decay_up", "a0", "w_aaa_up", "w_gate_up", "k_k", "k_a", "r_k", "lnx_g", "lnx_b", "w_rwkv_out",
            "s5_lam_re", "s5_lam_im", "s5_log_dt", "s5_b_re", "s5_b_im", "s5_c_re", "s5_c_im", "s5_d",
            "glu_w1", "glu_b1", "glu_w2", "glu_b2", "w_merge_out", "w_ffn_gate", "w_ffn_up", "w_ffn_down"]


def make_in_maps(inputs, cores):
    f = lambda a: np.ascontiguousarray(np.asarray(a, dtype=np.float32))
    shared = {n: f(inputs[n])[0] for n in _W_NAMES}
    maps = []
    for c in cores:
        m = dict(shared)
        m["x_p"] = f(inputs["x_prompt"][c])
        m["x_s"] = f(inputs["x_sample"][c * NS:(c + 1) * NS, 0])
        m["st_shift"] = f(inputs["state_shift"][0, c * NS:(c + 1) * NS])
        m["st_wkv"] = f(inputs["state_wkv"][0, c * NS:(c + 1) * NS]).reshape(NS * 8, 64 * 64)
        m["st_re"] = f(inputs["state_s5_re"][0, c * NS:(c + 1) * NS]).reshape(NS, 2048)
        m["st_im"] = f(inputs["state_s5_im"][0, c * NS:(c + 1) * NS]).reshape(NS, 2048)
        maps.append(m)
    return maps


def assemble(results):
    n = len(results)
    cat = lambda k: np.concatenate([np.asarray(r[k]) for r in results], axis=0)
    y_p = np.stack([np.asarray(r["y_p"]) for r in results], 0)
    y_s = cat("y_s").reshape(n * NS, 1, D)
    sh_p = cat("o_shift_p").reshape(1, n, SHIFT)
    wkv_p = np.stack([np.asarray(r["o_wkv_p"]) for r in results], 0).reshape(1, n, 8, 64, 64)
    re_p = cat("o_re_p").reshape(1, n, 32, 64)
    im_p = cat("o_im_p").reshape(1, n, 32, 64)
    sh_s = cat("o_shift_s").reshape(1, n * NS, SHIFT)
    wkv_s = cat("o_wkv_s").reshape(1, n * NS, 8, 64, 64)
    re_s = cat("o_re_s").reshape(1, n * NS, 32, 64)
    im_s = cat("o_im_s").reshape(1, n * NS, 32, 64)
    return tuple(np.ascontiguousarray(a, dtype=np.float32) for a in
                 (y_p, y_s, sh_p, wkv_p, re_p, im_p, sh_s, wkv_s, re_s, im_s))


def kernel(**inputs):
    nc = build()
    in_maps = make_in_maps(inputs, list(range(NCORES)))
    res = run_bass_kernel_spmd(nc, in_maps, core_ids=list(range(NCORES)))
    return assemble(res.results)
```

```python
import math
from contextlib import ExitStack
from functools import partial
import numpy as np
import concourse.bass as bass
import concourse.mybir as mybir
from concourse.bass_utils import run_bass_kernel_spmd

F32 = mybir.dt.float32
BF16 = mybir.dt.bfloat16
I32 = mybir.dt.int32
AF = mybir.ActivationFunctionType
ALU = mybir.AluOpType
AX = mybir.AxisListType

T = 2048
NS = 16
TT = T + NS
D = 1024
RW = 512
SHIFT = 1664
PROJ = 4224
DFF = 2816
NCORES = 8
CDEC = math.exp(-0.5)
TBLK = [(0, 512), (512, 512), (1024, 512), (1536, 512), (2048, 16)]
SAME_ENGINE_SYNC = True
LS = 4


class V:
    __slots__ = ("buf", "ap")

    def __init__(self, buf, ap):
        self.buf = buf
        self.ap = ap

    def __getitem__(self, idx):
        return V(self.buf, self.ap[idx])

    def bc(self, shape):
        return V(self.buf, self.ap.to_broadcast(list(shape)))

    def re(self, s, **kw):
        return V(self.buf, self.ap.rearrange(s, **kw))

    def un(self, axis):
        return V(self.buf, self.ap.unsqueeze(axis))

    def bitcast(self, dt):
        return V(self.buf, self.ap.bitcast(dt))


class Buf:
    def __init__(self, name, handle):
        self.name = name
        self.h = handle
        self.writes = {}
        self.reads = {}
        self.dsem = None
        self.dcnt = 0
        self.is_psum = False

    def __getitem__(self, idx):
        return V(self, self.h[idx])

    def v(self):
        return V(self, self.h[:])


class Kern:
    def __init__(self, nc, es):
        self.nc = nc
        self.es = es
        self.root_es = es
        self.eng = {"pe": nc.tensor, "act": nc.scalar, "dve": nc.vector, "pool": nc.gpsimd, "sp": nc.sync}
        self.sem = {}
        self.cnt = {}
        for e in ("pe", "act", "dve", "pool"):
            self.sem[e] = es.enter_context(nc.semaphore("s_" + e))
            self.cnt[e] = 0
        self.obs = {e: {} for e in self.eng}
        self.semname = {}
        self.dsems = []
        self.nbuf = 0
        self.psf_i = 0
        self.psb_i = 0

    def sb(self, name, shape, dt):
        self.nbuf += 1
        import os
        if os.environ.get("KDBG"):
            sz = int(np.prod(shape[1:])) * (2 if dt == BF16 else 4)
            self._tot = getattr(self, "_tot", 0) + sz
            print(f"alloc {name} {shape} {sz} depth={len(getattr(self, '_saved', []))} cum_noFree={self._tot}")
        h = self.es.enter_context(self.nc.sbuf_tensor(f"{name}_{self.nbuf}", list(shape), dt))
        return Buf(name, h)

    def barrier(self):
        for e in ("pe", "act", "dve", "pool", "sp"):
            for e2 in ("pe", "act", "dve", "pool"):
                if e2 != e and self.cnt[e2]:
                    self._wait(e, e2, self.sem[e2], self.cnt[e2])
            for ob in self.dsems:
                self._wait(e, "d:" + ob.name + str(id(ob)), ob.dsem, ob.dcnt)

    def push_scope(self):
        self._saved = getattr(self, "_saved", [])
        self._saved.append(self.es)
        self.es = ExitStack()

    def pop_scope(self):
        self.barrier()
        self.es.close()
        self.es = self._saved.pop()

    def ps(self, name, shape, dt):
        self.nbuf += 1
        h = self.es.enter_context(self.nc.psum_tensor(f"{name}_{self.nbuf}", list(shape), dt))
        b = Buf(name, h)
        b.is_psum = True
        return b

    def dram(self, name, shape, dt, kind="Internal"):
        h = self.nc.dram_tensor(name, list(shape), dt, kind=kind)
        return Buf(name, h.ap())

    def newsem(self, name):
        s = self.root_es.enter_context(self.nc.semaphore(name))
        return s

    def _wait(self, e, sem_key, sem, val):
        o = self.obs[e]
        if o.get(sem_key, 0) >= val:
            return
        self.eng[e].wait_ge(sem, val)
        o[sem_key] = val

    def _deps(self, e, W, R):
        need = {}
        for b in R:
            for k, (s, v) in b.writes.items():
                if need.get(k, (None, 0))[1] < v:
                    need[k] = (s, v)
            if b.is_psum:
                for k, (s, v) in b.reads.items():
                    if k != e and need.get(k, (None, 0))[1] < v:
                        need[k] = (s, v)
        for b in W:
            for k, (s, v) in b.writes.items():
                if need.get(k, (None, 0))[1] < v:
                    need[k] = (s, v)
            for k, (s, v) in b.reads.items():
                if need.get(k, (None, 0))[1] < v:
                    need[k] = (s, v)
        for k, (s, v) in need.items():
            if k == e:
                if e == "pe" or not SAME_ENGINE_SYNC:
                    continue
            self._wait(e, k, s, v)

    def _bufs(self, vs):
        out = []
        for x in vs:
            if isinstance(x, V):
                if x.buf not in out:
                    out.append(x.buf)
            elif isinstance(x, Buf):
                if x not in out:
                    out.append(x)
        return out

    def op(self, e, fn, W, R):
        Wb = self._bufs(W)
        Rb = self._bufs(R)
        self._deps(e, Wb, Rb)
        inst = fn()
        self.cnt[e] += 1
        c = self.cnt[e]
        inst.then_inc(self.sem[e], 1)
        for b in Wb:
            b.reads = {}
            b.writes[e] = (self.sem[e], c)
        for b in Rb:
            if b not in Wb:
                b.reads[e] = (self.sem[e], c)
        return inst

    def dma(self, q, out, in_, sem_buf=None, slow=False):
        Wb = self._bufs([out])
        Rb = self._bufs([in_])
        self._deps(q, Wb, Rb)
        ob = sem_buf if sem_buf is not None else out.buf
        if ob.dsem is None:
            ob.dsem = self.newsem("d_" + ob.name + str(len(self.dsems)))
            self.dsems.append(ob)
        if slow:
            with self.nc.allow_non_contiguous_dma(reason="small parameter layout load"):
                inst = self.eng[q].dma_start(out=out.ap, in_=in_.ap)
        else:
            inst = self.eng[q].dma_start(out=out.ap, in_=in_.ap)
        ob.dcnt += 16
        inst.then_inc(ob.dsem, 16)
        key = "d:" + ob.name + str(id(ob))
        for b in Wb:
            b.reads = {}
            b.writes[key] = (ob.dsem, ob.dcnt)
        for b in Rb:
            if b not in Wb:
                b.reads[key] = (ob.dsem, ob.dcnt)
        return inst

    def finish(self):
        for ob in self.dsems:
            self._wait("sp", "d:" + ob.name + str(id(ob)), ob.dsem, ob.dcnt)
        for e in ("pe", "act", "dve", "pool"):
            if self.cnt[e]:
                self._wait("sp", e, self.sem[e], self.cnt[e])

    @staticmethod
    def _a(x):
        return x.ap if isinstance(x, V) else x

    def act(self, out, in_, func, bias=None, scale=None, accum=None):
        kw = {}
        if bias is not None:
            kw["bias"] = self._a(bias)
        if scale is not None:
            kw["scale"] = self._a(scale)
        if accum is not None:
            kw["accum_out"] = self._a(accum)
        return self.op("act", lambda: self.nc.scalar.activation(out=out.ap, in_=in_.ap, func=func, **kw),
                       [out, accum], [in_, bias, scale])

    def tt(self, e, out, in0, in1, op):
        return self.op(e, lambda: self.eng[e].tensor_tensor(out=out.ap, in0=in0.ap, in1=in1.ap, op=op),
                       [out], [in0, in1])

    def ts(self, e, out, in0, s1, s2, op0, op1=None):
        if op1 is None:
            return self.op(e, lambda: self.eng[e].tensor_scalar(out=out.ap, in0=in0.ap, scalar1=self._a(s1),
                                                               scalar2=None, op0=op0), [out], [in0, s1])
        return self.op(e, lambda: self.eng[e].tensor_scalar(out=out.ap, in0=in0.ap, scalar1=self._a(s1),
                                                           scalar2=self._a(s2), op0=op0, op1=op1),
                       [out], [in0, s1, s2])

    def stt(self, out, in0, scalar, in1, op0, op1, e="dve"):
        return self.op(e, lambda: self.eng[e].scalar_tensor_tensor(out=out.ap, in0=in0.ap, scalar=self._a(scalar),
                                                                  in1=in1.ap, op0=op0, op1=op1),
                       [out], [in0, scalar, in1])

    def copy(self, e, out, in_):
        if e == "act":
            return self.act(out, in_, AF.Copy)
        return self.op(e, lambda: self.eng[e].tensor_copy(out=out.ap, in_=in_.ap), [out], [in_])

    def memset(self, e, out, val):
        return self.op(e, lambda: self.eng[e].memset(out.ap, val), [out], [])

    def recip(self, out, in_):
        return self.op("dve", lambda: self.nc.vector.reciprocal(out=out.ap, in_=in_.ap), [out], [in_])

    def reduce(self, out, in_, op=ALU.add, axis=AX.X):
        return self.op("dve", lambda: self.nc.vector.tensor_reduce(out=out.ap, in_=in_.ap, axis=axis, op=op),
                       [out], [in_])

    def scan(self, out, d0, d1, init, op0=ALU.mult, op1=ALU.add):
        return self.op("dve", lambda: self.nc.vector.tensor_tensor_scan(out=out.ap, data0=d0.ap, data1=d1.ap,
                                                                       initial=self._a(init), op0=op0, op1=op1),
                       [out], [d0, d1, init])

    def mm(self, out, lhsT, rhs, start=True, stop=True):
        return self.op("pe", lambda: self.nc.tensor.matmul(out.ap, lhsT=lhsT.ap, rhs=rhs.ap, start=start, stop=stop),
                       [out], [lhsT, rhs])

    def tr(self, out, in_, ident):
        return self.op("pe", lambda: self.nc.tensor.transpose(out.ap, in_.ap, ident.ap), [out], [in_, ident])

    def aselect(self, out, in_, pattern, cmp, fill, base, cm):
        return self.op("pool", lambda: self.nc.gpsimd.affine_select(out=out.ap, in_=in_.ap, pattern=pattern,
                                                                   compare_op=cmp, fill=fill, base=base,
                                                                   channel_multiplier=cm), [out], [in_])

    def iota(self, out, pattern, base, cm):
        return self.op("pool", lambda: self.nc.gpsimd.iota(out.ap, pattern=pattern, base=base,
                                                          channel_multiplier=cm), [out], [])


def run_merged_g(lists):
    idx = [0] * len(lists)
    while True:
        best, bf = -1, 2.0
        for li, L in enumerate(lists):
            if idx[li] < len(L):
                frac = idx[li] / len(L)
                if frac < bf:
                    best, bf = li, frac
        if best < 0:
            break
        lists[best][idx[best]]()
        idx[best] += 1


def merge_lists(lists):
    out = []
    idx = [0] * len(lists)
    while True:
        best, bf = -1, 2.0
        for li, L in enumerate(lists):
            if idx[li] < len(L):
                frac = idx[li] / len(L)
                if frac < bf:
                    best, bf = li, frac
        if best < 0:
            break
        out.append(lists[best][idx[best]])
        idx[best] += 1
    return out


def dap(buf, offset, ap):
    base = buf.h if not hasattr(buf.h, "ap") or isinstance(buf.h, bass.AP) else buf.h
    t = base.tensor if isinstance(base, bass.AP) else base
    return V(buf, bass.AP(t, offset, [list(x) for x in ap]))


def build(stage=99):
    nc = bass.Bass("TRN2", target_bir_lowering=False)
    es = ExitStack()
    K = Kern(nc, es)

    def din(name, shape, dt=F32):
        return Buf(name, nc.dram_tensor(name, list(shape), dt, kind="ExternalInput").ap())

    def dout(name, shape):
        return Buf(name, nc.dram_tensor(name, list(shape), F32, kind="ExternalOutput").ap())

    x_p = din("x_p", [T, D])
    x_s = din("x_s", [NS, D])
    st_shift = din("st_shift", [NS, SHIFT])
    st_wkv = din("st_wkv", [NS * 8, 64 * 64])
    st_re = din("st_re", [NS, 2048])
    st_im = din("st_im", [NS, 2048])
    g_pre_mix = din("norm_pre_mix", [D])
    g_post_mix = din("norm_post_mix", [D])
    g_pre_ffn = din("norm_pre_ffn", [D])
    g_post_ffn = din("norm_post_ffn", [D])
    w_in = din("w_in", [D, PROJ])
    mu_shift = din("mu_shift", [SHIFT])
    w0 = din("w0", [RW])
    w_decay_up = din("w_decay_up", [32, RW])
    a0 = din("a0", [RW])
    w_aaa_up = din("w_aaa_up", [32, RW])
    w_gate_up = din("w_gate_up", [64, RW])
    k_k = din("k_k", [RW])
    k_a = din("k_a", [RW])
    r_k = din("r_k", [RW])
    lnx_g = din("lnx_g", [RW])
    lnx_b = din("lnx_b", [RW])
    w_rwkv_out = din("w_rwkv_out", [RW, D])
    s5_lam_re = din("s5_lam_re", [32, 64])
    s5_lam_im = din("s5_lam_im", [32, 64])
    s5_log_dt = din("s5_log_dt", [32])
    s5_b_re = din("s5_b_re", [32, 64, 16])
    s5_b_im = din("s5_b_im", [32, 64, 16])
    s5_c_re = din("s5_c_re", [32, 16, 64])
    s5_c_im = din("s5_c_im", [32, 16, 64])
    s5_d = din("s5_d", [RW])
    glu_w1 = din("glu_w1", [RW, D])
    glu_b1 = din("glu_b1", [D])
    glu_w2 = din("glu_w2", [RW, D])
    glu_b2 = din("glu_b2", [D])
    w_merge_out = din("w_merge_out", [D, D])
    w_ffn_gate = din("w_ffn_gate", [D, DFF])
    w_ffn_up = din("w_ffn_up", [D, DFF])
    w_ffn_down = din("w_ffn_down", [DFF, D])

    y_p = dout("y_p", [T, D])
    y_s = dout("y_s", [NS, D])
    o_shift_p = dout("o_shift_p", [1, SHIFT])
    o_wkv_p = dout("o_wkv_p", [8, 64, 64])
    o_re_p = dout("o_re_p", [1, 2048])
    o_im_p = dout("o_im_p", [1, 2048])
    o_shift_s = dout("o_shift_s", [NS, SHIFT])
    o_wkv_s = dout("o_wkv_s", [NS * 8, 64 * 64])
    o_re_s = dout("o_re_s", [NS, 2048])
    o_im_s = dout("o_im_s", [NS, 2048])

    x1_d = K.dram("x1_scratch", [TT, D], F32)

    psf = [K.ps("psf", [128, 512], F32) for _ in range(8)]

    def PSF():
        K.psf_i += 1
        return psf[K.psf_i % len(psf)]

    class BV:
        def __init__(self, buf):
            self.buf = buf

        def v(self):
            return V(self.buf, self.buf.h[:].bitcast(BF16))

        def __getitem__(self, idx):
            return self.v()[idx]

    def PSB():
        return BV(PSF())

    ones_f = K.sb("ones_f", [128, 128], F32)
    identf = K.sb("identf", [128, 128], F32)
    identb = K.sb("identb", [128, 128], BF16)
    epsT = K.sb("epsT", [128, 1], F32)
    K.memset("pool", ones_f.v(), 1.0)
    K.memset("pool", epsT.v(), 1e-6)
    K.aselect(identf.v(), ones_f.v(), [[-1, 128]], ALU.is_equal, 0.0, 0, 1)
    K.copy("pool", identb.v(), identf.v())

    pvec = K.sb("pvec", [128, 128], F32)
    PV = {}
    _pv = [0]

    def load_fm(name, src, ntile):
        c0 = _pv[0]
        _pv[0] += ntile
        K.dma("sp", pvec[:, c0:c0 + ntile], dap(src, 0, [[1, 128], [128, ntile]]), sem_buf=pvec, slow=True)
        PV[name] = c0
        return c0

    load_fm("g_pre_mix", g_pre_mix, 8)
    load_fm("g_pre_ffn", g_pre_ffn, 8)
    load_fm("mu", mu_shift, 13)
    load_fm("w0", w0, 4)
    load_fm("a0", a0, 4)
    load_fm("k_k", k_k, 4)
    load_fm("k_a", k_a, 4)
    load_fm("r_k", r_k, 4)
    load_fm("s5_d", s5_d, 4)
    load_fm("glu_b1", glu_b1, 8)
    load_fm("glu_b2", glu_b2, 8)

    def early():
        while getattr(K, "_saved", []):
            K.pop_scope()
        K.finish()
        es.close()
        return nc

    if stage == 0.1:
        return early()

    def pcol(name, i=0):
        c = PV[name] + i
        return pvec[:, c:c + 1]

    hT = K.sb("hT", [128, 8, TT], BF16)
    def norm_transpose(src_tile_fn, gname, dst):
        K.push_scope()
        NB = 3
        xring = [K.sb("xring", [128, D], F32) for _ in range(NB)]
        xn = [K.sb("xn", [128, D], BF16) for _ in range(NB)]
        junk = K.sb("junk", [128, D], BF16)
        stat = [K.sb("stat", [128, 4], F32) for _ in range(NB)]

        def nt_a(i):
            rows = 128 if i < 16 else NS
            xt, st, xb = xring[i % NB], stat[i % NB], xn[i % NB]
            K.dma("sp", xt[:rows, :], src_tile_fn(i))
            K.act(junk[:rows, :], xt[:rows, :], AF.Square, accum=st[:rows, 0:1])
            K.act(st[:rows, 1:2], st[:rows, 0:1], AF.Sqrt, bias=epsT[:rows, :], scale=1.0 / D)
            K.recip(st[:rows, 2:3], st[:rows, 1:2])
            K.act(xb[:rows, :], xt[:rows, :], AF.Copy, scale=st[:rows, 2:3])

        def nt_b(i):
            rows = 128 if i < 16 else NS
            col0 = i * 128
            xb = xn[i % NB]
            pb = PSB()
            pv = pb.v().re("p (k t) -> p k t", k=8)
            for kt in range(8):
                K.tr(pv[:, kt, :rows], xb[:rows, kt * 128:(kt + 1) * 128], identb[:rows, :rows])
            g0 = PV[gname]
            K.tt("dve", dst[:, :, col0:col0 + rows], pv[:, :, :rows],
                 pvec[:, g0:g0 + 8].un(2).bc([128, 8, rows]), ALU.mult)

        for step in range(17 + 1):
            if step < 17:
                nt_a(step)
            if step >= 1:
                nt_b(step - 1)
        K.pop_scope()

    def x_tile(i):
        return x_p[i * 128:(i + 1) * 128, :] if i < 16 else x_s[:, :]

    norm_transpose(x_tile, "g_pre_mix", hT)
    if stage == 0.2:
        return early()

    wring = [K.sb("wring", [128, 8, 128], BF16) for _ in range(2)]
    wr_i = [0]

    def load_w_cols(src, c0, ncols_total, kt_n=8, width=128):
        wr_i[0] += 1
        wb = wring[wr_i[0] % len(wring)]
        K.dma("pool", wb[:, :kt_n, :width],
              dap(src, c0, [[ncols_total, 128], [128 * ncols_total, kt_n], [1, width]]))
        return wb

    def proj_fm(wb, kt_n, src_act, tb, ps):
        col0, n = TBLK[tb]
        for kt in range(kt_n):
            K.mm(ps[:, :n], wb[:, kt, :], src_act[:, kt, col0:col0 + n], start=(kt == 0), stop=(kt == kt_n - 1))

    hgl_d = K.dram("hgl_scratch", [128, 4 * TT], BF16)
    TWO_PI = 2.0 * math.pi

    K.push_scope()
    ubf = K.sb("ubf", [128, 4, TT], BF16)
    wu4 = [K.sb("wu4", [128, 8, 128], BF16) for _ in range(4)]
    for ft in range(4):
        K.dma("pool", wu4[ft].v(), dap(w_in, SHIFT + ft * 128, [[PROJ, 128], [128 * PROJ, 8], [1, 128]]))
        for tb in range(5):
            col0, n = TBLK[tb]
            ps = PSF()
            proj_fm(wu4[ft], 8, hT, tb, ps)
            K.copy("act" if tb % 2 else "dve", ubf[:, ft, col0:col0 + n], ps[:, :n])

    if stage == 1.1:
        return early()
    sp_ = K.sb("s5par", [128, 16, 24], F32)
    (P_LRE, P_LIM, P_DT, P_MAG, P_TH, P_FRE, P_FIM, P_LBRE, P_LBIM, P_T0, P_T1, P_T2, P_T3,
     P_TH2, P_MAG2, P_G0R, P_G0I, P_C1, P_S1) = range(19)

    def sp(k):
        return sp_[:, :, k]

    K.dma("sp", sp(P_LRE), dap(s5_lam_re, 0, [[1, 128], [128, 16]]), slow=True)
    K.dma("sp", sp(P_LIM), dap(s5_lam_im, 0, [[1, 128], [128, 16]]), slow=True)
    for gl in range(2):
        K.dma("sp", sp_[gl * 64:(gl + 1) * 64, :, P_DT], dap(s5_log_dt, gl, [[0, 64], [2, 16]]), slow=True)
    K.act(sp(P_DT), sp(P_DT), AF.Exp)
    K.tt("dve", sp(P_T0), sp(P_LRE), sp(P_DT), ALU.mult)
    K.act(sp(P_MAG), sp(P_T0), AF.Exp)
    K.copy("dve", sp(P_MAG2), sp(P_MAG))
    for _k in range(LS - 1):
        K.tt("dve", sp(P_MAG2), sp(P_MAG2), sp(P_MAG), ALU.mult)
    K.tt("dve", sp(P_TH), sp(P_LIM), sp(P_DT), ALU.mult)
    K.ts("dve", sp(P_TH2), sp(P_TH), float(LS), None, ALU.mult)

    NH = 512 // LS
    LC = min(128, NH)
    NRC = NH // LC
    pw = K.sb("pw", [128, 16, 2 * (LS + 1)], F32)
    cosT = K.sb("cosT", [128, 16, LC], F32)
    sinT = K.sb("sinT", [128, 16, LC], F32)
    BT = [K.sb("BT", [128, 32, 128], BF16) for _ in range(LS)]
    CTb = K.sb("CTb", [128, 32, 128], BF16)
    VTb = [K.sb("VTb", [128, 32, 128], BF16) for _ in range(LS - 1)]
    Kt = [K.sb("Kt", [128, 4, 128], BF16) for _ in range(LS - 1)]
    K.push_scope()
    ji = K.sb("ji", [128, LC], I32)
    jf = K.sb("jf", [128, LC], F32)
    ang = K.sb("ang", [128, 16, LC], F32)
    rr = K.sb("rr", [128, 16, LC], F32)
    kf = K.sb("kf", [128, 16, LC], F32)
    ki = K.sb("ki", [128, 16, LC], I32)
    K.iota(ji.v(), [[1, LC]], 1, 0)
    K.copy("dve", jf.v(), ji.v())
    K.tt("dve", ang.v(), sp(P_TH2).un(2).bc([128, 16, LC]), jf.v().un(1).bc([128, 16, LC]), ALU.mult)

    def sin_reduced(out, a_in, shift, rr_, kf_, ki_):
        C1 = 6.28125
        C2 = TWO_PI - C1
        K.ts("dve", rr_, a_in, shift, 1.0 / TWO_PI, ALU.add, ALU.mult)
        K.copy("dve", ki_, rr_)
        K.copy("dve", kf_, ki_)
        K.ts("dve", rr_, a_in, shift, None, ALU.add)
        K.stt(rr_, kf_, -C1, rr_, ALU.mult, ALU.add)
        K.stt(rr_, kf_, -C2, rr_, ALU.mult, ALU.add)
        K.ts("dve", kf_, rr_, math.pi, None, ALU.is_gt)
        K.stt(rr_, kf_, -TWO_PI, rr_, ALU.mult, ALU.add)
        K.ts("dve", kf_, rr_, -math.pi, None, ALU.is_lt)
        K.stt(rr_, kf_, TWO_PI, rr_, ALU.mult, ALU.add)
        K.ts("dve", rr_, rr_, -math.pi, math.pi, ALU.max, ALU.min)
        K.act(out, rr_, AF.Sin)

    sin_reduced(sinT.v(), ang.v(), 0.0, rr.v(), kf.v(), ki.v())
    sin_reduced(cosT.v(), ang.v(), 0.5 * math.pi, rr.v(), kf.v(), ki.v())
    sin_reduced(sp(P_S1), sp(P_TH), 0.0, rr[:, :, 0], kf[:, :, 0], ki[:, :, 0])
    sin_reduced(sp(P_C1), sp(P_TH), 0.5 * math.pi, rr[:, :, 0], kf[:, :, 0], ki[:, :, 0])
    K.tt("dve", sp(P_LBRE), sp(P_MAG), sp(P_C1), ALU.mult)
    K.tt("dve", sp(P_LBIM), sp(P_MAG), sp(P_S1), ALU.mult)
    K.tt("dve", sp(P_T0), sp(P_LRE), sp(P_LRE), ALU.mult)
    K.tt("dve", sp(P_T1), sp(P_LIM), sp(P_LIM), ALU.mult)
    K.tt("dve", sp(P_T0), sp(P_T0), sp(P_T1), ALU.add)
    K.recip(sp(P_T0), sp(P_T0))
    K.ts("dve", sp(P_T1), sp(P_LBRE), -1.0, None, ALU.add)
    K.tt("dve", sp(P_T2), sp(P_T1), sp(P_LRE), ALU.mult)
    K.tt("dve", sp(P_T3), sp(P_LBIM), sp(P_LIM), ALU.mult)
    K.tt("dve", sp(P_T2), sp(P_T2), sp(P_T3), ALU.add)
    K.tt("dve", sp(P_FRE), sp(P_T2), sp(P_T0), ALU.mult)
    K.tt("dve", sp(P_T2), sp(P_LBIM), sp(P_LRE), ALU.mult)
    K.tt("dve", sp(P_T3), sp(P_T1), sp(P_LIM), ALU.mult)
    K.tt("dve", sp(P_T2), sp(P_T2), sp(P_T3), ALU.subtract)
    K.tt("dve", sp(P_FIM), sp(P_T2), sp(P_T0), ALU.mult)
    K.memset("dve", pw[:, :, 0], 1.0)
    K.memset("dve", pw[:, :, 1], 0.0)
    for k in range(1, LS + 1):
        pr_, pi_ = pw[:, :, 2 * (k - 1)], pw[:, :, 2 * (k - 1) + 1]
        K.tt("dve", sp(P_T0), pr_, sp(P_LBRE), ALU.mult)
        K.tt("dve", sp(P_T1), pi_, sp(P_LBIM), ALU.mult)
        K.tt("dve", pw[:, :, 2 * k], sp(P_T0), sp(P_T1), ALU.subtract)
        K.tt("dve", sp(P_T0), pr_, sp(P_LBIM), ALU.mult)
        K.tt("dve", sp(P_T1), pi_, sp(P_LBRE), ALU.mult)
        K.tt("dve", pw[:, :, 2 * k + 1], sp(P_T0), sp(P_T1), ALU.add)
    K.pop_scope()
    if stage == 1.2:
        return early()

    K.push_scope()
    CTf = [K.sb("CTf", [128, 16, 128], F32) for _ in range(2)]
    K.push_scope()
    Zc = K.sb("Zc", [128, 4 * 512], F32)
    for part, src in enumerate((s5_c_re, s5_c_im)):
        K.memset("pool", Zc.v(), 0.0)
        for g8 in range(8):
            K.dma("sp", dap(Zc, 16 * g8 * 2048 + 64 * g8, [[2048, 16], [512, 4], [1, 64]]),
                  dap(src, g8 * 1024, [[64, 16], [8192, 4], [1, 64]]))
        for q in range(4):
            ps = PSF()
            for j in range(4):
                K.tr(ps[:, j * 128:(j + 1) * 128], Zc[:, q * 512 + j * 128:q * 512 + (j + 1) * 128], identf.v())
            dst = CTf[part][:, q * 4:q * 4 + 4, :]
            if part == 0:
                K.copy("dve", dst, ps.v().re("p (j c) -> p j c", j=4))
            else:
                K.ts("dve", dst, ps.v().re("p (j c) -> p j c", j=4), -1.0, None, ALU.mult)
    K.pop_scope()
    Xr = K.sb("Xr", [128, 16 * 128], F32)
    Xi = K.sb("Xi", [128, 16 * 128], F32)
    for X, src in ((Xr, s5_b_re), (Xi, s5_b_im)):
        K.memset("pool", X.v(), 0.0)
        for gl in range(2):
            for a in range(4):
                K.dma("sp", dap(X, gl * 64 * 2048 + 16 * gl + a * 512, [[2048, 64], [160, 4], [1, 16]]),
                      dap(src, gl * 1024 + a * 8192, [[16, 64], [2048, 4], [1, 16]]), slow=True)
    c4v = CTb.v().re("p (i two) c -> p i two c", two=2)
    K.copy("act", c4v[:, :, 0, :], CTf[0].v())
    K.copy("act", c4v[:, :, 1, :], CTf[1].v())
    T1 = K.sb("T1", [128, 16, 128], F32)
    Xs = [K.sb("Xs", [128, 16, 128], F32) for _ in range(2)]
    T2 = Xs[0]
    bcv = lambda vv: vv.un(2).bc([128, 16, 128])
    for jv in range(LS - 1):
        lr_, li_ = pw[:, :, 2 * (jv + 1)], pw[:, :, 2 * (jv + 1) + 1]
        v4v = VTb[jv].v().re("p (i two) c -> p i two c", two=2)
        K.tt("dve", T1.v(), CTf[0].v(), bcv(lr_), ALU.mult)
        K.tt("dve", T2.v(), CTf[1].v(), bcv(li_), ALU.mult)
        K.tt("dve", v4v[:, :, 0, :], T1.v(), T2.v(), ALU.add)
        K.tt("dve", T1.v(), CTf[1].v(), bcv(lr_), ALU.mult)
        K.tt("dve", T2.v(), CTf[0].v(), bcv(li_), ALU.mult)
        K.tt("dve", v4v[:, :, 1, :], T1.v(), T2.v(), ALU.subtract)
    gsc = K.sb("gsc", [128, 16, 4], F32)
    x3 = lambda b_: b_.v().re("p (i c) -> p i c", i=16)
    for v in range(LS):
        kpow = LS - 1 - v
        lr_, li_ = pw[:, :, 2 * kpow], pw[:, :, 2 * kpow + 1]
        K.tt("dve", gsc[:, :, 2], lr_, sp(P_FRE), ALU.mult)
        K.tt("dve", gsc[:, :, 3], li_, sp(P_FIM), ALU.mult)
        K.tt("dve", gsc[:, :, 0], gsc[:, :, 2], gsc[:, :, 3], ALU.subtract)
        K.tt("dve", gsc[:, :, 2], lr_, sp(P_FIM), ALU.mult)
        K.tt("dve", gsc[:, :, 3], li_, sp(P_FRE), ALU.mult)
        K.tt("dve", gsc[:, :, 1], gsc[:, :, 2], gsc[:, :, 3], ALU.add)
        gr, gi = bcv(gsc[:, :, 0]), bcv(gsc[:, :, 1])
        K.tt("dve", Xs[0].v(), x3(Xr), gr, ALU.mult)
        K.tt("dve", T1.v(), x3(Xi), gi, ALU.mult)
        K.tt("dve", Xs[0].v(), Xs[0].v(), T1.v(), ALU.subtract)
        K.tt("dve", Xs[1].v(), x3(Xr), gi, ALU.mult)
        K.tt("dve", T1.v(), x3(Xi), gr, ALU.mult)
        K.tt("dve", Xs[1].v(), Xs[1].v(), T1.v(), ALU.add)
        for part in range(2):
            for i4 in range(4):
                ps = PSF()
                for j in range(4):
                    K.tr(ps[:, j * 128:(j + 1) * 128], Xs[part][:, i4 * 4 + j, :], identf.v())
                dstb = BT[v].v().re("p (i two) c -> p i two c", two=2)[:, i4 * 4:i4 * 4 + 4, part, :]
                K.copy("act" if i4 % 2 else "dve", dstb, ps.v().re("p (j c) -> p j c", j=4))
        if kpow <= LS - 2:
            for q in range(4):
                ps = PSF()
                for j in range(4):
                    i = q * 4 + j
                    K.mm(ps[:, 0:128], Xs[0][:, i, :], CTf[0][:, i, :], start=(j == 0), stop=False)
                    K.mm(ps[:, 0:128], Xs[1][:, i, :], CTf[1][:, i, :], start=False, stop=(j == 3))
                K.copy("dve", Kt[kpow][:, q, :], ps[:, 0:128])
    K.pop_scope()
    if stage == 1.3:
        return early()

    x1b = [K.sb("x1sb", [128, 16, NS], BF16) for _ in range(2)]
    K.push_scope()
    s0_tm = [K.sb("s0tm", [NS, 2048], F32) for _ in range(2)]
    s0 = [K.sb("s0", [128, 16, NS], F32) for _ in range(2)]
    K.dma("sp", s0_tm[0].v(), st_re.v())
    K.dma("sp", s0_tm[1].v(), st_im.v())
    for part in range(2):
        ps = PSF()
        for i in range(16):
            K.tr(ps[:, i * NS:(i + 1) * NS], s0_tm[part][:, i * 128:(i + 1) * 128], identf[:NS, :NS])
        K.copy("dve", s0[part].v(), ps[:, :16 * NS].re("p (i n) -> p i n", i=16))
    psr = PSF()
    psi = PSF()
    for i in range(16):
        K.mm(psr[:, i * NS:(i + 1) * NS], BT[LS - 1][:, 2 * i, :], ubf[:, i // 4, T:TT])
        K.mm(psi[:, i * NS:(i + 1) * NS], BT[LS - 1][:, 2 * i + 1, :], ubf[:, i // 4, T:TT])
    sw = [K.sb("sw", [128, 16, NS], F32) for _ in range(2)]
    x1 = [K.sb("x1s", [128, 16, NS], F32) for _ in range(2)]
    bc16 = lambda k: sp(k).un(2).bc([128, 16, NS])
    pr3 = psr[:, :16 * NS].re("p (i n) -> p i n", i=16)
    pi3 = psi[:, :16 * NS].re("p (i n) -> p i n", i=16)
    K.tt("dve", sw[0].v(), s0[0].v(), bc16(P_LBRE), ALU.mult)
    K.tt("dve", sw[1].v(), s0[1].v(), bc16(P_LBIM), ALU.mult)
    K.tt("dve", sw[0].v(), sw[0].v(), sw[1].v(), ALU.subtract)
    K.tt("dve", x1[0].v(), sw[0].v(), pr3, ALU.add)
    K.tt("dve", sw[0].v(), s0[1].v(), bc16(P_LBRE), ALU.mult)
    K.tt("dve", sw[1].v(), s0[0].v(), bc16(P_LBIM), ALU.mult)
    K.tt("dve", sw[0].v(), sw[0].v(), sw[1].v(), ALU.add)
    K.tt("dve", x1[1].v(), sw[0].v(), pi3, ALU.add)
    for part in range(2):
        K.copy("act", x1b[part].v(), x1[part].v())
        xs_tm = s0_tm[part]
        for i4 in range(4):
            ps = PSF()
            for j in range(4):
                i = i4 * 4 + j
                K.tr(ps[:NS, j * 128:(j + 1) * 128], x1[part][:, i, :], identf.v())
            K.copy("dve", xs_tm[:, i4 * 512:(i4 + 1) * 512], ps[:NS, :])
        K.dma("sp", (o_re_s if part == 0 else o_im_s).v(), xs_tm.v())
    K.pop_scope()
    if stage == 1.4:
        return early()

    hgl = K.sb("hgl", [128, 4, TT], BF16)
    carry = K.sb("carry", [128, 16, 2], F32)
    K.memset("dve", carry.v(), 0.0)
    UN = [dict(bR=K.sb("bR", [128, NH], F32), bI=K.sb("bI", [128, NH], F32),
               wR=K.sb("wR", [128, NH], F32), wI=K.sb("wI", [128, NH], F32),
               t1=K.sb("t1", [128, NH], F32), t2=K.sb("t2", [128, NH], F32)) for _ in range(4)]
    xbr = [K.sb("xbr", [128, NH + 1], BF16) for _ in range(4)]
    xbi = [K.sb("xbi", [128, NH + 1], BF16) for _ in range(4)]
    yv = [K.sb("yv", [128, 512], F32) for _ in range(2)]
    y2 = [K.sb("y2", [128, 512], F32) for _ in range(2)]

    def gelu_to(dst_bf, yv_, y2_):
        K.act(y2_, yv_, AF.Square)
        K.act(y2_, y2_, AF.Identity, bias=1.0, scale=0.044715)
        K.tt("pool", y2_, y2_, yv_, ALU.mult)
        K.act(y2_, y2_, AF.Sigmoid, scale=1.5957691216057308)
        K.tt("pool", dst_bf, y2_, yv_, ALU.mult)

    vr = lambda b: b.v().re("p (a l) -> p a l", a=NRC)
    for q in range(4):
        for tb in range(4):
            col0, n = TBLK[tb]
            ugrp = ubf[:, q, col0:col0 + n].re("p (c j) -> p j c", j=LS)
            for j in range(4):
                i = q * 4 + j
                U = UN[j]
                psr = PSF()
                psi = PSF()
                for v in range(LS):
                    K.mm(psr[:, :NH], BT[v][:, 2 * i, :], ugrp[:, v, :], start=(v == 0), stop=(v == LS - 1))
                for v in range(LS):
                    K.mm(psi[:, :NH], BT[v][:, 2 * i + 1, :], ugrp[:, v, :], start=(v == 0), stop=(v == LS - 1))
                tC = cosT[:, i, :].un(1).bc([128, NRC, LC])
                tS = sinT[:, i, :].un(1).bc([128, NRC, LC])
                pr2 = psr[:, :NH].re("p (a l) -> p a l", a=NRC)
                pi2 = psi[:, :NH].re("p (a l) -> p a l", a=NRC)
                K.tt("dve", vr(U["t1"]), pr2, tC, ALU.mult)
                K.tt("dve", vr(U["t2"]), pi2, tS, ALU.mult)
                K.tt("pool", U["bR"].v(), U["t1"].v(), U["t2"].v(), ALU.add)
                K.tt("dve", vr(U["wR"]), pi2, tC, ALU.mult)
                K.tt("dve", vr(U["wI"]), pr2, tS, ALU.mult)
                K.tt("dve", U["bI"].v(), U["wR"].v(), U["wI"].v(), ALU.subtract)
                K.copy("act", xbr[j][:, 0:1], carry[:, i, 0:1])
                K.copy("act", xbi[j][:, 0:1], carry[:, i, 1:2])
            for a in range(NRC):
                cs = slice(a * LC, (a + 1) * LC)
                for j in range(4):
                    i = q * 4 + j
                    U = UN[j]
                    rho = sp_[:, i, P_MAG2:P_MAG2 + 1].bc([128, LC])
                    if a == 0:
                        ire, iim = carry[:, i, 0:1], carry[:, i, 1:2]
                    else:
                        ire, iim = U["bR"][:, a * LC - 1:a * LC], U["bI"][:, a * LC - 1:a * LC]
                    K.scan(U["wR"][:, cs], rho, U["bR"][:, cs], ire)
                    K.scan(U["wI"][:, cs], rho, U["bI"][:, cs], iim)
                for j in range(4):
                    i = q * 4 + j
                    U = UN[j]
                    K.tt("pool", U["t1"][:, cs], U["wR"][:, cs], cosT[:, i, :], ALU.mult)
                    K.tt("dve", U["t2"][:, cs], U["wI"][:, cs], sinT[:, i, :], ALU.mult)
                    K.tt("dve", U["bR"][:, cs], U["t1"][:, cs], U["t2"][:, cs], ALU.subtract)
                    K.tt("pool", U["t1"][:, cs], U["wR"][:, cs], sinT[:, i, :], ALU.mult)
                    K.tt("dve", U["t2"][:, cs], U["wI"][:, cs], cosT[:, i, :], ALU.mult)
                    K.tt("dve", U["bI"][:, cs], U["t1"][:, cs], U["t2"][:, cs], ALU.add)
            for j in range(4):
                i = q * 4 + j
                U = UN[j]
                K.copy("dve", carry[:, i, 0:1], U["bR"][:, NH - 1:NH])
                K.copy("dve", carry[:, i, 1:2], U["bI"][:, NH - 1:NH])
                K.copy("act", xbr[j][:, 1:NH + 1], U["bR"].v())
                K.copy("act", xbi[j][:, 1:NH + 1], U["bI"].v())
            yy = yv[tb % 2]
            yyp = yy.v().re("p (j c) -> p j c", j=LS)
            for jo in range(LS):
                psy = PSF()
                if jo == LS - 1:
                    for j in range(4):
                        i = q * 4 + j
                        K.mm(psy[:, :NH], CTb[:, 2 * i, :], xbr[j][:, 1:NH + 1], start=(j == 0), stop=False)
                        K.mm(psy[:, :NH], CTb[:, 2 * i + 1, :], xbi[j][:, 1:NH + 1], start=False, stop=(j == 3))
                else:
                    for j in range(4):
                        i = q * 4 + j
                        K.mm(psy[:, :NH], VTb[jo][:, 2 * i, :], xbr[j][:, 0:NH], start=(j == 0), stop=False)
                        K.mm(psy[:, :NH], VTb[jo][:, 2 * i + 1, :], xbi[j][:, 0:NH], start=False, stop=False)
                    for ii in range(jo + 1):
                        K.mm(psy[:, :NH], Kt[jo - ii][:, q, :], ugrp[:, ii, :], start=False, stop=(ii == jo))
                K.copy("act", yyp[:, jo, :], psy[:, :NH])
            psu = PSF()
            proj_fm(wu4[q], 8, hT, tb, psu)
            K.stt(yyp, psu.v().re("p (c j) -> p j c", j=LS), pcol("s5_d", q), yyp, ALU.mult, ALU.add)
            y2p = y2[tb % 2].v().re("p (j c) -> p j c", j=LS)
            gelu_to(hgl[:, q, col0:col0 + n].re("p (c j) -> p j c", j=LS), yyp, y2p)
        psy = PSF()
        for j in range(4):
            i = q * 4 + j
            K.mm(psy[:, :NS], CTb[:, 2 * i, :], x1b[0][:, i, :], start=(j == 0), stop=False)
            K.mm(psy[:, :NS], CTb[:, 2 * i + 1, :], x1b[1][:, i, :], start=False, stop=(j == 3))
        K.copy("act", yv[0][:, :NS], psy[:, :NS])
        psu = PSF()
        proj_fm(wu4[q], 8, hT, 4, psu)
        K.stt(yv[0][:, :NS], psu[:, :NS], pcol("s5_d", q), yv[0][:, :NS], ALU.mult, ALU.add)
        gelu_to(hgl[:, q, T:TT], yv[0][:, :NS], y2[0][:, :NS])
    if stage == 1.5:
        return early()
    K.dma("sp", dap(o_re_p, 0, [[1, 128], [128, 16]]), carry[:, :, 0], slow=True)
    K.dma("sp", dap(o_im_p, 0, [[1, 128], [128, 16]]), carry[:, :, 1], slow=True)
    K.dma("sp", hgl_d.v(), hgl.v().re("p k t -> p (k t)"))
    K.pop_scope()
    if stage == 2:
        zt = K.sb("zt", [128, 4096], F32)
        K.memset("pool", zt.v(), 0.0)
        K.dma("sp", o_shift_p.v(), zt[0:1, 0:SHIFT])
        K.dma("sp", o_shift_s.v(), zt[0:NS, 0:SHIFT])
        K.dma("sp", o_wkv_p.v().re("h v k -> h (v k)"), zt[0:8, :])
        K.dma("sp", o_wkv_s.v(), zt[:, :])
        for i in range(16):
            K.dma("sp", y_p[i * 128:(i + 1) * 128, :], zt[:, 0:D])
        K.dma("sp", y_s.v(), zt[0:NS, 0:D])
        return early()
    K.push_scope()
    lnxg = K.sb("lnxg", [128, RW], F32)
    lnxb = K.sb("lnxb", [128, RW], F32)
    K.dma("sp", lnxg.v(), dap(lnx_g, 0, [[0, 128], [1, RW]]))
    K.dma("sp", lnxb.v(), dap(lnx_b, 0, [[0, 128], [1, RW]]))
    par = K.sb("spar", [128, 3, 64], F32)
    for tk in range(NS):
        for j, src in enumerate((lnx_g, lnx_b, r_k)):
            K.dma("sp", par[tk * 8:(tk + 1) * 8, j, :], dap(src, 0, [[64, 8], [1, 64]]), sem_buf=par)

    lora_bf = K.sb("lora_bf", [128, TT], BF16)
    lora_up = K.sb("lora_up", [128, RW], BF16)
    oaT = K.sb("oaT", [128, 4, TT], BF16)
    o_d = K.dram("o_scratch", [T, RW], F32)
    bonus_d = K.dram("bonus_scratch", [T, RW], F32)
    samp_d = K.dram("samp_scratch", [NS * 8, 7, 64], F32)
    sampo_d = K.dram("sampo_scratch", [NS, RW], F32)

    K.push_scope()
    mask4 = K.sb("mask4", [128, 4, 128], F32)
    maskNT = K.sb("maskNT", [128, 128], F32)
    resetm = K.sb("resetm", [128, T], BF16)
    blk1 = K.sb("blk1", [128, 128], F32)
    eps_ln = K.sb("eps_ln", [128, 1], F32)
    K.memset("pool", eps_ln.v(), 64e-5)
    for j in range(4):
        K.aselect(mask4[:, j, :], ones_f.v(), [[1, 128]], ALU.is_gt if j % 2 == 0 else ALU.is_ge, 0.0, 0, -1)
    K.aselect(maskNT.v(), ones_f.v(), [[-1, 128]], ALU.is_gt, 0.0, 0, 1)
    K.memset("pool", resetm.v(), 1.0)
    K.memset("pool", resetm.v().re("p (c l) -> p c l", l=128)[:, :, 0:1], 0.0)
    K.memset("pool", blk1.v(), 0.0)
    K.memset("pool", blk1[0:64, 0:64], 1.0)
    K.memset("pool", blk1[64:128, 64:128], 1.0)

    K.dma("pool", lora_up[0:32, :], w_decay_up.v())
    K.dma("pool", lora_up[32:64, :], w_aaa_up.v())
    K.dma("pool", lora_up[64:128, :], w_gate_up.v())
    sh0T = K.sb("sh0T", [128, 13, NS], F32)
    shout = [K.sb("shout", [NS + 1, 128], F32) for _ in range(1)]
    K.push_scope()
    sh_tm = K.sb("sh_tm", [NS, SHIFT], F32)
    K.dma("sp", sh_tm.v(), st_shift.v())
    ps = PSF()
    for ft in range(13):
        K.tr(ps[:, ft * NS:(ft + 1) * NS], sh_tm[:, ft * 128:(ft + 1) * 128], identf[:NS, :NS])
    K.copy("dve", sh0T.v(), ps[:, :13 * NS].re("p (f n) -> p f n", f=13))
    K.pop_scope()

    omm = K.sb("omm", [128, 13], F32)
    K.ts("dve", omm.v(), pvec[:, PV["mu"]:PV["mu"] + 13], -1.0, 1.0, ALU.mult, ALU.add)
    omka = K.sb("omka", [128, 4], F32)
    K.ts("dve", omka.v(), pvec[:, PV["k_a"]:PV["k_a"] + 4], -1.0, 1.0, ALU.mult, ALU.add)

    SC = [K.sb("scr", [128, TT], F32) for _ in range(6)]

    def proj_shift_tile(ft, pr, tmp, xr_out, xr_out_dt_bf=None, samp32=None):
        wb = load_w_cols(w_in, ft * 128, PROJ)
        for tb in range(5):
            col0, n = TBLK[tb]
            ps = PSF()
            proj_fm(wb, 8, hT, tb, ps)
            K.copy("act", pr[:, col0:col0 + n], ps[:, :n])
            K.act(tmp[:, col0:col0 + n], ps[:, :n], AF.Copy, scale=omm[:, ft:ft + 1])
        ps = PSF()
        K.tr(ps[:NS + 1, :128], pr[:, T - 1:TT], identf.v())
        so = shout[0]
        K.copy("dve", so.v(), ps[:NS + 1, :128])
        K.dma("sp", o_shift_p[0:1, ft * 128:(ft + 1) * 128], so[0:1, :])
        K.dma("sp", o_shift_s[:, ft * 128:(ft + 1) * 128], so[1:NS + 1, :])
        mu_c = pcol("mu", ft)
        K.stt(xr_out[:, 1:T], pr[:, 0:T - 1], mu_c, tmp[:, 1:T], ALU.mult, ALU.add)
        K.copy("pool", xr_out[:, 0:1], tmp[:, 0:1])
        K.stt(xr_out[:, T:TT], sh0T[:, ft, :], mu_c, tmp[:, T:TT], ALU.mult, ALU.add)
        if samp32 is not None:
            K.stt(samp32, sh0T[:, ft, :], mu_c, tmp[:, T:TT], ALU.mult, ALU.add)

    proj_shift_tile(12, SC[0], SC[1], SC[2].v())
    K.act(lora_bf[0:32, :], SC[2][0:32, :], AF.Tanh)
    K.act(lora_bf[32:64, :], SC[2][32:64, :], AF.Copy)
    K.act(lora_bf[64:128, :], SC[2][64:128, :], AF.Sigmoid)

    vbf = K.sb("vbf", [128, TT], BF16)
    vs32 = K.sb("vs32", [128, NS], F32)
    AR = K.sb("AR", [128, 16, 2, 128], BF16)
    bti = K.sb("bti", [128, T], BF16)
    kti = K.sb("kti", [128, T], BF16)
    bhat = K.sb("bhat", [128, T], BF16)
    khat = K.sb("khat", [128, T], BF16)
    Wfm = bhat
    TMq = K.sb("TMq", [128, 16, 4, 128], BF16)
    gL = K.sb("gL", [128, 16], F32)
    Sf = K.sb("Sf", [128, 64], F32)
    Sb = K.sb("Sb", [128, 64], BF16)
    smp = K.sb("smp", [NS, 3, 128], F32)
    decs = K.sb("decs", [128, NS], F32)
    avs = K.sb("avs", [128, NS], F32)
    NPR = 16
    ABK = [K.sb("ABK", [128, 2, 128], BF16) for _ in range(NPR)]
    NPQ = 8
    ATMP = [K.sb("ATMP", [128, 2, 128], BF16) for _ in range(NPQ)]
    P0T = [K.sb("P0T", [128, 128], BF16) for _ in range(NPQ)]
    PMb = [[K.sb("PMb", [128, 3, 128], BF16) for _ in range(2)] for _ in range(NPQ)]
    Ybf = [K.sb("Ybf", [128, 64], BF16) for _ in range(NPQ)]
    QTb = [[K.sb("QTb", [128, 128], BF16) for _ in range(2)] for _ in range(NPQ)]
    Ubar = [K.sb("Ubar", [128, 64], F32) for _ in range(NPR)]
    Ut = [K.sb("Ut", [128, 64], BF16) for _ in range(4)]
    Otile = [K.sb("Otile", [128, 128], F32) for _ in range(2)]
    btile = [K.sb("btile", [128, 512], F32) for _ in range(1)]
    wkvo = K.sb("wkvo", [64, 128], F32)

    def derivedA(hp):
        xr_r, xr_k = SC[0], SC[1]
        proj_shift_tile(0 * 4 + hp, SC[2], SC[3], xr_r.v())
        yield
        proj_shift_tile(1 * 4 + hp, SC[2], SC[3], xr_k.v())
        yield
        proj_shift_tile(2 * 4 + hp, SC[2], SC[3], vbf.v(), samp32=vs32.v())
        yield
        sigd, csum, afm, kkn = SC[2], SC[3], SC[4], SC[5]
        hc = slice(hp * 128, (hp + 1) * 128)
        for tb in range(5):
            col0, n = TBLK[tb]
            ps = PSF()
            K.mm(ps[:, :n], lora_up[0:32, hc], lora_bf[0:32, col0:col0 + n])
            K.act(sigd[:, col0:col0 + n], ps[:, :n], AF.Sigmoid, bias=pcol("w0", hp))
            yield
            ps = PSF()
            K.mm(ps[:, :n], lora_up[32:64, hc], lora_bf[32:64, col0:col0 + n])
            K.act(afm[:, col0:col0 + n], ps[:, :n], AF.Sigmoid, bias=pcol("a0", hp))
            yield
        K.act(kkn.v(), xr_k.v(), AF.Square, scale=pcol("k_k", hp))
        yield
        for tb in range(5):
            col0, n = TBLK[tb]
            ps = PSF()
            K.mm(ps[:, :n], blk1.v(), kkn[:, col0:col0 + n])
            K.ts("dve", csum[:, col0:col0 + n], ps[:, :n], 1e-24, None, ALU.max)
            yield
        K.act(csum.v(), csum.v(), AF.Ln)
        yield
        K.act(csum.v(), csum.v(), AF.Exp, scale=-0.5)
        yield
        K.stt(kkn.v(), xr_k.v(), pcol("k_k", hp), csum.v(), ALU.mult, ALU.mult)
        yield
        K.ts("pool", csum.v(), afm.v(), pcol("k_a", hp), omka[:, hp:hp + 1], ALU.mult, ALU.add)
        yield
        K.tt("pool", xr_k.v(), xr_k.v(), csum.v(), ALU.mult)
        yield
        kmod = xr_k
        K.tt("pool", afm.v(), kkn.v(), afm.v(), ALU.mult)
        yield
        bfm = afm
        K.stt(csum.v(), xr_r.v(), pcol("r_k", hp), kmod.v(), ALU.mult, ALU.mult)
        yield
        for tb in range(4):
            col0, n = TBLK[tb]
            ps = PSF()
            K.mm(ps[:, :n], blk1.v(), csum[:, col0:col0 + n])
            K.tt("dve", csum[:, col0:col0 + n], ps[:, :n], vbf[:, col0:col0 + n], ALU.mult)
            yield
        for c4 in range(4):
            ps = PSF()
            for j in range(4):
                c = c4 * 4 + j
                K.tr(ps[:, j * 128:(j + 1) * 128], csum[:, c * 128:(c + 1) * 128], identf.v())
            bt = btile[0]
            K.copy("act", bt.v(), ps.v())
            yield
            K.dma("sp", dap(bonus_d, c4 * 512 * RW + hp * 128, [[RW, 128], [128 * RW, 4], [1, 128]]),
                  bt.v().re("p (j f) -> p j f", j=4))
            yield
        K.act(decs.v(), sigd[:, T:TT], AF.Exp, scale=-CDEC)
        yield
        K.ts("dve", avs.v(), kkn[:, T:TT], -1.0, None, ALU.mult)
        yield
        yield

    def derivedB(hp):
        xr_r, xr_k = SC[0], SC[1]
        sigd, csum, afm, kkn = SC[2], SC[3], SC[4], SC[5]
        kmod, bfm = xr_k, afm
        K.scan(csum[:, 0:T], resetm.v(), sigd[:, 0:T], 0.0)
        gt = sigd
        c3 = lambda b_: b_[:, 0:T].re("p (c l) -> p c l", l=128)
        K.act(gt[:, 0:T], csum[:, 0:T], AF.Exp, scale=-CDEC)
        K.tt("dve", AR[:, :, 1, :], c3(xr_r), c3(gt), ALU.mult)
        K.copy("dve", gL.v(), c3(gt)[:, :, 127])
        K.stt(AR[:, :, 0, 1:128], c3(kkn)[:, :, 1:128], -1.0, c3(gt)[:, :, 0:127], ALU.mult, ALU.mult)
        K.ts("dve", AR[:, :, 0, 0], c3(kkn)[:, :, 0], -1.0, None, ALU.mult)
        K.act(gt[:, 0:T], csum[:, 0:T], AF.Exp, scale=CDEC)
        K.tt("dve", bti.v(), bfm[:, 0:T], gt[:, 0:T], ALU.mult)
        K.tt("dve", kti.v(), kmod[:, 0:T], gt[:, 0:T], ALU.mult)
        K.tt("dve", c3(gt), c3(csum)[:, :, 127:128].bc([128, 16, 128]), c3(csum), ALU.subtract)
        K.act(gt[:, 0:T], gt[:, 0:T], AF.Exp, scale=-CDEC)
        K.tt("dve", bhat.v(), bfm[:, 0:T], gt[:, 0:T], ALU.mult)
        K.tt("dve", khat.v(), kmod[:, 0:T], gt[:, 0:T], ALU.mult)
        for c in range(16):
            pb = PSB()
            pv = pb.v().re("p (q t) -> p q t", q=8)
            cs = slice(c * 128, (c + 1) * 128)
            K.tr(pv[:, 0, :], vbf[:, cs], identb.v())
            K.tr(pv[:, 1, :], bhat[:, cs], identb.v())
            K.tr(pv[:, 2, :], khat[:, cs], identb.v())
            K.tr(pv[:, 3, :], AR[:, c, 0, :], identb.v())
            K.copy("act" if c % 2 else "dve", TMq[:, c, :, :], pv[:, 0:4, :])
        sq_src = [xr_r[:, T:TT], kmod[:, T:TT], vs32.v(), decs.v(), avs.v(), bfm[:, T:TT]]
        for half in range(2):
            ps = PSF()
            for j in range(3):
                K.tr(ps[:NS, j * 128:(j + 1) * 128], sq_src[half * 3 + j], identf.v())
            K.copy("dve", smp.v(), ps[:NS, :384].re("p (q f) -> p q f", q=3))
            for j in range(3):
                q = half * 3 + j
                K.dma("sp", dap(samp_d, (2 * hp) * 448 + q * 64, [[8 * 448, NS], [448, 2], [1, 64]]),
                      smp[:, j, :].re("p (h n) -> p h n", h=2))


    def chunk_thunks(hp):
        CH = []
        CH.append(lambda: K.memset('dve', Sf.v(), 0.0))
        CH.append(lambda: K.memset('pool', Sb.v(), 0.0))
        pairs = [(c, hl) for c in range(16) for hl in range(2)]

        def pre_a(sl, c, hl):
            rows = slice(hl * 64, (hl + 1) * 64)
            cs = slice(c * 128, (c + 1) * 128)
            ps = PSF()
            arv = AR[rows, c, :, :].re("p a t -> p (a t)")
            K.mm(ps[:, 0:256], bti[rows, cs], arv)
            K.mm(ps[:, 256:512], kti[rows, cs], arv)
            ps2 = PSF()
            K.mm(ps2[:, 0:128], AR[rows, c, 0, :], bti[rows, cs])
            p4 = ps.v().re("p (q t) -> p q t", q=4)
            K.tt("dve", ATMP[sl % NPQ].v(), p4[:, 0::2, :], mask4[:, 0::2, :], ALU.mult)
            K.tt("dve", ABK[sl].v(), p4[:, 1::2, :], mask4[:, 1::2, :], ALU.mult)
            K.tt("dve", P0T[sl % NPQ].v(), ps2[:, 0:128], maskNT.v(), ALU.mult)
            K.tt("pool", QTb[sl % NPQ][0].v(), P0T[sl % NPQ].v(), identb.v(), ALU.add)

        def pre_lev(sl, lev, gi):
            if lev == 0:
                Pj, PjT, Mp = ATMP[sl % NPQ][:, 0, :], P0T[sl % NPQ].v(), identb.v()
            else:
                src = PMb[sl % NPQ][lev % 2]
                Pj, PjT, Mp = src[:, 0, :], src[:, 1, :], src[:, 2, :]
            dst = PMb[sl % NPQ][(lev + 1) % 2]
            ps = PSF()
            if lev < 6:
                K.mm(ps[:, 0:128], PjT, Pj)
                K.mm(ps[:, 128:256], Pj, PjT)
            K.mm(ps[:, 256:384], QTb[sl % NPQ][lev % 2].v(), Mp, start=True, stop=True)
            eng = "act" if (gi % 8) not in (2, 5, 7) else "dve"
            if lev < 6:
                K.copy(eng, dst.v(), ps[:, 0:384].re("p (q t) -> p q t", q=3))
                K.tt("pool", QTb[sl % NPQ][(lev + 1) % 2].v(), dst[:, 1, :], identb.v(), ALU.add)
            else:
                K.copy(eng, dst[:, 2, :], ps[:, 256:384])

        def pre_w(sl, c, hl, gi):
            rows = slice(hl * 64, (hl + 1) * 64)
            cs = slice(c * 128, (c + 1) * 128)
            Mfin = PMb[sl % NPQ][1][:, 2, :]
            ps = PSF()
            K.mm(ps[:, 0:128], TMq[:, c, 3, :], Mfin)
            K.mm(ps[:, 128:192], ATMP[sl % NPQ][:, 1, :], TMq[:, c, 0, rows])
            eng = "act" if gi % 2 else "dve"
            K.copy(eng, Wfm[rows, cs], ps[rows, 0:128])
            K.copy(eng, Ybf[sl % NPQ].v(), ps[:, 128:192])

        def pre_u(sl, gi):
            Mfin = PMb[sl % NPQ][1][:, 2, :]
            ps = PSF()
            K.mm(ps[:, 0:64], Mfin, Ybf[sl % NPQ].v())
            K.copy("act" if gi % 2 else "dve", Ubar[sl].v(), ps[:, 0:64])

        def precompute(group):
            th = []
            for (sl, c, hl) in group:
                th.append(partial(pre_a, sl, c, hl))
            for lev in range(7):
                for gi, (sl, c, hl) in enumerate(group):
                    th.append(partial(pre_lev, sl, lev, gi))
            for gi, (sl, c, hl) in enumerate(group):
                th.append(partial(pre_w, sl, c, hl, gi))
            for gi, (sl, c, hl) in enumerate(group):
                th.append(partial(pre_u, sl, gi))
            return th

        def ser_u(sl, c, hl, gi):
            rows = slice(hl * 64, (hl + 1) * 64)
            cs = slice(c * 128, (c + 1) * 128)
            ut = Ut[gi % 4]
            ps = PSF()
            K.mm(ps[:, 0:64], Wfm[rows, cs], Sb[rows, :])
            K.tt("dve", ut.v(), ps[:, 0:64], Ubar[sl].v(), ALU.add)

        def ser_os(sl, c, hl, gi):
            rows = slice(hl * 64, (hl + 1) * 64)
            ut = Ut[gi % 4]
            pso = PSF()
            K.mm(pso[:, 0:64], AR[rows, c, 1, :], Sb[rows, :], start=True, stop=False)
            K.mm(pso[:, 0:64], ABK[sl][:, 0, :], ut.v(), start=False, stop=False)
            K.mm(pso[:, 0:64], ABK[sl][:, 1, :], TMq[:, c, 0, rows], start=False, stop=True)
            pss = PSF()
            K.mm(pss[:, 0:64], TMq[:, c, 1, :], ut.v(), start=True, stop=False)
            K.mm(pss[:, 0:64], TMq[:, c, 2, :], TMq[:, c, 0, rows], start=False, stop=True)
            K.stt(Sf[rows, :], Sf[rows, :], gL[rows, c:c + 1], pss[rows, 0:64], ALU.mult, ALU.add)
            K.copy("act", Sb[rows, :], Sf[rows, :])
            ot = Otile[c % 2]
            K.copy("act", ot[:, rows], pso[:, 0:64])
            if hl == 1:
                K.dma("sp", o_d[c * 128:(c + 1) * 128, hp * 128:(hp + 1) * 128], ot.v())

        def serial(group):
            th = []
            for gi, (sl, c, hl) in enumerate(group):
                th.append(partial(ser_u, sl, c, hl, gi))
                th.append(partial(ser_os, sl, c, hl, gi))
            return th

        def run_merged(lists):
            idx = [0] * len(lists)
            while True:
                best, bf = -1, 2.0
                for li, L in enumerate(lists):
                    if idx[li] < len(L):
                        frac = idx[li] / len(L)
                        if frac < bf:
                            best, bf = li, frac
                if best < 0:
                    break
                lists[best][idx[best]]()
                idx[best] += 1

        G = 8
        groups = []
        for g0 in range(0, 32, G):
            groups.append([((g0 + i) % NPR, pairs[g0 + i][0], pairs[g0 + i][1]) for i in range(G)])
        CH.extend(precompute(groups[0]))
        for gi_ in range(len(groups)):
            lists = [serial(groups[gi_])]
            if gi_ + 1 < len(groups):
                lists.append(precompute(groups[gi_ + 1]))
            CH.extend(merge_lists(lists))
        def fin_state():
            ps = PSF()
            K.tr(ps[:64, 0:128], Sf.v(), identf.v())
            K.copy("dve", wkvo.v(), ps[:64, 0:128])
            K.dma("sp", o_wkv_p[2 * hp:2 * hp + 2, :, :].re("h v k -> v h k"), wkvo.v().re("v (h k) -> v h k", h=2))

        CH.append(fin_state)
        return CH

    def drain(gen):
        for _ in gen:
            pass

    NA_EST = 150
    drain(derivedA(0))
    derivedB(0)
    for hp in range(4):
        CH = chunk_thunks(hp)
        OUTL = []
        gen = derivedA(hp + 1) if hp + 1 < 4 else None
        per = max(1, len(CH) // NA_EST)
        pero = max(1, len(CH) // max(1, len(OUTL)))
        oi = 0
        for ti, th in enumerate(CH):
            th()
            if gen is not None and ti % per == per - 1:
                try:
                    next(gen)
                except StopIteration:
                    gen = None
            if oi < len(OUTL) and ti % pero == pero - 1:
                OUTL[oi]()
                oi += 1
        while oi < len(OUTL):
            OUTL[oi]()
            oi += 1
        if gen is not None:
            drain(gen)
        if hp + 1 < 4:
            derivedB(hp + 1)

    K.pop_scope()
    if stage == 3:
        zt = K.sb("zt", [128, 4096], F32)
        K.memset("pool", zt.v(), 0.0)
        K.dma("sp", o_wkv_s.v(), zt[:, :])
        for i in range(16):
            K.dma("sp", y_p[i * 128:(i + 1) * 128, :], zt[:, 0:D])
        K.dma("sp", y_s.v(), zt[0:NS, 0:D])
        return early()

    def bcast_row(name, src, n):
        t = K.sb(name, [128, n], F32)
        K.dma("sp", t.v(), dap(src, 0, [[0, 128], [1, n]]))
        return t

    K.push_scope()
    hgl = K.sb("hgl2", [128, 4, TT], BF16)
    wmo = K.sb("wmo", [128, 8, D], BF16)
    gpost = K.sb("gpost_mix", [128, D], F32)
    PREF = [lambda: K.dma("sp", gpost.v(), dap(g_post_mix, 0, [[0, 128], [1, D]])),
            lambda: K.dma("sp", hgl.v().re("p k t -> p (k t)"), hgl_d.v())]
    for kt in range(8):
        PREF.append(partial(lambda kt_: K.dma("pool", wmo[:, kt_, :], w_merge_out[kt_ * 128:(kt_ + 1) * 128, :]), kt))
    mring = [K.sb("mring", [128, 8, 128], BF16) for _ in range(10)]
    mr_i = [0]

    def mload(src, c0, ncols_total, kt_n):
        mr_i[0] += 1
        wb = mring[mr_i[0] % len(mring)]
        K.dma("pool", wb[:, :kt_n, :], dap(src, c0, [[ncols_total, 128], [128 * ncols_total, kt_n], [1, 128]]))
        return wb

    GOFF = SHIFT + RW
    mw = {}

    def mg_load(f):
        mw[f] = (mload(w_rwkv_out, f * 128, D, 4), mload(w_in, GOFF + f * 128, PROJ, 8),
                 mload(w_in, GOFF + D + f * 128, PROJ, 8), mload(glu_w1, f * 128, D, 4), mload(glu_w2, f * 128, D, 4))

    PREF.append(partial(mg_load, 0))
    K.push_scope()
    eps_ln = K.sb("eps_ln2", [128, 1], F32)
    K.memset("pool", eps_ln.v(), 64e-5)
    NORF = 3
    fot_r = [K.sb("fot", [128, RW], F32) for _ in range(NORF)]
    fbt_r = [K.sb("fbt", [128, RW], F32) for _ in range(NORF)]
    fon_r = [K.sb("fon", [128, RW], F32) for _ in range(NORF)]
    fofb = [K.sb("fofb", [128, RW], BF16) for _ in range(NORF)]
    fst8 = [K.sb("fst8", [128, 6, 8], F32) for _ in range(NORF)]
    fh3 = lambda b_: b_.v().re("p (h n) -> p h n", h=8)

    def fout_a(c):
        r_ = c % NORF
        ot, bt, tmp, s8 = fot_r[r_], fbt_r[r_], fon_r[r_], fst8[r_]
        K.dma("sp", ot.v(), o_d[c * 128:(c + 1) * 128, :])
        K.dma("sp", bt.v(), bonus_d[c * 128:(c + 1) * 128, :])
        K.reduce(s8[:, 0, :], fh3(ot))
        K.act(tmp.v(), ot.v(), AF.Square)
        K.reduce(s8[:, 1, :], fh3(tmp))
        K.ts("dve", s8[:, 2, :], s8[:, 0, :], 1.0 / 64, None, ALU.mult)
        K.tt("dve", s8[:, 3, :], s8[:, 2, :], s8[:, 2, :], ALU.mult)
        K.stt(s8[:, 4, :], s8[:, 1, :], 1.0 / 64, s8[:, 3, :], ALU.mult, ALU.subtract)
        K.act(s8[:, 4, :], s8[:, 4, :], AF.Sqrt, bias=eps_ln.v(), scale=1.0)
        K.recip(s8[:, 5, :], s8[:, 4, :])
        K.stt(s8[:, 3, :], s8[:, 2, :], -1.0, s8[:, 5, :], ALU.mult, ALU.mult)

    def fout_b(c):
        r_ = c % NORF
        ot, bt, on, of, s8 = fot_r[r_], fbt_r[r_], fon_r[r_], fofb[r_], fst8[r_]
        for h in range(8):
            K.act(on[:, h * 64:(h + 1) * 64], ot[:, h * 64:(h + 1) * 64], AF.Identity,
                  bias=s8[:, 3, h:h + 1], scale=s8[:, 5, h:h + 1])
        K.tt("pool", bt.v(), bt.v(), lnxb.v(), ALU.add)
        K.tt("dve", on.v(), on.v(), lnxg.v(), ALU.mult)
        K.tt("dve", on.v(), on.v(), bt.v(), ALU.add)
        ps = PSF()
        K.mm(ps.v(), lora_bf[64:128, c * 128:(c + 1) * 128], lora_up[64:128, :])
        K.tt("dve", of.v(), on.v(), ps.v(), ALU.mult)
        pb = PSB()
        pv = pb.v().re("p (q t) -> p q t", q=8)
        for kt in range(4):
            K.tr(pv[:, kt, :], of[:, kt * 128:(kt + 1) * 128], identb.v())
        K.copy("act", oaT[:, :, c * 128:(c + 1) * 128], pv[:, 0:4, :])

    OUT = []
    for c in range(16 + 2):
        if c < 16:
            OUT.append(partial(fout_a, c))
        if 0 <= c - 2 < 16:
            OUT.append(partial(fout_b, c - 2))

    gs_tm = K.sb("gs_tm", [NS, RW], F32)
    ps = PSF()
    K.mm(ps[:NS, :], lora_bf[64:128, T:TT], lora_up[64:128, :])
    K.copy("act", gs_tm.v(), ps[:NS, :])
    K.dma("sp", dap(samp_d, 6 * 64, [[8 * 448, NS], [448, 8], [1, 64]]), gs_tm.v().re("p (h n) -> p h n", h=8))
    vec = K.sb("vec", [128, 7, 64], F32)
    K.dma("sp", vec.v().re("p q n -> p (q n)"), dap(samp_d, 0, [[448, 128], [1, 448]]))
    S0s = K.sb("S0s", [128, 64, 64], F32)
    S1s = K.sb("S1s", [128, 64, 64], F32)
    tS = K.sb("tS", [128, 64, 64], F32)
    K.dma("sp", S0s.v().re("p v k -> p (v k)"), st_wkv.v())
    r_, km_, v_, dec_, av_, b_, g_ = [vec[:, q, :] for q in range(7)]
    overv = lambda x: x.un(1).bc([128, 64, 64])
    overk = lambda x: x.un(2).bc([128, 64, 64])
    sv = K.sb("sv", [128, 8, 64], F32)
    ss = K.sb("ss", [128, 8], F32)
    eps2 = K.sb("eps_ln3", [128, 1], F32)
    K.memset("pool", eps2.v(), 64e-5)
    os_tm = K.sb("os_tm", [NS, RW], F32)
    os_bf = K.sb("os_bf", [NS, RW], BF16)
    SMP = [
        lambda: K.tt("dve", tS.v(), S0s.v(), overv(av_), ALU.mult),
        lambda: K.reduce(sv[:, 0, :], tS.v()),
        lambda: K.tt("pool", S1s.v(), S0s.v(), overv(dec_), ALU.mult),
        lambda: K.tt("dve", tS.v(), overk(sv[:, 0, :]), overv(b_), ALU.mult),
        lambda: K.tt("dve", S1s.v(), S1s.v(), tS.v(), ALU.add),
        lambda: K.tt("dve", tS.v(), overk(v_), overv(km_), ALU.mult),
        lambda: K.tt("dve", S1s.v(), S1s.v(), tS.v(), ALU.add),
        lambda: K.dma("sp", o_wkv_s.v(), S1s.v().re("p v k -> p (v k)")),
        lambda: K.tt("dve", tS.v(), S1s.v(), overv(r_), ALU.mult),
        lambda: K.reduce(sv[:, 1, :], tS.v()),
        lambda: K.reduce(ss[:, 0:1], sv[:, 1, :]),
        lambda: K.tt("dve", sv[:, 2, :], sv[:, 1, :], sv[:, 1, :], ALU.mult),
        lambda: K.reduce(ss[:, 1:2], sv[:, 2, :]),
        lambda: K.ts("dve", ss[:, 2:3], ss[:, 0:1], 1.0 / 64, None, ALU.mult),
        lambda: K.tt("dve", ss[:, 3:4], ss[:, 2:3], ss[:, 2:3], ALU.mult),
        lambda: K.stt(ss[:, 4:5], ss[:, 1:2], 1.0 / 64, ss[:, 3:4], ALU.mult, ALU.subtract),
        lambda: K.act(ss[:, 4:5], ss[:, 4:5], AF.Sqrt, bias=eps2.v(), scale=1.0),
        lambda: K.recip(ss[:, 5:6], ss[:, 4:5]),
        lambda: K.ts("dve", sv[:, 3, :], sv[:, 1, :], ss[:, 2:3], ss[:, 5:6], ALU.subtract, ALU.mult),
        lambda: K.tt("dve", sv[:, 3, :], sv[:, 3, :], par[:, 0, :], ALU.mult),
        lambda: K.tt("dve", sv[:, 3, :], sv[:, 3, :], par[:, 1, :], ALU.add),
        lambda: K.tt("dve", sv[:, 4, :], r_, km_, ALU.mult),
        lambda: K.tt("dve", sv[:, 4, :], sv[:, 4, :], par[:, 2, :], ALU.mult),
        lambda: K.reduce(ss[:, 6:7], sv[:, 4, :]),
        lambda: K.stt(sv[:, 3, :], v_, ss[:, 6:7], sv[:, 3, :], ALU.mult, ALU.add),
        lambda: K.tt("dve", sv[:, 5, :], sv[:, 3, :], g_, ALU.mult),
        lambda: K.dma("sp", dap(sampo_d, 0, [[64, 128], [1, 64]]), sv[:, 5, :]),
        lambda: K.dma("sp", os_tm.v(), sampo_d.v()),
        lambda: K.copy("dve", os_bf.v(), os_tm.v()),
    ]

    def smp_fin():
        pb = PSB()
        pv = pb.v().re("p (q t) -> p q t", q=8)
        for kt in range(4):
            K.tr(pv[:, kt, :NS], os_bf[:, kt * 128:(kt + 1) * 128], identb[:NS, :NS])
        K.copy("act", oaT[:, :, T:TT], pv[:, 0:4, :NS])

    SMP.append(smp_fin)
    run_merged_g([OUT, SMP, [lambda: None] * 6 + PREF])
    K.pop_scope()
    if stage == 4:
        zt = K.sb("zt", [128, 1024], F32)
        K.memset("pool", zt.v(), 0.0)
        for i in range(16):
            K.dma("sp", y_p[i * 128:(i + 1) * 128, :], zt[:, 0:D])
        K.dma("sp", y_s.v(), zt[0:NS, 0:D])
        return early()


    def pn_b(ps_halves, rows, res_src, gpost, dst, bufs):
        st, xres, tt_ = bufs[:3]
        K.dma("sp", xres[:rows, :], res_src)
        for hf in range(2):
            K.act(tt_[:rows, hf * 512:(hf + 1) * 512], ps_halves[hf][:rows, :], AF.Square, accum=st[:rows, hf:hf + 1])
        K.tt("dve", st[:rows, 2:3], st[:rows, 0:1], st[:rows, 1:2], ALU.add)
        K.act(st[:rows, 3:4], st[:rows, 2:3], AF.Sqrt, bias=epsT[:rows, :], scale=1.0 / D)
        K.recip(st[:rows, 4:5], st[:rows, 3:4])
        for hf in range(2):
            hs = slice(hf * 512, (hf + 1) * 512)
            K.stt(tt_[:rows, hs], ps_halves[hf][:rows, :], st[:rows, 4:5], gpost[:rows, hs], ALU.mult, ALU.mult)
        K.tt("dve", xres[:rows, :], xres[:rows, :], tt_[:rows, :], ALU.add)
        K.dma("sp", dst, xres[:rows, :])

    def pn_c(rows, bufs, nt):
        st, xres, tt_, xb = bufs
        gname, dstT, col0 = nt
        K.act(tt_[:rows, :], xres[:rows, :], AF.Square, accum=st[:rows, 5:6])
        K.act(st[:rows, 6:7], st[:rows, 5:6], AF.Sqrt, bias=epsT[:rows, :], scale=1.0 / D)
        K.recip(st[:rows, 7:8], st[:rows, 6:7])
        K.act(xb[:rows, :], xres[:rows, :], AF.Copy, scale=st[:rows, 7:8])
        pb = PSB()
        pv = pb.v().re("p (k t) -> p k t", k=8)
        for kt in range(8):
            K.tr(pv[:, kt, :rows], xb[:rows, kt * 128:(kt + 1) * 128], identb[:rows, :rows])
        g0 = PV[gname]
        K.tt("dve", dstT[:, :, col0:col0 + rows], pv[:, :, :rows],
             pvec[:, g0:g0 + 8].un(2).bc([128, 8, rows]), ALU.mult)

    mT = K.sb("mT", [128, 8, TT], BF16)
    mt = [[K.sb("mt", [128, 512], F32) for _ in range(4)] for _ in range(2)]
    mps = {}

    def mg_a1(f, tb, it):
        col0, n = TBLK[tb]
        cs = slice(col0, col0 + n)
        wA, wga, wgb, w1, w2 = mw[f]
        ps2, ps1 = PSF(), PSF()
        for kt in range(4):
            K.mm(ps2[:, :n], w2[:, kt, :], hgl[:, kt, cs], start=(kt == 0), stop=(kt == 3))
        for kt in range(4):
            K.mm(ps1[:, :n], w1[:, kt, :], hgl[:, kt, cs], start=(kt == 0), stop=(kt == 3))
        mps[(it, 1)] = (ps2, ps1)

    def mg_a2(f, tb, it):
        col0, n = TBLK[tb]
        cs = slice(col0, col0 + n)
        wA, wga, wgb, w1, w2 = mw[f]
        psgb = PSF()
        for kt in range(8):
            K.mm(psgb[:, :n], wgb[:, kt, :], hT[:, kt, cs], start=(kt == 0), stop=(kt == 7))
        mps[(it, 2)] = psgb

    def mg_a3(f, tb, it):
        col0, n = TBLK[tb]
        cs = slice(col0, col0 + n)
        wA, wga, wgb, w1, w2 = mw[f]
        psga, psA = PSF(), PSF()
        for kt in range(8):
            K.mm(psga[:, :n], wga[:, kt, :], hT[:, kt, cs], start=(kt == 0), stop=(kt == 7))
        for kt in range(4):
            K.mm(psA[:, :n], wA[:, kt, :], oaT[:, kt, cs], start=(kt == 0), stop=(kt == 3))
        mps[(it, 3)] = (psga, psA)

    def mg_b1(f, tb, it):
        col0, n = TBLK[tb]
        t0, t1_, t2_, t3_ = mt[it % 2]
        ps2, ps1 = mps.pop((it, 1))
        K.act(t0[:, :n], ps2[:, :n], AF.Sigmoid, bias=pcol("glu_b2", f))
        K.stt(t1_[:, :n], ps1[:, :n], pcol("glu_b1", f), t0[:, :n], ALU.add, ALU.mult)

    def mg_b2(f, tb, it):
        col0, n = TBLK[tb]
        t0, t1_, t2_, t3_ = mt[it % 2]
        psgb = mps.pop((it, 2))
        K.act(t2_[:, :n], psgb[:, :n], AF.Sigmoid)
        K.tt("dve", t1_[:, :n], t1_[:, :n], t2_[:, :n], ALU.mult)

    def mg_b3(f, tb, it):
        col0, n = TBLK[tb]
        cs = slice(col0, col0 + n)
        t0, t1_, t2_, t3_ = mt[it % 2]
        psga, psA = mps.pop((it, 3))
        K.act(t3_[:, :n], psga[:, :n], AF.Sigmoid)
        K.tt("dve", t3_[:, :n], psA[:, :n], t3_[:, :n], ALU.mult)
        K.tt("pool" if n > 16 else "dve", mT[:, f, cs], t3_[:, :n], t1_[:, :n], ALU.add)

    its = [(f, tb) for f in range(8) for tb in range(5)]
    for it, (f, tb) in enumerate(its):
        if tb == 0 and f + 1 < 8:
            mg_load(f + 1)
        mg_a1(f, tb, it)
        if it >= 1:
            pf, ptb = its[it - 1]
            mg_b3(pf, ptb, it - 1)
        mg_a2(f, tb, it)
        mg_b1(f, tb, it)
        mg_a3(f, tb, it)
        mg_b2(f, tb, it)
    mg_b3(its[-1][0], its[-1][1], len(its) - 1)
    pn_bufs = [(K.sb("pn_st", [128, 8], F32), K.sb("pn_x", [128, D], F32), K.sb("pn_t", [128, D], F32),
                K.sb("pn_xb", [128, D], BF16)) for _ in range(3)]
    mix_ps = {}

    def mix_a(i):
        rows = 128 if i < 16 else NS
        tcs = slice(i * 128, i * 128 + rows)
        halves = [PSF(), PSF()]
        for hf in range(2):
            for kt in range(8):
                K.mm(halves[hf][:rows, :], mT[:, kt, tcs], wmo[:, kt, hf * 512:(hf + 1) * 512],
                     start=(kt == 0), stop=(kt == 7))
        mix_ps[i] = halves

    def mix_b(i):
        rows = 128 if i < 16 else NS
        pn_b(mix_ps[i], rows, x_tile(i), gpost, x1_d[i * 128:i * 128 + rows, :], pn_bufs[i % 3])

    def mix_c(i):
        rows = 128 if i < 16 else NS
        pn_c(rows, pn_bufs[i % 3], ("g_pre_ffn", hT, i * 128))

    for step in range(17 + 2):
        if step < 17:
            mix_a(step)
        if 0 <= step - 1 < 17:
            mix_b(step - 1)
        if 0 <= step - 2 < 17:
            mix_c(step - 2)
    K.pop_scope()
    K.pop_scope()

    K.push_scope()
    NF = DFF // 128
    aT = K.sb("aT", [128, NF, TT], BF16)
    wd = K.sb("wd", [128, NF, D], BF16)
    gpost2 = bcast_row("gpost_ffn", g_post_ffn, D)
    fring = [K.sb("fring", [128, 8, 128], BF16) for _ in range(4)]
    fr_i = [0]

    def fload(src, c0):
        fr_i[0] += 1
        wb = fring[fr_i[0] % len(fring)]
        K.dma("pool", wb.v(), dap(src, c0, [[DFF, 128], [128 * DFF, 8], [1, 128]]))
        return wb

    sgt = [K.sb("sgt", [128, 512], F32) for _ in range(2)]
    it = 0
    for ft in range(NF):
        wg = fload(w_ffn_gate, ft * 128)
        wu = fload(w_ffn_up, ft * 128)
        if ft >= 1:
            K.dma("pool", wd[:, ft - 1, :], w_ffn_down[(ft - 1) * 128:ft * 128, :])
        if ft == NF - 1:
            K.dma("pool", wd[:, ft, :], w_ffn_down[ft * 128:(ft + 1) * 128, :])
        for tb in range(5):
            col0, n = TBLK[tb]
            cs = slice(col0, col0 + n)
            psg, psu = PSF(), PSF()
            for kt in range(8):
                K.mm(psg[:, :n], wg[:, kt, :], hT[:, kt, cs], start=(kt == 0), stop=(kt == 7))
            for kt in range(8):
                K.mm(psu[:, :n], wu[:, kt, :], hT[:, kt, cs], start=(kt == 0), stop=(kt == 7))
            sg = sgt[it % 2]
            it += 1
            K.act(sg[:, :n], psg[:, :n], AF.Silu)
            K.tt("dve", aT[:, ft, cs], psu[:, :n], sg[:, :n], ALU.mult)
    pn_bufs = [(K.sb("pn_st", [128, 8], F32), K.sb("pn_x", [128, D], F32), K.sb("pn_t", [128, D], F32))
               for _ in range(2)]
    dn_ps = {}

    def dn_a(i):
        rows = 128 if i < 16 else NS
        tcs = slice(i * 128, i * 128 + rows)
        halves = [PSF(), PSF()]
        for hf in range(2):
            for kt in range(NF):
                K.mm(halves[hf][:rows, :], aT[:, kt, tcs], wd[:, kt, hf * 512:(hf + 1) * 512],
                     start=(kt == 0), stop=(kt == NF - 1))
        dn_ps[i] = halves

    def dn_b(i):
        rows = 128 if i < 16 else NS
        dst = y_p[i * 128:(i + 1) * 128, :] if i < 16 else y_s[:, :]
        pn_b(dn_ps[i], rows, x1_d[i * 128:i * 128 + rows, :], gpost2, dst, pn_bufs[i % 2])

    for step in range(17 + 1):
        if step < 17:
            dn_a(step)
        if step - 1 >= 0:
            dn_b(step - 1)
    K.pop_scope()
    K.finish()
    es.close()
    return nc


_W_NAMES = ["norm_pre_mix", "norm_post_mix", "norm_pre_ffn", "norm_post_ffn", "w_in", "mu_shift", "w0",
            "w_decay_up", "a0", "w_aaa_up", "w_gate_up", "k_k", "k_a", "r_k", "lnx_g", "lnx_b", "w_rwkv_out",
            "s5_lam_re", "s5_lam_im", "s5_log_dt", "s5_b_re", "s5_b_im", "s5_c_re", "s5_c_im", "s5_d",
            "glu_w1", "glu_b1", "glu_w2", "glu_b2", "w_merge_out", "w_ffn_gate", "w_ffn_up", "w_ffn_down"]


def make_in_maps(inputs, cores):
    f = lambda a: np.ascontiguousarray(np.asarray(a, dtype=np.float32))
    shared = {n: f(inputs[n])[0] for n in _W_NAMES}
    maps = []
    for c in cores:
        m = dict(shared)
        m["x_p"] = f(inputs["x_prompt"][c])
        m["x_s"] = f(inputs["x_sample"][c * NS:(c + 1) * NS, 0])
        m["st_shift"] = f(inputs["state_shift"][0, c * NS:(c + 1) * NS])
        m["st_wkv"] = f(inputs["state_wkv"][0, c * NS:(c + 1) * NS]).reshape(NS * 8, 64 * 64)
        m["st_re"] = f(inputs["state_s5_re"][0, c * NS:(c + 1) * NS]).reshape(NS, 2048)
        m["st_im"] = f(inputs["state_s5_im"][0, c * NS:(c + 1) * NS]).reshape(NS, 2048)
        maps.append(m)
    return maps


def assemble(results):
    n = len(results)
    cat = lambda k: np.concatenate([np.asarray(r[k]) for r in results], axis=0)
    y_p = np.stack([np.asarray(r["y_p"]) for r in results], 0)
    y_s = cat("y_s").reshape(n * NS, 1, D)
    sh_p = cat("o_shift_p").reshape(1, n, SHIFT)
    wkv_p = np.stack([np.asarray(r["o_wkv_p"]) for r in results], 0).reshape(1, n, 8, 64, 64)
    re_p = cat("o_re_p").reshape(1, n, 32, 64)
    im_p = cat("o_im_p").reshape(1, n, 32, 64)
    sh_s = cat("o_shift_s").reshape(1, n * NS, SHIFT)
    wkv_s = cat("o_wkv_s").reshape(1, n * NS, 8, 64, 64)
    re_s = cat("o_re_s").reshape(1, n * NS, 32, 64)
    im_s = cat("o_im_s").reshape(1, n * NS, 32, 64)
    return tuple(np.ascontiguousarray(a, dtype=np.float32) for a in
                 (y_p, y_s, sh_p, wkv_p, re_p, im_p, sh_s, wkv_s, re_s, im_s))


def kernel(**inputs):
    nc = build()
    in_maps = make_in_maps(inputs, list(range(NCORES)))
    res = run_bass_kernel_spmd(nc, in_maps, core_ids=list(range(NCORES)))
    return assemble(res.results)
```

```python
import math
from contextlib import ExitStack
from functools import partial
import numpy as np
import concourse.bass as bass
import concourse.mybir as mybir
from concourse.bass_utils import run_bass_kernel_spmd

F32 = mybir.dt.float32
BF16 = mybir.dt.bfloat16
I32 = mybir.dt.int32
AF = mybir.ActivationFunctionType
ALU = mybir.AluOpType
AX = mybir.AxisListType

T = 2048
NS = 16
TT = T + NS
D = 1024
RW = 512
SHIFT = 1664
PROJ = 4224
DFF = 2816
NCORES = 8
CDEC = math.exp(-0.5)
TBLK = [(0, 512), (512, 512), (1024, 512), (1536, 512), (2048, 16)]
SAME_ENGINE_SYNC = True
LS = 4


class V:
    __slots__ = ("buf", "ap")

    def __init__(self, buf, ap):
        self.buf = buf
        self.ap = ap

    def __getitem__(self, idx):
        return V(self.buf, self.ap[idx])

    def bc(self, shape):
        return V(self.buf, self.ap.to_broadcast(list(shape)))

    def re(self, s, **kw):
        return V(self.buf, self.ap.rearrange(s, **kw))

    def un(self, axis):
        return V(self.buf, self.ap.unsqueeze(axis))

    def bitcast(self, dt):
        return V(self.buf, self.ap.bitcast(dt))


class Buf:
    def __init__(self, name, handle):
        self.name = name
        self.h = handle
        self.writes = {}
        self.reads = {}
        self.dsem = None
        self.dcnt = 0
        self.is_psum = False

    def __getitem__(self, idx):
        return V(self, self.h[idx])

    def v(self):
        return V(self, self.h[:])


class Kern:
    def __init__(self, nc, es):
        self.nc = nc
        self.es = es
        self.root_es = es
        self.eng = {"pe": nc.tensor, "act": nc.scalar, "dve": nc.vector, "pool": nc.gpsimd, "sp": nc.sync}
        self.sem = {}
        self.cnt = {}
        for e in ("pe", "act", "dve", "pool"):
            self.sem[e] = es.enter_context(nc.semaphore("s_" + e))
            self.cnt[e] = 0
        self.obs = {e: {} for e in self.eng}
        self.semname = {}
        self.dsems = []
        self.nbuf = 0
        self.psf_i = 0
        self.psb_i = 0

    def sb(self, name, shape, dt):
        self.nbuf += 1
        import os
        if os.environ.get("KDBG"):
            sz = int(np.prod(shape[1:])) * (2 if dt == BF16 else 4)
            self._tot = getattr(self, "_tot", 0) + sz
            print(f"alloc {name} {shape} {sz} depth={len(getattr(self, '_saved', []))} cum_noFree={self._tot}")
        h = self.es.enter_context(self.nc.sbuf_tensor(f"{name}_{self.nbuf}", list(shape), dt))
        return Buf(name, h)

    def barrier(self):
        for e in ("pe", "act", "dve", "pool", "sp"):
            for e2 in ("pe", "act", "dve", "pool"):
                if e2 != e and self.cnt[e2]:
                    self._wait(e, e2, self.sem[e2], self.cnt[e2])
            for ob in self.dsems:
                self._wait(e, "d:" + ob.name + str(id(ob)), ob.dsem, ob.dcnt)

    def push_scope(self):
        self._saved = getattr(self, "_saved", [])
        self._saved.append(self.es)
        self.es = ExitStack()

    def pop_scope(self):
        self.barrier()
        self.es.close()
        self.es = self._saved.pop()

    def ps(self, name, shape, dt):
        self.nbuf += 1
        h = self.es.enter_context(self.nc.psum_tensor(f"{name}_{self.nbuf}", list(shape), dt))
        b = Buf(name, h)
        b.is_psum = True
        return b

    def dram(self, name, shape, dt, kind="Internal"):
        h = self.nc.dram_tensor(name, list(shape), dt, kind=kind)
        return Buf(name, h.ap())

    def newsem(self, name):
        s = self.root_es.enter_context(self.nc.semaphore(name))
        return s

    def _wait(self, e, sem_key, sem, val):
        o = self.obs[e]
        if o.get(sem_key, 0) >= val:
            return
        self.eng[e].wait_ge(sem, val)
        o[sem_key] = val

    def _deps(self, e, W, R):
        need = {}
        for b in R:
            for k, (s, v) in b.writes.items():
                if need.get(k, (None, 0))[1] < v:
                    need[k] = (s, v)
            if b.is_psum:
                for k, (s, v) in b.reads.items():
                    if k != e and need.get(k, (None, 0))[1] < v:
                        need[k] = (s, v)
        for b in W:
            for k, (s, v) in b.writes.items():
                if need.get(k, (None, 0))[1] < v:
                    need[k] = (s, v)
            for k, (s, v) in b.reads.items():
                if need.get(k, (None, 0))[1] < v:
                    need[k] = (s, v)
        for k, (s, v) in need.items():
            if k == e:
                if e == "pe" or not SAME_ENGINE_SYNC:
                    continue
            self._wait(e, k, s, v)

    def _bufs(self, vs):
        out = []
        for x in vs:
            if isinstance(x, V):
                if x.buf not in out:
                    out.append(x.buf)
            elif isinstance(x, Buf):
                if x not in out:
                    out.append(x)
        return out

    def op(self, e, fn, W, R):
        Wb = self._bufs(W)
        Rb = self._bufs(R)
        self._deps(e, Wb, Rb)
        inst = fn()
        self.cnt[e] += 1
        c = self.cnt[e]
        inst.then_inc(self.sem[e], 1)
        for b in Wb:
            b.reads = {}
            b.writes[e] = (self.sem[e], c)
        for b in Rb:
            if b not in Wb:
                b.reads[e] = (self.sem[e], c)
        return inst

    def dma(self, q, out, in_, sem_buf=None, slow=False):
        Wb = self._bufs([out])
        Rb = self._bufs([in_])
        self._deps(q, Wb, Rb)
        ob = sem_buf if sem_buf is not None else out.buf
        if ob.dsem is None:
            ob.dsem = self.newsem("d_" + ob.name + str(len(self.dsems)))
            self.dsems.append(ob)
        if slow:
            with self.nc.allow_non_contiguous_dma(reason="small parameter layout load"):
                inst = self.eng[q].dma_start(out=out.ap, in_=in_.ap)
        else:
            inst = self.eng[q].dma_start(out=out.ap, in_=in_.ap)
        ob.dcnt += 16
        inst.then_inc(ob.dsem, 16)
        key = "d:" + ob.name + str(id(ob))
        for b in Wb:
            b.reads = {}
            b.writes[key] = (ob.dsem, ob.dcnt)
        for b in Rb:
            if b not in Wb:
                b.reads[key] = (ob.dsem, ob.dcnt)
        return inst

    def finish(self):
        for ob in self.dsems:
            self._wait("sp", "d:" + ob.name + str(id(ob)), ob.dsem, ob.dcnt)
        for e in ("pe", "act", "dve", "pool"):
            if self.cnt[e]:
                self._wait("sp", e, self.sem[e], self.cnt[e])

    @staticmethod
    def _a(x):
        return x.ap if isinstance(x, V) else x

    def act(self, out, in_, func, bias=None, scale=None, accum=None):
        kw = {}
        if bias is not None:
            kw["bias"] = self._a(bias)
        if scale is not None:
            kw["scale"] = self._a(scale)
        if accum is not None:
            kw["accum_out"] = self._a(accum)
        return self.op("act", lambda: self.nc.scalar.activation(out=out.ap, in_=in_.ap, func=func, **kw),
                       [out, accum], [in_, bias, scale])

    def tt(self, e, out, in0, in1, op):
        return self.op(e, lambda: self.eng[e].tensor_tensor(out=out.ap, in0=in0.ap, in1=in1.ap, op=op),
                       [out], [in0, in1])

    def ts(self, e, out, in0, s1, s2, op0, op1=None):
        if op1 is None:
            return self.op(e, lambda: self.eng[e].tensor_scalar(out=out.ap, in0=in0.ap, scalar1=self._a(s1),
                                                               scalar2=None, op0=op0), [out], [in0, s1])
        return self.op(e, lambda: self.eng[e].tensor_scalar(out=out.ap, in0=in0.ap, scalar1=self._a(s1),
                                                           scalar2=self._a(s2), op0=op0, op1=op1),
                       [out], [in0, s1, s2])

    def stt(self, out, in0, scalar, in1, op0, op1, e="dve"):
        return self.op(e, lambda: self.eng[e].scalar_tensor_tensor(out=out.ap, in0=in0.ap, scalar=self._a(scalar),
                                                                  in1=in1.ap, op0=op0, op1=op1),
                       [out], [in0, scalar, in1])

    def copy(self, e, out, in_):
        if e == "act":
            return self.act(out, in_, AF.Copy)
        return self.op(e, lambda: self.eng[e].tensor_copy(out=out.ap, in_=in_.ap), [out], [in_])

    def memset(self, e, out, val):
        return self.op(e, lambda: self.eng[e].memset(out.ap, val), [out], [])

    def recip(self, out, in_):
        return self.op("dve", lambda: self.nc.vector.reciprocal(out=out.ap, in_=in_.ap), [out], [in_])

    def reduce(self, out, in_, op=ALU.add, axis=AX.X):
        return self.op("dve", lambda: self.nc.vector.tensor_reduce(out=out.ap, in_=in_.ap, axis=axis, op=op),
                       [out], [in_])

    def scan(self, out, d0, d1, init, op0=ALU.mult, op1=ALU.add):
        return self.op("dve", lambda: self.nc.vector.tensor_tensor_scan(out=out.ap, data0=d0.ap, data1=d1.ap,
                                                                       initial=self._a(init), op0=op0, op1=op1),
                       [out], [d0, d1, init])

    def mm(self, out, lhsT, rhs, start=True, stop=True):
        return self.op("pe", lambda: self.nc.tensor.matmul(out.ap, lhsT=lhsT.ap, rhs=rhs.ap, start=start, stop=stop),
                       [out], [lhsT, rhs])

    def tr(self, out, in_, ident):
        return self.op("pe", lambda: self.nc.tensor.transpose(out.ap, in_.ap, ident.ap), [out], [in_, ident])

    def aselect(self, out, in_, pattern, cmp, fill, base, cm):
        return self.op("pool", lambda: self.nc.gpsimd.affine_select(out=out.ap, in_=in_.ap, pattern=pattern,
                                                                   compare_op=cmp, fill=fill, base=base,
                                                                   channel_multiplier=cm), [out], [in_])

    def iota(self, out, pattern, base, cm):
        return self.op("pool", lambda: self.nc.gpsimd.iota(out.ap, pattern=pattern, base=base,
                                                          channel_multiplier=cm), [out], [])


def run_merged_g(lists):
    idx = [0] * len(lists)
    while True:
        best, bf = -1, 2.0
        for li, L in enumerate(lists):
            if idx[li] < len(L):
                frac = idx[li] / len(L)
                if frac < bf:
                    best, bf = li, frac
        if best < 0:
            break
        lists[best][idx[best]]()
        idx[best] += 1


def merge_lists(lists):
    out = []
    idx = [0] * len(lists)
    while True:
        best, bf = -1, 2.0
        for li, L in enumerate(lists):
            if idx[li] < len(L):
                frac = idx[li] / len(L)
                if frac < bf:
                    best, bf = li, frac
        if best < 0:
            break
        out.append(lists[best][idx[best]])
        idx[best] += 1
    return out


def dap(buf, offset, ap):
    base = buf.h if not hasattr(buf.h, "ap") or isinstance(buf.h, bass.AP) else buf.h
    t = base.tensor if isinstance(base, bass.AP) else base
    return V(buf, bass.AP(t, offset, [list(x) for x in ap]))


def build(stage=99):
    nc = bass.Bass("TRN2", target_bir_lowering=False)
    es = ExitStack()
    K = Kern(nc, es)

    def din(name, shape, dt=F32):
        return Buf(name, nc.dram_tensor(name, list(shape), dt, kind="ExternalInput").ap())

    def dout(name, shape):
        return Buf(name, nc.dram_tensor(name, list(shape), F32, kind="ExternalOutput").ap())

    x_p = din("x_p", [T, D])
    x_s = din("x_s", [NS, D])
    st_shift = din("st_shift", [NS, SHIFT])
    st_wkv = din("st_wkv", [NS * 8, 64 * 64])
    st_re = din("st_re", [NS, 2048])
    st_im = din("st_im", [NS, 2048])
    g_pre_mix = din("norm_pre_mix", [D])
    g_post_mix = din("norm_post_mix", [D])
    g_pre_ffn = din("norm_pre_ffn", [D])
    g_post_ffn = din("norm_post_ffn", [D])
    w_in = din("w_in", [D, PROJ])
    mu_shift = din("mu_shift", [SHIFT])
    w0 = din("w0", [RW])
    w_decay_up = din("w_decay_up", [32, RW])
    a0 = din("a0", [RW])
    w_aaa_up = din("w_aaa_up", [32, RW])
    w_gate_up = din("w_gate_up", [64, RW])
    k_k = din("k_k", [RW])
    k_a = din("k_a", [RW])
    r_k = din("r_k", [RW])
    lnx_g = din("lnx_g", [RW])
    lnx_b = din("lnx_b", [RW])
    w_rwkv_out = din("w_rwkv_out", [RW, D])
    s5_lam_re = din("s5_lam_re", [32, 64])
    s5_lam_im = din("s5_lam_im", [32, 64])
    s5_log_dt = din("s5_log_dt", [32])
    s5_b_re = din("s5_b_re", [32, 64, 16])
    s5_b_im = din("s5_b_im", [32, 64, 16])
    s5_c_re = din("s5_c_re", [32, 16, 64])
    s5_c_im = din("s5_c_im", [32, 16, 64])
    s5_d = din("s5_d", [RW])
    glu_w1 = din("glu_w1", [RW, D])
    glu_b1 = din("glu_b1", [D])
    glu_w2 = din("glu_w2", [RW, D])
    glu_b2 = din("glu_b2", [D])
    w_merge_out = din("w_merge_out", [D, D])
    w_ffn_gate = din("w_ffn_gate", [D, DFF])
    w_ffn_up = din("w_ffn_up", [D, DFF])
    w_ffn_down = din("w_ffn_down", [DFF, D])

    y_p = dout("y_p", [T, D])
    y_s = dout("y_s", [NS, D])
    o_shift_p = dout("o_shift_p", [1, SHIFT])
    o_wkv_p = dout("o_wkv_p", [8, 64, 64])
    o_re_p = dout("o_re_p", [1, 2048])
    o_im_p = dout("o_im_p", [1, 2048])
    o_shift_s = dout("o_shift_s", [NS, SHIFT])
    o_wkv_s = dout("o_wkv_s", [NS * 8, 64 * 64])
    o_re_s = dout("o_re_s", [NS, 2048])
    o_im_s = dout("o_im_s", [NS, 2048])

    x1_d = K.dram("x1_scratch", [TT, D], F32)

    psf = [K.ps("psf", [128, 512], F32) for _ in range(8)]

    def PSF():
        K.psf_i += 1
        return psf[K.psf_i % len(psf)]

    class BV:
        def __init__(self, buf):
            self.buf = buf

        def v(self):
            return V(self.buf, self.buf.h[:].bitcast(BF16))

        def __getitem__(self, idx):
            return self.v()[idx]

    def PSB():
        return BV(PSF())

    ones_f = K.sb("ones_f", [128, 128], F32)
    identf = K.sb("identf", [128, 128], F32)
    identb = K.sb("identb", [128, 128], BF16)
    epsT = K.sb("epsT", [128, 1], F32)
    K.memset("pool", ones_f.v(), 1.0)
    K.memset("pool", epsT.v(), 1e-6)
    K.aselect(identf.v(), ones_f.v(), [[-1, 128]], ALU.is_equal, 0.0, 0, 1)
    K.copy("pool", identb.v(), identf.v())

    pvec = K.sb("pvec", [128, 128], F32)
    PV = {}
    _pv = [0]

    def load_fm(name, src, ntile):
        c0 = _pv[0]
        _pv[0] += ntile
        K.dma("sp", pvec[:, c0:c0 + ntile], dap(src, 0, [[1, 128], [128, ntile]]), sem_buf=pvec, slow=True)
        PV[name] = c0
        return c0

    load_fm("g_pre_mix", g_pre_mix, 8)
    load_fm("g_pre_ffn", g_pre_ffn, 8)
    load_fm("mu", mu_shift, 13)
    load_fm("w0", w0, 4)
    load_fm("a0", a0, 4)
    load_fm("k_k", k_k, 4)
    load_fm("k_a", k_a, 4)
    load_fm("r_k", r_k, 4)
    load_fm("s5_d", s5_d, 4)
    load_fm("glu_b1", glu_b1, 8)
    load_fm("glu_b2", glu_b2, 8)

    def early():
        while getattr(K, "_saved", []):
            K.pop_scope()
        K.finish()
        es.close()
        return nc

    if stage == 0.1:
        return early()

    def pcol(name, i=0):
        c = PV[name] + i
        return pvec[:, c:c + 1]

    hT = K.sb("hT", [128, 8, TT], BF16)
    def norm_transpose(src_tile_fn, gname, dst):
        K.push_scope()
        NB = 4
        xring = [K.sb("xring", [128, D], F32) for _ in range(NB)]
        xn = [K.sb("xn", [128, D], BF16) for _ in range(NB)]
        junk = K.sb("junk", [128, D], BF16)
        stat = [K.sb("stat", [128, 4], F32) for _ in range(NB)]

        def nt_a(i):
            rows = 128 if i < 16 else NS
            xt, st, xb = xring[i % NB], stat[i % NB], xn[i % NB]
            K.dma("sp", xt[:rows, :], src_tile_fn(i))
            K.act(junk[:rows, :], xt[:rows, :], AF.Square, accum=st[:rows, 0:1])
            K.act(st[:rows, 1:2], st[:rows, 0:1], AF.Sqrt, bias=epsT[:rows, :], scale=1.0 / D)
            K.recip(st[:rows, 2:3], st[:rows, 1:2])
            K.act(xb[:rows, :], xt[:rows, :], AF.Copy, scale=st[:rows, 2:3])

        def nt_b(i):
            rows = 128 if i < 16 else NS
            col0 = i * 128
            xb = xn[i % NB]
            pb = PSB()
            pv = pb.v().re("p (k t) -> p k t", k=8)
            for kt in range(8):
                K.tr(pv[:, kt, :rows], xb[:rows, kt * 128:(kt + 1) * 128], identb[:rows, :rows])
            g0 = PV[gname]
            K.tt("dve", dst[:, :, col0:col0 + rows], pv[:, :, :rows],
                 pvec[:, g0:g0 + 8].un(2).bc([128, 8, rows]), ALU.mult)

        for step in range(17 + 2):
            if step < 17:
                nt_a(step)
            if 0 <= step - 2 < 17:
                nt_b(step - 2)
        K.pop_scope()

    def x_tile(i):
        return x_p[i * 128:(i + 1) * 128, :] if i < 16 else x_s[:, :]

    norm_transpose(x_tile, "g_pre_mix", hT)
    if stage == 0.2:
        return early()

    wring = [K.sb("wring", [128, 8, 128], BF16) for _ in range(2)]
    wr_i = [0]

    def load_w_cols(src, c0, ncols_total, kt_n=8, width=128):
        wr_i[0] += 1
        wb = wring[wr_i[0] % len(wring)]
        K.dma("pool", wb[:, :kt_n, :width],
              dap(src, c0, [[ncols_total, 128], [128 * ncols_total, kt_n], [1, width]]))
        return wb

    def proj_fm(wb, kt_n, src_act, tb, ps):
        col0, n = TBLK[tb]
        for kt in range(kt_n):
            K.mm(ps[:, :n], wb[:, kt, :], src_act[:, kt, col0:col0 + n], start=(kt == 0), stop=(kt == kt_n - 1))

    hgl_d = K.dram("hgl_scratch", [128, 4 * TT], BF16)
    TWO_PI = 2.0 * math.pi

    K.push_scope()
    ubf = K.sb("ubf", [128, 4, TT], BF16)
    wu4 = [K.sb("wu4", [128, 8, 128], BF16) for _ in range(4)]
    for ft in range(4):
        K.dma("pool", wu4[ft].v(), dap(w_in, SHIFT + ft * 128, [[PROJ, 128], [128 * PROJ, 8], [1, 128]]))
        for tb in range(5):
            col0, n = TBLK[tb]
            ps = PSF()
            proj_fm(wu4[ft], 8, hT, tb, ps)
            K.copy("act" if tb % 2 else "dve", ubf[:, ft, col0:col0 + n], ps[:, :n])

    if stage == 1.1:
        return early()
    sp_ = K.sb("s5par", [128, 16, 24], F32)
    (P_LRE, P_LIM, P_DT, P_MAG, P_TH, P_FRE, P_FIM, P_LBRE, P_LBIM, P_T0, P_T1, P_T2, P_T3,
     P_TH2, P_MAG2, P_G0R, P_G0I, P_C1, P_S1) = range(19)

    def sp(k):
        return sp_[:, :, k]

    K.dma("sp", sp(P_LRE), dap(s5_lam_re, 0, [[1, 128], [128, 16]]), slow=True)
    K.dma("sp", sp(P_LIM), dap(s5_lam_im, 0, [[1, 128], [128, 16]]), slow=True)
    for gl in range(2):
        K.dma("sp", sp_[gl * 64:(gl + 1) * 64, :, P_DT], dap(s5_log_dt, gl, [[0, 64], [2, 16]]), slow=True)
    K.act(sp(P_DT), sp(P_DT), AF.Exp)
    K.tt("dve", sp(P_T0), sp(P_LRE), sp(P_DT), ALU.mult)
    K.act(sp(P_MAG), sp(P_T0), AF.Exp)
    K.copy("dve", sp(P_MAG2), sp(P_MAG))
    for _k in range(LS - 1):
        K.tt("dve", sp(P_MAG2), sp(P_MAG2), sp(P_MAG), ALU.mult)
    K.tt("dve", sp(P_TH), sp(P_LIM), sp(P_DT), ALU.mult)
    K.ts("dve", sp(P_TH2), sp(P_TH), float(LS), None, ALU.mult)

    NH = 512 // LS
    LC = min(128, NH)
    NRC = NH // LC
    pw = K.sb("pw", [128, 16, 2 * (LS + 1)], F32)
    cosT = K.sb("cosT", [128, 16, LC], F32)
    sinT = K.sb("sinT", [128, 16, LC], F32)
    BT = [K.sb("BT", [128, 32, 128], BF16) for _ in range(LS)]
    CTb = K.sb("CTb", [128, 32, 128], BF16)
    VTb = [K.sb("VTb", [128, 32, 128], BF16) for _ in range(LS - 1)]
    Kt = [K.sb("Kt", [128, 4, 128], BF16) for _ in range(LS - 1)]
    K.push_scope()
    ji = K.sb("ji", [128, LC], I32)
    jf = K.sb("jf", [128, LC], F32)
    ang = K.sb("ang", [128, 16, LC], F32)
    rr = K.sb("rr", [128, 16, LC], F32)
    kf = K.sb("kf", [128, 16, LC], F32)
    ki = K.sb("ki", [128, 16, LC], I32)
    K.iota(ji.v(), [[1, LC]], 1, 0)
    K.copy("dve", jf.v(), ji.v())
    K.tt("dve", ang.v(), sp(P_TH2).un(2).bc([128, 16, LC]), jf.v().un(1).bc([128, 16, LC]), ALU.mult)

    def sin_reduced(out, a_in, shift, rr_, kf_, ki_):
        C1 = 6.28125
        C2 = TWO_PI - C1
        K.ts("dve", rr_, a_in, shift, 1.0 / TWO_PI, ALU.add, ALU.mult)
        K.copy("dve", ki_, rr_)
        K.copy("dve", kf_, ki_)
        K.ts("dve", rr_, a_in, shift, None, ALU.add)
        K.stt(rr_, kf_, -C1, rr_, ALU.mult, ALU.add)
        K.stt(rr_, kf_, -C2, rr_, ALU.mult, ALU.add)
        K.ts("dve", kf_, rr_, math.pi, None, ALU.is_gt)
        K.stt(rr_, kf_, -TWO_PI, rr_, ALU.mult, ALU.add)
        K.ts("dve", kf_, rr_, -math.pi, None, ALU.is_lt)
        K.stt(rr_, kf_, TWO_PI, rr_, ALU.mult, ALU.add)
        K.ts("dve", rr_, rr_, -math.pi, math.pi, ALU.max, ALU.min)
        K.act(out, rr_, AF.Sin)

    sin_reduced(sinT.v(), ang.v(), 0.0, rr.v(), kf.v(), ki.v())
    K.ts("dve", rr.v(), rr.v(), 0.5 * math.pi, None, ALU.add)
    K.ts("dve", kf.v(), rr.v(), math.pi, None, ALU.is_gt)
    K.stt(rr.v(), kf.v(), -TWO_PI, rr.v(), ALU.mult, ALU.add)
    K.ts("dve", rr.v(), rr.v(), -math.pi, math.pi, ALU.max, ALU.min)
    K.act(cosT.v(), rr.v(), AF.Sin)
    sin_reduced(sp(P_S1), sp(P_TH), 0.0, rr[:, :, 0], kf[:, :, 0], ki[:, :, 0])
    sin_reduced(sp(P_C1), sp(P_TH), 0.5 * math.pi, rr[:, :, 0], kf[:, :, 0], ki[:, :, 0])
    K.tt("dve", sp(P_LBRE), sp(P_MAG), sp(P_C1), ALU.mult)
    K.tt("dve", sp(P_LBIM), sp(P_MAG), sp(P_S1), ALU.mult)
    K.tt("dve", sp(P_T0), sp(P_LRE), sp(P_LRE), ALU.mult)
    K.tt("dve", sp(P_T1), sp(P_LIM), sp(P_LIM), ALU.mult)
    K.tt("dve", sp(P_T0), sp(P_T0), sp(P_T1), ALU.add)
    K.recip(sp(P_T0), sp(P_T0))
    K.ts("dve", sp(P_T1), sp(P_LBRE), -1.0, None, ALU.add)
    K.tt("dve", sp(P_T2), sp(P_T1), sp(P_LRE), ALU.mult)
    K.tt("dve", sp(P_T3), sp(P_LBIM), sp(P_LIM), ALU.mult)
    K.tt("dve", sp(P_T2), sp(P_T2), sp(P_T3), ALU.add)
    K.tt("dve", sp(P_FRE), sp(P_T2), sp(P_T0), ALU.mult)
    K.tt("dve", sp(P_T2), sp(P_LBIM), sp(P_LRE), ALU.mult)
    K.tt("dve", sp(P_T3), sp(P_T1), sp(P_LIM), ALU.mult)
    K.tt("dve", sp(P_T2), sp(P_T2), sp(P_T3), ALU.subtract)
    K.tt("dve", sp(P_FIM), sp(P_T2), sp(P_T0), ALU.mult)
    K.memset("dve", pw[:, :, 0], 1.0)
    K.memset("dve", pw[:, :, 1], 0.0)
    for k in range(1, LS + 1):
        pr_, pi_ = pw[:, :, 2 * (k - 1)], pw[:, :, 2 * (k - 1) + 1]
        K.tt("dve", sp(P_T0), pr_, sp(P_LBRE), ALU.mult)
        K.tt("dve", sp(P_T1), pi_, sp(P_LBIM), ALU.mult)
        K.tt("dve", pw[:, :, 2 * k], sp(P_T0), sp(P_T1), ALU.subtract)
        K.tt("dve", sp(P_T0), pr_, sp(P_LBIM), ALU.mult)
        K.tt("dve", sp(P_T1), pi_, sp(P_LBRE), ALU.mult)
        K.tt("dve", pw[:, :, 2 * k + 1], sp(P_T0), sp(P_T1), ALU.add)
    K.pop_scope()
    if stage == 1.2:
        return early()

    K.push_scope()
    CTf = [K.sb("CTf", [128, 16, 128], F32) for _ in range(2)]
    K.push_scope()
    Zc = K.sb("Zc", [128, 4 * 512], F32)
    for part, src in enumerate((s5_c_re, s5_c_im)):
        K.memset("pool", Zc.v(), 0.0)
        for g8 in range(8):
            K.dma("sp", dap(Zc, 16 * g8 * 2048 + 64 * g8, [[2048, 16], [512, 4], [1, 64]]),
                  dap(src, g8 * 1024, [[64, 16], [8192, 4], [1, 64]]))
        for q in range(4):
            ps = PSF()
            for j in range(4):
                K.tr(ps[:, j * 128:(j + 1) * 128], Zc[:, q * 512 + j * 128:q * 512 + (j + 1) * 128], identf.v())
            dst = CTf[part][:, q * 4:q * 4 + 4, :]
            if part == 0:
                K.copy("dve", dst, ps.v().re("p (j c) -> p j c", j=4))
            else:
                K.ts("dve", dst, ps.v().re("p (j c) -> p j c", j=4), -1.0, None, ALU.mult)
    K.pop_scope()
    Xr = K.sb("Xr", [128, 16 * 128], F32)
    Xi = K.sb("Xi", [128, 16 * 128], F32)
    for X, src in ((Xr, s5_b_re), (Xi, s5_b_im)):
        K.memset("pool", X.v(), 0.0)
        for gl in range(2):
            for a in range(4):
                K.dma("sp", dap(X, gl * 64 * 2048 + 16 * gl + a * 512, [[2048, 64], [160, 4], [1, 16]]),
                      dap(src, gl * 1024 + a * 8192, [[16, 64], [2048, 4], [1, 16]]), slow=True)
    c4v = CTb.v().re("p (i two) c -> p i two c", two=2)
    K.copy("act", c4v[:, :, 0, :], CTf[0].v())
    K.copy("act", c4v[:, :, 1, :], CTf[1].v())
    T1 = K.sb("T1", [128, 16, 128], F32)
    Xs = [K.sb("Xs", [128, 16, 128], F32) for _ in range(2)]
    T2 = Xs[0]
    bcv = lambda vv: vv.un(2).bc([128, 16, 128])

    def act_scale(dst, src3, g2):
        for i_ in range(16):
            K.act(dst[:, i_, :], src3[:, i_, :], AF.Copy, scale=g2[:, i_:i_ + 1])

    for jv in range(LS - 1):
        lr_, li_ = pw[:, :, 2 * (jv + 1)], pw[:, :, 2 * (jv + 1) + 1]
        v4v = VTb[jv].v().re("p (i two) c -> p i two c", two=2)
        K.tt("dve", T1.v(), CTf[0].v(), bcv(lr_), ALU.mult)
        K.tt("dve", T2.v(), CTf[1].v(), bcv(li_), ALU.mult)
        K.tt("dve", v4v[:, :, 0, :], T1.v(), T2.v(), ALU.add)
        K.tt("dve", T1.v(), CTf[1].v(), bcv(lr_), ALU.mult)
        K.tt("dve", T2.v(), CTf[0].v(), bcv(li_), ALU.mult)
        K.tt("dve", v4v[:, :, 1, :], T1.v(), T2.v(), ALU.subtract)
    gsc = K.sb("gsc", [128, 16, 4], F32)
    x3 = lambda b_: b_.v().re("p (i c) -> p i c", i=16)
    for v in range(LS):
        kpow = LS - 1 - v
        lr_, li_ = pw[:, :, 2 * kpow], pw[:, :, 2 * kpow + 1]
        K.tt("dve", gsc[:, :, 2], lr_, sp(P_FRE), ALU.mult)
        K.tt("dve", gsc[:, :, 3], li_, sp(P_FIM), ALU.mult)
        K.tt("dve", gsc[:, :, 0], gsc[:, :, 2], gsc[:, :, 3], ALU.subtract)
        K.tt("dve", gsc[:, :, 2], lr_, sp(P_FIM), ALU.mult)
        K.tt("dve", gsc[:, :, 3], li_, sp(P_FRE), ALU.mult)
        K.tt("dve", gsc[:, :, 1], gsc[:, :, 2], gsc[:, :, 3], ALU.add)
        gr, gi = bcv(gsc[:, :, 0]), bcv(gsc[:, :, 1])
        K.tt("dve", Xs[0].v(), x3(Xr), gr, ALU.mult)
        K.tt("dve", T1.v(), x3(Xi), gi, ALU.mult)
        K.tt("dve", Xs[0].v(), Xs[0].v(), T1.v(), ALU.subtract)
        K.tt("dve", Xs[1].v(), x3(Xr), gi, ALU.mult)
        K.tt("dve", T1.v(), x3(Xi), gr, ALU.mult)
        K.tt("dve", Xs[1].v(), Xs[1].v(), T1.v(), ALU.add)
        for part in range(2):
            for i4 in range(4):
                ps = PSF()
                for j in range(4):
                    K.tr(ps[:, j * 128:(j + 1) * 128], Xs[part][:, i4 * 4 + j, :], identf.v())
                dstb = BT[v].v().re("p (i two) c -> p i two c", two=2)[:, i4 * 4:i4 * 4 + 4, part, :]
                K.copy("act" if i4 % 2 else "dve", dstb, ps.v().re("p (j c) -> p j c", j=4))
        if kpow <= LS - 2:
            for q in range(4):
                ps = PSF()
                for j in range(4):
                    i = q * 4 + j
                    K.mm(ps[:, 0:128], Xs[0][:, i, :], CTf[0][:, i, :], start=(j == 0), stop=False)
                    K.mm(ps[:, 0:128], Xs[1][:, i, :], CTf[1][:, i, :], start=False, stop=(j == 3))
                K.copy("dve", Kt[kpow][:, q, :], ps[:, 0:128])
    K.pop_scope()
    if stage == 1.3:
        return early()

    x1b = [K.sb("x1sb", [128, 16, NS], BF16) for _ in range(2)]
    K.push_scope()
    s0_tm = [K.sb("s0tm", [NS, 2048], F32) for _ in range(2)]
    s0 = [K.sb("s0", [128, 16, NS], F32) for _ in range(2)]
    K.dma("sp", s0_tm[0].v(), st_re.v())
    K.dma("sp", s0_tm[1].v(), st_im.v())
    for part in range(2):
        ps = PSF()
        for i in range(16):
            K.tr(ps[:, i * NS:(i + 1) * NS], s0_tm[part][:, i * 128:(i + 1) * 128], identf[:NS, :NS])
        K.copy("dve", s0[part].v(), ps[:, :16 * NS].re("p (i n) -> p i n", i=16))
    psr = PSF()
    psi = PSF()
    for i in range(16):
        K.mm(psr[:, i * NS:(i + 1) * NS], BT[LS - 1][:, 2 * i, :], ubf[:, i // 4, T:TT])
        K.mm(psi[:, i * NS:(i + 1) * NS], BT[LS - 1][:, 2 * i + 1, :], ubf[:, i // 4, T:TT])
    sw = [K.sb("sw", [128, 16, NS], F32) for _ in range(2)]
    x1 = [K.sb("x1s", [128, 16, NS], F32) for _ in range(2)]
    bc16 = lambda k: sp(k).un(2).bc([128, 16, NS])
    pr3 = psr[:, :16 * NS].re("p (i n) -> p i n", i=16)
    pi3 = psi[:, :16 * NS].re("p (i n) -> p i n", i=16)
    K.tt("dve", sw[0].v(), s0[0].v(), bc16(P_LBRE), ALU.mult)
    K.tt("dve", sw[1].v(), s0[1].v(), bc16(P_LBIM), ALU.mult)
    K.tt("dve", sw[0].v(), sw[0].v(), sw[1].v(), ALU.subtract)
    K.tt("dve", x1[0].v(), sw[0].v(), pr3, ALU.add)
    K.tt("dve", sw[0].v(), s0[1].v(), bc16(P_LBRE), ALU.mult)
    K.tt("dve", sw[1].v(), s0[0].v(), bc16(P_LBIM), ALU.mult)
    K.tt("dve", sw[0].v(), sw[0].v(), sw[1].v(), ALU.add)
    K.tt("dve", x1[1].v(), sw[0].v(), pi3, ALU.add)
    for part in range(2):
        K.copy("act", x1b[part].v(), x1[part].v())
        xs_tm = s0_tm[part]
        for i4 in range(4):
            ps = PSF()
            for j in range(4):
                i = i4 * 4 + j
                K.tr(ps[:NS, j * 128:(j + 1) * 128], x1[part][:, i, :], identf.v())
            K.copy("dve", xs_tm[:, i4 * 512:(i4 + 1) * 512], ps[:NS, :])
        K.dma("sp", (o_re_s if part == 0 else o_im_s).v(), xs_tm.v())
    K.pop_scope()
    if stage == 1.4:
        return early()

    hgl = K.sb("hgl", [128, 4, TT], BF16)
    carry = K.sb("carry", [128, 16, 2], F32)
    K.memset("dve", carry.v(), 0.0)
    UN = [dict(bR=K.sb("bR", [128, NH], F32), bI=K.sb("bI", [128, NH], F32),
               wR=K.sb("wR", [128, NH], F32), wI=K.sb("wI", [128, NH], F32),
               t1=K.sb("t1", [128, NH], F32), t2=K.sb("t2", [128, NH], F32)) for _ in range(4)]
    xbr = [K.sb("xbr", [128, NH + 1], BF16) for _ in range(4)]
    xbi = [K.sb("xbi", [128, NH + 1], BF16) for _ in range(4)]
    yv = [K.sb("yv", [128, 512], F32) for _ in range(2)]
    y2 = [K.sb("y2", [128, 512], F32) for _ in range(2)]

    def gelu_to(dst_bf, yv_, y2_):
        K.act(y2_, yv_, AF.Square)
        K.act(y2_, y2_, AF.Identity, bias=1.0, scale=0.044715)
        K.tt("pool", y2_, y2_, yv_, ALU.mult)
        K.act(y2_, y2_, AF.Sigmoid, scale=1.5957691216057308)
        K.tt("pool", dst_bf, y2_, yv_, ALU.mult)

    vr = lambda b: b.v().re("p (a l) -> p a l", a=NRC)
    for q in range(4):
        for tb in range(4):
            col0, n = TBLK[tb]
            ugrp = ubf[:, q, col0:col0 + n].re("p (c j) -> p j c", j=LS)
            for j in range(4):
                i = q * 4 + j
                U = UN[j]
                psr = PSF()
                psi = PSF()
                for v in range(LS):
                    K.mm(psr[:, :NH], BT[v][:, 2 * i, :], ugrp[:, v, :], start=(v == 0), stop=(v == LS - 1))
                for v in range(LS):
                    K.mm(psi[:, :NH], BT[v][:, 2 * i + 1, :], ugrp[:, v, :], start=(v == 0), stop=(v == LS - 1))
                tC = cosT[:, i, :].un(1).bc([128, NRC, LC])
                tS = sinT[:, i, :].un(1).bc([128, NRC, LC])
                pr2 = psr[:, :NH].re("p (a l) -> p a l", a=NRC)
                pi2 = psi[:, :NH].re("p (a l) -> p a l", a=NRC)
                K.tt("dve", vr(U["t1"]), pr2, tC, ALU.mult)
                K.tt("dve", vr(U["t2"]), pi2, tS, ALU.mult)
                K.tt("pool", U["bR"].v(), U["t1"].v(), U["t2"].v(), ALU.add)
                K.tt("dve", vr(U["wR"]), pi2, tC, ALU.mult)
                K.tt("dve", vr(U["wI"]), pr2, tS, ALU.mult)
                K.tt("dve", U["bI"].v(), U["wR"].v(), U["wI"].v(), ALU.subtract)
                K.copy("act", xbr[j][:, 0:1], carry[:, i, 0:1])
                K.copy("act", xbi[j][:, 0:1], carry[:, i, 1:2])
            for a in range(NRC):
                cs = slice(a * LC, (a + 1) * LC)
                for j in range(4):
                    i = q * 4 + j
                    U = UN[j]
                    rho = sp_[:, i, P_MAG2:P_MAG2 + 1].bc([128, LC])
                    if a == 0:
                        ire, iim = carry[:, i, 0:1], carry[:, i, 1:2]
                    else:
                        ire, iim = U["bR"][:, a * LC - 1:a * LC], U["bI"][:, a * LC - 1:a * LC]
                    K.scan(U["wR"][:, cs], rho, U["bR"][:, cs], ire)
                    K.scan(U["wI"][:, cs], rho, U["bI"][:, cs], iim)
                for j in range(4):
                    i = q * 4 + j
                    U = UN[j]
                    K.tt("pool", U["t1"][:, cs], U["wR"][:, cs], cosT[:, i, :], ALU.mult)
                    K.tt("dve", U["t2"][:, cs], U["wI"][:, cs], sinT[:, i, :], ALU.mult)
                    K.tt("dve", U["bR"][:, cs], U["t1"][:, cs], U["t2"][:, cs], ALU.subtract)
                    K.tt("pool", U["t1"][:, cs], U["wR"][:, cs], sinT[:, i, :], ALU.mult)
                    K.tt("dve", U["t2"][:, cs], U["wI"][:, cs], cosT[:, i, :], ALU.mult)
                    K.tt("dve", U["bI"][:, cs], U["t1"][:, cs], U["t2"][:, cs], ALU.add)
            for j in range(4):
                i = q * 4 + j
                U = UN[j]
                K.copy("dve", carry[:, i, 0:1], U["bR"][:, NH - 1:NH])
                K.copy("dve", carry[:, i, 1:2], U["bI"][:, NH - 1:NH])
                K.copy("act", xbr[j][:, 1:NH + 1], U["bR"].v())
                K.copy("act", xbi[j][:, 1:NH + 1], U["bI"].v())
            yy = yv[tb % 2]
            yyp = yy.v().re("p (j c) -> p j c", j=LS)
            for jo in range(LS):
                psy = PSF()
                if jo == LS - 1:
                    for j in range(4):
                        i = q * 4 + j
                        K.mm(psy[:, :NH], CTb[:, 2 * i, :], xbr[j][:, 1:NH + 1], start=(j == 0), stop=False)
                        K.mm(psy[:, :NH], CTb[:, 2 * i + 1, :], xbi[j][:, 1:NH + 1], start=False, stop=(j == 3))
                else:
                    for j in range(4):
                        i = q * 4 + j
                        K.mm(psy[:, :NH], VTb[jo][:, 2 * i, :], xbr[j][:, 0:NH], start=(j == 0), stop=False)
                        K.mm(psy[:, :NH], VTb[jo][:, 2 * i + 1, :], xbi[j][:, 0:NH], start=False, stop=False)
                    for ii in range(jo + 1):
                        K.mm(psy[:, :NH], Kt[jo - ii][:, q, :], ugrp[:, ii, :], start=False, stop=(ii == jo))
                K.copy("act", yyp[:, jo, :], psy[:, :NH])
            psu = PSF()
            proj_fm(wu4[q], 8, hT, tb, psu)
            K.stt(yyp, psu.v().re("p (c j) -> p j c", j=LS), pcol("s5_d", q), yyp, ALU.mult, ALU.add)
            y2p = y2[tb % 2].v().re("p (j c) -> p j c", j=LS)
            gelu_to(hgl[:, q, col0:col0 + n].re("p (c j) -> p j c", j=LS), yyp, y2p)
        psy = PSF()
        for j in range(4):
            i = q * 4 + j
            K.mm(psy[:, :NS], CTb[:, 2 * i, :], x1b[0][:, i, :], start=(j == 0), stop=False)
            K.mm(psy[:, :NS], CTb[:, 2 * i + 1, :], x1b[1][:, i, :], start=False, stop=(j == 3))
        K.copy("act", yv[0][:, :NS], psy[:, :NS])
        psu = PSF()
        proj_fm(wu4[q], 8, hT, 4, psu)
        K.stt(yv[0][:, :NS], psu[:, :NS], pcol("s5_d", q), yv[0][:, :NS], ALU.mult, ALU.add)
        gelu_to(hgl[:, q, T:TT], yv[0][:, :NS], y2[0][:, :NS])
    if stage == 1.5:
        return early()
    K.dma("sp", dap(o_re_p, 0, [[1, 128], [128, 16]]), carry[:, :, 0], slow=True)
    K.dma("sp", dap(o_im_p, 0, [[1, 128], [128, 16]]), carry[:, :, 1], slow=True)
    K.dma("sp", hgl_d.v(), hgl.v().re("p k t -> p (k t)"))
    K.pop_scope()
    if stage == 2:
        zt = K.sb("zt", [128, 4096], F32)
        K.memset("pool", zt.v(), 0.0)
        K.dma("sp", o_shift_p.v(), zt[0:1, 0:SHIFT])
        K.dma("sp", o_shift_s.v(), zt[0:NS, 0:SHIFT])
        K.dma("sp", o_wkv_p.v().re("h v k -> h (v k)"), zt[0:8, :])
        K.dma("sp", o_wkv_s.v(), zt[:, :])
        for i in range(16):
            K.dma("sp", y_p[i * 128:(i + 1) * 128, :], zt[:, 0:D])
        K.dma("sp", y_s.v(), zt[0:NS, 0:D])
        return early()
    K.push_scope()
    lnxg = K.sb("lnxg", [128, RW], F32)
    lnxb = K.sb("lnxb", [128, RW], F32)
    K.dma("sp", lnxg.v(), dap(lnx_g, 0, [[0, 128], [1, RW]]))
    K.dma("sp", lnxb.v(), dap(lnx_b, 0, [[0, 128], [1, RW]]))
    par = K.sb("spar", [128, 3, 64], F32)
    for tk in range(NS):
        for j, src in enumerate((lnx_g, lnx_b, r_k)):
            K.dma("sp", par[tk * 8:(tk + 1) * 8, j, :], dap(src, 0, [[64, 8], [1, 64]]), sem_buf=par)

    lora_bf = K.sb("lora_bf", [128, TT], BF16)
    lora_up = K.sb("lora_up", [128, RW], BF16)
    oaT = K.sb("oaT", [128, 4, TT], BF16)
    o_d = K.dram("o_scratch", [T, RW], F32)
    bonus_d = K.dram("bonus_scratch", [T, RW], F32)
    samp_d = K.dram("samp_scratch", [NS * 8, 7, 64], F32)
    sampo_d = K.dram("sampo_scratch", [NS, RW], F32)

    K.push_scope()
    mask4 = K.sb("mask4", [128, 4, 128], F32)
    maskNT = K.sb("maskNT", [128, 128], F32)
    resetm = K.sb("resetm", [128, T], BF16)
    blk1 = K.sb("blk1", [128, 128], F32)
    eps_ln = K.sb("eps_ln", [128, 1], F32)
    K.memset("pool", eps_ln.v(), 64e-5)
    for j in range(4):
        K.aselect(mask4[:, j, :], ones_f.v(), [[1, 128]], ALU.is_gt if j % 2 == 0 else ALU.is_ge, 0.0, 0, -1)
    K.aselect(maskNT.v(), ones_f.v(), [[-1, 128]], ALU.is_gt, 0.0, 0, 1)
    K.memset("pool", resetm.v(), 1.0)
    K.memset("pool", resetm.v().re("p (c l) -> p c l", l=128)[:, :, 0:1], 0.0)
    K.memset("pool", blk1.v(), 0.0)
    K.memset("pool", blk1[0:64, 0:64], 1.0)
    K.memset("pool", blk1[64:128, 64:128], 1.0)

    K.dma("pool", lora_up[0:32, :], w_decay_up.v())
    K.dma("pool", lora_up[32:64, :], w_aaa_up.v())
    K.dma("pool", lora_up[64:128, :], w_gate_up.v())
    sh0T = K.sb("sh0T", [128, 13, NS], F32)
    shout = [K.sb("shout", [NS + 1, 128], F32) for _ in range(1)]
    K.push_scope()
    sh_tm = K.sb("sh_tm", [NS, SHIFT], F32)
    K.dma("sp", sh_tm.v(), st_shift.v())
    ps = PSF()
    for ft in range(13):
        K.tr(ps[:, ft * NS:(ft + 1) * NS], sh_tm[:, ft * 128:(ft + 1) * 128], identf[:NS, :NS])
    K.copy("dve", sh0T.v(), ps[:, :13 * NS].re("p (f n) -> p f n", f=13))
    K.pop_scope()

    omm = K.sb("omm", [128, 13], F32)
    K.ts("dve", omm.v(), pvec[:, PV["mu"]:PV["mu"] + 13], -1.0, 1.0, ALU.mult, ALU.add)
    omka = K.sb("omka", [128, 4], F32)
    K.ts("dve", omka.v(), pvec[:, PV["k_a"]:PV["k_a"] + 4], -1.0, 1.0, ALU.mult, ALU.add)

    SC = [K.sb("scr", [128, TT], F32) for _ in range(6)]

    def proj_shift_tile(ft, pr, tmp, xr_out, xr_out_dt_bf=None, samp32=None):
        wb = load_w_cols(w_in, ft * 128, PROJ)
        for tb in range(5):
            col0, n = TBLK[tb]
            ps = PSF()
            proj_fm(wb, 8, hT, tb, ps)
            K.copy("act", pr[:, col0:col0 + n], ps[:, :n])
            K.act(tmp[:, col0:col0 + n], ps[:, :n], AF.Copy, scale=omm[:, ft:ft + 1])
        ps = PSF()
        K.tr(ps[:NS + 1, :128], pr[:, T - 1:TT], identf.v())
        so = shout[0]
        K.copy("dve", so.v(), ps[:NS + 1, :128])
        K.dma("sp", o_shift_p[0:1, ft * 128:(ft + 1) * 128], so[0:1, :])
        K.dma("sp", o_shift_s[:, ft * 128:(ft + 1) * 128], so[1:NS + 1, :])
        mu_c = pcol("mu", ft)
        K.stt(xr_out[:, 1:T], pr[:, 0:T - 1], mu_c, tmp[:, 1:T], ALU.mult, ALU.add)
        K.copy("pool", xr_out[:, 0:1], tmp[:, 0:1])
        K.stt(xr_out[:, T:TT], sh0T[:, ft, :], mu_c, tmp[:, T:TT], ALU.mult, ALU.add)
        if samp32 is not None:
            K.stt(samp32, sh0T[:, ft, :], mu_c, tmp[:, T:TT], ALU.mult, ALU.add)

    proj_shift_tile(12, SC[0], SC[1], SC[2].v())
    K.act(lora_bf[0:32, :], SC[2][0:32, :], AF.Tanh)
    K.act(lora_bf[32:64, :], SC[2][32:64, :], AF.Copy)
    K.act(lora_bf[64:128, :], SC[2][64:128, :], AF.Sigmoid)

    vbf = K.sb("vbf", [128, TT], BF16)
    vs32 = K.sb("vs32", [128, NS], F32)
    AR = K.sb("AR", [128, 16, 2, 128], BF16)
    bti = K.sb("bti", [128, T], BF16)
    kti = K.sb("kti", [128, T], BF16)
    bhat = K.sb("bhat", [128, T], BF16)
    khat = K.sb("khat", [128, T], BF16)
    Wfm = bhat
    TMq = K.sb("TMq", [128, 16, 4, 128], BF16)
    gL = K.sb("gL", [128, 16], F32)
    Sf = K.sb("Sf", [128, 64], F32)
    Sb = K.sb("Sb", [128, 64], BF16)
    smp = K.sb("smp", [NS, 3, 128], F32)
    decs = K.sb("decs", [128, NS], F32)
    avs = K.sb("avs", [128, NS], F32)
    NPR = 16
    ABK = [K.sb("ABK", [128, 2, 128], BF16) for _ in range(NPR)]
    NPQ = 8
    ATMP = [K.sb("ATMP", [128, 2, 128], BF16) for _ in range(NPQ)]
    P0T = [K.sb("P0T", [128, 128], BF16) for _ in range(NPQ)]
    PMb = [[K.sb("PMb", [128, 3, 128], BF16) for _ in range(2)] for _ in range(NPQ)]
    Ybf = [K.sb("Ybf", [128, 64], BF16) for _ in range(NPQ)]
    QTb = [[K.sb("QTb", [128, 128], BF16) for _ in range(2)] for _ in range(NPQ)]
    Ubar = [K.sb("Ubar", [128, 64], F32) for _ in range(NPR)]
    Ut = [K.sb("Ut", [128, 64], BF16) for _ in range(4)]
    Otile = [K.sb("Otile", [128, 128], F32) for _ in range(2)]
    btile = [K.sb("btile", [128, 512], F32) for _ in range(1)]
    wkvo = K.sb("wkvo", [64, 128], F32)

    def derivedA(hp):
        xr_r, xr_k = SC[0], SC[1]
        proj_shift_tile(0 * 4 + hp, SC[2], SC[3], xr_r.v())
        yield
        proj_shift_tile(1 * 4 + hp, SC[2], SC[3], xr_k.v())
        yield
        proj_shift_tile(2 * 4 + hp, SC[2], SC[3], vbf.v(), samp32=vs32.v())
        yield
        sigd, csum, afm, kkn = SC[2], SC[3], SC[4], SC[5]
        hc = slice(hp * 128, (hp + 1) * 128)
        for tb in range(5):
            col0, n = TBLK[tb]
            ps = PSF()
            K.mm(ps[:, :n], lora_up[0:32, hc], lora_bf[0:32, col0:col0 + n])
            K.act(sigd[:, col0:col0 + n], ps[:, :n], AF.Sigmoid, bias=pcol("w0", hp))
            yield
            ps = PSF()
            K.mm(ps[:, :n], lora_up[32:64, hc], lora_bf[32:64, col0:col0 + n])
            K.act(afm[:, col0:col0 + n], ps[:, :n], AF.Sigmoid, bias=pcol("a0", hp))
            yield
        K.act(kkn.v(), xr_k.v(), AF.Square, scale=pcol("k_k", hp))
        yield
        for tb in range(5):
            col0, n = TBLK[tb]
            ps = PSF()
            K.mm(ps[:, :n], blk1.v(), kkn[:, col0:col0 + n])
            K.ts("dve", csum[:, col0:col0 + n], ps[:, :n], 1e-24, None, ALU.max)
            yield
        K.act(csum.v(), csum.v(), AF.Ln)
        yield
        K.act(csum.v(), csum.v(), AF.Exp, scale=-0.5)
        yield
        K.stt(kkn.v(), xr_k.v(), pcol("k_k", hp), csum.v(), ALU.mult, ALU.mult)
        yield
        K.ts("dve" if hp == 0 else "pool", csum.v(), afm.v(), pcol("k_a", hp), omka[:, hp:hp + 1], ALU.mult, ALU.add)
        yield
        K.tt("dve" if hp == 0 else "pool", xr_k.v(), xr_k.v(), csum.v(), ALU.mult)
        yield
        kmod = xr_k
        K.tt("dve" if hp == 0 else "pool", afm.v(), kkn.v(), afm.v(), ALU.mult)
        yield
        bfm = afm
        K.stt(csum.v(), xr_r.v(), pcol("r_k", hp), kmod.v(), ALU.mult, ALU.mult)
        yield
        for tb in range(4):
            col0, n = TBLK[tb]
            ps = PSF()
            K.mm(ps[:, :n], blk1.v(), csum[:, col0:col0 + n])
            K.tt("dve", csum[:, col0:col0 + n], ps[:, :n], vbf[:, col0:col0 + n], ALU.mult)
            yield
        for c4 in range(4):
            ps = PSF()
            for j in range(4):
                c = c4 * 4 + j
                K.tr(ps[:, j * 128:(j + 1) * 128], csum[:, c * 128:(c + 1) * 128], identf.v())
            bt = btile[0]
            K.copy("act", bt.v(), ps.v())
            yield
            K.dma("sp", dap(bonus_d, c4 * 512 * RW + hp * 128, [[RW, 128], [128 * RW, 4], [1, 128]]),
                  bt.v().re("p (j f) -> p j f", j=4))
            yield
        K.act(decs.v(), sigd[:, T:TT], AF.Exp, scale=-CDEC)
        yield
        K.ts("dve", avs.v(), kkn[:, T:TT], -1.0, None, ALU.mult)
        yield
        yield

    def derivedB(hp):
        xr_r, xr_k = SC[0], SC[1]
        sigd, csum, afm, kkn = SC[2], SC[3], SC[4], SC[5]
        kmod, bfm = xr_k, afm
        K.scan(csum[:, 0:T], resetm.v(), sigd[:, 0:T], 0.0)
        gt = sigd
        c3 = lambda b_: b_[:, 0:T].re("p (c l) -> p c l", l=128)
        K.act(gt[:, 0:T], csum[:, 0:T], AF.Exp, scale=-CDEC)
        K.tt("dve", AR[:, :, 1, :], c3(xr_r), c3(gt), ALU.mult)
        K.copy("dve", gL.v(), c3(gt)[:, :, 127])
        K.stt(AR[:, :, 0, 1:128], c3(kkn)[:, :, 1:128], -1.0, c3(gt)[:, :, 0:127], ALU.mult, ALU.mult)
        K.ts("dve", AR[:, :, 0, 0], c3(kkn)[:, :, 0], -1.0, None, ALU.mult)
        K.act(gt[:, 0:T], csum[:, 0:T], AF.Exp, scale=CDEC)
        K.tt("dve", bti.v(), bfm[:, 0:T], gt[:, 0:T], ALU.mult)
        K.tt("dve", kti.v(), kmod[:, 0:T], gt[:, 0:T], ALU.mult)
        K.tt("dve", c3(gt), c3(csum)[:, :, 127:128].bc([128, 16, 128]), c3(csum), ALU.subtract)
        K.act(gt[:, 0:T], gt[:, 0:T], AF.Exp, scale=-CDEC)
        K.tt("dve", bhat.v(), bfm[:, 0:T], gt[:, 0:T], ALU.mult)
        K.tt("dve", khat.v(), kmod[:, 0:T], gt[:, 0:T], ALU.mult)
        for c in range(16):
            pb = PSB()
            pv = pb.v().re("p (q t) -> p q t", q=8)
            cs = slice(c * 128, (c + 1) * 128)
            K.tr(pv[:, 0, :], vbf[:, cs], identb.v())
            K.tr(pv[:, 1, :], bhat[:, cs], identb.v())
            K.tr(pv[:, 2, :], khat[:, cs], identb.v())
            K.tr(pv[:, 3, :], AR[:, c, 0, :], identb.v())
            K.copy("act" if c % 2 else "dve", TMq[:, c, :, :], pv[:, 0:4, :])
        sq_src = [xr_r[:, T:TT], kmod[:, T:TT], vs32.v(), decs.v(), avs.v(), bfm[:, T:TT]]
        for half in range(2):
            ps = PSF()
            for j in range(3):
                K.tr(ps[:NS, j * 128:(j + 1) * 128], sq_src[half * 3 + j], identf.v())
            K.copy("dve", smp.v(), ps[:NS, :384].re("p (q f) -> p q f", q=3))
            for j in range(3):
                q = half * 3 + j
                K.dma("sp", dap(samp_d, (2 * hp) * 448 + q * 64, [[8 * 448, NS], [448, 2], [1, 64]]),
                      smp[:, j, :].re("p (h n) -> p h n", h=2))


    def chunk_thunks(hp):
        CH = []
        CH.append(lambda: K.memset('dve', Sf.v(), 0.0))
        CH.append(lambda: K.memset('pool', Sb.v(), 0.0))
        pairs = [(c, hl) for c in range(16) for hl in range(2)]

        def pre_a(sl, c, hl):
            rows = slice(hl * 64, (hl + 1) * 64)
            cs = slice(c * 128, (c + 1) * 128)
            ps = PSF()
            arv = AR[rows, c, :, :].re("p a t -> p (a t)")
            K.mm(ps[:, 0:256], bti[rows, cs], arv)
            K.mm(ps[:, 256:512], kti[rows, cs], arv)
            ps2 = PSF()
            K.mm(ps2[:, 0:128], AR[rows, c, 0, :], bti[rows, cs])
            p4 = ps.v().re("p (q t) -> p q t", q=4)
            K.tt("dve", ATMP[sl % NPQ].v(), p4[:, 0::2, :], mask4[:, 0::2, :], ALU.mult)
            K.tt("dve", ABK[sl].v(), p4[:, 1::2, :], mask4[:, 1::2, :], ALU.mult)
            K.tt("dve", P0T[sl % NPQ].v(), ps2[:, 0:128], maskNT.v(), ALU.mult)
            K.tt("pool", QTb[sl % NPQ][0].v(), P0T[sl % NPQ].v(), identb.v(), ALU.add)

        def pre_lev(sl, lev, gi):
            if lev == 0:
                Pj, PjT, Mp = ATMP[sl % NPQ][:, 0, :], P0T[sl % NPQ].v(), identb.v()
            else:
                src = PMb[sl % NPQ][lev % 2]
                Pj, PjT, Mp = src[:, 0, :], src[:, 1, :], src[:, 2, :]
            dst = PMb[sl % NPQ][(lev + 1) % 2]
            ps = PSF()
            if lev < 6:
                K.mm(ps[:, 0:128], PjT, Pj)
                K.mm(ps[:, 128:256], Pj, PjT)
            K.mm(ps[:, 256:384], QTb[sl % NPQ][lev % 2].v(), Mp, start=True, stop=True)
            eng = "act" if (gi % 8) not in (2, 5, 7) else "dve"
            if lev < 6:
                K.copy(eng, dst.v(), ps[:, 0:384].re("p (q t) -> p q t", q=3))
                K.tt("pool", QTb[sl % NPQ][(lev + 1) % 2].v(), dst[:, 1, :], identb.v(), ALU.add)
            else:
                K.copy(eng, dst[:, 2, :], ps[:, 256:384])

        def pre_w(sl, c, hl, gi):
            rows = slice(hl * 64, (hl + 1) * 64)
            cs = slice(c * 128, (c + 1) * 128)
            Mfin = PMb[sl % NPQ][1][:, 2, :]
            ps = PSF()
            K.mm(ps[:, 0:128], TMq[:, c, 3, :], Mfin)
            K.mm(ps[:, 128:192], ATMP[sl % NPQ][:, 1, :], TMq[:, c, 0, rows])
            eng = "act" if gi % 2 else "dve"
            K.copy(eng, Wfm[rows, cs], ps[rows, 0:128])
            K.copy(eng, Ybf[sl % NPQ].v(), ps[:, 128:192])

        def pre_u(sl, gi):
            Mfin = PMb[sl % NPQ][1][:, 2, :]
            ps = PSF()
            K.mm(ps[:, 0:64], Mfin, Ybf[sl % NPQ].v())
            K.copy("act" if gi % 2 else "dve", Ubar[sl].v(), ps[:, 0:64])

        def precompute(group):
            th = []
            for (sl, c, hl) in group:
                th.append(partial(pre_a, sl, c, hl))
            for lev in range(7):
                for gi, (sl, c, hl) in enumerate(group):
                    th.append(partial(pre_lev, sl, lev, gi))
            for gi, (sl, c, hl) in enumerate(group):
                th.append(partial(pre_w, sl, c, hl, gi))
            for gi, (sl, c, hl) in enumerate(group):
                th.append(partial(pre_u, sl, gi))
            return th

        def ser_u(sl, c, hl, gi):
            rows = slice(hl * 64, (hl + 1) * 64)
            cs = slice(c * 128, (c + 1) * 128)
            ut = Ut[gi % 4]
            ps = PSF()
            K.mm(ps[:, 0:64], Wfm[rows, cs], Sb[rows, :])
            K.tt("dve", ut.v(), ps[:, 0:64], Ubar[sl].v(), ALU.add)

        def ser_os(sl, c, hl, gi):
            rows = slice(hl * 64, (hl + 1) * 64)
            ut = Ut[gi % 4]
            pso = PSF()
            K.mm(pso[:, 0:64], AR[rows, c, 1, :], Sb[rows, :], start=True, stop=False)
            K.mm(pso[:, 0:64], ABK[sl][:, 0, :], ut.v(), start=False, stop=False)
            K.mm(pso[:, 0:64], ABK[sl][:, 1, :], TMq[:, c, 0, rows], start=False, stop=True)
            pss = PSF()
            K.mm(pss[:, 0:64], TMq[:, c, 1, :], ut.v(), start=True, stop=False)
            K.mm(pss[:, 0:64], TMq[:, c, 2, :], TMq[:, c, 0, rows], start=False, stop=True)
            K.stt(Sf[rows, :], Sf[rows, :], gL[rows, c:c + 1], pss[rows, 0:64], ALU.mult, ALU.add)
            K.copy("act", Sb[rows, :], Sf[rows, :])
            ot = Otile[c % 2]
            K.copy("act", ot[:, rows], pso[:, 0:64])
            if hl == 1:
                K.dma("sp", o_d[c * 128:(c + 1) * 128, hp * 128:(hp + 1) * 128], ot.v())

        def serial(group):
            th = []
            for gi, (sl, c, hl) in enumerate(group):
                th.append(partial(ser_u, sl, c, hl, gi))
                th.append(partial(ser_os, sl, c, hl, gi))
            return th

        def run_merged(lists):
            idx = [0] * len(lists)
            while True:
                best, bf = -1, 2.0
                for li, L in enumerate(lists):
                    if idx[li] < len(L):
                        frac = idx[li] / len(L)
                        if frac < bf:
                            best, bf = li, frac
                if best < 0:
                    break
                lists[best][idx[best]]()
                idx[best] += 1

        G = 8
        groups = []
        for g0 in range(0, 32, G):
            groups.append([((g0 + i) % NPR, pairs[g0 + i][0], pairs[g0 + i][1]) for i in range(G)])
        CH.extend(precompute(groups[0]))
        for gi_ in range(len(groups)):
            lists = [serial(groups[gi_])]
            if gi_ + 1 < len(groups):
                lists.append(precompute(groups[gi_ + 1]))
            CH.extend(merge_lists(lists))
        def fin_state():
            ps = PSF()
            K.tr(ps[:64, 0:128], Sf.v(), identf.v())
            K.copy("dve", wkvo.v(), ps[:64, 0:128])
            K.dma("sp", o_wkv_p[2 * hp:2 * hp + 2, :, :].re("h v k -> v h k"), wkvo.v().re("v (h k) -> v h k", h=2))

        CH.append(fin_state)
        return CH

    def drain(gen):
        for _ in gen:
            pass

    NA_EST = 150
    drain(derivedA(0))
    derivedB(0)
    for hp in range(4):
        CH = chunk_thunks(hp)
        OUTL = []
        gen = derivedA(hp + 1) if hp + 1 < 4 else None
        per = max(1, len(CH) // NA_EST)
        pero = max(1, len(CH) // max(1, len(OUTL)))
        oi = 0
        for ti, th in enumerate(CH):
            th()
            if gen is not None and ti % per == per - 1:
                try:
                    next(gen)
                except StopIteration:
                    gen = None
            if oi < len(OUTL) and ti % pero == pero - 1:
                OUTL[oi]()
                oi += 1
        while oi < len(OUTL):
            OUTL[oi]()
            oi += 1
        if gen is not None:
            drain(gen)
        if hp + 1 < 4:
            derivedB(hp + 1)

    K.pop_scope()
    if stage == 3:
        zt = K.sb("zt", [128, 4096], F32)
        K.memset("pool", zt.v(), 0.0)
        K.dma("sp", o_wkv_s.v(), zt[:, :])
        for i in range(16):
            K.dma("sp", y_p[i * 128:(i + 1) * 128, :], zt[:, 0:D])
        K.dma("sp", y_s.v(), zt[0:NS, 0:D])
        return early()

    def bcast_row(name, src, n):
        t = K.sb(name, [128, n], F32)
        K.dma("sp", t.v(), dap(src, 0, [[0, 128], [1, n]]))
        return t

    K.push_scope()
    hgl = K.sb("hgl2", [128, 4, TT], BF16)
    wmo = K.sb("wmo", [128, 8, D], BF16)
    gpost = K.sb("gpost_mix", [128, D], F32)
    PREF = [lambda: K.dma("sp", gpost.v(), dap(g_post_mix, 0, [[0, 128], [1, D]])),
            lambda: K.dma("sp", hgl.v().re("p k t -> p (k t)"), hgl_d.v())]
    for kt in range(8):
        PREF.append(partial(lambda kt_: K.dma("pool", wmo[:, kt_, :], w_merge_out[kt_ * 128:(kt_ + 1) * 128, :]), kt))
    mring = [K.sb("mring", [128, 8, 128], BF16) for _ in range(10)]
    mr_i = [0]

    def mload(src, c0, ncols_total, kt_n):
        mr_i[0] += 1
        wb = mring[mr_i[0] % len(mring)]
        K.dma("pool", wb[:, :kt_n, :], dap(src, c0, [[ncols_total, 128], [128 * ncols_total, kt_n], [1, 128]]))
        return wb

    GOFF = SHIFT + RW
    mw = {}

    def mg_load(f):
        mw[f] = (mload(w_rwkv_out, f * 128, D, 4), mload(w_in, GOFF + f * 128, PROJ, 8),
                 mload(w_in, GOFF + D + f * 128, PROJ, 8), mload(glu_w1, f * 128, D, 4), mload(glu_w2, f * 128, D, 4))

    PREF.append(partial(mg_load, 0))
    K.push_scope()
    eps_ln = K.sb("eps_ln2", [128, 1], F32)
    K.memset("pool", eps_ln.v(), 64e-5)
    NORF = 3
    fot_r = [K.sb("fot", [128, RW], F32) for _ in range(NORF)]
    fbt_r = [K.sb("fbt", [128, RW], F32) for _ in range(NORF)]
    fon_r = [K.sb("fon", [128, RW], F32) for _ in range(NORF)]
    fofb = [K.sb("fofb", [128, RW], BF16) for _ in range(NORF)]
    fst8 = [K.sb("fst8", [128, 6, 8], F32) for _ in range(NORF)]
    fh3 = lambda b_: b_.v().re("p (h n) -> p h n", h=8)

    def fout_a(c):
        r_ = c % NORF
        ot, bt, tmp, s8 = fot_r[r_], fbt_r[r_], fon_r[r_], fst8[r_]
        K.dma("sp", ot.v(), o_d[c * 128:(c + 1) * 128, :])
        K.dma("sp", bt.v(), bonus_d[c * 128:(c + 1) * 128, :])
        K.reduce(s8[:, 0, :], fh3(ot))
        K.act(tmp.v(), ot.v(), AF.Square)
        K.reduce(s8[:, 1, :], fh3(tmp))
        K.ts("dve", s8[:, 2, :], s8[:, 0, :], 1.0 / 64, None, ALU.mult)
        K.tt("dve", s8[:, 3, :], s8[:, 2, :], s8[:, 2, :], ALU.mult)
        K.stt(s8[:, 4, :], s8[:, 1, :], 1.0 / 64, s8[:, 3, :], ALU.mult, ALU.subtract)
        K.act(s8[:, 4, :], s8[:, 4, :], AF.Sqrt, bias=eps_ln.v(), scale=1.0)
        K.recip(s8[:, 5, :], s8[:, 4, :])
        K.stt(s8[:, 3, :], s8[:, 2, :], -1.0, s8[:, 5, :], ALU.mult, ALU.mult)

    def fout_b(c):
        r_ = c % NORF
        ot, bt, on, of, s8 = fot_r[r_], fbt_r[r_], fon_r[r_], fofb[r_], fst8[r_]
        for h in range(8):
            K.act(on[:, h * 64:(h + 1) * 64], ot[:, h * 64:(h + 1) * 64], AF.Identity,
                  bias=s8[:, 3, h:h + 1], scale=s8[:, 5, h:h + 1])
        K.tt("pool", bt.v(), bt.v(), lnxb.v(), ALU.add)
        K.tt("dve", on.v(), on.v(), lnxg.v(), ALU.mult)
        K.tt("dve", on.v(), on.v(), bt.v(), ALU.add)
        ps = PSF()
        K.mm(ps.v(), lora_bf[64:128, c * 128:(c + 1) * 128], lora_up[64:128, :])
        K.tt("dve", of.v(), on.v(), ps.v(), ALU.mult)
        pb = PSB()
        pv = pb.v().re("p (q t) -> p q t", q=8)
        for kt in range(4):
            K.tr(pv[:, kt, :], of[:, kt * 128:(kt + 1) * 128], identb.v())
        K.copy("act", oaT[:, :, c * 128:(c + 1) * 128], pv[:, 0:4, :])

    OUT = []
    for c in range(16 + 2):
        if c < 16:
            OUT.append(partial(fout_a, c))
        if 0 <= c - 2 < 16:
            OUT.append(partial(fout_b, c - 2))

    gs_tm = K.sb("gs_tm", [NS, RW], F32)
    ps = PSF()
    K.mm(ps[:NS, :], lora_bf[64:128, T:TT], lora_up[64:128, :])
    K.copy("act", gs_tm.v(), ps[:NS, :])
    K.dma("sp", dap(samp_d, 6 * 64, [[8 * 448, NS], [448, 8], [1, 64]]), gs_tm.v().re("p (h n) -> p h n", h=8))
    vec = K.sb("vec", [128, 7, 64], F32)
    K.dma("sp", vec.v().re("p q n -> p (q n)"), dap(samp_d, 0, [[448, 128], [1, 448]]))
    S0s = K.sb("S0s", [128, 64, 64], F32)
    S1s = K.sb("S1s", [128, 64, 64], F32)
    tS = K.sb("tS", [128, 64, 64], F32)
    K.dma("sp", S0s.v().re("p v k -> p (v k)"), st_wkv.v())
    r_, km_, v_, dec_, av_, b_, g_ = [vec[:, q, :] for q in range(7)]
    overv = lambda x: x.un(1).bc([128, 64, 64])
    overk = lambda x: x.un(2).bc([128, 64, 64])
    sv = K.sb("sv", [128, 8, 64], F32)
    ss = K.sb("ss", [128, 8], F32)
    eps2 = K.sb("eps_ln3", [128, 1], F32)
    K.memset("pool", eps2.v(), 64e-5)
    os_tm = K.sb("os_tm", [NS, RW], F32)
    os_bf = K.sb("os_bf", [NS, RW], BF16)
    SMP = [
        lambda: K.tt("dve", tS.v(), S0s.v(), overv(av_), ALU.mult),
        lambda: K.reduce(sv[:, 0, :], tS.v()),
        lambda: K.tt("pool", S1s.v(), S0s.v(), overv(dec_), ALU.mult),
        lambda: K.tt("dve", tS.v(), overk(sv[:, 0, :]), overv(b_), ALU.mult),
        lambda: K.tt("dve", S1s.v(), S1s.v(), tS.v(), ALU.add),
        lambda: K.tt("dve", tS.v(), overk(v_), overv(km_), ALU.mult),
        lambda: K.tt("dve", S1s.v(), S1s.v(), tS.v(), ALU.add),
        lambda: K.dma("sp", o_wkv_s.v(), S1s.v().re("p v k -> p (v k)")),
        lambda: K.tt("dve", tS.v(), S1s.v(), overv(r_), ALU.mult),
        lambda: K.reduce(sv[:, 1, :], tS.v()),
        lambda: K.reduce(ss[:, 0:1], sv[:, 1, :]),
        lambda: K.tt("dve", sv[:, 2, :], sv[:, 1, :], sv[:, 1, :], ALU.mult),
        lambda: K.reduce(ss[:, 1:2], sv[:, 2, :]),
        lambda: K.ts("dve", ss[:, 2:3], ss[:, 0:1], 1.0 / 64, None, ALU.mult),
        lambda: K.tt("dve", ss[:, 3:4], ss[:, 2:3], ss[:, 2:3], ALU.mult),
        lambda: K.stt(ss[:, 4:5], ss[:, 1:2], 1.0 / 64, ss[:, 3:4], ALU.mult, ALU.subtract),
        lambda: K.act(ss[:, 4:5], ss[:, 4:5], AF.Sqrt, bias=eps2.v(), scale=1.0),
        lambda: K.recip(ss[:, 5:6], ss[:, 4:5]),
        lambda: K.ts("dve", sv[:, 3, :], sv[:, 1, :], ss[:, 2:3], ss[:, 5:6], ALU.subtract, ALU.mult),
        lambda: K.tt("dve", sv[:, 3, :], sv[:, 3, :], par[:, 0, :], ALU.mult),
        lambda: K.tt("dve", sv[:, 3, :], sv[:, 3, :], par[:, 1, :], ALU.add),
        lambda: K.tt("dve", sv[:, 4, :], r_, km_, ALU.mult),
        lambda: K.tt("dve", sv[:, 4, :], sv[:, 4, :], par[:, 2, :], ALU.mult),
        lambda: K.reduce(ss[:, 6:7], sv[:, 4, :]),
        lambda: K.stt(sv[:, 3, :], v_, ss[:, 6:7], sv[:, 3, :], ALU.mult, ALU.add),
        lambda: K.tt("dve", sv[:, 5, :], sv[:, 3, :], g_, ALU.mult),
        lambda: K.dma("sp", dap(sampo_d, 0, [[64, 128], [1, 64]]), sv[:, 5, :]),
        lambda: K.dma("sp", os_tm.v(), sampo_d.v()),
        lambda: K.copy("dve", os_bf.v(), os_tm.v()),
    ]

    def smp_fin():
        pb = PSB()
        pv = pb.v().re("p (q t) -> p q t", q=8)
        for kt in range(4):
            K.tr(pv[:, kt, :NS], os_bf[:, kt * 128:(kt + 1) * 128], identb[:NS, :NS])
        K.copy("act", oaT[:, :, T:TT], pv[:, 0:4, :NS])

    SMP.append(smp_fin)
    run_merged_g([OUT, SMP, [lambda: None] * 6 + PREF])
    K.pop_scope()
    if stage == 4:
        zt = K.sb("zt", [128, 1024], F32)
        K.memset("pool", zt.v(), 0.0)
        for i in range(16):
            K.dma("sp", y_p[i * 128:(i + 1) * 128, :], zt[:, 0:D])
        K.dma("sp", y_s.v(), zt[0:NS, 0:D])
        return early()


    def pn_b(ps_halves, rows, res_src, gpost, dst, bufs):
        st, xres, tt_ = bufs[:3]
        K.dma("sp", xres[:rows, :], res_src)
        for hf in range(2):
            K.act(tt_[:rows, hf * 512:(hf + 1) * 512], ps_halves[hf][:rows, :], AF.Square, accum=st[:rows, hf:hf + 1])
        K.tt("dve", st[:rows, 2:3], st[:rows, 0:1], st[:rows, 1:2], ALU.add)
        K.act(st[:rows, 3:4], st[:rows, 2:3], AF.Sqrt, bias=epsT[:rows, :], scale=1.0 / D)
        K.recip(st[:rows, 4:5], st[:rows, 3:4])
        for hf in range(2):
            hs = slice(hf * 512, (hf + 1) * 512)
            K.stt(tt_[:rows, hs], ps_halves[hf][:rows, :], st[:rows, 4:5], gpost[:rows, hs], ALU.mult, ALU.mult)
        K.tt("dve", xres[:rows, :], xres[:rows, :], tt_[:rows, :], ALU.add)
        K.dma("sp", dst, xres[:rows, :])

    def pn_c(rows, bufs, nt):
        st, xres, tt_, xb = bufs
        gname, dstT, col0 = nt
        K.act(tt_[:rows, :], xres[:rows, :], AF.Square, accum=st[:rows, 5:6])
        K.act(st[:rows, 6:7], st[:rows, 5:6], AF.Sqrt, bias=epsT[:rows, :], scale=1.0 / D)
        K.recip(st[:rows, 7:8], st[:rows, 6:7])
        K.act(xb[:rows, :], xres[:rows, :], AF.Copy, scale=st[:rows, 7:8])
        pb = PSB()
        pv = pb.v().re("p (k t) -> p k t", k=8)
        for kt in range(8):
            K.tr(pv[:, kt, :rows], xb[:rows, kt * 128:(kt + 1) * 128], identb[:rows, :rows])
        g0 = PV[gname]
        K.tt("dve", dstT[:, :, col0:col0 + rows], pv[:, :, :rows],
             pvec[:, g0:g0 + 8].un(2).bc([128, 8, rows]), ALU.mult)

    mT = K.sb("mT", [128, 8, TT], BF16)
    mt = [[K.sb("mt", [128, 512], F32) for _ in range(4)] for _ in range(2)]
    mps = {}

    def mg_a1(f, tb, it):
        col0, n = TBLK[tb]
        cs = slice(col0, col0 + n)
        wA, wga, wgb, w1, w2 = mw[f]
        ps2, ps1 = PSF(), PSF()
        for kt in range(4):
            K.mm(ps2[:, :n], w2[:, kt, :], hgl[:, kt, cs], start=(kt == 0), stop=(kt == 3))
        for kt in range(4):
            K.mm(ps1[:, :n], w1[:, kt, :], hgl[:, kt, cs], start=(kt == 0), stop=(kt == 3))
        mps[(it, 1)] = (ps2, ps1)

    def mg_a2(f, tb, it):
        col0, n = TBLK[tb]
        cs = slice(col0, col0 + n)
        wA, wga, wgb, w1, w2 = mw[f]
        psgb = PSF()
        for kt in range(8):
            K.mm(psgb[:, :n], wgb[:, kt, :], hT[:, kt, cs], start=(kt == 0), stop=(kt == 7))
        mps[(it, 2)] = psgb

    def mg_a3(f, tb, it):
        col0, n = TBLK[tb]
        cs = slice(col0, col0 + n)
        wA, wga, wgb, w1, w2 = mw[f]
        psga, psA = PSF(), PSF()
        for kt in range(8):
            K.mm(psga[:, :n], wga[:, kt, :], hT[:, kt, cs], start=(kt == 0), stop=(kt == 7))
        for kt in range(4):
            K.mm(psA[:, :n], wA[:, kt, :], oaT[:, kt, cs], start=(kt == 0), stop=(kt == 3))
        mps[(it, 3)] = (psga, psA)

    def mg_b1(f, tb, it):
        col0, n = TBLK[tb]
        t0, t1_, t2_, t3_ = mt[it % 2]
        ps2, ps1 = mps.pop((it, 1))
        K.act(t0[:, :n], ps2[:, :n], AF.Sigmoid, bias=pcol("glu_b2", f))
        K.stt(t1_[:, :n], ps1[:, :n], pcol("glu_b1", f), t0[:, :n], ALU.add, ALU.mult)

    def mg_b2(f, tb, it):
        col0, n = TBLK[tb]
        t0, t1_, t2_, t3_ = mt[it % 2]
        psgb = mps.pop((it, 2))
        K.act(t2_[:, :n], psgb[:, :n], AF.Sigmoid)
        K.tt("dve", t1_[:, :n], t1_[:, :n], t2_[:, :n], ALU.mult)

    def mg_b3(f, tb, it):
        col0, n = TBLK[tb]
        cs = slice(col0, col0 + n)
        t0, t1_, t2_, t3_ = mt[it % 2]
        psga, psA = mps.pop((it, 3))
        K.act(t3_[:, :n], psga[:, :n], AF.Sigmoid)
        K.tt("dve", t3_[:, :n], psA[:, :n], t3_[:, :n], ALU.mult)
        K.tt("pool" if n > 16 else "dve", mT[:, f, cs], t3_[:, :n], t1_[:, :n], ALU.add)

    its = [(f, tb) for f in range(8) for tb in range(5)]
    for it, (f, tb) in enumerate(its):
        if tb == 0 and f + 1 < 8:
            mg_load(f + 1)
        mg_a1(f, tb, it)
        if it >= 1:
            pf, ptb = its[it - 1]
            mg_b3(pf, ptb, it - 1)
        mg_a2(f, tb, it)
        mg_b1(f, tb, it)
        mg_a3(f, tb, it)
        mg_b2(f, tb, it)
    mg_b3(its[-1][0], its[-1][1], len(its) - 1)
    pn_bufs = [(K.sb("pn_st", [128, 8], F32), K.sb("pn_x", [128, D], F32), K.sb("pn_t", [128, D], F32),
                K.sb("pn_xb", [128, D], BF16)) for _ in range(3)]
    mix_ps = {}

    def mix_a(i):
        rows = 128 if i < 16 else NS
        tcs = slice(i * 128, i * 128 + rows)
        halves = [PSF(), PSF()]
        for hf in range(2):
            for kt in range(8):
                K.mm(halves[hf][:rows, :], mT[:, kt, tcs], wmo[:, kt, hf * 512:(hf + 1) * 512],
                     start=(kt == 0), stop=(kt == 7))
        mix_ps[i] = halves

    def mix_b(i):
        rows = 128 if i < 16 else NS
        pn_b(mix_ps[i], rows, x_tile(i), gpost, x1_d[i * 128:i * 128 + rows, :], pn_bufs[i % 3])

    def mix_c(i):
        rows = 128 if i < 16 else NS
        pn_c(rows, pn_bufs[i % 3], ("g_pre_ffn", hT, i * 128))

    for step in range(17 + 2):
        if step < 17:
            mix_a(step)
        if 0 <= step - 1 < 17:
            mix_b(step - 1)
        if 0 <= step - 2 < 17:
            mix_c(step - 2)
    K.pop_scope()
    K.pop_scope()

    K.push_scope()
    NF = DFF // 128
    aT = K.sb("aT", [128, NF, TT], BF16)
    wd = K.sb("wd", [128, NF, D], BF16)
    gpost2 = bcast_row("gpost_ffn", g_post_ffn, D)
    fring = [K.sb("fring", [128, 8, 128], BF16) for _ in range(4)]
    fr_i = [0]

    def fload(src, c0):
        fr_i[0] += 1
        wb = fring[fr_i[0] % len(fring)]
        K.dma("pool", wb.v(), dap(src, c0, [[DFF, 128], [128 * DFF, 8], [1, 128]]))
        return wb

    sgt = [K.sb("sgt", [128, 512], F32) for _ in range(2)]
    it = 0
    for ft in range(NF):
        wg = fload(w_ffn_gate, ft * 128)
        wu = fload(w_ffn_up, ft * 128)
        if ft >= 1:
            K.dma("pool", wd[:, ft - 1, :], w_ffn_down[(ft - 1) * 128:ft * 128, :])
        if ft == NF - 1:
            K.dma("pool", wd[:, ft, :], w_ffn_down[ft * 128:(ft + 1) * 128, :])
        for tb in range(5):
            col0, n = TBLK[tb]
            cs = slice(col0, col0 + n)
            psg, psu = PSF(), PSF()
            for kt in range(8):
                K.mm(psg[:, :n], wg[:, kt, :], hT[:, kt, cs], start=(kt == 0), stop=(kt == 7))
            for kt in range(8):
                K.mm(psu[:, :n], wu[:, kt, :], hT[:, kt, cs], start=(kt == 0), stop=(kt == 7))
            sg = sgt[it % 2]
            it += 1
            K.act(sg[:, :n], psg[:, :n], AF.Silu)
            K.tt("dve", aT[:, ft, cs], psu[:, :n], sg[:, :n], ALU.mult)
    pn_bufs = [(K.sb("pn_st", [128, 8], F32), K.sb("pn_x", [128, D], F32), K.sb("pn_t", [128, D], F32))
               for _ in range(2)]
    dn_ps = {}

    def dn_a(i):
        rows = 128 if i < 16 else NS
        tcs = slice(i * 128, i * 128 + rows)
        halves = [PSF(), PSF()]
        for hf in range(2):
            for kt in range(NF):
                K.mm(halves[hf][:rows, :], aT[:, kt, tcs], wd[:, kt, hf * 512:(hf + 1) * 512],
                     start=(kt == 0), stop=(kt == NF - 1))
        dn_ps[i] = halves

    def dn_b(i):
        rows = 128 if i < 16 else NS
        dst = y_p[i * 128:(i + 1) * 128, :] if i < 16 else y_s[:, :]
        pn_b(dn_ps[i], rows, x1_d[i * 128:i * 128 + rows, :], gpost2, dst, pn_bufs[i % 2])

    for step in range(17 + 1):
        if step < 17:
            dn_a(step)
        if step - 1 >= 0:
            dn_b(step - 1)
    K.pop_scope()
    K.finish()
    es.close()
    return nc


_W_NAMES = ["norm_pre_mix", "norm_post_mix", "norm_pre_ffn", "norm_post_ffn", "w_in", "mu_shift", "w0",
            "w_decay_up", "a0", "w_aaa_up", "w_gate_up", "k_k", "k_a", "r_k", "lnx_g", "lnx_b", "w_rwkv_out",
            "s5_lam_re", "s5_lam_im", "s5_log_dt", "s5_b_re", "s5_b_im", "s5_c_re", "s5_c_im", "s5_d",
            "glu_w1", "glu_b1", "glu_w2", "glu_b2", "w_merge_out", "w_ffn_gate", "w_ffn_up", "w_ffn_down"]


def make_in_maps(inputs, cores):
    f = lambda a: np.ascontiguousarray(np.asarray(a, dtype=np.float32))
    shared = {n: f(inputs[n])[0] for n in _W_NAMES}
    maps = []
    for c in cores:
        m = dict(shared)
        m["x_p"] = f(inputs["x_prompt"][c])
        m["x_s"] = f(inputs["x_sample"][c * NS:(c + 1) * NS, 0])
        m["st_shift"] = f(inputs["state_shift"][0, c * NS:(c + 1) * NS])
        m["st_wkv"] = f(inputs["state_wkv"][0, c * NS:(c + 1) * NS]).reshape(NS * 8, 64 * 64)
        m["st_re"] = f(inputs["state_s5_re"][0, c * NS:(c + 1) * NS]).reshape(NS, 2048)
        m["st_im"] = f(inputs["state_s5_im"][0, c * NS:(c + 1) * NS]).reshape(NS, 2048)
        maps.append(m)
    return maps


def assemble(results):
    n = len(results)
    cat = lambda k: np.concatenate([np.asarray(r[k]) for r in results], axis=0)
    y_p = np.stack([np.asarray(r["y_p"]) for r in results], 0)
    y_s = cat("y_s").reshape(n * NS, 1, D)
    sh_p = cat("o_shift_p").reshape(1, n, SHIFT)
    wkv_p = np.stack([np.asarray(r["o_wkv_p"]) for r in results], 0).reshape(1, n, 8, 64, 64)
    re_p = cat("o_re_p").reshape(1, n, 32, 64)
    im_p = cat("o_im_p").reshape(1, n, 32, 64)
    sh_s = cat("o_shift_s").reshape(1, n * NS, SHIFT)
    wkv_s = cat("o_wkv_s").reshape(1, n * NS, 8, 64, 64)
    re_s = cat("o_re_s").reshape(1, n * NS, 32, 64)
    im_s = cat("o_im_s").reshape(1, n * NS, 32, 64)
    return tuple(np.ascontiguousarray(a, dtype=np.float32) for a in
                 (y_p, y_s, sh_p, wkv_p, re_p, im_p, sh_s, wkv_s, re_s, im_s))


def kernel(**inputs):
    nc = build()
    in_maps = make_in_maps(inputs, list(range(NCORES)))
    res = run_bass_kernel_spmd(nc, in_maps, core_ids=list(range(NCORES)))
    return assemble(res.results)
```

```python
import math
from contextlib import ExitStack
from functools import partial
import numpy as np
import concourse.bass as bass
import concourse.mybir as mybir
from concourse.bass_utils import run_bass_kernel_spmd

F32 = mybir.dt.float32
BF16 = mybir.dt.bfloat16
I32 = mybir.dt.int32
AF = mybir.ActivationFunctionType
ALU = mybir.AluOpType
AX = mybir.AxisListType

T = 2048
NS = 16
TT = T + NS
D = 1024
RW = 512
SHIFT = 1664
PROJ = 4224
DFF = 2816
NCORES = 8
CDEC = math.exp(-0.5)
TBLK = [(0, 512), (512, 512), (1024, 512), (1536, 512), (2048, 16)]
SAME_ENGINE_SYNC = True
LS = 4


class V:
    __slots__ = ("buf", "ap")

    def __init__(self, buf, ap):
        self.buf = buf
        self.ap = ap

    def __getitem__(self, idx):
        return V(self.buf, self.ap[idx])

    def bc(self, shape):
        return V(self.buf, self.ap.to_broadcast(list(shape)))

    def re(self, s, **kw):
        return V(self.buf, self.ap.rearrange(s, **kw))

    def un(self, axis):
        return V(self.buf, self.ap.unsqueeze(axis))

    def bitcast(self, dt):
        return V(self.buf, self.ap.bitcast(dt))


class Buf:
    def __init__(self, name, handle):
        self.name = name
        self.h = handle
        self.writes = {}
        self.reads = {}
        self.dsem = None
        self.dcnt = 0
        self.is_psum = False

    def __getitem__(self, idx):
        return V(self, self.h[idx])

    def v(self):
        return V(self, self.h[:])


class Kern:
    def __init__(self, nc, es):
        self.nc = nc
        self.es = es
        self.root_es = es
        self.eng = {"pe": nc.tensor, "act": nc.scalar, "dve": nc.vector, "pool": nc.gpsimd, "sp": nc.sync}
        self.sem = {}
        self.cnt = {}
        for e in ("pe", "act", "dve", "pool"):
            self.sem[e] = es.enter_context(nc.semaphore("s_" + e))
            self.cnt[e] = 0
        self.obs = {e: {} for e in self.eng}
        self.semname = {}
        self.dsems = []
        self.nbuf = 0
        self.psf_i = 0
        self.psb_i = 0

    def sb(self, name, shape, dt):
        self.nbuf += 1
        import os
        if os.environ.get("KDBG"):
            sz = int(np.prod(shape[1:])) * (2 if dt == BF16 else 4)
            self._tot = getattr(self, "_tot", 0) + sz
            print(f"alloc {name} {shape} {sz} depth={len(getattr(self, '_saved', []))} cum_noFree={self._tot}")
        h = self.es.enter_context(self.nc.sbuf_tensor(f"{name}_{self.nbuf}", list(shape), dt))
        b = Buf(name, h)
        b.writes = dict(getattr(self, "pending", {}))
        if not hasattr(self, "scope_bufs"):
            self.scope_bufs = [[]]
        self.scope_bufs[-1].append(b)
        return b

    def barrier(self):
        for e in ("pe", "act", "dve", "pool", "sp"):
            for e2 in ("pe", "act", "dve", "pool"):
                if e2 != e and self.cnt[e2]:
                    self._wait(e, e2, self.sem[e2], self.cnt[e2])
            for ob in self.dsems:
                self._wait(e, "d:" + ob.name + str(id(ob)), ob.dsem, ob.dcnt)

    def push_scope(self):
        self._saved = getattr(self, "_saved", [])
        self._saved.append(self.es)
        self.es = ExitStack()
        if not hasattr(self, "scope_bufs"):
            self.scope_bufs = [[]]
        self.scope_bufs.append([])

    def pop_scope(self):
        self.pending = getattr(self, "pending", {})
        for b in self.scope_bufs.pop():
            for d in (b.writes, b.reads):
                for k, (sm, v) in d.items():
                    if self.pending.get(k, (None, 0))[1] < v:
                        self.pending[k] = (sm, v)
        self.es.close()
        self.es = self._saved.pop()

    def ps(self, name, shape, dt):
        self.nbuf += 1
        h = self.es.enter_context(self.nc.psum_tensor(f"{name}_{self.nbuf}", list(shape), dt))
        b = Buf(name, h)
        b.is_psum = True
        return b

    def dram(self, name, shape, dt, kind="Internal"):
        h = self.nc.dram_tensor(name, list(shape), dt, kind=kind)
        return Buf(name, h.ap())

    def newsem(self, name):
        s = self.root_es.enter_context(self.nc.semaphore(name))
        return s

    def _wait(self, e, sem_key, sem, val):
        o = self.obs[e]
        if o.get(sem_key, 0) >= val:
            return
        self.eng[e].wait_ge(sem, val)
        o[sem_key] = val

    def _deps(self, e, W, R):
        need = {}
        for b in R:
            for k, (s, v) in b.writes.items():
                if need.get(k, (None, 0))[1] < v:
                    need[k] = (s, v)
            if b.is_psum:
                for k, (s, v) in b.reads.items():
                    if k != e and need.get(k, (None, 0))[1] < v:
                        need[k] = (s, v)
        for b in W:
            for k, (s, v) in b.writes.items():
                if need.get(k, (None, 0))[1] < v:
                    need[k] = (s, v)
            for k, (s, v) in b.reads.items():
                if need.get(k, (None, 0))[1] < v:
                    need[k] = (s, v)
        for k, (s, v) in need.items():
            if k == e:
                if e == "pe" or not SAME_ENGINE_SYNC:
                    continue
            self._wait(e, k, s, v)

    def _bufs(self, vs):
        out = []
        for x in vs:
            if isinstance(x, V):
                if x.buf not in out:
                    out.append(x.buf)
            elif isinstance(x, Buf):
                if x not in out:
                    out.append(x)
        return out

    def op(self, e, fn, W, R):
        Wb = self._bufs(W)
        Rb = self._bufs(R)
        self._deps(e, Wb, Rb)
        inst = fn()
        self.cnt[e] += 1
        c = self.cnt[e]
        inst.then_inc(self.sem[e], 1)
        for b in Wb:
            b.reads = {}
            b.writes[e] = (self.sem[e], c)
        for b in Rb:
            if b not in Wb:
                b.reads[e] = (self.sem[e], c)
        return inst

    def dma(self, q, out, in_, sem_buf=None, slow=False):
        Wb = self._bufs([out])
        Rb = self._bufs([in_])
        self._deps(q, Wb, Rb)
        ob = sem_buf if sem_buf is not None else out.buf
        if ob.dsem is None:
            ob.dsem = self.newsem("d_" + ob.name + str(len(self.dsems)))
            self.dsems.append(ob)
        if slow:
            with self.nc.allow_non_contiguous_dma(reason="small parameter layout load"):
                inst = self.eng[q].dma_start(out=out.ap, in_=in_.ap)
        else:
            inst = self.eng[q].dma_start(out=out.ap, in_=in_.ap)
        ob.dcnt += 16
        inst.then_inc(ob.dsem, 16)
        key = "d:" + ob.name + str(id(ob))
        for b in Wb:
            b.reads = {}
            b.writes[key] = (ob.dsem, ob.dcnt)
        for b in Rb:
            if b not in Wb:
                b.reads[key] = (ob.dsem, ob.dcnt)
        return inst

    def finish(self):
        for ob in self.dsems:
            self._wait("sp", "d:" + ob.name + str(id(ob)), ob.dsem, ob.dcnt)
        for e in ("pe", "act", "dve", "pool"):
            if self.cnt[e]:
                self._wait("sp", e, self.sem[e], self.cnt[e])

    @staticmethod
    def _a(x):
        return x.ap if isinstance(x, V) else x

    def act(self, out, in_, func, bias=None, scale=None, accum=None):
        kw = {}
        if bias is not None:
            kw["bias"] = self._a(bias)
        if scale is not None:
            kw["scale"] = self._a(scale)
        if accum is not None:
            kw["accum_out"] = self._a(accum)
        return self.op("act", lambda: self.nc.scalar.activation(out=out.ap, in_=in_.ap, func=func, **kw),
                       [out, accum], [in_, bias, scale])

    def tt(self, e, out, in0, in1, op):
        return self.op(e, lambda: self.eng[e].tensor_tensor(out=out.ap, in0=in0.ap, in1=in1.ap, op=op),
                       [out], [in0, in1])

    def ts(self, e, out, in0, s1, s2, op0, op1=None):
        if op1 is None:
            return self.op(e, lambda: self.eng[e].tensor_scalar(out=out.ap, in0=in0.ap, scalar1=self._a(s1),
                                                               scalar2=None, op0=op0), [out], [in0, s1])
        return self.op(e, lambda: self.eng[e].tensor_scalar(out=out.ap, in0=in0.ap, scalar1=self._a(s1),
                                                           scalar2=self._a(s2), op0=op0, op1=op1),
                       [out], [in0, s1, s2])

    def stt(self, out, in0, scalar, in1, op0, op1, e="dve"):
        return self.op(e, lambda: self.eng[e].scalar_tensor_tensor(out=out.ap, in0=in0.ap, scalar=self._a(scalar),
                                                                  in1=in1.ap, op0=op0, op1=op1),
                       [out], [in0, scalar, in1])

    def copy(self, e, out, in_):
        if e == "act":
            return self.act(out, in_, AF.Copy)
        return self.op(e, lambda: self.eng[e].tensor_copy(out=out.ap, in_=in_.ap), [out], [in_])

    def memset(self, e, out, val):
        return self.op(e, lambda: self.eng[e].memset(out.ap, val), [out], [])

    def recip(self, out, in_):
        return self.op("dve", lambda: self.nc.vector.reciprocal(out=out.ap, in_=in_.ap), [out], [in_])

    def reduce(self, out, in_, op=ALU.add, axis=AX.X):
        return self.op("dve", lambda: self.nc.vector.tensor_reduce(out=out.ap, in_=in_.ap, axis=axis, op=op),
                       [out], [in_])

    def scan(self, out, d0, d1, init, op0=ALU.mult, op1=ALU.add):
        return self.op("dve", lambda: self.nc.vector.tensor_tensor_scan(out=out.ap, data0=d0.ap, data1=d1.ap,
                                                                       initial=self._a(init), op0=op0, op1=op1),
                       [out], [d0, d1, init])

    def mm(self, out, lhsT, rhs, start=True, stop=True):
        return self.op("pe", lambda: self.nc.tensor.matmul(out.ap, lhsT=lhsT.ap, rhs=rhs.ap, start=start, stop=stop),
                       [out], [lhsT, rhs])

    def tr(self, out, in_, ident):
        return self.op("pe", lambda: self.nc.tensor.transpose(out.ap, in_.ap, ident.ap), [out], [in_, ident])

    def aselect(self, out, in_, pattern, cmp, fill, base, cm):
        return self.op("pool", lambda: self.nc.gpsimd.affine_select(out=out.ap, in_=in_.ap, pattern=pattern,
                                                                   compare_op=cmp, fill=fill, base=base,
                                                                   channel_multiplier=cm), [out], [in_])

    def iota(self, out, pattern, base, cm):
        return self.op("pool", lambda: self.nc.gpsimd.iota(out.ap, pattern=pattern, base=base,
                                                          channel_multiplier=cm), [out], [])


def run_merged_g(lists):
    idx = [0] * len(lists)
    while True:
        best, bf = -1, 2.0
        for li, L in enumerate(lists):
            if idx[li] < len(L):
                frac = idx[li] / len(L)
                if frac < bf:
                    best, bf = li, frac
        if best < 0:
            break
        lists[best][idx[best]]()
        idx[best] += 1


def merge_lists(lists):
    out = []
    idx = [0] * len(lists)
    while True:
        best, bf = -1, 2.0
        for li, L in enumerate(lists):
            if idx[li] < len(L):
                frac = idx[li] / len(L)
                if frac < bf:
                    best, bf = li, frac
        if best < 0:
            break
        out.append(lists[best][idx[best]])
        idx[best] += 1
    return out


def dap(buf, offset, ap):
    base = buf.h if not hasattr(buf.h, "ap") or isinstance(buf.h, bass.AP) else buf.h
    t = base.tensor if isinstance(base, bass.AP) else base
    return V(buf, bass.AP(t, offset, [list(x) for x in ap]))


def build(stage=99):
    nc = bass.Bass("TRN2", target_bir_lowering=False)
    es = ExitStack()
    K = Kern(nc, es)

    def din(name, shape, dt=F32):
        return Buf(name, nc.dram_tensor(name, list(shape), dt, kind="ExternalInput").ap())

    def dout(name, shape):
        return Buf(name, nc.dram_tensor(name, list(shape), F32, kind="ExternalOutput").ap())

    x_p = din("x_p", [T, D])
    x_s = din("x_s", [NS, D])
    st_shift = din("st_shift", [NS, SHIFT])
    st_wkv = din("st_wkv", [NS * 8, 64 * 64])
    st_re = din("st_re", [NS, 2048])
    st_im = din("st_im", [NS, 2048])
    g_pre_mix = din("norm_pre_mix", [D])
    g_post_mix = din("norm_post_mix", [D])
    g_pre_ffn = din("norm_pre_ffn", [D])
    g_post_ffn = din("norm_post_ffn", [D])
    w_in = din("w_in", [D, PROJ])
    mu_shift = din("mu_shift", [SHIFT])
    w0 = din("w0", [RW])
    w_decay_up = din("w_decay_up", [32, RW])
    a0 = din("a0", [RW])
    w_aaa_up = din("w_aaa_up", [32, RW])
    w_gate_up = din("w_gate_up", [64, RW])
    k_k = din("k_k", [RW])
    k_a = din("k_a", [RW])
    r_k = din("r_k", [RW])
    lnx_g = din("lnx_g", [RW])
    lnx_b = din("lnx_b", [RW])
    w_rwkv_out = din("w_rwkv_out", [RW, D])
    s5_lam_re = din("s5_lam_re", [32, 64])
    s5_lam_im = din("s5_lam_im", [32, 64])
    s5_log_dt = din("s5_log_dt", [32])
    s5_b_re = din("s5_b_re", [32, 64, 16])
    s5_b_im = din("s5_b_im", [32, 64, 16])
    s5_c_re = din("s5_c_re", [32, 16, 64])
    s5_c_im = din("s5_c_im", [32, 16, 64])
    s5_d = din("s5_d", [RW])
    glu_w1 = din("glu_w1", [RW, D])
    glu_b1 = din("glu_b1", [D])
    glu_w2 = din("glu_w2", [RW, D])
    glu_b2 = din("glu_b2", [D])
    w_merge_out = din("w_merge_out", [D, D])
    w_ffn_gate = din("w_ffn_gate", [D, DFF])
    w_ffn_up = din("w_ffn_up", [D, DFF])
    w_ffn_down = din("w_ffn_down", [DFF, D])

    y_p = dout("y_p", [T, D])
    y_s = dout("y_s", [NS, D])
    o_shift_p = dout("o_shift_p", [1, SHIFT])
    o_wkv_p = dout("o_wkv_p", [8, 64, 64])
    o_re_p = dout("o_re_p", [1, 2048])
    o_im_p = dout("o_im_p", [1, 2048])
    o_shift_s = dout("o_shift_s", [NS, SHIFT])
    o_wkv_s = dout("o_wkv_s", [NS * 8, 64 * 64])
    o_re_s = dout("o_re_s", [NS, 2048])
    o_im_s = dout("o_im_s", [NS, 2048])

    x1_d = K.dram("x1_scratch", [TT, D], F32)

    psf = [K.ps("psf", [128, 512], F32) for _ in range(8)]

    def PSF():
        K.psf_i += 1
        return psf[K.psf_i % len(psf)]

    class BV:
        def __init__(self, buf):
            self.buf = buf

        def v(self):
            return V(self.buf, self.buf.h[:].bitcast(BF16))

        def __getitem__(self, idx):
            return self.v()[idx]

    def PSB():
        return BV(PSF())

    ones_f = K.sb("ones_f", [128, 128], F32)
    identf = K.sb("identf", [128, 128], F32)
    identb = K.sb("identb", [128, 128], BF16)
    epsT = K.sb("epsT", [128, 1], F32)
    K.memset("pool", ones_f.v(), 1.0)
    K.memset("pool", epsT.v(), 1e-6)
    K.aselect(identf.v(), ones_f.v(), [[-1, 128]], ALU.is_equal, 0.0, 0, 1)
    K.copy("pool", identb.v(), identf.v())

    pvec = K.sb("pvec", [128, 128], F32)
    PV = {}
    _pv = [0]

    def load_fm(name, src, ntile):
        c0 = _pv[0]
        _pv[0] += ntile
        K.dma("sp", pvec[:, c0:c0 + ntile], dap(src, 0, [[1, 128], [128, ntile]]), sem_buf=pvec, slow=True)
        PV[name] = c0
        return c0

    load_fm("g_pre_mix", g_pre_mix, 8)
    load_fm("g_pre_ffn", g_pre_ffn, 8)
    load_fm("mu", mu_shift, 13)
    load_fm("w0", w0, 4)
    load_fm("a0", a0, 4)
    load_fm("k_k", k_k, 4)
    load_fm("k_a", k_a, 4)
    load_fm("r_k", r_k, 4)
    load_fm("s5_d", s5_d, 4)
    load_fm("glu_b1", glu_b1, 8)
    load_fm("glu_b2", glu_b2, 8)

    def early():
        while getattr(K, "_saved", []):
            K.pop_scope()
        K.finish()
        es.close()
        return nc

    if stage == 0.1:
        return early()

    def pcol(name, i=0):
        c = PV[name] + i
        return pvec[:, c:c + 1]

    hT = K.sb("hT", [128, 8, TT], BF16)
    def norm_transpose(src_tile_fn, gname, dst):
        K.push_scope()
        NB = 4
        xring = [K.sb("xring", [128, D], F32) for _ in range(NB)]
        xn = [K.sb("xn", [128, D], BF16) for _ in range(NB)]
        junk = K.sb("junk", [128, D], BF16)
        stat = [K.sb("stat", [128, 4], F32) for _ in range(NB)]

        def nt_a(i):
            rows = 128 if i < 16 else NS
            xt, st, xb = xring[i % NB], stat[i % NB], xn[i % NB]
            K.dma("sp", xt[:rows, :], src_tile_fn(i))
            K.act(junk[:rows, :], xt[:rows, :], AF.Square, accum=st[:rows, 0:1])
            K.act(st[:rows, 1:2], st[:rows, 0:1], AF.Sqrt, bias=epsT[:rows, :], scale=1.0 / D)
            K.recip(st[:rows, 2:3], st[:rows, 1:2])
            K.act(xb[:rows, :], xt[:rows, :], AF.Copy, scale=st[:rows, 2:3])

        def nt_b(i):
            rows = 128 if i < 16 else NS
            col0 = i * 128
            xb = xn[i % NB]
            pb = PSB()
            pv = pb.v().re("p (k t) -> p k t", k=8)
            for kt in range(8):
                K.tr(pv[:, kt, :rows], xb[:rows, kt * 128:(kt + 1) * 128], identb[:rows, :rows])
            g0 = PV[gname]
            K.tt("dve", dst[:, :, col0:col0 + rows], pv[:, :, :rows],
                 pvec[:, g0:g0 + 8].un(2).bc([128, 8, rows]), ALU.mult)

        for step in range(17 + 2):
            if step < 17:
                nt_a(step)
            if 0 <= step - 2 < 17:
                nt_b(step - 2)
        K.pop_scope()

    def x_tile(i):
        return x_p[i * 128:(i + 1) * 128, :] if i < 16 else x_s[:, :]

    norm_transpose(x_tile, "g_pre_mix", hT)
    if stage == 0.2:
        return early()

    wring = [K.sb("wring", [128, 8, 128], BF16) for _ in range(2)]
    wr_i = [0]

    def load_w_cols(src, c0, ncols_total, kt_n=8, width=128):
        wr_i[0] += 1
        wb = wring[wr_i[0] % len(wring)]
        K.dma("pool", wb[:, :kt_n, :width],
              dap(src, c0, [[ncols_total, 128], [128 * ncols_total, kt_n], [1, width]]))
        return wb

    def proj_fm(wb, kt_n, src_act, tb, ps):
        col0, n = TBLK[tb]
        for kt in range(kt_n):
            K.mm(ps[:, :n], wb[:, kt, :], src_act[:, kt, col0:col0 + n], start=(kt == 0), stop=(kt == kt_n - 1))

    hgl_d = K.dram("hgl_scratch", [128, 4 * TT], BF16)
    TWO_PI = 2.0 * math.pi

    K.push_scope()
    ubf = K.sb("ubf", [128, 4, TT], BF16)
    wu4 = [K.sb("wu4", [128, 8, 128], BF16) for _ in range(4)]
    for ft in range(4):
        K.dma("pool", wu4[ft].v(), dap(w_in, SHIFT + ft * 128, [[PROJ, 128], [128 * PROJ, 8], [1, 128]]))
        for tb in range(5):
            col0, n = TBLK[tb]
            ps = PSF()
            proj_fm(wu4[ft], 8, hT, tb, ps)
            K.copy("act" if tb % 2 else "dve", ubf[:, ft, col0:col0 + n], ps[:, :n])

    if stage == 1.1:
        return early()
    sp_ = K.sb("s5par", [128, 16, 24], F32)
    (P_LRE, P_LIM, P_DT, P_MAG, P_TH, P_FRE, P_FIM, P_LBRE, P_LBIM, P_T0, P_T1, P_T2, P_T3,
     P_TH2, P_MAG2, P_G0R, P_G0I, P_C1, P_S1) = range(19)

    def sp(k):
        return sp_[:, :, k]

    K.dma("sp", sp(P_LRE), dap(s5_lam_re, 0, [[1, 128], [128, 16]]), slow=True)
    K.dma("sp", sp(P_LIM), dap(s5_lam_im, 0, [[1, 128], [128, 16]]), slow=True)
    for gl in range(2):
        K.dma("sp", sp_[gl * 64:(gl + 1) * 64, :, P_DT], dap(s5_log_dt, gl, [[0, 64], [2, 16]]), slow=True)
    K.act(sp(P_DT), sp(P_DT), AF.Exp)
    K.tt("dve", sp(P_T0), sp(P_LRE), sp(P_DT), ALU.mult)
    K.act(sp(P_MAG), sp(P_T0), AF.Exp)
    K.copy("dve", sp(P_MAG2), sp(P_MAG))
    for _k in range(LS - 1):
        K.tt("dve", sp(P_MAG2), sp(P_MAG2), sp(P_MAG), ALU.mult)
    K.tt("dve", sp(P_TH), sp(P_LIM), sp(P_DT), ALU.mult)
    K.ts("dve", sp(P_TH2), sp(P_TH), float(LS), None, ALU.mult)

    NH = 512 // LS
    LC = min(128, NH)
    NRC = NH // LC
    pw = K.sb("pw", [128, 16, 2 * (LS + 1)], F32)
    cosT = K.sb("cosT", [128, 16, LC], F32)
    sinT = K.sb("sinT", [128, 16, LC], F32)
    BT = [K.sb("BT", [128, 32, 128], BF16) for _ in range(LS)]
    CTb = K.sb("CTb", [128, 32, 128], BF16)
    VTb = [K.sb("VTb", [128, 32, 128], BF16) for _ in range(LS - 1)]
    Kt = [K.sb("Kt", [128, 4, 128], BF16) for _ in range(LS - 1)]
    K.push_scope()
    ji = K.sb("ji", [128, LC], I32)
    jf = K.sb("jf", [128, LC], F32)
    ang = K.sb("ang", [128, 16, LC], F32)
    rr = K.sb("rr", [128, 16, LC], F32)
    kf = K.sb("kf", [128, 16, LC], F32)
    ki = K.sb("ki", [128, 16, LC], I32)
    K.iota(ji.v(), [[1, LC]], 1, 0)
    K.copy("dve", jf.v(), ji.v())
    K.tt("dve", ang.v(), sp(P_TH2).un(2).bc([128, 16, LC]), jf.v().un(1).bc([128, 16, LC]), ALU.mult)

    def sin_reduced(out, a_in, shift, rr_, kf_, ki_):
        C1 = 6.28125
        C2 = TWO_PI - C1
        K.ts("dve", rr_, a_in, shift, 1.0 / TWO_PI, ALU.add, ALU.mult)
        K.copy("dve", ki_, rr_)
        K.copy("dve", kf_, ki_)
        K.ts("dve", rr_, a_in, shift, None, ALU.add)
        K.stt(rr_, kf_, -C1, rr_, ALU.mult, ALU.add)
        K.stt(rr_, kf_, -C2, rr_, ALU.mult, ALU.add)
        K.ts("dve", kf_, rr_, math.pi, None, ALU.is_gt)
        K.stt(rr_, kf_, -TWO_PI, rr_, ALU.mult, ALU.add)
        K.ts("dve", kf_, rr_, -math.pi, None, ALU.is_lt)
        K.stt(rr_, kf_, TWO_PI, rr_, ALU.mult, ALU.add)
        K.ts("dve", rr_, rr_, -math.pi, math.pi, ALU.max, ALU.min)
        K.act(out, rr_, AF.Sin)

    sin_reduced(sinT.v(), ang.v(), 0.0, rr.v(), kf.v(), ki.v())
    K.ts("dve", rr.v(), rr.v(), 0.5 * math.pi, None, ALU.add)
    K.ts("dve", kf.v(), rr.v(), math.pi, None, ALU.is_gt)
    K.stt(rr.v(), kf.v(), -TWO_PI, rr.v(), ALU.mult, ALU.add)
    K.ts("dve", rr.v(), rr.v(), -math.pi, math.pi, ALU.max, ALU.min)
    K.act(cosT.v(), rr.v(), AF.Sin)
    sin_reduced(sp(P_S1), sp(P_TH), 0.0, rr[:, :, 0], kf[:, :, 0], ki[:, :, 0])
    sin_reduced(sp(P_C1), sp(P_TH), 0.5 * math.pi, rr[:, :, 0], kf[:, :, 0], ki[:, :, 0])
    K.tt("dve", sp(P_LBRE), sp(P_MAG), sp(P_C1), ALU.mult)
    K.tt("dve", sp(P_LBIM), sp(P_MAG), sp(P_S1), ALU.mult)
    K.tt("dve", sp(P_T0), sp(P_LRE), sp(P_LRE), ALU.mult)
    K.tt("dve", sp(P_T1), sp(P_LIM), sp(P_LIM), ALU.mult)
    K.tt("dve", sp(P_T0), sp(P_T0), sp(P_T1), ALU.add)
    K.recip(sp(P_T0), sp(P_T0))
    K.ts("dve", sp(P_T1), sp(P_LBRE), -1.0, None, ALU.add)
    K.tt("dve", sp(P_T2), sp(P_T1), sp(P_LRE), ALU.mult)
    K.tt("dve", sp(P_T3), sp(P_LBIM), sp(P_LIM), ALU.mult)
    K.tt("dve", sp(P_T2), sp(P_T2), sp(P_T3), ALU.add)
    K.tt("dve", sp(P_FRE), sp(P_T2), sp(P_T0), ALU.mult)
    K.tt("dve", sp(P_T2), sp(P_LBIM), sp(P_LRE), ALU.mult)
    K.tt("dve", sp(P_T3), sp(P_T1), sp(P_LIM), ALU.mult)
    K.tt("dve", sp(P_T2), sp(P_T2), sp(P_T3), ALU.subtract)
    K.tt("dve", sp(P_FIM), sp(P_T2), sp(P_T0), ALU.mult)
    K.memset("dve", pw[:, :, 0], 1.0)
    K.memset("dve", pw[:, :, 1], 0.0)
    for k in range(1, LS + 1):
        pr_, pi_ = pw[:, :, 2 * (k - 1)], pw[:, :, 2 * (k - 1) + 1]
        K.tt("dve", sp(P_T0), pr_, sp(P_LBRE), ALU.mult)
        K.tt("dve", sp(P_T1), pi_, sp(P_LBIM), ALU.mult)
        K.tt("dve", pw[:, :, 2 * k], sp(P_T0), sp(P_T1), ALU.subtract)
        K.tt("dve", sp(P_T0), pr_, sp(P_LBIM), ALU.mult)
        K.tt("dve", sp(P_T1), pi_, sp(P_LBRE), ALU.mult)
        K.tt("dve", pw[:, :, 2 * k + 1], sp(P_T0), sp(P_T1), ALU.add)
    K.pop_scope()
    if stage == 1.2:
        return early()

    K.push_scope()
    CTf = [K.sb("CTf", [128, 16, 128], F32) for _ in range(2)]
    K.push_scope()
    Zc = K.sb("Zc", [128, 4 * 512], F32)
    for part, src in enumerate((s5_c_re, s5_c_im)):
        K.memset("pool", Zc.v(), 0.0)
        for g8 in range(8):
            K.dma("sp", dap(Zc, 16 * g8 * 2048 + 64 * g8, [[2048, 16], [512, 4], [1, 64]]),
                  dap(src, g8 * 1024, [[64, 16], [8192, 4], [1, 64]]))
        for q in range(4):
            ps = PSF()
            for j in range(4):
                K.tr(ps[:, j * 128:(j + 1) * 128], Zc[:, q * 512 + j * 128:q * 512 + (j + 1) * 128], identf.v())
            dst = CTf[part][:, q * 4:q * 4 + 4, :]
            if part == 0:
                K.copy("dve", dst, ps.v().re("p (j c) -> p j c", j=4))
            else:
                K.ts("dve", dst, ps.v().re("p (j c) -> p j c", j=4), -1.0, None, ALU.mult)
    K.pop_scope()
    Xr = K.sb("Xr", [128, 16 * 128], F32)
    Xi = K.sb("Xi", [128, 16 * 128], F32)
    for X, src in ((Xr, s5_b_re), (Xi, s5_b_im)):
        K.memset("pool", X.v(), 0.0)
        for gl in range(2):
            for a in range(4):
                K.dma("sp", dap(X, gl * 64 * 2048 + 16 * gl + a * 512, [[2048, 64], [160, 4], [1, 16]]),
                      dap(src, gl * 1024 + a * 8192, [[16, 64], [2048, 4], [1, 16]]), slow=True)
    c4v = CTb.v().re("p (i two) c -> p i two c", two=2)
    K.copy("act", c4v[:, :, 0, :], CTf[0].v())
    K.copy("act", c4v[:, :, 1, :], CTf[1].v())
    T1 = K.sb("T1", [128, 16, 128], F32)
    Xs = [K.sb("Xs", [128, 16, 128], F32) for _ in range(2)]
    T2 = Xs[0]
    bcv = lambda vv: vv.un(2).bc([128, 16, 128])

    def act_scale(dst, src3, g2):
        for i_ in range(16):
            K.act(dst[:, i_, :], src3[:, i_, :], AF.Copy, scale=g2[:, i_:i_ + 1])

    for jv in range(LS - 1):
        lr_, li_ = pw[:, :, 2 * (jv + 1)], pw[:, :, 2 * (jv + 1) + 1]
        v4v = VTb[jv].v().re("p (i two) c -> p i two c", two=2)
        K.tt("dve", T1.v(), CTf[0].v(), bcv(lr_), ALU.mult)
        K.tt("dve", T2.v(), CTf[1].v(), bcv(li_), ALU.mult)
        K.tt("dve", v4v[:, :, 0, :], T1.v(), T2.v(), ALU.add)
        K.tt("dve", T1.v(), CTf[1].v(), bcv(lr_), ALU.mult)
        K.tt("dve", T2.v(), CTf[0].v(), bcv(li_), ALU.mult)
        K.tt("dve", v4v[:, :, 1, :], T1.v(), T2.v(), ALU.subtract)
    gsc = K.sb("gsc", [128, 16, 4], F32)
    x3 = lambda b_: b_.v().re("p (i c) -> p i c", i=16)
    for v in range(LS):
        kpow = LS - 1 - v
        lr_, li_ = pw[:, :, 2 * kpow], pw[:, :, 2 * kpow + 1]
        K.tt("dve", gsc[:, :, 2], lr_, sp(P_FRE), ALU.mult)
        K.tt("dve", gsc[:, :, 3], li_, sp(P_FIM), ALU.mult)
        K.tt("dve", gsc[:, :, 0], gsc[:, :, 2], gsc[:, :, 3], ALU.subtract)
        K.tt("dve", gsc[:, :, 2], lr_, sp(P_FIM), ALU.mult)
        K.tt("dve", gsc[:, :, 3], li_, sp(P_FRE), ALU.mult)
        K.tt("dve", gsc[:, :, 1], gsc[:, :, 2], gsc[:, :, 3], ALU.add)
        gr, gi = bcv(gsc[:, :, 0]), bcv(gsc[:, :, 1])
        K.tt("dve", Xs[0].v(), x3(Xr), gr, ALU.mult)
        K.tt("dve", T1.v(), x3(Xi), gi, ALU.mult)
        K.tt("dve", Xs[0].v(), Xs[0].v(), T1.v(), ALU.subtract)
        K.tt("dve", Xs[1].v(), x3(Xr), gi, ALU.mult)
        K.tt("dve", T1.v(), x3(Xi), gr, ALU.mult)
        K.tt("dve", Xs[1].v(), Xs[1].v(), T1.v(), ALU.add)
        for part in range(2):
            for i4 in range(4):
                ps = PSF()
                for j in range(4):
                    K.tr(ps[:, j * 128:(j + 1) * 128], Xs[part][:, i4 * 4 + j, :], identf.v())
                dstb = BT[v].v().re("p (i two) c -> p i two c", two=2)[:, i4 * 4:i4 * 4 + 4, part, :]
                K.copy("act" if i4 % 2 else "dve", dstb, ps.v().re("p (j c) -> p j c", j=4))
        if kpow <= LS - 2:
            for q in range(4):
                ps = PSF()
                for j in range(4):
                    i = q * 4 + j
                    K.mm(ps[:, 0:128], Xs[0][:, i, :], CTf[0][:, i, :], start=(j == 0), stop=False)
                    K.mm(ps[:, 0:128], Xs[1][:, i, :], CTf[1][:, i, :], start=False, stop=(j == 3))
                K.copy("dve", Kt[kpow][:, q, :], ps[:, 0:128])
    K.pop_scope()
    if stage == 1.3:
        return early()

    x1b = [K.sb("x1sb", [128, 16, NS], BF16) for _ in range(2)]
    K.push_scope()
    s0_tm = [K.sb("s0tm", [NS, 2048], F32) for _ in range(2)]
    s0 = [K.sb("s0", [128, 16, NS], F32) for _ in range(2)]
    K.dma("sp", s0_tm[0].v(), st_re.v())
    K.dma("sp", s0_tm[1].v(), st_im.v())
    for part in range(2):
        ps = PSF()
        for i in range(16):
            K.tr(ps[:, i * NS:(i + 1) * NS], s0_tm[part][:, i * 128:(i + 1) * 128], identf[:NS, :NS])
        K.copy("dve", s0[part].v(), ps[:, :16 * NS].re("p (i n) -> p i n", i=16))
    psr = PSF()
    psi = PSF()
    for i in range(16):
        K.mm(psr[:, i * NS:(i + 1) * NS], BT[LS - 1][:, 2 * i, :], ubf[:, i // 4, T:TT])
        K.mm(psi[:, i * NS:(i + 1) * NS], BT[LS - 1][:, 2 * i + 1, :], ubf[:, i // 4, T:TT])
    sw = [K.sb("sw", [128, 16, NS], F32) for _ in range(2)]
    x1 = [K.sb("x1s", [128, 16, NS], F32) for _ in range(2)]
    bc16 = lambda k: sp(k).un(2).bc([128, 16, NS])
    pr3 = psr[:, :16 * NS].re("p (i n) -> p i n", i=16)
    pi3 = psi[:, :16 * NS].re("p (i n) -> p i n", i=16)
    K.tt("dve", sw[0].v(), s0[0].v(), bc16(P_LBRE), ALU.mult)
    K.tt("dve", sw[1].v(), s0[1].v(), bc16(P_LBIM), ALU.mult)
    K.tt("dve", sw[0].v(), sw[0].v(), sw[1].v(), ALU.subtract)
    K.tt("dve", x1[0].v(), sw[0].v(), pr3, ALU.add)
    K.tt("dve", sw[0].v(), s0[1].v(), bc16(P_LBRE), ALU.mult)
    K.tt("dve", sw[1].v(), s0[0].v(), bc16(P_LBIM), ALU.mult)
    K.tt("dve", sw[0].v(), sw[0].v(), sw[1].v(), ALU.add)
    K.tt("dve", x1[1].v(), sw[0].v(), pi3, ALU.add)
    for part in range(2):
        K.copy("act", x1b[part].v(), x1[part].v())
        xs_tm = s0_tm[part]
        for i4 in range(4):
            ps = PSF()
            for j in range(4):
                i = i4 * 4 + j
                K.tr(ps[:NS, j * 128:(j + 1) * 128], x1[part][:, i, :], identf.v())
            K.copy("dve", xs_tm[:, i4 * 512:(i4 + 1) * 512], ps[:NS, :])
        K.dma("sp", (o_re_s if part == 0 else o_im_s).v(), xs_tm.v())
    K.pop_scope()
    if stage == 1.4:
        return early()

    hgl = K.sb("hgl", [128, 4, TT], BF16)
    carry = K.sb("carry", [128, 16, 2], F32)
    K.memset("dve", carry.v(), 0.0)
    UN = [dict(bR=K.sb("bR", [128, NH], F32), bI=K.sb("bI", [128, NH], F32),
               wR=K.sb("wR", [128, NH], F32), wI=K.sb("wI", [128, NH], F32),
               t1=K.sb("t1", [128, NH], F32), t2=K.sb("t2", [128, NH], F32)) for _ in range(4)]
    xbr = [K.sb("xbr", [128, NH + 1], BF16) for _ in range(4)]
    xbi = [K.sb("xbi", [128, NH + 1], BF16) for _ in range(4)]
    yv = [K.sb("yv", [128, 512], F32) for _ in range(2)]
    y2 = [K.sb("y2", [128, 512], F32) for _ in range(2)]

    def gelu_to(dst_bf, yv_, y2_):
        K.act(y2_, yv_, AF.Square)
        K.act(y2_, y2_, AF.Identity, bias=1.0, scale=0.044715)
        K.tt("pool", y2_, y2_, yv_, ALU.mult)
        K.act(y2_, y2_, AF.Sigmoid, scale=1.5957691216057308)
        K.tt("pool", dst_bf, y2_, yv_, ALU.mult)

    vr = lambda b: b.v().re("p (a l) -> p a l", a=NRC)
    for q in range(4):
        for tb in range(4):
            col0, n = TBLK[tb]
            ugrp = ubf[:, q, col0:col0 + n].re("p (c j) -> p j c", j=LS)
            for j in range(4):
                i = q * 4 + j
                U = UN[j]
                psr = PSF()
                psi = PSF()
                for v in range(LS):
                    K.mm(psr[:, :NH], BT[v][:, 2 * i, :], ugrp[:, v, :], start=(v == 0), stop=(v == LS - 1))
                for v in range(LS):
                    K.mm(psi[:, :NH], BT[v][:, 2 * i + 1, :], ugrp[:, v, :], start=(v == 0), stop=(v == LS - 1))
                tC = cosT[:, i, :].un(1).bc([128, NRC, LC])
                tS = sinT[:, i, :].un(1).bc([128, NRC, LC])
                pr2 = psr[:, :NH].re("p (a l) -> p a l", a=NRC)
                pi2 = psi[:, :NH].re("p (a l) -> p a l", a=NRC)
                K.tt("dve", vr(U["t1"]), pr2, tC, ALU.mult)
                K.tt("dve", vr(U["t2"]), pi2, tS, ALU.mult)
                K.tt("pool", U["bR"].v(), U["t1"].v(), U["t2"].v(), ALU.add)
                K.tt("dve", vr(U["wR"]), pi2, tC, ALU.mult)
                K.tt("dve", vr(U["wI"]), pr2, tS, ALU.mult)
                K.tt("dve", U["bI"].v(), U["wR"].v(), U["wI"].v(), ALU.subtract)
                K.copy("act", xbr[j][:, 0:1], carry[:, i, 0:1])
                K.copy("act", xbi[j][:, 0:1], carry[:, i, 1:2])
            for a in range(NRC):
                cs = slice(a * LC, (a + 1) * LC)
                for j in range(4):
                    i = q * 4 + j
                    U = UN[j]
                    rho = sp_[:, i, P_MAG2:P_MAG2 + 1].bc([128, LC])
                    if a == 0:
                        ire, iim = carry[:, i, 0:1], carry[:, i, 1:2]
                    else:
                        ire, iim = U["bR"][:, a * LC - 1:a * LC], U["bI"][:, a * LC - 1:a * LC]
                    K.scan(U["wR"][:, cs], rho, U["bR"][:, cs], ire)
                    K.scan(U["wI"][:, cs], rho, U["bI"][:, cs], iim)
                for j in range(4):
                    i = q * 4 + j
                    U = UN[j]
                    K.tt("pool", U["t1"][:, cs], U["wR"][:, cs], cosT[:, i, :], ALU.mult)
                    K.tt("dve", U["t2"][:, cs], U["wI"][:, cs], sinT[:, i, :], ALU.mult)
                    K.tt("dve", U["bR"][:, cs], U["t1"][:, cs], U["t2"][:, cs], ALU.subtract)
                    K.tt("pool", U["t1"][:, cs], U["wR"][:, cs], sinT[:, i, :], ALU.mult)
                    K.tt("dve", U["t2"][:, cs], U["wI"][:, cs], cosT[:, i, :], ALU.mult)
                    K.tt("dve", U["bI"][:, cs], U["t1"][:, cs], U["t2"][:, cs], ALU.add)
            for j in range(4):
                i = q * 4 + j
                U = UN[j]
                K.copy("dve", carry[:, i, 0:1], U["bR"][:, NH - 1:NH])
                K.copy("dve", carry[:, i, 1:2], U["bI"][:, NH - 1:NH])
                K.copy("act", xbr[j][:, 1:NH + 1], U["bR"].v())
                K.copy("act", xbi[j][:, 1:NH + 1], U["bI"].v())
            yy = yv[tb % 2]
            yyp = yy.v().re("p (j c) -> p j c", j=LS)
            for jo in range(LS):
                psy = PSF()
                if jo == LS - 1:
                    for j in range(4):
                        i = q * 4 + j
                        K.mm(psy[:, :NH], CTb[:, 2 * i, :], xbr[j][:, 1:NH + 1], start=(j == 0), stop=False)
                        K.mm(psy[:, :NH], CTb[:, 2 * i + 1, :], xbi[j][:, 1:NH + 1], start=False, stop=(j == 3))
                else:
                    for j in range(4):
                        i = q * 4 + j
                        K.mm(psy[:, :NH], VTb[jo][:, 2 * i, :], xbr[j][:, 0:NH], start=(j == 0), stop=False)
                        K.mm(psy[:, :NH], VTb[jo][:, 2 * i + 1, :], xbi[j][:, 0:NH], start=False, stop=False)
                    for ii in range(jo + 1):
                        K.mm(psy[:, :NH], Kt[jo - ii][:, q, :], ugrp[:, ii, :], start=False, stop=(ii == jo))
                K.copy("act", yyp[:, jo, :], psy[:, :NH])
            psu = PSF()
            proj_fm(wu4[q], 8, hT, tb, psu)
            K.stt(yyp, psu.v().re("p (c j) -> p j c", j=LS), pcol("s5_d", q), yyp, ALU.mult, ALU.add)
            y2p = y2[tb % 2].v().re("p (j c) -> p j c", j=LS)
            gelu_to(hgl[:, q, col0:col0 + n].re("p (c j) -> p j c", j=LS), yyp, y2p)
        psy = PSF()
        for j in range(4):
            i = q * 4 + j
            K.mm(psy[:, :NS], CTb[:, 2 * i, :], x1b[0][:, i, :], start=(j == 0), stop=False)
            K.mm(psy[:, :NS], CTb[:, 2 * i + 1, :], x1b[1][:, i, :], start=False, stop=(j == 3))
        K.copy("act", yv[0][:, :NS], psy[:, :NS])
        psu = PSF()
        proj_fm(wu4[q], 8, hT, 4, psu)
        K.stt(yv[0][:, :NS], psu[:, :NS], pcol("s5_d", q), yv[0][:, :NS], ALU.mult, ALU.add)
        gelu_to(hgl[:, q, T:TT], yv[0][:, :NS], y2[0][:, :NS])
    if stage == 1.5:
        return early()
    K.dma("sp", dap(o_re_p, 0, [[1, 128], [128, 16]]), carry[:, :, 0], slow=True)
    K.dma("sp", dap(o_im_p, 0, [[1, 128], [128, 16]]), carry[:, :, 1], slow=True)
    K.dma("sp", hgl_d.v(), hgl.v().re("p k t -> p (k t)"))
    K.pop_scope()
    if stage == 2:
        zt = K.sb("zt", [128, 4096], F32)
        K.memset("pool", zt.v(), 0.0)
        K.dma("sp", o_shift_p.v(), zt[0:1, 0:SHIFT])
        K.dma("sp", o_shift_s.v(), zt[0:NS, 0:SHIFT])
        K.dma("sp", o_wkv_p.v().re("h v k -> h (v k)"), zt[0:8, :])
        K.dma("sp", o_wkv_s.v(), zt[:, :])
        for i in range(16):
            K.dma("sp", y_p[i * 128:(i + 1) * 128, :], zt[:, 0:D])
        K.dma("sp", y_s.v(), zt[0:NS, 0:D])
        return early()
    K.push_scope()
    lnxg = K.sb("lnxg", [128, RW], F32)
    lnxb = K.sb("lnxb", [128, RW], F32)
    K.dma("sp", lnxg.v(), dap(lnx_g, 0, [[0, 128], [1, RW]]))
    K.dma("sp", lnxb.v(), dap(lnx_b, 0, [[0, 128], [1, RW]]))
    par = K.sb("spar", [128, 3, 64], F32)
    for tk in range(NS):
        for j, src in enumerate((lnx_g, lnx_b, r_k)):
            K.dma("sp", par[tk * 8:(tk + 1) * 8, j, :], dap(src, 0, [[64, 8], [1, 64]]), sem_buf=par)

    lora_bf = K.sb("lora_bf", [128, TT], BF16)
    lora_up = K.sb("lora_up", [128, RW], BF16)
    oaT = K.sb("oaT", [128, 4, TT], BF16)
    o_d = K.dram("o_scratch", [T, RW], F32)
    bonus_d = K.dram("bonus_scratch", [T, RW], F32)
    samp_d = K.dram("samp_scratch", [NS * 8, 7, 64], F32)
    sampo_d = K.dram("sampo_scratch", [NS, RW], F32)

    K.push_scope()
    mask4 = K.sb("mask4", [128, 4, 128], F32)
    maskNT = K.sb("maskNT", [128, 128], F32)
    resetm = K.sb("resetm", [128, T], BF16)
    blk1 = K.sb("blk1", [128, 128], F32)
    eps_ln = K.sb("eps_ln", [128, 1], F32)
    K.memset("pool", eps_ln.v(), 64e-5)
    for j in range(4):
        K.aselect(mask4[:, j, :], ones_f.v(), [[1, 128]], ALU.is_gt if j % 2 == 0 else ALU.is_ge, 0.0, 0, -1)
    K.aselect(maskNT.v(), ones_f.v(), [[-1, 128]], ALU.is_gt, 0.0, 0, 1)
    K.memset("pool", resetm.v(), 1.0)
    K.memset("pool", resetm.v().re("p (c l) -> p c l", l=128)[:, :, 0:1], 0.0)
    K.memset("pool", blk1.v(), 0.0)
    K.memset("pool", blk1[0:64, 0:64], 1.0)
    K.memset("pool", blk1[64:128, 64:128], 1.0)

    K.dma("pool", lora_up[0:32, :], w_decay_up.v())
    K.dma("pool", lora_up[32:64, :], w_aaa_up.v())
    K.dma("pool", lora_up[64:128, :], w_gate_up.v())
    sh0T = K.sb("sh0T", [128, 13, NS], F32)
    shout = [K.sb("shout", [NS + 1, 128], F32) for _ in range(1)]
    K.push_scope()
    sh_tm = K.sb("sh_tm", [NS, SHIFT], F32)
    K.dma("sp", sh_tm.v(), st_shift.v())
    ps = PSF()
    for ft in range(13):
        K.tr(ps[:, ft * NS:(ft + 1) * NS], sh_tm[:, ft * 128:(ft + 1) * 128], identf[:NS, :NS])
    K.copy("dve", sh0T.v(), ps[:, :13 * NS].re("p (f n) -> p f n", f=13))
    K.pop_scope()

    omm = K.sb("omm", [128, 13], F32)
    K.ts("dve", omm.v(), pvec[:, PV["mu"]:PV["mu"] + 13], -1.0, 1.0, ALU.mult, ALU.add)
    omka = K.sb("omka", [128, 4], F32)
    K.ts("dve", omka.v(), pvec[:, PV["k_a"]:PV["k_a"] + 4], -1.0, 1.0, ALU.mult, ALU.add)

    SC = [K.sb("scr", [128, TT], F32) for _ in range(6)]

    def proj_shift_tile(ft, pr, tmp, xr_out, xr_out_dt_bf=None, samp32=None):
        wb = load_w_cols(w_in, ft * 128, PROJ)
        for tb in range(5):
            col0, n = TBLK[tb]
            ps = PSF()
            proj_fm(wb, 8, hT, tb, ps)
            K.copy("act", pr[:, col0:col0 + n], ps[:, :n])
            K.act(tmp[:, col0:col0 + n], ps[:, :n], AF.Copy, scale=omm[:, ft:ft + 1])
        ps = PSF()
        K.tr(ps[:NS + 1, :128], pr[:, T - 1:TT], identf.v())
        so = shout[0]
        K.copy("dve", so.v(), ps[:NS + 1, :128])
        K.dma("sp", o_shift_p[0:1, ft * 128:(ft + 1) * 128], so[0:1, :])
        K.dma("sp", o_shift_s[:, ft * 128:(ft + 1) * 128], so[1:NS + 1, :])
        mu_c = pcol("mu", ft)
        K.stt(xr_out[:, 1:T], pr[:, 0:T - 1], mu_c, tmp[:, 1:T], ALU.mult, ALU.add)
        K.copy("pool", xr_out[:, 0:1], tmp[:, 0:1])
        K.stt(xr_out[:, T:TT], sh0T[:, ft, :], mu_c, tmp[:, T:TT], ALU.mult, ALU.add)
        if samp32 is not None:
            K.stt(samp32, sh0T[:, ft, :], mu_c, tmp[:, T:TT], ALU.mult, ALU.add)

    proj_shift_tile(12, SC[0], SC[1], SC[2].v())
    K.act(lora_bf[0:32, :], SC[2][0:32, :], AF.Tanh)
    K.act(lora_bf[32:64, :], SC[2][32:64, :], AF.Copy)
    K.act(lora_bf[64:128, :], SC[2][64:128, :], AF.Sigmoid)

    vbf = K.sb("vbf", [128, TT], BF16)
    vs32 = K.sb("vs32", [128, NS], F32)
    AR = K.sb("AR", [128, 16, 2, 128], BF16)
    bti = K.sb("bti", [128, T], BF16)
    kti = K.sb("kti", [128, T], BF16)
    bhat = K.sb("bhat", [128, T], BF16)
    khat = K.sb("khat", [128, T], BF16)
    Wfm = bhat
    TMq = K.sb("TMq", [128, 16, 4, 128], BF16)
    gL = K.sb("gL", [128, 16], F32)
    Sf = K.sb("Sf", [128, 64], F32)
    Sb = K.sb("Sb", [128, 64], BF16)
    smp = K.sb("smp", [NS, 3, 128], F32)
    decs = K.sb("decs", [128, NS], F32)
    avs = K.sb("avs", [128, NS], F32)
    NPR = 16
    ABK = [K.sb("ABK", [128, 2, 128], BF16) for _ in range(NPR)]
    NPQ = 8
    ATMP = [K.sb("ATMP", [128, 2, 128], BF16) for _ in range(NPQ)]
    P0T = [K.sb("P0T", [128, 128], BF16) for _ in range(NPQ)]
    PMb = [[K.sb("PMb", [128, 3, 128], BF16) for _ in range(2)] for _ in range(NPQ)]
    Ybf = [K.sb("Ybf", [128, 64], BF16) for _ in range(NPQ)]
    QTb = [[K.sb("QTb", [128, 128], BF16) for _ in range(2)] for _ in range(NPQ)]
    Ubar = [K.sb("Ubar", [128, 64], F32) for _ in range(NPR)]
    Ut = [K.sb("Ut", [128, 64], BF16) for _ in range(4)]
    Otile = [K.sb("Otile", [128, 128], F32) for _ in range(2)]
    btile = [K.sb("btile", [128, 512], F32) for _ in range(1)]
    wkvo = K.sb("wkvo", [64, 128], F32)

    def derivedA(hp):
        xr_r, xr_k = SC[0], SC[1]
        proj_shift_tile(0 * 4 + hp, SC[2], SC[3], xr_r.v())
        yield
        proj_shift_tile(1 * 4 + hp, SC[2], SC[3], xr_k.v())
        yield
        proj_shift_tile(2 * 4 + hp, SC[2], SC[3], vbf.v(), samp32=vs32.v())
        yield
        sigd, csum, afm, kkn = SC[2], SC[3], SC[4], SC[5]
        hc = slice(hp * 128, (hp + 1) * 128)
        for tb in range(5):
            col0, n = TBLK[tb]
            ps = PSF()
            K.mm(ps[:, :n], lora_up[0:32, hc], lora_bf[0:32, col0:col0 + n])
            K.act(sigd[:, col0:col0 + n], ps[:, :n], AF.Sigmoid, bias=pcol("w0", hp))
            yield
            ps = PSF()
            K.mm(ps[:, :n], lora_up[32:64, hc], lora_bf[32:64, col0:col0 + n])
            K.act(afm[:, col0:col0 + n], ps[:, :n], AF.Sigmoid, bias=pcol("a0", hp))
            yield
        K.act(kkn.v(), xr_k.v(), AF.Square, scale=pcol("k_k", hp))
        yield
        for tb in range(5):
            col0, n = TBLK[tb]
            ps = PSF()
            K.mm(ps[:, :n], blk1.v(), kkn[:, col0:col0 + n])
            K.ts("dve", csum[:, col0:col0 + n], ps[:, :n], 1e-24, None, ALU.max)
            yield
        K.act(csum.v(), csum.v(), AF.Ln)
        yield
        K.act(csum.v(), csum.v(), AF.Exp, scale=-0.5)
        yield
        K.stt(kkn.v(), xr_k.v(), pcol("k_k", hp), csum.v(), ALU.mult, ALU.mult)
        yield
        K.ts("dve" if hp == 0 else "pool", csum.v(), afm.v(), pcol("k_a", hp), omka[:, hp:hp + 1], ALU.mult, ALU.add)
        yield
        K.tt("dve" if hp == 0 else "pool", xr_k.v(), xr_k.v(), csum.v(), ALU.mult)
        yield
        kmod = xr_k
        K.tt("dve" if hp == 0 else "pool", afm.v(), kkn.v(), afm.v(), ALU.mult)
        yield
        bfm = afm
        K.stt(csum.v(), xr_r.v(), pcol("r_k", hp), kmod.v(), ALU.mult, ALU.mult)
        yield
        for tb in range(4):
            col0, n = TBLK[tb]
            ps = PSF()
            K.mm(ps[:, :n], blk1.v(), csum[:, col0:col0 + n])
            K.tt("dve", csum[:, col0:col0 + n], ps[:, :n], vbf[:, col0:col0 + n], ALU.mult)
            yield
        for c4 in range(4):
            ps = PSF()
            for j in range(4):
                c = c4 * 4 + j
                K.tr(ps[:, j * 128:(j + 1) * 128], csum[:, c * 128:(c + 1) * 128], identf.v())
            bt = btile[0]
            K.copy("act", bt.v(), ps.v())
            yield
            K.dma("sp", dap(bonus_d, c4 * 512 * RW + hp * 128, [[RW, 128], [128 * RW, 4], [1, 128]]),
                  bt.v().re("p (j f) -> p j f", j=4))
            yield
        K.act(decs.v(), sigd[:, T:TT], AF.Exp, scale=-CDEC)
        yield
        K.ts("dve", avs.v(), kkn[:, T:TT], -1.0, None, ALU.mult)
        yield
        yield

    def derivedB(hp):
        xr_r, xr_k = SC[0], SC[1]
        sigd, csum, afm, kkn = SC[2], SC[3], SC[4], SC[5]
        kmod, bfm = xr_k, afm
        K.scan(csum[:, 0:T], resetm.v(), sigd[:, 0:T], 0.0)
        gt = sigd
        c3 = lambda b_: b_[:, 0:T].re("p (c l) -> p c l", l=128)
        K.act(gt[:, 0:T], csum[:, 0:T], AF.Exp, scale=-CDEC)
        K.tt("dve", AR[:, :, 1, :], c3(xr_r), c3(gt), ALU.mult)
        K.copy("dve", gL.v(), c3(gt)[:, :, 127])
        K.stt(AR[:, :, 0, 1:128], c3(kkn)[:, :, 1:128], -1.0, c3(gt)[:, :, 0:127], ALU.mult, ALU.mult)
        K.ts("dve", AR[:, :, 0, 0], c3(kkn)[:, :, 0], -1.0, None, ALU.mult)
        K.act(gt[:, 0:T], csum[:, 0:T], AF.Exp, scale=CDEC)
        K.tt("dve", bti.v(), bfm[:, 0:T], gt[:, 0:T], ALU.mult)
        K.tt("dve", kti.v(), kmod[:, 0:T], gt[:, 0:T], ALU.mult)
        K.tt("dve", c3(gt), c3(csum)[:, :, 127:128].bc([128, 16, 128]), c3(csum), ALU.subtract)
        K.act(gt[:, 0:T], gt[:, 0:T], AF.Exp, scale=-CDEC)
        K.tt("dve", bhat.v(), bfm[:, 0:T], gt[:, 0:T], ALU.mult)
        K.tt("dve", khat.v(), kmod[:, 0:T], gt[:, 0:T], ALU.mult)
        for c in range(16):
            pb = PSB()
            pv = pb.v().re("p (q t) -> p q t", q=8)
            cs = slice(c * 128, (c + 1) * 128)
            K.tr(pv[:, 0, :], vbf[:, cs], identb.v())
            K.tr(pv[:, 1, :], bhat[:, cs], identb.v())
            K.tr(pv[:, 2, :], khat[:, cs], identb.v())
            K.tr(pv[:, 3, :], AR[:, c, 0, :], identb.v())
            K.copy("act" if c % 2 else "dve", TMq[:, c, :, :], pv[:, 0:4, :])
        sq_src = [xr_r[:, T:TT], kmod[:, T:TT], vs32.v(), decs.v(), avs.v(), bfm[:, T:TT]]
        for half in range(2):
            ps = PSF()
            for j in range(3):
                K.tr(ps[:NS, j * 128:(j + 1) * 128], sq_src[half * 3 + j], identf.v())
            K.copy("dve", smp.v(), ps[:NS, :384].re("p (q f) -> p q f", q=3))
            for j in range(3):
                q = half * 3 + j
                K.dma("sp", dap(samp_d, (2 * hp) * 448 + q * 64, [[8 * 448, NS], [448, 2], [1, 64]]),
                      smp[:, j, :].re("p (h n) -> p h n", h=2))


    def chunk_thunks(hp):
        CH = []
        CH.append(lambda: K.memset('dve', Sf.v(), 0.0))
        CH.append(lambda: K.memset('pool', Sb.v(), 0.0))
        pairs = [(c, hl) for c in range(16) for hl in range(2)]

        def pre_a(sl, c, hl):
            rows = slice(hl * 64, (hl + 1) * 64)
            cs = slice(c * 128, (c + 1) * 128)
            ps = PSF()
            arv = AR[rows, c, :, :].re("p a t -> p (a t)")
            K.mm(ps[:, 0:256], bti[rows, cs], arv)
            K.mm(ps[:, 256:512], kti[rows, cs], arv)
            ps2 = PSF()
            K.mm(ps2[:, 0:128], AR[rows, c, 0, :], bti[rows, cs])
            p4 = ps.v().re("p (q t) -> p q t", q=4)
            K.tt("dve", ATMP[sl % NPQ].v(), p4[:, 0::2, :], mask4[:, 0::2, :], ALU.mult)
            K.tt("dve", ABK[sl].v(), p4[:, 1::2, :], mask4[:, 1::2, :], ALU.mult)
            K.tt("dve", P0T[sl % NPQ].v(), ps2[:, 0:128], maskNT.v(), ALU.mult)
            K.tt("pool", QTb[sl % NPQ][0].v(), P0T[sl % NPQ].v(), identb.v(), ALU.add)

        def pre_lev(sl, lev, gi):
            if lev == 0:
                Pj, PjT, Mp = ATMP[sl % NPQ][:, 0, :], P0T[sl % NPQ].v(), identb.v()
            else:
                src = PMb[sl % NPQ][lev % 2]
                Pj, PjT, Mp = src[:, 0, :], src[:, 1, :], src[:, 2, :]
            dst = PMb[sl % NPQ][(lev + 1) % 2]
            ps = PSF()
            if lev < 6:
                K.mm(ps[:, 0:128], PjT, Pj)
                K.mm(ps[:, 128:256], Pj, PjT)
            K.mm(ps[:, 256:384], QTb[sl % NPQ][lev % 2].v(), Mp, start=True, stop=True)
            eng = "act" if (gi % 8) not in (2, 5, 7) else "dve"
            if lev < 6:
                K.copy(eng, dst.v(), ps[:, 0:384].re("p (q t) -> p q t", q=3))
                K.tt("pool", QTb[sl % NPQ][(lev + 1) % 2].v(), dst[:, 1, :], identb.v(), ALU.add)
            else:
                K.copy(eng, dst[:, 2, :], ps[:, 256:384])

        def pre_w(sl, c, hl, gi):
            rows = slice(hl * 64, (hl + 1) * 64)
            cs = slice(c * 128, (c + 1) * 128)
            Mfin = PMb[sl % NPQ][1][:, 2, :]
            ps = PSF()
            K.mm(ps[:, 0:128], TMq[:, c, 3, :], Mfin)
            K.mm(ps[:, 128:192], ATMP[sl % NPQ][:, 1, :], TMq[:, c, 0, rows])
            eng = "act" if gi % 2 else "dve"
            K.copy(eng, Wfm[rows, cs], ps[rows, 0:128])
            K.copy(eng, Ybf[sl % NPQ].v(), ps[:, 128:192])

        def pre_u(sl, gi):
            Mfin = PMb[sl % NPQ][1][:, 2, :]
            ps = PSF()
            K.mm(ps[:, 0:64], Mfin, Ybf[sl % NPQ].v())
            K.copy("act" if gi % 2 else "dve", Ubar[sl].v(), ps[:, 0:64])

        def precompute(group):
            th = []
            for (sl, c, hl) in group:
                th.append(partial(pre_a, sl, c, hl))
            for lev in range(7):
                for gi, (sl, c, hl) in enumerate(group):
                    th.append(partial(pre_lev, sl, lev, gi))
            for gi, (sl, c, hl) in enumerate(group):
                th.append(partial(pre_w, sl, c, hl, gi))
            for gi, (sl, c, hl) in enumerate(group):
                th.append(partial(pre_u, sl, gi))
            return th

        def ser_u(sl, c, hl, gi):
            rows = slice(hl * 64, (hl + 1) * 64)
            cs = slice(c * 128, (c + 1) * 128)
            ut = Ut[gi % 4]
            ps = PSF()
            K.mm(ps[:, 0:64], Wfm[rows, cs], Sb[rows, :])
            K.tt("dve", ut.v(), ps[:, 0:64], Ubar[sl].v(), ALU.add)

        def ser_os(sl, c, hl, gi):
            rows = slice(hl * 64, (hl + 1) * 64)
            ut = Ut[gi % 4]
            pso = PSF()
            K.mm(pso[:, 0:64], AR[rows, c, 1, :], Sb[rows, :], start=True, stop=False)
            K.mm(pso[:, 0:64], ABK[sl][:, 0, :], ut.v(), start=False, stop=False)
            K.mm(pso[:, 0:64], ABK[sl][:, 1, :], TMq[:, c, 0, rows], start=False, stop=True)
            pss = PSF()
            K.mm(pss[:, 0:64], TMq[:, c, 1, :], ut.v(), start=True, stop=False)
            K.mm(pss[:, 0:64], TMq[:, c, 2, :], TMq[:, c, 0, rows], start=False, stop=True)
            K.stt(Sf[rows, :], Sf[rows, :], gL[rows, c:c + 1], pss[rows, 0:64], ALU.mult, ALU.add)
            K.copy("act", Sb[rows, :], Sf[rows, :])
            ot = Otile[c % 2]
            K.copy("act", ot[:, rows], pso[:, 0:64])
            if hl == 1:
                K.dma("sp", o_d[c * 128:(c + 1) * 128, hp * 128:(hp + 1) * 128], ot.v())

        def serial(group):
            th = []
            for gi, (sl, c, hl) in enumerate(group):
                th.append(partial(ser_u, sl, c, hl, gi))
                th.append(partial(ser_os, sl, c, hl, gi))
            return th

        def run_merged(lists):
            idx = [0] * len(lists)
            while True:
                best, bf = -1, 2.0
                for li, L in enumerate(lists):
                    if idx[li] < len(L):
                        frac = idx[li] / len(L)
                        if frac < bf:
                            best, bf = li, frac
                if best < 0:
                    break
                lists[best][idx[best]]()
                idx[best] += 1

        G = 8
        groups = []
        for g0 in range(0, 32, G):
            groups.append([((g0 + i) % NPR, pairs[g0 + i][0], pairs[g0 + i][1]) for i in range(G)])
        CH.extend(precompute(groups[0]))
        for gi_ in range(len(groups)):
            lists = [serial(groups[gi_])]
            if gi_ + 1 < len(groups):
                lists.append(precompute(groups[gi_ + 1]))
            CH.extend(merge_lists(lists))
        def fin_state():
            ps = PSF()
            K.tr(ps[:64, 0:128], Sf.v(), identf.v())
            K.copy("dve", wkvo.v(), ps[:64, 0:128])
            K.dma("sp", o_wkv_p[2 * hp:2 * hp + 2, :, :].re("h v k -> v h k"), wkvo.v().re("v (h k) -> v h k", h=2))

        CH.append(fin_state)
        return CH

    def drain(gen):
        for _ in gen:
            pass

    NA_EST = 150
    drain(derivedA(0))
    derivedB(0)
    for hp in range(4):
        CH = chunk_thunks(hp)
        OUTL = []
        gen = derivedA(hp + 1) if hp + 1 < 4 else None
        per = max(1, len(CH) // NA_EST)
        pero = max(1, len(CH) // max(1, len(OUTL)))
        oi = 0
        for ti, th in enumerate(CH):
            th()
            if gen is not None and ti % per == per - 1:
                try:
                    next(gen)
                except StopIteration:
                    gen = None
            if oi < len(OUTL) and ti % pero == pero - 1:
                OUTL[oi]()
                oi += 1
        while oi < len(OUTL):
            OUTL[oi]()
            oi += 1
        if gen is not None:
            drain(gen)
        if hp + 1 < 4:
            derivedB(hp + 1)

    K.pop_scope()
    if stage == 3:
        zt = K.sb("zt", [128, 4096], F32)
        K.memset("pool", zt.v(), 0.0)
        K.dma("sp", o_wkv_s.v(), zt[:, :])
        for i in range(16):
            K.dma("sp", y_p[i * 128:(i + 1) * 128, :], zt[:, 0:D])
        K.dma("sp", y_s.v(), zt[0:NS, 0:D])
        return early()

    def bcast_row(name, src, n):
        t = K.sb(name, [128, n], F32)
        K.dma("sp", t.v(), dap(src, 0, [[0, 128], [1, n]]))
        return t

    K.push_scope()
    hgl = K.sb("hgl2", [128, 4, TT], BF16)
    wmo = K.sb("wmo", [128, 8, D], BF16)
    gpost = K.sb("gpost_mix", [128, D], F32)
    PREF = [lambda: K.dma("sp", gpost.v(), dap(g_post_mix, 0, [[0, 128], [1, D]])),
            lambda: K.dma("sp", hgl.v().re("p k t -> p (k t)"), hgl_d.v())]
    for kt in range(8):
        PREF.append(partial(lambda kt_: K.dma("pool", wmo[:, kt_, :], w_merge_out[kt_ * 128:(kt_ + 1) * 128, :]), kt))
    mring = [K.sb("mring", [128, 8, 128], BF16) for _ in range(10)]
    mr_i = [0]

    def mload(src, c0, ncols_total, kt_n):
        mr_i[0] += 1
        wb = mring[mr_i[0] % len(mring)]
        K.dma("pool", wb[:, :kt_n, :], dap(src, c0, [[ncols_total, 128], [128 * ncols_total, kt_n], [1, 128]]))
        return wb

    GOFF = SHIFT + RW
    mw = {}

    def mg_load(f):
        mw[f] = (mload(w_rwkv_out, f * 128, D, 4), mload(w_in, GOFF + f * 128, PROJ, 8),
                 mload(w_in, GOFF + D + f * 128, PROJ, 8), mload(glu_w1, f * 128, D, 4), mload(glu_w2, f * 128, D, 4))

    PREF.append(partial(mg_load, 0))
    K.push_scope()
    eps_ln = K.sb("eps_ln2", [128, 1], F32)
    K.memset("pool", eps_ln.v(), 64e-5)
    NORF = 3
    fot_r = [K.sb("fot", [128, RW], F32) for _ in range(NORF)]
    fbt_r = [K.sb("fbt", [128, RW], F32) for _ in range(NORF)]
    fon_r = [K.sb("fon", [128, RW], F32) for _ in range(NORF)]
    fofb = [K.sb("fofb", [128, RW], BF16) for _ in range(NORF)]
    fst8 = [K.sb("fst8", [128, 6, 8], F32) for _ in range(NORF)]
    fh3 = lambda b_: b_.v().re("p (h n) -> p h n", h=8)

    def fout_a(c):
        r_ = c % NORF
        ot, bt, tmp, s8 = fot_r[r_], fbt_r[r_], fon_r[r_], fst8[r_]
        K.dma("sp", ot.v(), o_d[c * 128:(c + 1) * 128, :])
        K.dma("sp", bt.v(), bonus_d[c * 128:(c + 1) * 128, :])
        K.reduce(s8[:, 0, :], fh3(ot))
        K.act(tmp.v(), ot.v(), AF.Square)
        K.reduce(s8[:, 1, :], fh3(tmp))
        K.ts("dve", s8[:, 2, :], s8[:, 0, :], 1.0 / 64, None, ALU.mult)
        K.tt("dve", s8[:, 3, :], s8[:, 2, :], s8[:, 2, :], ALU.mult)
        K.stt(s8[:, 4, :], s8[:, 1, :], 1.0 / 64, s8[:, 3, :], ALU.mult, ALU.subtract)
        K.act(s8[:, 4, :], s8[:, 4, :], AF.Sqrt, bias=eps_ln.v(), scale=1.0)
        K.recip(s8[:, 5, :], s8[:, 4, :])
        K.stt(s8[:, 3, :], s8[:, 2, :], -1.0, s8[:, 5, :], ALU.mult, ALU.mult)

    def fout_b(c):
        r_ = c % NORF
        ot, bt, on, of, s8 = fot_r[r_], fbt_r[r_], fon_r[r_], fofb[r_], fst8[r_]
        for h in range(8):
            K.act(on[:, h * 64:(h + 1) * 64], ot[:, h * 64:(h + 1) * 64], AF.Identity,
                  bias=s8[:, 3, h:h + 1], scale=s8[:, 5, h:h + 1])
        K.tt("pool", bt.v(), bt.v(), lnxb.v(), ALU.add)
        K.tt("dve", on.v(), on.v(), lnxg.v(), ALU.mult)
        K.tt("dve", on.v(), on.v(), bt.v(), ALU.add)
        ps = PSF()
        K.mm(ps.v(), lora_bf[64:128, c * 128:(c + 1) * 128], lora_up[64:128, :])
        K.tt("dve", of.v(), on.v(), ps.v(), ALU.mult)
        pb = PSB()
        pv = pb.v().re("p (q t) -> p q t", q=8)
        for kt in range(4):
            K.tr(pv[:, kt, :], of[:, kt * 128:(kt + 1) * 128], identb.v())
        K.copy("act", oaT[:, :, c * 128:(c + 1) * 128], pv[:, 0:4, :])

    OUT = []
    for c in range(16 + 2):
        if c < 16:
            OUT.append(partial(fout_a, c))
        if 0 <= c - 2 < 16:
            OUT.append(partial(fout_b, c - 2))

    gs_tm = K.sb("gs_tm", [NS, RW], F32)
    ps = PSF()
    K.mm(ps[:NS, :], lora_bf[64:128, T:TT], lora_up[64:128, :])
    K.copy("act", gs_tm.v(), ps[:NS, :])
    K.dma("sp", dap(samp_d, 6 * 64, [[8 * 448, NS], [448, 8], [1, 64]]), gs_tm.v().re("p (h n) -> p h n", h=8))
    vec = K.sb("vec", [128, 7, 64], F32)
    K.dma("sp", vec.v().re("p q n -> p (q n)"), dap(samp_d, 0, [[448, 128], [1, 448]]))
    S0s = K.sb("S0s", [128, 64, 64], F32)
    S1s = K.sb("S1s", [128, 64, 64], F32)
    tS = K.sb("tS", [128, 64, 64], F32)
    K.dma("sp", S0s.v().re("p v k -> p (v k)"), st_wkv.v())
    r_, km_, v_, dec_, av_, b_, g_ = [vec[:, q, :] for q in range(7)]
    overv = lambda x: x.un(1).bc([128, 64, 64])
    overk = lambda x: x.un(2).bc([128, 64, 64])
    sv = K.sb("sv", [128, 8, 64], F32)
    ss = K.sb("ss", [128, 8], F32)
    eps2 = K.sb("eps_ln3", [128, 1], F32)
    K.memset("pool", eps2.v(), 64e-5)
    os_tm = K.sb("os_tm", [NS, RW], F32)
    os_bf = K.sb("os_bf", [NS, RW], BF16)
    SMP = [
        lambda: K.tt("dve", tS.v(), S0s.v(), overv(av_), ALU.mult),
        lambda: K.reduce(sv[:, 0, :], tS.v()),
        lambda: K.tt("pool", S1s.v(), S0s.v(), overv(dec_), ALU.mult),
        lambda: K.tt("dve", tS.v(), overk(sv[:, 0, :]), overv(b_), ALU.mult),
        lambda: K.tt("dve", S1s.v(), S1s.v(), tS.v(), ALU.add),
        lambda: K.tt("dve", tS.v(), overk(v_), overv(km_), ALU.mult),
        lambda: K.tt("dve", S1s.v(), S1s.v(), tS.v(), ALU.add),
        lambda: K.dma("sp", o_wkv_s.v(), S1s.v().re("p v k -> p (v k)")),
        lambda: K.tt("dve", tS.v(), S1s.v(), overv(r_), ALU.mult),
        lambda: K.reduce(sv[:, 1, :], tS.v()),
        lambda: K.reduce(ss[:, 0:1], sv[:, 1, :]),
        lambda: K.tt("dve", sv[:, 2, :], sv[:, 1, :], sv[:, 1, :], ALU.mult),
        lambda: K.reduce(ss[:, 1:2], sv[:, 2, :]),
        lambda: K.ts("dve", ss[:, 2:3], ss[:, 0:1], 1.0 / 64, None, ALU.mult),
        lambda: K.tt("dve", ss[:, 3:4], ss[:, 2:3], ss[:, 2:3], ALU.mult),
        lambda: K.stt(ss[:, 4:5], ss[:, 1:2], 1.0 / 64, ss[:, 3:4], ALU.mult, ALU.subtract),
        lambda: K.act(ss[:, 4:5], ss[:, 4:5], AF.Sqrt, bias=eps2.v(), scale=1.0),
        lambda: K.recip(ss[:, 5:6], ss[:, 4:5]),
        lambda: K.ts("dve", sv[:, 3, :], sv[:, 1, :], ss[:, 2:3], ss[:, 5:6], ALU.subtract, ALU.mult),
        lambda: K.tt("dve", sv[:, 3, :], sv[:, 3, :], par[:, 0, :], ALU.mult),
        lambda: K.tt("dve", sv[:, 3, :], sv[:, 3, :], par[:, 1, :], ALU.add),
        lambda: K.tt("dve", sv[:, 4, :], r_, km_, ALU.mult),
        lambda: K.tt("dve", sv[:, 4, :], sv[:, 4, :], par[:, 2, :], ALU.mult),
        lambda: K.reduce(ss[:, 6:7], sv[:, 4, :]),
        lambda: K.stt(sv[:, 3, :], v_, ss[:, 6:7], sv[:, 3, :], ALU.mult, ALU.add),
        lambda: K.tt("dve", sv[:, 5, :], sv[:, 3, :], g_, ALU.mult),
        lambda: K.dma("sp", dap(sampo_d, 0, [[64, 128], [1, 64]]), sv[:, 5, :]),
        lambda: K.dma("sp", os_tm.v(), sampo_d.v()),
        lambda: K.copy("dve", os_bf.v(), os_tm.v()),
    ]

    def smp_fin():
        pb = PSB()
        pv = pb.v().re("p (q t) -> p q t", q=8)
        for kt in range(4):
            K.tr(pv[:, kt, :NS], os_bf[:, kt * 128:(kt + 1) * 128], identb[:NS, :NS])
        K.copy("act", oaT[:, :, T:TT], pv[:, 0:4, :NS])

    SMP.append(smp_fin)
    run_merged_g([OUT, SMP, [lambda: None] * 6 + PREF])
    K.pop_scope()
    if stage == 4:
        zt = K.sb("zt", [128, 1024], F32)
        K.memset("pool", zt.v(), 0.0)
        for i in range(16):
            K.dma("sp", y_p[i * 128:(i + 1) * 128, :], zt[:, 0:D])
        K.dma("sp", y_s.v(), zt[0:NS, 0:D])
        return early()


    def pn_b(ps_halves, rows, res_src, gpost, dst, bufs):
        st, xres, tt_ = bufs[:3]
        K.dma("sp", xres[:rows, :], res_src)
        for hf in range(2):
            K.act(tt_[:rows, hf * 512:(hf + 1) * 512], ps_halves[hf][:rows, :], AF.Square, accum=st[:rows, hf:hf + 1])
        K.tt("dve", st[:rows, 2:3], st[:rows, 0:1], st[:rows, 1:2], ALU.add)
        K.act(st[:rows, 3:4], st[:rows, 2:3], AF.Sqrt, bias=epsT[:rows, :], scale=1.0 / D)
        K.recip(st[:rows, 4:5], st[:rows, 3:4])
        for hf in range(2):
            hs = slice(hf * 512, (hf + 1) * 512)
            K.stt(tt_[:rows, hs], ps_halves[hf][:rows, :], st[:rows, 4:5], gpost[:rows, hs], ALU.mult, ALU.mult)
        K.tt("dve", xres[:rows, :], xres[:rows, :], tt_[:rows, :], ALU.add)
        K.dma("sp", dst, xres[:rows, :])

    def pn_c(rows, bufs, nt):
        st, xres, tt_, xb = bufs
        gname, dstT, col0 = nt
        K.act(tt_[:rows, :], xres[:rows, :], AF.Square, accum=st[:rows, 5:6])
        K.act(st[:rows, 6:7], st[:rows, 5:6], AF.Sqrt, bias=epsT[:rows, :], scale=1.0 / D)
        K.recip(st[:rows, 7:8], st[:rows, 6:7])
        K.act(xb[:rows, :], xres[:rows, :], AF.Copy, scale=st[:rows, 7:8])
        pb = PSB()
        pv = pb.v().re("p (k t) -> p k t", k=8)
        for kt in range(8):
            K.tr(pv[:, kt, :rows], xb[:rows, kt * 128:(kt + 1) * 128], identb[:rows, :rows])
        g0 = PV[gname]
        K.tt("dve", dstT[:, :, col0:col0 + rows], pv[:, :, :rows],
             pvec[:, g0:g0 + 8].un(2).bc([128, 8, rows]), ALU.mult)

    mT = K.sb("mT", [128, 8, TT], BF16)
    mt = [[K.sb("mt", [128, 512], F32) for _ in range(4)] for _ in range(2)]
    mps = {}

    def mg_a1(f, tb, it):
        col0, n = TBLK[tb]
        cs = slice(col0, col0 + n)
        wA, wga, wgb, w1, w2 = mw[f]
        ps2, ps1 = PSF(), PSF()
        for kt in range(4):
            K.mm(ps2[:, :n], w2[:, kt, :], hgl[:, kt, cs], start=(kt == 0), stop=(kt == 3))
        for kt in range(4):
            K.mm(ps1[:, :n], w1[:, kt, :], hgl[:, kt, cs], start=(kt == 0), stop=(kt == 3))
        mps[(it, 1)] = (ps2, ps1)

    def mg_a2(f, tb, it):
        col0, n = TBLK[tb]
        cs = slice(col0, col0 + n)
        wA, wga, wgb, w1, w2 = mw[f]
        psgb = PSF()
        for kt in range(8):
            K.mm(psgb[:, :n], wgb[:, kt, :], hT[:, kt, cs], start=(kt == 0), stop=(kt == 7))
        mps[(it, 2)] = psgb

    def mg_a3(f, tb, it):
        col0, n = TBLK[tb]
        cs = slice(col0, col0 + n)
        wA, wga, wgb, w1, w2 = mw[f]
        psga, psA = PSF(), PSF()
        for kt in range(8):
            K.mm(psga[:, :n], wga[:, kt, :], hT[:, kt, cs], start=(kt == 0), stop=(kt == 7))
        for kt in range(4):
            K.mm(psA[:, :n], wA[:, kt, :], oaT[:, kt, cs], start=(kt == 0), stop=(kt == 3))
        mps[(it, 3)] = (psga, psA)

    def mg_b1(f, tb, it):
        col0, n = TBLK[tb]
        t0, t1_, t2_, t3_ = mt[it % 2]
        ps2, ps1 = mps.pop((it, 1))
        K.act(t0[:, :n], ps2[:, :n], AF.Sigmoid, bias=pcol("glu_b2", f))
        K.stt(t1_[:, :n], ps1[:, :n], pcol("glu_b1", f), t0[:, :n], ALU.add, ALU.mult)

    def mg_b2(f, tb, it):
        col0, n = TBLK[tb]
        t0, t1_, t2_, t3_ = mt[it % 2]
        psgb = mps.pop((it, 2))
        K.act(t2_[:, :n], psgb[:, :n], AF.Sigmoid)
        K.tt("dve", t1_[:, :n], t1_[:, :n], t2_[:, :n], ALU.mult)

    def mg_b3(f, tb, it):
        col0, n = TBLK[tb]
        cs = slice(col0, col0 + n)
        t0, t1_, t2_, t3_ = mt[it % 2]
        psga, psA = mps.pop((it, 3))
        K.act(t3_[:, :n], psga[:, :n], AF.Sigmoid)
        K.tt("dve", t3_[:, :n], psA[:, :n], t3_[:, :n], ALU.mult)
        K.tt("pool" if n > 16 else "dve", mT[:, f, cs], t3_[:, :n], t1_[:, :n], ALU.add)

    its = [(f, tb) for f in range(8) for tb in range(5)]
    for it, (f, tb) in enumerate(its):
        if tb == 0 and f + 1 < 8:
            mg_load(f + 1)
        mg_a1(f, tb, it)
        if it >= 1:
            pf, ptb = its[it - 1]
            mg_b3(pf, ptb, it - 1)
        mg_a2(f, tb, it)
        mg_b1(f, tb, it)
        mg_a3(f, tb, it)
        mg_b2(f, tb, it)
    mg_b3(its[-1][0], its[-1][1], len(its) - 1)
    pn_bufs = [(K.sb("pn_st", [128, 8], F32), K.sb("pn_x", [128, D], F32), K.sb("pn_t", [128, D], F32),
                K.sb("pn_xb", [128, D], BF16)) for _ in range(3)]
    mix_ps = {}

    def mix_a(i):
        rows = 128 if i < 16 else NS
        tcs = slice(i * 128, i * 128 + rows)
        halves = [PSF(), PSF()]
        for hf in range(2):
            for kt in range(8):
                K.mm(halves[hf][:rows, :], mT[:, kt, tcs], wmo[:, kt, hf * 512:(hf + 1) * 512],
                     start=(kt == 0), stop=(kt == 7))
        mix_ps[i] = halves

    def mix_b(i):
        rows = 128 if i < 16 else NS
        pn_b(mix_ps[i], rows, x_tile(i), gpost, x1_d[i * 128:i * 128 + rows, :], pn_bufs[i % 3])

    def mix_c(i):
        rows = 128 if i < 16 else NS
        pn_c(rows, pn_bufs[i % 3], ("g_pre_ffn", hT, i * 128))

    for step in range(17 + 2):
        if step < 17:
            mix_a(step)
        if 0 <= step - 1 < 17:
            mix_b(step - 1)
        if 0 <= step - 2 < 17:
            mix_c(step - 2)
    K.pop_scope()
    K.pop_scope()

    K.push_scope()
    NF = DFF // 128
    aT = K.sb("aT", [128, NF, TT], BF16)
    wd = K.sb("wd", [128, NF, D], BF16)
    gpost2 = bcast_row("gpost_ffn", g_post_ffn, D)
    fring = [K.sb("fring", [128, 8, 128], BF16) for _ in range(4)]
    fr_i = [0]

    def fload(src, c0):
        fr_i[0] += 1
        wb = fring[fr_i[0] % len(fring)]
        K.dma("pool", wb.v(), dap(src, c0, [[DFF, 128], [128 * DFF, 8], [1, 128]]))
        return wb

    sgt = [K.sb("sgt", [128, 512], F32) for _ in range(2)]
    it = 0
    for ft in range(NF):
        wg = fload(w_ffn_gate, ft * 128)
        wu = fload(w_ffn_up, ft * 128)
        if ft >= 1:
            K.dma("pool", wd[:, ft - 1, :], w_ffn_down[(ft - 1) * 128:ft * 128, :])
        if ft == NF - 1:
            K.dma("pool", wd[:, ft, :], w_ffn_down[ft * 128:(ft + 1) * 128, :])
        for tb in range(5):
            col0, n = TBLK[tb]
            cs = slice(col0, col0 + n)
            psg, psu = PSF(), PSF()
            for kt in range(8):
                K.mm(psg[:, :n], wg[:, kt, :], hT[:, kt, cs], start=(kt == 0), stop=(kt == 7))
            for kt in range(8):
                K.mm(psu[:, :n], wu[:, kt, :], hT[:, kt, cs], start=(kt == 0), stop=(kt == 7))
            sg = sgt[it % 2]
            it += 1
            K.act(sg[:, :n], psg[:, :n], AF.Silu)
            K.tt("dve", aT[:, ft, cs], psu[:, :n], sg[:, :n], ALU.mult)
    pn_bufs = [(K.sb("pn_st", [128, 8], F32), K.sb("pn_x", [128, D], F32), K.sb("pn_t", [128, D], F32))
               for _ in range(2)]
    dn_ps = {}

    def dn_a(i):
        rows = 128 if i < 16 else NS
        tcs = slice(i * 128, i * 128 + rows)
        halves = [PSF(), PSF()]
        for hf in range(2):
            for kt in range(NF):
                K.mm(halves[hf][:rows, :], aT[:, kt, tcs], wd[:, kt, hf * 512:(hf + 1) * 512],
                     start=(kt == 0), stop=(kt == NF - 1))
        dn_ps[i] = halves

    def dn_b(i):
        rows = 128 if i < 16 else NS
        dst = y_p[i * 128:(i + 1) * 128, :] if i < 16 else y_s[:, :]
        pn_b(dn_ps[i], rows, x1_d[i * 128:i * 128 + rows, :], gpost2, dst, pn_bufs[i % 2])

    for step in range(17 + 1):
        if step < 17:
            dn_a(step)
        if step - 1 >= 0:
            dn_b(step - 1)
    K.pop_scope()
    K.finish()
    es.close()
    return nc


_W_NAMES = ["norm_pre_mix", "norm_post_mix", "norm_pre_ffn", "norm_post_ffn", "w_in", "mu_shift", "w0",
            "w_decay_up", "a0", "w_aaa_up", "w_gate_up", "k_k", "k_a", "r_k", "lnx_g", "lnx_b", "w_rwkv_out",
            "s5_lam_re", "s5_lam_im", "s5_log_dt", "s5_b_re", "s5_b_im", "s5_c_re", "s5_c_im", "s5_d",
            "glu_w1", "glu_b1", "glu_w2", "glu_b2", "w_merge_out", "w_ffn_gate", "w_ffn_up", "w_ffn_down"]


def make_in_maps(inputs, cores):
    f = lambda a: np.ascontiguousarray(np.asarray(a, dtype=np.float32))
    shared = {n: f(inputs[n])[0] for n in _W_NAMES}
    maps = []
    for c in cores:
        m = dict(shared)
        m["x_p"] = f(inputs["x_prompt"][c])
        m["x_s"] = f(inputs["x_sample"][c * NS:(c + 1) * NS, 0])
        m["st_shift"] = f(inputs["state_shift"][0, c * NS:(c + 1) * NS])
        m["st_wkv"] = f(inputs["state_wkv"][0, c * NS:(c + 1) * NS]).reshape(NS * 8, 64 * 64)
        m["st_re"] = f(inputs["state_s5_re"][0, c * NS:(c + 1) * NS]).reshape(NS, 2048)
        m["st_im"] = f(inputs["state_s5_im"][0, c * NS:(c + 1) * NS]).reshape(NS, 2048)
        maps.append(m)
    return maps


def assemble(results):
    n = len(results)
    cat = lambda k: np.concatenate([np.asarray(r[k]) for r in results], axis=0)
    y_p = np.stack([np.asarray(r["y_p"]) for r in results], 0)
    y_s = cat("y_s").reshape(n * NS, 1, D)
    sh_p = cat("o_shift_p").reshape(1, n, SHIFT)
    wkv_p = np.stack([np.asarray(r["o_wkv_p"]) for r in results], 0).reshape(1, n, 8, 64, 64)
    re_p = cat("o_re_p").reshape(1, n, 32, 64)
    im_p = cat("o_im_p").reshape(1, n, 32, 64)
    sh_s = cat("o_shift_s").reshape(1, n * NS, SHIFT)
    wkv_s = cat("o_wkv_s").reshape(1, n * NS, 8, 64, 64)
    re_s = cat("o_re_s").reshape(1, n * NS, 32, 64)
    im_s = cat("o_im_s").reshape(1, n * NS, 32, 64)
    return tuple(np.ascontiguousarray(a, dtype=np.float32) for a in
                 (y_p, y_s, sh_p, wkv_p, re_p, im_p, sh_s, wkv_s, re_s, im_s))


def kernel(**inputs):
    nc = build()
    in_maps = make_in_maps(inputs, list(range(NCORES)))
    res = run_bass_kernel_spmd(nc, in_maps, core_ids=list(range(NCORES)))
    return assemble(res.results)
```

```python
import math
from contextlib import ExitStack
from functools import partial
import numpy as np
import concourse.bass as bass
import concourse.mybir as mybir
from concourse.bass_utils import run_bass_kernel_spmd

F32 = mybir.dt.float32
BF16 = mybir.dt.bfloat16
I32 = mybir.dt.int32
AF = mybir.ActivationFunctionType
ALU = mybir.AluOpType
AX = mybir.AxisListType

T = 2048
NS = 16
TT = T + NS
D = 1024
RW = 512
SHIFT = 1664
PROJ = 4224
DFF = 2816
NCORES = 8
CDEC = math.exp(-0.5)
TBLK = [(0, 512), (512, 512), (1024, 512), (1536, 512), (2048, 16)]
SAME_ENGINE_SYNC = True
LS = 4


class V:
    __slots__ = ("buf", "ap")

    def __init__(self, buf, ap):
        self.buf = buf
        self.ap = ap

    def __getitem__(self, idx):
        return V(self.buf, self.ap[idx])

    def bc(self, shape):
        return V(self.buf, self.ap.to_broadcast(list(shape)))

    def re(self, s, **kw):
        return V(self.buf, self.ap.rearrange(s, **kw))

    def un(self, axis):
        return V(self.buf, self.ap.unsqueeze(axis))

    def bitcast(self, dt):
        return V(self.buf, self.ap.bitcast(dt))


class Buf:
    def __init__(self, name, handle):
        self.name = name
        self.h = handle
        self.writes = {}
        self.reads = {}
        self.dsem = None
        self.dcnt = 0
        self.is_psum = False

    def __getitem__(self, idx):
        return V(self, self.h[idx])

    def v(self):
        return V(self, self.h[:])


class Kern:
    def __init__(self, nc, es):
        self.nc = nc
        self.es = es
        self.root_es = es
        self.eng = {"pe": nc.tensor, "act": nc.scalar, "dve": nc.vector, "pool": nc.gpsimd, "sp": nc.sync}
        self.sem = {}
        self.cnt = {}
        for e in ("pe", "act", "dve", "pool"):
            self.sem[e] = es.enter_context(nc.semaphore("s_" + e))
            self.cnt[e] = 0
        self.obs = {e: {} for e in self.eng}
        self.semname = {}
        self.dsems = []
        self.nbuf = 0
        self.psf_i = 0
        self.psb_i = 0

    def sb(self, name, shape, dt):
        self.nbuf += 1
        import os
        if os.environ.get("KDBG"):
            sz = int(np.prod(shape[1:])) * (2 if dt == BF16 else 4)
            self._tot = getattr(self, "_tot", 0) + sz
            print(f"alloc {name} {shape} {sz} depth={len(getattr(self, '_saved', []))} cum_noFree={self._tot}")
        h = self.es.enter_context(self.nc.sbuf_tensor(f"{name}_{self.nbuf}", list(shape), dt))
        b = Buf(name, h)
        b.writes = dict(getattr(self, "pending", {}))
        if not hasattr(self, "scope_bufs"):
            self.scope_bufs = [[]]
        self.scope_bufs[-1].append(b)
        return b

    def barrier(self):
        for e in ("pe", "act", "dve", "pool", "sp"):
            for e2 in ("pe", "act", "dve", "pool"):
                if e2 != e and self.cnt[e2]:
                    self._wait(e, e2, self.sem[e2], self.cnt[e2])
            for ob in self.dsems:
                self._wait(e, "d:" + ob.name + str(id(ob)), ob.dsem, ob.dcnt)

    def push_scope(self):
        self._saved = getattr(self, "_saved", [])
        self._saved.append(self.es)
        self.es = ExitStack()
        if not hasattr(self, "scope_bufs"):
            self.scope_bufs = [[]]
        self.scope_bufs.append([])

    def pop_scope(self):
        self.pending = getattr(self, "pending", {})
        for b in self.scope_bufs.pop():
            for d in (b.writes, b.reads):
                for k, (sm, v) in d.items():
                    if self.pending.get(k, (None, 0))[1] < v:
                        self.pending[k] = (sm, v)
        self.es.close()
        self.es = self._saved.pop()

    def ps(self, name, shape, dt):
        self.nbuf += 1
        h = self.es.enter_context(self.nc.psum_tensor(f"{name}_{self.nbuf}", list(shape), dt))
        b = Buf(name, h)
        b.is_psum = True
        return b

    def dram(self, name, shape, dt, kind="Internal"):
        h = self.nc.dram_tensor(name, list(shape), dt, kind=kind)
        return Buf(name, h.ap())

    def newsem(self, name):
        s = self.root_es.enter_context(self.nc.semaphore(name))
        return s

    def _wait(self, e, sem_key, sem, val):
        o = self.obs[e]
        if o.get(sem_key, 0) >= val:
            return
        self.eng[e].wait_ge(sem, val)
        o[sem_key] = val

    def _deps(self, e, W, R):
        need = {}
        for b in R:
            for k, (s, v) in b.writes.items():
                if need.get(k, (None, 0))[1] < v:
                    need[k] = (s, v)
            if b.is_psum:
                for k, (s, v) in b.reads.items():
                    if k != e and need.get(k, (None, 0))[1] < v:
                        need[k] = (s, v)
        for b in W:
            for k, (s, v) in b.writes.items():
                if need.get(k, (None, 0))[1] < v:
                    need[k] = (s, v)
            for k, (s, v) in b.reads.items():
                if need.get(k, (None, 0))[1] < v:
                    need[k] = (s, v)
        for k, (s, v) in need.items():
            if k == e:
                if e == "pe" or not SAME_ENGINE_SYNC:
                    continue
            self._wait(e, k, s, v)

    def _bufs(self, vs):
        out = []
        for x in vs:
            if isinstance(x, V):
                if x.buf not in out:
                    out.append(x.buf)
            elif isinstance(x, Buf):
                if x not in out:
                    out.append(x)
        return out

    def op(self, e, fn, W, R):
        Wb = self._bufs(W)
        Rb = self._bufs(R)
        self._deps(e, Wb, Rb)
        inst = fn()
        self.cnt[e] += 1
        c = self.cnt[e]
        inst.then_inc(self.sem[e], 1)
        for b in Wb:
            b.reads = {}
            b.writes[e] = (self.sem[e], c)
        for b in Rb:
            if b not in Wb:
                b.reads[e] = (self.sem[e], c)
        return inst

    def dma(self, q, out, in_, sem_buf=None, slow=False):
        Wb = self._bufs([out])
        Rb = self._bufs([in_])
        self._deps(q, Wb, Rb)
        ob = sem_buf if sem_buf is not None else out.buf
        if ob.dsem is None:
            ob.dsem = self.newsem("d_" + ob.name + str(len(self.dsems)))
            self.dsems.append(ob)
        if slow:
            with self.nc.allow_non_contiguous_dma(reason="small parameter layout load"):
                inst = self.eng[q].dma_start(out=out.ap, in_=in_.ap)
        else:
            inst = self.eng[q].dma_start(out=out.ap, in_=in_.ap)
        ob.dcnt += 16
        inst.then_inc(ob.dsem, 16)
        key = "d:" + ob.name + str(id(ob))
        for b in Wb:
            b.reads = {}
            b.writes[key] = (ob.dsem, ob.dcnt)
        for b in Rb:
            if b not in Wb:
                b.reads[key] = (ob.dsem, ob.dcnt)
        return inst

    def finish(self):
        for ob in self.dsems:
            self._wait("sp", "d:" + ob.name + str(id(ob)), ob.dsem, ob.dcnt)
        for e in ("pe", "act", "dve", "pool"):
            if self.cnt[e]:
                self._wait("sp", e, self.sem[e], self.cnt[e])

    @staticmethod
    def _a(x):
        return x.ap if isinstance(x, V) else x

    def act(self, out, in_, func, bias=None, scale=None, accum=None):
        kw = {}
        if bias is not None:
            kw["bias"] = self._a(bias)
        if scale is not None:
            kw["scale"] = self._a(scale)
        if accum is not None:
            kw["accum_out"] = self._a(accum)
        return self.op("act", lambda: self.nc.scalar.activation(out=out.ap, in_=in_.ap, func=func, **kw),
                       [out, accum], [in_, bias, scale])

    def tt(self, e, out, in0, in1, op):
        return self.op(e, lambda: self.eng[e].tensor_tensor(out=out.ap, in0=in0.ap, in1=in1.ap, op=op),
                       [out], [in0, in1])

    def ts(self, e, out, in0, s1, s2, op0, op1=None):
        if op1 is None:
            return self.op(e, lambda: self.eng[e].tensor_scalar(out=out.ap, in0=in0.ap, scalar1=self._a(s1),
                                                               scalar2=None, op0=op0), [out], [in0, s1])
        return self.op(e, lambda: self.eng[e].tensor_scalar(out=out.ap, in0=in0.ap, scalar1=self._a(s1),
                                                           scalar2=self._a(s2), op0=op0, op1=op1),
                       [out], [in0, s1, s2])

    def stt(self, out, in0, scalar, in1, op0, op1, e="dve"):
        return self.op(e, lambda: self.eng[e].scalar_tensor_tensor(out=out.ap, in0=in0.ap, scalar=self._a(scalar),
                                                                  in1=in1.ap, op0=op0, op1=op1),
                       [out], [in0, scalar, in1])

    def copy(self, e, out, in_):
        if e == "act":
            return self.act(out, in_, AF.Copy)
        return self.op(e, lambda: self.eng[e].tensor_copy(out=out.ap, in_=in_.ap), [out], [in_])

    def memset(self, e, out, val):
        return self.op(e, lambda: self.eng[e].memset(out.ap, val), [out], [])

    def recip(self, out, in_):
        return self.op("dve", lambda: self.nc.vector.reciprocal(out=out.ap, in_=in_.ap), [out], [in_])

    def reduce(self, out, in_, op=ALU.add, axis=AX.X):
        return self.op("dve", lambda: self.nc.vector.tensor_reduce(out=out.ap, in_=in_.ap, axis=axis, op=op),
                       [out], [in_])

    def scan(self, out, d0, d1, init, op0=ALU.mult, op1=ALU.add):
        return self.op("dve", lambda: self.nc.vector.tensor_tensor_scan(out=out.ap, data0=d0.ap, data1=d1.ap,
                                                                       initial=self._a(init), op0=op0, op1=op1),
                       [out], [d0, d1, init])

    def mm(self, out, lhsT, rhs, start=True, stop=True):
        return self.op("pe", lambda: self.nc.tensor.matmul(out.ap, lhsT=lhsT.ap, rhs=rhs.ap, start=start, stop=stop),
                       [out], [lhsT, rhs])

    def tr(self, out, in_, ident):
        return self.op("pe", lambda: self.nc.tensor.transpose(out.ap, in_.ap, ident.ap), [out], [in_, ident])

    def aselect(self, out, in_, pattern, cmp, fill, base, cm):
        return self.op("pool", lambda: self.nc.gpsimd.affine_select(out=out.ap, in_=in_.ap, pattern=pattern,
                                                                   compare_op=cmp, fill=fill, base=base,
                                                                   channel_multiplier=cm), [out], [in_])

    def iota(self, out, pattern, base, cm):
        return self.op("pool", lambda: self.nc.gpsimd.iota(out.ap, pattern=pattern, base=base,
                                                          channel_multiplier=cm), [out], [])


def run_merged_g(lists):
    idx = [0] * len(lists)
    while True:
        best, bf = -1, 2.0
        for li, L in enumerate(lists):
            if idx[li] < len(L):
                frac = idx[li] / len(L)
                if frac < bf:
                    best, bf = li, frac
        if best < 0:
            break
        lists[best][idx[best]]()
        idx[best] += 1


def merge_lists(lists):
    out = []
    idx = [0] * len(lists)
    while True:
        best, bf = -1, 2.0
        for li, L in enumerate(lists):
            if idx[li] < len(L):
                frac = idx[li] / len(L)
                if frac < bf:
                    best, bf = li, frac
        if best < 0:
            break
        out.append(lists[best][idx[best]])
        idx[best] += 1
    return out


def dap(buf, offset, ap):
    base = buf.h if not hasattr(buf.h, "ap") or isinstance(buf.h, bass.AP) else buf.h
    t = base.tensor if isinstance(base, bass.AP) else base
    return V(buf, bass.AP(t, offset, [list(x) for x in ap]))


def build(stage=99):
    nc = bass.Bass("TRN2", target_bir_lowering=False)
    es = ExitStack()
    K = Kern(nc, es)

    def din(name, shape, dt=F32):
        return Buf(name, nc.dram_tensor(name, list(shape), dt, kind="ExternalInput").ap())

    def dout(name, shape):
        return Buf(name, nc.dram_tensor(name, list(shape), F32, kind="ExternalOutput").ap())

    x_p = din("x_p", [T, D])
    x_s = din("x_s", [NS, D])
    st_shift = din("st_shift", [NS, SHIFT])
    st_wkv = din("st_wkv", [NS * 8, 64 * 64])
    st_re = din("st_re", [NS, 2048])
    st_im = din("st_im", [NS, 2048])
    g_pre_mix = din("norm_pre_mix", [D])
    g_post_mix = din("norm_post_mix", [D])
    g_pre_ffn = din("norm_pre_ffn", [D])
    g_post_ffn = din("norm_post_ffn", [D])
    w_in = din("w_in", [D, PROJ])
    mu_shift = din("mu_shift", [SHIFT])
    w0 = din("w0", [RW])
    w_decay_up = din("w_decay_up", [32, RW])
    a0 = din("a0", [RW])
    w_aaa_up = din("w_aaa_up", [32, RW])
    w_gate_up = din("w_gate_up", [64, RW])
    k_k = din("k_k", [RW])
    k_a = din("k_a", [RW])
    r_k = din("r_k", [RW])
    lnx_g = din("lnx_g", [RW])
    lnx_b = din("lnx_b", [RW])
    w_rwkv_out = din("w_rwkv_out", [RW, D])
    s5_lam_re = din("s5_lam_re", [32, 64])
    s5_lam_im = din("s5_lam_im", [32, 64])
    s5_log_dt = din("s5_log_dt", [32])
    s5_b_re = din("s5_b_re", [32, 64, 16])
    s5_b_im = din("s5_b_im", [32, 64, 16])
    s5_c_re = din("s5_c_re", [32, 16, 64])
    s5_c_im = din("s5_c_im", [32, 16, 64])
    s5_d = din("s5_d", [RW])
    glu_w1 = din("glu_w1", [RW, D])
    glu_b1 = din("glu_b1", [D])
    glu_w2 = din("glu_w2", [RW, D])
    glu_b2 = din("glu_b2", [D])
    w_merge_out = din("w_merge_out", [D, D])
    w_ffn_gate = din("w_ffn_gate", [D, DFF])
    w_ffn_up = din("w_ffn_up", [D, DFF])
    w_ffn_down = din("w_ffn_down", [DFF, D])

    y_p = dout("y_p", [T, D])
    y_s = dout("y_s", [NS, D])
    o_shift_p = dout("o_shift_p", [1, SHIFT])
    o_wkv_p = dout("o_wkv_p", [8, 64, 64])
    o_re_p = dout("o_re_p", [1, 2048])
    o_im_p = dout("o_im_p", [1, 2048])
    o_shift_s = dout("o_shift_s", [NS, SHIFT])
    o_wkv_s = dout("o_wkv_s", [NS * 8, 64 * 64])
    o_re_s = dout("o_re_s", [NS, 2048])
    o_im_s = dout("o_im_s", [NS, 2048])

    x1_d = K.dram("x1_scratch", [TT, D], F32)

    psf = [K.ps("psf", [128, 512], F32) for _ in range(8)]

    def PSF():
        K.psf_i += 1
        return psf[K.psf_i % len(psf)]

    class BV:
        def __init__(self, buf):
            self.buf = buf

        def v(self):
            return V(self.buf, self.buf.h[:].bitcast(BF16))

        def __getitem__(self, idx):
            return self.v()[idx]

    def PSB():
        return BV(PSF())

    ones_f = K.sb("ones_f", [128, 128], F32)
    identf = K.sb("identf", [128, 128], F32)
    identb = K.sb("identb", [128, 128], BF16)
    epsT = K.sb("epsT", [128, 1], F32)
    K.memset("pool", ones_f.v(), 1.0)
    K.memset("pool", epsT.v(), 1e-6)
    K.aselect(identf.v(), ones_f.v(), [[-1, 128]], ALU.is_equal, 0.0, 0, 1)
    K.copy("pool", identb.v(), identf.v())

    pvec = K.sb("pvec", [128, 128], F32)
    PV = {}
    _pv = [0]

    def load_fm(name, src, ntile):
        c0 = _pv[0]
        _pv[0] += ntile
        K.dma("sp", pvec[:, c0:c0 + ntile], dap(src, 0, [[1, 128], [128, ntile]]), sem_buf=pvec, slow=True)
        PV[name] = c0
        return c0

    load_fm("g_pre_mix", g_pre_mix, 8)
    load_fm("g_pre_ffn", g_pre_ffn, 8)
    load_fm("mu", mu_shift, 13)
    load_fm("w0", w0, 4)
    load_fm("a0", a0, 4)
    load_fm("k_k", k_k, 4)
    load_fm("k_a", k_a, 4)
    load_fm("r_k", r_k, 4)
    load_fm("s5_d", s5_d, 4)
    load_fm("glu_b1", glu_b1, 8)
    load_fm("glu_b2", glu_b2, 8)

    def early():
        while getattr(K, "_saved", []):
            K.pop_scope()
        K.finish()
        es.close()
        return nc

    if stage == 0.1:
        return early()

    def pcol(name, i=0):
        c = PV[name] + i
        return pvec[:, c:c + 1]

    hT = K.sb("hT", [128, 8, TT], BF16)
    def norm_transpose(src_tile_fn, gname, dst):
        K.push_scope()
        NB = 4
        xring = [K.sb("xring", [128, D], F32) for _ in range(NB)]
        xn = [K.sb("xn", [128, D], BF16) for _ in range(NB)]
        junk = K.sb("junk", [128, D], BF16)
        stat = [K.sb("stat", [128, 4], F32) for _ in range(NB)]

        def nt_a(i):
            rows = 128 if i < 16 else NS
            xt, st, xb = xring[i % NB], stat[i % NB], xn[i % NB]
            K.dma("sp", xt[:rows, :], src_tile_fn(i))
            K.act(junk[:rows, :], xt[:rows, :], AF.Square, accum=st[:rows, 0:1])
            K.act(st[:rows, 1:2], st[:rows, 0:1], AF.Sqrt, bias=epsT[:rows, :], scale=1.0 / D)
            K.recip(st[:rows, 2:3], st[:rows, 1:2])
            K.act(xb[:rows, :], xt[:rows, :], AF.Copy, scale=st[:rows, 2:3])

        def nt_b(i):
            rows = 128 if i < 16 else NS
            col0 = i * 128
            xb = xn[i % NB]
            pb = PSB()
            pv = pb.v().re("p (k t) -> p k t", k=8)
            for kt in range(8):
                K.tr(pv[:, kt, :rows], xb[:rows, kt * 128:(kt + 1) * 128], identb[:rows, :rows])
            g0 = PV[gname]
            K.tt("dve", dst[:, :, col0:col0 + rows], pv[:, :, :rows],
                 pvec[:, g0:g0 + 8].un(2).bc([128, 8, rows]), ALU.mult)

        for step in range(17 + 2):
            if step < 17:
                nt_a(step)
            if 0 <= step - 2 < 17:
                nt_b(step - 2)
        K.pop_scope()

    def x_tile(i):
        return x_p[i * 128:(i + 1) * 128, :] if i < 16 else x_s[:, :]

    norm_transpose(x_tile, "g_pre_mix", hT)
    if stage == 0.2:
        return early()

    wring = [K.sb("wring", [128, 8, 128], BF16) for _ in range(3)]
    wr_i = [0]

    def load_w_cols(src, c0, ncols_total, kt_n=8, width=128):
        wr_i[0] += 1
        wb = wring[wr_i[0] % len(wring)]
        K.dma("pool", wb[:, :kt_n, :width],
              dap(src, c0, [[ncols_total, 128], [128 * ncols_total, kt_n], [1, width]]))
        return wb

    def proj_fm(wb, kt_n, src_act, tb, ps):
        col0, n = TBLK[tb]
        for kt in range(kt_n):
            K.mm(ps[:, :n], wb[:, kt, :], src_act[:, kt, col0:col0 + n], start=(kt == 0), stop=(kt == kt_n - 1))

    hgl_d = K.dram("hgl_scratch", [128, 4 * TT], BF16)
    TWO_PI = 2.0 * math.pi

    K.push_scope()
    ubf = K.sb("ubf", [128, 4, TT], BF16)
    wu4 = [K.sb("wu4", [128, 8, 128], BF16) for _ in range(4)]
    for ft in range(4):
        K.dma("pool", wu4[ft].v(), dap(w_in, SHIFT + ft * 128, [[PROJ, 128], [128 * PROJ, 8], [1, 128]]))
        for tb in range(5):
            col0, n = TBLK[tb]
            ps = PSF()
            proj_fm(wu4[ft], 8, hT, tb, ps)
            K.copy("act" if tb % 2 else "dve", ubf[:, ft, col0:col0 + n], ps[:, :n])

    if stage == 1.1:
        return early()
    sp_ = K.sb("s5par", [128, 16, 24], F32)
    (P_LRE, P_LIM, P_DT, P_MAG, P_TH, P_FRE, P_FIM, P_LBRE, P_LBIM, P_T0, P_T1, P_T2, P_T3,
     P_TH2, P_MAG2, P_G0R, P_G0I, P_C1, P_S1) = range(19)

    def sp(k):
        return sp_[:, :, k]

    K.dma("sp", sp(P_LRE), dap(s5_lam_re, 0, [[1, 128], [128, 16]]), slow=True)
    K.dma("sp", sp(P_LIM), dap(s5_lam_im, 0, [[1, 128], [128, 16]]), slow=True)
    for gl in range(2):
        K.dma("sp", sp_[gl * 64:(gl + 1) * 64, :, P_DT], dap(s5_log_dt, gl, [[0, 64], [2, 16]]), slow=True)
    K.act(sp(P_DT), sp(P_DT), AF.Exp)
    K.tt("dve", sp(P_T0), sp(P_LRE), sp(P_DT), ALU.mult)
    K.act(sp(P_MAG), sp(P_T0), AF.Exp)
    K.copy("dve", sp(P_MAG2), sp(P_MAG))
    for _k in range(LS - 1):
        K.tt("dve", sp(P_MAG2), sp(P_MAG2), sp(P_MAG), ALU.mult)
    K.tt("dve", sp(P_TH), sp(P_LIM), sp(P_DT), ALU.mult)
    K.ts("dve", sp(P_TH2), sp(P_TH), float(LS), None, ALU.mult)

    NH = 512 // LS
    LC = min(128, NH)
    NRC = NH // LC
    pw = K.sb("pw", [128, 16, 2 * (LS + 1)], F32)
    cosT = K.sb("cosT", [128, 16, LC], F32)
    sinT = K.sb("sinT", [128, 16, LC], F32)
    BT = [K.sb("BT", [128, 32, 128], BF16) for _ in range(LS)]
    CTb = K.sb("CTb", [128, 32, 128], BF16)
    VTb = [K.sb("VTb", [128, 32, 128], BF16) for _ in range(LS - 1)]
    Kt = [K.sb("Kt", [128, 4, 128], BF16) for _ in range(LS - 1)]
    K.push_scope()
    ji = K.sb("ji", [128, LC], I32)
    jf = K.sb("jf", [128, LC], F32)
    ang = K.sb("ang", [128, 16, LC], F32)
    rr = K.sb("rr", [128, 16, LC], F32)
    kf = K.sb("kf", [128, 16, LC], F32)
    ki = K.sb("ki", [128, 16, LC], I32)
    K.iota(ji.v(), [[1, LC]], 1, 0)
    K.copy("dve", jf.v(), ji.v())
    K.tt("dve", ang.v(), sp(P_TH2).un(2).bc([128, 16, LC]), jf.v().un(1).bc([128, 16, LC]), ALU.mult)

    def sin_reduced(out, a_in, shift, rr_, kf_, ki_):
        C1 = 6.28125
        C2 = TWO_PI - C1
        K.ts("dve", rr_, a_in, shift, 1.0 / TWO_PI, ALU.add, ALU.mult)
        K.copy("dve", ki_, rr_)
        K.copy("dve", kf_, ki_)
        K.ts("dve", rr_, a_in, shift, None, ALU.add)
        K.stt(rr_, kf_, -C1, rr_, ALU.mult, ALU.add)
        K.stt(rr_, kf_, -C2, rr_, ALU.mult, ALU.add)
        K.ts("dve", kf_, rr_, math.pi, None, ALU.is_gt)
        K.stt(rr_, kf_, -TWO_PI, rr_, ALU.mult, ALU.add)
        K.ts("dve", kf_, rr_, -math.pi, None, ALU.is_lt)
        K.stt(rr_, kf_, TWO_PI, rr_, ALU.mult, ALU.add)
        K.ts("dve", rr_, rr_, -math.pi, math.pi, ALU.max, ALU.min)
        K.act(out, rr_, AF.Sin)

    sin_reduced(sinT.v(), ang.v(), 0.0, rr.v(), kf.v(), ki.v())
    K.ts("dve", rr.v(), rr.v(), 0.5 * math.pi, None, ALU.add)
    K.ts("dve", kf.v(), rr.v(), math.pi, None, ALU.is_gt)
    K.stt(rr.v(), kf.v(), -TWO_PI, rr.v(), ALU.mult, ALU.add)
    K.ts("dve", rr.v(), rr.v(), -math.pi, math.pi, ALU.max, ALU.min)
    K.act(cosT.v(), rr.v(), AF.Sin)
    sin_reduced(sp(P_S1), sp(P_TH), 0.0, rr[:, :, 0], kf[:, :, 0], ki[:, :, 0])
    sin_reduced(sp(P_C1), sp(P_TH), 0.5 * math.pi, rr[:, :, 0], kf[:, :, 0], ki[:, :, 0])
    K.tt("dve", sp(P_LBRE), sp(P_MAG), sp(P_C1), ALU.mult)
    K.tt("dve", sp(P_LBIM), sp(P_MAG), sp(P_S1), ALU.mult)
    K.tt("dve", sp(P_T0), sp(P_LRE), sp(P_LRE), ALU.mult)
    K.tt("dve", sp(P_T1), sp(P_LIM), sp(P_LIM), ALU.mult)
    K.tt("dve", sp(P_T0), sp(P_T0), sp(P_T1), ALU.add)
    K.recip(sp(P_T0), sp(P_T0))
    K.ts("dve", sp(P_T1), sp(P_LBRE), -1.0, None, ALU.add)
    K.tt("dve", sp(P_T2), sp(P_T1), sp(P_LRE), ALU.mult)
    K.tt("dve", sp(P_T3), sp(P_LBIM), sp(P_LIM), ALU.mult)
    K.tt("dve", sp(P_T2), sp(P_T2), sp(P_T3), ALU.add)
    K.tt("dve", sp(P_FRE), sp(P_T2), sp(P_T0), ALU.mult)
    K.tt("dve", sp(P_T2), sp(P_LBIM), sp(P_LRE), ALU.mult)
    K.tt("dve", sp(P_T3), sp(P_T1), sp(P_LIM), ALU.mult)
    K.tt("dve", sp(P_T2), sp(P_T2), sp(P_T3), ALU.subtract)
    K.tt("dve", sp(P_FIM), sp(P_T2), sp(P_T0), ALU.mult)
    K.memset("dve", pw[:, :, 0], 1.0)
    K.memset("dve", pw[:, :, 1], 0.0)
    for k in range(1, LS + 1):
        pr_, pi_ = pw[:, :, 2 * (k - 1)], pw[:, :, 2 * (k - 1) + 1]
        K.tt("dve", sp(P_T0), pr_, sp(P_LBRE), ALU.mult)
        K.tt("dve", sp(P_T1), pi_, sp(P_LBIM), ALU.mult)
        K.tt("dve", pw[:, :, 2 * k], sp(P_T0), sp(P_T1), ALU.subtract)
        K.tt("dve", sp(P_T0), pr_, sp(P_LBIM), ALU.mult)
        K.tt("dve", sp(P_T1), pi_, sp(P_LBRE), ALU.mult)
        K.tt("dve", pw[:, :, 2 * k + 1], sp(P_T0), sp(P_T1), ALU.add)
    K.pop_scope()
    if stage == 1.2:
        return early()

    K.push_scope()
    CTf = [K.sb("CTf", [128, 16, 128], F32) for _ in range(2)]
    K.push_scope()
    Zc = K.sb("Zc", [128, 4 * 512], F32)
    for part, src in enumerate((s5_c_re, s5_c_im)):
        K.memset("pool", Zc.v(), 0.0)
        for g8 in range(8):
            K.dma("sp", dap(Zc, 16 * g8 * 2048 + 64 * g8, [[2048, 16], [512, 4], [1, 64]]),
                  dap(src, g8 * 1024, [[64, 16], [8192, 4], [1, 64]]))
        for q in range(4):
            ps = PSF()
            for j in range(4):
                K.tr(ps[:, j * 128:(j + 1) * 128], Zc[:, q * 512 + j * 128:q * 512 + (j + 1) * 128], identf.v())
            dst = CTf[part][:, q * 4:q * 4 + 4, :]
            if part == 0:
                K.copy("dve", dst, ps.v().re("p (j c) -> p j c", j=4))
            else:
                K.ts("dve", dst, ps.v().re("p (j c) -> p j c", j=4), -1.0, None, ALU.mult)
    K.pop_scope()
    Xr = K.sb("Xr", [128, 16 * 128], F32)
    Xi = K.sb("Xi", [128, 16 * 128], F32)
    for X, src in ((Xr, s5_b_re), (Xi, s5_b_im)):
        K.memset("pool", X.v(), 0.0)
        for gl in range(2):
            for a in range(4):
                K.dma("sp", dap(X, gl * 64 * 2048 + 16 * gl + a * 512, [[2048, 64], [160, 4], [1, 16]]),
                      dap(src, gl * 1024 + a * 8192, [[16, 64], [2048, 4], [1, 16]]), slow=True)
    c4v = CTb.v().re("p (i two) c -> p i two c", two=2)
    K.copy("act", c4v[:, :, 0, :], CTf[0].v())
    K.copy("act", c4v[:, :, 1, :], CTf[1].v())
    T1 = K.sb("T1", [128, 16, 128], F32)
    Xs = [K.sb("Xs", [128, 16, 128], F32) for _ in range(2)]
    T2 = Xs[0]
    bcv = lambda vv: vv.un(2).bc([128, 16, 128])

    def act_scale(dst, src3, g2):
        for i_ in range(16):
            K.act(dst[:, i_, :], src3[:, i_, :], AF.Copy, scale=g2[:, i_:i_ + 1])

    for jv in range(LS - 1):
        lr_, li_ = pw[:, :, 2 * (jv + 1)], pw[:, :, 2 * (jv + 1) + 1]
        v4v = VTb[jv].v().re("p (i two) c -> p i two c", two=2)
        K.tt("dve", T1.v(), CTf[0].v(), bcv(lr_), ALU.mult)
        K.tt("dve", T2.v(), CTf[1].v(), bcv(li_), ALU.mult)
        K.tt("dve", v4v[:, :, 0, :], T1.v(), T2.v(), ALU.add)
        K.tt("dve", T1.v(), CTf[1].v(), bcv(lr_), ALU.mult)
        K.tt("dve", T2.v(), CTf[0].v(), bcv(li_), ALU.mult)
        K.tt("dve", v4v[:, :, 1, :], T1.v(), T2.v(), ALU.subtract)
    gsc = K.sb("gsc", [128, 16, 4], F32)
    x3 = lambda b_: b_.v().re("p (i c) -> p i c", i=16)
    for v in range(LS):
        kpow = LS - 1 - v
        lr_, li_ = pw[:, :, 2 * kpow], pw[:, :, 2 * kpow + 1]
        K.tt("dve", gsc[:, :, 2], lr_, sp(P_FRE), ALU.mult)
        K.tt("dve", gsc[:, :, 3], li_, sp(P_FIM), ALU.mult)
        K.tt("dve", gsc[:, :, 0], gsc[:, :, 2], gsc[:, :, 3], ALU.subtract)
        K.tt("dve", gsc[:, :, 2], lr_, sp(P_FIM), ALU.mult)
        K.tt("dve", gsc[:, :, 3], li_, sp(P_FRE), ALU.mult)
        K.tt("dve", gsc[:, :, 1], gsc[:, :, 2], gsc[:, :, 3], ALU.add)
        gr, gi = bcv(gsc[:, :, 0]), bcv(gsc[:, :, 1])
        K.tt("dve", Xs[0].v(), x3(Xr), gr, ALU.mult)
        K.tt("dve", T1.v(), x3(Xi), gi, ALU.mult)
        K.tt("dve", Xs[0].v(), Xs[0].v(), T1.v(), ALU.subtract)
        K.tt("dve", Xs[1].v(), x3(Xr), gi, ALU.mult)
        K.tt("dve", T1.v(), x3(Xi), gr, ALU.mult)
        K.tt("dve", Xs[1].v(), Xs[1].v(), T1.v(), ALU.add)
        for part in range(2):
            for i4 in range(4):
                ps = PSF()
                for j in range(4):
                    K.tr(ps[:, j * 128:(j + 1) * 128], Xs[part][:, i4 * 4 + j, :], identf.v())
                dstb = BT[v].v().re("p (i two) c -> p i two c", two=2)[:, i4 * 4:i4 * 4 + 4, part, :]
                K.copy("act" if i4 % 2 else "dve", dstb, ps.v().re("p (j c) -> p j c", j=4))
        if kpow <= LS - 2:
            for q in range(4):
                ps = PSF()
                for j in range(4):
                    i = q * 4 + j
                    K.mm(ps[:, 0:128], Xs[0][:, i, :], CTf[0][:, i, :], start=(j == 0), stop=False)
                    K.mm(ps[:, 0:128], Xs[1][:, i, :], CTf[1][:, i, :], start=False, stop=(j == 3))
                K.copy("dve", Kt[kpow][:, q, :], ps[:, 0:128])
    K.pop_scope()
    if stage == 1.3:
        return early()

    x1b = [K.sb("x1sb", [128, 16, NS], BF16) for _ in range(2)]
    K.push_scope()
    s0_tm = [K.sb("s0tm", [NS, 2048], F32) for _ in range(2)]
    s0 = [K.sb("s0", [128, 16, NS], F32) for _ in range(2)]
    K.dma("sp", s0_tm[0].v(), st_re.v())
    K.dma("sp", s0_tm[1].v(), st_im.v())
    for part in range(2):
        ps = PSF()
        for i in range(16):
            K.tr(ps[:, i * NS:(i + 1) * NS], s0_tm[part][:, i * 128:(i + 1) * 128], identf[:NS, :NS])
        K.copy("dve", s0[part].v(), ps[:, :16 * NS].re("p (i n) -> p i n", i=16))
    psr = PSF()
    psi = PSF()
    for i in range(16):
        K.mm(psr[:, i * NS:(i + 1) * NS], BT[LS - 1][:, 2 * i, :], ubf[:, i // 4, T:TT])
        K.mm(psi[:, i * NS:(i + 1) * NS], BT[LS - 1][:, 2 * i + 1, :], ubf[:, i // 4, T:TT])
    sw = [K.sb("sw", [128, 16, NS], F32) for _ in range(2)]
    x1 = [K.sb("x1s", [128, 16, NS], F32) for _ in range(2)]
    bc16 = lambda k: sp(k).un(2).bc([128, 16, NS])
    pr3 = psr[:, :16 * NS].re("p (i n) -> p i n", i=16)
    pi3 = psi[:, :16 * NS].re("p (i n) -> p i n", i=16)
    K.tt("dve", sw[0].v(), s0[0].v(), bc16(P_LBRE), ALU.mult)
    K.tt("dve", sw[1].v(), s0[1].v(), bc16(P_LBIM), ALU.mult)
    K.tt("dve", sw[0].v(), sw[0].v(), sw[1].v(), ALU.subtract)
    K.tt("dve", x1[0].v(), sw[0].v(), pr3, ALU.add)
    K.tt("dve", sw[0].v(), s0[1].v(), bc16(P_LBRE), ALU.mult)
    K.tt("dve", sw[1].v(), s0[0].v(), bc16(P_LBIM), ALU.mult)
    K.tt("dve", sw[0].v(), sw[0].v(), sw[1].v(), ALU.add)
    K.tt("dve", x1[1].v(), sw[0].v(), pi3, ALU.add)
    for part in range(2):
        K.copy("act", x1b[part].v(), x1[part].v())
        xs_tm = s0_tm[part]
        for i4 in range(4):
            ps = PSF()
            for j in range(4):
                i = i4 * 4 + j
                K.tr(ps[:NS, j * 128:(j + 1) * 128], x1[part][:, i, :], identf.v())
            K.copy("dve", xs_tm[:, i4 * 512:(i4 + 1) * 512], ps[:NS, :])
        K.dma("sp", (o_re_s if part == 0 else o_im_s).v(), xs_tm.v())
    K.pop_scope()
    if stage == 1.4:
        return early()

    hgl = K.sb("hgl", [128, 4, TT], BF16)
    carry = K.sb("carry", [128, 16, 2], F32)
    K.memset("dve", carry.v(), 0.0)
    UN = [dict(bR=K.sb("bR", [128, NH], F32), bI=K.sb("bI", [128, NH], F32),
               wR=K.sb("wR", [128, NH], F32), wI=K.sb("wI", [128, NH], F32),
               t1=K.sb("t1", [128, NH], F32), t2=K.sb("t2", [128, NH], F32)) for _ in range(4)]
    xbr = [K.sb("xbr", [128, NH + 1], BF16) for _ in range(4)]
    xbi = [K.sb("xbi", [128, NH + 1], BF16) for _ in range(4)]
    yv = [K.sb("yv", [128, 512], F32) for _ in range(2)]
    y2 = [K.sb("y2", [128, 512], F32) for _ in range(2)]

    def gelu_to(dst_bf, yv_, y2_):
        K.act(y2_, yv_, AF.Square)
        K.act(y2_, y2_, AF.Identity, bias=1.0, scale=0.044715)
        K.tt("pool", y2_, y2_, yv_, ALU.mult)
        K.act(y2_, y2_, AF.Sigmoid, scale=1.5957691216057308)
        K.tt("pool", dst_bf, y2_, yv_, ALU.mult)

    vr = lambda b: b.v().re("p (a l) -> p a l", a=NRC)
    for q in range(4):
        for tb in range(4):
            col0, n = TBLK[tb]
            ugrp = ubf[:, q, col0:col0 + n].re("p (c j) -> p j c", j=LS)
            for j in range(4):
                i = q * 4 + j
                U = UN[j]
                psr = PSF()
                psi = PSF()
                for v in range(LS):
                    K.mm(psr[:, :NH], BT[v][:, 2 * i, :], ugrp[:, v, :], start=(v == 0), stop=(v == LS - 1))
                for v in range(LS):
                    K.mm(psi[:, :NH], BT[v][:, 2 * i + 1, :], ugrp[:, v, :], start=(v == 0), stop=(v == LS - 1))
                tC = cosT[:, i, :].un(1).bc([128, NRC, LC])
                tS = sinT[:, i, :].un(1).bc([128, NRC, LC])
                pr2 = psr[:, :NH].re("p (a l) -> p a l", a=NRC)
                pi2 = psi[:, :NH].re("p (a l) -> p a l", a=NRC)
                K.tt("dve", vr(U["t1"]), pr2, tC, ALU.mult)
                K.tt("dve", vr(U["t2"]), pi2, tS, ALU.mult)
                K.tt("pool", U["bR"].v(), U["t1"].v(), U["t2"].v(), ALU.add)
                K.tt("dve", vr(U["wR"]), pi2, tC, ALU.mult)
                K.tt("dve", vr(U["wI"]), pr2, tS, ALU.mult)
                K.tt("dve", U["bI"].v(), U["wR"].v(), U["wI"].v(), ALU.subtract)
                K.copy("act", xbr[j][:, 0:1], carry[:, i, 0:1])
                K.copy("act", xbi[j][:, 0:1], carry[:, i, 1:2])
            for a in range(NRC):
                cs = slice(a * LC, (a + 1) * LC)
                for j in range(4):
                    i = q * 4 + j
                    U = UN[j]
                    rho = sp_[:, i, P_MAG2:P_MAG2 + 1].bc([128, LC])
                    if a == 0:
                        ire, iim = carry[:, i, 0:1], carry[:, i, 1:2]
                    else:
                        ire, iim = U["bR"][:, a * LC - 1:a * LC], U["bI"][:, a * LC - 1:a * LC]
                    K.scan(U["wR"][:, cs], rho, U["bR"][:, cs], ire)
                    K.scan(U["wI"][:, cs], rho, U["bI"][:, cs], iim)
                for j in range(4):
                    i = q * 4 + j
                    U = UN[j]
                    K.tt("pool", U["t1"][:, cs], U["wR"][:, cs], cosT[:, i, :], ALU.mult)
                    K.tt("dve", U["t2"][:, cs], U["wI"][:, cs], sinT[:, i, :], ALU.mult)
                    K.tt("dve", U["bR"][:, cs], U["t1"][:, cs], U["t2"][:, cs], ALU.subtract)
                    K.tt("pool", U["t1"][:, cs], U["wR"][:, cs], sinT[:, i, :], ALU.mult)
                    K.tt("dve", U["t2"][:, cs], U["wI"][:, cs], cosT[:, i, :], ALU.mult)
                    K.tt("dve", U["bI"][:, cs], U["t1"][:, cs], U["t2"][:, cs], ALU.add)
            for j in range(4):
                i = q * 4 + j
                U = UN[j]
                K.copy("dve", carry[:, i, 0:1], U["bR"][:, NH - 1:NH])
                K.copy("dve", carry[:, i, 1:2], U["bI"][:, NH - 1:NH])
                K.copy("act", xbr[j][:, 1:NH + 1], U["bR"].v())
                K.copy("act", xbi[j][:, 1:NH + 1], U["bI"].v())
            yy = yv[tb % 2]
            yyp = yy.v().re("p (j c) -> p j c", j=LS)
            for jo in range(LS):
                psy = PSF()
                if jo == LS - 1:
                    for j in range(4):
                        i = q * 4 + j
                        K.mm(psy[:, :NH], CTb[:, 2 * i, :], xbr[j][:, 1:NH + 1], start=(j == 0), stop=False)
                        K.mm(psy[:, :NH], CTb[:, 2 * i + 1, :], xbi[j][:, 1:NH + 1], start=False, stop=(j == 3))
                else:
                    for j in range(4):
                        i = q * 4 + j
                        K.mm(psy[:, :NH], VTb[jo][:, 2 * i, :], xbr[j][:, 0:NH], start=(j == 0), stop=False)
                        K.mm(psy[:, :NH], VTb[jo][:, 2 * i + 1, :], xbi[j][:, 0:NH], start=False, stop=False)
                    for ii in range(jo + 1):
                        K.mm(psy[:, :NH], Kt[jo - ii][:, q, :], ugrp[:, ii, :], start=False, stop=(ii == jo))
                K.copy("act", yyp[:, jo, :], psy[:, :NH])
            psu = PSF()
            proj_fm(wu4[q], 8, hT, tb, psu)
            K.stt(yyp, psu.v().re("p (c j) -> p j c", j=LS), pcol("s5_d", q), yyp, ALU.mult, ALU.add)
            y2p = y2[tb % 2].v().re("p (j c) -> p j c", j=LS)
            gelu_to(hgl[:, q, col0:col0 + n].re("p (c j) -> p j c", j=LS), yyp, y2p)
        psy = PSF()
        for j in range(4):
            i = q * 4 + j
            K.mm(psy[:, :NS], CTb[:, 2 * i, :], x1b[0][:, i, :], start=(j == 0), stop=False)
            K.mm(psy[:, :NS], CTb[:, 2 * i + 1, :], x1b[1][:, i, :], start=False, stop=(j == 3))
        K.copy("act", yv[0][:, :NS], psy[:, :NS])
        psu = PSF()
        proj_fm(wu4[q], 8, hT, 4, psu)
        K.stt(yv[0][:, :NS], psu[:, :NS], pcol("s5_d", q), yv[0][:, :NS], ALU.mult, ALU.add)
        gelu_to(hgl[:, q, T:TT], yv[0][:, :NS], y2[0][:, :NS])
    if stage == 1.5:
        return early()
    K.dma("sp", dap(o_re_p, 0, [[1, 128], [128, 16]]), carry[:, :, 0], slow=True)
    K.dma("sp", dap(o_im_p, 0, [[1, 128], [128, 16]]), carry[:, :, 1], slow=True)
    K.dma("sp", hgl_d.v(), hgl.v().re("p k t -> p (k t)"))
    K.pop_scope()
    if stage == 2:
        zt = K.sb("zt", [128, 4096], F32)
        K.memset("pool", zt.v(), 0.0)
        K.dma("sp", o_shift_p.v(), zt[0:1, 0:SHIFT])
        K.dma("sp", o_shift_s.v(), zt[0:NS, 0:SHIFT])
        K.dma("sp", o_wkv_p.v().re("h v k -> h (v k)"), zt[0:8, :])
        K.dma("sp", o_wkv_s.v(), zt[:, :])
        for i in range(16):
            K.dma("sp", y_p[i * 128:(i + 1) * 128, :], zt[:, 0:D])
        K.dma("sp", y_s.v(), zt[0:NS, 0:D])
        return early()
    K.push_scope()
    lnxg = K.sb("lnxg", [128, RW], F32)
    lnxb = K.sb("lnxb", [128, RW], F32)
    K.dma("sp", lnxg.v(), dap(lnx_g, 0, [[0, 128], [1, RW]]))
    K.dma("sp", lnxb.v(), dap(lnx_b, 0, [[0, 128], [1, RW]]))
    par = K.sb("spar", [128, 3, 64], F32)
    for tk in range(NS):
        for j, src in enumerate((lnx_g, lnx_b, r_k)):
            K.dma("sp", par[tk * 8:(tk + 1) * 8, j, :], dap(src, 0, [[64, 8], [1, 64]]), sem_buf=par)

    lora_bf = K.sb("lora_bf", [128, TT], BF16)
    lora_up = K.sb("lora_up", [128, RW], BF16)
    oaT = K.sb("oaT", [128, 4, TT], BF16)
    o_d = K.dram("o_scratch", [T, RW], F32)
    bonus_d = K.dram("bonus_scratch", [T, RW], F32)
    samp_d = K.dram("samp_scratch", [NS * 8, 7, 64], F32)
    sampo_d = K.dram("sampo_scratch", [NS, RW], F32)

    K.push_scope()
    mask4 = K.sb("mask4", [128, 4, 128], F32)
    maskNT = K.sb("maskNT", [128, 128], F32)
    resetm = K.sb("resetm", [128, T], BF16)
    blk1 = K.sb("blk1", [128, 128], F32)
    eps_ln = K.sb("eps_ln", [128, 1], F32)
    K.memset("pool", eps_ln.v(), 64e-5)
    for j in range(4):
        K.aselect(mask4[:, j, :], ones_f.v(), [[1, 128]], ALU.is_gt if j % 2 == 0 else ALU.is_ge, 0.0, 0, -1)
    K.aselect(maskNT.v(), ones_f.v(), [[-1, 128]], ALU.is_gt, 0.0, 0, 1)
    K.memset("pool", resetm.v(), 1.0)
    K.memset("pool", resetm.v().re("p (c l) -> p c l", l=128)[:, :, 0:1], 0.0)
    K.memset("pool", blk1.v(), 0.0)
    K.memset("pool", blk1[0:64, 0:64], 1.0)
    K.memset("pool", blk1[64:128, 64:128], 1.0)

    K.dma("pool", lora_up[0:32, :], w_decay_up.v())
    K.dma("pool", lora_up[32:64, :], w_aaa_up.v())
    K.dma("pool", lora_up[64:128, :], w_gate_up.v())
    sh0T = K.sb("sh0T", [128, 13, NS], F32)
    shout = [K.sb("shout", [NS + 1, 128], F32) for _ in range(1)]
    K.push_scope()
    sh_tm = K.sb("sh_tm", [NS, SHIFT], F32)
    K.dma("sp", sh_tm.v(), st_shift.v())
    ps = PSF()
    for ft in range(13):
        K.tr(ps[:, ft * NS:(ft + 1) * NS], sh_tm[:, ft * 128:(ft + 1) * 128], identf[:NS, :NS])
    K.copy("dve", sh0T.v(), ps[:, :13 * NS].re("p (f n) -> p f n", f=13))
    K.pop_scope()

    omm = K.sb("omm", [128, 13], F32)
    K.ts("dve", omm.v(), pvec[:, PV["mu"]:PV["mu"] + 13], -1.0, 1.0, ALU.mult, ALU.add)
    omka = K.sb("omka", [128, 4], F32)
    K.ts("dve", omka.v(), pvec[:, PV["k_a"]:PV["k_a"] + 4], -1.0, 1.0, ALU.mult, ALU.add)

    SC = [K.sb("scr", [128, TT], F32) for _ in range(6)]

    def proj_shift_tile(ft, pr, tmp, xr_out, xr_out_dt_bf=None, samp32=None, wb=None):
        if wb is None:
            wb = load_w_cols(w_in, ft * 128, PROJ)
        for tb in range(5):
            col0, n = TBLK[tb]
            ps = PSF()
            proj_fm(wb, 8, hT, tb, ps)
            K.copy("act", pr[:, col0:col0 + n], ps[:, :n])
            K.act(tmp[:, col0:col0 + n], ps[:, :n], AF.Copy, scale=omm[:, ft:ft + 1])
        ps = PSF()
        K.tr(ps[:NS + 1, :128], pr[:, T - 1:TT], identf.v())
        so = shout[0]
        K.copy("dve", so.v(), ps[:NS + 1, :128])
        K.dma("sp", o_shift_p[0:1, ft * 128:(ft + 1) * 128], so[0:1, :])
        K.dma("sp", o_shift_s[:, ft * 128:(ft + 1) * 128], so[1:NS + 1, :])
        mu_c = pcol("mu", ft)
        K.stt(xr_out[:, 1:T], pr[:, 0:T - 1], mu_c, tmp[:, 1:T], ALU.mult, ALU.add)
        K.copy("pool", xr_out[:, 0:1], tmp[:, 0:1])
        K.stt(xr_out[:, T:TT], sh0T[:, ft, :], mu_c, tmp[:, T:TT], ALU.mult, ALU.add)
        if samp32 is not None:
            K.stt(samp32, sh0T[:, ft, :], mu_c, tmp[:, T:TT], ALU.mult, ALU.add)

    proj_shift_tile(12, SC[0], SC[1], SC[2].v())
    K.act(lora_bf[0:32, :], SC[2][0:32, :], AF.Tanh)
    K.act(lora_bf[32:64, :], SC[2][32:64, :], AF.Copy)
    K.act(lora_bf[64:128, :], SC[2][64:128, :], AF.Sigmoid)

    vbf = K.sb("vbf", [128, TT], BF16)
    vs32 = K.sb("vs32", [128, NS], F32)
    AR = K.sb("AR", [128, 16, 2, 128], BF16)
    bti = K.sb("bti", [128, T], BF16)
    kti = K.sb("kti", [128, T], BF16)
    bhat = K.sb("bhat", [128, T], BF16)
    khat = K.sb("khat", [128, T], BF16)
    Wfm = bhat
    TMq = K.sb("TMq", [128, 16, 4, 128], BF16)
    gL = K.sb("gL", [128, 16], F32)
    Sf = K.sb("Sf", [128, 64], F32)
    Sb = K.sb("Sb", [128, 64], BF16)
    smp = K.sb("smp", [NS, 3, 128], F32)
    decs = K.sb("decs", [128, NS], F32)
    avs = K.sb("avs", [128, NS], F32)
    NPR = 16
    ABK = [K.sb("ABK", [128, 2, 128], BF16) for _ in range(NPR)]
    NPQ = 8
    ATMP = [K.sb("ATMP", [128, 2, 128], BF16) for _ in range(NPQ)]
    P0T = [K.sb("P0T", [128, 128], BF16) for _ in range(NPQ)]
    PMb = [[K.sb("PMb", [128, 3, 128], BF16) for _ in range(2)] for _ in range(NPQ)]
    Ybf = [K.sb("Ybf", [128, 64], BF16) for _ in range(NPQ)]
    QTb = [[K.sb("QTb", [128, 128], BF16) for _ in range(2)] for _ in range(NPQ)]
    Ubar = [K.sb("Ubar", [128, 64], F32) for _ in range(NPR)]
    Ut = [K.sb("Ut", [128, 64], BF16) for _ in range(4)]
    Otile = [K.sb("Otile", [128, 128], F32) for _ in range(2)]
    btile = [K.sb("btile", [128, 512], F32) for _ in range(1)]
    wkvo = Otile[0]

    def derivedA(hp):
        xr_r, xr_k = SC[0], SC[1]
        wbs = [load_w_cols(w_in, (q_ * 4 + hp) * 128, PROJ) for q_ in range(3)]
        yield
        proj_shift_tile(0 * 4 + hp, SC[2], SC[3], xr_r.v(), wb=wbs[0])
        yield
        proj_shift_tile(1 * 4 + hp, SC[2], SC[3], xr_k.v(), wb=wbs[1])
        yield
        proj_shift_tile(2 * 4 + hp, SC[2], SC[3], vbf.v(), samp32=vs32.v(), wb=wbs[2])
        yield
        sigd, csum, afm, kkn = SC[2], SC[3], SC[4], SC[5]
        hc = slice(hp * 128, (hp + 1) * 128)
        for tb in range(5):
            col0, n = TBLK[tb]
            ps = PSF()
            K.mm(ps[:, :n], lora_up[0:32, hc], lora_bf[0:32, col0:col0 + n])
            K.act(sigd[:, col0:col0 + n], ps[:, :n], AF.Sigmoid, bias=pcol("w0", hp))
            yield
            ps = PSF()
            K.mm(ps[:, :n], lora_up[32:64, hc], lora_bf[32:64, col0:col0 + n])
            K.act(afm[:, col0:col0 + n], ps[:, :n], AF.Sigmoid, bias=pcol("a0", hp))
            yield
        K.act(kkn.v(), xr_k.v(), AF.Square, scale=pcol("k_k", hp))
        yield
        for tb in range(5):
            col0, n = TBLK[tb]
            ps = PSF()
            K.mm(ps[:, :n], blk1.v(), kkn[:, col0:col0 + n])
            K.ts("dve", csum[:, col0:col0 + n], ps[:, :n], 1e-24, None, ALU.max)
            yield
        K.act(csum.v(), csum.v(), AF.Ln)
        yield
        K.act(csum.v(), csum.v(), AF.Exp, scale=-0.5)
        yield
        K.stt(kkn.v(), xr_k.v(), pcol("k_k", hp), csum.v(), ALU.mult, ALU.mult)
        yield
        K.ts("dve" if hp == 0 else "pool", csum.v(), afm.v(), pcol("k_a", hp), omka[:, hp:hp + 1], ALU.mult, ALU.add)
        yield
        K.tt("dve" if hp == 0 else "pool", xr_k.v(), xr_k.v(), csum.v(), ALU.mult)
        yield
        kmod = xr_k
        K.tt("dve" if hp == 0 else "pool", afm.v(), kkn.v(), afm.v(), ALU.mult)
        yield
        bfm = afm
        K.stt(csum.v(), xr_r.v(), pcol("r_k", hp), kmod.v(), ALU.mult, ALU.mult)
        yield
        for tb in range(4):
            col0, n = TBLK[tb]
            ps = PSF()
            K.mm(ps[:, :n], blk1.v(), csum[:, col0:col0 + n])
            K.tt("dve", csum[:, col0:col0 + n], ps[:, :n], vbf[:, col0:col0 + n], ALU.mult)
            yield
        for c4 in range(4):
            ps = PSF()
            for j in range(4):
                c = c4 * 4 + j
                K.tr(ps[:, j * 128:(j + 1) * 128], csum[:, c * 128:(c + 1) * 128], identf.v())
            bt = btile[0]
            K.copy("act", bt.v(), ps.v())
            yield
            K.dma("sp", dap(bonus_d, c4 * 512 * RW + hp * 128, [[RW, 128], [128 * RW, 4], [1, 128]]),
                  bt.v().re("p (j f) -> p j f", j=4))
            yield
        K.act(decs.v(), sigd[:, T:TT], AF.Exp, scale=-CDEC)
        yield
        K.ts("dve", avs.v(), kkn[:, T:TT], -1.0, None, ALU.mult)
        yield
        yield

    def derivedB(hp):
        xr_r, xr_k = SC[0], SC[1]
        sigd, csum, afm, kkn = SC[2], SC[3], SC[4], SC[5]
        kmod, bfm = xr_k, afm
        K.scan(csum[:, 0:T], resetm.v(), sigd[:, 0:T], 0.0)
        gt = sigd
        c3 = lambda b_: b_[:, 0:T].re("p (c l) -> p c l", l=128)
        K.act(gt[:, 0:T], csum[:, 0:T], AF.Exp, scale=-CDEC)
        K.tt("dve", AR[:, :, 1, :], c3(xr_r), c3(gt), ALU.mult)
        K.copy("dve", gL.v(), c3(gt)[:, :, 127])
        K.stt(AR[:, :, 0, 1:128], c3(kkn)[:, :, 1:128], -1.0, c3(gt)[:, :, 0:127], ALU.mult, ALU.mult)
        K.ts("dve", AR[:, :, 0, 0], c3(kkn)[:, :, 0], -1.0, None, ALU.mult)
        K.act(gt[:, 0:T], csum[:, 0:T], AF.Exp, scale=CDEC)
        K.tt("dve", bti.v(), bfm[:, 0:T], gt[:, 0:T], ALU.mult)
        K.tt("dve", kti.v(), kmod[:, 0:T], gt[:, 0:T], ALU.mult)
        K.tt("dve", c3(gt), c3(csum)[:, :, 127:128].bc([128, 16, 128]), c3(csum), ALU.subtract)
        K.act(gt[:, 0:T], gt[:, 0:T], AF.Exp, scale=-CDEC)
        K.tt("dve", bhat.v(), bfm[:, 0:T], gt[:, 0:T], ALU.mult)
        K.tt("dve", khat.v(), kmod[:, 0:T], gt[:, 0:T], ALU.mult)
        for c in range(16):
            pb = PSB()
            pv = pb.v().re("p (q t) -> p q t", q=8)
            cs = slice(c * 128, (c + 1) * 128)
            K.tr(pv[:, 0, :], vbf[:, cs], identb.v())
            K.tr(pv[:, 1, :], bhat[:, cs], identb.v())
            K.tr(pv[:, 2, :], khat[:, cs], identb.v())
            K.tr(pv[:, 3, :], AR[:, c, 0, :], identb.v())
            K.copy("act" if c % 2 else "dve", TMq[:, c, :, :], pv[:, 0:4, :])
        sq_src = [xr_r[:, T:TT], kmod[:, T:TT], vs32.v(), decs.v(), avs.v(), bfm[:, T:TT]]
        for half in range(2):
            ps = PSF()
            for j in range(3):
                K.tr(ps[:NS, j * 128:(j + 1) * 128], sq_src[half * 3 + j], identf.v())
            K.copy("dve", smp.v(), ps[:NS, :384].re("p (q f) -> p q f", q=3))
            for j in range(3):
                q = half * 3 + j
                K.dma("sp", dap(samp_d, (2 * hp) * 448 + q * 64, [[8 * 448, NS], [448, 2], [1, 64]]),
                      smp[:, j, :].re("p (h n) -> p h n", h=2))


    def chunk_thunks(hp):
        CH = []
        CH.append(lambda: K.memset('dve', Sf.v(), 0.0))
        CH.append(lambda: K.memset('pool', Sb.v(), 0.0))
        pairs = [(c, hl) for c in range(16) for hl in range(2)]

        def pre_a(sl, c, hl):
            rows = slice(hl * 64, (hl + 1) * 64)
            cs = slice(c * 128, (c + 1) * 128)
            ps = PSF()
            arv = AR[rows, c, :, :].re("p a t -> p (a t)")
            K.mm(ps[:, 0:256], bti[rows, cs], arv)
            K.mm(ps[:, 256:512], kti[rows, cs], arv)
            ps2 = PSF()
            K.mm(ps2[:, 0:128], AR[rows, c, 0, :], bti[rows, cs])
            p4 = ps.v().re("p (q t) -> p q t", q=4)
            K.tt("dve", ATMP[sl % NPQ].v(), p4[:, 0::2, :], mask4[:, 0::2, :], ALU.mult)
            K.tt("dve", ABK[sl].v(), p4[:, 1::2, :], mask4[:, 1::2, :], ALU.mult)
            K.tt("dve", P0T[sl % NPQ].v(), ps2[:, 0:128], maskNT.v(), ALU.mult)
            K.tt("pool", QTb[sl % NPQ][0].v(), P0T[sl % NPQ].v(), identb.v(), ALU.add)

        def pre_lev(sl, lev, gi):
            if lev == 0:
                Pj, PjT, Mp = ATMP[sl % NPQ][:, 0, :], P0T[sl % NPQ].v(), identb.v()
            else:
                src = PMb[sl % NPQ][lev % 2]
                Pj, PjT, Mp = src[:, 0, :], src[:, 1, :], src[:, 2, :]
            dst = PMb[sl % NPQ][(lev + 1) % 2]
            ps = PSF()
            if lev < 6:
                K.mm(ps[:, 0:128], PjT, Pj)
                K.mm(ps[:, 128:256], Pj, PjT)
            K.mm(ps[:, 256:384], QTb[sl % NPQ][lev % 2].v(), Mp, start=True, stop=True)
            eng = "act" if (gi % 8) not in (2, 5, 7) else "dve"
            if lev < 6:
                K.copy(eng, dst.v(), ps[:, 0:384].re("p (q t) -> p q t", q=3))
                K.tt("pool", QTb[sl % NPQ][(lev + 1) % 2].v(), dst[:, 1, :], identb.v(), ALU.add)
            else:
                K.copy(eng, dst[:, 2, :], ps[:, 256:384])

        def pre_w(sl, c, hl, gi):
            rows = slice(hl * 64, (hl + 1) * 64)
            cs = slice(c * 128, (c + 1) * 128)
            Mfin = PMb[sl % NPQ][1][:, 2, :]
            ps = PSF()
            K.mm(ps[:, 0:128], TMq[:, c, 3, :], Mfin)
            K.mm(ps[:, 128:192], ATMP[sl % NPQ][:, 1, :], TMq[:, c, 0, rows])
            eng = "act" if gi % 2 else "dve"
            K.copy(eng, Wfm[rows, cs], ps[rows, 0:128])
            K.copy(eng, Ybf[sl % NPQ].v(), ps[:, 128:192])

        def pre_u(sl, gi):
            Mfin = PMb[sl % NPQ][1][:, 2, :]
            ps = PSF()
            K.mm(ps[:, 0:64], Mfin, Ybf[sl % NPQ].v())
            K.copy("act" if gi % 2 else "dve", Ubar[sl].v(), ps[:, 0:64])

        def precompute(group):
            th = []
            for (sl, c, hl) in group:
                th.append(partial(pre_a, sl, c, hl))
            for lev in range(7):
                for gi, (sl, c, hl) in enumerate(group):
                    th.append(partial(pre_lev, sl, lev, gi))
            for gi, (sl, c, hl) in enumerate(group):
                th.append(partial(pre_w, sl, c, hl, gi))
            for gi, (sl, c, hl) in enumerate(group):
                th.append(partial(pre_u, sl, gi))
            return th

        def ser_u(sl, c, hl, gi):
            rows = slice(hl * 64, (hl + 1) * 64)
            cs = slice(c * 128, (c + 1) * 128)
            ut = Ut[gi % 4]
            ps = PSF()
            K.mm(ps[:, 0:64], Wfm[rows, cs], Sb[rows, :])
            K.tt("dve", ut.v(), ps[:, 0:64], Ubar[sl].v(), ALU.add)

        def ser_os(sl, c, hl, gi):
            rows = slice(hl * 64, (hl + 1) * 64)
            ut = Ut[gi % 4]
            pso = PSF()
            K.mm(pso[:, 0:64], AR[rows, c, 1, :], Sb[rows, :], start=True, stop=False)
            K.mm(pso[:, 0:64], ABK[sl][:, 0, :], ut.v(), start=False, stop=False)
            K.mm(pso[:, 0:64], ABK[sl][:, 1, :], TMq[:, c, 0, rows], start=False, stop=True)
            pss = PSF()
            K.mm(pss[:, 0:64], TMq[:, c, 1, :], ut.v(), start=True, stop=False)
            K.mm(pss[:, 0:64], TMq[:, c, 2, :], TMq[:, c, 0, rows], start=False, stop=True)
            K.stt(Sf[rows, :], Sf[rows, :], gL[rows, c:c + 1], pss[rows, 0:64], ALU.mult, ALU.add)
            K.copy("act", Sb[rows, :], Sf[rows, :])
            ot = Otile[c % 2]
            K.copy("act", ot[:, rows], pso[:, 0:64])
            if hl == 1:
                K.dma("sp", o_d[c * 128:(c + 1) * 128, hp * 128:(hp + 1) * 128], ot.v())

        def serial(group):
            th = []
            for gi, (sl, c, hl) in enumerate(group):
                th.append(partial(ser_u, sl, c, hl, gi))
                th.append(partial(ser_os, sl, c, hl, gi))
            return th

        def run_merged(lists):
            idx = [0] * len(lists)
            while True:
                best, bf = -1, 2.0
                for li, L in enumerate(lists):
                    if idx[li] < len(L):
                        frac = idx[li] / len(L)
                        if frac < bf:
                            best, bf = li, frac
                if best < 0:
                    break
                lists[best][idx[best]]()
                idx[best] += 1

        G = 8
        groups = []
        for g0 in range(0, 32, G):
            groups.append([((g0 + i) % NPR, pairs[g0 + i][0], pairs[g0 + i][1]) for i in range(G)])
        CH.extend(precompute(groups[0]))
        for gi_ in range(len(groups)):
            lists = [serial(groups[gi_])]
            if gi_ + 1 < len(groups):
                lists.append(precompute(groups[gi_ + 1]))
            CH.extend(merge_lists(lists))
        def fin_state():
            ps = PSF()
            K.tr(ps[:64, 0:128], Sf.v(), identf.v())
            K.copy("dve", wkvo[:64, :], ps[:64, 0:128])
            K.dma("sp", o_wkv_p[2 * hp:2 * hp + 2, :, :].re("h v k -> v h k"), wkvo[:64, :].re("v (h k) -> v h k", h=2))

        CH.append(fin_state)
        return CH

    def drain(gen):
        for _ in gen:
            pass

    NA_EST = 150
    drain(derivedA(0))
    derivedB(0)
    for hp in range(4):
        CH = chunk_thunks(hp)
        OUTL = []
        gen = derivedA(hp + 1) if hp + 1 < 4 else None
        per = max(1, len(CH) // NA_EST)
        pero = max(1, len(CH) // max(1, len(OUTL)))
        oi = 0
        for ti, th in enumerate(CH):
            th()
            if gen is not None and ti % per == per - 1:
                try:
                    next(gen)
                except StopIteration:
                    gen = None
            if oi < len(OUTL) and ti % pero == pero - 1:
                OUTL[oi]()
                oi += 1
        while oi < len(OUTL):
            OUTL[oi]()
            oi += 1
        if gen is not None:
            drain(gen)
        if hp + 1 < 4:
            derivedB(hp + 1)

    K.pop_scope()
    if stage == 3:
        zt = K.sb("zt", [128, 4096], F32)
        K.memset("pool", zt.v(), 0.0)
        K.dma("sp", o_wkv_s.v(), zt[:, :])
        for i in range(16):
            K.dma("sp", y_p[i * 128:(i + 1) * 128, :], zt[:, 0:D])
        K.dma("sp", y_s.v(), zt[0:NS, 0:D])
        return early()

    def bcast_row(name, src, n):
        t = K.sb(name, [128, n], F32)
        K.dma("sp", t.v(), dap(src, 0, [[0, 128], [1, n]]))
        return t

    K.push_scope()
    hgl = K.sb("hgl2", [128, 4, TT], BF16)
    wmo = K.sb("wmo", [128, 8, D], BF16)
    gpost = K.sb("gpost_mix", [128, D], F32)
    PREF = [lambda: K.dma("sp", gpost.v(), dap(g_post_mix, 0, [[0, 128], [1, D]])),
            lambda: K.dma("sp", hgl.v().re("p k t -> p (k t)"), hgl_d.v())]
    for kt in range(8):
        PREF.append(partial(lambda kt_: K.dma("pool", wmo[:, kt_, :], w_merge_out[kt_ * 128:(kt_ + 1) * 128, :]), kt))
    mring = [K.sb("mring", [128, 8, 128], BF16) for _ in range(10)]
    mr_i = [0]

    def mload(src, c0, ncols_total, kt_n):
        mr_i[0] += 1
        wb = mring[mr_i[0] % len(mring)]
        K.dma("pool", wb[:, :kt_n, :], dap(src, c0, [[ncols_total, 128], [128 * ncols_total, kt_n], [1, 128]]))
        return wb

    GOFF = SHIFT + RW
    mw = {}

    def mg_load(f):
        mw[f] = (mload(w_rwkv_out, f * 128, D, 4), mload(w_in, GOFF + f * 128, PROJ, 8),
                 mload(w_in, GOFF + D + f * 128, PROJ, 8), mload(glu_w1, f * 128, D, 4), mload(glu_w2, f * 128, D, 4))

    PREF.append(partial(mg_load, 0))
    K.push_scope()
    eps_ln = K.sb("eps_ln2", [128, 1], F32)
    K.memset("pool", eps_ln.v(), 64e-5)
    NORF = 3
    fot_r = [K.sb("fot", [128, RW], F32) for _ in range(NORF)]
    fbt_r = [K.sb("fbt", [128, RW], F32) for _ in range(NORF)]
    fon_r = [K.sb("fon", [128, RW], F32) for _ in range(NORF)]
    fofb = [K.sb("fofb", [128, RW], BF16) for _ in range(NORF)]
    fst8 = [K.sb("fst8", [128, 6, 8], F32) for _ in range(NORF)]
    fh3 = lambda b_: b_.v().re("p (h n) -> p h n", h=8)

    def fout_a(c):
        r_ = c % NORF
        ot, bt, tmp, s8 = fot_r[r_], fbt_r[r_], fon_r[r_], fst8[r_]
        K.dma("sp", ot.v(), o_d[c * 128:(c + 1) * 128, :])
        K.dma("sp", bt.v(), bonus_d[c * 128:(c + 1) * 128, :])
        K.reduce(s8[:, 0, :], fh3(ot))
        K.act(tmp.v(), ot.v(), AF.Square)
        K.reduce(s8[:, 1, :], fh3(tmp))
        K.ts("dve", s8[:, 2, :], s8[:, 0, :], 1.0 / 64, None, ALU.mult)
        K.tt("dve", s8[:, 3, :], s8[:, 2, :], s8[:, 2, :], ALU.mult)
        K.stt(s8[:, 4, :], s8[:, 1, :], 1.0 / 64, s8[:, 3, :], ALU.mult, ALU.subtract)
        K.act(s8[:, 4, :], s8[:, 4, :], AF.Sqrt, bias=eps_ln.v(), scale=1.0)
        K.recip(s8[:, 5, :], s8[:, 4, :])
        K.stt(s8[:, 3, :], s8[:, 2, :], -1.0, s8[:, 5, :], ALU.mult, ALU.mult)

    def fout_b(c):
        r_ = c % NORF
        ot, bt, on, of, s8 = fot_r[r_], fbt_r[r_], fon_r[r_], fofb[r_], fst8[r_]
        for h in range(8):
            K.act(on[:, h * 64:(h + 1) * 64], ot[:, h * 64:(h + 1) * 64], AF.Identity,
                  bias=s8[:, 3, h:h + 1], scale=s8[:, 5, h:h + 1])
        K.tt("pool", bt.v(), bt.v(), lnxb.v(), ALU.add)
        K.tt("dve", on.v(), on.v(), lnxg.v(), ALU.mult)
        K.tt("dve", on.v(), on.v(), bt.v(), ALU.add)
        ps = PSF()
        K.mm(ps.v(), lora_bf[64:128, c * 128:(c + 1) * 128], lora_up[64:128, :])
        K.tt("dve", of.v(), on.v(), ps.v(), ALU.mult)
        pb = PSB()
        pv = pb.v().re("p (q t) -> p q t", q=8)
        for kt in range(4):
            K.tr(pv[:, kt, :], of[:, kt * 128:(kt + 1) * 128], identb.v())
        K.copy("act", oaT[:, :, c * 128:(c + 1) * 128], pv[:, 0:4, :])

    OUT = []
    for c in range(16 + 2):
        if c < 16:
            OUT.append(partial(fout_a, c))
        if 0 <= c - 2 < 16:
            OUT.append(partial(fout_b, c - 2))

    gs_tm = K.sb("gs_tm", [NS, RW], F32)
    ps = PSF()
    K.mm(ps[:NS, :], lora_bf[64:128, T:TT], lora_up[64:128, :])
    K.copy("act", gs_tm.v(), ps[:NS, :])
    K.dma("sp", dap(samp_d, 6 * 64, [[8 * 448, NS], [448, 8], [1, 64]]), gs_tm.v().re("p (h n) -> p h n", h=8))
    vec = K.sb("vec", [128, 7, 64], F32)
    K.dma("sp", vec.v().re("p q n -> p (q n)"), dap(samp_d, 0, [[448, 128], [1, 448]]))
    S0s = K.sb("S0s", [128, 64, 64], F32)
    S1s = K.sb("S1s", [128, 64, 64], F32)
    tS = K.sb("tS", [128, 64, 64], F32)
    K.dma("sp", S0s.v().re("p v k -> p (v k)"), st_wkv.v())
    r_, km_, v_, dec_, av_, b_, g_ = [vec[:, q, :] for q in range(7)]
    overv = lambda x: x.un(1).bc([128, 64, 64])
    overk = lambda x: x.un(2).bc([128, 64, 64])
    sv = K.sb("sv", [128, 8, 64], F32)
    ss = K.sb("ss", [128, 8], F32)
    eps2 = K.sb("eps_ln3", [128, 1], F32)
    K.memset("pool", eps2.v(), 64e-5)
    os_tm = K.sb("os_tm", [NS, RW], F32)
    os_bf = K.sb("os_bf", [NS, RW], BF16)
    SMP = [
        lambda: K.tt("dve", tS.v(), S0s.v(), overv(av_), ALU.mult),
        lambda: K.reduce(sv[:, 0, :], tS.v()),
        lambda: K.tt("pool", S1s.v(), S0s.v(), overv(dec_), ALU.mult),
        lambda: K.tt("dve", tS.v(), overk(sv[:, 0, :]), overv(b_), ALU.mult),
        lambda: K.tt("dve", S1s.v(), S1s.v(), tS.v(), ALU.add),
        lambda: K.tt("dve", tS.v(), overk(v_), overv(km_), ALU.mult),
        lambda: K.tt("dve", S1s.v(), S1s.v(), tS.v(), ALU.add),
        lambda: K.dma("sp", o_wkv_s.v(), S1s.v().re("p v k -> p (v k)")),
        lambda: K.tt("dve", tS.v(), S1s.v(), overv(r_), ALU.mult),
        lambda: K.reduce(sv[:, 1, :], tS.v()),
        lambda: K.reduce(ss[:, 0:1], sv[:, 1, :]),
        lambda: K.tt("dve", sv[:, 2, :], sv[:, 1, :], sv[:, 1, :], ALU.mult),
        lambda: K.reduce(ss[:, 1:2], sv[:, 2, :]),
        lambda: K.ts("dve", ss[:, 2:3], ss[:, 0:1], 1.0 / 64, None, ALU.mult),
        lambda: K.tt("dve", ss[:, 3:4], ss[:, 2:3], ss[:, 2:3], ALU.mult),
        lambda: K.stt(ss[:, 4:5], ss[:, 1:2], 1.0 / 64, ss[:, 3:4], ALU.mult, ALU.subtract),
        lambda: K.act(ss[:, 4:5], ss[:, 4:5], AF.Sqrt, bias=eps2.v(), scale=1.0),
        lambda: K.recip(ss[:, 5:6], ss[:, 4:5]),
        lambda: K.ts("dve", sv[:, 3, :], sv[:, 1, :], ss[:, 2:3], ss[:, 5:6], ALU.subtract, ALU.mult),
        lambda: K.tt("dve", sv[:, 3, :], sv[:, 3, :], par[:, 0, :], ALU.mult),
        lambda: K.tt("dve", sv[:, 3, :], sv[:, 3, :], par[:, 1, :], ALU.add),
        lambda: K.tt("dve", sv[:, 4, :], r_, km_, ALU.mult),
        lambda: K.tt("dve", sv[:, 4, :], sv[:, 4, :], par[:, 2, :], ALU.mult),
        lambda: K.reduce(ss[:, 6:7], sv[:, 4, :]),
        lambda: K.stt(sv[:, 3, :], v_, ss[:, 6:7], sv[:, 3, :], ALU.mult, ALU.add),
        lambda: K.tt("dve", sv[:, 5, :], sv[:, 3, :], g_, ALU.mult),
        lambda: K.dma("sp", dap(sampo_d, 0, [[64, 128], [1, 64]]), sv[:, 5, :]),
        lambda: K.dma("sp", os_tm.v(), sampo_d.v()),
        lambda: K.copy("dve", os_bf.v(), os_tm.v()),
    ]

    def smp_fin():
        pb = PSB()
        pv = pb.v().re("p (q t) -> p q t", q=8)
        for kt in range(4):
            K.tr(pv[:, kt, :NS], os_bf[:, kt * 128:(kt + 1) * 128], identb[:NS, :NS])
        K.copy("act", oaT[:, :, T:TT], pv[:, 0:4, :NS])

    SMP.append(smp_fin)
    run_merged_g([OUT, SMP, [lambda: None] * 6 + PREF])
    K.pop_scope()
    if stage == 4:
        zt = K.sb("zt", [128, 1024], F32)
        K.memset("pool", zt.v(), 0.0)
        for i in range(16):
            K.dma("sp", y_p[i * 128:(i + 1) * 128, :], zt[:, 0:D])
        K.dma("sp", y_s.v(), zt[0:NS, 0:D])
        return early()


    def pn_b(ps_halves, rows, res_src, gpost, dst, bufs):
        st, xres, tt_ = bufs[:3]
        K.dma("sp", xres[:rows, :], res_src)
        for hf in range(2):
            K.act(tt_[:rows, hf * 512:(hf + 1) * 512], ps_halves[hf][:rows, :], AF.Square, accum=st[:rows, hf:hf + 1])
        K.tt("dve", st[:rows, 2:3], st[:rows, 0:1], st[:rows, 1:2], ALU.add)
        K.act(st[:rows, 3:4], st[:rows, 2:3], AF.Sqrt, bias=epsT[:rows, :], scale=1.0 / D)
        K.recip(st[:rows, 4:5], st[:rows, 3:4])
        for hf in range(2):
            hs = slice(hf * 512, (hf + 1) * 512)
            K.stt(tt_[:rows, hs], ps_halves[hf][:rows, :], st[:rows, 4:5], gpost[:rows, hs], ALU.mult, ALU.mult)
        K.tt("dve", xres[:rows, :], xres[:rows, :], tt_[:rows, :], ALU.add)
        K.dma("sp", dst, xres[:rows, :])

    def pn_c(rows, bufs, nt):
        st, xres, tt_, xb = bufs
        gname, dstT, col0 = nt
        K.act(tt_[:rows, :], xres[:rows, :], AF.Square, accum=st[:rows, 5:6])
        K.act(st[:rows, 6:7], st[:rows, 5:6], AF.Sqrt, bias=epsT[:rows, :], scale=1.0 / D)
        K.recip(st[:rows, 7:8], st[:rows, 6:7])
        K.act(xb[:rows, :], xres[:rows, :], AF.Copy, scale=st[:rows, 7:8])
        pb = PSB()
        pv = pb.v().re("p (k t) -> p k t", k=8)
        for kt in range(8):
            K.tr(pv[:, kt, :rows], xb[:rows, kt * 128:(kt + 1) * 128], identb[:rows, :rows])
        g0 = PV[gname]
        K.tt("dve", dstT[:, :, col0:col0 + rows], pv[:, :, :rows],
             pvec[:, g0:g0 + 8].un(2).bc([128, 8, rows]), ALU.mult)

    mT = K.sb("mT", [128, 8, TT], BF16)
    mt = [[K.sb("mt", [128, 512], F32) for _ in range(4)] for _ in range(2)]
    mps = {}

    def mg_a1(f, tb, it):
        col0, n = TBLK[tb]
        cs = slice(col0, col0 + n)
        wA, wga, wgb, w1, w2 = mw[f]
        ps2, ps1 = PSF(), PSF()
        for kt in range(4):
            K.mm(ps2[:, :n], w2[:, kt, :], hgl[:, kt, cs], start=(kt == 0), stop=(kt == 3))
        for kt in range(4):
            K.mm(ps1[:, :n], w1[:, kt, :], hgl[:, kt, cs], start=(kt == 0), stop=(kt == 3))
        mps[(it, 1)] = (ps2, ps1)

    def mg_a2(f, tb, it):
        col0, n = TBLK[tb]
        cs = slice(col0, col0 + n)
        wA, wga, wgb, w1, w2 = mw[f]
        psgb = PSF()
        for kt in range(8):
            K.mm(psgb[:, :n], wgb[:, kt, :], hT[:, kt, cs], start=(kt == 0), stop=(kt == 7))
        mps[(it, 2)] = psgb

    def mg_a3(f, tb, it):
        col0, n = TBLK[tb]
        cs = slice(col0, col0 + n)
        wA, wga, wgb, w1, w2 = mw[f]
        psga, psA = PSF(), PSF()
        for kt in range(8):
            K.mm(psga[:, :n], wga[:, kt, :], hT[:, kt, cs], start=(kt == 0), stop=(kt == 7))
        for kt in range(4):
            K.mm(psA[:, :n], wA[:, kt, :], oaT[:, kt, cs], start=(kt == 0), stop=(kt == 3))
        mps[(it, 3)] = (psga, psA)

    def mg_b1(f, tb, it):
        col0, n = TBLK[tb]
        t0, t1_, t2_, t3_ = mt[it % 2]
        ps2, ps1 = mps.pop((it, 1))
        K.act(t0[:, :n], ps2[:, :n], AF.Sigmoid, bias=pcol("glu_b2", f))
        K.stt(t1_[:, :n], ps1[:, :n], pcol("glu_b1", f), t0[:, :n], ALU.add, ALU.mult)

    def mg_b2(f, tb, it):
        col0, n = TBLK[tb]
        t0, t1_, t2_, t3_ = mt[it % 2]
        psgb = mps.pop((it, 2))
        K.act(t2_[:, :n], psgb[:, :n], AF.Sigmoid)
        K.tt("dve", t1_[:, :n], t1_[:, :n], t2_[:, :n], ALU.mult)

    def mg_b3(f, tb, it):
        col0, n = TBLK[tb]
        cs = slice(col0, col0 + n)
        t0, t1_, t2_, t3_ = mt[it % 2]
        psga, psA = mps.pop((it, 3))
        K.act(t3_[:, :n], psga[:, :n], AF.Sigmoid)
        K.tt("dve", t3_[:, :n], psA[:, :n], t3_[:, :n], ALU.mult)
        K.tt("pool" if n > 16 else "dve", mT[:, f, cs], t3_[:, :n], t1_[:, :n], ALU.add)

    its = [(f, tb) for f in range(8) for tb in range(5)]
    for it, (f, tb) in enumerate(its):
        if tb == 0 and f + 1 < 8:
            mg_load(f + 1)
        mg_a1(f, tb, it)
        if it >= 1:
            pf, ptb = its[it - 1]
            mg_b3(pf, ptb, it - 1)
        mg_a2(f, tb, it)
        mg_b1(f, tb, it)
        mg_a3(f, tb, it)
        mg_b2(f, tb, it)
    mg_b3(its[-1][0], its[-1][1], len(its) - 1)
    pn_bufs = [(K.sb("pn_st", [128, 8], F32), K.sb("pn_x", [128, D], F32), K.sb("pn_t", [128, D], F32),
                K.sb("pn_xb", [128, D], BF16)) for _ in range(3)]
    mix_ps = {}

    def mix_a(i):
        rows = 128 if i < 16 else NS
        tcs = slice(i * 128, i * 128 + rows)
        halves = [PSF(), PSF()]
        for hf in range(2):
            for kt in range(8):
                K.mm(halves[hf][:rows, :], mT[:, kt, tcs], wmo[:, kt, hf * 512:(hf + 1) * 512],
                     start=(kt == 0), stop=(kt == 7))
        mix_ps[i] = halves

    def mix_b(i):
        rows = 128 if i < 16 else NS
        pn_b(mix_ps[i], rows, x_tile(i), gpost, x1_d[i * 128:i * 128 + rows, :], pn_bufs[i % 3])

    def mix_c(i):
        rows = 128 if i < 16 else NS
        pn_c(rows, pn_bufs[i % 3], ("g_pre_ffn", hT, i * 128))

    for step in range(17 + 2):
        if step < 17:
            mix_a(step)
        if 0 <= step - 1 < 17:
            mix_b(step - 1)
        if 0 <= step - 2 < 17:
            mix_c(step - 2)
    K.pop_scope()
    K.pop_scope()

    K.push_scope()
    NF = DFF // 128
    aT = K.sb("aT", [128, NF, TT], BF16)
    wd = K.sb("wd", [128, NF, D], BF16)
    gpost2 = bcast_row("gpost_ffn", g_post_ffn, D)
    fring = [K.sb("fring", [128, 8, 128], BF16) for _ in range(4)]
    fr_i = [0]

    def fload(src, c0):
        fr_i[0] += 1
        wb = fring[fr_i[0] % len(fring)]
        K.dma("pool", wb.v(), dap(src, c0, [[DFF, 128], [128 * DFF, 8], [1, 128]]))
        return wb

    sgt = [K.sb("sgt", [128, 512], F32) for _ in range(2)]
    it = 0
    for ft in range(NF):
        wg = fload(w_ffn_gate, ft * 128)
        wu = fload(w_ffn_up, ft * 128)
        if ft >= 1:
            K.dma("pool", wd[:, ft - 1, :], w_ffn_down[(ft - 1) * 128:ft * 128, :])
        if ft == NF - 1:
            K.dma("pool", wd[:, ft, :], w_ffn_down[ft * 128:(ft + 1) * 128, :])
        for tb in range(5):
            col0, n = TBLK[tb]
            cs = slice(col0, col0 + n)
            psg, psu = PSF(), PSF()
            for kt in range(8):
                K.mm(psg[:, :n], wg[:, kt, :], hT[:, kt, cs], start=(kt == 0), stop=(kt == 7))
            for kt in range(8):
                K.mm(psu[:, :n], wu[:, kt, :], hT[:, kt, cs], start=(kt == 0), stop=(kt == 7))
            sg = sgt[it % 2]
            it += 1
            K.act(sg[:, :n], psg[:, :n], AF.Silu)
            K.tt("dve", aT[:, ft, cs], psu[:, :n], sg[:, :n], ALU.mult)
    pn_bufs = [(K.sb("pn_st", [128, 8], F32), K.sb("pn_x", [128, D], F32), K.sb("pn_t", [128, D], F32))
               for _ in range(2)]
    dn_ps = {}

    def dn_a(i):
        rows = 128 if i < 16 else NS
        tcs = slice(i * 128, i * 128 + rows)
        halves = [PSF(), PSF()]
        for hf in range(2):
            for kt in range(NF):
                K.mm(halves[hf][:rows, :], aT[:, kt, tcs], wd[:, kt, hf * 512:(hf + 1) * 512],
                     start=(kt == 0), stop=(kt == NF - 1))
        dn_ps[i] = halves

    def dn_b(i):
        rows = 128 if i < 16 else NS
        dst = y_p[i * 128:(i + 1) * 128, :] if i < 16 else y_s[:, :]
        pn_b(dn_ps[i], rows, x1_d[i * 128:i * 128 + rows, :], gpost2, dst, pn_bufs[i % 2])

    for step in range(17 + 1):
        if step < 17:
            dn_a(step)
        if step - 1 >= 0:
            dn_b(step - 1)
    K.pop_scope()
    K.finish()
    es.close()
    return nc


_W_NAMES = ["norm_pre_mix", "norm_post_mix", "norm_pre_ffn", "norm_post_ffn", "w_in", "mu_shift", "w0",
            "w_decay_up", "a0", "w_aaa_up", "w_gate_up", "k_k", "k_a", "r_k", "lnx_g", "lnx_b", "w_rwkv_out",
            "s5_lam_re", "s5_lam_im", "s5_log_dt", "s5_b_re", "s5_b_im", "s5_c_re", "s5_c_im", "s5_d",
            "glu_w1", "glu_b1", "glu_w2", "glu_b2", "w_merge_out", "w_ffn_gate", "w_ffn_up", "w_ffn_down"]


def make_in_maps(inputs, cores):
    f = lambda a: np.ascontiguousarray(np.asarray(a, dtype=np.float32))
    shared = {n: f(inputs[n])[0] for n in _W_NAMES}
    maps = []
    for c in cores:
        m = dict(shared)
        m["x_p"] = f(inputs["x_prompt"][c])
        m["x_s"] = f(inputs["x_sample"][c * NS:(c + 1) * NS, 0])
        m["st_shift"] = f(inputs["state_shift"][0, c * NS:(c + 1) * NS])
        m["st_wkv"] = f(inputs["state_wkv"][0, c * NS:(c + 1) * NS]).reshape(NS * 8, 64 * 64)
        m["st_re"] = f(inputs["state_s5_re"][0, c * NS:(c + 1) * NS]).reshape(NS, 2048)
        m["st_im"] = f(inputs["state_s5_im"][0, c * NS:(c + 1) * NS]).reshape(NS, 2048)
        maps.append(m)
    return maps


def assemble(results):
    n = len(results)
    cat = lambda k: np.concatenate([np.asarray(r[k]) for r in results], axis=0)
    y_p = np.stack([np.asarray(r["y_p"]) for r in results], 0)
    y_s = cat("y_s").reshape(n * NS, 1, D)
    sh_p = cat("o_shift_p").reshape(1, n, SHIFT)
    wkv_p = np.stack([np.asarray(r["o_wkv_p"]) for r in results], 0).reshape(1, n, 8, 64, 64)
    re_p = cat("o_re_p").reshape(1, n, 32, 64)
    im_p = cat("o_im_p").reshape(1, n, 32, 64)
    sh_s = cat("o_shift_s").reshape(1, n * NS, SHIFT)
    wkv_s = cat("o_wkv_s").reshape(1, n * NS, 8, 64, 64)
    re_s = cat("o_re_s").reshape(1, n * NS, 32, 64)
    im_s = cat("o_im_s").reshape(1, n * NS, 32, 64)
    return tuple(np.ascontiguousarray(a, dtype=np.float32) for a in
                 (y_p, y_s, sh_p, wkv_p, re_p, im_p, sh_s, wkv_s, re_s, im_s))


def kernel(**inputs):
    nc = build()
    in_maps = make_in_maps(inputs, list(range(NCORES)))
    res = run_bass_kernel_spmd(nc, in_maps, core_ids=list(range(NCORES)))
    return assemble(res.results)
```

```python
import math
from contextlib import ExitStack
from functools import partial
import numpy as np
import concourse.bass as bass
import concourse.mybir as mybir
from concourse.bass_utils import run_bass_kernel_spmd

F32 = mybir.dt.float32
BF16 = mybir.dt.bfloat16
I32 = mybir.dt.int32
AF = mybir.ActivationFunctionType
ALU = mybir.AluOpType
AX = mybir.AxisListType

T = 2048
NS = 16
TT = T + NS
D = 1024
RW = 512
SHIFT = 1664
PROJ = 4224
DFF = 2816
NCORES = 8
CDEC = math.exp(-0.5)
TBLK = [(0, 512), (512, 512), (1024, 512), (1536, 512), (2048, 16)]
SAME_ENGINE_SYNC = True
LS = 4


class V:
    __slots__ = ("buf", "ap")

    def __init__(self, buf, ap):
        self.buf = buf
        self.ap = ap

    def __getitem__(self, idx):
        return V(self.buf, self.ap[idx])

    def bc(self, shape):
        return V(self.buf, self.ap.to_broadcast(list(shape)))

    def re(self, s, **kw):
        return V(self.buf, self.ap.rearrange(s, **kw))

    def un(self, axis):
        return V(self.buf, self.ap.unsqueeze(axis))

    def bitcast(self, dt):
        return V(self.buf, self.ap.bitcast(dt))


class Buf:
    def __init__(self, name, handle):
        self.name = name
        self.h = handle
        self.writes = {}
        self.reads = {}
        self.dsem = None
        self.dcnt = 0
        self.is_psum = False

    def __getitem__(self, idx):
        return V(self, self.h[idx])

    def v(self):
        return V(self, self.h[:])


class Kern:
    def __init__(self, nc, es):
        self.nc = nc
        self.es = es
        self.root_es = es
        self.eng = {"pe": nc.tensor, "act": nc.scalar, "dve": nc.vector, "pool": nc.gpsimd, "sp": nc.sync}
        self.sem = {}
        self.cnt = {}
        for e in ("pe", "act", "dve", "pool"):
            self.sem[e] = es.enter_context(nc.semaphore("s_" + e))
            self.cnt[e] = 0
        self.obs = {e: {} for e in self.eng}
        self.semname = {}
        self.dsems = []
        self.nbuf = 0
        self.psf_i = 0
        self.psb_i = 0

    def sb(self, name, shape, dt):
        self.nbuf += 1
        import os
        if os.environ.get("KDBG"):
            sz = int(np.prod(shape[1:])) * (2 if dt == BF16 else 4)
            self._tot = getattr(self, "_tot", 0) + sz
            print(f"alloc {name} {shape} {sz} depth={len(getattr(self, '_saved', []))} cum_noFree={self._tot}")
        h = self.es.enter_context(self.nc.sbuf_tensor(f"{name}_{self.nbuf}", list(shape), dt))
        b = Buf(name, h)
        b.writes = dict(getattr(self, "pending", {}))
        if not hasattr(self, "scope_bufs"):
            self.scope_bufs = [[]]
        self.scope_bufs[-1].append(b)
        return b

    def barrier(self):
        for e in ("pe", "act", "dve", "pool", "sp"):
            for e2 in ("pe", "act", "dve", "pool"):
                if e2 != e and self.cnt[e2]:
                    self._wait(e, e2, self.sem[e2], self.cnt[e2])
            for ob in self.dsems:
                self._wait(e, "d:" + ob.name + str(id(ob)), ob.dsem, ob.dcnt)

    def push_scope(self):
        self._saved = getattr(self, "_saved", [])
        self._saved.append(self.es)
        self.es = ExitStack()
        if not hasattr(self, "scope_bufs"):
            self.scope_bufs = [[]]
        self.scope_bufs.append([])

    def pop_scope(self):
        self.pending = getattr(self, "pending", {})
        for b in self.scope_bufs.pop():
            for d in (b.writes, b.reads):
                for k, (sm, v) in d.items():
                    if self.pending.get(k, (None, 0))[1] < v:
                        self.pending[k] = (sm, v)
        self.es.close()
        self.es = self._saved.pop()

    def ps(self, name, shape, dt):
        self.nbuf += 1
        h = self.es.enter_context(self.nc.psum_tensor(f"{name}_{self.nbuf}", list(shape), dt))
        b = Buf(name, h)
        b.is_psum = True
        return b

    def dram(self, name, shape, dt, kind="Internal"):
        h = self.nc.dram_tensor(name, list(shape), dt, kind=kind)
        return Buf(name, h.ap())

    def newsem(self, name):
        s = self.root_es.enter_context(self.nc.semaphore(name))
        return s

    def _wait(self, e, sem_key, sem, val):
        o = self.obs[e]
        if o.get(sem_key, 0) >= val:
            return
        self.eng[e].wait_ge(sem, val)
        o[sem_key] = val

    def _deps(self, e, W, R):
        need = {}
        for b in R:
            for k, (s, v) in b.writes.items():
                if need.get(k, (None, 0))[1] < v:
                    need[k] = (s, v)
            if b.is_psum:
                for k, (s, v) in b.reads.items():
                    if k != e and need.get(k, (None, 0))[1] < v:
                        need[k] = (s, v)
        for b in W:
            for k, (s, v) in b.writes.items():
                if need.get(k, (None, 0))[1] < v:
                    need[k] = (s, v)
            for k, (s, v) in b.reads.items():
                if need.get(k, (None, 0))[1] < v:
                    need[k] = (s, v)
        for k, (s, v) in need.items():
            if k == e:
                if e == "pe" or not SAME_ENGINE_SYNC:
                    continue
            self._wait(e, k, s, v)

    def _bufs(self, vs):
        out = []
        for x in vs:
            if isinstance(x, V):
                if x.buf not in out:
                    out.append(x.buf)
            elif isinstance(x, Buf):
                if x not in out:
                    out.append(x)
        return out

    def op(self, e, fn, W, R):
        Wb = self._bufs(W)
        Rb = self._bufs(R)
        self._deps(e, Wb, Rb)
        inst = fn()
        self.cnt[e] += 1
        c = self.cnt[e]
        inst.then_inc(self.sem[e], 1)
        for b in Wb:
            b.reads = {}
            b.writes[e] = (self.sem[e], c)
        for b in Rb:
            if b not in Wb:
                b.reads[e] = (self.sem[e], c)
        return inst

    def dma(self, q, out, in_, sem_buf=None, slow=False):
        Wb = self._bufs([out])
        Rb = self._bufs([in_])
        self._deps(q, Wb, Rb)
        ob = sem_buf if sem_buf is not None else out.buf
        if ob.dsem is None:
            ob.dsem = self.newsem("d_" + ob.name + str(len(self.dsems)))
            self.dsems.append(ob)
        if slow:
            with self.nc.allow_non_contiguous_dma(reason="small parameter layout load"):
                inst = self.eng[q].dma_start(out=out.ap, in_=in_.ap)
        else:
            inst = self.eng[q].dma_start(out=out.ap, in_=in_.ap)
        ob.dcnt += 16
        inst.then_inc(ob.dsem, 16)
        key = "d:" + ob.name + str(id(ob))
        for b in Wb:
            b.reads = {}
            b.writes[key] = (ob.dsem, ob.dcnt)
        for b in Rb:
            if b not in Wb:
                b.reads[key] = (ob.dsem, ob.dcnt)
        return inst

    def finish(self):
        for ob in self.dsems:
            self._wait("sp", "d:" + ob.name + str(id(ob)), ob.dsem, ob.dcnt)
        for e in ("pe", "act", "dve", "pool"):
            if self.cnt[e]:
                self._wait("sp", e, self.sem[e], self.cnt[e])

    @staticmethod
    def _a(x):
        return x.ap if isinstance(x, V) else x

    def act(self, out, in_, func, bias=None, scale=None, accum=None):
        kw = {}
        if bias is not None:
            kw["bias"] = self._a(bias)
        if scale is not None:
            kw["scale"] = self._a(scale)
        if accum is not None:
            kw["accum_out"] = self._a(accum)
        return self.op("act", lambda: self.nc.scalar.activation(out=out.ap, in_=in_.ap, func=func, **kw),
                       [out, accum], [in_, bias, scale])

    def tt(self, e, out, in0, in1, op):
        return self.op(e, lambda: self.eng[e].tensor_tensor(out=out.ap, in0=in0.ap, in1=in1.ap, op=op),
                       [out], [in0, in1])

    def ts(self, e, out, in0, s1, s2, op0, op1=None):
        if op1 is None:
            return self.op(e, lambda: self.eng[e].tensor_scalar(out=out.ap, in0=in0.ap, scalar1=self._a(s1),
                                                               scalar2=None, op0=op0), [out], [in0, s1])
        return self.op(e, lambda: self.eng[e].tensor_scalar(out=out.ap, in0=in0.ap, scalar1=self._a(s1),
                                                           scalar2=self._a(s2), op0=op0, op1=op1),
                       [out], [in0, s1, s2])

    def stt(self, out, in0, scalar, in1, op0, op1, e="dve"):
        return self.op(e, lambda: self.eng[e].scalar_tensor_tensor(out=out.ap, in0=in0.ap, scalar=self._a(scalar),
                                                                  in1=in1.ap, op0=op0, op1=op1),
                       [out], [in0, scalar, in1])

    def copy(self, e, out, in_):
        if e == "act":
            return self.act(out, in_, AF.Copy)
        return self.op(e, lambda: self.eng[e].tensor_copy(out=out.ap, in_=in_.ap), [out], [in_])

    def memset(self, e, out, val):
        return self.op(e, lambda: self.eng[e].memset(out.ap, val), [out], [])

    def recip(self, out, in_):
        return self.op("dve", lambda: self.nc.vector.reciprocal(out=out.ap, in_=in_.ap), [out], [in_])

    def reduce(self, out, in_, op=ALU.add, axis=AX.X):
        return self.op("dve", lambda: self.nc.vector.tensor_reduce(out=out.ap, in_=in_.ap, axis=axis, op=op),
                       [out], [in_])

    def scan(self, out, d0, d1, init, op0=ALU.mult, op1=ALU.add):
        return self.op("dve", lambda: self.nc.vector.tensor_tensor_scan(out=out.ap, data0=d0.ap, data1=d1.ap,
                                                                       initial=self._a(init), op0=op0, op1=op1),
                       [out], [d0, d1, init])

    def mm(self, out, lhsT, rhs, start=True, stop=True):
        return self.op("pe", lambda: self.nc.tensor.matmul(out.ap, lhsT=lhsT.ap, rhs=rhs.ap, start=start, stop=stop),
                       [out], [lhsT, rhs])

    def tr(self, out, in_, ident):
        return self.op("pe", lambda: self.nc.tensor.transpose(out.ap, in_.ap, ident.ap), [out], [in_, ident])

    def aselect(self, out, in_, pattern, cmp, fill, base, cm):
        return self.op("pool", lambda: self.nc.gpsimd.affine_select(out=out.ap, in_=in_.ap, pattern=pattern,
                                                                   compare_op=cmp, fill=fill, base=base,
                                                                   channel_multiplier=cm), [out], [in_])

    def iota(self, out, pattern, base, cm):
        return self.op("pool", lambda: self.nc.gpsimd.iota(out.ap, pattern=pattern, base=base,
                                                          channel_multiplier=cm), [out], [])


def run_merged_g(lists):
    idx = [0] * len(lists)
    while True:
        best, bf = -1, 2.0
        for li, L in enumerate(lists):
            if idx[li] < len(L):
                frac = idx[li] / len(L)
                if frac < bf:
                    best, bf = li, frac
        if best < 0:
            break
        lists[best][idx[best]]()
        idx[best] += 1


def merge_lists(lists):
    out = []
    idx = [0] * len(lists)
    while True:
        best, bf = -1, 2.0
        for li, L in enumerate(lists):
            if idx[li] < len(L):
                frac = idx[li] / len(L)
                if frac < bf:
                    best, bf = li, frac
        if best < 0:
            break
        out.append(lists[best][idx[best]])
        idx[best] += 1
    return out


def dap(buf, offset, ap):
    base = buf.h if not hasattr(buf.h, "ap") or isinstance(buf.h, bass.AP) else buf.h
    t = base.tensor if isinstance(base, bass.AP) else base
    return V(buf, bass.AP(t, offset, [list(x) for x in ap]))


def build(stage=99):
    nc = bass.Bass("TRN2", target_bir_lowering=False)
    es = ExitStack()
    K = Kern(nc, es)

    def din(name, shape, dt=F32):
        return Buf(name, nc.dram_tensor(name, list(shape), dt, kind="ExternalInput").ap())

    def dout(name, shape):
        return Buf(name, nc.dram_tensor(name, list(shape), F32, kind="ExternalOutput").ap())

    x_p = din("x_p", [T, D])
    x_s = din("x_s", [NS, D])
    st_shift = din("st_shift", [NS, SHIFT])
    st_wkv = din("st_wkv", [NS * 8, 64 * 64])
    st_re = din("st_re", [NS, 2048])
    st_im = din("st_im", [NS, 2048])
    g_pre_mix = din("norm_pre_mix", [D])
    g_post_mix = din("norm_post_mix", [D])
    g_pre_ffn = din("norm_pre_ffn", [D])
    g_post_ffn = din("norm_post_ffn", [D])
    w_in = din("w_in", [D, PROJ])
    mu_shift = din("mu_shift", [SHIFT])
    w0 = din("w0", [RW])
    w_decay_up = din("w_decay_up", [32, RW])
    a0 = din("a0", [RW])
    w_aaa_up = din("w_aaa_up", [32, RW])
    w_gate_up = din("w_gate_up", [64, RW])
    k_k = din("k_k", [RW])
    k_a = din("k_a", [RW])
    r_k = din("r_k", [RW])
    lnx_g = din("lnx_g", [RW])
    lnx_b = din("lnx_b", [RW])
    w_rwkv_out = din("w_rwkv_out", [RW, D])
    s5_lam_re = din("s5_lam_re", [32, 64])
    s5_lam_im = din("s5_lam_im", [32, 64])
    s5_log_dt = din("s5_log_dt", [32])
    s5_b_re = din("s5_b_re", [32, 64, 16])
    s5_b_im = din("s5_b_im", [32, 64, 16])
    s5_c_re = din("s5_c_re", [32, 16, 64])
    s5_c_im = din("s5_c_im", [32, 16, 64])
    s5_d = din("s5_d", [RW])
    glu_w1 = din("glu_w1", [RW, D])
    glu_b1 = din("glu_b1", [D])
    glu_w2 = din("glu_w2", [RW, D])
    glu_b2 = din("glu_b2", [D])
    w_merge_out = din("w_merge_out", [D, D])
    w_ffn_gate = din("w_ffn_gate", [D, DFF])
    w_ffn_up = din("w_ffn_up", [D, DFF])
    w_ffn_down = din("w_ffn_down", [DFF, D])

    y_p = dout("y_p", [T, D])
    y_s = dout("y_s", [NS, D])
    o_shift_p = dout("o_shift_p", [1, SHIFT])
    o_wkv_p = dout("o_wkv_p", [8, 64, 64])
    o_re_p = dout("o_re_p", [1, 2048])
    o_im_p = dout("o_im_p", [1, 2048])
    o_shift_s = dout("o_shift_s", [NS, SHIFT])
    o_wkv_s = dout("o_wkv_s", [NS * 8, 64 * 64])
    o_re_s = dout("o_re_s", [NS, 2048])
    o_im_s = dout("o_im_s", [NS, 2048])

    x1_d = K.dram("x1_scratch", [TT, D], F32)

    psf = [K.ps("psf", [128, 512], F32) for _ in range(8)]

    def PSF():
        K.psf_i += 1
        return psf[K.psf_i % len(psf)]

    class BV:
        def __init__(self, buf):
            self.buf = buf

        def v(self):
            return V(self.buf, self.buf.h[:].bitcast(BF16))

        def __getitem__(self, idx):
            return self.v()[idx]

    def PSB():
        return BV(PSF())

    ones_f = K.sb("ones_f", [128, 128], F32)
    identf = K.sb("identf", [128, 128], F32)
    identb = K.sb("identb", [128, 128], BF16)
    epsT = K.sb("epsT", [128, 1], F32)
    K.memset("pool", ones_f.v(), 1.0)
    K.memset("pool", epsT.v(), 1e-6)
    K.aselect(identf.v(), ones_f.v(), [[-1, 128]], ALU.is_equal, 0.0, 0, 1)
    K.copy("pool", identb.v(), identf.v())

    pvec = K.sb("pvec", [128, 128], F32)
    PV = {}
    _pv = [0]

    def load_fm(name, src, ntile):
        c0 = _pv[0]
        _pv[0] += ntile
        K.dma("sp", pvec[:, c0:c0 + ntile], dap(src, 0, [[1, 128], [128, ntile]]), sem_buf=pvec, slow=True)
        PV[name] = c0
        return c0

    load_fm("g_pre_mix", g_pre_mix, 8)
    load_fm("g_pre_ffn", g_pre_ffn, 8)
    load_fm("mu", mu_shift, 13)
    load_fm("w0", w0, 4)
    load_fm("a0", a0, 4)
    load_fm("k_k", k_k, 4)
    load_fm("k_a", k_a, 4)
    load_fm("r_k", r_k, 4)
    load_fm("s5_d", s5_d, 4)
    load_fm("glu_b1", glu_b1, 8)
    load_fm("glu_b2", glu_b2, 8)

    def early():
        while getattr(K, "_saved", []):
            K.pop_scope()
        K.finish()
        es.close()
        return nc

    if stage == 0.1:
        return early()

    def pcol(name, i=0):
        c = PV[name] + i
        return pvec[:, c:c + 1]

    hT = K.sb("hT", [128, 8, TT], BF16)
    def norm_transpose(src_tile_fn, gname, dst):
        K.push_scope()
        NB = 4
        xring = [K.sb("xring", [128, D], F32) for _ in range(NB)]
        xn = [K.sb("xn", [128, D], BF16) for _ in range(NB)]
        junk = K.sb("junk", [128, D], BF16)
        stat = [K.sb("stat", [128, 4], F32) for _ in range(NB)]

        def nt_a(i):
            rows = 128 if i < 16 else NS
            xt, st, xb = xring[i % NB], stat[i % NB], xn[i % NB]
            K.dma("sp", xt[:rows, :], src_tile_fn(i))
            K.act(junk[:rows, :], xt[:rows, :], AF.Square, accum=st[:rows, 0:1])
            K.act(st[:rows, 1:2], st[:rows, 0:1], AF.Sqrt, bias=epsT[:rows, :], scale=1.0 / D)
            K.recip(st[:rows, 2:3], st[:rows, 1:2])
            K.act(xb[:rows, :], xt[:rows, :], AF.Copy, scale=st[:rows, 2:3])

        def nt_b(i):
            rows = 128 if i < 16 else NS
            col0 = i * 128
            xb = xn[i % NB]
            pb = PSB()
            pv = pb.v().re("p (k t) -> p k t", k=8)
            for kt in range(8):
                K.tr(pv[:, kt, :rows], xb[:rows, kt * 128:(kt + 1) * 128], identb[:rows, :rows])
            g0 = PV[gname]
            K.tt("dve", dst[:, :, col0:col0 + rows], pv[:, :, :rows],
                 pvec[:, g0:g0 + 8].un(2).bc([128, 8, rows]), ALU.mult)

        for step in range(17 + 2):
            if step < 17:
                nt_a(step)
            if 0 <= step - 2 < 17:
                nt_b(step - 2)
        K.pop_scope()

    def x_tile(i):
        return x_p[i * 128:(i + 1) * 128, :] if i < 16 else x_s[:, :]

    norm_transpose(x_tile, "g_pre_mix", hT)
    if stage == 0.2:
        return early()

    wring = [K.sb("wring", [128, 8, 128], BF16) for _ in range(3)]
    wr_i = [0]

    def load_w_cols(src, c0, ncols_total, kt_n=8, width=128):
        wr_i[0] += 1
        wb = wring[wr_i[0] % len(wring)]
        K.dma("pool", wb[:, :kt_n, :width],
              dap(src, c0, [[ncols_total, 128], [128 * ncols_total, kt_n], [1, width]]))
        return wb

    def proj_fm(wb, kt_n, src_act, tb, ps):
        col0, n = TBLK[tb]
        for kt in range(kt_n):
            K.mm(ps[:, :n], wb[:, kt, :], src_act[:, kt, col0:col0 + n], start=(kt == 0), stop=(kt == kt_n - 1))

    hgl_d = K.dram("hgl_scratch", [128, 4 * TT], BF16)
    TWO_PI = 2.0 * math.pi

    K.push_scope()
    ubf = K.sb("ubf", [128, 4, TT], BF16)
    wu4 = [K.sb("wu4", [128, 8, 128], BF16) for _ in range(4)]
    for ft in range(4):
        K.dma("pool", wu4[ft].v(), dap(w_in, SHIFT + ft * 128, [[PROJ, 128], [128 * PROJ, 8], [1, 128]]))
        for tb in range(5):
            col0, n = TBLK[tb]
            ps = PSF()
            proj_fm(wu4[ft], 8, hT, tb, ps)
            K.copy("act" if tb % 2 else "dve", ubf[:, ft, col0:col0 + n], ps[:, :n])

    if stage == 1.1:
        return early()
    sp_ = K.sb("s5par", [128, 16, 24], F32)
    (P_LRE, P_LIM, P_DT, P_MAG, P_TH, P_FRE, P_FIM, P_LBRE, P_LBIM, P_T0, P_T1, P_T2, P_T3,
     P_TH2, P_MAG2, P_G0R, P_G0I, P_C1, P_S1) = range(19)

    def sp(k):
        return sp_[:, :, k]

    K.dma("sp", sp(P_LRE), dap(s5_lam_re, 0, [[1, 128], [128, 16]]), slow=True)
    K.dma("sp", sp(P_LIM), dap(s5_lam_im, 0, [[1, 128], [128, 16]]), slow=True)
    for gl in range(2):
        K.dma("sp", sp_[gl * 64:(gl + 1) * 64, :, P_DT], dap(s5_log_dt, gl, [[0, 64], [2, 16]]), slow=True)
    K.act(sp(P_DT), sp(P_DT), AF.Exp)
    K.tt("dve", sp(P_T0), sp(P_LRE), sp(P_DT), ALU.mult)
    K.act(sp(P_MAG), sp(P_T0), AF.Exp)
    K.copy("dve", sp(P_MAG2), sp(P_MAG))
    for _k in range(LS - 1):
        K.tt("dve", sp(P_MAG2), sp(P_MAG2), sp(P_MAG), ALU.mult)
    K.tt("dve", sp(P_TH), sp(P_LIM), sp(P_DT), ALU.mult)
    K.ts("dve", sp(P_TH2), sp(P_TH), float(LS), None, ALU.mult)

    NH = 512 // LS
    LC = min(128, NH)
    NRC = NH // LC
    pw = K.sb("pw", [128, 16, 2 * (LS + 1)], F32)
    cosT = K.sb("cosT", [128, 16, LC], F32)
    sinT = K.sb("sinT", [128, 16, LC], F32)
    BT = [K.sb("BT", [128, 32, 128], BF16) for _ in range(LS)]
    CTb = K.sb("CTb", [128, 32, 128], BF16)
    VTb = [K.sb("VTb", [128, 32, 128], BF16) for _ in range(LS - 1)]
    Kt = [K.sb("Kt", [128, 4, 128], BF16) for _ in range(LS - 1)]
    K.push_scope()
    ji = K.sb("ji", [128, LC], I32)
    jf = K.sb("jf", [128, LC], F32)
    ang = K.sb("ang", [128, 16, LC], F32)
    rr = K.sb("rr", [128, 16, LC], F32)
    kf = K.sb("kf", [128, 16, LC], F32)
    ki = K.sb("ki", [128, 16, LC], I32)
    K.iota(ji.v(), [[1, LC]], 1, 0)
    K.copy("dve", jf.v(), ji.v())
    K.tt("dve", ang.v(), sp(P_TH2).un(2).bc([128, 16, LC]), jf.v().un(1).bc([128, 16, LC]), ALU.mult)

    def sin_reduced(out, a_in, shift, rr_, kf_, ki_):
        C1 = 6.28125
        C2 = TWO_PI - C1
        K.ts("dve", rr_, a_in, shift, 1.0 / TWO_PI, ALU.add, ALU.mult)
        K.copy("dve", ki_, rr_)
        K.copy("dve", kf_, ki_)
        K.ts("dve", rr_, a_in, shift, None, ALU.add)
        K.stt(rr_, kf_, -C1, rr_, ALU.mult, ALU.add)
        K.stt(rr_, kf_, -C2, rr_, ALU.mult, ALU.add)
        K.ts("dve", kf_, rr_, math.pi, None, ALU.is_gt)
        K.stt(rr_, kf_, -TWO_PI, rr_, ALU.mult, ALU.add)
        K.ts("dve", kf_, rr_, -math.pi, None, ALU.is_lt)
        K.stt(rr_, kf_, TWO_PI, rr_, ALU.mult, ALU.add)
        K.ts("dve", rr_, rr_, -math.pi, math.pi, ALU.max, ALU.min)
        K.act(out, rr_, AF.Sin)

    sin_reduced(sinT.v(), ang.v(), 0.0, rr.v(), kf.v(), ki.v())
    K.ts("dve", rr.v(), rr.v(), 0.5 * math.pi, None, ALU.add)
    K.ts("dve", kf.v(), rr.v(), math.pi, None, ALU.is_gt)
    K.stt(rr.v(), kf.v(), -TWO_PI, rr.v(), ALU.mult, ALU.add)
    K.ts("dve", rr.v(), rr.v(), -math.pi, math.pi, ALU.max, ALU.min)
    K.act(cosT.v(), rr.v(), AF.Sin)
    sin_reduced(sp(P_S1), sp(P_TH), 0.0, rr[:, :, 0], kf[:, :, 0], ki[:, :, 0])
    sin_reduced(sp(P_C1), sp(P_TH), 0.5 * math.pi, rr[:, :, 0], kf[:, :, 0], ki[:, :, 0])
    K.tt("dve", sp(P_LBRE), sp(P_MAG), sp(P_C1), ALU.mult)
    K.tt("dve", sp(P_LBIM), sp(P_MAG), sp(P_S1), ALU.mult)
    K.tt("dve", sp(P_T0), sp(P_LRE), sp(P_LRE), ALU.mult)
    K.tt("dve", sp(P_T1), sp(P_LIM), sp(P_LIM), ALU.mult)
    K.tt("dve", sp(P_T0), sp(P_T0), sp(P_T1), ALU.add)
    K.recip(sp(P_T0), sp(P_T0))
    K.ts("dve", sp(P_T1), sp(P_LBRE), -1.0, None, ALU.add)
    K.tt("dve", sp(P_T2), sp(P_T1), sp(P_LRE), ALU.mult)
    K.tt("dve", sp(P_T3), sp(P_LBIM), sp(P_LIM), ALU.mult)
    K.tt("dve", sp(P_T2), sp(P_T2), sp(P_T3), ALU.add)
    K.tt("dve", sp(P_FRE), sp(P_T2), sp(P_T0), ALU.mult)
    K.tt("dve", sp(P_T2), sp(P_LBIM), sp(P_LRE), ALU.mult)
    K.tt("dve", sp(P_T3), sp(P_T1), sp(P_LIM), ALU.mult)
    K.tt("dve", sp(P_T2), sp(P_T2), sp(P_T3), ALU.subtract)
    K.tt("dve", sp(P_FIM), sp(P_T2), sp(P_T0), ALU.mult)
    K.memset("dve", pw[:, :, 0], 1.0)
    K.memset("dve", pw[:, :, 1], 0.0)
    for k in range(1, LS + 1):
        pr_, pi_ = pw[:, :, 2 * (k - 1)], pw[:, :, 2 * (k - 1) + 1]
        K.tt("dve", sp(P_T0), pr_, sp(P_LBRE), ALU.mult)
        K.tt("dve", sp(P_T1), pi_, sp(P_LBIM), ALU.mult)
        K.tt("dve", pw[:, :, 2 * k], sp(P_T0), sp(P_T1), ALU.subtract)
        K.tt("dve", sp(P_T0), pr_, sp(P_LBIM), ALU.mult)
        K.tt("dve", sp(P_T1), pi_, sp(P_LBRE), ALU.mult)
        K.tt("dve", pw[:, :, 2 * k + 1], sp(P_T0), sp(P_T1), ALU.add)
    K.pop_scope()
    if stage == 1.2:
        return early()

    K.push_scope()
    CTf = [K.sb("CTf", [128, 16, 128], F32) for _ in range(2)]
    K.push_scope()
    Zc = K.sb("Zc", [128, 4 * 512], F32)
    for part, src in enumerate((s5_c_re, s5_c_im)):
        K.memset("pool", Zc.v(), 0.0)
        for g8 in range(8):
            K.dma("sp", dap(Zc, 16 * g8 * 2048 + 64 * g8, [[2048, 16], [512, 4], [1, 64]]),
                  dap(src, g8 * 1024, [[64, 16], [8192, 4], [1, 64]]))
        for q in range(4):
            ps = PSF()
            for j in range(4):
                K.tr(ps[:, j * 128:(j + 1) * 128], Zc[:, q * 512 + j * 128:q * 512 + (j + 1) * 128], identf.v())
            dst = CTf[part][:, q * 4:q * 4 + 4, :]
            if part == 0:
                K.copy("dve", dst, ps.v().re("p (j c) -> p j c", j=4))
            else:
                K.ts("dve", dst, ps.v().re("p (j c) -> p j c", j=4), -1.0, None, ALU.mult)
    K.pop_scope()
    Xr = K.sb("Xr", [128, 16 * 128], F32)
    Xi = K.sb("Xi", [128, 16 * 128], F32)
    for X, src in ((Xr, s5_b_re), (Xi, s5_b_im)):
        K.memset("pool", X.v(), 0.0)
        for gl in range(2):
            for a in range(4):
                K.dma("sp", dap(X, gl * 64 * 2048 + 16 * gl + a * 512, [[2048, 64], [160, 4], [1, 16]]),
                      dap(src, gl * 1024 + a * 8192, [[16, 64], [2048, 4], [1, 16]]), slow=True)
    c4v = CTb.v().re("p (i two) c -> p i two c", two=2)
    K.copy("act", c4v[:, :, 0, :], CTf[0].v())
    K.copy("act", c4v[:, :, 1, :], CTf[1].v())
    T1 = K.sb("T1", [128, 16, 128], F32)
    Xs = [K.sb("Xs", [128, 16, 128], F32) for _ in range(2)]
    T2 = Xs[0]
    bcv = lambda vv: vv.un(2).bc([128, 16, 128])

    def act_scale(dst, src3, g2):
        for i_ in range(16):
            K.act(dst[:, i_, :], src3[:, i_, :], AF.Copy, scale=g2[:, i_:i_ + 1])

    for jv in range(LS - 1):
        lr_, li_ = pw[:, :, 2 * (jv + 1)], pw[:, :, 2 * (jv + 1) + 1]
        v4v = VTb[jv].v().re("p (i two) c -> p i two c", two=2)
        K.tt("dve", T1.v(), CTf[0].v(), bcv(lr_), ALU.mult)
        K.tt("dve", T2.v(), CTf[1].v(), bcv(li_), ALU.mult)
        K.tt("dve", v4v[:, :, 0, :], T1.v(), T2.v(), ALU.add)
        K.tt("dve", T1.v(), CTf[1].v(), bcv(lr_), ALU.mult)
        K.tt("dve", T2.v(), CTf[0].v(), bcv(li_), ALU.mult)
        K.tt("dve", v4v[:, :, 1, :], T1.v(), T2.v(), ALU.subtract)
    gsc = K.sb("gsc", [128, 16, 4], F32)
    x3 = lambda b_: b_.v().re("p (i c) -> p i c", i=16)
    for v in range(LS):
        kpow = LS - 1 - v
        lr_, li_ = pw[:, :, 2 * kpow], pw[:, :, 2 * kpow + 1]
        K.tt("dve", gsc[:, :, 2], lr_, sp(P_FRE), ALU.mult)
        K.tt("dve", gsc[:, :, 3], li_, sp(P_FIM), ALU.mult)
        K.tt("dve", gsc[:, :, 0], gsc[:, :, 2], gsc[:, :, 3], ALU.subtract)
        K.tt("dve", gsc[:, :, 2], lr_, sp(P_FIM), ALU.mult)
        K.tt("dve", gsc[:, :, 3], li_, sp(P_FRE), ALU.mult)
        K.tt("dve", gsc[:, :, 1], gsc[:, :, 2], gsc[:, :, 3], ALU.add)
        gr, gi = bcv(gsc[:, :, 0]), bcv(gsc[:, :, 1])
        K.tt("dve", Xs[0].v(), x3(Xr), gr, ALU.mult)
        K.tt("dve", T1.v(), x3(Xi), gi, ALU.mult)
        K.tt("dve", Xs[0].v(), Xs[0].v(), T1.v(), ALU.subtract)
        K.tt("dve", Xs[1].v(), x3(Xr), gi, ALU.mult)
        K.tt("dve", T1.v(), x3(Xi), gr, ALU.mult)
        K.tt("dve", Xs[1].v(), Xs[1].v(), T1.v(), ALU.add)
        for part in range(2):
            for i4 in range(4):
                ps = PSF()
                for j in range(4):
                    K.tr(ps[:, j * 128:(j + 1) * 128], Xs[part][:, i4 * 4 + j, :], identf.v())
                dstb = BT[v].v().re("p (i two) c -> p i two c", two=2)[:, i4 * 4:i4 * 4 + 4, part, :]
                K.copy("act" if i4 % 2 else "dve", dstb, ps.v().re("p (j c) -> p j c", j=4))
        if kpow <= LS - 2:
            for q in range(4):
                ps = PSF()
                for j in range(4):
                    i = q * 4 + j
                    K.mm(ps[:, 0:128], Xs[0][:, i, :], CTf[0][:, i, :], start=(j == 0), stop=False)
                    K.mm(ps[:, 0:128], Xs[1][:, i, :], CTf[1][:, i, :], start=False, stop=(j == 3))
                K.copy("dve", Kt[kpow][:, q, :], ps[:, 0:128])
    K.pop_scope()
    if stage == 1.3:
        return early()

    x1b = [K.sb("x1sb", [128, 16, NS], BF16) for _ in range(2)]
    K.push_scope()
    s0_tm = [K.sb("s0tm", [NS, 2048], F32) for _ in range(2)]
    s0 = [K.sb("s0", [128, 16, NS], F32) for _ in range(2)]
    K.dma("sp", s0_tm[0].v(), st_re.v())
    K.dma("sp", s0_tm[1].v(), st_im.v())
    for part in range(2):
        ps = PSF()
        for i in range(16):
            K.tr(ps[:, i * NS:(i + 1) * NS], s0_tm[part][:, i * 128:(i + 1) * 128], identf[:NS, :NS])
        K.copy("dve", s0[part].v(), ps[:, :16 * NS].re("p (i n) -> p i n", i=16))
    psr = PSF()
    psi = PSF()
    for i in range(16):
        K.mm(psr[:, i * NS:(i + 1) * NS], BT[LS - 1][:, 2 * i, :], ubf[:, i // 4, T:TT])
        K.mm(psi[:, i * NS:(i + 1) * NS], BT[LS - 1][:, 2 * i + 1, :], ubf[:, i // 4, T:TT])
    sw = [K.sb("sw", [128, 16, NS], F32) for _ in range(2)]
    x1 = [K.sb("x1s", [128, 16, NS], F32) for _ in range(2)]
    bc16 = lambda k: sp(k).un(2).bc([128, 16, NS])
    pr3 = psr[:, :16 * NS].re("p (i n) -> p i n", i=16)
    pi3 = psi[:, :16 * NS].re("p (i n) -> p i n", i=16)
    K.tt("dve", sw[0].v(), s0[0].v(), bc16(P_LBRE), ALU.mult)
    K.tt("dve", sw[1].v(), s0[1].v(), bc16(P_LBIM), ALU.mult)
    K.tt("dve", sw[0].v(), sw[0].v(), sw[1].v(), ALU.subtract)
    K.tt("dve", x1[0].v(), sw[0].v(), pr3, ALU.add)
    K.tt("dve", sw[0].v(), s0[1].v(), bc16(P_LBRE), ALU.mult)
    K.tt("dve", sw[1].v(), s0[0].v(), bc16(P_LBIM), ALU.mult)
    K.tt("dve", sw[0].v(), sw[0].v(), sw[1].v(), ALU.add)
    K.tt("dve", x1[1].v(), sw[0].v(), pi3, ALU.add)
    for part in range(2):
        K.copy("act", x1b[part].v(), x1[part].v())
        xs_tm = s0_tm[part]
        for i4 in range(4):
            ps = PSF()
            for j in range(4):
                i = i4 * 4 + j
                K.tr(ps[:NS, j * 128:(j + 1) * 128], x1[part][:, i, :], identf.v())
            K.copy("dve", xs_tm[:, i4 * 512:(i4 + 1) * 512], ps[:NS, :])
        K.dma("sp", (o_re_s if part == 0 else o_im_s).v(), xs_tm.v())
    K.pop_scope()
    if stage == 1.4:
        return early()

    hgl = K.sb("hgl", [128, 4, TT], BF16)
    carry = K.sb("carry", [128, 16, 2], F32)
    K.memset("dve", carry.v(), 0.0)
    UN = [dict(bR=K.sb("bR", [128, NH], F32), bI=K.sb("bI", [128, NH], F32),
               wR=K.sb("wR", [128, NH], F32), wI=K.sb("wI", [128, NH], F32),
               t1=K.sb("t1", [128, NH], F32), t2=K.sb("t2", [128, NH], F32)) for _ in range(4)]
    xbr = [K.sb("xbr", [128, NH + 1], BF16) for _ in range(4)]
    xbi = [K.sb("xbi", [128, NH + 1], BF16) for _ in range(4)]
    yv = [K.sb("yv", [128, 512], F32) for _ in range(2)]
    y2 = [K.sb("y2", [128, 512], F32) for _ in range(2)]

    def gelu_to(dst_bf, yv_, y2_):
        K.act(y2_, yv_, AF.Square)
        K.act(y2_, y2_, AF.Identity, bias=1.0, scale=0.044715)
        K.tt("pool", y2_, y2_, yv_, ALU.mult)
        K.act(y2_, y2_, AF.Sigmoid, scale=1.5957691216057308)
        K.tt("pool", dst_bf, y2_, yv_, ALU.mult)

    vr = lambda b: b.v().re("p (a l) -> p a l", a=NRC)
    for q in range(4):
        for tb in range(4):
            col0, n = TBLK[tb]
            ugrp = ubf[:, q, col0:col0 + n].re("p (c j) -> p j c", j=LS)
            for j in range(4):
                i = q * 4 + j
                U = UN[j]
                psr = PSF()
                psi = PSF()
                for v in range(LS):
                    K.mm(psr[:, :NH], BT[v][:, 2 * i, :], ugrp[:, v, :], start=(v == 0), stop=(v == LS - 1))
                for v in range(LS):
                    K.mm(psi[:, :NH], BT[v][:, 2 * i + 1, :], ugrp[:, v, :], start=(v == 0), stop=(v == LS - 1))
                tC = cosT[:, i, :].un(1).bc([128, NRC, LC])
                tS = sinT[:, i, :].un(1).bc([128, NRC, LC])
                pr2 = psr[:, :NH].re("p (a l) -> p a l", a=NRC)
                pi2 = psi[:, :NH].re("p (a l) -> p a l", a=NRC)
                K.tt("dve", vr(U["t1"]), pr2, tC, ALU.mult)
                K.tt("dve", vr(U["t2"]), pi2, tS, ALU.mult)
                K.tt("pool", U["bR"].v(), U["t1"].v(), U["t2"].v(), ALU.add)
                K.tt("dve", vr(U["wR"]), pi2, tC, ALU.mult)
                K.tt("dve", vr(U["wI"]), pr2, tS, ALU.mult)
                K.tt("dve", U["bI"].v(), U["wR"].v(), U["wI"].v(), ALU.subtract)
                K.copy("act", xbr[j][:, 0:1], carry[:, i, 0:1])
                K.copy("act", xbi[j][:, 0:1], carry[:, i, 1:2])
            for a in range(NRC):
                cs = slice(a * LC, (a + 1) * LC)
                for j in range(4):
                    i = q * 4 + j
                    U = UN[j]
                    rho = sp_[:, i, P_MAG2:P_MAG2 + 1].bc([128, LC])
                    if a == 0:
                        ire, iim = carry[:, i, 0:1], carry[:, i, 1:2]
                    else:
                        ire, iim = U["bR"][:, a * LC - 1:a * LC], U["bI"][:, a * LC - 1:a * LC]
                    K.scan(U["wR"][:, cs], rho, U["bR"][:, cs], ire)
                    K.scan(U["wI"][:, cs], rho, U["bI"][:, cs], iim)
                for j in range(4):
                    i = q * 4 + j
                    U = UN[j]
                    K.tt("pool", U["t1"][:, cs], U["wR"][:, cs], cosT[:, i, :], ALU.mult)
                    K.tt("dve", U["t2"][:, cs], U["wI"][:, cs], sinT[:, i, :], ALU.mult)
                    K.tt("dve", U["bR"][:, cs], U["t1"][:, cs], U["t2"][:, cs], ALU.subtract)
                    K.tt("pool", U["t1"][:, cs], U["wR"][:, cs], sinT[:, i, :], ALU.mult)
                    K.tt("dve", U["t2"][:, cs], U["wI"][:, cs], cosT[:, i, :], ALU.mult)
                    K.tt("dve", U["bI"][:, cs], U["t1"][:, cs], U["t2"][:, cs], ALU.add)
            for j in range(4):
                i = q * 4 + j
                U = UN[j]
                K.copy("dve", carry[:, i, 0:1], U["bR"][:, NH - 1:NH])
                K.copy("dve", carry[:, i, 1:2], U["bI"][:, NH - 1:NH])
                K.copy("act", xbr[j][:, 1:NH + 1], U["bR"].v())
                K.copy("act", xbi[j][:, 1:NH + 1], U["bI"].v())
            yy = yv[tb % 2]
            yyp = yy.v().re("p (j c) -> p j c", j=LS)
            for jo in range(LS):
                psy = PSF()
                if jo == LS - 1:
                    for j in range(4):
                        i = q * 4 + j
                        K.mm(psy[:, :NH], CTb[:, 2 * i, :], xbr[j][:, 1:NH + 1], start=(j == 0), stop=False)
                        K.mm(psy[:, :NH], CTb[:, 2 * i + 1, :], xbi[j][:, 1:NH + 1], start=False, stop=(j == 3))
                else:
                    for j in range(4):
                        i = q * 4 + j
                        K.mm(psy[:, :NH], VTb[jo][:, 2 * i, :], xbr[j][:, 0:NH], start=(j == 0), stop=False)
                        K.mm(psy[:, :NH], VTb[jo][:, 2 * i + 1, :], xbi[j][:, 0:NH], start=False, stop=False)
                    for ii in range(jo + 1):
                        K.mm(psy[:, :NH], Kt[jo - ii][:, q, :], ugrp[:, ii, :], start=False, stop=(ii == jo))
                K.copy("act", yyp[:, jo, :], psy[:, :NH])
            psu = PSF()
            proj_fm(wu4[q], 8, hT, tb, psu)
            K.stt(yyp, psu.v().re("p (c j) -> p j c", j=LS), pcol("s5_d", q), yyp, ALU.mult, ALU.add)
            y2p = y2[tb % 2].v().re("p (j c) -> p j c", j=LS)
            gelu_to(hgl[:, q, col0:col0 + n].re("p (c j) -> p j c", j=LS), yyp, y2p)
        psy = PSF()
        for j in range(4):
            i = q * 4 + j
            K.mm(psy[:, :NS], CTb[:, 2 * i, :], x1b[0][:, i, :], start=(j == 0), stop=False)
            K.mm(psy[:, :NS], CTb[:, 2 * i + 1, :], x1b[1][:, i, :], start=False, stop=(j == 3))
        K.copy("act", yv[0][:, :NS], psy[:, :NS])
        psu = PSF()
        proj_fm(wu4[q], 8, hT, 4, psu)
        K.stt(yv[0][:, :NS], psu[:, :NS], pcol("s5_d", q), yv[0][:, :NS], ALU.mult, ALU.add)
        gelu_to(hgl[:, q, T:TT], yv[0][:, :NS], y2[0][:, :NS])
    if stage == 1.5:
        return early()
    K.dma("sp", dap(o_re_p, 0, [[1, 128], [128, 16]]), carry[:, :, 0], slow=True)
    K.dma("sp", dap(o_im_p, 0, [[1, 128], [128, 16]]), carry[:, :, 1], slow=True)
    K.dma("sp", hgl_d.v(), hgl.v().re("p k t -> p (k t)"))
    K.pop_scope()
    if stage == 2:
        zt = K.sb("zt", [128, 4096], F32)
        K.memset("pool", zt.v(), 0.0)
        K.dma("sp", o_shift_p.v(), zt[0:1, 0:SHIFT])
        K.dma("sp", o_shift_s.v(), zt[0:NS, 0:SHIFT])
        K.dma("sp", o_wkv_p.v().re("h v k -> h (v k)"), zt[0:8, :])
        K.dma("sp", o_wkv_s.v(), zt[:, :])
        for i in range(16):
            K.dma("sp", y_p[i * 128:(i + 1) * 128, :], zt[:, 0:D])
        K.dma("sp", y_s.v(), zt[0:NS, 0:D])
        return early()
    K.push_scope()
    lnxg = K.sb("lnxg", [128, RW], F32)
    lnxb = K.sb("lnxb", [128, RW], F32)
    K.dma("sp", lnxg.v(), dap(lnx_g, 0, [[0, 128], [1, RW]]))
    K.dma("sp", lnxb.v(), dap(lnx_b, 0, [[0, 128], [1, RW]]))
    par = K.sb("spar", [128, 3, 64], F32)
    for tk in range(NS):
        for j, src in enumerate((lnx_g, lnx_b, r_k)):
            K.dma("sp", par[tk * 8:(tk + 1) * 8, j, :], dap(src, 0, [[64, 8], [1, 64]]), sem_buf=par)

    lora_bf = K.sb("lora_bf", [128, TT], BF16)
    lora_up = K.sb("lora_up", [128, RW], BF16)
    oaT = K.sb("oaT", [128, 4, TT], BF16)
    o_d = K.dram("o_scratch", [T, RW], F32)
    bonus_d = K.dram("bonus_scratch", [T, RW], F32)
    samp_d = K.dram("samp_scratch", [NS * 8, 7, 64], F32)
    sampo_d = K.dram("sampo_scratch", [NS, RW], F32)

    K.push_scope()
    mask4 = K.sb("mask4", [128, 4, 128], F32)
    maskNT = K.sb("maskNT", [128, 128], F32)
    resetm = K.sb("resetm", [128, T], BF16)
    blk1 = K.sb("blk1", [128, 128], F32)
    eps_ln = K.sb("eps_ln", [128, 1], F32)
    K.memset("pool", eps_ln.v(), 64e-5)
    for j in range(4):
        K.aselect(mask4[:, j, :], ones_f.v(), [[1, 128]], ALU.is_gt if j % 2 == 0 else ALU.is_ge, 0.0, 0, -1)
    K.aselect(maskNT.v(), ones_f.v(), [[-1, 128]], ALU.is_gt, 0.0, 0, 1)
    K.memset("pool", resetm.v(), 1.0)
    K.memset("pool", resetm.v().re("p (c l) -> p c l", l=128)[:, :, 0:1], 0.0)
    K.memset("pool", blk1.v(), 0.0)
    K.memset("pool", blk1[0:64, 0:64], 1.0)
    K.memset("pool", blk1[64:128, 64:128], 1.0)

    K.dma("pool", lora_up[0:32, :], w_decay_up.v())
    K.dma("pool", lora_up[32:64, :], w_aaa_up.v())
    K.dma("pool", lora_up[64:128, :], w_gate_up.v())
    sh0T = K.sb("sh0T", [128, 13, NS], F32)
    shout = [K.sb("shout", [NS + 1, 128], F32) for _ in range(1)]
    K.push_scope()
    sh_tm = K.sb("sh_tm", [NS, SHIFT], F32)
    K.dma("sp", sh_tm.v(), st_shift.v())
    ps = PSF()
    for ft in range(13):
        K.tr(ps[:, ft * NS:(ft + 1) * NS], sh_tm[:, ft * 128:(ft + 1) * 128], identf[:NS, :NS])
    K.copy("dve", sh0T.v(), ps[:, :13 * NS].re("p (f n) -> p f n", f=13))
    K.pop_scope()

    omm = K.sb("omm", [128, 13], F32)
    K.ts("dve", omm.v(), pvec[:, PV["mu"]:PV["mu"] + 13], -1.0, 1.0, ALU.mult, ALU.add)
    omka = K.sb("omka", [128, 4], F32)
    K.ts("dve", omka.v(), pvec[:, PV["k_a"]:PV["k_a"] + 4], -1.0, 1.0, ALU.mult, ALU.add)

    SC = [K.sb("scr", [128, TT], F32) for _ in range(6)]

    def proj_shift_tile(ft, pr, tmp, xr_out, xr_out_dt_bf=None, samp32=None, wb=None):
        if wb is None:
            wb = load_w_cols(w_in, ft * 128, PROJ)
        for tb in range(5):
            col0, n = TBLK[tb]
            ps = PSF()
            proj_fm(wb, 8, hT, tb, ps)
            K.copy("act", pr[:, col0:col0 + n], ps[:, :n])
            K.act(tmp[:, col0:col0 + n], ps[:, :n], AF.Copy, scale=omm[:, ft:ft + 1])
        ps = PSF()
        K.tr(ps[:NS + 1, :128], pr[:, T - 1:TT], identf.v())
        so = shout[0]
        K.copy("dve", so.v(), ps[:NS + 1, :128])
        K.dma("sp", o_shift_p[0:1, ft * 128:(ft + 1) * 128], so[0:1, :])
        K.dma("sp", o_shift_s[:, ft * 128:(ft + 1) * 128], so[1:NS + 1, :])
        mu_c = pcol("mu", ft)
        K.stt(xr_out[:, 1:T], pr[:, 0:T - 1], mu_c, tmp[:, 1:T], ALU.mult, ALU.add)
        K.copy("pool", xr_out[:, 0:1], tmp[:, 0:1])
        K.stt(xr_out[:, T:TT], sh0T[:, ft, :], mu_c, tmp[:, T:TT], ALU.mult, ALU.add)
        if samp32 is not None:
            K.stt(samp32, sh0T[:, ft, :], mu_c, tmp[:, T:TT], ALU.mult, ALU.add)

    proj_shift_tile(12, SC[0], SC[1], SC[2].v())
    K.act(lora_bf[0:32, :], SC[2][0:32, :], AF.Tanh)
    K.act(lora_bf[32:64, :], SC[2][32:64, :], AF.Copy)
    K.act(lora_bf[64:128, :], SC[2][64:128, :], AF.Sigmoid)

    vbf = K.sb("vbf", [128, TT], BF16)
    vs32 = K.sb("vs32", [128, NS], F32)
    AR = K.sb("AR", [128, 16, 2, 128], BF16)
    bti = K.sb("bti", [128, T], BF16)
    kti = K.sb("kti", [128, T], BF16)
    bhat = K.sb("bhat", [128, T], BF16)
    khat = K.sb("khat", [128, T], BF16)
    Wfm = bhat
    TMq = K.sb("TMq", [128, 16, 4, 128], BF16)
    gL = K.sb("gL", [128, 16], F32)
    Sf = K.sb("Sf", [128, 64], F32)
    Sb = K.sb("Sb", [128, 64], BF16)
    smp = K.sb("smp", [NS, 3, 128], F32)
    decs = K.sb("decs", [128, NS], F32)
    avs = K.sb("avs", [128, NS], F32)
    NPR = 16
    ABK = [K.sb("ABK", [128, 2, 128], BF16) for _ in range(NPR)]
    NPQ = 8
    ATMP = [K.sb("ATMP", [128, 2, 128], BF16) for _ in range(NPQ)]
    P0T = [K.sb("P0T", [128, 128], BF16) for _ in range(NPQ)]
    PMb = [[K.sb("PMb", [128, 3, 128], BF16) for _ in range(2)] for _ in range(NPQ)]
    Ybf = [K.sb("Ybf", [128, 64], BF16) for _ in range(NPQ)]
    QTb = [[K.sb("QTb", [128, 128], BF16) for _ in range(2)] for _ in range(NPQ)]
    Ubar = [K.sb("Ubar", [128, 64], F32) for _ in range(NPR)]
    Ut = [K.sb("Ut", [128, 64], BF16) for _ in range(4)]
    Otile = [K.sb("Otile", [128, 128], F32) for _ in range(2)]
    btile = [K.sb("btile", [128, 512], F32) for _ in range(1)]
    wkvo = Otile[0]

    def derivedA(hp):
        xr_r, xr_k = SC[0], SC[1]
        wbs = [load_w_cols(w_in, (q_ * 4 + hp) * 128, PROJ) for q_ in range(3)]
        yield
        proj_shift_tile(0 * 4 + hp, SC[2], SC[3], xr_r.v(), wb=wbs[0])
        yield
        proj_shift_tile(1 * 4 + hp, SC[2], SC[3], xr_k.v(), wb=wbs[1])
        yield
        proj_shift_tile(2 * 4 + hp, SC[2], SC[3], vbf.v(), samp32=vs32.v(), wb=wbs[2])
        yield
        sigd, csum, afm, kkn = SC[2], SC[3], SC[4], SC[5]
        hc = slice(hp * 128, (hp + 1) * 128)
        for tb in range(5):
            col0, n = TBLK[tb]
            ps = PSF()
            K.mm(ps[:, :n], lora_up[0:32, hc], lora_bf[0:32, col0:col0 + n])
            K.act(sigd[:, col0:col0 + n], ps[:, :n], AF.Sigmoid, bias=pcol("w0", hp))
            yield
            ps = PSF()
            K.mm(ps[:, :n], lora_up[32:64, hc], lora_bf[32:64, col0:col0 + n])
            K.act(afm[:, col0:col0 + n], ps[:, :n], AF.Sigmoid, bias=pcol("a0", hp))
            yield
        K.act(kkn.v(), xr_k.v(), AF.Square, scale=pcol("k_k", hp))
        yield
        for tb in range(5):
            col0, n = TBLK[tb]
            ps = PSF()
            K.mm(ps[:, :n], blk1.v(), kkn[:, col0:col0 + n])
            K.ts("dve", csum[:, col0:col0 + n], ps[:, :n], 1e-24, None, ALU.max)
            yield
        K.act(csum.v(), csum.v(), AF.Ln)
        yield
        K.act(csum.v(), csum.v(), AF.Exp, scale=-0.5)
        yield
        K.stt(kkn.v(), xr_k.v(), pcol("k_k", hp), csum.v(), ALU.mult, ALU.mult)
        yield
        K.ts("dve" if hp == 0 else "pool", csum.v(), afm.v(), pcol("k_a", hp), omka[:, hp:hp + 1], ALU.mult, ALU.add)
        yield
        K.tt("dve" if hp == 0 else "pool", xr_k.v(), xr_k.v(), csum.v(), ALU.mult)
        yield
        kmod = xr_k
        K.tt("dve" if hp == 0 else "pool", afm.v(), kkn.v(), afm.v(), ALU.mult)
        yield
        bfm = afm
        K.stt(csum.v(), xr_r.v(), pcol("r_k", hp), kmod.v(), ALU.mult, ALU.mult)
        yield
        for tb in range(4):
            col0, n = TBLK[tb]
            ps = PSF()
            K.mm(ps[:, :n], blk1.v(), csum[:, col0:col0 + n])
            K.tt("dve", csum[:, col0:col0 + n], ps[:, :n], vbf[:, col0:col0 + n], ALU.mult)
            yield
        for c4 in range(4):
            ps = PSF()
            for j in range(4):
                c = c4 * 4 + j
                K.tr(ps[:, j * 128:(j + 1) * 128], csum[:, c * 128:(c + 1) * 128], identf.v())
            bt = btile[0]
            K.copy("act", bt.v(), ps.v())
            yield
            K.dma("sp", dap(bonus_d, c4 * 512 * RW + hp * 128, [[RW, 128], [128 * RW, 4], [1, 128]]),
                  bt.v().re("p (j f) -> p j f", j=4))
            yield
        K.act(decs.v(), sigd[:, T:TT], AF.Exp, scale=-CDEC)
        yield
        K.ts("dve", avs.v(), kkn[:, T:TT], -1.0, None, ALU.mult)
        yield
        yield

    def derivedB(hp):
        xr_r, xr_k = SC[0], SC[1]
        sigd, csum, afm, kkn = SC[2], SC[3], SC[4], SC[5]
        kmod, bfm = xr_k, afm
        K.scan(csum[:, 0:T], resetm.v(), sigd[:, 0:T], 0.0)
        gt = sigd
        c3 = lambda b_: b_[:, 0:T].re("p (c l) -> p c l", l=128)
        K.act(gt[:, 0:T], csum[:, 0:T], AF.Exp, scale=-CDEC)
        K.tt("dve", AR[:, :, 1, :], c3(xr_r), c3(gt), ALU.mult)
        K.copy("dve", gL.v(), c3(gt)[:, :, 127])
        K.stt(AR[:, :, 0, 1:128], c3(kkn)[:, :, 1:128], -1.0, c3(gt)[:, :, 0:127], ALU.mult, ALU.mult)
        K.ts("dve", AR[:, :, 0, 0], c3(kkn)[:, :, 0], -1.0, None, ALU.mult)
        K.act(gt[:, 0:T], csum[:, 0:T], AF.Exp, scale=CDEC)
        K.tt("dve", bti.v(), bfm[:, 0:T], gt[:, 0:T], ALU.mult)
        K.tt("dve", kti.v(), kmod[:, 0:T], gt[:, 0:T], ALU.mult)
        K.tt("dve", c3(gt), c3(csum)[:, :, 127:128].bc([128, 16, 128]), c3(csum), ALU.subtract)
        K.act(gt[:, 0:T], gt[:, 0:T], AF.Exp, scale=-CDEC)
        K.tt("dve", bhat.v(), bfm[:, 0:T], gt[:, 0:T], ALU.mult)
        K.tt("dve", khat.v(), kmod[:, 0:T], gt[:, 0:T], ALU.mult)
        for c in range(16):
            pb = PSB()
            pv = pb.v().re("p (q t) -> p q t", q=8)
            cs = slice(c * 128, (c + 1) * 128)
            K.tr(pv[:, 0, :], vbf[:, cs], identb.v())
            K.tr(pv[:, 1, :], bhat[:, cs], identb.v())
            K.tr(pv[:, 2, :], khat[:, cs], identb.v())
            K.tr(pv[:, 3, :], AR[:, c, 0, :], identb.v())
            K.copy("act" if c % 2 else "dve", TMq[:, c, :, :], pv[:, 0:4, :])
        sq_src = [xr_r[:, T:TT], kmod[:, T:TT], vs32.v(), decs.v(), avs.v(), bfm[:, T:TT]]
        for half in range(2):
            ps = PSF()
            for j in range(3):
                K.tr(ps[:NS, j * 128:(j + 1) * 128], sq_src[half * 3 + j], identf.v())
            K.copy("dve", smp.v(), ps[:NS, :384].re("p (q f) -> p q f", q=3))
            for j in range(3):
                q = half * 3 + j
                K.dma("sp", dap(samp_d, (2 * hp) * 448 + q * 64, [[8 * 448, NS], [448, 2], [1, 64]]),
                      smp[:, j, :].re("p (h n) -> p h n", h=2))


    def chunk_thunks(hp):
        CH = []
        CH.append(lambda: K.memset('dve', Sf.v(), 0.0))
        CH.append(lambda: K.memset('pool', Sb.v(), 0.0))
        pairs = [(c, hl) for c in range(16) for hl in range(2)]

        def pre_a(sl, c, hl):
            rows = slice(hl * 64, (hl + 1) * 64)
            cs = slice(c * 128, (c + 1) * 128)
            ps = PSF()
            arv = AR[rows, c, :, :].re("p a t -> p (a t)")
            K.mm(ps[:, 0:256], bti[rows, cs], arv)
            K.mm(ps[:, 256:512], kti[rows, cs], arv)
            ps2 = PSF()
            K.mm(ps2[:, 0:128], AR[rows, c, 0, :], bti[rows, cs])
            p4 = ps.v().re("p (q t) -> p q t", q=4)
            K.tt("dve", ATMP[sl % NPQ].v(), p4[:, 0::2, :], mask4[:, 0::2, :], ALU.mult)
            K.tt("dve", ABK[sl].v(), p4[:, 1::2, :], mask4[:, 1::2, :], ALU.mult)
            K.tt("dve", P0T[sl % NPQ].v(), ps2[:, 0:128], maskNT.v(), ALU.mult)
            K.tt("pool", QTb[sl % NPQ][0].v(), P0T[sl % NPQ].v(), identb.v(), ALU.add)

        def pre_lev(sl, lev, gi):
            if lev == 0:
                Pj, PjT, Mp = ATMP[sl % NPQ][:, 0, :], P0T[sl % NPQ].v(), identb.v()
            else:
                src = PMb[sl % NPQ][lev % 2]
                Pj, PjT, Mp = src[:, 0, :], src[:, 1, :], src[:, 2, :]
            dst = PMb[sl % NPQ][(lev + 1) % 2]
            ps = PSF()
            if lev < 6:
                K.mm(ps[:, 0:128], PjT, Pj)
                K.mm(ps[:, 128:256], Pj, PjT)
            K.mm(ps[:, 256:384], QTb[sl % NPQ][lev % 2].v(), Mp, start=True, stop=True)
            eng = "act" if (gi % 8) not in (2, 5, 7) else "dve"
            if lev < 6:
                K.copy(eng, dst.v(), ps[:, 0:384].re("p (q t) -> p q t", q=3))
                K.tt("pool", QTb[sl % NPQ][(lev + 1) % 2].v(), dst[:, 1, :], identb.v(), ALU.add)
            else:
                K.copy(eng, dst[:, 2, :], ps[:, 256:384])

        def pre_w(sl, c, hl, gi):
            rows = slice(hl * 64, (hl + 1) * 64)
            cs = slice(c * 128, (c + 1) * 128)
            Mfin = PMb[sl % NPQ][1][:, 2, :]
            ps = PSF()
            K.mm(ps[:, 0:128], TMq[:, c, 3, :], Mfin)
            K.mm(ps[:, 128:192], ATMP[sl % NPQ][:, 1, :], TMq[:, c, 0, rows])
            eng = "act" if gi % 2 else "dve"
            K.copy(eng, Wfm[rows, cs], ps[rows, 0:128])
            K.copy(eng, Ybf[sl % NPQ].v(), ps[:, 128:192])

        def pre_u(sl, gi):
            Mfin = PMb[sl % NPQ][1][:, 2, :]
            ps = PSF()
            K.mm(ps[:, 0:64], Mfin, Ybf[sl % NPQ].v())
            K.copy("act" if gi % 2 else "dve", Ubar[sl].v(), ps[:, 0:64])

        def precompute(group):
            th = []
            for (sl, c, hl) in group:
                th.append(partial(pre_a, sl, c, hl))
            for lev in range(7):
                for gi, (sl, c, hl) in enumerate(group):
                    th.append(partial(pre_lev, sl, lev, gi))
            for gi, (sl, c, hl) in enumerate(group):
                th.append(partial(pre_w, sl, c, hl, gi))
            for gi, (sl, c, hl) in enumerate(group):
                th.append(partial(pre_u, sl, gi))
            return th

        def ser_u(sl, c, hl, gi):
            rows = slice(hl * 64, (hl + 1) * 64)
            cs = slice(c * 128, (c + 1) * 128)
            ut = Ut[gi % 4]
            ps = PSF()
            K.mm(ps[:, 0:64], Wfm[rows, cs], Sb[rows, :])
            K.tt("dve", ut.v(), ps[:, 0:64], Ubar[sl].v(), ALU.add)

        def ser_os(sl, c, hl, gi):
            rows = slice(hl * 64, (hl + 1) * 64)
            ut = Ut[gi % 4]
            pso = PSF()
            K.mm(pso[:, 0:64], AR[rows, c, 1, :], Sb[rows, :], start=True, stop=False)
            K.mm(pso[:, 0:64], ABK[sl][:, 0, :], ut.v(), start=False, stop=False)
            K.mm(pso[:, 0:64], ABK[sl][:, 1, :], TMq[:, c, 0, rows], start=False, stop=True)
            pss = PSF()
            K.mm(pss[:, 0:64], TMq[:, c, 1, :], ut.v(), start=True, stop=False)
            K.mm(pss[:, 0:64], TMq[:, c, 2, :], TMq[:, c, 0, rows], start=False, stop=True)
            K.stt(Sf[rows, :], Sf[rows, :], gL[rows, c:c + 1], pss[rows, 0:64], ALU.mult, ALU.add)
            K.copy("act", Sb[rows, :], Sf[rows, :])
            ot = Otile[c % 2]
            K.copy("act", ot[:, rows], pso[:, 0:64])
            if hl == 1:
                K.dma("sp", o_d[c * 128:(c + 1) * 128, hp * 128:(hp + 1) * 128], ot.v())

        def serial(group):
            th = []
            for gi, (sl, c, hl) in enumerate(group):
                th.append(partial(ser_u, sl, c, hl, gi))
                th.append(partial(ser_os, sl, c, hl, gi))
            return th

        def run_merged(lists):
            idx = [0] * len(lists)
            while True:
                best, bf = -1, 2.0
                for li, L in enumerate(lists):
                    if idx[li] < len(L):
                        frac = idx[li] / len(L)
                        if frac < bf:
                            best, bf = li, frac
                if best < 0:
                    break
                lists[best][idx[best]]()
                idx[best] += 1

        G = 8
        groups = []
        for g0 in range(0, 32, G):
            groups.append([((g0 + i) % NPR, pairs[g0 + i][0], pairs[g0 + i][1]) for i in range(G)])
        CH.extend(precompute(groups[0]))
        for gi_ in range(len(groups)):
            lists = [serial(groups[gi_])]
            if gi_ + 1 < len(groups):
                lists.append(precompute(groups[gi_ + 1]))
            CH.extend(merge_lists(lists))
        def fin_state():
            ps = PSF()
            K.tr(ps[:64, 0:128], Sf.v(), identf.v())
            K.copy("dve", wkvo[:64, :], ps[:64, 0:128])
            K.dma("sp", o_wkv_p[2 * hp:2 * hp + 2, :, :].re("h v k -> v h k"), wkvo[:64, :].re("v (h k) -> v h k", h=2))

        CH.append(fin_state)
        return CH

    def drain(gen):
        for _ in gen:
            pass

    NA_EST = 150
    drain(derivedA(0))
    derivedB(0)
    for hp in range(4):
        CH = chunk_thunks(hp)
        OUTL = []
        gen = derivedA(hp + 1) if hp + 1 < 4 else None
        per = max(1, len(CH) // NA_EST)
        pero = max(1, len(CH) // max(1, len(OUTL)))
        oi = 0
        for ti, th in enumerate(CH):
            th()
            if gen is not None and ti % per == per - 1:
                try:
                    next(gen)
                except StopIteration:
                    gen = None
            if oi < len(OUTL) and ti % pero == pero - 1:
                OUTL[oi]()
                oi += 1
        while oi < len(OUTL):
            OUTL[oi]()
            oi += 1
        if gen is not None:
            drain(gen)
        if hp + 1 < 4:
            derivedB(hp + 1)

    K.pop_scope()
    if stage == 3:
        zt = K.sb("zt", [128, 4096], F32)
        K.memset("pool", zt.v(), 0.0)
        K.dma("sp", o_wkv_s.v(), zt[:, :])
        for i in range(16):
            K.dma("sp", y_p[i * 128:(i + 1) * 128, :], zt[:, 0:D])
        K.dma("sp", y_s.v(), zt[0:NS, 0:D])
        return early()

    def bcast_row(name, src, n):
        t = K.sb(name, [128, n], F32)
        K.dma("sp", t.v(), dap(src, 0, [[0, 128], [1, n]]))
        return t

    K.push_scope()
    hgl = K.sb("hgl2", [128, 4, TT], BF16)
    wmo = K.sb("wmo", [128, 8, D], BF16)
    gpost = K.sb("gpost_mix", [128, D], F32)
    PREF = [lambda: K.dma("sp", gpost.v(), dap(g_post_mix, 0, [[0, 128], [1, D]])),
            lambda: K.dma("sp", hgl.v().re("p k t -> p (k t)"), hgl_d.v())]
    for kt in range(8):
        PREF.append(partial(lambda kt_: K.dma("pool", wmo[:, kt_, :], w_merge_out[kt_ * 128:(kt_ + 1) * 128, :]), kt))
    mring = [K.sb("mring", [128, 8, 128], BF16) for _ in range(10)]
    mr_i = [0]

    def mload(src, c0, ncols_total, kt_n):
        mr_i[0] += 1
        wb = mring[mr_i[0] % len(mring)]
        K.dma("pool", wb[:, :kt_n, :], dap(src, c0, [[ncols_total, 128], [128 * ncols_total, kt_n], [1, 128]]))
        return wb

    GOFF = SHIFT + RW
    mw = {}

    def mg_load(f):
        mw[f] = (mload(w_rwkv_out, f * 128, D, 4), mload(w_in, GOFF + f * 128, PROJ, 8),
                 mload(w_in, GOFF + D + f * 128, PROJ, 8), mload(glu_w1, f * 128, D, 4), mload(glu_w2, f * 128, D, 4))

    PREF.append(partial(mg_load, 0))
    K.push_scope()
    eps_ln = K.sb("eps_ln2", [128, 1], F32)
    K.memset("pool", eps_ln.v(), 64e-5)
    NORF = 3
    fot_r = [K.sb("fot", [128, RW], F32) for _ in range(NORF)]
    fbt_r = [K.sb("fbt", [128, RW], F32) for _ in range(NORF)]
    fon_r = [K.sb("fon", [128, RW], F32) for _ in range(NORF)]
    fofb = [K.sb("fofb", [128, RW], BF16) for _ in range(NORF)]
    fst8 = [K.sb("fst8", [128, 6, 8], F32) for _ in range(NORF)]
    fh3 = lambda b_: b_.v().re("p (h n) -> p h n", h=8)

    def fout_a(c):
        r_ = c % NORF
        ot, bt, tmp, s8 = fot_r[r_], fbt_r[r_], fon_r[r_], fst8[r_]
        K.dma("sp", ot.v(), o_d[c * 128:(c + 1) * 128, :])
        K.dma("sp", bt.v(), bonus_d[c * 128:(c + 1) * 128, :])
        K.reduce(s8[:, 0, :], fh3(ot))
        K.act(tmp.v(), ot.v(), AF.Square)
        K.reduce(s8[:, 1, :], fh3(tmp))
        K.ts("dve", s8[:, 2, :], s8[:, 0, :], 1.0 / 64, None, ALU.mult)
        K.tt("dve", s8[:, 3, :], s8[:, 2, :], s8[:, 2, :], ALU.mult)
        K.stt(s8[:, 4, :], s8[:, 1, :], 1.0 / 64, s8[:, 3, :], ALU.mult, ALU.subtract)
        K.act(s8[:, 4, :], s8[:, 4, :], AF.Sqrt, bias=eps_ln.v(), scale=1.0)
        K.recip(s8[:, 5, :], s8[:, 4, :])
        K.stt(s8[:, 3, :], s8[:, 2, :], -1.0, s8[:, 5, :], ALU.mult, ALU.mult)

    def fout_b(c):
        r_ = c % NORF
        ot, bt, on, of, s8 = fot_r[r_], fbt_r[r_], fon_r[r_], fofb[r_], fst8[r_]
        for h in range(8):
            K.act(on[:, h * 64:(h + 1) * 64], ot[:, h * 64:(h + 1) * 64], AF.Identity,
                  bias=s8[:, 3, h:h + 1], scale=s8[:, 5, h:h + 1])
        K.tt("pool", bt.v(), bt.v(), lnxb.v(), ALU.add)
        K.tt("dve", on.v(), on.v(), lnxg.v(), ALU.mult)
        K.tt("dve", on.v(), on.v(), bt.v(), ALU.add)
        ps = PSF()
        K.mm(ps.v(), lora_bf[64:128, c * 128:(c + 1) * 128], lora_up[64:128, :])
        K.tt("dve", of.v(), on.v(), ps.v(), ALU.mult)
        pb = PSB()
        pv = pb.v().re("p (q t) -> p q t", q=8)
        for kt in range(4):
            K.tr(pv[:, kt, :], of[:, kt * 128:(kt + 1) * 128], identb.v())
        K.copy("act", oaT[:, :, c * 128:(c + 1) * 128], pv[:, 0:4, :])

    OUT = []
    for c in range(16 + 2):
        if c < 16:
            OUT.append(partial(fout_a, c))
        if 0 <= c - 2 < 16:
            OUT.append(partial(fout_b, c - 2))

    gs_tm = K.sb("gs_tm", [NS, RW], F32)
    ps = PSF()
    K.mm(ps[:NS, :], lora_bf[64:128, T:TT], lora_up[64:128, :])
    K.copy("act", gs_tm.v(), ps[:NS, :])
    K.dma("sp", dap(samp_d, 6 * 64, [[8 * 448, NS], [448, 8], [1, 64]]), gs_tm.v().re("p (h n) -> p h n", h=8))
    vec = K.sb("vec", [128, 7, 64], F32)
    K.dma("sp", vec.v().re("p q n -> p (q n)"), dap(samp_d, 0, [[448, 128], [1, 448]]))
    S0s = K.sb("S0s", [128, 64, 64], F32)
    S1s = K.sb("S1s", [128, 64, 64], F32)
    tS = K.sb("tS", [128, 64, 64], F32)
    K.dma("sp", S0s.v().re("p v k -> p (v k)"), st_wkv.v())
    r_, km_, v_, dec_, av_, b_, g_ = [vec[:, q, :] for q in range(7)]
    overv = lambda x: x.un(1).bc([128, 64, 64])
    overk = lambda x: x.un(2).bc([128, 64, 64])
    sv = K.sb("sv", [128, 8, 64], F32)
    ss = K.sb("ss", [128, 8], F32)
    eps2 = K.sb("eps_ln3", [128, 1], F32)
    K.memset("pool", eps2.v(), 64e-5)
    os_tm = K.sb("os_tm", [NS, RW], F32)
    os_bf = K.sb("os_bf", [NS, RW], BF16)
    SMP = [
        lambda: K.tt("dve", tS.v(), S0s.v(), overv(av_), ALU.mult),
        lambda: K.reduce(sv[:, 0, :], tS.v()),
        lambda: K.tt("pool", S1s.v(), S0s.v(), overv(dec_), ALU.mult),
        lambda: K.tt("dve", tS.v(), overk(sv[:, 0, :]), overv(b_), ALU.mult),
        lambda: K.tt("dve", S1s.v(), S1s.v(), tS.v(), ALU.add),
        lambda: K.tt("dve", tS.v(), overk(v_), overv(km_), ALU.mult),
        lambda: K.tt("dve", S1s.v(), S1s.v(), tS.v(), ALU.add),
        lambda: K.dma("sp", o_wkv_s.v(), S1s.v().re("p v k -> p (v k)")),
        lambda: K.tt("dve", tS.v(), S1s.v(), overv(r_), ALU.mult),
        lambda: K.reduce(sv[:, 1, :], tS.v()),
        lambda: K.reduce(ss[:, 0:1], sv[:, 1, :]),
        lambda: K.tt("dve", sv[:, 2, :], sv[:, 1, :], sv[:, 1, :], ALU.mult),
        lambda: K.reduce(ss[:, 1:2], sv[:, 2, :]),
        lambda: K.ts("dve", ss[:, 2:3], ss[:, 0:1], 1.0 / 64, None, ALU.mult),
        lambda: K.tt("dve", ss[:, 3:4], ss[:, 2:3], ss[:, 2:3], ALU.mult),
        lambda: K.stt(ss[:, 4:5], ss[:, 1:2], 1.0 / 64, ss[:, 3:4], ALU.mult, ALU.subtract),
        lambda: K.act(ss[:, 4:5], ss[:, 4:5], AF.Sqrt, bias=eps2.v(), scale=1.0),
        lambda: K.recip(ss[:, 5:6], ss[:, 4:5]),
        lambda: K.ts("dve", sv[:, 3, :], sv[:, 1, :], ss[:, 2:3], ss[:, 5:6], ALU.subtract, ALU.mult),
        lambda: K.tt("dve", sv[:, 3, :], sv[:, 3, :], par[:, 0, :], ALU.mult),
        lambda: K.tt("dve", sv[:, 3, :], sv[:, 3, :], par[:, 1, :], ALU.add),
        lambda: K.tt("dve", sv[:, 4, :], r_, km_, ALU.mult),
        lambda: K.tt("dve", sv[:, 4, :], sv[:, 4, :], par[:, 2, :], ALU.mult),
        lambda: K.reduce(ss[:, 6:7], sv[:, 4, :]),
        lambda: K.stt(sv[:, 3, :], v_, ss[:, 6:7], sv[:, 3, :], ALU.mult, ALU.add),
        lambda: K.tt("dve", sv[:, 5, :], sv[:, 3, :], g_, ALU.mult),
        lambda: K.dma("sp", dap(sampo_d, 0, [[64, 128], [1, 64]]), sv[:, 5, :]),
        lambda: K.dma("sp", os_tm.v(), sampo_d.v()),
        lambda: K.copy("dve", os_bf.v(), os_tm.v()),
    ]

    def smp_fin():
        pb = PSB()
        pv = pb.v().re("p (q t) -> p q t", q=8)
        for kt in range(4):
            K.tr(pv[:, kt, :NS], os_bf[:, kt * 128:(kt + 1) * 128], identb[:NS, :NS])
        K.copy("act", oaT[:, :, T:TT], pv[:, 0:4, :NS])

    SMP.append(smp_fin)
    run_merged_g([OUT, SMP, [lambda: None] * 6 + PREF])
    K.pop_scope()
    if stage == 4:
        zt = K.sb("zt", [128, 1024], F32)
        K.memset("pool", zt.v(), 0.0)
        for i in range(16):
            K.dma("sp", y_p[i * 128:(i + 1) * 128, :], zt[:, 0:D])
        K.dma("sp", y_s.v(), zt[0:NS, 0:D])
        return early()


    def pn_b(ps_halves, rows, res_src, gpost, dst, bufs):
        st, xres, tt_ = bufs[:3]
        K.dma("sp", xres[:rows, :], res_src)
        for hf in range(2):
            K.act(tt_[:rows, hf * 512:(hf + 1) * 512], ps_halves[hf][:rows, :], AF.Square, accum=st[:rows, hf:hf + 1])
        K.tt("dve", st[:rows, 2:3], st[:rows, 0:1], st[:rows, 1:2], ALU.add)
        K.act(st[:rows, 3:4], st[:rows, 2:3], AF.Sqrt, bias=epsT[:rows, :], scale=1.0 / D)
        K.recip(st[:rows, 4:5], st[:rows, 3:4])
        for hf in range(2):
            hs = slice(hf * 512, (hf + 1) * 512)
            K.stt(tt_[:rows, hs], ps_halves[hf][:rows, :], st[:rows, 4:5], gpost[:rows, hs], ALU.mult, ALU.mult)
        K.tt("dve", xres[:rows, :], xres[:rows, :], tt_[:rows, :], ALU.add)
        K.dma("sp", dst, xres[:rows, :])

    def pn_c(rows, bufs, nt):
        st, xres, tt_, xb = bufs
        gname, dstT, col0 = nt
        K.act(tt_[:rows, :], xres[:rows, :], AF.Square, accum=st[:rows, 5:6])
        K.act(st[:rows, 6:7], st[:rows, 5:6], AF.Sqrt, bias=epsT[:rows, :], scale=1.0 / D)
        K.recip(st[:rows, 7:8], st[:rows, 6:7])
        K.act(xb[:rows, :], xres[:rows, :], AF.Copy, scale=st[:rows, 7:8])
        pb = PSB()
        pv = pb.v().re("p (k t) -> p k t", k=8)
        for kt in range(8):
            K.tr(pv[:, kt, :rows], xb[:rows, kt * 128:(kt + 1) * 128], identb[:rows, :rows])
        g0 = PV[gname]
        K.tt("dve", dstT[:, :, col0:col0 + rows], pv[:, :, :rows],
             pvec[:, g0:g0 + 8].un(2).bc([128, 8, rows]), ALU.mult)

    mT = K.sb("mT", [128, 8, TT], BF16)
    mt = [[K.sb("mt", [128, 512], F32) for _ in range(4)] for _ in range(2)]
    mps = {}

    def mg_a1(f, tb, it):
        col0, n = TBLK[tb]
        cs = slice(col0, col0 + n)
        wA, wga, wgb, w1, w2 = mw[f]
        ps2, ps1 = PSF(), PSF()
        for kt in range(4):
            K.mm(ps2[:, :n], w2[:, kt, :], hgl[:, kt, cs], start=(kt == 0), stop=(kt == 3))
        for kt in range(4):
            K.mm(ps1[:, :n], w1[:, kt, :], hgl[:, kt, cs], start=(kt == 0), stop=(kt == 3))
        mps[(it, 1)] = (ps2, ps1)

    def mg_a2(f, tb, it):
        col0, n = TBLK[tb]
        cs = slice(col0, col0 + n)
        wA, wga, wgb, w1, w2 = mw[f]
        psgb = PSF()
        for kt in range(8):
            K.mm(psgb[:, :n], wgb[:, kt, :], hT[:, kt, cs], start=(kt == 0), stop=(kt == 7))
        mps[(it, 2)] = psgb

    def mg_a3(f, tb, it):
        col0, n = TBLK[tb]
        cs = slice(col0, col0 + n)
        wA, wga, wgb, w1, w2 = mw[f]
        psga, psA = PSF(), PSF()
        for kt in range(8):
            K.mm(psga[:, :n], wga[:, kt, :], hT[:, kt, cs], start=(kt == 0), stop=(kt == 7))
        for kt in range(4):
            K.mm(psA[:, :n], wA[:, kt, :], oaT[:, kt, cs], start=(kt == 0), stop=(kt == 3))
        mps[(it, 3)] = (psga, psA)

    def mg_b1(f, tb, it):
        col0, n = TBLK[tb]
        t0, t1_, t2_, t3_ = mt[it % 2]
        ps2, ps1 = mps.pop((it, 1))
        K.act(t0[:, :n], ps2[:, :n], AF.Sigmoid, bias=pcol("glu_b2", f))
        K.stt(t1_[:, :n], ps1[:, :n], pcol("glu_b1", f), t0[:, :n], ALU.add, ALU.mult)

    def mg_b2(f, tb, it):
        col0, n = TBLK[tb]
        t0, t1_, t2_, t3_ = mt[it % 2]
        psgb = mps.pop((it, 2))
        K.act(t2_[:, :n], psgb[:, :n], AF.Sigmoid)
        K.tt("dve", t1_[:, :n], t1_[:, :n], t2_[:, :n], ALU.mult)

    def mg_b3(f, tb, it):
        col0, n = TBLK[tb]
        cs = slice(col0, col0 + n)
        t0, t1_, t2_, t3_ = mt[it % 2]
        psga, psA = mps.pop((it, 3))
        K.act(t3_[:, :n], psga[:, :n], AF.Sigmoid)
        K.tt("dve", t3_[:, :n], psA[:, :n], t3_[:, :n], ALU.mult)
        K.tt("pool" if n > 16 else "dve", mT[:, f, cs], t3_[:, :n], t1_[:, :n], ALU.add)

    its = [(f, tb) for f in range(8) for tb in range(5)]
    for it, (f, tb) in enumerate(its):
        if tb == 0 and f + 1 < 8:
            mg_load(f + 1)
        mg_a1(f, tb, it)
        if it >= 1:
            pf, ptb = its[it - 1]
            mg_b3(pf, ptb, it - 1)
        mg_a2(f, tb, it)
        mg_b1(f, tb, it)
        mg_a3(f, tb, it)
        mg_b2(f, tb, it)
    mg_b3(its[-1][0], its[-1][1], len(its) - 1)
    pn_bufs = [(K.sb("pn_st", [128, 8], F32), K.sb("pn_x", [128, D], F32), K.sb("pn_t", [128, D], F32),
                K.sb("pn_xb", [128, D], BF16)) for _ in range(3)]
    mix_ps = {}

    def mix_a(i):
        rows = 128 if i < 16 else NS
        tcs = slice(i * 128, i * 128 + rows)
        halves = [PSF(), PSF()]
        for hf in range(2):
            for kt in range(8):
                K.mm(halves[hf][:rows, :], mT[:, kt, tcs], wmo[:, kt, hf * 512:(hf + 1) * 512],
                     start=(kt == 0), stop=(kt == 7))
        mix_ps[i] = halves

    def mix_b(i):
        rows = 128 if i < 16 else NS
        pn_b(mix_ps[i], rows, x_tile(i), gpost, x1_d[i * 128:i * 128 + rows, :], pn_bufs[i % 3])

    def mix_c(i):
        rows = 128 if i < 16 else NS
        pn_c(rows, pn_bufs[i % 3], ("g_pre_ffn", hT, i * 128))

    for step in range(17 + 3):
        if step < 17:
            mix_a(step)
        if 0 <= step - 2 < 17:
            mix_b(step - 2)
        if 0 <= step - 3 < 17:
            mix_c(step - 3)
    K.pop_scope()
    K.pop_scope()

    K.push_scope()
    NF = DFF // 128
    aT = K.sb("aT", [128, NF, TT], BF16)
    wd = K.sb("wd", [128, NF, D], BF16)
    gpost2 = bcast_row("gpost_ffn", g_post_ffn, D)
    fring = [K.sb("fring", [128, 8, 128], BF16) for _ in range(4)]
    fr_i = [0]

    def fload(src, c0):
        fr_i[0] += 1
        wb = fring[fr_i[0] % len(fring)]
        K.dma("pool", wb.v(), dap(src, c0, [[DFF, 128], [128 * DFF, 8], [1, 128]]))
        return wb

    sgt = [K.sb("sgt", [128, 512], F32) for _ in range(2)]
    it = 0
    for ft in range(NF):
        wg = fload(w_ffn_gate, ft * 128)
        wu = fload(w_ffn_up, ft * 128)
        if ft >= 1:
            K.dma("pool", wd[:, ft - 1, :], w_ffn_down[(ft - 1) * 128:ft * 128, :])
        if ft == NF - 1:
            K.dma("pool", wd[:, ft, :], w_ffn_down[ft * 128:(ft + 1) * 128, :])
        for tb in range(5):
            col0, n = TBLK[tb]
            cs = slice(col0, col0 + n)
            psg, psu = PSF(), PSF()
            for kt in range(8):
                K.mm(psg[:, :n], wg[:, kt, :], hT[:, kt, cs], start=(kt == 0), stop=(kt == 7))
            for kt in range(8):
                K.mm(psu[:, :n], wu[:, kt, :], hT[:, kt, cs], start=(kt == 0), stop=(kt == 7))
            sg = sgt[it % 2]
            it += 1
            K.act(sg[:, :n], psg[:, :n], AF.Silu)
            K.tt("dve", aT[:, ft, cs], psu[:, :n], sg[:, :n], ALU.mult)
    pn_bufs = [(K.sb("pn_st", [128, 8], F32), K.sb("pn_x", [128, D], F32), K.sb("pn_t", [128, D], F32))
               for _ in range(2)]
    dn_ps = {}

    def dn_a(i):
        rows = 128 if i < 16 else NS
        tcs = slice(i * 128, i * 128 + rows)
        halves = [PSF(), PSF()]
        for hf in range(2):
            for kt in range(NF):
                K.mm(halves[hf][:rows, :], aT[:, kt, tcs], wd[:, kt, hf * 512:(hf + 1) * 512],
                     start=(kt == 0), stop=(kt == NF - 1))
        dn_ps[i] = halves

    def dn_b(i):
        rows = 128 if i < 16 else NS
        dst = y_p[i * 128:(i + 1) * 128, :] if i < 16 else y_s[:, :]
        pn_b(dn_ps[i], rows, x1_d[i * 128:i * 128 + rows, :], gpost2, dst, pn_bufs[i % 2])

    for step in range(17 + 1):
        if step < 17:
            dn_a(step)
        if step - 1 >= 0:
            dn_b(step - 1)
    K.pop_scope()
    K.finish()
    es.close()
    return nc


_W_NAMES = ["norm_pre_mix", "norm_post_mix", "norm_pre_ffn", "norm_post_ffn", "w_in", "mu_shift", "w0",
            "w_decay_up", "a0", "w_aaa_up", "w_gate_up", "k_k", "k_a", "r_k", "lnx_g", "lnx_b", "w_rwkv_out",
            "s5_lam_re", "s5_lam_im", "s5_log_dt", "s5_b_re", "s5_b_im", "s5_c_re", "s5_c_im", "s5_d",
            "glu_w1", "glu_b1", "glu_w2", "glu_b2", "w_merge_out", "w_ffn_gate", "w_ffn_up", "w_ffn_down"]


def make_in_maps(inputs, cores):
    f = lambda a: np.ascontiguousarray(np.asarray(a, dtype=np.float32))
    shared = {n: f(inputs[n])[0] for n in _W_NAMES}
    maps = []
    for c in cores:
        m = dict(shared)
        m["x_p"] = f(inputs["x_prompt"][c])
        m["x_s"] = f(inputs["x_sample"][c * NS:(c + 1) * NS, 0])
        m["st_shift"] = f(inputs["state_shift"][0, c * NS:(c + 1) * NS])
        m["st_wkv"] = f(inputs["state_wkv"][0, c * NS:(c + 1) * NS]).reshape(NS * 8, 64 * 64)
        m["st_re"] = f(inputs["state_s5_re"][0, c * NS:(c + 1) * NS]).reshape(NS, 2048)
        m["st_im"] = f(inputs["state_s5_im"][0, c * NS:(c + 1) * NS]).reshape(NS, 2048)
        maps.append(m)
    return maps


def assemble(results):
    n = len(results)
    cat = lambda k: np.concatenate([np.asarray(r[k]) for r in results], axis=0)
    y_p = np.stack([np.asarray(r["y_p"]) for r in results], 0)
    y_s = cat("y_s").reshape(n * NS, 1, D)
    sh_p = cat("o_shift_p").reshape(1, n, SHIFT)
    wkv_p = np.stack([np.asarray(r["o_wkv_p"]) for r in results], 0).reshape(1, n, 8, 64, 64)
    re_p = cat("o_re_p").reshape(1, n, 32, 64)
    im_p = cat("o_im_p").reshape(1, n, 32, 64)
    sh_s = cat("o_shift_s").reshape(1, n * NS, SHIFT)
    wkv_s = cat("o_wkv_s").reshape(1, n * NS, 8, 64, 64)
    re_s = cat("o_re_s").reshape(1, n * NS, 32, 64)
    im_s = cat("o_im_s").reshape(1, n * NS, 32, 64)
    return tuple(np.ascontiguousarray(a, dtype=np.float32) for a in
                 (y_p, y_s, sh_p, wkv_p, re_p, im_p, sh_s, wkv_s, re_s, im_s))


def kernel(**inputs):
    nc = build()
    in_maps = make_in_maps(inputs, list(range(NCORES)))
    res = run_bass_kernel_spmd(nc, in_maps, core_ids=list(range(NCORES)))
    return assemble(res.results)
```
